# Optimizing a Trainium2 kernel written in Bass

```python
import math
import jax
import jax.numpy as jnp
from jax import lax
import numpy as np

D_MODEL = 2048
BATCH = 8
SEQ = 2048
DEPTH = 1

GRID_W = 64
CTX_LEN = 256
NORM_EPS = 1e-6

RW_HEADS = 16
RW_HEAD_DIM = 64
RW_WIDTH = RW_HEADS * RW_HEAD_DIM
RW_DECAY_RANK = 96
RW_ICLR_RANK = 96
RW_GATE_RANK = 64
RW_GN_EPS = 64e-5
RW_COLS = 3 * RW_WIDTH + 2 * RW_DECAY_RANK + 2 * RW_ICLR_RANK + RW_GATE_RANK
RW_SPLITS = (RW_WIDTH, 2 * RW_WIDTH, 3 * RW_WIDTH, 3 * RW_WIDTH + 2 * RW_DECAY_RANK,
             3 * RW_WIDTH + 2 * RW_DECAY_RANK + 2 * RW_ICLR_RANK)

GDN_HEADS = 8
GDN_HEAD_DIM = 128
GDN_WIDTH = GDN_HEADS * GDN_HEAD_DIM
GDN_CONV = 5
GDN_CHUNK = 64
GDN_CONV_COLS = 3 * GDN_WIDTH
GDN_AB_COLS = 4 * GDN_HEADS

GATE_COLS = 2 * D_MODEL
IN_SPLITS = (RW_COLS, RW_COLS + GDN_CONV_COLS, RW_COLS + GDN_CONV_COLS + GDN_WIDTH,
             RW_COLS + GDN_CONV_COLS + GDN_WIDTH + GDN_AB_COLS)
N_IN = IN_SPLITS[-1] + GATE_COLS

FFN_HIDDEN = -(-8 * D_MODEL // (3 * 256)) * 256

kernel_name = 'hybrid_rwkv7_gdn_dit_block'


def rmsnorm(x, w):
    xf = x.astype(jnp.float32)
    y = xf * lax.rsqrt(jnp.mean(xf * xf, axis=-1, keepdims=True) + NORM_EPS)
    return (y * w.astype(jnp.float32)).astype(x.dtype)


def modulate(x, shift, scale):
    return x * (1.0 + scale) + shift


def l2norm(x):
    return x * lax.rsqrt(jnp.sum(x * x, axis=-1, keepdims=True) + NORM_EPS)


def grid_shift(p, rows):
    b, t, ch = p.shape
    g = p.reshape(b, rows, GRID_W, ch // 4, 4)
    left = jnp.pad(g[:, :, :-1, :, 0], ((0, 0), (0, 0), (1, 0), (0, 0)))
    right = jnp.pad(g[:, :, 1:, :, 1], ((0, 0), (0, 0), (0, 1), (0, 0)))
    up = jnp.pad(g[:, :-1, :, :, 2], ((0, 0), (1, 0), (0, 0), (0, 0)))
    down = jnp.pad(g[:, 1:, :, :, 3], ((0, 0), (0, 1), (0, 0), (0, 0)))
    return jnp.stack([left, right, up, down], axis=-1).reshape(b, t, ch)


def seq_shift(p):
    b, t, ch = p.shape
    g = p.reshape(b, t, ch // 2, 2)
    prev = jnp.pad(g[:, :-1, :, 0], ((0, 0), (1, 0), (0, 0)))
    nxt = jnp.pad(g[:, 1:, :, 1], ((0, 0), (0, 1), (0, 0)))
    return jnp.stack([prev, nxt], axis=-1).reshape(b, t, ch)


def depthwise_conv(x, w):
    return lax.conv_general_dilated(x, w[:, None, :], window_strides=(1,),
                                    padding=[(GDN_CONV // 2, GDN_CONV // 2)],
                                    dimension_numbers=('NWC', 'WIO', 'NWC'),
                                    feature_group_count=x.shape[-1])


def prefix_scan(scan_fn, ctx_in, lat_in, s0, reverse):
    flip = (lambda a: jnp.flip(a, axis=1)) if reverse else (lambda a: a)
    y_ctx, s_ctx = scan_fn(*[flip(a) for a in ctx_in], s0)
    y_lat, _ = scan_fn(*[flip(a) for a in lat_in], s_ctx)
    return flip(y_lat), flip(y_ctx)


def rwkv7_scan(r, w, k, v, kk, bb, s0):
    def step(s, inp):
        r_t, w_t, k_t, v_t, kk_t, b_t = inp
        sa = jnp.einsum('bhij,bhj->bhi', s, kk_t)
        s = s * w_t[:, :, None, :] - sa[..., None] * b_t[:, :, None, :] + v_t[..., None] * k_t[:, :, None, :]
        return s, jnp.einsum('bhij,bhj->bhi', s, r_t)
    xs = tuple(jnp.moveaxis(a, 1, 0) for a in (r, w, k, v, kk, bb))
    s_fin, y = lax.scan(step, s0, xs)
    return jnp.moveaxis(y, 0, 1), s_fin


def rwkv_features(p, p_shift, mu, w0, w2, a0, a2, g2, k_k, k_a):
    b, t, _ = p.shape
    xs = (p + (p_shift - p) * mu).astype(jnp.float32)
    r, k, v, wd, ad, gd = jnp.split(xs, RW_SPLITS, axis=-1)
    heads = lambda a: a.reshape(a.shape[:-1] + (RW_HEADS, RW_HEAD_DIM))
    lora_w = jnp.einsum('btdr,drc->btdc', jnp.tanh(wd.reshape(b, t, 2, RW_DECAY_RANK)), w2)
    w_log = -jax.nn.softplus(-(w0 + lora_w)) - 0.5
    decay = jnp.exp(-jnp.exp(w_log))
    iclr = jax.nn.sigmoid(a0 + jnp.einsum('btdr,drc->btdc', ad.reshape(b, t, 2, RW_ICLR_RANK), a2))
    gate = jax.nn.sigmoid(gd) @ g2
    kk = l2norm(heads(k * k_k))
    k_dir = heads(k[:, :, None, :] * (1.0 + (iclr - 1.0) * k_a))
    b_dir = kk[:, :, None] * heads(iclr)
    return heads(r), heads(v), gate, kk, heads(decay), k_dir, b_dir


def rwkv_dir_inputs(f, d):
    r, v, _, kk, decay, k_dir, b_dir = f
    return (r, decay[:, :, d], k_dir[:, :, d], v, kk, b_dir[:, :, d])


def rwkv_readout(y, f, r_k, ln_w, ln_b):
    r, v, gate, _, _, k_dir, _ = f
    b, t = y.shape[:2]
    mu = jnp.mean(y, axis=-1, keepdims=True)
    var = jnp.mean(jnp.square(y - mu), axis=-1, keepdims=True)
    yn = ((y - mu) * lax.rsqrt(var + RW_GN_EPS)).reshape(b, t, RW_WIDTH) * ln_w + ln_b
    k_bonus = jnp.mean(k_dir, axis=2)
    bonus = (jnp.sum(r * k_bonus * r_k, axis=-1, keepdims=True) * v).reshape(b, t, RW_WIDTH)
    return (yn + bonus) * gate


def gdn_chunked(q, k, v, g, beta, s0):
    b, t, h, dk = q.shape
    dv = v.shape[-1]
    n = t // GDN_CHUNK

    def chunks(a):
        a = a.reshape((b, n, GDN_CHUNK, h) + a.shape[3:])
        return jnp.moveaxis(a, (1, 3), (0, 2))

    qc, kc, vc = chunks(q * dk ** -0.5), chunks(k), chunks(v)
    bc = chunks(beta)
    gc = jnp.cumsum(chunks(g), axis=-1)
    idx = jnp.arange(GDN_CHUNK)
    incl = idx[:, None] >= idx[None, :]
    strict = idx[:, None] > idx[None, :]
    diff = gc[..., :, None] - gc[..., None, :]
    decay = jnp.where(incl, jnp.exp(jnp.where(incl, diff, 0.0)), 0.0)
    kb = kc * bc[..., None]
    lmat = jnp.where(strict, jnp.einsum('nbhik,nbhjk->nbhij', kb, kc) * decay, 0.0)
    rhs = jnp.concatenate([vc * bc[..., None], kb * jnp.exp(gc)[..., None]], axis=-1)
    sol = lax.linalg.triangular_solve(lmat, rhs, left_side=True, lower=True, unit_diagonal=True)
    u, wk = sol[..., :dv], sol[..., dv:]
    qk = jnp.where(incl, jnp.einsum('nbhik,nbhjk->nbhij', qc, kc) * decay, 0.0)
    q_dec = qc * jnp.exp(gc)[..., None]
    k_tail = kc * jnp.exp(gc[..., -1:] - gc)[..., None]
    g_last = jnp.exp(gc[..., -1])

    def step(s, inp):
        qk_i, u_i, w_i, qd_i, kt_i, gl_i = inp
        v_new = u_i - jnp.einsum('bhck,bhkv->bhcv', w_i, s)
        o = jnp.einsum('bhck,bhkv->bhcv', qd_i, s) + jnp.einsum('bhcj,bhjv->bhcv', qk_i, v_new)
        s = s * gl_i[..., None, None] + jnp.einsum('bhck,bhcv->bhkv', kt_i, v_new)
        return s, o

    s_fin, o = lax.scan(step, s0, (qk, u, wk, q_dec, k_tail, g_last))
    o = jnp.moveaxis(o, (0, 2), (1, 3)).reshape(b, t, h, dv)
    return o, s_fin


def gdn_features(p_conv, p_ab, conv_w, a_log, dt_bias):
    b, t, _ = p_conv.shape
    u = jax.nn.silu(depthwise_conv(p_conv, conv_w)).astype(jnp.float32)
    q, k, v = jnp.split(u, 3, axis=-1)
    heads = lambda a: a.reshape(b, t, GDN_HEADS, GDN_HEAD_DIM)
    ab = p_ab.astype(jnp.float32).reshape(b, t, 2, 2, GDN_HEADS)
    g = -jnp.exp(a_log.astype(jnp.float32)) * jax.nn.softplus(ab[:, :, 0] + dt_bias)
    beta = jax.nn.sigmoid(ab[:, :, 1])
    return l2norm(heads(q)), l2norm(heads(k)), heads(v), g, beta


def gdn_dir_inputs(f, d):
    q, k, v, g, beta = f
    return (q, k, v, g[:, :, d], beta[:, :, d])


def gdn_readout(o, z, norm_w):
    b, t = o.shape[:2]
    on = o * lax.rsqrt(jnp.mean(o * o, axis=-1, keepdims=True) + NORM_EPS) * norm_w
    return (on * jax.nn.silu(z.astype(jnp.float32)).reshape(o.shape)).reshape(b, t, GDN_WIDTH)


def gated_merge(o_a, o_b, gate_cols, p_a, p_b, w_out):
    g_a, g_b = jnp.split(gate_cols, 2, axis=-1)
    m = jax.nn.sigmoid(g_a) * (o_a @ p_a) + jax.nn.sigmoid(g_b) * (o_b @ p_b)
    return m @ w_out


def swiglu(h, w_gate_up, w_down):
    gate, up = jnp.split(h @ w_gate_up, 2, axis=-1)
    return (jax.nn.silu(gate) * up) @ w_down


def mix_tokens(h, hc, w_in, rw_mu, rw_w0, rw_w2, rw_a0, rw_a2, rw_g2, rw_k_k, rw_k_a, rw_r_k,
               rw_ln_w, rw_ln_b, gdn_conv_w, gdn_a_log, gdn_dt_bias, gdn_norm_w,
               merge_p_a, merge_p_b, w_out, need_ctx):
    rows = h.shape[1] // GRID_W
    bsz = h.shape[0]
    rw, cv, z, ab, gt = jnp.split(h @ w_in, IN_SPLITS, axis=-1)
    rw_c, cv_c, z_c, ab_c, gt_c = jnp.split(hc @ w_in, IN_SPLITS, axis=-1)

    fa = rwkv_features(rw, grid_shift(rw, rows), rw_mu, rw_w0, rw_w2, rw_a0, rw_a2, rw_g2, rw_k_k, rw_k_a)
    fa_c = rwkv_features(rw_c, seq_shift(rw_c), rw_mu, rw_w0, rw_w2, rw_a0, rw_a2, rw_g2, rw_k_k, rw_k_a)
    s0a = jnp.zeros((bsz, RW_HEADS, RW_HEAD_DIM, RW_HEAD_DIM), jnp.float32)
    out_a = [prefix_scan(rwkv7_scan, rwkv_dir_inputs(fa_c, d), rwkv_dir_inputs(fa, d), s0a, d == 1)
             for d in range(2)]
    o_a = rwkv_readout(out_a[0][0] + out_a[1][0], fa, rw_r_k, rw_ln_w, rw_ln_b).astype(h.dtype)

    fb = gdn_features(cv, ab, gdn_conv_w, gdn_a_log, gdn_dt_bias)
    fb_c = gdn_features(cv_c, ab_c, gdn_conv_w, gdn_a_log, gdn_dt_bias)
    s0b = jnp.zeros((bsz, GDN_HEADS, GDN_HEAD_DIM, GDN_HEAD_DIM), jnp.float32)
    out_b = [prefix_scan(gdn_chunked, gdn_dir_inputs(fb_c, d), gdn_dir_inputs(fb, d), s0b, d == 1)
             for d in range(2)]
    o_b = gdn_readout(out_b[0][0] + out_b[1][0], z, gdn_norm_w).astype(h.dtype)

    y = gated_merge(o_a, o_b, gt, merge_p_a, merge_p_b, w_out)
    if not need_ctx:
        return y, None
    o_a_c = rwkv_readout(out_a[0][1] + out_a[1][1], fa_c, rw_r_k, rw_ln_w, rw_ln_b).astype(hc.dtype)
    o_b_c = gdn_readout(out_b[0][1] + out_b[1][1], z_c, gdn_norm_w).astype(hc.dtype)
    return y, gated_merge(o_a_c, o_b_c, gt_c, merge_p_a, merge_p_b, w_out)


def setup_inputs(seed: int = 0) -> dict:
    key = jax.random.key(seed)
    ks = jax.random.split(key, 32)
    f32 = jnp.float32
    L, D = DEPTH, D_MODEL

    def nrm(k, shape, scale):
        return scale * jax.random.normal(k, shape, f32)

    dt = jnp.exp(jax.random.uniform(ks[21], (L, 2, GDN_HEADS), f32, math.log(1e-3), math.log(1e-1)))
    return {
        'x': nrm(ks[0], (BATCH, SEQ, D), 1.0),
        'c': nrm(ks[1], (BATCH, D), 1.0),
        'ctx': nrm(ks[2], (BATCH, CTX_LEN, D), 1.0),
        'c_ctx': nrm(ks[3], (D,), 1.0),
        'w_ada': nrm(ks[4], (L, D, 6 * D), 0.5 * D ** -0.5),
        'b_ada': nrm(ks[5], (L, 6 * D), 0.02),
        'norm1_w': 1.0 + nrm(ks[6], (L, D), 0.1),
        'w_in': nrm(ks[7], (L, D, N_IN), D ** -0.5),
        'rw_mu': jax.random.uniform(ks[8], (L, RW_COLS), f32),
        'rw_w0': jnp.linspace(-6.0, 2.0, RW_WIDTH, dtype=f32) + nrm(ks[9], (L, 2, RW_WIDTH), 0.1),
        'rw_w2': nrm(ks[10], (L, 2, RW_DECAY_RANK, RW_WIDTH), 0.5 * RW_DECAY_RANK ** -0.5),
        'rw_a0': nrm(ks[11], (L, 2, RW_WIDTH), 0.5),
        'rw_a2': nrm(ks[12], (L, 2, RW_ICLR_RANK, RW_WIDTH), RW_ICLR_RANK ** -0.5),
        'rw_g2': nrm(ks[13], (L, RW_GATE_RANK, RW_WIDTH), RW_GATE_RANK ** -0.5),
        'rw_k_k': 0.85 + nrm(ks[14], (L, RW_WIDTH), 0.1),
        'rw_k_a': 1.0 + nrm(ks[15], (L, RW_WIDTH), 0.1),
        'rw_r_k': nrm(ks[16], (L, RW_HEADS, RW_HEAD_DIM), 0.1),
        'rw_ln_w': 1.0 + nrm(ks[17], (L, RW_WIDTH), 0.1),
        'rw_ln_b': nrm(ks[18], (L, RW_WIDTH), 0.01),
        'gdn_conv_w': nrm(ks[19], (L, GDN_CONV, GDN_CONV_COLS), GDN_CONV ** -0.5),
        'gdn_a_log': jnp.log(jax.random.uniform(ks[20], (L, 2, GDN_HEADS), f32, 1.0, 16.0)),
        'gdn_dt_bias': dt + jnp.log(-jnp.expm1(-dt)),
        'gdn_norm_w': 1.0 + nrm(ks[22], (L, GDN_HEAD_DIM), 0.1),
        'merge_p_a': nrm(ks[23], (L, RW_WIDTH, D), RW_WIDTH ** -0.5),
        'merge_p_b': nrm(ks[24], (L, GDN_WIDTH, D), GDN_WIDTH ** -0.5),
        'w_out': nrm(ks[25], (L, D, D), D ** -0.5),
        'norm2_w': 1.0 + nrm(ks[26], (L, D), 0.1),
        'ffn_w_gate_up': nrm(ks[27], (L, D, 2 * FFN_HIDDEN), D ** -0.5),
        'ffn_w_down': nrm(ks[28], (L, FFN_HIDDEN, D), FFN_HIDDEN ** -0.5),
        'final_norm_w': 1.0 + nrm(ks[29], (D,), 0.1),
    }


def reference(x, c, ctx, c_ctx, w_ada, b_ada, norm1_w, w_in, rw_mu, rw_w0, rw_w2, rw_a0, rw_a2,
              rw_g2, rw_k_k, rw_k_a, rw_r_k, rw_ln_w, rw_ln_b, gdn_conv_w, gdn_a_log,
              gdn_dt_bias, gdn_norm_w, merge_p_a, merge_p_b, w_out, norm2_w, ffn_w_gate_up,
              ffn_w_down, final_norm_w):
    silu_c = jax.nn.silu(c)[:, None, :]
    silu_cc = jax.nn.silu(c_ctx)
    for layer in range(DEPTH):
        need_ctx = layer < DEPTH - 1
        mod = silu_c @ w_ada[layer] + b_ada[layer]
        mod_ctx = silu_cc @ w_ada[layer] + b_ada[layer]
        sh1, sc1, gt1, sh2, sc2, gt2 = jnp.split(mod, 6, axis=-1)
        csh1, csc1, cgt1, csh2, csc2, cgt2 = jnp.split(mod_ctx, 6, axis=-1)
        h = modulate(rmsnorm(x, norm1_w[layer]), sh1, sc1)
        hc = modulate(rmsnorm(ctx, norm1_w[layer]), csh1, csc1)
        y, yc = mix_tokens(h, hc, w_in[layer], rw_mu[layer], rw_w0[layer], rw_w2[layer],
                           rw_a0[layer], rw_a2[layer], rw_g2[layer], rw_k_k[layer], rw_k_a[layer],
                           rw_r_k[layer], rw_ln_w[layer], rw_ln_b[layer], gdn_conv_w[layer],
                           gdn_a_log[layer], gdn_dt_bias[layer], gdn_norm_w[layer],
                           merge_p_a[layer], merge_p_b[layer], w_out[layer], need_ctx)
        x = x + gt1 * y
        x = x + gt2 * swiglu(modulate(rmsnorm(x, norm2_w[layer]), sh2, sc2),
                             ffn_w_gate_up[layer], ffn_w_down[layer])
        if need_ctx:
            ctx = ctx + cgt1 * yc
            ctx = ctx + cgt2 * swiglu(modulate(rmsnorm(ctx, norm2_w[layer]), csh2, csc2),
                                      ffn_w_gate_up[layer], ffn_w_down[layer])
    return rmsnorm(x, final_norm_w)
```

```python
import numpy as np
from contextlib import ExitStack
import concourse.bass as bass
import concourse.mybir as mybir
from concourse.bass_utils import run_bass_kernel_spmd

F32 = mybir.dt.float32
BF16 = mybir.dt.bfloat16
AF = mybir.ActivationFunctionType
ALU = mybir.AluOpType
AX = mybir.AxisListType

COMPUTE = ("pe", "act", "dve", "pool")
NDSEM = 24

D = 2048
TC = 256
TL = 2048
TT = TC + TL
NCH = TT // 128
NIN = 94
FFN = 5632
EPS = 1e-6
DEC = 0.6065306597126334


class Buf:
    __slots__ = ("name", "lw", "rd")

    def __init__(self, name=""):
        self.name = name
        self.lw = None
        self.rd = {}


class Sched:
    def __init__(self):
        self.ops = []
        self.last = {}
        self.dmas = []
        self.bar = set()
        self.bar_seen = set()

    def barrier(self):
        self.bar = set(self.last.values()) | set(self.dmas)
        self.dmas = []
        self.bar_seen = set()

    def add(self, eng, fn, reads=(), writes=(), dma=False):
        i = len(self.ops)
        deps = set()
        if eng not in self.bar_seen:
            deps |= self.bar
            self.bar_seen.add(eng)
        self.last[eng] = i
        if dma:
            self.dmas.append(i)
        for b in reads:
            if b.lw is not None:
                deps.add(b.lw)
        for b in writes:
            if b.lw is not None:
                deps.add(b.lw)
            deps.update(b.rd.values())
        key = ("d", i) if dma else eng
        for b in reads:
            b.rd[key] = i
        for b in writes:
            b.lw = i
            b.rd = {}
        self.ops.append((eng, fn, deps, dma))
        return i

    def emit(self, nc, stack):
        ops = self.ops
        engs = {"pe": nc.tensor, "act": nc.scalar, "dve": nc.vector, "pool": nc.gpsimd, "sp": nc.sync}
        names = list(engs)
        csem = {e: stack.enter_context(nc.semaphore("c_" + e)) for e in COMPUTE}
        dsem = {e: [stack.enter_context(nc.semaphore("d_%s%d" % (e, k))) for k in range(NDSEM)]
                for e in ("sp", "act", "pool")}
        comp = [None] * len(ops)
        cnt = {e: 0 for e in COMPUTE}
        dcnt = {e: 0 for e in dsem}
        prevslot = [None] * len(ops)
        for i, (eng, fn, deps, dma) in enumerate(ops):
            if dma:
                j = dcnt[eng]
                dcnt[eng] += 1
                comp[i] = (dsem[eng][j % NDSEM], 16 * (j // NDSEM + 1))
                if j >= NDSEM:
                    prevslot[i] = (dsem[eng][j % NDSEM], 16 * (j // NDSEM))
            else:
                cnt[eng] += 1
                comp[i] = (csem[eng], cnt[eng])
        per = {e: [] for e in names}
        for i, op in enumerate(ops):
            per[op[0]].append(i)
        block = stack.enter_context(nc.Block())

        def run(ename):
            def body(e):
                known = {}
                for i in per[ename]:
                    eng, fn, deps, dma = ops[i]
                    need = {}
                    cands = [comp[d] for d in deps]
                    if prevslot[i] is not None:
                        cands.append(prevslot[i])
                    for sm, v in cands:
                        k = id(sm)
                        if known.get(k, 0) >= v:
                            continue
                        if k not in need or need[k][1] < v:
                            need[k] = (sm, v)
                    for k, (sm, v) in need.items():
                        e.wait_ge(sm, v)
                        known[k] = v
                    ins = fn(e)
                    sm, v = comp[i]
                    ins.then_inc(sm, 16 if dma else 1)
                if ename in dsem:
                    last = {}
                    for i in per[ename]:
                        if ops[i][3]:
                            sm, v = comp[i]
                            last[id(sm)] = (sm, v)
                    for sm, v in last.values():
                        e.wait_ge(sm, v)
            return body

        block.tensor(run("pe"))
        block.scalar(run("act"))
        block.vector(run("dve"))
        block.gpsimd(run("pool"))
        block.sync(run("sp"))


class T:
    def __init__(self, h, name):
        self.h = h
        self.b = Buf(name)

    def __getitem__(self, k):
        return self.h[k]


def _bufs(xs):
    return [x if isinstance(x, Buf) else x.b for x in xs]


class KB:
    def __init__(self, nc):
        self.nc = nc
        self.S = Sched()
        self.n = 0

    def sb(self, st, shape, dt, name=None):
        self.n += 1
        name = name or "t%d" % self.n
        if not hasattr(self, "used"):
            self.used = set()
        while name in self.used:
            name = name + "_"
        self.used.add(name)
        return T(st.enter_context(self.nc.sbuf_tensor(name, list(shape), dt)), name)

    def ps(self, st, shape, dt, name=None):
        self.n += 1
        name = name or "p%d" % self.n
        return T(st.enter_context(self.nc.psum_tensor(name, list(shape), dt)), name)

    def dram(self, name, shape, dt, kind="Internal"):
        h = self.nc.dram_tensor(name, list(shape), dt, kind=kind)
        t = T(h.ap(), name)
        return t

    def act(self, out, in_, func, r, w, scale=1.0, bias=0.0, accum=None):
        kw = {}
        if accum is not None:
            kw["accum_out"] = accum
        self.S.add("act", lambda e: e.activation(out=out, in_=in_, func=func, scale=scale, bias=bias, **kw),
                   _bufs(r), _bufs(w))

    def tt(self, out, in0, in1, op, r, w, eng="dve"):
        self.S.add(eng, lambda e: e.tensor_tensor(out=out, in0=in0, in1=in1, op=op), _bufs(r), _bufs(w))

    def ts(self, out, in0, s1, s2, op0, op1, r, w, eng="dve", accum=None):
        kw = {}
        if accum is not None:
            kw["accum_out"] = accum
        if op1 is None:
            self.S.add(eng, lambda e: e.tensor_scalar(out=out, in0=in0, scalar1=s1, scalar2=None, op0=op0, **kw),
                       _bufs(r), _bufs(w))
        else:
            self.S.add(eng, lambda e: e.tensor_scalar(out=out, in0=in0, scalar1=s1, scalar2=s2, op0=op0, op1=op1, **kw),
                       _bufs(r), _bufs(w))

    def stt(self, out, in0, scalar, in1, op0, op1, r, w):
        self.S.add("dve", lambda e: e.scalar_tensor_tensor(out=out, in0=in0, scalar=scalar, in1=in1, op0=op0, op1=op1),
                   _bufs(r), _bufs(w))

    def copy(self, out, in_, r, w, eng="dve"):
        self.S.add(eng, lambda e: e.tensor_copy(out=out, in_=in_), _bufs(r), _bufs(w))

    def memset(self, out, val, w, eng="pool"):
        self.S.add(eng, lambda e: e.memset(out, val), [], _bufs(w))

    def recip(self, out, in_, r, w):
        self.S.add("dve", lambda e: e.reciprocal(out=out, in_=in_), _bufs(r), _bufs(w))

    def scan(self, out, d0, d1, init, op0, op1, r, w):
        self.S.add("dve", lambda e: e.tensor_tensor_scan(out=out, data0=d0, data1=d1, initial=init, op0=op0, op1=op1),
                   _bufs(r), _bufs(w))

    def mm(self, out, lhsT, rhs, start, stop, r, w):
        self.S.add("pe", lambda e: e.matmul(out, lhsT=lhsT, rhs=rhs, start=start, stop=stop), _bufs(r), _bufs(w))

    def tr(self, out, in_, ident, r, w):
        self.S.add("pe", lambda e: e.transpose(out=out, in_=in_, identity=ident), _bufs(r), _bufs(w))

    def dma(self, q, out, in_, r, w, **kw):
        self.S.add(q, lambda e: e.dma_start(out=out, in_=in_, **kw), _bufs(r), _bufs(w), dma=True)


def inverse_workspace(K, st, C):
    W = {}
    W["nmask"] = K.sb(st, [128, 2, 7, 128], BF16, "nmask_sb")
    K.dma("pool", W["nmask"][:], C["nmask"][:, :, :, :], [], [W["nmask"]])
    W["ident_b"] = C["ident_b"]
    for nm in ("Xa", "Xb", "Ya", "Yb", "LsX", "LsY", "M1", "M2", "Mt", "R"):
        W[nm] = K.sb(st, [128, 4, 128], BF16, "iw_" + nm)
    W["PI"] = [K.ps(st, [128, 4, 128], F32) for _ in range(2)]
    W["cnt"] = 0
    return W


def inverse_units(K, C, LL, n, d, XTb, W, nunits=None):
    LLf = LL[:].rearrange("p j f t -> p (j f) t")
    nm = W["nmask"]
    nunits = 2 * n if nunits is None else nunits
    gs = min(4, nunits)
    idb = W["ident_b"][:].unsqueeze(1).to_broadcast([128, gs, 128])

    def pi():
        W["cnt"] += 1
        return W["PI"][W["cnt"] % 2]
    mx, my = (0, 1) if d == 0 else (1, 0)
    for g0 in range(0, nunits, gs):
        Lv = LLf[:, 2 * g0:2 * g0 + 2 * gs:2, :]
        LTv = LLf[:, 2 * g0 + 1:2 * g0 + 2 * gs:2, :]

        class V_:
            def __init__(s_, t): s_.t = t; s_.b = t.b
            def __getitem__(s_, k):
                if k == slice(None): return s_.t[:, 0:gs, :]
                return s_.t[k]
        X, Xn, Y, Yn = (V_(W[k_]) for k_ in ("Xa", "Xb", "Ya", "Yb"))
        LsX, LsY, M1, M2, Mt, R = (V_(W[k_]) for k_ in ("LsX", "LsY", "M1", "M2", "Mt", "R"))
        bc = lambda m, lv: nm[:, m, lv, :].unsqueeze(1).to_broadcast([128, gs, 128])
        K.tt(LsX[:], Lv, bc(mx, 0), ALU.mult, [LL, nm], [LsX], eng="pool")
        K.tt(X[:], LsX[:], idb, ALU.add, [LsX, W["ident_b"]], [X], eng="pool")
        K.tt(LsY[:], LTv, bc(my, 0), ALU.mult, [LL, nm], [LsY], eng="pool")
        K.tt(Y[:], LsY[:], idb, ALU.add, [LsY, W["ident_b"]], [Y], eng="pool")
        K.tt(Mt[:], Lv, idb, ALU.add, [LL, W["ident_b"]], [Mt], eng="pool")
        for lv in range(1, 7):
            K.tt(LsX[:], Lv, bc(mx, lv), ALU.mult, [LL, nm], [LsX], eng="pool")
            K.tt(LsY[:], LTv, bc(my, lv), ALU.mult, [LL, nm], [LsY], eng="pool")
            Q = pi()
            for u in range(gs):
                K.mm(Q[:, u, :], LsY[:, u, :], X[:, u, :], True, True, [LsY, X], [Q])
            K.act(M1[:], Q[:, 0:gs, :], AF.Copy, [Q], [M1])
            Q = pi()
            for u in range(gs):
                K.mm(Q[:, u, :], LsX[:, u, :], Y[:, u, :], True, True, [LsX, Y], [Q])
            K.act(M2[:], Q[:, 0:gs, :], AF.Copy, [Q], [M2])
            Q = pi()
            for u in range(gs):
                K.mm(Q[:, u, :], Y[:, u, :], M1[:, u, :], True, True, [Y, M1], [Q])
            K.tt(Xn[:], X[:], Q[:, 0:gs, :], ALU.add, [X, Q], [Xn])
            Q = pi()
            for u in range(gs):
                K.mm(Q[:, u, :], X[:, u, :], M2[:, u, :], True, True, [X, M2], [Q])
            K.tt(Yn[:], Y[:], Q[:, 0:gs, :], ALU.add, [Y, Q], [Yn])
            X, Xn, Y, Yn = Xn, X, Yn, Y
        Q = pi()
        for u in range(gs):
            K.mm(Q[:, u, :], Mt[:, u, :], Y[:, u, :], True, True, [Mt, Y], [Q])
        K.stt(R[:], Q[:, 0:gs, :], -1.0, idb, ALU.mult, ALU.add, [Q, W["ident_b"]], [R])
        Q = pi()
        for u in range(gs):
            K.mm(Q[:, u, :], X[:, u, :], R[:, u, :], True, True, [X, R], [Q])
        K.tt(XTb[:, g0:g0 + gs, :], Y[:], Q[:, 0:gs, :], ALU.add, [Y, Q], [XTb])


SEGS = [(0, 2)] + [(2 + 4 * i, 4) for i in range(4)]


def rwkv_phase(K, st, C):
    XS, smallT, oaT, ident_b, ident_f = C["XS"], C["smallT"], C["oaT"], C["ident_b"], C["ident_f"]
    dbg = C["dbg"]
    sb = lambda shape, dt, name=None: K.sb(st, shape, dt, name)
    w0 = sb([128, 2, 8], F32); a0 = sb([128, 2, 8], F32)
    kkw = sb([128, 8], F32); ka = sb([128, 8], F32); omka = sb([128, 8], F32); rk = sb([128, 8], F32)
    for t_, d_ in ((w0, C["w0"]), (a0, C["a0"])):
        K.dma("sp", t_[:], d_[:, :, :], [], [t_])
    for t_, d_ in ((kkw, C["kkw"]), (ka, C["ka"]), (rk, C["rk"])):
        K.dma("sp", t_[:], d_[:, :], [], [t_])
    K.ts(omka[:], ka[:], -1.0, 1.0, ALU.mult, ALU.add, [ka], [omka])
    w2b = sb([128, 2, 1024], BF16); a2b = sb([128, 2, 1024], BF16); g2b = sb([64, 1024], BF16)
    K.memset(w2b[:], 0.0, [w2b])
    K.memset(a2b[:], 0.0, [a2b])
    K.dma("pool", w2b[0:96, :, :], C["w2"][:, :, :].rearrange("d r c -> r d c"), [], [w2b])
    K.dma("pool", a2b[0:96, :, :], C["a2"][:, :, :].rearrange("d r c -> r d c"), [], [a2b])
    K.dma("pool", g2b[:], C["g2"][:, :], [], [g2b])
    lnw = sb([128, 1024], F32); lnb = sb([128, 1024], F32)
    K.dma("sp", lnw[:], C["lnw"][0:1, :].to_broadcast([128, 1024]), [], [lnw])
    K.dma("sp", lnb[:], C["lnb"][0:1, :].to_broadcast([128, 1024]), [], [lnb])
    m1f = sb([128, 2, 4, 128], F32); m2f = sb([128, 2, 3, 128], F32)
    K.dma("sp", m1f[:], C["m1"][:, :, :, :], [], [m1f])
    K.dma("sp", m2f[:], C["m2"][:, :, :, :], [], [m2f])
    rmask = sb([128, 512], F32); bones = sb([128, 128], F32); hsel = sb([128, 2], F32)
    K.dma("sp", rmask[:], C["rmask"][:, :], [], [rmask])
    K.dma("sp", bones[:], C["bones"][:, :], [], [bones])
    K.dma("sp", hsel[:], C["hsel"][:, :], [], [hsel])
    f32t = lambda nm=None: sb([128, 512], F32, nm)
    bft = lambda nm=None: sb([128, 512], BF16, nm)
    Xr, Xk, Xv = f32t("Xr"), f32t("Xk"), f32t("Xv")
    sig, A, B, Cc, Dd = f32t("sig"), f32t("A"), f32t("B"), f32t("Cc"), f32t("Dd")
    e1, e2, e3, e4 = f32t("e1"), f32t("e2"), f32t("e3"), f32t("e4")
    icl, icl0, kq, sq, rn, kkt, kd, bd, tmp = (f32t(nm) for nm in ("icl", "icl0", "kq", "sq", "rn", "kkt", "kd", "bd", "tmp"))
    gam = sb([128, 4], F32, "gam")
    rt, at, kt, bt, KH, BH, vb, rkr = (bft(nm) for nm in ("rt", "at", "kt", "bt", "KH", "BH", "vb", "rkr"))
    KHt = sb([128, 4, 128], BF16, "KHt"); BHnt = sb([128, 4, 128], BF16, "BHnt"); Vt = sb([128, 4, 128], BF16, "Vt")
    LL = sb([128, 4, 4, 128], BF16, "LL")
    AA = sb([128, 4, 2, 3, 128], BF16, "AA")
    XTb = sb([128, 8, 128], BF16, "XTb")
    IW = inverse_workspace(K, st, C)
    Hf = sb([128, 64], F32, "Hf"); Hb = sb([128, 64], BF16, "Hb")
    P1s = sb([128, 128], BF16, "P1s"); Us = sb([128, 128], BF16, "Us")
    ybuf = sb([128, 16, 128], F32, "ybuf")
    ytot = sb([128, 4, 128], F32, "ytot"); yc = sb([128, 4, 128], F32, "yc"); ysq = sb([128, 4, 128], F32)
    mean = sb([128, 8], F32); var = sb([128, 8], F32)
    bsum = sb([128, 4, 2], F32)
    gate = sb([128, 4, 128], F32)
    oat = sb([128, 4, 128], BF16)
    PF = [K.ps(st, [128, 512], F32) for _ in range(1)]
    PTr = K.ps(st, [128, 8, 128], BF16)
    PG = [K.ps(st, [128, 4, 128], F32) for _ in range(2)]
    PSq = K.ps(st, [128, 512], F32)
    PSh = K.ps(st, [128, 512], F32)
    PS_P1, PS_U, PS_Y, PS_H = PSq, PSq, PSq, PSh
    cnt = {"pf": 0, "pg": 0, "pi": 0, "tr": 0}

    def nxt(lst, key):
        cnt[key] += 1
        return lst[cnt[key] % len(lst)]

    def transp(src, dst, n, scale=None):
        half = cnt["tr"] % 2
        cnt["tr"] += 1
        for j in range(n):
            K.tr(PTr[:, half * 4 + j, :], src[:, j * 128:(j + 1) * 128], ident_b[:], [src, ident_b], [PTr])
        if scale is None:
            K.copy(dst[:, :n, :], PTr[:, half * 4:half * 4 + n, :], [PTr], [dst])
        else:
            K.act(dst[:, :n, :], PTr[:, half * 4:half * 4 + n, :], AF.Copy, [PTr], [dst], scale=scale)

    for hp in range(C["nhp"]):
        hc = slice(hp * 128, (hp + 1) * 128)
        for d in range(2):
            K.memset(Hf[:], 0.0, [Hf])
            K.memset(Hb[:], 0.0, [Hb])
            order = SEGS if d == 0 else [SEGS[0], SEGS[4], SEGS[3], SEGS[2], SEGS[1]]
            for (c0, n) in order:
                t0, N = c0 * 128, n * 128
                latent = c0 >= 2
                tk = slice(t0, t0 + N)
                for X_, row in ((Xr, 0), (Xk, 1024), (Xv, 2048)):
                    K.dma("sp", X_[:, :N], XS[row + hp * 128:row + hp * 128 + 128, tk], [XS], [X_])
                P = nxt(PF, "pf")
                K.mm(P[:, :N], w2b[:, d, hc], smallT[:, d, tk], True, True, [w2b, smallT], [P])
                K.act(sig[:, :N], P[:, :N], AF.Sigmoid, [P, w0], [sig], bias=w0[:, d, hp:hp + 1])
                K.scan(A[:, :N], rmask[:, :N], sig[:, :N], 0.0, ALU.mult, ALU.add, [rmask, sig], [A])
                K.tt(B[:, :N], A[:, :N], sig[:, :N], ALU.subtract, [A, sig], [B], eng="pool")
                v3 = lambda t_: t_[:, :N].rearrange("p (c t) -> p c t", t=128)
                tot = v3(A)[:, :, 127:128]
                K.tt(v3(Cc), tot.to_broadcast([128, n, 128]), v3(A), ALU.subtract, [A], [Cc])
                K.tt(Dd[:, :N], Cc[:, :N], sig[:, :N], ALU.add, [Cc, sig], [Dd], eng="pool")
                Gi, Gx, Gt = (A, B, Cc) if d == 0 else (Dd, Cc, B)
                K.act(e1[:, :N], Gi[:, :N], AF.Exp, [Gi], [e1], scale=-DEC)
                K.act(e2[:, :N], Gx[:, :N], AF.Exp, [Gx], [e2], scale=-DEC)
                K.act(e3[:, :N], Gi[:, :N], AF.Exp, [Gi], [e3], scale=DEC)
                K.act(e4[:, :N], Gt[:, :N], AF.Exp, [Gt], [e4], scale=-DEC)
                K.act(gam[:, :n], v3(A)[:, :, 127], AF.Exp, [A], [gam], scale=-DEC)
                P = nxt(PF, "pf")
                K.mm(P[:, :N], a2b[:, d, hc], smallT[:, 2 + d, tk], True, True, [a2b, smallT], [P])
                K.act(icl[:, :N], P[:, :N], AF.Sigmoid, [P, a0], [icl], bias=a0[:, d, hp:hp + 1])
                if d == 1 and latent:
                    P = nxt(PF, "pf")
                    K.mm(P[:, :N], a2b[:, 0, hc], smallT[:, 2, tk], True, True, [a2b, smallT], [P])
                    K.act(icl0[:, :N], P[:, :N], AF.Sigmoid, [P, a0], [icl0], bias=a0[:, 0, hp:hp + 1])
                K.ts(kq[:, :N], Xk[:, :N], kkw[:, hp:hp + 1], None, ALU.mult, None, [Xk, kkw], [kq], eng="pool")
                K.act(sq[:, :N], kq[:, :N], AF.Square, [kq], [sq])
                P = nxt(PF, "pf")
                K.mm(P[:, :N], bones[:], sq[:, :N], True, True, [bones, sq], [P])
                K.act(rn[:, :N], P[:, :N], AF.Sqrt, [P], [rn], bias=EPS)
                K.recip(rn[:, :N], rn[:, :N], [rn], [rn])
                K.tt(kkt[:, :N], kq[:, :N], rn[:, :N], ALU.mult, [kq, rn], [kkt])
                K.ts(tmp[:, :N], icl[:, :N], ka[:, hp:hp + 1], omka[:, hp:hp + 1], ALU.mult, ALU.add, [icl, ka, omka], [tmp])
                K.tt(kd[:, :N], tmp[:, :N], Xk[:, :N], ALU.mult, [tmp, Xk], [kd])
                K.tt(bd[:, :N], kkt[:, :N], icl[:, :N], ALU.mult, [kkt, icl], [bd], eng="pool")
                K.tt(rt[:, :N], Xr[:, :N], e1[:, :N], ALU.mult, [Xr, e1], [rt])
                K.tt(at[:, :N], kkt[:, :N], e2[:, :N], ALU.mult, [kkt, e2], [at], eng="pool")
                K.tt(kt[:, :N], kd[:, :N], e3[:, :N], ALU.mult, [kd, e3], [kt])
                K.tt(bt[:, :N], bd[:, :N], e3[:, :N], ALU.mult, [bd, e3], [bt], eng="pool")
                K.tt(KH[:, :N], kd[:, :N], e4[:, :N], ALU.mult, [kd, e4], [KH])
                K.tt(BH[:, :N], bd[:, :N], e4[:, :N], ALU.mult, [bd, e4], [BH], eng="pool")
                K.act(vb[:, :N], Xv[:, :N], AF.Copy, [Xv], [vb])
                transp(KH, KHt, n)
                transp(BH, BHnt, n, scale=-1.0)
                transp(vb, Vt, n)
                for j in range(n):
                    cs = slice(j * 128, (j + 1) * 128)
                    G = nxt(PG, "pg")
                    for e in range(2):
                        ps_ = slice(64 * e, 64 * e + 64)
                        K.mm(G[:, 2 * e, :], at[ps_, cs], bt[ps_, cs], True, True, [at, bt], [G])
                        K.mm(G[:, 2 * e + 1, :], bt[ps_, cs], at[ps_, cs], True, True, [at, bt], [G])
                    K.tt(LL[:, j, :, :], G[:], m1f[:, d, :, :], ALU.mult, [G, m1f], [LL])
                    for e in range(2):
                        ps_ = slice(64 * e, 64 * e + 64)
                        G = nxt(PG, "pg")
                        K.mm(G[:, 0, :], kt[ps_, cs], at[ps_, cs], True, True, [kt, at], [G])
                        K.mm(G[:, 1, :], kt[ps_, cs], rt[ps_, cs], True, True, [kt, rt], [G])
                        K.mm(G[:, 2, :], bt[ps_, cs], rt[ps_, cs], True, True, [bt, rt], [G])
                        K.tt(AA[:, j, e, :, :], G[:, 0:3, :], m2f[:, d, :, :], ALU.mult, [G, m2f], [AA])
                inverse_units(K, C, LL, n, d, XTb, IW)
                jl = list(range(n)) if d == 0 else list(range(n - 1, -1, -1))
                for j in jl:
                    cs = slice(j * 128, (j + 1) * 128)
                    for e in range(2):
                        ps_ = slice(64 * e, 64 * e + 64)
                        vs = slice(64 * e, 64 * e + 64)
                        K.mm(PS_P1[:, 0 + 64 * e:64 + 64 * e], at[ps_, cs], Hb[ps_, :], True, False, [at, Hb], [PS_P1])
                        K.mm(PS_P1[:, 0 + 64 * e:64 + 64 * e], AA[:, j, e, 0, :], Vt[:, j, vs], False, True, [AA, Vt], [PS_P1])
                    K.act(P1s[:], PS_P1[:, 0:128], AF.Copy, [PS_P1], [P1s])
                    for e in range(2):
                        vs = slice(64 * e, 64 * e + 64)
                        K.mm(PS_U[:, 128 + 64 * e:192 + 64 * e], XTb[:, 2 * j + e, :], P1s[:, vs], True, True, [XTb, P1s], [PS_U])
                    K.copy(Us[:], PS_U[:, 128:256], [PS_U], [Us])
                    if latent:
                        for e in range(2):
                            ps_ = slice(64 * e, 64 * e + 64)
                            vs = slice(64 * e, 64 * e + 64)
                            yo = PS_Y[:, 256 + 64 * e:320 + 64 * e]
                            K.mm(yo, rt[ps_, cs], Hb[ps_, :], True, False, [rt, Hb], [PS_Y])
                            K.mm(yo, AA[:, j, e, 1, :], Vt[:, j, vs], False, False, [AA, Vt], [PS_Y])
                            K.mm(yo, AA[:, j, e, 2, :], Us[:, vs], False, True, [AA, Us], [PS_Y])
                    K.mm(PS_H[:, 384:512], KHt[:, j, :], Vt[:, j, :], True, False, [KHt, Vt], [PS_H])
                    K.mm(PS_H[:, 384:512], BHnt[:, j, :], Us[:], False, True, [BHnt, Us], [PS_H])
                    gcol = gam[:, j:j + 1]
                    for e in range(2):
                        ps_ = slice(64 * e, 64 * e + 64)
                        K.stt(Hf[ps_, :], Hf[ps_, :], gam[ps_, j:j + 1], PS_H[ps_, 384 + 64 * e:448 + 64 * e], ALU.mult, ALU.add,
                              [Hf, gam, PS_H], [Hf])
                    K.act(Hb[:], Hf[:], AF.Copy, [Hf], [Hb])
                    if latent:
                        cg = c0 - 2 + j
                        if d == 0:
                            K.act(ybuf[:, cg, :], PS_Y[:, 256:384], AF.Copy, [PS_Y], [ybuf])
                        else:
                            K.tt(ytot[:, j, :], ybuf[:, cg, :], PS_Y[:, 256:384], ALU.add, [ybuf, PS_Y], [ytot])
                if d == 1 and latent:
                    if dbg and hp < 8:
                        K.dma("sp", dbg["yf"][hp, :, c0 - 2:c0 - 2 + n, :], ytot[:], [ytot], [])
                    yv = ytot[:].rearrange("p j (e c) -> p (j e) c", c=64)
                    ycv = yc[:].rearrange("p j (e c) -> p (j e) c", c=64)
                    sqv = ysq[:].rearrange("p j (e c) -> p (j e) c", c=64)
                    K.S.add("dve", lambda e_: e_.tensor_reduce(out=mean[:], in_=yv, axis=AX.X, op=ALU.add), _bufs([ytot]), _bufs([mean]))
                    K.ts(mean[:], mean[:], 1.0 / 64, None, ALU.mult, None, [mean], [mean])
                    K.tt(ycv, yv, mean[:].unsqueeze(2).to_broadcast([128, 8, 64]), ALU.subtract, [ytot, mean], [yc])
                    K.tt(sqv, ycv, ycv, ALU.mult, [yc], [ysq], eng="pool")
                    K.S.add("dve", lambda e_: e_.tensor_reduce(out=var[:], in_=sqv, axis=AX.X, op=ALU.add), _bufs([ysq]), _bufs([var]))
                    K.act(var[:], var[:], AF.Sqrt, [var], [var], scale=1.0 / 64, bias=64e-5)
                    K.recip(var[:], var[:], [var], [var])
                    K.tt(ycv, ycv, var[:].unsqueeze(2).to_broadcast([128, 8, 64]), ALU.mult, [yc, var], [yc])
                    K.tt(yc[:], yc[:], lnw[:, hc].unsqueeze(1).to_broadcast([128, 4, 128]), ALU.mult, [yc, lnw], [yc])
                    K.tt(yc[:], yc[:], lnb[:, hc].unsqueeze(1).to_broadcast([128, 4, 128]), ALU.add, [yc, lnb], [yc])
                    K.tt(tmp[:, :N], icl[:, :N], icl0[:, :N], ALU.add, [icl, icl0], [tmp])
                    K.ts(tmp[:, :N], tmp[:, :N], 0.5, None, ALU.mult, None, [tmp], [tmp])
                    K.ts(tmp[:, :N], tmp[:, :N], ka[:, hp:hp + 1], omka[:, hp:hp + 1], ALU.mult, ALU.add, [tmp, ka, omka], [tmp])
                    K.tt(tmp[:, :N], tmp[:, :N], Xk[:, :N], ALU.mult, [tmp, Xk], [tmp])
                    K.stt(sq[:, :N], tmp[:, :N], rk[:, hp:hp + 1], Xr[:, :N], ALU.mult, ALU.mult, [tmp, rk, Xr], [sq])
                    P = nxt(PF, "pf")
                    for j in range(n):
                        K.mm(P[:, 2 * j:2 * j + 2], sq[:, j * 128:(j + 1) * 128], hsel[:], True, True, [sq, hsel], [P])
                    K.copy(bsum[:].rearrange("p j e -> p (j e)"), P[:, 0:2 * n], [P], [bsum])
                    K.copy(ysq[:], Vt[:], [Vt], [ysq], eng="pool")
                    K.tt(sqv, sqv, bsum[:].rearrange("p j e -> p (j e)").unsqueeze(2).to_broadcast([128, 8, 64]), ALU.mult,
                         [ysq, bsum], [ysq])
                    K.tt(yc[:], yc[:], ysq[:], ALU.add, [yc, ysq], [yc])
                    P = nxt(PF, "pf")
                    for j in range(n):
                        K.mm(P[:, j * 128:(j + 1) * 128], smallT[0:64, 4, t0 + j * 128:t0 + (j + 1) * 128], g2b[0:64, hc], True, True,
                             [smallT, g2b], [P])
                    K.tt(oat[:], yc[:], P[:].rearrange("p (j c) -> p j c", c=128), ALU.mult, [yc, P], [oat])
                    half = cnt["tr"] % 2
                    cnt["tr"] += 1
                    for j in range(n):
                        K.tr(PTr[:, half * 4 + j, :], oat[:, j, :], ident_b[:], [oat, ident_b], [PTr])
                    K.copy(oaT[:, hp, t0 - 256:t0 - 256 + N].rearrange("p (j t) -> p j t", t=128), PTr[:, half * 4:half * 4 + n, :],
                           [PTr], [oaT])


def gdn_phase(K, st, C):
    US, SZ, abT, obT, ident_b, ident_f = C["US"], C["SZ"], C["abT"], C["obT"], C["ident_b"], C["ident_f"]
    sb = lambda shape, dt, name=None: K.sb(st, shape, dt, name)
    selg = sb([64, 16, 128], F32); selb = sb([64, 16, 128], F32)
    K.dma("sp", selg[:], C["selg"][:, :, :], [], [selg])
    K.dma("sp", selb[:], C["selb"][:, :, :], [], [selb])
    bigm = sb([128, 2, 2, 128], F32); offd = sb([128, 128], F32); ones = sb([128, 128], F32)
    K.dma("sp", bigm[:], C["bigm"][:, :, :, :], [], [bigm])
    K.dma("sp", offd[:], C["offd"][:, :], [], [offd])
    K.dma("sp", ones[:], C["ones"][:, :], [], [ones])
    rmask = sb([128, 512], F32)
    K.dma("sp", rmask[:], C["rmask"][:, :], [], [rmask])
    alog = sb([64, 1], F32); dtb = sb([64, 1], F32); nea = sb([64, 1], F32)
    K.dma("sp", alog[:], C["alog"][:, :], [], [alog])
    K.dma("sp", dtb[:], C["dtb"][:, :], [], [dtb])
    gnw = sb([128, 128], F32)
    K.dma("sp", gnw[:], C["gnw"][0:1, :].to_broadcast([128, 128]), [], [gnw])
    K.act(nea[:], alog[:], AF.Exp, [alog], [nea])
    K.ts(nea[:], nea[:], -1.0, None, ALU.mult, None, [nea], [nea])
    GB = [sb([64, TT], F32, "GB%d" % d) for d in range(2)]
    tokT = [sb([128, NCH, 64], F32, "tokT%d" % d) for d in range(2)]
    with ExitStack() as s0:
        gt = K.sb(s0, [16, TT], F32); A = K.sb(s0, [16, TT], F32); Bx = K.sb(s0, [16, TT], F32)
        K.act(gt[:], abT[0:16, :], AF.Exp, [abT, dtb], [gt], bias=dtb[0:16, :])
        K.act(gt[:], gt[:], AF.Ln, [gt], [gt], bias=1.0)
        K.ts(gt[:], gt[:], nea[0:16, :], None, ALU.mult, None, [gt, nea], [gt])
        for d in range(2):
            K.memset(GB[d][:], 0.0, [GB[d]])
            K.act(GB[d][32:48, :], abT[32:48, :], AF.Sigmoid, [abT], [GB[d]])
        for t0 in range(0, TT, 512):
            N = min(512, TT - t0)
            K.scan(A[:, t0:t0 + N], rmask[0:16, :N], gt[:, t0:t0 + N], 0.0, ALU.mult, ALU.add, [rmask, gt], [A])
        K.copy(GB[0][0:16, :], A[:], [A], [GB[0]], eng="pool")
        K.tt(Bx[:], A[:], gt[:], ALU.subtract, [A, gt], [Bx])
        v3 = lambda t_: t_[:].rearrange("p (c t) -> p c t", t=128)
        tot = v3(A)[:, :, 127:128]
        K.tt(v3(GB[1])[0:16], tot.to_broadcast([16, NCH, 128]), v3(Bx), ALU.subtract, [A, Bx], [GB[1]])
        ptk = K.ps(s0, [128, 8, 64], F32)
        for d in range(2):
            for c8 in range(0, NCH, 8):
                nn = min(8, NCH - c8)
                for j in range(nn):
                    c = c8 + j
                    K.tr(ptk[:, j, :], GB[d][:, c * 128:(c + 1) * 128], ident_f[0:64, 0:64], [GB[d], ident_f], [ptk])
                K.copy(tokT[d][:, c8:c8 + nn, :], ptk[:, 0:nn, :], [ptk], [tokT[d]])
    K.S.barrier()
    f32t = lambda nm=None: sb([128, 512], F32, nm)
    bft = lambda nm=None: sb([128, 512], BF16, nm)
    Xq, Xk, Xv = f32t("gXq"), f32t("gXk"), f32t("gXv")
    sq, rn, qn, kn, bcG, eG, bcB, KB, tmp, tmp2, sz = (f32t("g_" + nm) for nm in
                                                       ("sq", "rn", "qn", "kn", "bcG", "eG", "bcB", "KB", "tmp", "tmp2", "sz"))
    knb, qnb, kbT, nKBG, Qd, Ktl, vb = (bft("g_" + nm) for nm in ("knb", "qnb", "kbT", "nKBG", "Qd", "Ktl", "vb"))
    Ktt = sb([128, 4, 128], BF16, "g_Ktt"); Vt = sb([128, 4, 128], BF16, "g_Vt")
    Dc = sb([128, 4, 2, 128], F32, "g_Dc"); DiT = sb([128, 4, 128], F32, "g_DiT"); Dtmp = sb([128, 4, 128], F32, "g_Dtmp")
    LLg = sb([128, 2, 4, 128], BF16, "g_LL")
    QKt = sb([128, 4, 128], BF16, "g_QKt")
    XTb = sb([128, 4, 128], BF16, "g_XTb")
    IW = inverse_workspace(K, st, C)
    Sf = sb([128, 128], F32, "g_Sf"); Sb = sb([128, 128], BF16, "g_Sb")
    P1s = sb([128, 128], BF16, "g_P1s"); VNs = sb([128, 128], BF16, "g_VNs")
    obuf = sb([128, 16, 128], F32, "g_obuf")
    otot = sb([128, 4, 128], F32, "g_otot"); osq = sb([128, 4, 128], F32, "g_osq")
    ss = sb([128, 4], F32); onb = sb([128, 4, 128], BF16, "g_onb")
    PF = K.ps(st, [128, 512], F32)
    PTr = K.ps(st, [128, 8, 128], BF16)
    PG = [K.ps(st, [128, 4, 128], F32) for _ in range(2)]
    PSq = K.ps(st, [128, 512], F32)
    PSh = K.ps(st, [128, 512], F32)
    cnt = {"pg": 0, "tr": 0}

    def transp(src, dst, n):
        half = cnt["tr"] % 2
        cnt["tr"] += 1
        for j in range(n):
            K.tr(PTr[:, half * 4 + j, :], src[:, j * 128:(j + 1) * 128], ident_b[:], [src, ident_b], [PTr])
        K.copy(dst[:, :n, :], PTr[:, half * 4:half * 4 + n, :], [PTr], [dst])

    for h in range(C["nh"]):
        for d in range(2):
            r = d * 8 + h
            K.memset(Sf[:], 0.0, [Sf])
            K.memset(Sb[:], 0.0, [Sb])
            order = SEGS if d == 0 else [SEGS[0], SEGS[4], SEGS[3], SEGS[2], SEGS[1]]
            for (c0, n) in order:
                t0, N = c0 * 128, n * 128
                latent = c0 >= 2
                tk = slice(t0, t0 + N)
                v3 = lambda t_: t_[:, :N].rearrange("p (c t) -> p c t", t=128)
                for X_, row in ((Xq, 0), (Xk, 1024), (Xv, 2048)):
                    K.dma("sp", X_[:, :N], US[row + h * 128:row + h * 128 + 128, tk], [US], [X_])
                for X_, o_, sc_ in ((Xq, qn, 128 ** -0.5), (Xk, kn, 1.0)):
                    K.act(sq[:, :N], X_[:, :N], AF.Square, [X_], [sq])
                    K.mm(PF[:, :N], ones[:], sq[:, :N], True, True, [ones, sq], [PF])
                    K.act(rn[:, :N], PF[:, :N], AF.Sqrt, [PF], [rn], bias=EPS)
                    K.recip(rn[:, :N], rn[:, :N], [rn], [rn])
                    K.stt(o_[:, :N], X_[:, :N], sc_, rn[:, :N], ALU.mult, ALU.mult, [X_, rn], [o_])
                K.mm(PF[:, :N], selg[:, r, :], GB[d][:, tk], True, True, [selg, GB[d]], [PF])
                K.act(bcG[:, :N], PF[:, :N], AF.Copy, [PF], [bcG])
                K.act(eG[:, :N], PF[:, :N], AF.Exp, [PF], [eG])
                K.mm(PF[:, :N], selb[:, r, :], GB[d][:, tk], True, True, [selb, GB[d]], [PF])
                K.act(bcB[:, :N], PF[:, :N], AF.Copy, [PF], [bcB])
                lastcol = 127 if d == 0 else 0
                glast = v3(bcG)[:, :, lastcol:lastcol + 1]
                K.tt(KB[:, :N], kn[:, :N], bcB[:, :N], ALU.mult, [kn, bcB], [KB])
                K.copy(kbT[:, :N], KB[:, :N], [KB], [kbT], eng="pool")
                K.copy(knb[:, :N], kn[:, :N], [kn], [knb], eng="pool")
                K.act(qnb[:, :N], qn[:, :N], AF.Copy, [qn], [qnb])
                K.stt(nKBG[:, :N], KB[:, :N], -1.0, eG[:, :N], ALU.mult, ALU.mult, [KB, eG], [nKBG])
                K.tt(Qd[:, :N], qn[:, :N], eG[:, :N], ALU.mult, [qn, eG], [Qd], eng="pool")
                K.tt(v3(tmp), glast.to_broadcast([128, n, 128]), v3(bcG), ALU.subtract, [bcG], [tmp])
                K.act(tmp[:, :N], tmp[:, :N], AF.Exp, [tmp], [tmp])
                K.tt(Ktl[:, :N], kn[:, :N], tmp[:, :N], ALU.mult, [kn, tmp], [Ktl])
                K.act(vb[:, :N], Xv[:, :N], AF.Copy, [Xv], [vb])
                transp(Ktl, Ktt, n)
                transp(vb, Vt, n)
                gct = tokT[d][:, c0:c0 + n, r:r + 1].to_broadcast([128, n, 128])
                bg3 = v3(bcG)
                K.tt(Dtmp[:, :n, :], bg3, bigm[:, d, 0, :].unsqueeze(1).to_broadcast([128, n, 128]), ALU.add, [bcG, bigm], [Dtmp])
                K.tt(Dtmp[:, :n, :], Dtmp[:, :n, :], gct, ALU.subtract, [Dtmp, tokT[d]], [Dtmp], eng="pool")
                K.act(Dtmp[:, :n, :], Dtmp[:, :n, :], AF.Exp, [Dtmp], [Dtmp], scale=-1.0)
                K.tt(Dc[:, :n, 0, :], Dtmp[:, :n, :], offd[:].unsqueeze(1).to_broadcast([128, n, 128]), ALU.mult, [Dtmp, offd], [Dc],
                     eng="pool")
                K.tt(DiT[:, :n, :], bg3, bigm[:, d, 1, :].unsqueeze(1).to_broadcast([128, n, 128]), ALU.add, [bcG, bigm], [DiT])
                K.tt(DiT[:, :n, :], DiT[:, :n, :], gct, ALU.subtract, [DiT, tokT[d]], [DiT], eng="pool")
                K.act(DiT[:, :n, :], DiT[:, :n, :], AF.Exp, [DiT], [DiT])
                K.tt(Dc[:, :n, 1, :], DiT[:, :n, :], offd[:].unsqueeze(1).to_broadcast([128, n, 128]), ALU.mult, [DiT, offd], [Dc],
                     eng="pool")
                LLv = LLg[:].rearrange("p a b t -> p (a b) t")
                for j in range(n):
                    cs = slice(j * 128, (j + 1) * 128)
                    cnt["pg"] += 1
                    G = PG[cnt["pg"] % 2]
                    K.mm(G[:, 0, :], kbT[:, cs], knb[:, cs], True, True, [kbT, knb], [G])
                    K.mm(G[:, 1, :], knb[:, cs], kbT[:, cs], True, True, [kbT, knb], [G])
                    K.mm(G[:, 2, :], knb[:, cs], qnb[:, cs], True, True, [qnb, knb], [G])
                    K.tt(LLv[:, 2 * j:2 * j + 2, :], G[:, 0:2, :], Dc[:, j, :, :], ALU.mult, [G, Dc], [LLg])
                    K.tt(QKt[:, j, :], G[:, 2, :], DiT[:, j, :], ALU.mult, [G, DiT], [QKt])
                inverse_units(K, C, LLg, n, d, XTb, IW, nunits=n)
                jl = list(range(n)) if d == 0 else list(range(n - 1, -1, -1))
                for j in jl:
                    cs = slice(j * 128, (j + 1) * 128)
                    c = c0 + j
                    K.mm(PSq[:, 0:128], nKBG[:, cs], Sb[:], True, True, [nKBG, Sb], [PSq])
                    K.stt(P1s[:], Vt[:, j, :], tokT[d][:, c, 32 + r:33 + r], PSq[:, 0:128], ALU.mult, ALU.add,
                          [Vt, tokT[d], PSq], [P1s])
                    K.mm(PSq[:, 128:256], XTb[:, j, :], P1s[:], True, True, [XTb, P1s], [PSq])
                    K.act(VNs[:], PSq[:, 128:256], AF.Copy, [PSq], [VNs])
                    if latent:
                        K.mm(PSq[:, 256:384], Qd[:, cs], Sb[:], True, False, [Qd, Sb], [PSq])
                        K.mm(PSq[:, 256:384], QKt[:, j, :], VNs[:], False, True, [QKt, VNs], [PSq])
                    K.mm(PSh[:, 0:128], Ktt[:, j, :], VNs[:], True, True, [Ktt, VNs], [PSh])
                    gl = eG[:, j * 128 + lastcol:j * 128 + lastcol + 1]
                    K.stt(Sf[:], Sf[:], gl, PSh[:, 0:128], ALU.mult, ALU.add, [Sf, eG, PSh], [Sf])
                    K.act(Sb[:], Sf[:], AF.Copy, [Sf], [Sb])
                    if latent:
                        cg = c - 2
                        if d == 0:
                            K.act(obuf[:, cg, :], PSq[:, 256:384], AF.Copy, [PSq], [obuf])
                        else:
                            K.tt(otot[:, j, :], obuf[:, cg, :], PSq[:, 256:384], ALU.add, [obuf, PSq], [otot])
                if d == 1 and latent:
                    if C["dbg"]:
                        K.dma("sp", C["dbg"]["of"][h, :, c0 - 2:c0 - 2 + n, :], otot[:], [otot], [])
                    K.tt(osq[:], otot[:], otot[:], ALU.mult, [otot], [osq], eng="pool")
                    K.S.add("dve", lambda e_: e_.tensor_reduce(out=ss[:], in_=osq[:], axis=AX.X, op=ALU.add), _bufs([osq]), _bufs([ss]))
                    K.act(ss[:], ss[:], AF.Sqrt, [ss], [ss], scale=1.0 / 128, bias=EPS)
                    K.recip(ss[:], ss[:], [ss], [ss])
                    K.tt(osq[:], otot[:], ss[:].unsqueeze(2).to_broadcast([128, 4, 128]), ALU.mult, [otot, ss], [osq])
                    K.tt(onb[:], osq[:], gnw[:].unsqueeze(1).to_broadcast([128, 4, 128]), ALU.mult, [osq, gnw], [onb])
                    K.dma("sp", sz[:, :N], SZ[h * 128:(h + 1) * 128, t0 - 256:t0 - 256 + N], [SZ], [sz])
                    half = cnt["tr"] % 2
                    cnt["tr"] += 1
                    for j in range(n):
                        K.tr(PTr[:, half * 4 + j, :], onb[:, j, :], ident_b[:], [onb, ident_b], [PTr])
                    K.tt(obT[:, h, t0 - 256:t0 - 256 + N].rearrange("p (j t) -> p j t", t=128), PTr[:, half * 4:half * 4 + n, :],
                         sz[:, :N].rearrange("p (j t) -> p j t", t=128), ALU.mult, [PTr, sz], [obT])


def merge_phase(K, top, C):
    oaT, obT, ident_b, ident_f, modT, s2 = C["oaT"], C["obT"], C["ident_b"], C["ident_f"], C["modT"], C["s2"]
    SG, X1, x_d, out_d = C["SG"], C["X1"], C["x"], C["out"]
    MODS = C["MODS"]
    dbg = C["dbg"]
    bc = K.sb(top, [128, 1, D], F32, "bc_rows")
    with ExitStack() as s0:
        pt = K.ps(s0, [16, 2, 128], F32)
        rows = K.sb(s0, [16, 2, 128], F32)
        for i, sec in enumerate((2, 5)):
            K.tr(pt[:, i, :], modT[:, sec * 16:(sec + 1) * 16, 0], ident_f[:], [modT, ident_f], [pt])
        K.copy(rows[:], pt[:], [pt], [rows])
        for i in range(2):
            K.dma("sp", MODS[i * 16:(i + 1) * 16, :], rows[:, i, :], [rows], [MODS])
        for i in range(1):
            K.dma("sp", bc[:, i, :], MODS[i * 16:(i + 1) * 16, :].rearrange("(o a) b -> o (a b)", o=1).to_broadcast([128, D]),
                  [MODS], [bc])
    K.S.barrier()
    p3 = ExitStack()
    mT = K.sb(p3, [128, 16, TL], BF16, "mT")
    with ExitStack() as s1:
        wa = [K.sb(s1, [128, 8, 256], BF16) for _ in range(2)]
        wbb = [K.sb(s1, [128, 8, 256], BF16) for _ in range(2)]
        sga = [K.sb(s1, [128, TL], BF16)] * 2
        sgb = [K.sb(s1, [128, TL], BF16)] * 2
        t1 = [K.sb(s1, [128, 512], F32) for _ in range(2)]
        t2 = [K.sb(s1, [128, 512], F32) for _ in range(2)]
        pa = [K.ps(s1, [128, 512], F32) for _ in range(2)]
        pb = [K.ps(s1, [128, 512], F32) for _ in range(2)]
        it = 0
        for sc in range(8):
            K.dma("pool", wa[sc % 2][:], C["p_a"][:, sc * 256:(sc + 1) * 256].rearrange("(k p) n -> p k n", p=128), [], [wa[sc % 2]])
            K.dma("pool", wbb[sc % 2][:], C["p_b"][:, sc * 256:(sc + 1) * 256].rearrange("(k p) n -> p k n", p=128), [], [wbb[sc % 2]])
            for ff in range(2):
                f = sc * 2 + ff
                ga, gb = sga[f % 2], sgb[f % 2]
                K.dma("sp", ga[:], SG[f * 128:(f + 1) * 128, :], [SG], [ga])
                K.dma("sp", gb[:], SG[2048 + f * 128:2048 + (f + 1) * 128, :], [SG], [gb])
                for n in range(4):
                    ts_ = slice(n * 512, (n + 1) * 512)
                    A_, B_, T1, T2 = pa[it % 2], pb[it % 2], t1[it % 2], t2[it % 2]
                    it += 1
                    for k in range(8):
                        K.mm(A_[:], wa[sc % 2][:, k, ff * 128:(ff + 1) * 128], oaT[:, k, ts_], k == 0, k == 7, [wa[sc % 2], oaT], [A_])
                    for k in range(8):
                        K.mm(B_[:], wbb[sc % 2][:, k, ff * 128:(ff + 1) * 128], obT[:, k, ts_], k == 0, k == 7, [wbb[sc % 2], obT], [B_])
                    K.tt(T1[:], A_[:], ga[:, ts_], ALU.mult, [A_, ga], [T1])
                    K.tt(T2[:], B_[:], gb[:, ts_], ALU.mult, [B_, gb], [T2])
                    K.tt(mT[:, f, ts_], T1[:], T2[:], ALU.add, [T1, T2], [mT], eng="pool")
    K.S.barrier()
    with ExitStack() as s2_:
        wo = [[K.sb(s2_, [128, 8, 512], BF16) for _ in range(2)] for _ in range(2)]
        xt = [K.sb(s2_, [128, 512], F32) for _ in range(3)]
        tt_ = [K.sb(s2_, [128, 512], F32) for _ in range(3)]
        pp = [K.ps(s2_, [128, 512], F32) for _ in range(4)]
        it = 0
        for n in range(4):
            ns = slice(n * 512, (n + 1) * 512)
            w = wo[n % 2]
            for kh in range(2):
                K.dma("pool", w[kh][:], C["w_out"][kh * 1024:(kh + 1) * 1024, ns].rearrange("(k p) n -> p k n", p=128), [], [w[kh]])
            for t in range(16):
                P, X_, T_ = pp[it % 4], xt[it % 3], tt_[it % 3]
                it += 1
                K.dma("sp", X_[:], x_d[t * 128:(t + 1) * 128, ns], [], [X_])
                for k in range(16):
                    K.mm(P[:], mT[:, k, t * 128:(t + 1) * 128], w[k // 8][:, k % 8, :], k == 0, k == 15, [mT, w[k // 8]], [P])
                K.tt(T_[:], P[:], bc[:, 0, ns], ALU.mult, [P, bc], [T_])
                K.tt(T_[:], T_[:], X_[:], ALU.add, [T_, X_], [T_], eng="pool")
                K.dma("sp", X1[t * 128:(t + 1) * 128, ns], T_[:], [T_], [X1])
    p3.close()


def ffn_phase(K, top, C):
    ident_b, ident_f, modT, s2 = C["ident_b"], C["ident_f"], C["modT"], C["s2"]
    X1, out_d, MODS = C["X1"], C["out"], C["MODS"]
    G = 512
    with ExitStack() as s4:
        bc = K.sb(s4, [128, 3, D], F32, "bc_rows4")
        K.dma("sp", bc[:, 1, :], MODS[16:32, :].rearrange("(o a) b -> o (a b)", o=1).to_broadcast([128, D]), [MODS], [bc])
        K.dma("sp", bc[:, 2, :], C["fnw"][0:1, :].to_broadcast([128, D]), [], [bc])
        h2T = K.sb(s4, [128, 16, G], BF16, "h2T")
        actT = K.sb(s4, [128, 44, G], BF16, "actT")
        x1t = [K.sb(s4, [128, D], F32, "x1t%d" % i) for i in range(4)]
        xb = K.sb(s4, [128, D], BF16)
        junk = K.sb(s4, [128, D], BF16)
        ss = K.sb(s4, [128, 1], F32); rs = K.sb(s4, [128, 1], F32)
        wg = [[K.sb(s4, [128, 8, 256], BF16) for _ in range(2)] for _ in range(2)]
        wu = [[K.sb(s4, [128, 8, 256], BF16) for _ in range(2)] for _ in range(2)]
        wd = [K.sb(s4, [128, 4, 512], BF16) for _ in range(4)]
        sgt = [K.sb(s4, [128, G], F32) for _ in range(2)]
        tq = [K.sb(s4, [128, 512], F32) for _ in range(2)]
        ot = [K.sb(s4, [128, D], F32) for _ in range(2)]
        ptr = [K.ps(s4, [128, 8, 128], BF16) for _ in range(1)]
        pgu = [K.ps(s4, [128, 512], F32) for _ in range(3)]
        pdn = [K.ps(s4, [128, 512], F32) for _ in range(4)]
        igu = 0
        for grp in range(TL // G):
            for t in range(4):
                X_ = x1t[t]
                row0 = grp * G + t * 128
                K.dma("sp", X_[:], X1[row0:row0 + 128, :], [X1], [X_])
                K.act(junk[:], X_[:], AF.Square, [X_], [junk, ss], accum=ss[:])
                K.act(rs[:], ss[:], AF.Sqrt, [ss], [rs], scale=1.0 / D, bias=EPS)
                K.recip(rs[:], rs[:], [rs], [rs])
                K.act(xb[:], X_[:], AF.Copy, [X_, rs], [xb], scale=rs[:])
                for g in range(4):
                    P = ptr[0]
                    for j in range(4):
                        k = g * 4 + j
                        K.tr(P[:, j, :], xb[:, k * 128:(k + 1) * 128], ident_b[:], [xb, ident_b], [P])
                    for j in range(4):
                        k = g * 4 + j
                        K.act(h2T[:, k, t * 128:(t + 1) * 128], P[:, j, :], AF.Identity, [P, s2, modT], [h2T],
                              scale=s2[:, k:k + 1], bias=modT[:, 48 + k, 0:1])
            for sc in range(22):
                w1, w2 = wg[sc % 2], wu[sc % 2]
                for kh in range(2):
                    K.dma("pool", w1[kh][:], C["w_gu"][kh * 1024:(kh + 1) * 1024, sc * 256:(sc + 1) * 256].rearrange("(k p) n -> p k n", p=128),
                          [], [w1[kh]])
                    K.dma("pool", w2[kh][:], C["w_gu"][kh * 1024:(kh + 1) * 1024, FFN + sc * 256:FFN + (sc + 1) * 256].rearrange("(k p) n -> p k n", p=128),
                          [], [w2[kh]])
                for jj in range(2):
                    j = sc * 2 + jj
                    Pg, Pu = pgu[igu % 3], pgu[(igu + 1) % 3]
                    SGt = sgt[(igu // 2) % 2]
                    igu += 2
                    for k in range(16):
                        K.mm(Pg[:, :G], w1[k // 8][:, k % 8, jj * 128:(jj + 1) * 128], h2T[:, k, :], k == 0, k == 15, [w1[k // 8], h2T], [Pg])
                    for k in range(16):
                        K.mm(Pu[:, :G], w2[k // 8][:, k % 8, jj * 128:(jj + 1) * 128], h2T[:, k, :], k == 0, k == 15, [w2[k // 8], h2T], [Pu])
                    K.act(SGt[:], Pg[:, :G], AF.Silu, [Pg], [SGt])
                    K.tt(actT[:, j, :], SGt[:], Pu[:, :G], ALU.mult, [SGt, Pu], [actT])
            iw = 0
            for n in range(4):
                ns = slice(n * 512, (n + 1) * 512)
                for k4 in range(11):
                    W = wd[iw % 4]
                    iw += 1
                    K.dma("pool", W[:], C["w_dn"][k4 * 512:(k4 + 1) * 512, ns].rearrange("(k p) n -> p k n", p=128), [], [W])
                    for kk in range(4):
                        k = k4 * 4 + kk
                        for t in range(4):
                            K.mm(pdn[t][:], actT[:, k, t * 128:(t + 1) * 128], W[:, kk, :], k == 0, k == 43, [actT, W], [pdn[t]])
                for t in range(4):
                    T_ = tq[t % 2]
                    K.tt(T_[:], pdn[t][:], bc[:, 1, ns], ALU.mult, [pdn[t], bc], [T_])
                    K.tt(x1t[t][:, ns], x1t[t][:, ns], T_[:], ALU.add, [x1t[t], T_], [x1t[t]], eng="pool")
            for t in range(4):
                X_ = x1t[t]
                O_ = ot[t % 2]
                row0 = grp * G + t * 128
                K.act(junk[:], X_[:], AF.Square, [X_], [junk, ss], accum=ss[:])
                K.act(rs[:], ss[:], AF.Sqrt, [ss], [rs], scale=1.0 / D, bias=EPS)
                K.recip(rs[:], rs[:], [rs], [rs])
                K.stt(O_[:], X_[:], rs[:], bc[:, 2, :], ALU.mult, ALU.mult, [X_, rs, bc], [O_])
                K.dma("sp", out_d[row0:row0 + 128, :], O_[:], [O_], [out_d])

NHP = 8
NGH = 8


def build(debug=False, only=None):
    nc = bass.Bass("TRN2", target_bir_lowering=False)
    K = KB(nc)
    inp = lambda name, shape: K.dram(name, shape, F32, kind="ExternalInput")
    x_d = inp("x", [TL, D])
    ctx_d = inp("ctx", [TC, D])
    cc_d = inp("cc", [128, 16, 2])
    wada_d = inp("w_ada", [D, 6 * D])
    bada_d = inp("b_ada", [128, 96])
    n1w_d = inp("norm1_w", [128, 16])
    n2w_d = inp("norm2_w", [128, 16])
    fnw_d = inp("final_norm_w", [1, D])
    win_d = inp("w_in", [D, NIN * 128])
    mu_d = inp("rw_mu", [128, 29])
    cmask_d = inp("cmask", [128, 8])
    convw_d = inp("gdn_conv_w", [128, 24, 5])
    ident_d = inp("ident", [128, 128])
    w0_d = inp("rw_w0", [128, 2, 8])
    a0_d = inp("rw_a0", [128, 2, 8])
    kkw_d = inp("rw_k_k", [128, 8])
    ka_d = inp("rw_k_a", [128, 8])
    rk_d = inp("rw_r_k", [128, 8])
    w2_d = inp("rw_w2", [2, 96, 1024])
    a2_d = inp("rw_a2", [2, 96, 1024])
    g2_d = inp("rw_g2", [64, 1024])
    lnw_d = inp("rw_ln_w", [1, 1024])
    lnb_d = inp("rw_ln_b", [1, 1024])
    m1_d = inp("m1", [128, 2, 4, 128])
    m2_d = inp("m2", [128, 2, 3, 128])
    rmask_d = inp("rmask", [128, 512])
    bones_d = inp("blockones", [128, 128])
    hsel_d = inp("headsel", [128, 2])
    nmask_d = inp("nmask", [128, 2, 7, 128])
    selg_d = inp("selg", [64, 16, 128])
    selb_d = inp("selb", [64, 16, 128])
    bigm_d = inp("bigm", [128, 2, 2, 128])
    offd_d = inp("offd", [128, 128])
    ones_d = inp("ones", [128, 128])
    alog_d = inp("gdn_a_log", [64, 1])
    dtb_d = inp("gdn_dt_bias", [64, 1])
    gnw_d = inp("gdn_norm_w", [1, 128])
    pa_d = inp("merge_p_a", [1024, D])
    pb_d = inp("merge_p_b", [1024, D])
    wout_d = inp("w_out", [D, D])
    wgu_d = inp("ffn_w_gate_up", [D, 2 * FFN])
    wdn_d = inp("ffn_w_down", [FFN, D])
    MODS = K.dram("MODS", [32, 128], F32)
    out_d = K.dram("out", [TL, D], F32, kind="ExternalOutput")
    dbg = {}
    if debug:
        dbg["xs"] = K.dram("dbg_xs", [24 * 128, TT], F32, kind="ExternalOutput")
        dbg["u"] = K.dram("dbg_u", [24 * 128, TT], F32, kind="ExternalOutput")
        dbg["mod"] = K.dram("dbg_mod", [128, 96 * 2], F32, kind="ExternalOutput")
        dbg["sm"] = K.dram("dbg_sm", [128, 5, TT], F32, kind="ExternalOutput")
        dbg["oa"] = K.dram("dbg_oa", [128, 8, TL], F32, kind="ExternalOutput")
        dbg["yf"] = K.dram("dbg_yf", [8, 128, 16, 128], F32, kind="ExternalOutput")
        dbg["ob"] = K.dram("dbg_ob", [128, 8, TL], F32, kind="ExternalOutput")
        dbg["of"] = K.dram("dbg_of", [8, 128, 16, 128], F32, kind="ExternalOutput")
    if only:
        XS = inp("XS_in", [24 * 128, TT])
        small_in = inp("small_in", [128, 5, TT])
        US = inp("US_in", [24 * 128, TT])
        SZ = inp("SZ_in", [8 * 128, TL])
        ab_in = inp("ab_in", [128, TT])
    else:
        XS = dbg["xs"] if debug else K.dram("XS", [24 * 128, TT], F32)
    if not only:
        US = dbg["u"] if debug else K.dram("US", [24 * 128, TT], F32)
        SZ = K.dram("SZ", [8 * 128, TL], F32)
    SG = K.dram("SG", [32 * 128, TL], BF16)
    X1 = K.dram("X1", [TL, D], F32)

    with ExitStack() as top:
        ident_f = K.sb(top, [128, 128], F32, "ident_f")
        ident_b = K.sb(top, [128, 128], BF16, "ident_b")
        K.dma("sp", ident_f[:], ident_d[:, :], [], [ident_f])
        K.copy(ident_b[:], ident_f[:], [ident_f], [ident_b])
        modT = K.sb(top, [128, 96, 2], F32, "modT")
        s1 = K.sb(top, [128, 16, 2], F32, "s1")
        s2 = K.sb(top, [128, 16], F32, "s2")
        scopeO = ExitStack()
        abT = K.sb(scopeO, [128, TT], F32, "abT")
        oaT = K.sb(scopeO, [128, 8, TL], BF16, "oaT")
        scopeA = ExitStack()
        smallT = K.sb(scopeA, [128, 5, TT], BF16, "smallT")

        if only:
            K.dma("pool", smallT[:], small_in[:, :, :], [], [smallT])
            K.dma("sp", abT[:], ab_in[:, :], [], [abT])
        with ExitStack() as p0:
          if not only:
                ccf = K.sb(p0, [128, 16, 2], F32)
                ccs = K.sb(p0, [128, 16, 2], F32)
                ccb = K.sb(p0, [128, 16, 2], BF16)
                bada = K.sb(p0, [128, 96], F32)
                n1w = K.sb(p0, [128, 16], F32)
                n2w = K.sb(p0, [128, 16], F32)
                K.dma("sp", ccf[:], cc_d[:, :, :], [], [ccf])
                K.dma("sp", bada[:], bada_d[:, :], [], [bada])
                K.dma("sp", n1w[:], n1w_d[:, :], [], [n1w])
                K.dma("sp", n2w[:], n2w_d[:, :], [], [n2w])
                K.act(ccs[:], ccf[:], AF.Silu, [ccf], [ccs])
                K.copy(ccb[:], ccs[:], [ccs], [ccb])
                wb = [[K.sb(p0, [128, 8, 512], BF16) for _ in range(2)] for _ in range(2)]
                pm = K.ps(p0, [128, 96, 2], F32)
                for sc in range(24):
                    w = wb[sc % 2]
                    for kh in range(2):
                        K.dma("pool", w[kh][:],
                              wada_d[kh * 1024:(kh + 1) * 1024, sc * 512:(sc + 1) * 512].rearrange("(k p) n -> p k n", p=128),
                              [], [w[kh]])
                    for jj in range(4):
                        j = sc * 4 + jj
                        for k in range(16):
                            K.mm(pm[:, j, :], w[k // 8][:, k % 8, jj * 128:(jj + 1) * 128], ccb[:, k, :], k == 0, k == 15,
                                 [w[k // 8], ccb], [pm])
                K.tt(modT[:], pm[:], bada[:].unsqueeze(2).to_broadcast([128, 96, 2]), ALU.add, [pm, bada], [modT])
                for v in range(2):
                    K.stt(s1[:, :, v], modT[:, 16:32, v], 1.0, n1w[:], ALU.add, ALU.mult, [modT, n1w], [s1])
                K.stt(s2[:], modT[:, 64:80, 0], 1.0, n2w[:], ALU.add, ALU.mult, [modT, n2w], [s2])
                if debug:
                    K.dma("sp", dbg["mod"][:, :], modT[:].rearrange("p a b -> p (a b)"), [modT], [])

        K.S.barrier()
        with ExitStack() as p1:
          if not only:
                hT = K.sb(p1, [128, 16, TT], BF16, "hT")
                mu = K.sb(p1, [128, 29], F32)
                omm = K.sb(p1, [128, 29], F32)
                cmask = K.sb(p1, [128, 8], F32)
                coef = K.sb(p1, [128, 6, 29], F32)
                convw = K.sb(p1, [128, 24, 5], F32)
                K.dma("sp", mu[:], mu_d[:, :], [], [mu])
                K.dma("sp", cmask[:], cmask_d[:, :], [], [cmask])
                K.dma("sp", convw[:], convw_d[:, :, :], [], [convw])
                K.ts(omm[:], mu[:], -1.0, 1.0, ALU.mult, ALU.add, [mu], [omm])
                for m in range(6):
                    K.ts(coef[:, m, :], mu[:], cmask[:, m:m + 1], None, ALU.mult, None, [mu, cmask], [coef])
                with ExitStack() as pa:
                    xt = [K.sb(pa, [128, D], F32) for _ in range(2)]
                    xb = [K.sb(pa, [128, D], BF16) for _ in range(2)]
                    junk = K.sb(pa, [128, D], BF16)
                    ss = [K.sb(pa, [128, 1], F32) for _ in range(2)]
                    rs = [K.sb(pa, [128, 1], F32) for _ in range(2)]
                    pt = [K.ps(pa, [128, 8, 128], BF16) for _ in range(2)]
                    npt = 0
                    for t in range(NCH):
                        X, XB, SS, RS = xt[t % 2], xb[t % 2], ss[t % 2], rs[t % 2]
                        src = ctx_d[t * 128:(t + 1) * 128, :] if t < 2 else x_d[(t - 2) * 128:(t - 1) * 128, :]
                        v = 1 if t < 2 else 0
                        K.dma("sp", X[:], src, [], [X])
                        K.act(junk[:], X[:], AF.Square, [X], [junk, SS], accum=SS[:])
                        K.act(RS[:], SS[:], AF.Sqrt, [SS], [RS], scale=1.0 / D, bias=EPS)
                        K.recip(RS[:], RS[:], [RS], [RS])
                        K.act(XB[:], X[:], AF.Copy, [X, RS], [XB], scale=RS[:])
                        for g in range(4):
                            P = pt[npt % 2]
                            npt += 1
                            for j in range(4):
                                k = g * 4 + j
                                K.tr(P[:, j, :], XB[:, k * 128:(k + 1) * 128], ident_b[:], [XB, ident_b], [P])
                            for j in range(4):
                                k = g * 4 + j
                                K.act(hT[:, k, t * 128:(t + 1) * 128], P[:, j, :], AF.Identity, [P, s1, modT], [hT],
                                      scale=s1[:, k, v:v + 1], bias=modT[:, k, v:v + 1])
                K.S.barrier()
                with ExitStack() as pb:
                    wb = [[K.sb(pb, [128, 8, 512], BF16) for _ in range(2)] for _ in range(2)]
                    stage = [K.sb(pb, [128, TT], F32) for _ in range(2)]
                    post = [K.sb(pb, [128, TT], F32) for _ in range(1)] * 2
                    postb = [K.sb(pb, [128, TL], BF16) for _ in range(1)] * 2
                    pp = [K.ps(pb, [128, 512], F32) for _ in range(4)]
                    npp = 0
                    ntile = [(0, 256)] + [(256 + i * 512, 512) for i in range(4)]
                    for sc in range(24):
                        ncol = min(512, NIN * 128 - sc * 512)
                        w = wb[sc % 2]
                        for kh in range(2):
                            K.dma("pool", w[kh][:, :, :ncol],
                                  win_d[kh * 1024:(kh + 1) * 1024, sc * 512:sc * 512 + ncol].rearrange("(k p) n -> p k n", p=128),
                                  [], [w[kh]])
                        for jj in range(ncol // 128):
                            q = sc * 4 + jj
                            lat_only = (53 <= q <= 60) or q >= 62
                            stg = stage[q % 2]
                            for (t0, tn) in ntile:
                                if lat_only and t0 == 0:
                                    continue
                                P = pp[npp % 4]
                                npp += 1
                                for k in range(16):
                                    K.mm(P[:, :tn], w[k // 8][:, k % 8, jj * 128:(jj + 1) * 128], hT[:, k, t0:t0 + tn],
                                         k == 0, k == 15, [w[k // 8], hT], [P])
                                if q <= 28 or 29 <= q <= 52 or q == 61:
                                    K.act(stg[:, t0:t0 + tn], P[:, :tn], AF.Copy, [P], [stg])
                                elif 53 <= q <= 60:
                                    K.act(stg[:, t0:t0 + tn], P[:, :tn], AF.Silu, [P], [stg])
                                else:
                                    K.act(postb[q % 2][:, t0 - 256:t0 - 256 + tn], P[:, :tn], AF.Sigmoid, [P], [postb[q % 2]])
                            if q <= 28:
                                xs = post[q % 2]
                                K.ts(xs[:], stg[:], omm[:, q:q + 1], None, ALU.mult, None, [stg, omm], [xs])
                                pl = stg[:, 256:TT].rearrange("p (r c) -> p r c", c=64)
                                xl = xs[:, 256:TT].rearrange("p (r c) -> p r c", c=64)
                                sh = [(xl[:, :, 1:64], pl[:, :, 0:63]), (xl[:, :, 0:63], pl[:, :, 1:64]),
                                      (xl[:, 1:32, :], pl[:, 0:31, :]), (xl[:, 0:31, :], pl[:, 1:32, :]),
                                      (xs[:, 1:256], stg[:, 0:255]), (xs[:, 0:255], stg[:, 1:256])]
                                for m, (o, i) in enumerate(sh):
                                    K.stt(o, i, coef[:, m, q:q + 1], o, ALU.mult, ALU.add, [stg, xs, coef], [xs])
                                if q < 24:
                                    K.dma("sp", XS[q * 128:(q + 1) * 128, :], xs[:], [xs], [XS])
                                elif q < 26:
                                    K.act(smallT[:, q - 24, :], xs[:], AF.Tanh, [xs], [smallT])
                                elif q < 28:
                                    K.act(smallT[:, q - 24, :], xs[:], AF.Copy, [xs], [smallT])
                                else:
                                    K.act(smallT[:, 4, :], xs[:], AF.Sigmoid, [xs], [smallT])
                            elif q <= 52:
                                g = q - 29
                                acc = post[q % 2]
                                K.ts(acc[:], stg[:], convw[:, g, 2:3], None, ALU.mult, None, [stg, convw], [acc])
                                for (a, b) in ((0, 256), (256, TT)):
                                    for j, o in ((0, 2), (1, 1), (3, -1), (4, -2)):
                                        if o > 0:
                                            ov, iv = acc[:, a + o:b], stg[:, a:b - o]
                                        else:
                                            ov, iv = acc[:, a:b + o], stg[:, a - o:b]
                                        K.stt(ov, iv, convw[:, g, j:j + 1], ov, ALU.mult, ALU.add, [stg, acc, convw], [acc])
                                K.act(acc[:], acc[:], AF.Silu, [acc], [acc])
                                K.dma("sp", US[g * 128:(g + 1) * 128, :], acc[:], [acc], [US])
                            elif q <= 60:
                                K.dma("sp", SZ[(q - 53) * 128:(q - 52) * 128, :], stg[:, 256:TT], [stg], [SZ])
                            elif q == 61:
                                K.copy(abT[:], stg[:], [stg], [abT], eng="pool")
                            else:
                                K.dma("sp", SG[(q - 62) * 128:(q - 61) * 128, :], postb[q % 2][:], [postb[q % 2]], [SG])
                K.S.barrier()
                if debug:
                    smf = K.sb(p1, [128, 5, TT], F32)
                    K.copy(smf[:], smallT[:], [smallT], [smf])
                    K.dma("sp", dbg["sm"][:, :, :], smf[:], [smf], [])

        K.S.barrier()
        with ExitStack() as p2:
            rwkv_phase(K, p2, dict(XS=XS, smallT=smallT, oaT=oaT, ident_b=ident_b, ident_f=ident_f,
                                   w0=w0_d, a0=a0_d, kkw=kkw_d, ka=ka_d, rk=rk_d, w2=w2_d, a2=a2_d, g2=g2_d,
                                   lnw=lnw_d, lnb=lnb_d, m1=m1_d, m2=m2_d, rmask=rmask_d, bones=bones_d, hsel=hsel_d, nmask=nmask_d,
                                   dbg=dbg, nhp=NHP))
        K.S.barrier()
        if debug:
            with ExitStack() as pd:
                of = K.sb(pd, [128, 8, TL], F32)
                K.copy(of[:, 0:NHP], oaT[:, 0:NHP], [oaT], [of])
                K.dma("sp", dbg["oa"][:, 0:NHP, :], of[:, 0:NHP], [of], [])
            K.S.barrier()
        scopeA.close()
        obT = K.sb(scopeO, [128, 8, TL], BF16, "obT")
        with ExitStack() as p2b:
            gdn_phase(K, p2b, dict(US=US, SZ=SZ, abT=abT, obT=obT, ident_b=ident_b, ident_f=ident_f, selg=selg_d, selb=selb_d,
                                   bigm=bigm_d, offd=offd_d, ones=ones_d, rmask=rmask_d, alog=alog_d, dtb=dtb_d, gnw=gnw_d,
                                   nmask=nmask_d, dbg=dbg, nh=NGH))
        K.S.barrier()
        if debug:
            with ExitStack() as pd:
                of = K.sb(pd, [128, 8, TL], F32)
                K.copy(of[:, 0:NGH], obT[:, 0:NGH], [obT], [of])
                K.dma("sp", dbg["ob"][:, 0:NGH, :], of[:, 0:NGH], [of], [])
            K.S.barrier()
        C34 = dict(oaT=oaT, obT=obT, ident_b=ident_b, ident_f=ident_f, modT=modT, s2=s2, SG=SG, X1=X1,
                   x=x_d, out=out_d, MODS=MODS, fnw=fnw_d, p_a=pa_d, p_b=pb_d, w_out=wout_d, w_gu=wgu_d,
                   w_dn=wdn_d, dbg=dbg)
        merge_phase(K, scopeO, C34)
        scopeO.close()
        K.S.barrier()
        ffn_phase(K, top, C34)
        K.S.emit(nc, top)
    return nc


def _fm(v, nchunk):
    return np.ascontiguousarray(np.asarray(v, np.float32).reshape(nchunk, 128).T)


def _pad_cols(a, n):
    out = np.zeros(a.shape[:-1] + (n,), np.float32)
    out[..., :a.shape[-1]] = a
    return out


def prep_shared(inputs):
    w_in = np.asarray(inputs["w_in"][0], np.float32)
    RW = 3520
    segs = [w_in[:, 0:3072]]
    for (a, b) in ((3072, 3168), (3168, 3264), (3264, 3360), (3360, 3456), (3456, 3520)):
        segs.append(_pad_cols(w_in[:, a:b], 128))
    segs.append(w_in[:, RW:RW + 3072 + 1024])
    abc = np.zeros((D, 128), np.float32)
    abc[:, 0:16] = w_in[:, 7616:7632]
    abc[:, 32:48] = w_in[:, 7632:7648]
    segs.append(abc)
    segs.append(w_in[:, 7648:])
    win = np.ascontiguousarray(np.concatenate(segs, axis=1))
    assert win.shape == (D, NIN * 128)
    mu = np.asarray(inputs["rw_mu"][0], np.float32)
    mus = [mu[0:3072]]
    for (a, b) in ((3072, 3168), (3168, 3264), (3264, 3360), (3360, 3456), (3456, 3520)):
        mus.append(_pad_cols(mu[a:b], 128))
    mu_fm = _fm(np.concatenate(mus), 29)
    p = np.arange(128)
    cmask = np.zeros((128, 8), np.float32)
    for m in range(4):
        cmask[:, m] = (p % 4 == m)
    cmask[:, 4] = (p % 2 == 0)
    cmask[:, 5] = (p % 2 == 1)
    convw = np.asarray(inputs["gdn_conv_w"][0], np.float32)
    convw_fm = np.ascontiguousarray(convw.reshape(5, 24, 128).transpose(2, 1, 0))
    sh = {
        "w_ada": np.ascontiguousarray(inputs["w_ada"][0], np.float32),
        "b_ada": _fm(inputs["b_ada"][0], 96),
        "norm1_w": _fm(inputs["norm1_w"][0], 16),
        "norm2_w": _fm(inputs["norm2_w"][0], 16),
        "final_norm_w": np.ascontiguousarray(np.asarray(inputs["final_norm_w"], np.float32).reshape(1, D)),
        "w_in": win,
        "rw_mu": mu_fm,
        "cmask": cmask,
        "gdn_conv_w": convw_fm,
        "ident": np.eye(128, dtype=np.float32),
    }
    g = lambda k: np.asarray(inputs[k][0], np.float32)
    sh["rw_w0"] = np.ascontiguousarray(g("rw_w0").reshape(2, 8, 128).transpose(2, 0, 1))
    sh["rw_a0"] = np.ascontiguousarray(g("rw_a0").reshape(2, 8, 128).transpose(2, 0, 1))
    sh["rw_k_k"] = _fm(g("rw_k_k"), 8)
    sh["rw_k_a"] = _fm(g("rw_k_a"), 8)
    sh["rw_r_k"] = _fm(g("rw_r_k").reshape(-1), 8)
    sh["rw_w2"] = np.ascontiguousarray(g("rw_w2"))
    sh["rw_a2"] = np.ascontiguousarray(g("rw_a2"))
    sh["rw_g2"] = np.ascontiguousarray(g("rw_g2"))
    sh["rw_ln_w"] = np.ascontiguousarray(g("rw_ln_w").reshape(1, 1024))
    sh["rw_ln_b"] = np.ascontiguousarray(g("rw_ln_b").reshape(1, 1024))
    r_ = np.arange(128)[:, None]; c_ = np.arange(128)[None, :]
    SL = (c_ < r_).astype(np.float32); SU = (c_ > r_).astype(np.float32)
    IL = (c_ <= r_).astype(np.float32); IU = (c_ >= r_).astype(np.float32)
    m1 = np.stack([np.stack([SL, SU, SL, SU], 0), np.stack([SU, SL, SU, SL], 0)], 0)
    m2 = np.stack([np.stack([SU, IU, -IU], 0), np.stack([SL, IL, -IL], 0)], 0)
    sh["m1"] = np.ascontiguousarray(m1.transpose(2, 0, 1, 3))
    sh["m2"] = np.ascontiguousarray(m2.transpose(2, 0, 1, 3))
    rmask = np.ones((128, 512), np.float32); rmask[:, ::128] = 0.0
    sh["rmask"] = rmask
    bo = np.zeros((128, 128), np.float32); bo[:64, :64] = 1.0; bo[64:, 64:] = 1.0
    sh["blockones"] = bo
    hs = np.zeros((128, 2), np.float32); hs[:64, 0] = 1.0; hs[64:, 1] = 1.0
    sh["headsel"] = hs
    nmk = np.zeros((2, 7, 128, 128), np.float32)
    for lv in range(7):
        bsz = 1 << lv
        low = ((r_ // (2 * bsz) == c_ // (2 * bsz)) & ((r_ // bsz) % 2 == 1) & ((c_ // bsz) % 2 == 0)).astype(np.float32)
        nmk[0, lv] = -low
        nmk[1, lv] = -low.T
    sh["nmask"] = np.ascontiguousarray(nmk.transpose(2, 0, 1, 3))
    selg = np.zeros((64, 16, 128), np.float32); selb = np.zeros((64, 16, 128), np.float32)
    for r0 in range(16):
        selg[r0, r0, :] = 1.0
        selb[32 + r0, r0, :] = 1.0
    sh["selg"] = selg; sh["selb"] = selb
    BIG = 1.0e4
    bigm = np.stack([np.stack([BIG * SU, -BIG * SL], 0), np.stack([BIG * SL, -BIG * SU], 0)], 0)
    sh["bigm"] = np.ascontiguousarray(bigm.transpose(2, 0, 1, 3))
    sh["offd"] = (1.0 - np.eye(128)).astype(np.float32)
    sh["ones"] = np.ones((128, 128), np.float32)
    al = np.zeros((64, 1), np.float32); al[0:16, 0] = g("gdn_a_log").reshape(-1)
    db = np.zeros((64, 1), np.float32); db[0:16, 0] = g("gdn_dt_bias").reshape(-1)
    sh["gdn_a_log"] = al; sh["gdn_dt_bias"] = db
    sh["gdn_norm_w"] = np.ascontiguousarray(g("gdn_norm_w").reshape(1, 128))
    for k_ in ("merge_p_a", "merge_p_b", "w_out", "ffn_w_gate_up", "ffn_w_down"):
        sh[k_] = np.ascontiguousarray(g(k_))
    return sh


def make_in_maps(inputs):
    sh = prep_shared(inputs)
    maps = []
    for b in range(8):
        m = dict(sh)
        m["x"] = np.ascontiguousarray(inputs["x"][b], np.float32)
        m["ctx"] = np.ascontiguousarray(inputs["ctx"][b], np.float32)
        cc = np.stack([np.asarray(inputs["c"][b], np.float32), np.asarray(inputs["c_ctx"], np.float32)], axis=-1)
        m["cc"] = np.ascontiguousarray(cc.reshape(16, 128, 2).transpose(1, 0, 2))
        maps.append(m)
    return maps


_NC = None


def kernel(**inputs):
    global _NC
    if _NC is None:
        _NC = build()
    maps = make_in_maps(inputs)
    res = run_bass_kernel_spmd(_NC, maps, core_ids=list(range(8)))
    return np.stack([r["out"] for r in res.results], axis=0).astype(np.float32)
```

```python
import numpy as np
from contextlib import ExitStack
import concourse.bass as bass
import concourse.mybir as mybir
from concourse.bass_utils import run_bass_kernel_spmd

F32 = mybir.dt.float32
BF16 = mybir.dt.bfloat16
AF = mybir.ActivationFunctionType
ALU = mybir.AluOpType
AX = mybir.AxisListType

COMPUTE = ("pe", "act", "dve", "pool")
NDSEM = 24

D = 2048
TC = 256
TL = 2048
TT = TC + TL
NCH = TT // 128
NIN = 94
FFN = 5632
EPS = 1e-6
DEC = 0.6065306597126334


class Buf:
    __slots__ = ("name", "lw", "rd")

    def __init__(self, name=""):
        self.name = name
        self.lw = None
        self.rd = {}


class Sched:
    def __init__(self):
        self.ops = []
        self.last = {}
        self.dmas = []
        self.bar = set()
        self.bar_seen = set()

    def barrier(self):
        if not hasattr(self, "marks"):
            self.marks = []
        self.marks.append({e: sum(1 for o in self.ops if o[0] == e and not o[3]) for e in COMPUTE})
        self.bar = set(self.last.values()) | set(self.dmas)
        self.dmas = []
        self.bar_seen = set()

    def add(self, eng, fn, reads=(), writes=(), dma=False):
        i = len(self.ops)
        deps = set()
        if eng not in self.bar_seen:
            deps |= self.bar
            self.bar_seen.add(eng)
        self.last[eng] = i
        if dma:
            self.dmas.append(i)
        for b in reads:
            if b.lw is not None:
                deps.add(b.lw)
        for b in writes:
            if b.lw is not None:
                deps.add(b.lw)
            deps.update(b.rd.values())
        key = ("d", i) if dma else eng
        for b in reads:
            b.rd[key] = i
        for b in writes:
            b.lw = i
            b.rd = {}
        if eng == "pe" and getattr(self, "relax", False):
            deps = set(d for d in deps if not (self.ops[d][0] == "pe" and not self.ops[d][3]))
        self.ops.append((eng, fn, deps, dma))
        return i

    def emit(self, nc, stack):
        ops = self.ops
        engs = {"pe": nc.tensor, "act": nc.scalar, "dve": nc.vector, "pool": nc.gpsimd, "sp": nc.sync}
        names = list(engs)
        csem = {e: stack.enter_context(nc.semaphore("c_" + e)) for e in COMPUTE}
        dsem = {e: [stack.enter_context(nc.semaphore("d_%s%d" % (e, k))) for k in range(NDSEM)]
                for e in ("sp", "act", "pool")}
        comp = [None] * len(ops)
        cnt = {e: 0 for e in COMPUTE}
        dcnt = {e: 0 for e in dsem}
        prevslot = [None] * len(ops)
        for i, (eng, fn, deps, dma) in enumerate(ops):
            if dma:
                j = dcnt[eng]
                dcnt[eng] += 1
                comp[i] = (dsem[eng][j % NDSEM], 16 * (j // NDSEM + 1))
                if j >= NDSEM:
                    prevslot[i] = (dsem[eng][j % NDSEM], 16 * (j // NDSEM))
            else:
                cnt[eng] += 1
                comp[i] = (csem[eng], cnt[eng])
        per = {e: [] for e in names}
        for i, op in enumerate(ops):
            per[op[0]].append(i)
        block = stack.enter_context(nc.Block())

        def run(ename):
            def body(e):
                known = {}
                for i in per[ename]:
                    eng, fn, deps, dma = ops[i]
                    need = {}
                    cands = [comp[d] for d in deps]
                    if prevslot[i] is not None:
                        cands.append(prevslot[i])
                    for sm, v in cands:
                        k = id(sm)
                        if known.get(k, 0) >= v:
                            continue
                        if k not in need or need[k][1] < v:
                            need[k] = (sm, v)
                    for k, (sm, v) in need.items():
                        e.wait_ge(sm, v)
                        known[k] = v
                    ins = fn(e)
                    sm, v = comp[i]
                    ins.then_inc(sm, 16 if dma else 1)
                if ename in dsem:
                    last = {}
                    for i in per[ename]:
                        if ops[i][3]:
                            sm, v = comp[i]
                            last[id(sm)] = (sm, v)
                    for sm, v in last.values():
                        e.wait_ge(sm, v)
            return body

        block.tensor(run("pe"))
        block.scalar(run("act"))
        block.vector(run("dve"))
        block.gpsimd(run("pool"))
        block.sync(run("sp"))


class T:
    def __init__(self, h, name):
        self.h = h
        self.b = Buf(name)

    def __getitem__(self, k):
        return self.h[k]


def _bufs(xs):
    return [x if isinstance(x, Buf) else x.b for x in xs]


class KB:
    def __init__(self, nc):
        self.nc = nc
        self.S = Sched()
        self.n = 0

    def sb(self, st, shape, dt, name=None):
        self.n += 1
        name = name or "t%d" % self.n
        if not hasattr(self, "used"):
            self.used = set()
        while name in self.used:
            name = name + "_"
        self.used.add(name)
        return T(st.enter_context(self.nc.sbuf_tensor(name, list(shape), dt)), name)

    def ps(self, st, shape, dt, name=None):
        self.n += 1
        name = name or "p%d" % self.n
        return T(st.enter_context(self.nc.psum_tensor(name, list(shape), dt)), name)

    def dram(self, name, shape, dt, kind="Internal"):
        h = self.nc.dram_tensor(name, list(shape), dt, kind=kind)
        t = T(h.ap(), name)
        return t

    def act(self, out, in_, func, r, w, scale=1.0, bias=0.0, accum=None):
        kw = {}
        if accum is not None:
            kw["accum_out"] = accum
        self.S.add("act", lambda e: e.activation(out=out, in_=in_, func=func, scale=scale, bias=bias, **kw),
                   _bufs(r), _bufs(w))

    def tt(self, out, in0, in1, op, r, w, eng="dve"):
        self.S.add(eng, lambda e: e.tensor_tensor(out=out, in0=in0, in1=in1, op=op), _bufs(r), _bufs(w))

    def ts(self, out, in0, s1, s2, op0, op1, r, w, eng="dve", accum=None):
        kw = {}
        if accum is not None:
            kw["accum_out"] = accum
        if op1 is None:
            self.S.add(eng, lambda e: e.tensor_scalar(out=out, in0=in0, scalar1=s1, scalar2=None, op0=op0, **kw),
                       _bufs(r), _bufs(w))
        else:
            self.S.add(eng, lambda e: e.tensor_scalar(out=out, in0=in0, scalar1=s1, scalar2=s2, op0=op0, op1=op1, **kw),
                       _bufs(r), _bufs(w))

    def stt(self, out, in0, scalar, in1, op0, op1, r, w):
        self.S.add("dve", lambda e: e.scalar_tensor_tensor(out=out, in0=in0, scalar=scalar, in1=in1, op0=op0, op1=op1),
                   _bufs(r), _bufs(w))

    def copy(self, out, in_, r, w, eng="dve"):
        self.S.add(eng, lambda e: e.tensor_copy(out=out, in_=in_), _bufs(r), _bufs(w))

    def memset(self, out, val, w, eng="pool"):
        self.S.add(eng, lambda e: e.memset(out, val), [], _bufs(w))

    def recip(self, out, in_, r, w):
        self.S.add("dve", lambda e: e.reciprocal(out=out, in_=in_), _bufs(r), _bufs(w))

    def scan(self, out, d0, d1, init, op0, op1, r, w):
        self.S.add("dve", lambda e: e.tensor_tensor_scan(out=out, data0=d0, data1=d1, initial=init, op0=op0, op1=op1),
                   _bufs(r), _bufs(w))

    def mm(self, out, lhsT, rhs, start, stop, r, w):
        self.S.add("pe", lambda e: e.matmul(out, lhsT=lhsT, rhs=rhs, start=start, stop=stop), _bufs(r), _bufs(w))

    def tr(self, out, in_, ident, r, w):
        self.S.add("pe", lambda e: e.transpose(out=out, in_=in_, identity=ident), _bufs(r), _bufs(w))

    def dma(self, q, out, in_, r, w, **kw):
        self.S.add(q, lambda e: e.dma_start(out=out, in_=in_, **kw), _bufs(r), _bufs(w), dma=True)


def inverse_workspace(K, st, C):
    W = {}
    W["nmask"] = K.sb(st, [128, 2, 7, 128], BF16, "nmask_sb")
    K.dma("pool", W["nmask"][:], C["nmask"][:, :, :, :], [], [W["nmask"]])
    W["ident_b"] = C["ident_b"]
    for nm in ("Xa", "Xb", "Ya", "Yb", "LsX", "LsY", "M1", "M2", "Mt", "R"):
        W[nm] = K.sb(st, [128, 4, 128], BF16, "iw_" + nm)
    W["PI"] = [K.ps(st, [128, 4, 128], F32) for _ in range(2)]
    W["cnt"] = 0
    return W


def inverse_units(K, C, LL, n, d, XTb, W, nunits=None):
    LLf = LL[:].rearrange("p j f t -> p (j f) t")
    nm = W["nmask"]
    nunits = 2 * n if nunits is None else nunits
    gs = min(4, nunits)
    idb = W["ident_b"][:].unsqueeze(1).to_broadcast([128, gs, 128])

    def pi():
        W["cnt"] += 1
        return W["PI"][W["cnt"] % 2]
    mx, my = (0, 1) if d == 0 else (1, 0)
    for g0 in range(0, nunits, gs):
        Lv = LLf[:, 2 * g0:2 * g0 + 2 * gs:2, :]
        LTv = LLf[:, 2 * g0 + 1:2 * g0 + 2 * gs:2, :]

        class V_:
            def __init__(s_, t): s_.t = t; s_.b = t.b
            def __getitem__(s_, k):
                if k == slice(None): return s_.t[:, 0:gs, :]
                return s_.t[k]
        X, Xn, Y, Yn = (V_(W[k_]) for k_ in ("Xa", "Xb", "Ya", "Yb"))
        LsX, LsY, M1, M2, Mt, R = (V_(W[k_]) for k_ in ("LsX", "LsY", "M1", "M2", "Mt", "R"))
        bc = lambda m, lv: nm[:, m, lv, :].unsqueeze(1).to_broadcast([128, gs, 128])
        K.tt(LsX[:], Lv, bc(mx, 0), ALU.mult, [LL, nm], [LsX], eng="pool")
        K.tt(X[:], LsX[:], idb, ALU.add, [LsX, W["ident_b"]], [X], eng="pool")
        K.tt(LsY[:], LTv, bc(my, 0), ALU.mult, [LL, nm], [LsY], eng="pool")
        K.tt(Y[:], LsY[:], idb, ALU.add, [LsY, W["ident_b"]], [Y], eng="pool")
        K.tt(Mt[:], Lv, idb, ALU.add, [LL, W["ident_b"]], [Mt], eng="pool")
        for lv in range(1, 7):
            K.tt(LsX[:], Lv, bc(mx, lv), ALU.mult, [LL, nm], [LsX], eng="pool")
            K.tt(LsY[:], LTv, bc(my, lv), ALU.mult, [LL, nm], [LsY], eng="pool")
            Q = pi()
            for u in range(gs):
                K.mm(Q[:, u, :], LsY[:, u, :], X[:, u, :], True, True, [LsY, X], [Q])
            K.act(M1[:], Q[:, 0:gs, :], AF.Copy, [Q], [M1])
            Q = pi()
            for u in range(gs):
                K.mm(Q[:, u, :], LsX[:, u, :], Y[:, u, :], True, True, [LsX, Y], [Q])
            K.act(M2[:], Q[:, 0:gs, :], AF.Copy, [Q], [M2])
            Q = pi()
            for u in range(gs):
                K.mm(Q[:, u, :], Y[:, u, :], M1[:, u, :], True, True, [Y, M1], [Q])
            K.tt(Xn[:], X[:], Q[:, 0:gs, :], ALU.add, [X, Q], [Xn])
            Q = pi()
            for u in range(gs):
                K.mm(Q[:, u, :], X[:, u, :], M2[:, u, :], True, True, [X, M2], [Q])
            K.tt(Yn[:], Y[:], Q[:, 0:gs, :], ALU.add, [Y, Q], [Yn])
            X, Xn, Y, Yn = Xn, X, Yn, Y
        Q = pi()
        for u in range(gs):
            K.mm(Q[:, u, :], Mt[:, u, :], Y[:, u, :], True, True, [Mt, Y], [Q])
        K.stt(R[:], Q[:, 0:gs, :], -1.0, idb, ALU.mult, ALU.add, [Q, W["ident_b"]], [R])
        Q = pi()
        for u in range(gs):
            K.mm(Q[:, u, :], X[:, u, :], R[:, u, :], True, True, [X, R], [Q])
        K.tt(XTb[:, g0:g0 + gs, :], Y[:], Q[:, 0:gs, :], ALU.add, [Y, Q], [XTb])


SEGS = [(0, 2)] + [(2 + 4 * i, 4) for i in range(4)]


def rwkv_phase(K, st, C):
    XS, smallT, oaT, ident_b, ident_f = C["XS"], C["smallT"], C["oaT"], C["ident_b"], C["ident_f"]
    dbg = C["dbg"]
    sb = lambda shape, dt, name=None: K.sb(st, shape, dt, name)
    w0 = sb([128, 2, 8], F32); a0 = sb([128, 2, 8], F32)
    kkw = sb([128, 8], F32); ka = sb([128, 8], F32); omka = sb([128, 8], F32); rk = sb([128, 8], F32)
    for t_, d_ in ((w0, C["w0"]), (a0, C["a0"])):
        K.dma("sp", t_[:], d_[:, :, :], [], [t_])
    for t_, d_ in ((kkw, C["kkw"]), (ka, C["ka"]), (rk, C["rk"])):
        K.dma("sp", t_[:], d_[:, :], [], [t_])
    K.ts(omka[:], ka[:], -1.0, 1.0, ALU.mult, ALU.add, [ka], [omka])
    w2b = sb([128, 2, 1024], BF16); a2b = sb([128, 2, 1024], BF16); g2b = sb([64, 1024], BF16)
    K.memset(w2b[:], 0.0, [w2b])
    K.memset(a2b[:], 0.0, [a2b])
    K.dma("pool", w2b[0:96, :, :], C["w2"][:, :, :].rearrange("d r c -> r d c"), [], [w2b])
    K.dma("pool", a2b[0:96, :, :], C["a2"][:, :, :].rearrange("d r c -> r d c"), [], [a2b])
    K.dma("pool", g2b[:], C["g2"][:, :], [], [g2b])
    lnw = sb([128, 1024], F32); lnb = sb([128, 1024], F32)
    K.dma("sp", lnw[:], C["lnw"][0:1, :].to_broadcast([128, 1024]), [], [lnw])
    K.dma("sp", lnb[:], C["lnb"][0:1, :].to_broadcast([128, 1024]), [], [lnb])
    m1f = sb([128, 2, 4, 128], F32); m2f = sb([128, 2, 3, 128], F32)
    K.dma("sp", m1f[:], C["m1"][:, :, :, :], [], [m1f])
    K.dma("sp", m2f[:], C["m2"][:, :, :, :], [], [m2f])
    rmask = sb([128, 512], F32); bones = sb([128, 128], F32); hsel = sb([128, 2], F32)
    K.dma("sp", rmask[:], C["rmask"][:, :], [], [rmask])
    K.dma("sp", bones[:], C["bones"][:, :], [], [bones])
    K.dma("sp", hsel[:], C["hsel"][:, :], [], [hsel])
    f32t = lambda nm=None: sb([128, 512], F32, nm)
    bft = lambda nm=None: sb([128, 512], BF16, nm)
    Xr, Xk, Xv = f32t("Xr"), f32t("Xk"), f32t("Xv")
    sig, A, B, Cc, Dd = f32t("sig"), f32t("A"), f32t("B"), f32t("Cc"), f32t("Dd")
    e1, e2, e3, e4 = f32t("e1"), f32t("e2"), f32t("e3"), f32t("e4")
    icl, icl0, kq, sq, rn, kkt, kd, bd, tmp = (f32t(nm) for nm in ("icl", "icl0", "kq", "sq", "rn", "kkt", "kd", "bd", "tmp"))
    gam = sb([128, 4], F32, "gam")
    rt, at, kt, bt, KH, BH, vb, rkr = (bft(nm) for nm in ("rt", "at", "kt", "bt", "KH", "BH", "vb", "rkr"))
    KHt = sb([128, 4, 128], BF16, "KHt"); BHnt = sb([128, 4, 128], BF16, "BHnt"); Vt = sb([128, 4, 128], BF16, "Vt")
    LL = sb([128, 4, 4, 128], BF16, "LL")
    AA = sb([128, 4, 2, 3, 128], BF16, "AA")
    XTb = sb([128, 8, 128], BF16, "XTb")
    IW = inverse_workspace(K, st, C)
    Hf = sb([128, 64], F32, "Hf"); Hb = sb([128, 64], BF16, "Hb")
    P1s = sb([128, 128], BF16, "P1s"); Us = sb([128, 128], BF16, "Us")
    ybuf = sb([128, 16, 128], F32, "ybuf")
    ytot = sb([128, 4, 128], F32, "ytot"); yc = sb([128, 4, 128], F32, "yc"); ysq = sb([128, 4, 128], F32)
    mean = sb([128, 8], F32); var = sb([128, 8], F32)
    bsum = sb([128, 4, 2], F32)
    gate = sb([128, 4, 128], F32)
    oat = sb([128, 4, 128], BF16)
    PF = [K.ps(st, [128, 512], F32) for _ in range(1)]
    PTr = K.ps(st, [128, 8, 128], BF16)
    PG = [K.ps(st, [128, 4, 128], F32) for _ in range(2)]
    PSq = K.ps(st, [128, 512], F32)
    PSh = K.ps(st, [128, 512], F32)
    PS_P1, PS_U, PS_Y, PS_H = PSq, PSq, PSq, PSh
    cnt = {"pf": 0, "pg": 0, "pi": 0, "tr": 0}

    def nxt(lst, key):
        cnt[key] += 1
        return lst[cnt[key] % len(lst)]

    def transp(src, dst, n, scale=None):
        half = cnt["tr"] % 2
        cnt["tr"] += 1
        for j in range(n):
            K.tr(PTr[:, half * 4 + j, :], src[:, j * 128:(j + 1) * 128], ident_b[:], [src, ident_b], [PTr])
        if scale is None:
            K.copy(dst[:, :n, :], PTr[:, half * 4:half * 4 + n, :], [PTr], [dst])
        else:
            K.act(dst[:, :n, :], PTr[:, half * 4:half * 4 + n, :], AF.Copy, [PTr], [dst], scale=scale)

    for hp in range(C["nhp"]):
        hc = slice(hp * 128, (hp + 1) * 128)
        for d in range(2):
            K.memset(Hf[:], 0.0, [Hf])
            K.memset(Hb[:], 0.0, [Hb])
            order = SEGS if d == 0 else [SEGS[0], SEGS[4], SEGS[3], SEGS[2], SEGS[1]]
            for (c0, n) in order:
                t0, N = c0 * 128, n * 128
                latent = c0 >= 2
                tk = slice(t0, t0 + N)
                for X_, row in ((Xr, 0), (Xk, 1024), (Xv, 2048)):
                    K.dma("sp", X_[:, :N], XS[row + hp * 128:row + hp * 128 + 128, tk], [XS], [X_])
                P = nxt(PF, "pf")
                K.mm(P[:, :N], w2b[:, d, hc], smallT[:, d, tk], True, True, [w2b, smallT], [P])
                K.act(sig[:, :N], P[:, :N], AF.Sigmoid, [P, w0], [sig], bias=w0[:, d, hp:hp + 1])
                K.scan(A[:, :N], rmask[:, :N], sig[:, :N], 0.0, ALU.mult, ALU.add, [rmask, sig], [A])
                K.tt(B[:, :N], A[:, :N], sig[:, :N], ALU.subtract, [A, sig], [B], eng="pool")
                v3 = lambda t_: t_[:, :N].rearrange("p (c t) -> p c t", t=128)
                tot = v3(A)[:, :, 127:128]
                K.tt(v3(Cc), tot.to_broadcast([128, n, 128]), v3(A), ALU.subtract, [A], [Cc])
                K.tt(Dd[:, :N], Cc[:, :N], sig[:, :N], ALU.add, [Cc, sig], [Dd], eng="pool")
                Gi, Gx, Gt = (A, B, Cc) if d == 0 else (Dd, Cc, B)
                K.act(e1[:, :N], Gi[:, :N], AF.Exp, [Gi], [e1], scale=-DEC)
                K.act(e2[:, :N], Gx[:, :N], AF.Exp, [Gx], [e2], scale=-DEC)
                K.act(e3[:, :N], Gi[:, :N], AF.Exp, [Gi], [e3], scale=DEC)
                K.act(e4[:, :N], Gt[:, :N], AF.Exp, [Gt], [e4], scale=-DEC)
                K.act(gam[:, :n], v3(A)[:, :, 127], AF.Exp, [A], [gam], scale=-DEC)
                P = nxt(PF, "pf")
                K.mm(P[:, :N], a2b[:, d, hc], smallT[:, 2 + d, tk], True, True, [a2b, smallT], [P])
                K.act(icl[:, :N], P[:, :N], AF.Sigmoid, [P, a0], [icl], bias=a0[:, d, hp:hp + 1])
                if d == 1 and latent:
                    P = nxt(PF, "pf")
                    K.mm(P[:, :N], a2b[:, 0, hc], smallT[:, 2, tk], True, True, [a2b, smallT], [P])
                    K.act(icl0[:, :N], P[:, :N], AF.Sigmoid, [P, a0], [icl0], bias=a0[:, 0, hp:hp + 1])
                K.ts(kq[:, :N], Xk[:, :N], kkw[:, hp:hp + 1], None, ALU.mult, None, [Xk, kkw], [kq], eng="pool")
                K.act(sq[:, :N], kq[:, :N], AF.Square, [kq], [sq])
                P = nxt(PF, "pf")
                K.mm(P[:, :N], bones[:], sq[:, :N], True, True, [bones, sq], [P])
                K.act(rn[:, :N], P[:, :N], AF.Sqrt, [P], [rn], bias=EPS)
                K.recip(rn[:, :N], rn[:, :N], [rn], [rn])
                K.tt(kkt[:, :N], kq[:, :N], rn[:, :N], ALU.mult, [kq, rn], [kkt])
                K.ts(tmp[:, :N], icl[:, :N], ka[:, hp:hp + 1], omka[:, hp:hp + 1], ALU.mult, ALU.add, [icl, ka, omka], [tmp])
                K.tt(kd[:, :N], tmp[:, :N], Xk[:, :N], ALU.mult, [tmp, Xk], [kd])
                K.tt(bd[:, :N], kkt[:, :N], icl[:, :N], ALU.mult, [kkt, icl], [bd], eng="pool")
                K.tt(rt[:, :N], Xr[:, :N], e1[:, :N], ALU.mult, [Xr, e1], [rt])
                K.tt(at[:, :N], kkt[:, :N], e2[:, :N], ALU.mult, [kkt, e2], [at], eng="pool")
                K.tt(kt[:, :N], kd[:, :N], e3[:, :N], ALU.mult, [kd, e3], [kt])
                K.tt(bt[:, :N], bd[:, :N], e3[:, :N], ALU.mult, [bd, e3], [bt], eng="pool")
                K.tt(KH[:, :N], kd[:, :N], e4[:, :N], ALU.mult, [kd, e4], [KH])
                K.tt(BH[:, :N], bd[:, :N], e4[:, :N], ALU.mult, [bd, e4], [BH], eng="pool")
                K.act(vb[:, :N], Xv[:, :N], AF.Copy, [Xv], [vb])
                transp(KH, KHt, n)
                transp(BH, BHnt, n, scale=-1.0)
                transp(vb, Vt, n)
                for j in range(n):
                    cs = slice(j * 128, (j + 1) * 128)
                    G = nxt(PG, "pg")
                    for e in range(2):
                        ps_ = slice(64 * e, 64 * e + 64)
                        K.mm(G[:, 2 * e, :], at[ps_, cs], bt[ps_, cs], True, True, [at, bt], [G])
                        K.mm(G[:, 2 * e + 1, :], bt[ps_, cs], at[ps_, cs], True, True, [at, bt], [G])
                    K.tt(LL[:, j, :, :], G[:], m1f[:, d, :, :], ALU.mult, [G, m1f], [LL])
                    for e in range(2):
                        ps_ = slice(64 * e, 64 * e + 64)
                        G = nxt(PG, "pg")
                        K.mm(G[:, 0, :], kt[ps_, cs], at[ps_, cs], True, True, [kt, at], [G])
                        K.mm(G[:, 1, :], kt[ps_, cs], rt[ps_, cs], True, True, [kt, rt], [G])
                        K.mm(G[:, 2, :], bt[ps_, cs], rt[ps_, cs], True, True, [bt, rt], [G])
                        K.tt(AA[:, j, e, :, :], G[:, 0:3, :], m2f[:, d, :, :], ALU.mult, [G, m2f], [AA])
                inverse_units(K, C, LL, n, d, XTb, IW)
                jl = list(range(n)) if d == 0 else list(range(n - 1, -1, -1))
                for j in jl:
                    cs = slice(j * 128, (j + 1) * 128)
                    for e in range(2):
                        ps_ = slice(64 * e, 64 * e + 64)
                        vs = slice(64 * e, 64 * e + 64)
                        K.mm(PS_P1[:, 0 + 64 * e:64 + 64 * e], at[ps_, cs], Hb[ps_, :], True, False, [at, Hb], [PS_P1])
                        K.mm(PS_P1[:, 0 + 64 * e:64 + 64 * e], AA[:, j, e, 0, :], Vt[:, j, vs], False, True, [AA, Vt], [PS_P1])
                    K.act(P1s[:], PS_P1[:, 0:128], AF.Copy, [PS_P1], [P1s])
                    for e in range(2):
                        vs = slice(64 * e, 64 * e + 64)
                        K.mm(PS_U[:, 128 + 64 * e:192 + 64 * e], XTb[:, 2 * j + e, :], P1s[:, vs], True, True, [XTb, P1s], [PS_U])
                    K.copy(Us[:], PS_U[:, 128:256], [PS_U], [Us])
                    if latent:
                        for e in range(2):
                            ps_ = slice(64 * e, 64 * e + 64)
                            vs = slice(64 * e, 64 * e + 64)
                            yo = PS_Y[:, 256 + 64 * e:320 + 64 * e]
                            K.mm(yo, rt[ps_, cs], Hb[ps_, :], True, False, [rt, Hb], [PS_Y])
                            K.mm(yo, AA[:, j, e, 1, :], Vt[:, j, vs], False, False, [AA, Vt], [PS_Y])
                            K.mm(yo, AA[:, j, e, 2, :], Us[:, vs], False, True, [AA, Us], [PS_Y])
                    K.mm(PS_H[:, 384:512], KHt[:, j, :], Vt[:, j, :], True, False, [KHt, Vt], [PS_H])
                    K.mm(PS_H[:, 384:512], BHnt[:, j, :], Us[:], False, True, [BHnt, Us], [PS_H])
                    gcol = gam[:, j:j + 1]
                    for e in range(2):
                        ps_ = slice(64 * e, 64 * e + 64)
                        K.stt(Hf[ps_, :], Hf[ps_, :], gam[ps_, j:j + 1], PS_H[ps_, 384 + 64 * e:448 + 64 * e], ALU.mult, ALU.add,
                              [Hf, gam, PS_H], [Hf])
                    K.act(Hb[:], Hf[:], AF.Copy, [Hf], [Hb])
                    if latent:
                        cg = c0 - 2 + j
                        if d == 0:
                            K.act(ybuf[:, cg, :], PS_Y[:, 256:384], AF.Copy, [PS_Y], [ybuf])
                        else:
                            K.tt(ytot[:, j, :], ybuf[:, cg, :], PS_Y[:, 256:384], ALU.add, [ybuf, PS_Y], [ytot])
                if d == 1 and latent:
                    if dbg and hp < 8:
                        K.dma("sp", dbg["yf"][hp, :, c0 - 2:c0 - 2 + n, :], ytot[:], [ytot], [])
                    yv = ytot[:].rearrange("p j (e c) -> p (j e) c", c=64)
                    ycv = yc[:].rearrange("p j (e c) -> p (j e) c", c=64)
                    sqv = ysq[:].rearrange("p j (e c) -> p (j e) c", c=64)
                    K.S.add("dve", lambda e_: e_.tensor_reduce(out=mean[:], in_=yv, axis=AX.X, op=ALU.add), _bufs([ytot]), _bufs([mean]))
                    K.ts(mean[:], mean[:], 1.0 / 64, None, ALU.mult, None, [mean], [mean])
                    K.tt(ycv, yv, mean[:].unsqueeze(2).to_broadcast([128, 8, 64]), ALU.subtract, [ytot, mean], [yc])
                    K.tt(sqv, ycv, ycv, ALU.mult, [yc], [ysq], eng="pool")
                    K.S.add("dve", lambda e_: e_.tensor_reduce(out=var[:], in_=sqv, axis=AX.X, op=ALU.add), _bufs([ysq]), _bufs([var]))
                    K.act(var[:], var[:], AF.Sqrt, [var], [var], scale=1.0 / 64, bias=64e-5)
                    K.recip(var[:], var[:], [var], [var])
                    K.tt(ycv, ycv, var[:].unsqueeze(2).to_broadcast([128, 8, 64]), ALU.mult, [yc, var], [yc])
                    K.tt(yc[:], yc[:], lnw[:, hc].unsqueeze(1).to_broadcast([128, 4, 128]), ALU.mult, [yc, lnw], [yc])
                    K.tt(yc[:], yc[:], lnb[:, hc].unsqueeze(1).to_broadcast([128, 4, 128]), ALU.add, [yc, lnb], [yc])
                    K.tt(tmp[:, :N], icl[:, :N], icl0[:, :N], ALU.add, [icl, icl0], [tmp])
                    K.ts(tmp[:, :N], tmp[:, :N], 0.5, None, ALU.mult, None, [tmp], [tmp])
                    K.ts(tmp[:, :N], tmp[:, :N], ka[:, hp:hp + 1], omka[:, hp:hp + 1], ALU.mult, ALU.add, [tmp, ka, omka], [tmp])
                    K.tt(tmp[:, :N], tmp[:, :N], Xk[:, :N], ALU.mult, [tmp, Xk], [tmp])
                    K.stt(sq[:, :N], tmp[:, :N], rk[:, hp:hp + 1], Xr[:, :N], ALU.mult, ALU.mult, [tmp, rk, Xr], [sq])
                    P = nxt(PF, "pf")
                    for j in range(n):
                        K.mm(P[:, 2 * j:2 * j + 2], sq[:, j * 128:(j + 1) * 128], hsel[:], True, True, [sq, hsel], [P])
                    K.copy(bsum[:].rearrange("p j e -> p (j e)"), P[:, 0:2 * n], [P], [bsum])
                    K.copy(ysq[:], Vt[:], [Vt], [ysq], eng="pool")
                    K.tt(sqv, sqv, bsum[:].rearrange("p j e -> p (j e)").unsqueeze(2).to_broadcast([128, 8, 64]), ALU.mult,
                         [ysq, bsum], [ysq])
                    K.tt(yc[:], yc[:], ysq[:], ALU.add, [yc, ysq], [yc])
                    P = nxt(PF, "pf")
                    for j in range(n):
                        K.mm(P[:, j * 128:(j + 1) * 128], smallT[0:64, 4, t0 + j * 128:t0 + (j + 1) * 128], g2b[0:64, hc], True, True,
                             [smallT, g2b], [P])
                    K.tt(oat[:], yc[:], P[:].rearrange("p (j c) -> p j c", c=128), ALU.mult, [yc, P], [oat])
                    half = cnt["tr"] % 2
                    cnt["tr"] += 1
                    for j in range(n):
                        K.tr(PTr[:, half * 4 + j, :], oat[:, j, :], ident_b[:], [oat, ident_b], [PTr])
                    K.copy(oaT[:, hp, t0 - 256:t0 - 256 + N].rearrange("p (j t) -> p j t", t=128), PTr[:, half * 4:half * 4 + n, :],
                           [PTr], [oaT])


def gdn_phase(K, st, C):
    US, SZ, abT, obT, ident_b, ident_f = C["US"], C["SZ"], C["abT"], C["obT"], C["ident_b"], C["ident_f"]
    sb = lambda shape, dt, name=None: K.sb(st, shape, dt, name)
    selg = sb([64, 16, 128], F32); selb = sb([64, 16, 128], F32)
    K.dma("sp", selg[:], C["selg"][:, :, :], [], [selg])
    K.dma("sp", selb[:], C["selb"][:, :, :], [], [selb])
    bigm = sb([128, 2, 2, 128], F32); offd = sb([128, 128], F32); ones = sb([128, 128], F32)
    K.dma("sp", bigm[:], C["bigm"][:, :, :, :], [], [bigm])
    K.dma("sp", offd[:], C["offd"][:, :], [], [offd])
    K.dma("sp", ones[:], C["ones"][:, :], [], [ones])
    rmask = sb([128, 512], F32)
    K.dma("sp", rmask[:], C["rmask"][:, :], [], [rmask])
    alog = sb([64, 1], F32); dtb = sb([64, 1], F32); nea = sb([64, 1], F32)
    K.dma("sp", alog[:], C["alog"][:, :], [], [alog])
    K.dma("sp", dtb[:], C["dtb"][:, :], [], [dtb])
    gnw = sb([128, 128], F32)
    K.dma("sp", gnw[:], C["gnw"][0:1, :].to_broadcast([128, 128]), [], [gnw])
    K.act(nea[:], alog[:], AF.Exp, [alog], [nea])
    K.ts(nea[:], nea[:], -1.0, None, ALU.mult, None, [nea], [nea])
    GB = [sb([64, TT], F32, "GB%d" % d) for d in range(2)]
    tokT = [sb([128, NCH, 64], F32, "tokT%d" % d) for d in range(2)]
    with ExitStack() as s0:
        gt = K.sb(s0, [16, TT], F32); A = K.sb(s0, [16, TT], F32); Bx = K.sb(s0, [16, TT], F32)
        K.act(gt[:], abT[0:16, :], AF.Exp, [abT, dtb], [gt], bias=dtb[0:16, :])
        K.act(gt[:], gt[:], AF.Ln, [gt], [gt], bias=1.0)
        K.ts(gt[:], gt[:], nea[0:16, :], None, ALU.mult, None, [gt, nea], [gt])
        for d in range(2):
            K.memset(GB[d][:], 0.0, [GB[d]])
            K.act(GB[d][32:48, :], abT[32:48, :], AF.Sigmoid, [abT], [GB[d]])
        for t0 in range(0, TT, 512):
            N = min(512, TT - t0)
            K.scan(A[:, t0:t0 + N], rmask[0:16, :N], gt[:, t0:t0 + N], 0.0, ALU.mult, ALU.add, [rmask, gt], [A])
        K.copy(GB[0][0:16, :], A[:], [A], [GB[0]], eng="pool")
        K.tt(Bx[:], A[:], gt[:], ALU.subtract, [A, gt], [Bx])
        v3 = lambda t_: t_[:].rearrange("p (c t) -> p c t", t=128)
        tot = v3(A)[:, :, 127:128]
        K.tt(v3(GB[1])[0:16], tot.to_broadcast([16, NCH, 128]), v3(Bx), ALU.subtract, [A, Bx], [GB[1]])
        ptk = K.ps(s0, [128, 8, 64], F32)
        for d in range(2):
            for c8 in range(0, NCH, 8):
                nn = min(8, NCH - c8)
                for j in range(nn):
                    c = c8 + j
                    K.tr(ptk[:, j, :], GB[d][:, c * 128:(c + 1) * 128], ident_f[0:64, 0:64], [GB[d], ident_f], [ptk])
                K.copy(tokT[d][:, c8:c8 + nn, :], ptk[:, 0:nn, :], [ptk], [tokT[d]])
    K.S.barrier()
    f32t = lambda nm=None: sb([128, 512], F32, nm)
    bft = lambda nm=None: sb([128, 512], BF16, nm)
    Xq, Xk, Xv = f32t("gXq"), f32t("gXk"), f32t("gXv")
    sq, rn, qn, kn, bcG, eG, bcB, KB, tmp, tmp2, sz = (f32t("g_" + nm) for nm in
                                                       ("sq", "rn", "qn", "kn", "bcG", "eG", "bcB", "KB", "tmp", "tmp2", "sz"))
    knb, qnb, kbT, nKBG, Qd, Ktl, vb = (bft("g_" + nm) for nm in ("knb", "qnb", "kbT", "nKBG", "Qd", "Ktl", "vb"))
    Ktt = sb([128, 4, 128], BF16, "g_Ktt"); Vt = sb([128, 4, 128], BF16, "g_Vt")
    Dc = sb([128, 4, 2, 128], F32, "g_Dc"); DiT = sb([128, 4, 128], F32, "g_DiT"); Dtmp = sb([128, 4, 128], F32, "g_Dtmp")
    LLg = sb([128, 2, 4, 128], BF16, "g_LL")
    QKt = sb([128, 4, 128], BF16, "g_QKt")
    XTb = sb([128, 4, 128], BF16, "g_XTb")
    IW = inverse_workspace(K, st, C)
    Sf = sb([128, 128], F32, "g_Sf"); Sb = sb([128, 128], BF16, "g_Sb")
    P1s = sb([128, 128], BF16, "g_P1s"); VNs = sb([128, 128], BF16, "g_VNs")
    obuf = sb([128, 16, 128], F32, "g_obuf")
    otot = sb([128, 4, 128], F32, "g_otot"); osq = sb([128, 4, 128], F32, "g_osq")
    ss = sb([128, 4], F32); onb = sb([128, 4, 128], BF16, "g_onb")
    PF = K.ps(st, [128, 512], F32)
    PTr = K.ps(st, [128, 8, 128], BF16)
    PG = [K.ps(st, [128, 4, 128], F32) for _ in range(2)]
    PSq = K.ps(st, [128, 512], F32)
    PSh = K.ps(st, [128, 512], F32)
    cnt = {"pg": 0, "tr": 0}

    def transp(src, dst, n):
        half = cnt["tr"] % 2
        cnt["tr"] += 1
        for j in range(n):
            K.tr(PTr[:, half * 4 + j, :], src[:, j * 128:(j + 1) * 128], ident_b[:], [src, ident_b], [PTr])
        K.copy(dst[:, :n, :], PTr[:, half * 4:half * 4 + n, :], [PTr], [dst])

    for h in range(C["nh"]):
        for d in range(2):
            r = d * 8 + h
            K.memset(Sf[:], 0.0, [Sf])
            K.memset(Sb[:], 0.0, [Sb])
            order = SEGS if d == 0 else [SEGS[0], SEGS[4], SEGS[3], SEGS[2], SEGS[1]]
            for (c0, n) in order:
                t0, N = c0 * 128, n * 128
                latent = c0 >= 2
                tk = slice(t0, t0 + N)
                v3 = lambda t_: t_[:, :N].rearrange("p (c t) -> p c t", t=128)
                for X_, row in ((Xq, 0), (Xk, 1024), (Xv, 2048)):
                    K.dma("sp", X_[:, :N], US[row + h * 128:row + h * 128 + 128, tk], [US], [X_])
                for X_, o_, sc_ in ((Xq, qn, 128 ** -0.5), (Xk, kn, 1.0)):
                    K.act(sq[:, :N], X_[:, :N], AF.Square, [X_], [sq])
                    K.mm(PF[:, :N], ones[:], sq[:, :N], True, True, [ones, sq], [PF])
                    K.act(rn[:, :N], PF[:, :N], AF.Sqrt, [PF], [rn], bias=EPS)
                    K.recip(rn[:, :N], rn[:, :N], [rn], [rn])
                    K.stt(o_[:, :N], X_[:, :N], sc_, rn[:, :N], ALU.mult, ALU.mult, [X_, rn], [o_])
                K.mm(PF[:, :N], selg[:, r, :], GB[d][:, tk], True, True, [selg, GB[d]], [PF])
                K.act(bcG[:, :N], PF[:, :N], AF.Copy, [PF], [bcG])
                K.act(eG[:, :N], PF[:, :N], AF.Exp, [PF], [eG])
                K.mm(PF[:, :N], selb[:, r, :], GB[d][:, tk], True, True, [selb, GB[d]], [PF])
                K.act(bcB[:, :N], PF[:, :N], AF.Copy, [PF], [bcB])
                lastcol = 127 if d == 0 else 0
                glast = v3(bcG)[:, :, lastcol:lastcol + 1]
                K.tt(KB[:, :N], kn[:, :N], bcB[:, :N], ALU.mult, [kn, bcB], [KB])
                K.copy(kbT[:, :N], KB[:, :N], [KB], [kbT], eng="pool")
                K.copy(knb[:, :N], kn[:, :N], [kn], [knb], eng="pool")
                K.act(qnb[:, :N], qn[:, :N], AF.Copy, [qn], [qnb])
                K.stt(nKBG[:, :N], KB[:, :N], -1.0, eG[:, :N], ALU.mult, ALU.mult, [KB, eG], [nKBG])
                K.tt(Qd[:, :N], qn[:, :N], eG[:, :N], ALU.mult, [qn, eG], [Qd], eng="pool")
                K.tt(v3(tmp), glast.to_broadcast([128, n, 128]), v3(bcG), ALU.subtract, [bcG], [tmp])
                K.act(tmp[:, :N], tmp[:, :N], AF.Exp, [tmp], [tmp])
                K.tt(Ktl[:, :N], kn[:, :N], tmp[:, :N], ALU.mult, [kn, tmp], [Ktl])
                K.act(vb[:, :N], Xv[:, :N], AF.Copy, [Xv], [vb])
                transp(Ktl, Ktt, n)
                transp(vb, Vt, n)
                gct = tokT[d][:, c0:c0 + n, r:r + 1].to_broadcast([128, n, 128])
                bg3 = v3(bcG)
                K.tt(Dtmp[:, :n, :], bg3, bigm[:, d, 0, :].unsqueeze(1).to_broadcast([128, n, 128]), ALU.add, [bcG, bigm], [Dtmp])
                K.tt(Dtmp[:, :n, :], Dtmp[:, :n, :], gct, ALU.subtract, [Dtmp, tokT[d]], [Dtmp], eng="pool")
                K.act(Dtmp[:, :n, :], Dtmp[:, :n, :], AF.Exp, [Dtmp], [Dtmp], scale=-1.0)
                K.tt(Dc[:, :n, 0, :], Dtmp[:, :n, :], offd[:].unsqueeze(1).to_broadcast([128, n, 128]), ALU.mult, [Dtmp, offd], [Dc],
                     eng="pool")
                K.tt(DiT[:, :n, :], bg3, bigm[:, d, 1, :].unsqueeze(1).to_broadcast([128, n, 128]), ALU.add, [bcG, bigm], [DiT])
                K.tt(DiT[:, :n, :], DiT[:, :n, :], gct, ALU.subtract, [DiT, tokT[d]], [DiT], eng="pool")
                K.act(DiT[:, :n, :], DiT[:, :n, :], AF.Exp, [DiT], [DiT])
                K.tt(Dc[:, :n, 1, :], DiT[:, :n, :], offd[:].unsqueeze(1).to_broadcast([128, n, 128]), ALU.mult, [DiT, offd], [Dc],
                     eng="pool")
                LLv = LLg[:].rearrange("p a b t -> p (a b) t")
                for j in range(n):
                    cs = slice(j * 128, (j + 1) * 128)
                    cnt["pg"] += 1
                    G = PG[cnt["pg"] % 2]
                    K.mm(G[:, 0, :], kbT[:, cs], knb[:, cs], True, True, [kbT, knb], [G])
                    K.mm(G[:, 1, :], knb[:, cs], kbT[:, cs], True, True, [kbT, knb], [G])
                    K.mm(G[:, 2, :], knb[:, cs], qnb[:, cs], True, True, [qnb, knb], [G])
                    K.tt(LLv[:, 2 * j:2 * j + 2, :], G[:, 0:2, :], Dc[:, j, :, :], ALU.mult, [G, Dc], [LLg])
                    K.tt(QKt[:, j, :], G[:, 2, :], DiT[:, j, :], ALU.mult, [G, DiT], [QKt])
                inverse_units(K, C, LLg, n, d, XTb, IW, nunits=n)
                jl = list(range(n)) if d == 0 else list(range(n - 1, -1, -1))
                for j in jl:
                    cs = slice(j * 128, (j + 1) * 128)
                    c = c0 + j
                    K.mm(PSq[:, 0:128], nKBG[:, cs], Sb[:], True, True, [nKBG, Sb], [PSq])
                    K.stt(P1s[:], Vt[:, j, :], tokT[d][:, c, 32 + r:33 + r], PSq[:, 0:128], ALU.mult, ALU.add,
                          [Vt, tokT[d], PSq], [P1s])
                    K.mm(PSq[:, 128:256], XTb[:, j, :], P1s[:], True, True, [XTb, P1s], [PSq])
                    K.act(VNs[:], PSq[:, 128:256], AF.Copy, [PSq], [VNs])
                    if latent:
                        K.mm(PSq[:, 256:384], Qd[:, cs], Sb[:], True, False, [Qd, Sb], [PSq])
                        K.mm(PSq[:, 256:384], QKt[:, j, :], VNs[:], False, True, [QKt, VNs], [PSq])
                    K.mm(PSh[:, 0:128], Ktt[:, j, :], VNs[:], True, True, [Ktt, VNs], [PSh])
                    gl = eG[:, j * 128 + lastcol:j * 128 + lastcol + 1]
                    K.stt(Sf[:], Sf[:], gl, PSh[:, 0:128], ALU.mult, ALU.add, [Sf, eG, PSh], [Sf])
                    K.act(Sb[:], Sf[:], AF.Copy, [Sf], [Sb])
                    if latent:
                        cg = c - 2
                        if d == 0:
                            K.act(obuf[:, cg, :], PSq[:, 256:384], AF.Copy, [PSq], [obuf])
                        else:
                            K.tt(otot[:, j, :], obuf[:, cg, :], PSq[:, 256:384], ALU.add, [obuf, PSq], [otot])
                if d == 1 and latent:
                    if C["dbg"]:
                        K.dma("sp", C["dbg"]["of"][h, :, c0 - 2:c0 - 2 + n, :], otot[:], [otot], [])
                    K.tt(osq[:], otot[:], otot[:], ALU.mult, [otot], [osq], eng="pool")
                    K.S.add("dve", lambda e_: e_.tensor_reduce(out=ss[:], in_=osq[:], axis=AX.X, op=ALU.add), _bufs([osq]), _bufs([ss]))
                    K.act(ss[:], ss[:], AF.Sqrt, [ss], [ss], scale=1.0 / 128, bias=EPS)
                    K.recip(ss[:], ss[:], [ss], [ss])
                    K.tt(osq[:], otot[:], ss[:].unsqueeze(2).to_broadcast([128, 4, 128]), ALU.mult, [otot, ss], [osq])
                    K.tt(onb[:], osq[:], gnw[:].unsqueeze(1).to_broadcast([128, 4, 128]), ALU.mult, [osq, gnw], [onb])
                    K.dma("sp", sz[:, :N], SZ[h * 128:(h + 1) * 128, t0 - 256:t0 - 256 + N], [SZ], [sz])
                    half = cnt["tr"] % 2
                    cnt["tr"] += 1
                    for j in range(n):
                        K.tr(PTr[:, half * 4 + j, :], onb[:, j, :], ident_b[:], [onb, ident_b], [PTr])
                    K.tt(obT[:, h, t0 - 256:t0 - 256 + N].rearrange("p (j t) -> p j t", t=128), PTr[:, half * 4:half * 4 + n, :],
                         sz[:, :N].rearrange("p (j t) -> p j t", t=128), ALU.mult, [PTr, sz], [obT])


def write_mods(K, C):
    modT, ident_f, MODS = C["modT"], C["ident_f"], C["MODS"]
    with ExitStack() as s0:
        pt = K.ps(s0, [16, 2, 128], F32)
        rows = K.sb(s0, [16, 2, 128], F32)
        for i, sec in enumerate((2, 5)):
            K.tr(pt[:, i, :], modT[:, sec * 16:(sec + 1) * 16, 0], ident_f[:], [modT, ident_f], [pt])
        K.copy(rows[:], pt[:], [pt], [rows])
        for i in range(2):
            K.dma("sp", MODS[i * 16:(i + 1) * 16, :], rows[:, i, :], [rows], [MODS])
    K.S.barrier()


def merge_phase(K, top, C):
    oaT, obT, ident_b, ident_f, modT, s2 = C["oaT"], C["obT"], C["ident_b"], C["ident_f"], C["modT"], C["s2"]
    SG, X1, x_d, out_d = C["SG"], C["X1"], C["x"], C["out"]
    MODS = C["MODS"]
    dbg = C["dbg"]
    bc = K.sb(top, [128, 1, D], F32, "bc_rows")
    K.dma("sp", bc[:, 0, :], MODS[0:16, :].rearrange("(o a) b -> o (a b)", o=1).to_broadcast([128, D]), [MODS], [bc])
    K.S.barrier()
    p3 = ExitStack()
    mT = K.sb(p3, [128, 16, TL], BF16, "mT")
    with ExitStack() as s1:
        wa = [K.sb(s1, [128, 8, 256], BF16) for _ in range(2)]
        wbb = [K.sb(s1, [128, 8, 256], BF16) for _ in range(2)]
        sga = [K.sb(s1, [128, TL], BF16)] * 2
        sgb = [K.sb(s1, [128, TL], BF16)] * 2
        t1 = [K.sb(s1, [128, 512], F32) for _ in range(2)]
        t2 = [K.sb(s1, [128, 512], F32) for _ in range(2)]
        pa = [K.ps(s1, [128, 512], F32) for _ in range(2)]
        pb = [K.ps(s1, [128, 512], F32) for _ in range(2)]
        it = 0
        for sc in range(8):
            K.dma("pool", wa[sc % 2][:], C["p_a"][:, sc * 256:(sc + 1) * 256].rearrange("(k p) n -> p k n", p=128), [], [wa[sc % 2]])
            K.dma("pool", wbb[sc % 2][:], C["p_b"][:, sc * 256:(sc + 1) * 256].rearrange("(k p) n -> p k n", p=128), [], [wbb[sc % 2]])
            for ff in range(2):
                f = sc * 2 + ff
                ga, gb = sga[f % 2], sgb[f % 2]
                K.dma("sp", ga[:], SG[f * 128:(f + 1) * 128, :], [SG], [ga])
                K.dma("sp", gb[:], SG[2048 + f * 128:2048 + (f + 1) * 128, :], [SG], [gb])
                for n in range(4):
                    ts_ = slice(n * 512, (n + 1) * 512)
                    A_, B_, T1, T2 = pa[it % 2], pb[it % 2], t1[it % 2], t2[it % 2]
                    it += 1
                    for k in range(8):
                        K.mm(A_[:], wa[sc % 2][:, k, ff * 128:(ff + 1) * 128], oaT[:, k, ts_], k == 0, k == 7, [wa[sc % 2], oaT], [A_])
                    for k in range(8):
                        K.mm(B_[:], wbb[sc % 2][:, k, ff * 128:(ff + 1) * 128], obT[:, k, ts_], k == 0, k == 7, [wbb[sc % 2], obT], [B_])
                    K.tt(T1[:], A_[:], ga[:, ts_], ALU.mult, [A_, ga], [T1])
                    K.tt(T2[:], B_[:], gb[:, ts_], ALU.mult, [B_, gb], [T2])
                    K.tt(mT[:, f, ts_], T1[:], T2[:], ALU.add, [T1, T2], [mT], eng="pool")
    K.S.barrier()
    with ExitStack() as s2_:
        wo = [[K.sb(s2_, [128, 8, 512], BF16) for _ in range(2)] for _ in range(2)]
        xt = [K.sb(s2_, [128, 512], F32) for _ in range(3)]
        tt_ = [K.sb(s2_, [128, 512], F32) for _ in range(3)]
        pp = [K.ps(s2_, [128, 512], F32) for _ in range(4)]
        it = 0
        for n in range(4):
            ns = slice(n * 512, (n + 1) * 512)
            w = wo[n % 2]
            for kh in range(2):
                K.dma("pool", w[kh][:], C["w_out"][kh * 1024:(kh + 1) * 1024, ns].rearrange("(k p) n -> p k n", p=128), [], [w[kh]])
            for t in range(16):
                P, X_, T_ = pp[it % 4], xt[it % 3], tt_[it % 3]
                it += 1
                K.dma("sp", X_[:], x_d[t * 128:(t + 1) * 128, ns], [], [X_])
                for k in range(16):
                    K.mm(P[:], mT[:, k, t * 128:(t + 1) * 128], w[k // 8][:, k % 8, :], k == 0, k == 15, [mT, w[k // 8]], [P])
                K.tt(T_[:], P[:], bc[:, 0, ns], ALU.mult, [P, bc], [T_])
                K.tt(T_[:], T_[:], X_[:], ALU.add, [T_, X_], [T_], eng="pool")
                K.dma("sp", X1[t * 128:(t + 1) * 128, ns], T_[:], [T_], [X1])
    p3.close()


def ffn_phase(K, top, C):
    ident_b, ident_f, modT, s2 = C["ident_b"], C["ident_f"], C["modT"], C["s2"]
    X1, out_d, MODS = C["X1"], C["out"], C["MODS"]
    G = 512
    with ExitStack() as s4:
        bc = K.sb(s4, [128, 3, D], F32, "bc_rows4")
        K.dma("sp", bc[:, 1, :], MODS[16:32, :].rearrange("(o a) b -> o (a b)", o=1).to_broadcast([128, D]), [MODS], [bc])
        K.dma("sp", bc[:, 2, :], C["fnw"][0:1, :].to_broadcast([128, D]), [], [bc])
        h2T = K.sb(s4, [128, 16, G], BF16, "h2T")
        actT = K.sb(s4, [128, 44, G], BF16, "actT")
        x1t = [K.sb(s4, [128, D], F32, "x1t%d" % i) for i in range(4)]
        xb = K.sb(s4, [128, D], BF16)
        junk = K.sb(s4, [128, D], BF16)
        ss = K.sb(s4, [128, 1], F32); rs = K.sb(s4, [128, 1], F32)
        wg = [[K.sb(s4, [128, 8, 256], BF16) for _ in range(2)] for _ in range(2)]
        wu = [[K.sb(s4, [128, 8, 256], BF16) for _ in range(2)] for _ in range(2)]
        wd = [K.sb(s4, [128, 4, 512], BF16) for _ in range(4)]
        sgt = [K.sb(s4, [128, G], F32) for _ in range(2)]
        tq = [K.sb(s4, [128, 512], F32) for _ in range(2)]
        ot = [K.sb(s4, [128, D], F32) for _ in range(2)]
        ptr = [K.ps(s4, [128, 8, 128], BF16) for _ in range(1)]
        pgu = [K.ps(s4, [128, 512], F32) for _ in range(3)]
        pdn = [K.ps(s4, [128, 512], F32) for _ in range(4)]
        igu = 0
        for grp in range(NGRP):
            for t in range(4):
                X_ = x1t[t]
                row0 = grp * G + t * 128
                K.dma("sp", X_[:], X1[row0:row0 + 128, :], [X1], [X_])
                K.act(junk[:], X_[:], AF.Square, [X_], [junk, ss], accum=ss[:])
                K.act(rs[:], ss[:], AF.Sqrt, [ss], [rs], scale=1.0 / D, bias=EPS)
                K.recip(rs[:], rs[:], [rs], [rs])
                K.act(xb[:], X_[:], AF.Copy, [X_, rs], [xb], scale=rs[:])
                for g in range(4):
                    P = ptr[0]
                    for j in range(4):
                        k = g * 4 + j
                        K.tr(P[:, j, :], xb[:, k * 128:(k + 1) * 128], ident_b[:], [xb, ident_b], [P])
                    for j in range(4):
                        k = g * 4 + j
                        K.act(h2T[:, k, t * 128:(t + 1) * 128], P[:, j, :], AF.Identity, [P, s2, modT], [h2T],
                              scale=s2[:, k:k + 1], bias=modT[:, 48 + k, 0:1])
            for sc in range(22):
                w1, w2 = wg[sc % 2], wu[sc % 2]
                for kh in range(2):
                    K.dma("pool", w1[kh][:], C["w_gu"][kh * 1024:(kh + 1) * 1024, sc * 256:(sc + 1) * 256].rearrange("(k p) n -> p k n", p=128),
                          [], [w1[kh]])
                    K.dma("pool", w2[kh][:], C["w_gu"][kh * 1024:(kh + 1) * 1024, FFN + sc * 256:FFN + (sc + 1) * 256].rearrange("(k p) n -> p k n", p=128),
                          [], [w2[kh]])
                for jj in range(2):
                    j = sc * 2 + jj
                    Pg, Pu = pgu[igu % 3], pgu[(igu + 1) % 3]
                    SGt = sgt[(igu // 2) % 2]
                    igu += 2
                    for k in range(16):
                        K.mm(Pg[:, :G], w1[k // 8][:, k % 8, jj * 128:(jj + 1) * 128], h2T[:, k, :], k == 0, k == 15, [w1[k // 8], h2T], [Pg])
                    for k in range(16):
                        K.mm(Pu[:, :G], w2[k // 8][:, k % 8, jj * 128:(jj + 1) * 128], h2T[:, k, :], k == 0, k == 15, [w2[k // 8], h2T], [Pu])
                    K.act(SGt[:], Pg[:, :G], AF.Silu, [Pg], [SGt])
                    K.tt(actT[:, j, :], SGt[:], Pu[:, :G], ALU.mult, [SGt, Pu], [actT])
            iw = 0
            for n in range(4):
                ns = slice(n * 512, (n + 1) * 512)
                for k4 in range(11):
                    W = wd[iw % 4]
                    iw += 1
                    K.dma("pool", W[:], C["w_dn"][k4 * 512:(k4 + 1) * 512, ns].rearrange("(k p) n -> p k n", p=128), [], [W])
                    for kk in range(4):
                        k = k4 * 4 + kk
                        for t in range(4):
                            K.mm(pdn[t][:], actT[:, k, t * 128:(t + 1) * 128], W[:, kk, :], k == 0, k == 43, [actT, W], [pdn[t]])
                for t in range(4):
                    T_ = tq[t % 2]
                    K.tt(T_[:], pdn[t][:], bc[:, 1, ns], ALU.mult, [pdn[t], bc], [T_])
                    K.tt(x1t[t][:, ns], x1t[t][:, ns], T_[:], ALU.add, [x1t[t], T_], [x1t[t]], eng="pool")
            for t in range(4):
                X_ = x1t[t]
                O_ = ot[t % 2]
                row0 = grp * G + t * 128
                K.act(junk[:], X_[:], AF.Square, [X_], [junk, ss], accum=ss[:])
                K.act(rs[:], ss[:], AF.Sqrt, [ss], [rs], scale=1.0 / D, bias=EPS)
                K.recip(rs[:], rs[:], [rs], [rs])
                K.stt(O_[:], X_[:], rs[:], bc[:, 2, :], ALU.mult, ALU.mult, [X_, rs, bc], [O_])
                K.dma("sp", out_d[row0:row0 + 128, :], O_[:], [O_], [out_d])

NHP = 8
NGH = 8
NGRP = 4
RELAX = [True, True, True, False]


def build(debug=False, only=None):
    nc = bass.Bass("TRN2", target_bir_lowering=False)
    K = KB(nc)
    inp = lambda name, shape: K.dram(name, shape, F32, kind="ExternalInput")
    x_d = inp("x", [TL, D])
    ctx_d = inp("ctx", [TC, D])
    cc_d = inp("cc", [128, 16, 2])
    wada_d = inp("w_ada", [D, 6 * D])
    bada_d = inp("b_ada", [128, 96])
    n1w_d = inp("norm1_w", [128, 16])
    n2w_d = inp("norm2_w", [128, 16])
    fnw_d = inp("final_norm_w", [1, D])
    win_d = inp("w_in", [D, NIN * 128])
    mu_d = inp("rw_mu", [128, 29])
    cmask_d = inp("cmask", [128, 8])
    convw_d = inp("gdn_conv_w", [128, 24, 5])
    ident_d = inp("ident", [128, 128])
    w0_d = inp("rw_w0", [128, 2, 8])
    a0_d = inp("rw_a0", [128, 2, 8])
    kkw_d = inp("rw_k_k", [128, 8])
    ka_d = inp("rw_k_a", [128, 8])
    rk_d = inp("rw_r_k", [128, 8])
    w2_d = inp("rw_w2", [2, 96, 1024])
    a2_d = inp("rw_a2", [2, 96, 1024])
    g2_d = inp("rw_g2", [64, 1024])
    lnw_d = inp("rw_ln_w", [1, 1024])
    lnb_d = inp("rw_ln_b", [1, 1024])
    m1_d = inp("m1", [128, 2, 4, 128])
    m2_d = inp("m2", [128, 2, 3, 128])
    rmask_d = inp("rmask", [128, 512])
    bones_d = inp("blockones", [128, 128])
    hsel_d = inp("headsel", [128, 2])
    nmask_d = inp("nmask", [128, 2, 7, 128])
    selg_d = inp("selg", [64, 16, 128])
    selb_d = inp("selb", [64, 16, 128])
    bigm_d = inp("bigm", [128, 2, 2, 128])
    offd_d = inp("offd", [128, 128])
    ones_d = inp("ones", [128, 128])
    alog_d = inp("gdn_a_log", [64, 1])
    dtb_d = inp("gdn_dt_bias", [64, 1])
    gnw_d = inp("gdn_norm_w", [1, 128])
    pa_d = inp("merge_p_a", [1024, D])
    pb_d = inp("merge_p_b", [1024, D])
    wout_d = inp("w_out", [D, D])
    wgu_d = inp("ffn_w_gate_up", [D, 2 * FFN])
    wdn_d = inp("ffn_w_down", [FFN, D])
    MODS = K.dram("MODS", [32, 128], F32)
    out_d = K.dram("out", [TL, D], F32, kind="ExternalOutput")
    dbg = {}
    if debug:
        dbg["xs"] = K.dram("dbg_xs", [24 * 128, TT], F32, kind="ExternalOutput")
        dbg["u"] = K.dram("dbg_u", [24 * 128, TT], F32, kind="ExternalOutput")
        dbg["mod"] = K.dram("dbg_mod", [128, 96 * 2], F32, kind="ExternalOutput")
        dbg["sm"] = K.dram("dbg_sm", [128, 5, TT], F32, kind="ExternalOutput")
        dbg["oa"] = K.dram("dbg_oa", [128, 8, TL], F32, kind="ExternalOutput")
        dbg["yf"] = K.dram("dbg_yf", [8, 128, 16, 128], F32, kind="ExternalOutput")
        dbg["ob"] = K.dram("dbg_ob", [128, 8, TL], F32, kind="ExternalOutput")
        dbg["of"] = K.dram("dbg_of", [8, 128, 16, 128], F32, kind="ExternalOutput")
    if only:
        XS = inp("XS_in", [24 * 128, TT])
        small_in = inp("small_in", [128, 5, TT])
        US = inp("US_in", [24 * 128, TT])
        SZ = inp("SZ_in", [8 * 128, TL])
        ab_in = inp("ab_in", [128, TT])
    else:
        XS = dbg["xs"] if debug else K.dram("XS", [24 * 128, TT], F32)
    if not only:
        US = dbg["u"] if debug else K.dram("US", [24 * 128, TT], F32)
        SZ = K.dram("SZ", [8 * 128, TL], F32)
    SG = K.dram("SG", [32 * 128, TL], BF16)
    X1 = inp("X1_in", [TL, D]) if only == "ffn" else K.dram("X1", [TL, D], F32)

    with ExitStack() as top:
        ident_f = K.sb(top, [128, 128], F32, "ident_f")
        ident_b = K.sb(top, [128, 128], BF16, "ident_b")
        K.dma("sp", ident_f[:], ident_d[:, :], [], [ident_f])
        K.copy(ident_b[:], ident_f[:], [ident_f], [ident_b])
        modT = K.sb(top, [128, 96, 2], F32, "modT")
        s1 = K.sb(top, [128, 16, 2], F32, "s1")
        s2 = K.sb(top, [128, 16], F32, "s2")
        scopeO = ExitStack()
        abT = K.sb(scopeO, [128, TT], F32, "abT")
        oaT = K.sb(scopeO, [128, 8, TL], BF16, "oaT")
        scopeA = ExitStack()
        smallT = K.sb(scopeA, [128, 5, TT], BF16, "smallT")

        if only == "rwkv":
            K.dma("pool", smallT[:], small_in[:, :, :], [], [smallT])
            K.dma("sp", abT[:], ab_in[:, :], [], [abT])
        with ExitStack() as p0:
          if only != "rwkv":
                ccf = K.sb(p0, [128, 16, 2], F32)
                ccs = K.sb(p0, [128, 16, 2], F32)
                ccb = K.sb(p0, [128, 16, 2], BF16)
                bada = K.sb(p0, [128, 96], F32)
                n1w = K.sb(p0, [128, 16], F32)
                n2w = K.sb(p0, [128, 16], F32)
                K.dma("sp", ccf[:], cc_d[:, :, :], [], [ccf])
                K.dma("sp", bada[:], bada_d[:, :], [], [bada])
                K.dma("sp", n1w[:], n1w_d[:, :], [], [n1w])
                K.dma("sp", n2w[:], n2w_d[:, :], [], [n2w])
                K.act(ccs[:], ccf[:], AF.Silu, [ccf], [ccs])
                K.copy(ccb[:], ccs[:], [ccs], [ccb])
                wb = [[K.sb(p0, [128, 8, 512], BF16) for _ in range(2)] for _ in range(2)]
                pm = K.ps(p0, [128, 96, 2], F32)
                for sc in range(24):
                    w = wb[sc % 2]
                    for kh in range(2):
                        K.dma("pool", w[kh][:],
                              wada_d[kh * 1024:(kh + 1) * 1024, sc * 512:(sc + 1) * 512].rearrange("(k p) n -> p k n", p=128),
                              [], [w[kh]])
                    for jj in range(4):
                        j = sc * 4 + jj
                        for k in range(16):
                            K.mm(pm[:, j, :], w[k // 8][:, k % 8, jj * 128:(jj + 1) * 128], ccb[:, k, :], k == 0, k == 15,
                                 [w[k // 8], ccb], [pm])
                K.tt(modT[:], pm[:], bada[:].unsqueeze(2).to_broadcast([128, 96, 2]), ALU.add, [pm, bada], [modT])
                for v in range(2):
                    K.stt(s1[:, :, v], modT[:, 16:32, v], 1.0, n1w[:], ALU.add, ALU.mult, [modT, n1w], [s1])
                K.stt(s2[:], modT[:, 64:80, 0], 1.0, n2w[:], ALU.add, ALU.mult, [modT, n2w], [s2])
                if debug:
                    K.dma("sp", dbg["mod"][:, :], modT[:].rearrange("p a b -> p (a b)"), [modT], [])

        K.S.barrier()
        with ExitStack() as p1:
          if not only:
                hT = K.sb(p1, [128, 16, TT], BF16, "hT")
                mu = K.sb(p1, [128, 29], F32)
                omm = K.sb(p1, [128, 29], F32)
                cmask = K.sb(p1, [128, 8], F32)
                coef = K.sb(p1, [128, 6, 29], F32)
                convw = K.sb(p1, [128, 24, 5], F32)
                K.dma("sp", mu[:], mu_d[:, :], [], [mu])
                K.dma("sp", cmask[:], cmask_d[:, :], [], [cmask])
                K.dma("sp", convw[:], convw_d[:, :, :], [], [convw])
                K.ts(omm[:], mu[:], -1.0, 1.0, ALU.mult, ALU.add, [mu], [omm])
                for m in range(6):
                    K.ts(coef[:, m, :], mu[:], cmask[:, m:m + 1], None, ALU.mult, None, [mu, cmask], [coef])
                with ExitStack() as pa:
                    xt = [K.sb(pa, [128, D], F32) for _ in range(2)]
                    xb = [K.sb(pa, [128, D], BF16) for _ in range(2)]
                    junk = K.sb(pa, [128, D], BF16)
                    ss = [K.sb(pa, [128, 1], F32) for _ in range(2)]
                    rs = [K.sb(pa, [128, 1], F32) for _ in range(2)]
                    pt = [K.ps(pa, [128, 8, 128], BF16) for _ in range(2)]
                    npt = 0
                    for t in range(NCH):
                        X, XB, SS, RS = xt[t % 2], xb[t % 2], ss[t % 2], rs[t % 2]
                        src = ctx_d[t * 128:(t + 1) * 128, :] if t < 2 else x_d[(t - 2) * 128:(t - 1) * 128, :]
                        v = 1 if t < 2 else 0
                        K.dma("sp", X[:], src, [], [X])
                        K.act(junk[:], X[:], AF.Square, [X], [junk, SS], accum=SS[:])
                        K.act(RS[:], SS[:], AF.Sqrt, [SS], [RS], scale=1.0 / D, bias=EPS)
                        K.recip(RS[:], RS[:], [RS], [RS])
                        K.act(XB[:], X[:], AF.Copy, [X, RS], [XB], scale=RS[:])
                        for g in range(4):
                            P = pt[npt % 2]
                            npt += 1
                            for j in range(4):
                                k = g * 4 + j
                                K.tr(P[:, j, :], XB[:, k * 128:(k + 1) * 128], ident_b[:], [XB, ident_b], [P])
                            for j in range(4):
                                k = g * 4 + j
                                K.act(hT[:, k, t * 128:(t + 1) * 128], P[:, j, :], AF.Identity, [P, s1, modT], [hT],
                                      scale=s1[:, k, v:v + 1], bias=modT[:, k, v:v + 1])
                K.S.relax = RELAX[0]
                K.S.barrier()
                with ExitStack() as pb:
                    wb = [[K.sb(pb, [128, 8, 512], BF16) for _ in range(2)] for _ in range(2)]
                    stage = [K.sb(pb, [128, TT], F32) for _ in range(2)]
                    post = [K.sb(pb, [128, TT], F32) for _ in range(1)] * 2
                    postb = [K.sb(pb, [128, TL], BF16) for _ in range(1)] * 2
                    pp = [K.ps(pb, [128, 512], F32) for _ in range(4)]
                    npp = 0
                    ntile = [(0, 256)] + [(256 + i * 512, 512) for i in range(4)]
                    for sc in range(24):
                        ncol = min(512, NIN * 128 - sc * 512)
                        w = wb[sc % 2]
                        for kh in range(2):
                            K.dma("pool", w[kh][:, :, :ncol],
                                  win_d[kh * 1024:(kh + 1) * 1024, sc * 512:sc * 512 + ncol].rearrange("(k p) n -> p k n", p=128),
                                  [], [w[kh]])
                        for jj in range(ncol // 128):
                            q = sc * 4 + jj
                            lat_only = (53 <= q <= 60) or q >= 62
                            stg = stage[q % 2]
                            for (t0, tn) in ntile:
                                if lat_only and t0 == 0:
                                    continue
                                P = pp[npp % 4]
                                npp += 1
                                for k in range(16):
                                    K.mm(P[:, :tn], w[k // 8][:, k % 8, jj * 128:(jj + 1) * 128], hT[:, k, t0:t0 + tn],
                                         k == 0, k == 15, [w[k // 8], hT], [P])
                                if q <= 28 or 29 <= q <= 52 or q == 61:
                                    K.act(stg[:, t0:t0 + tn], P[:, :tn], AF.Copy, [P], [stg])
                                elif 53 <= q <= 60:
                                    K.act(stg[:, t0:t0 + tn], P[:, :tn], AF.Silu, [P], [stg])
                                else:
                                    K.act(postb[q % 2][:, t0 - 256:t0 - 256 + tn], P[:, :tn], AF.Sigmoid, [P], [postb[q % 2]])
                            if q <= 28:
                                xs = post[q % 2]
                                K.ts(xs[:], stg[:], omm[:, q:q + 1], None, ALU.mult, None, [stg, omm], [xs])
                                pl = stg[:, 256:TT].rearrange("p (r c) -> p r c", c=64)
                                xl = xs[:, 256:TT].rearrange("p (r c) -> p r c", c=64)
                                sh = [(xl[:, :, 1:64], pl[:, :, 0:63]), (xl[:, :, 0:63], pl[:, :, 1:64]),
                                      (xl[:, 1:32, :], pl[:, 0:31, :]), (xl[:, 0:31, :], pl[:, 1:32, :]),
                                      (xs[:, 1:256], stg[:, 0:255]), (xs[:, 0:255], stg[:, 1:256])]
                                for m, (o, i) in enumerate(sh):
                                    K.stt(o, i, coef[:, m, q:q + 1], o, ALU.mult, ALU.add, [stg, xs, coef], [xs])
                                if q < 24:
                                    K.dma("sp", XS[q * 128:(q + 1) * 128, :], xs[:], [xs], [XS])
                                elif q < 26:
                                    K.act(smallT[:, q - 24, :], xs[:], AF.Tanh, [xs], [smallT])
                                elif q < 28:
                                    K.act(smallT[:, q - 24, :], xs[:], AF.Copy, [xs], [smallT])
                                else:
                                    K.act(smallT[:, 4, :], xs[:], AF.Sigmoid, [xs], [smallT])
                            elif q <= 52:
                                g = q - 29
                                acc = post[q % 2]
                                K.ts(acc[:], stg[:], convw[:, g, 2:3], None, ALU.mult, None, [stg, convw], [acc])
                                for (a, b) in ((0, 256), (256, TT)):
                                    for j, o in ((0, 2), (1, 1), (3, -1), (4, -2)):
                                        if o > 0:
                                            ov, iv = acc[:, a + o:b], stg[:, a:b - o]
                                        else:
                                            ov, iv = acc[:, a:b + o], stg[:, a - o:b]
                                        K.stt(ov, iv, convw[:, g, j:j + 1], ov, ALU.mult, ALU.add, [stg, acc, convw], [acc])
                                K.act(acc[:], acc[:], AF.Silu, [acc], [acc])
                                K.dma("sp", US[g * 128:(g + 1) * 128, :], acc[:], [acc], [US])
                            elif q <= 60:
                                K.dma("sp", SZ[(q - 53) * 128:(q - 52) * 128, :], stg[:, 256:TT], [stg], [SZ])
                            elif q == 61:
                                K.copy(abT[:], stg[:], [stg], [abT], eng="pool")
                            else:
                                K.dma("sp", SG[(q - 62) * 128:(q - 61) * 128, :], postb[q % 2][:], [postb[q % 2]], [SG])
                K.S.barrier()
                if debug:
                    smf = K.sb(p1, [128, 5, TT], F32)
                    K.copy(smf[:], smallT[:], [smallT], [smf])
                    K.dma("sp", dbg["sm"][:, :, :], smf[:], [smf], [])

        K.S.relax = RELAX[3]
        K.S.barrier()
        with ExitStack() as p2:
          if only != "ffn":
            rwkv_phase(K, p2, dict(XS=XS, smallT=smallT, oaT=oaT, ident_b=ident_b, ident_f=ident_f,
                                   w0=w0_d, a0=a0_d, kkw=kkw_d, ka=ka_d, rk=rk_d, w2=w2_d, a2=a2_d, g2=g2_d,
                                   lnw=lnw_d, lnb=lnb_d, m1=m1_d, m2=m2_d, rmask=rmask_d, bones=bones_d, hsel=hsel_d, nmask=nmask_d,
                                   dbg=dbg, nhp=NHP))
        K.S.barrier()
        if debug:
            with ExitStack() as pd:
                of = K.sb(pd, [128, 8, TL], F32)
                K.copy(of[:, 0:NHP], oaT[:, 0:NHP], [oaT], [of])
                K.dma("sp", dbg["oa"][:, 0:NHP, :], of[:, 0:NHP], [of], [])
            K.S.barrier()
        scopeA.close()
        obT = K.sb(scopeO, [128, 8, TL], BF16, "obT")
        with ExitStack() as p2b:
          if only != "ffn":
            gdn_phase(K, p2b, dict(US=US, SZ=SZ, abT=abT, obT=obT, ident_b=ident_b, ident_f=ident_f, selg=selg_d, selb=selb_d,
                                   bigm=bigm_d, offd=offd_d, ones=ones_d, rmask=rmask_d, alog=alog_d, dtb=dtb_d, gnw=gnw_d,
                                   nmask=nmask_d, dbg=dbg, nh=NGH))
        K.S.barrier()
        if debug:
            with ExitStack() as pd:
                of = K.sb(pd, [128, 8, TL], F32)
                K.copy(of[:, 0:NGH], obT[:, 0:NGH], [obT], [of])
                K.dma("sp", dbg["ob"][:, 0:NGH, :], of[:, 0:NGH], [of], [])
            K.S.barrier()
        C34 = dict(oaT=oaT, obT=obT, ident_b=ident_b, ident_f=ident_f, modT=modT, s2=s2, SG=SG, X1=X1,
                   x=x_d, out=out_d, MODS=MODS, fnw=fnw_d, p_a=pa_d, p_b=pb_d, w_out=wout_d, w_gu=wgu_d,
                   w_dn=wdn_d, dbg=dbg)
        K.S.relax = RELAX[1]
        if only != "rwkv":
            write_mods(K, C34)
        if not only:
            merge_phase(K, scopeO, C34)
        scopeO.close()
        K.S.relax = RELAX[2]
        K.S.barrier()
        if only != "rwkv":
            ffn_phase(K, top, C34)
        else:
            with ExitStack() as pz:
                z = K.sb(pz, [128, D], F32)
                K.memset(z[:], 0.0, [z])
                K.dma("sp", out_d[0:128, :], z[:], [z], [out_d])
        K.S.emit(nc, top)
    nc._marks = getattr(K.S, "marks", [])
    return nc


def _fm(v, nchunk):
    return np.ascontiguousarray(np.asarray(v, np.float32).reshape(nchunk, 128).T)


def _pad_cols(a, n):
    out = np.zeros(a.shape[:-1] + (n,), np.float32)
    out[..., :a.shape[-1]] = a
    return out


def prep_shared(inputs):
    w_in = np.asarray(inputs["w_in"][0], np.float32)
    RW = 3520
    segs = [w_in[:, 0:3072]]
    for (a, b) in ((3072, 3168), (3168, 3264), (3264, 3360), (3360, 3456), (3456, 3520)):
        segs.append(_pad_cols(w_in[:, a:b], 128))
    segs.append(w_in[:, RW:RW + 3072 + 1024])
    abc = np.zeros((D, 128), np.float32)
    abc[:, 0:16] = w_in[:, 7616:7632]
    abc[:, 32:48] = w_in[:, 7632:7648]
    segs.append(abc)
    segs.append(w_in[:, 7648:])
    win = np.ascontiguousarray(np.concatenate(segs, axis=1))
    assert win.shape == (D, NIN * 128)
    mu = np.asarray(inputs["rw_mu"][0], np.float32)
    mus = [mu[0:3072]]
    for (a, b) in ((3072, 3168), (3168, 3264), (3264, 3360), (3360, 3456), (3456, 3520)):
        mus.append(_pad_cols(mu[a:b], 128))
    mu_fm = _fm(np.concatenate(mus), 29)
    p = np.arange(128)
    cmask = np.zeros((128, 8), np.float32)
    for m in range(4):
        cmask[:, m] = (p % 4 == m)
    cmask[:, 4] = (p % 2 == 0)
    cmask[:, 5] = (p % 2 == 1)
    convw = np.asarray(inputs["gdn_conv_w"][0], np.float32)
    convw_fm = np.ascontiguousarray(convw.reshape(5, 24, 128).transpose(2, 1, 0))
    sh = {
        "w_ada": np.ascontiguousarray(inputs["w_ada"][0], np.float32),
        "b_ada": _fm(inputs["b_ada"][0], 96),
        "norm1_w": _fm(inputs["norm1_w"][0], 16),
        "norm2_w": _fm(inputs["norm2_w"][0], 16),
        "final_norm_w": np.ascontiguousarray(np.asarray(inputs["final_norm_w"], np.float32).reshape(1, D)),
        "w_in": win,
        "rw_mu": mu_fm,
        "cmask": cmask,
        "gdn_conv_w": convw_fm,
        "ident": np.eye(128, dtype=np.float32),
    }
    g = lambda k: np.asarray(inputs[k][0], np.float32)
    sh["rw_w0"] = np.ascontiguousarray(g("rw_w0").reshape(2, 8, 128).transpose(2, 0, 1))
    sh["rw_a0"] = np.ascontiguousarray(g("rw_a0").reshape(2, 8, 128).transpose(2, 0, 1))
    sh["rw_k_k"] = _fm(g("rw_k_k"), 8)
    sh["rw_k_a"] = _fm(g("rw_k_a"), 8)
    sh["rw_r_k"] = _fm(g("rw_r_k").reshape(-1), 8)
    sh["rw_w2"] = np.ascontiguousarray(g("rw_w2"))
    sh["rw_a2"] = np.ascontiguousarray(g("rw_a2"))
    sh["rw_g2"] = np.ascontiguousarray(g("rw_g2"))
    sh["rw_ln_w"] = np.ascontiguousarray(g("rw_ln_w").reshape(1, 1024))
    sh["rw_ln_b"] = np.ascontiguousarray(g("rw_ln_b").reshape(1, 1024))
    r_ = np.arange(128)[:, None]; c_ = np.arange(128)[None, :]
    SL = (c_ < r_).astype(np.float32); SU = (c_ > r_).astype(np.float32)
    IL = (c_ <= r_).astype(np.float32); IU = (c_ >= r_).astype(np.float32)
    m1 = np.stack([np.stack([SL, SU, SL, SU], 0), np.stack([SU, SL, SU, SL], 0)], 0)
    m2 = np.stack([np.stack([SU, IU, -IU], 0), np.stack([SL, IL, -IL], 0)], 0)
    sh["m1"] = np.ascontiguousarray(m1.transpose(2, 0, 1, 3))
    sh["m2"] = np.ascontiguousarray(m2.transpose(2, 0, 1, 3))
    rmask = np.ones((128, 512), np.float32); rmask[:, ::128] = 0.0
    sh["rmask"] = rmask
    bo = np.zeros((128, 128), np.float32); bo[:64, :64] = 1.0; bo[64:, 64:] = 1.0
    sh["blockones"] = bo
    hs = np.zeros((128, 2), np.float32); hs[:64, 0] = 1.0; hs[64:, 1] = 1.0
    sh["headsel"] = hs
    nmk = np.zeros((2, 7, 128, 128), np.float32)
    for lv in range(7):
        bsz = 1 << lv
        low = ((r_ // (2 * bsz) == c_ // (2 * bsz)) & ((r_ // bsz) % 2 == 1) & ((c_ // bsz) % 2 == 0)).astype(np.float32)
        nmk[0, lv] = -low
        nmk[1, lv] = -low.T
    sh["nmask"] = np.ascontiguousarray(nmk.transpose(2, 0, 1, 3))
    selg = np.zeros((64, 16, 128), np.float32); selb = np.zeros((64, 16, 128), np.float32)
    for r0 in range(16):
        selg[r0, r0, :] = 1.0
        selb[32 + r0, r0, :] = 1.0
    sh["selg"] = selg; sh["selb"] = selb
    BIG = 1.0e4
    bigm = np.stack([np.stack([BIG * SU, -BIG * SL], 0), np.stack([BIG * SL, -BIG * SU], 0)], 0)
    sh["bigm"] = np.ascontiguousarray(bigm.transpose(2, 0, 1, 3))
    sh["offd"] = (1.0 - np.eye(128)).astype(np.float32)
    sh["ones"] = np.ones((128, 128), np.float32)
    al = np.zeros((64, 1), np.float32); al[0:16, 0] = g("gdn_a_log").reshape(-1)
    db = np.zeros((64, 1), np.float32); db[0:16, 0] = g("gdn_dt_bias").reshape(-1)
    sh["gdn_a_log"] = al; sh["gdn_dt_bias"] = db
    sh["gdn_norm_w"] = np.ascontiguousarray(g("gdn_norm_w").reshape(1, 128))
    for k_ in ("merge_p_a", "merge_p_b", "w_out", "ffn_w_gate_up", "ffn_w_down"):
        sh[k_] = np.ascontiguousarray(g(k_))
    return sh


def make_in_maps(inputs):
    sh = prep_shared(inputs)
    maps = []
    for b in range(8):
        m = dict(sh)
        m["x"] = np.ascontiguousarray(inputs["x"][b], np.float32)
        m["ctx"] = np.ascontiguousarray(inputs["ctx"][b], np.float32)
        cc = np.stack([np.asarray(inputs["c"][b], np.float32), np.asarray(inputs["c_ctx"], np.float32)], axis=-1)
        m["cc"] = np.ascontiguousarray(cc.reshape(16, 128, 2).transpose(1, 0, 2))
        maps.append(m)
    return maps


_NC = None


def kernel(**inputs):
    global _NC
    if _NC is None:
        _NC = build()
    maps = make_in_maps(inputs)
    res = run_bass_kernel_spmd(_NC, maps, core_ids=list(range(8)))
    return np.stack([r["out"] for r in res.results], axis=0).astype(np.float32)
```

```python
import numpy as np
from contextlib import ExitStack
import concourse.bass as bass
import concourse.mybir as mybir
from concourse.bass_utils import run_bass_kernel_spmd

F32 = mybir.dt.float32
BF16 = mybir.dt.bfloat16
AF = mybir.ActivationFunctionType
ALU = mybir.AluOpType
AX = mybir.AxisListType

COMPUTE = ("pe", "act", "dve", "pool")
NDSEM = 24

D = 2048
TC = 256
TL = 2048
TT = TC + TL
NCH = TT // 128
NIN = 94
FFN = 5632
EPS = 1e-6
DEC = 0.6065306597126334


class Buf:
    __slots__ = ("name", "lw", "rd")

    def __init__(self, name=""):
        self.name = name
        self.lw = None
        self.rd = {}


class Sched:
    def __init__(self):
        self.ops = []
        self.last = {}
        self.dmas = []
        self.bar = set()
        self.bar_seen = set()

    def pe_strict(self, on):
        if on:
            self._saved_relax = getattr(self, "relax", False)
            self.relax = False
        else:
            self.relax = self._saved_relax
            if self.relax and "pe" in self.last:
                self.pe_fence = self.last["pe"]

    def barrier(self):
        if not hasattr(self, "marks"):
            self.marks = []
        self.marks.append({e: sum(1 for o in self.ops if o[0] == e and not o[3]) for e in COMPUTE})
        self.bar = set(self.last.values()) | set(self.dmas)
        self.dmas = []
        self.bar_seen = set()

    def add(self, eng, fn, reads=(), writes=(), dma=False):
        i = len(self.ops)
        deps = set()
        if eng not in self.bar_seen:
            deps |= self.bar
            self.bar_seen.add(eng)
        self.last[eng] = i
        if dma:
            self.dmas.append(i)
        for b in reads:
            if b.lw is not None:
                deps.add(b.lw)
        for b in writes:
            if b.lw is not None:
                deps.add(b.lw)
            deps.update(b.rd.values())
        key = ("d", i) if dma else eng
        for b in reads:
            b.rd[key] = i
        for b in writes:
            b.lw = i
            b.rd = {}
        if eng == "pe" and getattr(self, "relax", False):
            deps = set(d for d in deps if not (self.ops[d][0] == "pe" and not self.ops[d][3]))
            if getattr(self, "pe_fence", None) is not None:
                deps.add(self.pe_fence)
                self.pe_fence = None
        self.ops.append((eng, fn, deps, dma))
        return i

    def emit(self, nc, stack):
        ops = self.ops
        engs = {"pe": nc.tensor, "act": nc.scalar, "dve": nc.vector, "pool": nc.gpsimd, "sp": nc.sync}
        names = list(engs)
        csem = {e: stack.enter_context(nc.semaphore("c_" + e)) for e in COMPUTE}
        dsem = {e: [stack.enter_context(nc.semaphore("d_%s%d" % (e, k))) for k in range(NDSEM)]
                for e in ("sp", "act", "pool")}
        comp = [None] * len(ops)
        cnt = {e: 0 for e in COMPUTE}
        dcnt = {e: 0 for e in dsem}
        prevslot = [None] * len(ops)
        for i, (eng, fn, deps, dma) in enumerate(ops):
            if dma:
                j = dcnt[eng]
                dcnt[eng] += 1
                comp[i] = (dsem[eng][j % NDSEM], 16 * (j // NDSEM + 1))
                if j >= NDSEM:
                    prevslot[i] = (dsem[eng][j % NDSEM], 16 * (j // NDSEM))
            else:
                cnt[eng] += 1
                comp[i] = (csem[eng], cnt[eng])
        per = {e: [] for e in names}
        for i, op in enumerate(ops):
            per[op[0]].append(i)
        block = stack.enter_context(nc.Block())

        def run(ename):
            def body(e):
                known = {}
                for i in per[ename]:
                    eng, fn, deps, dma = ops[i]
                    need = {}
                    cands = [comp[d] for d in deps]
                    if prevslot[i] is not None:
                        cands.append(prevslot[i])
                    for sm, v in cands:
                        k = id(sm)
                        if known.get(k, 0) >= v:
                            continue
                        if k not in need or need[k][1] < v:
                            need[k] = (sm, v)
                    for k, (sm, v) in need.items():
                        e.wait_ge(sm, v)
                        known[k] = v
                    ins = fn(e)
                    sm, v = comp[i]
                    ins.then_inc(sm, 16 if dma else 1)
                if ename in dsem:
                    last = {}
                    for i in per[ename]:
                        if ops[i][3]:
                            sm, v = comp[i]
                            last[id(sm)] = (sm, v)
                    for sm, v in last.values():
                        e.wait_ge(sm, v)
            return body

        block.tensor(run("pe"))
        block.scalar(run("act"))
        block.vector(run("dve"))
        block.gpsimd(run("pool"))
        block.sync(run("sp"))


class T:
    def __init__(self, h, name):
        self.h = h
        self.b = Buf(name)

    def __getitem__(self, k):
        return self.h[k]


def _bufs(xs):
    return [x if isinstance(x, Buf) else x.b for x in xs]


class KB:
    def __init__(self, nc):
        self.nc = nc
        self.S = Sched()
        self.n = 0

    def sb(self, st, shape, dt, name=None):
        self.n += 1
        name = name or "t%d" % self.n
        if not hasattr(self, "used"):
            self.used = set()
        while name in self.used:
            name = name + "_"
        self.used.add(name)
        return T(st.enter_context(self.nc.sbuf_tensor(name, list(shape), dt)), name)

    def ps(self, st, shape, dt, name=None):
        self.n += 1
        name = name or "p%d" % self.n
        return T(st.enter_context(self.nc.psum_tensor(name, list(shape), dt)), name)

    def dram(self, name, shape, dt, kind="Internal"):
        h = self.nc.dram_tensor(name, list(shape), dt, kind=kind)
        t = T(h.ap(), name)
        return t

    def act(self, out, in_, func, r, w, scale=1.0, bias=0.0, accum=None):
        kw = {}
        if accum is not None:
            kw["accum_out"] = accum
        self.S.add("act", lambda e: e.activation(out=out, in_=in_, func=func, scale=scale, bias=bias, **kw),
                   _bufs(r), _bufs(w))

    def tt(self, out, in0, in1, op, r, w, eng="dve"):
        self.S.add(eng, lambda e: e.tensor_tensor(out=out, in0=in0, in1=in1, op=op), _bufs(r), _bufs(w))

    def ts(self, out, in0, s1, s2, op0, op1, r, w, eng="dve", accum=None):
        kw = {}
        if accum is not None:
            kw["accum_out"] = accum
        if op1 is None:
            self.S.add(eng, lambda e: e.tensor_scalar(out=out, in0=in0, scalar1=s1, scalar2=None, op0=op0, **kw),
                       _bufs(r), _bufs(w))
        else:
            self.S.add(eng, lambda e: e.tensor_scalar(out=out, in0=in0, scalar1=s1, scalar2=s2, op0=op0, op1=op1, **kw),
                       _bufs(r), _bufs(w))

    def stt(self, out, in0, scalar, in1, op0, op1, r, w):
        self.S.add("dve", lambda e: e.scalar_tensor_tensor(out=out, in0=in0, scalar=scalar, in1=in1, op0=op0, op1=op1),
                   _bufs(r), _bufs(w))

    def copy(self, out, in_, r, w, eng="dve"):
        self.S.add(eng, lambda e: e.tensor_copy(out=out, in_=in_), _bufs(r), _bufs(w))

    def memset(self, out, val, w, eng="pool"):
        self.S.add(eng, lambda e: e.memset(out, val), [], _bufs(w))

    def recip(self, out, in_, r, w):
        self.S.add("dve", lambda e: e.reciprocal(out=out, in_=in_), _bufs(r), _bufs(w))

    def scan(self, out, d0, d1, init, op0, op1, r, w):
        self.S.add("dve", lambda e: e.tensor_tensor_scan(out=out, data0=d0, data1=d1, initial=init, op0=op0, op1=op1),
                   _bufs(r), _bufs(w))

    def mm(self, out, lhsT, rhs, start, stop, r, w):
        self.S.add("pe", lambda e: e.matmul(out, lhsT=lhsT, rhs=rhs, start=start, stop=stop), _bufs(r), _bufs(w))

    def tr(self, out, in_, ident, r, w):
        self.S.add("pe", lambda e: e.transpose(out=out, in_=in_, identity=ident), _bufs(r), _bufs(w))

    def dma(self, q, out, in_, r, w, **kw):
        self.S.add(q, lambda e: e.dma_start(out=out, in_=in_, **kw), _bufs(r), _bufs(w), dma=True)


def inverse_workspace(K, st, C):
    W = {}
    W["nmask"] = K.sb(st, [128, 2, 7, 128], BF16, "nmask_sb")
    K.dma("pool", W["nmask"][:], C["nmask"][:, :, :, :], [], [W["nmask"]])
    W["ident_b"] = C["ident_b"]
    W["sets"] = []
    for g in range(2):
        W["sets"].append({nm: K.sb(st, [128, 4, 128], BF16, "iw%d_%s" % (g, nm))
                          for nm in ("Xa", "Xb", "Ya", "Yb", "LsX", "LsY", "M1", "M2", "Mt", "R")})
    W["PI"] = [K.ps(st, [128, 4, 128], F32) for _ in range(2)]
    W["cnt"] = 0
    return W


def inverse_units(K, C, LL, n, d, XTb, W, nunits=None):
    LLf = LL[:].rearrange("p j f t -> p (j f) t")
    nm = W["nmask"]
    nunits = 2 * n if nunits is None else nunits
    gs = min(4, nunits)
    idb = W["ident_b"][:].unsqueeze(1).to_broadcast([128, gs, 128])
    bc = lambda m, lv: nm[:, m, lv, :].unsqueeze(1).to_broadcast([128, gs, 128])

    def pi():
        W["cnt"] += 1
        return W["PI"][W["cnt"] % 2]
    mx, my = (0, 1) if d == 0 else (1, 0)

    class V_:
        def __init__(s_, t):
            s_.t = t
            s_.b = t.b

        def __getitem__(s_, k):
            if k == slice(None):
                return s_.t[:, 0:gs, :]
            return s_.t[k]
    groups = []
    for gi, g0 in enumerate(range(0, nunits, gs)):
        S_ = W["sets"][gi % 2]
        st_ = {k_: V_(v_) for k_, v_ in S_.items()}
        st_["g0"] = g0
        st_["Lv"] = LLf[:, 2 * g0:2 * g0 + 2 * gs:2, :]
        st_["LTv"] = LLf[:, 2 * g0 + 1:2 * g0 + 2 * gs:2, :]
        st_["X"], st_["Xn"], st_["Y"], st_["Yn"] = st_["Xa"], st_["Xb"], st_["Ya"], st_["Yb"]
        groups.append(st_)
    for G in groups:
        K.tt(G["LsX"][:], G["Lv"], bc(mx, 0), ALU.mult, [LL, nm], [G["LsX"]], eng="pool")
        K.tt(G["X"][:], G["LsX"][:], idb, ALU.add, [G["LsX"], W["ident_b"]], [G["X"]], eng="pool")
        K.tt(G["LsY"][:], G["LTv"], bc(my, 0), ALU.mult, [LL, nm], [G["LsY"]], eng="pool")
        K.tt(G["Y"][:], G["LsY"][:], idb, ALU.add, [G["LsY"], W["ident_b"]], [G["Y"]], eng="pool")
        K.tt(G["Mt"][:], G["Lv"], idb, ALU.add, [LL, W["ident_b"]], [G["Mt"]], eng="pool")
    for lv in range(1, 7):
        for G in groups:
            K.tt(G["LsX"][:], G["Lv"], bc(mx, lv), ALU.mult, [LL, nm], [G["LsX"]], eng="pool")
            K.tt(G["LsY"][:], G["LTv"], bc(my, lv), ALU.mult, [LL, nm], [G["LsY"]], eng="pool")
        for G in groups:
            X, Y, LsX, LsY, M1, M2 = G["X"], G["Y"], G["LsX"], G["LsY"], G["M1"], G["M2"]
            Q = pi()
            for u in range(gs):
                K.mm(Q[:, u, :], LsY[:, u, :], X[:, u, :], True, True, [LsY, X], [Q])
            K.act(M1[:], Q[:, 0:gs, :], AF.Copy, [Q], [M1])
            Q = pi()
            for u in range(gs):
                K.mm(Q[:, u, :], LsX[:, u, :], Y[:, u, :], True, True, [LsX, Y], [Q])
            K.act(M2[:], Q[:, 0:gs, :], AF.Copy, [Q], [M2])
        for G in groups:
            X, Y, Xn, Yn, M1, M2 = G["X"], G["Y"], G["Xn"], G["Yn"], G["M1"], G["M2"]
            Q = pi()
            for u in range(gs):
                K.mm(Q[:, u, :], Y[:, u, :], M1[:, u, :], True, True, [Y, M1], [Q])
            K.tt(Xn[:], X[:], Q[:, 0:gs, :], ALU.add, [X, Q], [Xn])
            Q = pi()
            for u in range(gs):
                K.mm(Q[:, u, :], X[:, u, :], M2[:, u, :], True, True, [X, M2], [Q])
            K.tt(Yn[:], Y[:], Q[:, 0:gs, :], ALU.add, [Y, Q], [Yn])
            G["X"], G["Xn"], G["Y"], G["Yn"] = Xn, X, Yn, Y
    for G in groups:
        Q = pi()
        for u in range(gs):
            K.mm(Q[:, u, :], G["Mt"][:, u, :], G["Y"][:, u, :], True, True, [G["Mt"], G["Y"]], [Q])
        K.stt(G["R"][:], Q[:, 0:gs, :], -1.0, idb, ALU.mult, ALU.add, [Q, W["ident_b"]], [G["R"]])
    for G in groups:
        Q = pi()
        for u in range(gs):
            K.mm(Q[:, u, :], G["X"][:, u, :], G["R"][:, u, :], True, True, [G["X"], G["R"]], [Q])
        K.tt(XTb[:, G["g0"]:G["g0"] + gs, :], G["Y"][:], Q[:, 0:gs, :], ALU.add, [G["Y"], Q], [XTb])


SEGS = [(0, 2)] + [(2 + 4 * i, 4) for i in range(4)]


def rwkv_phase(K, st, C):
    XS, smallT, oaT, ident_b, ident_f = C["XS"], C["smallT"], C["oaT"], C["ident_b"], C["ident_f"]
    dbg = C["dbg"]
    sb = lambda shape, dt, name=None: K.sb(st, shape, dt, name)
    w0 = sb([128, 2, 8], F32); a0 = sb([128, 2, 8], F32)
    kkw = sb([128, 8], F32); ka = sb([128, 8], F32); omka = sb([128, 8], F32); rk = sb([128, 8], F32)
    for t_, d_ in ((w0, C["w0"]), (a0, C["a0"])):
        K.dma("sp", t_[:], d_[:, :, :], [], [t_])
    for t_, d_ in ((kkw, C["kkw"]), (ka, C["ka"]), (rk, C["rk"])):
        K.dma("sp", t_[:], d_[:, :], [], [t_])
    K.ts(omka[:], ka[:], -1.0, 1.0, ALU.mult, ALU.add, [ka], [omka])
    w2b = sb([128, 2, 1024], BF16); a2b = sb([128, 2, 1024], BF16); g2b = sb([64, 1024], BF16)
    K.memset(w2b[:], 0.0, [w2b])
    K.memset(a2b[:], 0.0, [a2b])
    K.dma("pool", w2b[0:96, :, :], C["w2"][:, :, :].rearrange("d r c -> r d c"), [], [w2b])
    K.dma("pool", a2b[0:96, :, :], C["a2"][:, :, :].rearrange("d r c -> r d c"), [], [a2b])
    K.dma("pool", g2b[:], C["g2"][:, :], [], [g2b])
    lnw = sb([128, 1024], F32); lnb = sb([128, 1024], F32)
    K.dma("sp", lnw[:], C["lnw"][0:1, :].to_broadcast([128, 1024]), [], [lnw])
    K.dma("sp", lnb[:], C["lnb"][0:1, :].to_broadcast([128, 1024]), [], [lnb])
    m1f = sb([128, 2, 4, 128], F32); m2f = sb([128, 2, 3, 128], F32)
    K.dma("sp", m1f[:], C["m1"][:, :, :, :], [], [m1f])
    K.dma("sp", m2f[:], C["m2"][:, :, :, :], [], [m2f])
    rmask = sb([128, 512], F32); bones = sb([128, 128], F32); hsel = sb([128, 2], F32)
    K.dma("sp", rmask[:], C["rmask"][:, :], [], [rmask])
    K.dma("sp", bones[:], C["bones"][:, :], [], [bones])
    K.dma("sp", hsel[:], C["hsel"][:, :], [], [hsel])
    f32t = lambda nm=None: sb([128, 512], F32, nm)
    bft = lambda nm=None: sb([128, 512], BF16, nm)
    Xr, Xk, Xv = f32t("Xr"), f32t("Xk"), f32t("Xv")
    sig, A, B, Cc, Dd = f32t("sig"), f32t("A"), f32t("B"), f32t("Cc"), f32t("Dd")
    e1, e2, e3, e4 = f32t("e1"), f32t("e2"), f32t("e3"), f32t("e4")
    icl, icl0, kq, sq, rn, kkt, kd, bd, tmp = (f32t(nm) for nm in ("icl", "icl0", "kq", "sq", "rn", "kkt", "kd", "bd", "tmp"))
    gam = sb([128, 4], F32, "gam")
    rt, at, kt, bt, KH, BH, vb, rkr = (bft(nm) for nm in ("rt", "at", "kt", "bt", "KH", "BH", "vb", "rkr"))
    KHt = sb([128, 4, 128], BF16, "KHt"); BHnt = sb([128, 4, 128], BF16, "BHnt"); Vt = sb([128, 4, 128], BF16, "Vt")
    LL = sb([128, 4, 4, 128], BF16, "LL")
    AA = sb([128, 4, 2, 3, 128], BF16, "AA")
    XTb = sb([128, 8, 128], BF16, "XTb")
    IW = inverse_workspace(K, st, C)
    Hf = sb([128, 128], F32, "Hf"); Hb = sb([128, 128], BF16, "Hb")
    P1s = sb([128, 128], BF16, "P1s"); Us = sb([128, 128], BF16, "Us")
    ybuf = sb([128, 16, 128], F32, "ybuf")
    ytot = sb([128, 4, 128], F32, "ytot"); yc = sb([128, 4, 128], F32, "yc"); ysq = sb([128, 4, 128], F32)
    mean = sb([128, 8], F32); var = sb([128, 8], F32)
    bsum = sb([128, 4, 2], F32)
    gate = sb([128, 4, 128], F32)
    oat = sb([128, 4, 128], BF16)
    PF = [K.ps(st, [128, 512], F32) for _ in range(1)]
    PTr = K.ps(st, [128, 8, 128], BF16)
    PG = [K.ps(st, [128, 4, 128], F32) for _ in range(2)]
    PSq = K.ps(st, [128, 512], F32)
    PSh = K.ps(st, [128, 512], F32)
    PS_P1, PS_U, PS_Y, PS_H = PSq, PSq, PSq, PSh
    cnt = {"pf": 0, "pg": 0, "pi": 0, "tr": 0}

    def nxt(lst, key):
        cnt[key] += 1
        return lst[cnt[key] % len(lst)]

    def transp(src, dst, n, scale=None):
        half = cnt["tr"] % 2
        cnt["tr"] += 1
        for j in range(n):
            K.tr(PTr[:, half * 4 + j, :], src[:, j * 128:(j + 1) * 128], ident_b[:], [src, ident_b], [PTr])
        if scale is None:
            K.copy(dst[:, :n, :], PTr[:, half * 4:half * 4 + n, :], [PTr], [dst])
        else:
            K.act(dst[:, :n, :], PTr[:, half * 4:half * 4 + n, :], AF.Copy, [PTr], [dst], scale=scale)

    for hp in range(C["nhp"]):
        hc = slice(hp * 128, (hp + 1) * 128)
        for d in range(2):
            K.memset(Hf[:], 0.0, [Hf])
            K.memset(Hb[:], 0.0, [Hb])
            order = SEGS if d == 0 else [SEGS[0], SEGS[4], SEGS[3], SEGS[2], SEGS[1]]
            for (c0, n) in order:
                t0, N = c0 * 128, n * 128
                latent = c0 >= 2
                tk = slice(t0, t0 + N)
                for X_, row in ((Xr, 0), (Xk, 1024), (Xv, 2048)):
                    K.dma("sp", X_[:, :N], XS[row + hp * 128:row + hp * 128 + 128, tk], [XS], [X_])
                P = nxt(PF, "pf")
                K.mm(P[:, :N], w2b[:, d, hc], smallT[:, d, tk], True, True, [w2b, smallT], [P])
                K.act(sig[:, :N], P[:, :N], AF.Sigmoid, [P, w0], [sig], bias=w0[:, d, hp:hp + 1])
                K.scan(A[:, :N], rmask[:, :N], sig[:, :N], 0.0, ALU.mult, ALU.add, [rmask, sig], [A])
                K.tt(B[:, :N], A[:, :N], sig[:, :N], ALU.subtract, [A, sig], [B], eng="pool")
                v3 = lambda t_: t_[:, :N].rearrange("p (c t) -> p c t", t=128)
                tot = v3(A)[:, :, 127:128]
                K.tt(v3(Cc), tot.to_broadcast([128, n, 128]), v3(A), ALU.subtract, [A], [Cc])
                K.tt(Dd[:, :N], Cc[:, :N], sig[:, :N], ALU.add, [Cc, sig], [Dd], eng="pool")
                Gi, Gx, Gt = (A, B, Cc) if d == 0 else (Dd, Cc, B)
                K.act(e1[:, :N], Gi[:, :N], AF.Exp, [Gi], [e1], scale=-DEC)
                K.act(e2[:, :N], Gx[:, :N], AF.Exp, [Gx], [e2], scale=-DEC)
                K.act(e3[:, :N], Gi[:, :N], AF.Exp, [Gi], [e3], scale=DEC)
                K.act(e4[:, :N], Gt[:, :N], AF.Exp, [Gt], [e4], scale=-DEC)
                K.act(gam[:, :n], v3(A)[:, :, 127], AF.Exp, [A], [gam], scale=-DEC)
                P = nxt(PF, "pf")
                K.mm(P[:, :N], a2b[:, d, hc], smallT[:, 2 + d, tk], True, True, [a2b, smallT], [P])
                K.act(icl[:, :N], P[:, :N], AF.Sigmoid, [P, a0], [icl], bias=a0[:, d, hp:hp + 1])
                if d == 1 and latent:
                    P = nxt(PF, "pf")
                    K.mm(P[:, :N], a2b[:, 0, hc], smallT[:, 2, tk], True, True, [a2b, smallT], [P])
                    K.act(icl0[:, :N], P[:, :N], AF.Sigmoid, [P, a0], [icl0], bias=a0[:, 0, hp:hp + 1])
                K.ts(kq[:, :N], Xk[:, :N], kkw[:, hp:hp + 1], None, ALU.mult, None, [Xk, kkw], [kq], eng="pool")
                K.act(sq[:, :N], kq[:, :N], AF.Square, [kq], [sq])
                P = nxt(PF, "pf")
                K.mm(P[:, :N], bones[:], sq[:, :N], True, True, [bones, sq], [P])
                K.act(rn[:, :N], P[:, :N], AF.Sqrt, [P], [rn], bias=EPS)
                K.recip(rn[:, :N], rn[:, :N], [rn], [rn])
                K.tt(kkt[:, :N], kq[:, :N], rn[:, :N], ALU.mult, [kq, rn], [kkt])
                K.ts(tmp[:, :N], icl[:, :N], ka[:, hp:hp + 1], omka[:, hp:hp + 1], ALU.mult, ALU.add, [icl, ka, omka], [tmp])
                K.tt(kd[:, :N], tmp[:, :N], Xk[:, :N], ALU.mult, [tmp, Xk], [kd])
                K.tt(bd[:, :N], kkt[:, :N], icl[:, :N], ALU.mult, [kkt, icl], [bd], eng="pool")
                K.tt(rt[:, :N], Xr[:, :N], e1[:, :N], ALU.mult, [Xr, e1], [rt])
                K.tt(at[:, :N], kkt[:, :N], e2[:, :N], ALU.mult, [kkt, e2], [at], eng="pool")
                K.tt(kt[:, :N], kd[:, :N], e3[:, :N], ALU.mult, [kd, e3], [kt])
                K.tt(bt[:, :N], bd[:, :N], e3[:, :N], ALU.mult, [bd, e3], [bt], eng="pool")
                K.tt(KH[:, :N], kd[:, :N], e4[:, :N], ALU.mult, [kd, e4], [KH])
                K.tt(BH[:, :N], bd[:, :N], e4[:, :N], ALU.mult, [bd, e4], [BH], eng="pool")
                K.act(vb[:, :N], Xv[:, :N], AF.Copy, [Xv], [vb])
                transp(KH, KHt, n)
                transp(BH, BHnt, n, scale=-1.0)
                transp(vb, Vt, n)
                K.S.pe_strict(True)
                for j in range(n):
                    cs = slice(j * 128, (j + 1) * 128)
                    G = nxt(PG, "pg")
                    for e in range(2):
                        ps_ = slice(64 * e, 64 * e + 64)
                        K.mm(G[:, 2 * e, :], at[ps_, cs], bt[ps_, cs], True, True, [at, bt], [G])
                        K.mm(G[:, 2 * e + 1, :], bt[ps_, cs], at[ps_, cs], True, True, [at, bt], [G])
                    K.tt(LL[:, j, :, :], G[:], m1f[:, d, :, :], ALU.mult, [G, m1f], [LL])
                    for e in range(2):
                        ps_ = slice(64 * e, 64 * e + 64)
                        G = nxt(PG, "pg")
                        K.mm(G[:, 0, :], kt[ps_, cs], at[ps_, cs], True, True, [kt, at], [G])
                        K.mm(G[:, 1, :], kt[ps_, cs], rt[ps_, cs], True, True, [kt, rt], [G])
                        K.mm(G[:, 2, :], bt[ps_, cs], rt[ps_, cs], True, True, [bt, rt], [G])
                        K.tt(AA[:, j, e, :, :], G[:, 0:3, :], m2f[:, d, :, :], ALU.mult, [G, m2f], [AA])
                K.S.pe_strict(False)
                inverse_units(K, C, LL, n, d, XTb, IW)
                jl = list(range(n)) if d == 0 else list(range(n - 1, -1, -1))
                for j in jl:
                    cs = slice(j * 128, (j + 1) * 128)
                    K.mm(PS_P1[:, 0:128], at[:, cs], Hb[:], True, False, [at, Hb], [PS_P1])
                    for e in range(2):
                        vs = slice(64 * e, 64 * e + 64)
                        K.mm(PS_P1[:, 64 * e:64 + 64 * e], AA[:, j, e, 0, :], Vt[:, j, vs], False, e == 1, [AA, Vt], [PS_P1])
                    K.act(P1s[:], PS_P1[:, 0:128], AF.Copy, [PS_P1], [P1s])
                    for e in range(2):
                        vs = slice(64 * e, 64 * e + 64)
                        K.mm(PS_U[:, 128 + 64 * e:192 + 64 * e], XTb[:, 2 * j + e, :], P1s[:, vs], True, True, [XTb, P1s], [PS_U])
                    K.copy(Us[:], PS_U[:, 128:256], [PS_U], [Us])
                    if latent:
                        K.mm(PS_Y[:, 256:384], rt[:, cs], Hb[:], True, False, [rt, Hb], [PS_Y])
                        for e in range(2):
                            vs = slice(64 * e, 64 * e + 64)
                            yo = PS_Y[:, 256 + 64 * e:320 + 64 * e]
                            K.mm(yo, AA[:, j, e, 1, :], Vt[:, j, vs], False, False, [AA, Vt], [PS_Y])
                            K.mm(yo, AA[:, j, e, 2, :], Us[:, vs], False, e == 1, [AA, Us], [PS_Y])
                    K.mm(PS_H[:, 384:512], KHt[:, j, :], Vt[:, j, :], True, False, [KHt, Vt], [PS_H])
                    K.mm(PS_H[:, 384:512], BHnt[:, j, :], Us[:], False, True, [BHnt, Us], [PS_H])
                    for e in range(2):
                        ps_ = slice(64 * e, 64 * e + 64)
                        vs = slice(64 * e, 64 * e + 64)
                        K.stt(Hf[ps_, vs], Hf[ps_, vs], gam[ps_, j:j + 1], PS_H[ps_, 384 + 64 * e:448 + 64 * e], ALU.mult, ALU.add,
                              [Hf, gam, PS_H], [Hf])
                    K.act(Hb[:], Hf[:], AF.Copy, [Hf], [Hb])
                    if latent:
                        cg = c0 - 2 + j
                        if d == 0:
                            K.act(ybuf[:, cg, :], PS_Y[:, 256:384], AF.Copy, [PS_Y], [ybuf])
                        else:
                            K.tt(ytot[:, j, :], ybuf[:, cg, :], PS_Y[:, 256:384], ALU.add, [ybuf, PS_Y], [ytot])
                if d == 1 and latent:
                    if dbg and hp < 8:
                        K.dma("sp", dbg["yf"][hp, :, c0 - 2:c0 - 2 + n, :], ytot[:], [ytot], [])
                    yv = ytot[:].rearrange("p j (e c) -> p (j e) c", c=64)
                    ycv = yc[:].rearrange("p j (e c) -> p (j e) c", c=64)
                    sqv = ysq[:].rearrange("p j (e c) -> p (j e) c", c=64)
                    K.S.add("dve", lambda e_: e_.tensor_reduce(out=mean[:], in_=yv, axis=AX.X, op=ALU.add), _bufs([ytot]), _bufs([mean]))
                    K.ts(mean[:], mean[:], 1.0 / 64, None, ALU.mult, None, [mean], [mean])
                    K.tt(ycv, yv, mean[:].unsqueeze(2).to_broadcast([128, 8, 64]), ALU.subtract, [ytot, mean], [yc])
                    K.tt(sqv, ycv, ycv, ALU.mult, [yc], [ysq], eng="pool")
                    K.S.add("dve", lambda e_: e_.tensor_reduce(out=var[:], in_=sqv, axis=AX.X, op=ALU.add), _bufs([ysq]), _bufs([var]))
                    K.act(var[:], var[:], AF.Sqrt, [var], [var], scale=1.0 / 64, bias=64e-5)
                    K.recip(var[:], var[:], [var], [var])
                    K.tt(ycv, ycv, var[:].unsqueeze(2).to_broadcast([128, 8, 64]), ALU.mult, [yc, var], [yc])
                    K.tt(yc[:], yc[:], lnw[:, hc].unsqueeze(1).to_broadcast([128, 4, 128]), ALU.mult, [yc, lnw], [yc])
                    K.tt(yc[:], yc[:], lnb[:, hc].unsqueeze(1).to_broadcast([128, 4, 128]), ALU.add, [yc, lnb], [yc])
                    K.tt(tmp[:, :N], icl[:, :N], icl0[:, :N], ALU.add, [icl, icl0], [tmp])
                    K.ts(tmp[:, :N], tmp[:, :N], 0.5, None, ALU.mult, None, [tmp], [tmp])
                    K.ts(tmp[:, :N], tmp[:, :N], ka[:, hp:hp + 1], omka[:, hp:hp + 1], ALU.mult, ALU.add, [tmp, ka, omka], [tmp])
                    K.tt(tmp[:, :N], tmp[:, :N], Xk[:, :N], ALU.mult, [tmp, Xk], [tmp])
                    K.stt(sq[:, :N], tmp[:, :N], rk[:, hp:hp + 1], Xr[:, :N], ALU.mult, ALU.mult, [tmp, rk, Xr], [sq])
                    P = nxt(PF, "pf")
                    for j in range(n):
                        K.mm(P[:, 2 * j:2 * j + 2], sq[:, j * 128:(j + 1) * 128], hsel[:], True, True, [sq, hsel], [P])
                    K.copy(bsum[:].rearrange("p j e -> p (j e)"), P[:, 0:2 * n], [P], [bsum])
                    K.copy(ysq[:], Vt[:], [Vt], [ysq], eng="pool")
                    K.tt(sqv, sqv, bsum[:].rearrange("p j e -> p (j e)").unsqueeze(2).to_broadcast([128, 8, 64]), ALU.mult,
                         [ysq, bsum], [ysq])
                    K.tt(yc[:], yc[:], ysq[:], ALU.add, [yc, ysq], [yc])
                    P = nxt(PF, "pf")
                    for j in range(n):
                        K.mm(P[:, j * 128:(j + 1) * 128], smallT[0:64, 4, t0 + j * 128:t0 + (j + 1) * 128], g2b[0:64, hc], True, True,
                             [smallT, g2b], [P])
                    K.tt(oat[:], yc[:], P[:].rearrange("p (j c) -> p j c", c=128), ALU.mult, [yc, P], [oat])
                    half = cnt["tr"] % 2
                    cnt["tr"] += 1
                    for j in range(n):
                        K.tr(PTr[:, half * 4 + j, :], oat[:, j, :], ident_b[:], [oat, ident_b], [PTr])
                    K.copy(oaT[:, hp, t0 - 256:t0 - 256 + N].rearrange("p (j t) -> p j t", t=128), PTr[:, half * 4:half * 4 + n, :],
                           [PTr], [oaT])


def gdn_phase(K, st, C):
    US, SZ, abT, obT, ident_b, ident_f = C["US"], C["SZ"], C["abT"], C["obT"], C["ident_b"], C["ident_f"]
    sb = lambda shape, dt, name=None: K.sb(st, shape, dt, name)
    selg = sb([64, 16, 128], F32); selb = sb([64, 16, 128], F32)
    K.dma("sp", selg[:], C["selg"][:, :, :], [], [selg])
    K.dma("sp", selb[:], C["selb"][:, :, :], [], [selb])
    bigm = sb([128, 2, 2, 128], F32); offd = sb([128, 128], F32); ones = sb([128, 128], F32)
    K.dma("sp", bigm[:], C["bigm"][:, :, :, :], [], [bigm])
    K.dma("sp", offd[:], C["offd"][:, :], [], [offd])
    K.dma("sp", ones[:], C["ones"][:, :], [], [ones])
    rmask = sb([128, 512], F32)
    K.dma("sp", rmask[:], C["rmask"][:, :], [], [rmask])
    alog = sb([64, 1], F32); dtb = sb([64, 1], F32); nea = sb([64, 1], F32)
    K.dma("sp", alog[:], C["alog"][:, :], [], [alog])
    K.dma("sp", dtb[:], C["dtb"][:, :], [], [dtb])
    gnw = sb([128, 128], F32)
    K.dma("sp", gnw[:], C["gnw"][0:1, :].to_broadcast([128, 128]), [], [gnw])
    K.act(nea[:], alog[:], AF.Exp, [alog], [nea])
    K.ts(nea[:], nea[:], -1.0, None, ALU.mult, None, [nea], [nea])
    GB = [sb([64, TT], F32, "GB%d" % d) for d in range(2)]
    tokT = [sb([128, NCH, 64], F32, "tokT%d" % d) for d in range(2)]
    with ExitStack() as s0:
        gt = K.sb(s0, [16, TT], F32); A = K.sb(s0, [16, TT], F32); Bx = K.sb(s0, [16, TT], F32)
        K.act(gt[:], abT[0:16, :], AF.Exp, [abT, dtb], [gt], bias=dtb[0:16, :])
        K.act(gt[:], gt[:], AF.Ln, [gt], [gt], bias=1.0)
        K.ts(gt[:], gt[:], nea[0:16, :], None, ALU.mult, None, [gt, nea], [gt])
        for d in range(2):
            K.memset(GB[d][:], 0.0, [GB[d]])
            K.act(GB[d][32:48, :], abT[32:48, :], AF.Sigmoid, [abT], [GB[d]])
        for t0 in range(0, TT, 512):
            N = min(512, TT - t0)
            K.scan(A[:, t0:t0 + N], rmask[0:16, :N], gt[:, t0:t0 + N], 0.0, ALU.mult, ALU.add, [rmask, gt], [A])
        K.copy(GB[0][0:16, :], A[:], [A], [GB[0]], eng="pool")
        K.tt(Bx[:], A[:], gt[:], ALU.subtract, [A, gt], [Bx])
        v3 = lambda t_: t_[:].rearrange("p (c t) -> p c t", t=128)
        tot = v3(A)[:, :, 127:128]
        K.tt(v3(GB[1])[0:16], tot.to_broadcast([16, NCH, 128]), v3(Bx), ALU.subtract, [A, Bx], [GB[1]])
        ptk = K.ps(s0, [128, 8, 64], F32)
        for d in range(2):
            for c8 in range(0, NCH, 8):
                nn = min(8, NCH - c8)
                for j in range(nn):
                    c = c8 + j
                    K.tr(ptk[:, j, :], GB[d][:, c * 128:(c + 1) * 128], ident_f[0:64, 0:64], [GB[d], ident_f], [ptk])
                K.copy(tokT[d][:, c8:c8 + nn, :], ptk[:, 0:nn, :], [ptk], [tokT[d]])
    K.S.barrier()
    f32t = lambda nm=None: sb([128, 512], F32, nm)
    bft = lambda nm=None: sb([128, 512], BF16, nm)
    Xq, Xk, Xv = f32t("gXq"), f32t("gXk"), f32t("gXv")
    sq, rn, qn, kn, bcG, eG, bcB, tmp, sz = (f32t("g_" + nm) for nm in
                                             ("sq", "rn", "qn", "kn", "bcG", "eG", "bcB", "tmp", "sz"))
    knb, qnb, kbT, nKBG, Qd, Ktl, vb = (bft("g_" + nm) for nm in ("knb", "qnb", "kbT", "nKBG", "Qd", "Ktl", "vb"))
    Ktt = sb([128, 4, 128], BF16, "g_Ktt"); Vt = sb([128, 4, 128], BF16, "g_Vt")
    Dc = sb([128, 4, 2, 128], F32, "g_Dc"); DiT = sb([128, 4, 128], F32, "g_DiT"); Dtmp = sb([128, 4, 128], F32, "g_Dtmp")
    LLg = sb([128, 2, 4, 128], BF16, "g_LL")
    QKt = sb([128, 4, 128], BF16, "g_QKt")
    XTb = sb([128, 4, 128], BF16, "g_XTb")
    IW = inverse_workspace(K, st, C)
    Sf = sb([128, 128], F32, "g_Sf"); Sb = sb([128, 128], BF16, "g_Sb")
    P1s = sb([128, 128], BF16, "g_P1s"); VNs = sb([128, 128], BF16, "g_VNs")
    obuf = sb([128, 16, 128], F32, "g_obuf")
    otot = sb([128, 4, 128], F32, "g_otot"); osq = sb([128, 4, 128], F32, "g_osq")
    ss = sb([128, 4], F32); onb = sb([128, 4, 128], BF16, "g_onb")
    PF = K.ps(st, [128, 512], F32)
    PTr = K.ps(st, [128, 8, 128], BF16)
    PG = [K.ps(st, [128, 4, 128], F32) for _ in range(2)]
    PSq = K.ps(st, [128, 512], F32)
    PSh = K.ps(st, [128, 512], F32)
    cnt = {"pg": 0, "tr": 0}

    def transp(src, dst, n):
        half = cnt["tr"] % 2
        cnt["tr"] += 1
        for j in range(n):
            K.tr(PTr[:, half * 4 + j, :], src[:, j * 128:(j + 1) * 128], ident_b[:], [src, ident_b], [PTr])
        K.copy(dst[:, :n, :], PTr[:, half * 4:half * 4 + n, :], [PTr], [dst])

    for h in range(C["nh"]):
        for d in range(2):
            r = d * 8 + h
            K.memset(Sf[:], 0.0, [Sf])
            K.memset(Sb[:], 0.0, [Sb])
            order = SEGS if d == 0 else [SEGS[0], SEGS[4], SEGS[3], SEGS[2], SEGS[1]]
            for (c0, n) in order:
                t0, N = c0 * 128, n * 128
                latent = c0 >= 2
                tk = slice(t0, t0 + N)
                v3 = lambda t_: t_[:, :N].rearrange("p (c t) -> p c t", t=128)
                for X_, row in ((Xq, 0), (Xk, 1024), (Xv, 2048)):
                    K.dma("sp", X_[:, :N], US[row + h * 128:row + h * 128 + 128, tk], [US], [X_])
                for X_, o_, sc_ in ((Xq, qn, 128 ** -0.5), (Xk, kn, 1.0)):
                    K.act(sq[:, :N], X_[:, :N], AF.Square, [X_], [sq])
                    K.mm(PF[:, :N], ones[:], sq[:, :N], True, True, [ones, sq], [PF])
                    K.act(rn[:, :N], PF[:, :N], AF.Sqrt, [PF], [rn], bias=EPS)
                    K.recip(rn[:, :N], rn[:, :N], [rn], [rn])
                    K.stt(o_[:, :N], X_[:, :N], sc_, rn[:, :N], ALU.mult, ALU.mult, [X_, rn], [o_])
                K.mm(PF[:, :N], selg[:, r, :], GB[d][:, tk], True, True, [selg, GB[d]], [PF])
                K.act(bcG[:, :N], PF[:, :N], AF.Copy, [PF], [bcG])
                K.act(eG[:, :N], PF[:, :N], AF.Exp, [PF], [eG])
                K.mm(PF[:, :N], selb[:, r, :], GB[d][:, tk], True, True, [selb, GB[d]], [PF])
                K.act(bcB[:, :N], PF[:, :N], AF.Copy, [PF], [bcB])
                lastcol = 127 if d == 0 else 0
                glast = v3(bcG)[:, :, lastcol:lastcol + 1]
                K.tt(kbT[:, :N], kn[:, :N], bcB[:, :N], ALU.mult, [kn, bcB], [kbT])
                K.copy(knb[:, :N], kn[:, :N], [kn], [knb], eng="pool")
                K.act(qnb[:, :N], qn[:, :N], AF.Copy, [qn], [qnb])
                K.stt(nKBG[:, :N], kbT[:, :N], -1.0, eG[:, :N], ALU.mult, ALU.mult, [kbT, eG], [nKBG])
                K.tt(Qd[:, :N], qn[:, :N], eG[:, :N], ALU.mult, [qn, eG], [Qd], eng="pool")
                K.tt(v3(tmp), glast.to_broadcast([128, n, 128]), v3(bcG), ALU.subtract, [bcG], [tmp])
                K.act(tmp[:, :N], tmp[:, :N], AF.Exp, [tmp], [tmp])
                K.tt(Ktl[:, :N], kn[:, :N], tmp[:, :N], ALU.mult, [kn, tmp], [Ktl])
                K.act(vb[:, :N], Xv[:, :N], AF.Copy, [Xv], [vb])
                transp(Ktl, Ktt, n)
                transp(vb, Vt, n)
                gct = tokT[d][:, c0:c0 + n, r:r + 1].to_broadcast([128, n, 128])
                bg3 = v3(bcG)
                K.tt(Dtmp[:, :n, :], bg3, bigm[:, d, 0, :].unsqueeze(1).to_broadcast([128, n, 128]), ALU.add, [bcG, bigm], [Dtmp])
                K.tt(Dtmp[:, :n, :], Dtmp[:, :n, :], gct, ALU.subtract, [Dtmp, tokT[d]], [Dtmp], eng="pool")
                K.act(Dtmp[:, :n, :], Dtmp[:, :n, :], AF.Exp, [Dtmp], [Dtmp], scale=-1.0)
                K.tt(Dc[:, :n, 0, :], Dtmp[:, :n, :], offd[:].unsqueeze(1).to_broadcast([128, n, 128]), ALU.mult, [Dtmp, offd], [Dc],
                     eng="pool")
                K.tt(DiT[:, :n, :], bg3, bigm[:, d, 1, :].unsqueeze(1).to_broadcast([128, n, 128]), ALU.add, [bcG, bigm], [DiT])
                K.tt(DiT[:, :n, :], DiT[:, :n, :], gct, ALU.subtract, [DiT, tokT[d]], [DiT], eng="pool")
                K.act(DiT[:, :n, :], DiT[:, :n, :], AF.Exp, [DiT], [DiT])
                K.tt(Dc[:, :n, 1, :], DiT[:, :n, :], offd[:].unsqueeze(1).to_broadcast([128, n, 128]), ALU.mult, [DiT, offd], [Dc],
                     eng="pool")
                LLv = LLg[:].rearrange("p a b t -> p (a b) t")
                for j in range(n):
                    cs = slice(j * 128, (j + 1) * 128)
                    cnt["pg"] += 1
                    G = PG[cnt["pg"] % 2]
                    K.mm(G[:, 0, :], kbT[:, cs], knb[:, cs], True, True, [kbT, knb], [G])
                    K.mm(G[:, 1, :], knb[:, cs], kbT[:, cs], True, True, [kbT, knb], [G])
                    K.mm(G[:, 2, :], knb[:, cs], qnb[:, cs], True, True, [qnb, knb], [G])
                    K.tt(LLv[:, 2 * j:2 * j + 2, :], G[:, 0:2, :], Dc[:, j, :, :], ALU.mult, [G, Dc], [LLg])
                    K.tt(QKt[:, j, :], G[:, 2, :], DiT[:, j, :], ALU.mult, [G, DiT], [QKt])
                inverse_units(K, C, LLg, n, d, XTb, IW, nunits=n)
                jl = list(range(n)) if d == 0 else list(range(n - 1, -1, -1))
                for j in jl:
                    cs = slice(j * 128, (j + 1) * 128)
                    c = c0 + j
                    K.mm(PSq[:, 0:128], nKBG[:, cs], Sb[:], True, True, [nKBG, Sb], [PSq])
                    K.stt(P1s[:], Vt[:, j, :], tokT[d][:, c, 32 + r:33 + r], PSq[:, 0:128], ALU.mult, ALU.add,
                          [Vt, tokT[d], PSq], [P1s])
                    K.mm(PSq[:, 128:256], XTb[:, j, :], P1s[:], True, True, [XTb, P1s], [PSq])
                    K.act(VNs[:], PSq[:, 128:256], AF.Copy, [PSq], [VNs])
                    if latent:
                        K.mm(PSq[:, 256:384], Qd[:, cs], Sb[:], True, False, [Qd, Sb], [PSq])
                        K.mm(PSq[:, 256:384], QKt[:, j, :], VNs[:], False, True, [QKt, VNs], [PSq])
                    K.mm(PSh[:, 0:128], Ktt[:, j, :], VNs[:], True, True, [Ktt, VNs], [PSh])
                    gl = eG[:, j * 128 + lastcol:j * 128 + lastcol + 1]
                    K.stt(Sf[:], Sf[:], gl, PSh[:, 0:128], ALU.mult, ALU.add, [Sf, eG, PSh], [Sf])
                    K.act(Sb[:], Sf[:], AF.Copy, [Sf], [Sb])
                    if latent:
                        cg = c - 2
                        if d == 0:
                            K.act(obuf[:, cg, :], PSq[:, 256:384], AF.Copy, [PSq], [obuf])
                        else:
                            K.tt(otot[:, j, :], obuf[:, cg, :], PSq[:, 256:384], ALU.add, [obuf, PSq], [otot])
                if d == 1 and latent:
                    if C["dbg"]:
                        K.dma("sp", C["dbg"]["of"][h, :, c0 - 2:c0 - 2 + n, :], otot[:], [otot], [])
                    K.tt(osq[:], otot[:], otot[:], ALU.mult, [otot], [osq], eng="pool")
                    K.S.add("dve", lambda e_: e_.tensor_reduce(out=ss[:], in_=osq[:], axis=AX.X, op=ALU.add), _bufs([osq]), _bufs([ss]))
                    K.act(ss[:], ss[:], AF.Sqrt, [ss], [ss], scale=1.0 / 128, bias=EPS)
                    K.recip(ss[:], ss[:], [ss], [ss])
                    K.tt(osq[:], otot[:], ss[:].unsqueeze(2).to_broadcast([128, 4, 128]), ALU.mult, [otot, ss], [osq])
                    K.tt(onb[:], osq[:], gnw[:].unsqueeze(1).to_broadcast([128, 4, 128]), ALU.mult, [osq, gnw], [onb])
                    K.dma("sp", sz[:, :N], SZ[h * 128:(h + 1) * 128, t0 - 256:t0 - 256 + N], [SZ], [sz])
                    half = cnt["tr"] % 2
                    cnt["tr"] += 1
                    for j in range(n):
                        K.tr(PTr[:, half * 4 + j, :], onb[:, j, :], ident_b[:], [onb, ident_b], [PTr])
                    K.tt(obT[:, h, t0 - 256:t0 - 256 + N].rearrange("p (j t) -> p j t", t=128), PTr[:, half * 4:half * 4 + n, :],
                         sz[:, :N].rearrange("p (j t) -> p j t", t=128), ALU.mult, [PTr, sz], [obT])


def write_mods(K, C):
    modT, ident_f, MODS = C["modT"], C["ident_f"], C["MODS"]
    with ExitStack() as s0:
        pt = K.ps(s0, [16, 2, 128], F32)
        rows = K.sb(s0, [16, 2, 128], F32)
        for i, sec in enumerate((2, 5)):
            K.tr(pt[:, i, :], modT[:, sec * 16:(sec + 1) * 16, 0], ident_f[:], [modT, ident_f], [pt])
        K.copy(rows[:], pt[:], [pt], [rows])
        for i in range(2):
            K.dma("sp", MODS[i * 16:(i + 1) * 16, :], rows[:, i, :], [rows], [MODS])
    K.S.barrier()


def merge_phase(K, top, C):
    oaT, obT, ident_b, ident_f, modT, s2 = C["oaT"], C["obT"], C["ident_b"], C["ident_f"], C["modT"], C["s2"]
    SG, X1, x_d, out_d = C["SG"], C["X1"], C["x"], C["out"]
    MODS = C["MODS"]
    dbg = C["dbg"]
    bc = K.sb(top, [128, 1, D], F32, "bc_rows")
    K.dma("sp", bc[:, 0, :], MODS[0:16, :].rearrange("(o a) b -> o (a b)", o=1).to_broadcast([128, D]), [MODS], [bc])
    K.S.barrier()
    p3 = ExitStack()
    mT = K.sb(p3, [128, 16, TL], BF16, "mT")
    with ExitStack() as s1:
        wa = [K.sb(s1, [128, 8, 256], BF16) for _ in range(2)]
        wbb = [K.sb(s1, [128, 8, 256], BF16) for _ in range(2)]
        sga = [K.sb(s1, [128, TL], BF16)] * 2
        sgb = [K.sb(s1, [128, TL], BF16)] * 2
        t1 = [K.sb(s1, [128, 512], F32) for _ in range(2)]
        t2 = [K.sb(s1, [128, 512], F32) for _ in range(2)]
        pa = [K.ps(s1, [128, 512], F32) for _ in range(2)]
        pb = [K.ps(s1, [128, 512], F32) for _ in range(2)]
        it = 0
        for sc in range(8):
            K.dma("pool", wa[sc % 2][:], C["p_a"][:, sc * 256:(sc + 1) * 256].rearrange("(k p) n -> p k n", p=128), [], [wa[sc % 2]])
            K.dma("pool", wbb[sc % 2][:], C["p_b"][:, sc * 256:(sc + 1) * 256].rearrange("(k p) n -> p k n", p=128), [], [wbb[sc % 2]])
            for ff in range(2):
                f = sc * 2 + ff
                ga, gb = sga[f % 2], sgb[f % 2]
                K.dma("sp", ga[:], SG[f * 128:(f + 1) * 128, :], [SG], [ga])
                K.dma("sp", gb[:], SG[2048 + f * 128:2048 + (f + 1) * 128, :], [SG], [gb])
                for n in range(4):
                    ts_ = slice(n * 512, (n + 1) * 512)
                    A_, B_, T1, T2 = pa[it % 2], pb[it % 2], t1[it % 2], t2[it % 2]
                    it += 1
                    for k in range(8):
                        K.mm(A_[:], wa[sc % 2][:, k, ff * 128:(ff + 1) * 128], oaT[:, k, ts_], k == 0, k == 7, [wa[sc % 2], oaT], [A_])
                    for k in range(8):
                        K.mm(B_[:], wbb[sc % 2][:, k, ff * 128:(ff + 1) * 128], obT[:, k, ts_], k == 0, k == 7, [wbb[sc % 2], obT], [B_])
                    K.tt(T1[:], A_[:], ga[:, ts_], ALU.mult, [A_, ga], [T1])
                    K.tt(T2[:], B_[:], gb[:, ts_], ALU.mult, [B_, gb], [T2])
                    K.tt(mT[:, f, ts_], T1[:], T2[:], ALU.add, [T1, T2], [mT], eng="pool")
    K.S.barrier()
    with ExitStack() as s2_:
        wo = [[K.sb(s2_, [128, 8, 512], BF16) for _ in range(2)] for _ in range(2)]
        xt = [K.sb(s2_, [128, 512], F32) for _ in range(3)]
        tt_ = [K.sb(s2_, [128, 512], F32) for _ in range(3)]
        pp = [K.ps(s2_, [128, 512], F32) for _ in range(4)]
        it = 0
        for n in range(4):
            ns = slice(n * 512, (n + 1) * 512)
            w = wo[n % 2]
            for kh in range(2):
                K.dma("pool", w[kh][:], C["w_out"][kh * 1024:(kh + 1) * 1024, ns].rearrange("(k p) n -> p k n", p=128), [], [w[kh]])
            for t in range(16):
                P, X_, T_ = pp[it % 4], xt[it % 3], tt_[it % 3]
                it += 1
                K.dma("sp", X_[:], x_d[t * 128:(t + 1) * 128, ns], [], [X_])
                for k in range(16):
                    K.mm(P[:], mT[:, k, t * 128:(t + 1) * 128], w[k // 8][:, k % 8, :], k == 0, k == 15, [mT, w[k // 8]], [P])
                K.tt(T_[:], P[:], bc[:, 0, ns], ALU.mult, [P, bc], [T_])
                K.tt(T_[:], T_[:], X_[:], ALU.add, [T_, X_], [T_], eng="pool")
                K.dma("sp", X1[t * 128:(t + 1) * 128, ns], T_[:], [T_], [X1])
    p3.close()


def ffn_phase(K, top, C):
    ident_b, ident_f, modT, s2 = C["ident_b"], C["ident_f"], C["modT"], C["s2"]
    X1, out_d, MODS = C["X1"], C["out"], C["MODS"]
    G = 512
    with ExitStack() as s4:
        bc = K.sb(s4, [128, 3, D], F32, "bc_rows4")
        K.dma("sp", bc[:, 1, :], MODS[16:32, :].rearrange("(o a) b -> o (a b)", o=1).to_broadcast([128, D]), [MODS], [bc])
        K.dma("sp", bc[:, 2, :], C["fnw"][0:1, :].to_broadcast([128, D]), [], [bc])
        h2T = K.sb(s4, [128, 16, G], BF16, "h2T")
        actT = K.sb(s4, [128, 44, G], BF16, "actT")
        x1t = [K.sb(s4, [128, D], F32, "x1t%d" % i) for i in range(4)]
        xb = K.sb(s4, [128, D], BF16)
        junk = K.sb(s4, [128, D], BF16)
        ss = K.sb(s4, [128, 1], F32); rs = K.sb(s4, [128, 1], F32)
        wg = [[K.sb(s4, [128, 8, 256], BF16) for _ in range(2)] for _ in range(2)]
        wu = [[K.sb(s4, [128, 8, 256], BF16) for _ in range(2)] for _ in range(2)]
        wd = [K.sb(s4, [128, 4, 512], BF16) for _ in range(4)]
        sgt = [K.sb(s4, [128, G], F32) for _ in range(2)]
        tq = [K.sb(s4, [128, 512], F32) for _ in range(2)]
        ot = [K.sb(s4, [128, D], F32) for _ in range(2)]
        ptr = [K.ps(s4, [128, 8, 128], BF16) for _ in range(1)]
        pgu = [K.ps(s4, [128, 512], F32) for _ in range(3)]
        pdn = [K.ps(s4, [128, 512], F32) for _ in range(4)]
        igu = 0
        for grp in range(NGRP):
            for t in range(4):
                X_ = x1t[t]
                row0 = grp * G + t * 128
                K.dma("sp", X_[:], X1[row0:row0 + 128, :], [X1], [X_])
                K.act(junk[:], X_[:], AF.Square, [X_], [junk, ss], accum=ss[:])
                K.act(rs[:], ss[:], AF.Sqrt, [ss], [rs], scale=1.0 / D, bias=EPS)
                K.recip(rs[:], rs[:], [rs], [rs])
                K.act(xb[:], X_[:], AF.Copy, [X_, rs], [xb], scale=rs[:])
                for g in range(4):
                    P = ptr[0]
                    for j in range(4):
                        k = g * 4 + j
                        K.tr(P[:, j, :], xb[:, k * 128:(k + 1) * 128], ident_b[:], [xb, ident_b], [P])
                    for j in range(4):
                        k = g * 4 + j
                        K.act(h2T[:, k, t * 128:(t + 1) * 128], P[:, j, :], AF.Identity, [P, s2, modT], [h2T],
                              scale=s2[:, k:k + 1], bias=modT[:, 48 + k, 0:1])
            for sc in range(22):
                w1, w2 = wg[sc % 2], wu[sc % 2]
                for kh in range(2):
                    K.dma("pool", w1[kh][:], C["w_gu"][kh * 1024:(kh + 1) * 1024, sc * 256:(sc + 1) * 256].rearrange("(k p) n -> p k n", p=128),
                          [], [w1[kh]])
                    K.dma("pool", w2[kh][:], C["w_gu"][kh * 1024:(kh + 1) * 1024, FFN + sc * 256:FFN + (sc + 1) * 256].rearrange("(k p) n -> p k n", p=128),
                          [], [w2[kh]])
                for jj in range(2):
                    j = sc * 2 + jj
                    Pg, Pu = pgu[igu % 3], pgu[(igu + 1) % 3]
                    SGt = sgt[(igu // 2) % 2]
                    igu += 2
                    for k in range(16):
                        K.mm(Pg[:, :G], w1[k // 8][:, k % 8, jj * 128:(jj + 1) * 128], h2T[:, k, :], k == 0, k == 15, [w1[k // 8], h2T], [Pg])
                    for k in range(16):
                        K.mm(Pu[:, :G], w2[k // 8][:, k % 8, jj * 128:(jj + 1) * 128], h2T[:, k, :], k == 0, k == 15, [w2[k // 8], h2T], [Pu])
                    K.act(SGt[:], Pg[:, :G], AF.Silu, [Pg], [SGt])
                    K.tt(actT[:, j, :], SGt[:], Pu[:, :G], ALU.mult, [SGt, Pu], [actT])
            iw = 0
            for n in range(4):
                ns = slice(n * 512, (n + 1) * 512)
                for k4 in range(11):
                    W = wd[iw % 4]
                    iw += 1
                    K.dma("pool", W[:], C["w_dn"][k4 * 512:(k4 + 1) * 512, ns].rearrange("(k p) n -> p k n", p=128), [], [W])
                    for kk in range(4):
                        k = k4 * 4 + kk
                        for t in range(4):
                            K.mm(pdn[t][:], actT[:, k, t * 128:(t + 1) * 128], W[:, kk, :], k == 0, k == 43, [actT, W], [pdn[t]])
                for t in range(4):
                    T_ = tq[t % 2]
                    K.tt(T_[:], pdn[t][:], bc[:, 1, ns], ALU.mult, [pdn[t], bc], [T_])
                    K.tt(x1t[t][:, ns], x1t[t][:, ns], T_[:], ALU.add, [x1t[t], T_], [x1t[t]], eng="pool")
            for t in range(4):
                X_ = x1t[t]
                O_ = ot[t % 2]
                row0 = grp * G + t * 128
                K.act(junk[:], X_[:], AF.Square, [X_], [junk, ss], accum=ss[:])
                K.act(rs[:], ss[:], AF.Sqrt, [ss], [rs], scale=1.0 / D, bias=EPS)
                K.recip(rs[:], rs[:], [rs], [rs])
                K.stt(O_[:], X_[:], rs[:], bc[:, 2, :], ALU.mult, ALU.mult, [X_, rs, bc], [O_])
                K.dma("sp", out_d[row0:row0 + 128, :], O_[:], [O_], [out_d])

NHP = 8
NGH = 8
NGRP = 4
RELAX = [True, True, True, True, True]


def build(debug=False, only=None):
    nc = bass.Bass("TRN2", target_bir_lowering=False)
    K = KB(nc)
    inp = lambda name, shape: K.dram(name, shape, F32, kind="ExternalInput")
    x_d = inp("x", [TL, D])
    ctx_d = inp("ctx", [TC, D])
    cc_d = inp("cc", [128, 16, 2])
    wada_d = inp("w_ada", [D, 6 * D])
    bada_d = inp("b_ada", [128, 96])
    n1w_d = inp("norm1_w", [128, 16])
    n2w_d = inp("norm2_w", [128, 16])
    fnw_d = inp("final_norm_w", [1, D])
    win_d = inp("w_in", [D, NIN * 128])
    mu_d = inp("rw_mu", [128, 29])
    cmask_d = inp("cmask", [128, 8])
    convw_d = inp("gdn_conv_w", [128, 24, 5])
    ident_d = inp("ident", [128, 128])
    w0_d = inp("rw_w0", [128, 2, 8])
    a0_d = inp("rw_a0", [128, 2, 8])
    kkw_d = inp("rw_k_k", [128, 8])
    ka_d = inp("rw_k_a", [128, 8])
    rk_d = inp("rw_r_k", [128, 8])
    w2_d = inp("rw_w2", [2, 96, 1024])
    a2_d = inp("rw_a2", [2, 96, 1024])
    g2_d = inp("rw_g2", [64, 1024])
    lnw_d = inp("rw_ln_w", [1, 1024])
    lnb_d = inp("rw_ln_b", [1, 1024])
    m1_d = inp("m1", [128, 2, 4, 128])
    m2_d = inp("m2", [128, 2, 3, 128])
    rmask_d = inp("rmask", [128, 512])
    bones_d = inp("blockones", [128, 128])
    hsel_d = inp("headsel", [128, 2])
    nmask_d = inp("nmask", [128, 2, 7, 128])
    selg_d = inp("selg", [64, 16, 128])
    selb_d = inp("selb", [64, 16, 128])
    bigm_d = inp("bigm", [128, 2, 2, 128])
    offd_d = inp("offd", [128, 128])
    ones_d = inp("ones", [128, 128])
    alog_d = inp("gdn_a_log", [64, 1])
    dtb_d = inp("gdn_dt_bias", [64, 1])
    gnw_d = inp("gdn_norm_w", [1, 128])
    pa_d = inp("merge_p_a", [1024, D])
    pb_d = inp("merge_p_b", [1024, D])
    wout_d = inp("w_out", [D, D])
    wgu_d = inp("ffn_w_gate_up", [D, 2 * FFN])
    wdn_d = inp("ffn_w_down", [FFN, D])
    MODS = K.dram("MODS", [32, 128], F32)
    out_d = K.dram("out", [TL, D], F32, kind="ExternalOutput")
    dbg = {}
    if debug:
        dbg["xs"] = K.dram("dbg_xs", [24 * 128, TT], F32, kind="ExternalOutput")
        dbg["u"] = K.dram("dbg_u", [24 * 128, TT], F32, kind="ExternalOutput")
        dbg["mod"] = K.dram("dbg_mod", [128, 96 * 2], F32, kind="ExternalOutput")
        dbg["sm"] = K.dram("dbg_sm", [128, 5, TT], F32, kind="ExternalOutput")
        dbg["oa"] = K.dram("dbg_oa", [128, 8, TL], F32, kind="ExternalOutput")
        dbg["yf"] = K.dram("dbg_yf", [8, 128, 16, 128], F32, kind="ExternalOutput")
        dbg["ob"] = K.dram("dbg_ob", [128, 8, TL], F32, kind="ExternalOutput")
        dbg["of"] = K.dram("dbg_of", [8, 128, 16, 128], F32, kind="ExternalOutput")
    if only:
        XS = inp("XS_in", [24 * 128, TT])
        small_in = inp("small_in", [128, 5, TT])
        US = inp("US_in", [24 * 128, TT])
        SZ = inp("SZ_in", [8 * 128, TL])
        ab_in = inp("ab_in", [128, TT])
    else:
        XS = dbg["xs"] if debug else K.dram("XS", [24 * 128, TT], F32)
    if not only:
        US = dbg["u"] if debug else K.dram("US", [24 * 128, TT], F32)
        SZ = K.dram("SZ", [8 * 128, TL], F32)
    SG = K.dram("SG", [32 * 128, TL], BF16)
    X1 = inp("X1_in", [TL, D]) if only == "ffn" else K.dram("X1", [TL, D], F32)

    with ExitStack() as top:
        ident_f = K.sb(top, [128, 128], F32, "ident_f")
        ident_b = K.sb(top, [128, 128], BF16, "ident_b")
        K.dma("sp", ident_f[:], ident_d[:, :], [], [ident_f])
        K.copy(ident_b[:], ident_f[:], [ident_f], [ident_b])
        modT = K.sb(top, [128, 96, 2], F32, "modT")
        s1 = K.sb(top, [128, 16, 2], F32, "s1")
        s2 = K.sb(top, [128, 16], F32, "s2")
        scopeO = ExitStack()
        abT = K.sb(scopeO, [128, TT], F32, "abT")
        oaT = K.sb(scopeO, [128, 8, TL], BF16, "oaT")
        scopeA = ExitStack()
        smallT = K.sb(scopeA, [128, 5, TT], BF16, "smallT")

        if only == "rwkv":
            K.dma("pool", smallT[:], small_in[:, :, :], [], [smallT])
            K.dma("sp", abT[:], ab_in[:, :], [], [abT])
        with ExitStack() as p0:
          if only != "rwkv":
                ccf = K.sb(p0, [128, 16, 2], F32)
                ccs = K.sb(p0, [128, 16, 2], F32)
                ccb = K.sb(p0, [128, 16, 2], BF16)
                bada = K.sb(p0, [128, 96], F32)
                n1w = K.sb(p0, [128, 16], F32)
                n2w = K.sb(p0, [128, 16], F32)
                K.dma("sp", ccf[:], cc_d[:, :, :], [], [ccf])
                K.dma("sp", bada[:], bada_d[:, :], [], [bada])
                K.dma("sp", n1w[:], n1w_d[:, :], [], [n1w])
                K.dma("sp", n2w[:], n2w_d[:, :], [], [n2w])
                K.act(ccs[:], ccf[:], AF.Silu, [ccf], [ccs])
                K.copy(ccb[:], ccs[:], [ccs], [ccb])
                wb = [[K.sb(p0, [128, 8, 512], BF16) for _ in range(2)] for _ in range(2)]
                pm = K.ps(p0, [128, 96, 2], F32)
                for sc in range(24):
                    w = wb[sc % 2]
                    for kh in range(2):
                        K.dma("pool", w[kh][:],
                              wada_d[kh * 1024:(kh + 1) * 1024, sc * 512:(sc + 1) * 512].rearrange("(k p) n -> p k n", p=128),
                              [], [w[kh]])
                    for jj in range(4):
                        j = sc * 4 + jj
                        for k in range(16):
                            K.mm(pm[:, j, :], w[k // 8][:, k % 8, jj * 128:(jj + 1) * 128], ccb[:, k, :], k == 0, k == 15,
                                 [w[k // 8], ccb], [pm])
                K.tt(modT[:], pm[:], bada[:].unsqueeze(2).to_broadcast([128, 96, 2]), ALU.add, [pm, bada], [modT])
                for v in range(2):
                    K.stt(s1[:, :, v], modT[:, 16:32, v], 1.0, n1w[:], ALU.add, ALU.mult, [modT, n1w], [s1])
                K.stt(s2[:], modT[:, 64:80, 0], 1.0, n2w[:], ALU.add, ALU.mult, [modT, n2w], [s2])
                if debug:
                    K.dma("sp", dbg["mod"][:, :], modT[:].rearrange("p a b -> p (a b)"), [modT], [])

        K.S.barrier()
        with ExitStack() as p1:
          if not only:
                hT = K.sb(p1, [128, 16, TT], BF16, "hT")
                mu = K.sb(p1, [128, 29], F32)
                omm = K.sb(p1, [128, 29], F32)
                cmask = K.sb(p1, [128, 8], F32)
                coef = K.sb(p1, [128, 6, 29], F32)
                convw = K.sb(p1, [128, 24, 5], F32)
                K.dma("sp", mu[:], mu_d[:, :], [], [mu])
                K.dma("sp", cmask[:], cmask_d[:, :], [], [cmask])
                K.dma("sp", convw[:], convw_d[:, :, :], [], [convw])
                K.ts(omm[:], mu[:], -1.0, 1.0, ALU.mult, ALU.add, [mu], [omm])
                for m in range(6):
                    K.ts(coef[:, m, :], mu[:], cmask[:, m:m + 1], None, ALU.mult, None, [mu, cmask], [coef])
                with ExitStack() as pa:
                    xt = [K.sb(pa, [128, D], F32) for _ in range(2)]
                    xb = [K.sb(pa, [128, D], BF16) for _ in range(2)]
                    junk = K.sb(pa, [128, D], BF16)
                    ss = [K.sb(pa, [128, 1], F32) for _ in range(2)]
                    rs = [K.sb(pa, [128, 1], F32) for _ in range(2)]
                    pt = [K.ps(pa, [128, 8, 128], BF16) for _ in range(2)]
                    npt = 0
                    for t in range(NCH):
                        X, XB, SS, RS = xt[t % 2], xb[t % 2], ss[t % 2], rs[t % 2]
                        src = ctx_d[t * 128:(t + 1) * 128, :] if t < 2 else x_d[(t - 2) * 128:(t - 1) * 128, :]
                        v = 1 if t < 2 else 0
                        K.dma("sp", X[:], src, [], [X])
                        K.act(junk[:], X[:], AF.Square, [X], [junk, SS], accum=SS[:])
                        K.act(RS[:], SS[:], AF.Sqrt, [SS], [RS], scale=1.0 / D, bias=EPS)
                        K.recip(RS[:], RS[:], [RS], [RS])
                        K.act(XB[:], X[:], AF.Copy, [X, RS], [XB], scale=RS[:])
                        for g in range(4):
                            P = pt[npt % 2]
                            npt += 1
                            for j in range(4):
                                k = g * 4 + j
                                K.tr(P[:, j, :], XB[:, k * 128:(k + 1) * 128], ident_b[:], [XB, ident_b], [P])
                            for j in range(4):
                                k = g * 4 + j
                                K.act(hT[:, k, t * 128:(t + 1) * 128], P[:, j, :], AF.Identity, [P, s1, modT], [hT],
                                      scale=s1[:, k, v:v + 1], bias=modT[:, k, v:v + 1])
                K.S.relax = RELAX[0]
                K.S.barrier()
                with ExitStack() as pb:
                    wb = [[K.sb(pb, [128, 8, 512], BF16) for _ in range(2)] for _ in range(2)]
                    stage = [K.sb(pb, [128, TT], F32) for _ in range(2)]
                    post = [K.sb(pb, [128, TT], F32) for _ in range(1)] * 2
                    postb = [K.sb(pb, [128, TL], BF16) for _ in range(1)] * 2
                    pp = [K.ps(pb, [128, 512], F32) for _ in range(4)]
                    npp = 0
                    ntile = [(0, 256)] + [(256 + i * 512, 512) for i in range(4)]
                    for sc in range(24):
                        ncol = min(512, NIN * 128 - sc * 512)
                        w = wb[sc % 2]
                        for kh in range(2):
                            K.dma("pool", w[kh][:, :, :ncol],
                                  win_d[kh * 1024:(kh + 1) * 1024, sc * 512:sc * 512 + ncol].rearrange("(k p) n -> p k n", p=128),
                                  [], [w[kh]])
                        for jj in range(ncol // 128):
                            q = sc * 4 + jj
                            lat_only = (53 <= q <= 60) or q >= 62
                            stg = stage[q % 2]
                            for (t0, tn) in ntile:
                                if lat_only and t0 == 0:
                                    continue
                                P = pp[npp % 4]
                                npp += 1
                                for k in range(16):
                                    K.mm(P[:, :tn], w[k // 8][:, k % 8, jj * 128:(jj + 1) * 128], hT[:, k, t0:t0 + tn],
                                         k == 0, k == 15, [w[k // 8], hT], [P])
                                if q <= 28 or 29 <= q <= 52 or q == 61:
                                    K.act(stg[:, t0:t0 + tn], P[:, :tn], AF.Copy, [P], [stg])
                                elif 53 <= q <= 60:
                                    K.act(stg[:, t0:t0 + tn], P[:, :tn], AF.Silu, [P], [stg])
                                else:
                                    K.act(postb[q % 2][:, t0 - 256:t0 - 256 + tn], P[:, :tn], AF.Sigmoid, [P], [postb[q % 2]])
                            if q <= 28:
                                xs = post[q % 2]
                                K.ts(xs[:], stg[:], omm[:, q:q + 1], None, ALU.mult, None, [stg, omm], [xs])
                                pl = stg[:, 256:TT].rearrange("p (r c) -> p r c", c=64)
                                xl = xs[:, 256:TT].rearrange("p (r c) -> p r c", c=64)
                                sh = [(xl[:, :, 1:64], pl[:, :, 0:63]), (xl[:, :, 0:63], pl[:, :, 1:64]),
                                      (xl[:, 1:32, :], pl[:, 0:31, :]), (xl[:, 0:31, :], pl[:, 1:32, :]),
                                      (xs[:, 1:256], stg[:, 0:255]), (xs[:, 0:255], stg[:, 1:256])]
                                for m, (o, i) in enumerate(sh):
                                    K.stt(o, i, coef[:, m, q:q + 1], o, ALU.mult, ALU.add, [stg, xs, coef], [xs])
                                if q < 24:
                                    K.dma("sp", XS[q * 128:(q + 1) * 128, :], xs[:], [xs], [XS])
                                elif q < 26:
                                    K.act(smallT[:, q - 24, :], xs[:], AF.Tanh, [xs], [smallT])
                                elif q < 28:
                                    K.act(smallT[:, q - 24, :], xs[:], AF.Copy, [xs], [smallT])
                                else:
                                    K.act(smallT[:, 4, :], xs[:], AF.Sigmoid, [xs], [smallT])
                            elif q <= 52:
                                g = q - 29
                                acc = post[q % 2]
                                K.ts(acc[:], stg[:], convw[:, g, 2:3], None, ALU.mult, None, [stg, convw], [acc])
                                for (a, b) in ((0, 256), (256, TT)):
                                    for j, o in ((0, 2), (1, 1), (3, -1), (4, -2)):
                                        if o > 0:
                                            ov, iv = acc[:, a + o:b], stg[:, a:b - o]
                                        else:
                                            ov, iv = acc[:, a:b + o], stg[:, a - o:b]
                                        K.stt(ov, iv, convw[:, g, j:j + 1], ov, ALU.mult, ALU.add, [stg, acc, convw], [acc])
                                K.act(acc[:], acc[:], AF.Silu, [acc], [acc])
                                K.dma("sp", US[g * 128:(g + 1) * 128, :], acc[:], [acc], [US])
                            elif q <= 60:
                                K.dma("sp", SZ[(q - 53) * 128:(q - 52) * 128, :], stg[:, 256:TT], [stg], [SZ])
                            elif q == 61:
                                K.copy(abT[:], stg[:], [stg], [abT], eng="pool")
                            else:
                                K.dma("sp", SG[(q - 62) * 128:(q - 61) * 128, :], postb[q % 2][:], [postb[q % 2]], [SG])
                K.S.barrier()
                if debug:
                    smf = K.sb(p1, [128, 5, TT], F32)
                    K.copy(smf[:], smallT[:], [smallT], [smf])
                    K.dma("sp", dbg["sm"][:, :, :], smf[:], [smf], [])

        K.S.relax = RELAX[3]
        K.S.barrier()
        with ExitStack() as p2:
          if only != "ffn":
            rwkv_phase(K, p2, dict(XS=XS, smallT=smallT, oaT=oaT, ident_b=ident_b, ident_f=ident_f,
                                   w0=w0_d, a0=a0_d, kkw=kkw_d, ka=ka_d, rk=rk_d, w2=w2_d, a2=a2_d, g2=g2_d,
                                   lnw=lnw_d, lnb=lnb_d, m1=m1_d, m2=m2_d, rmask=rmask_d, bones=bones_d, hsel=hsel_d, nmask=nmask_d,
                                   dbg=dbg, nhp=NHP))
        K.S.barrier()
        if debug:
            with ExitStack() as pd:
                of = K.sb(pd, [128, 8, TL], F32)
                K.copy(of[:, 0:NHP], oaT[:, 0:NHP], [oaT], [of])
                K.dma("sp", dbg["oa"][:, 0:NHP, :], of[:, 0:NHP], [of], [])
            K.S.barrier()
        scopeA.close()
        obT = K.sb(scopeO, [128, 8, TL], BF16, "obT")
        K.S.relax = RELAX[4]
        with ExitStack() as p2b:
          if only != "ffn":
            gdn_phase(K, p2b, dict(US=US, SZ=SZ, abT=abT, obT=obT, ident_b=ident_b, ident_f=ident_f, selg=selg_d, selb=selb_d,
                                   bigm=bigm_d, offd=offd_d, ones=ones_d, rmask=rmask_d, alog=alog_d, dtb=dtb_d, gnw=gnw_d,
                                   nmask=nmask_d, dbg=dbg, nh=NGH))
        K.S.barrier()
        if debug:
            with ExitStack() as pd:
                of = K.sb(pd, [128, 8, TL], F32)
                K.copy(of[:, 0:NGH], obT[:, 0:NGH], [obT], [of])
                K.dma("sp", dbg["ob"][:, 0:NGH, :], of[:, 0:NGH], [of], [])
            K.S.barrier()
        C34 = dict(oaT=oaT, obT=obT, ident_b=ident_b, ident_f=ident_f, modT=modT, s2=s2, SG=SG, X1=X1,
                   x=x_d, out=out_d, MODS=MODS, fnw=fnw_d, p_a=pa_d, p_b=pb_d, w_out=wout_d, w_gu=wgu_d,
                   w_dn=wdn_d, dbg=dbg)
        K.S.relax = RELAX[1]
        if only != "rwkv":
            write_mods(K, C34)
        if not only:
            merge_phase(K, scopeO, C34)
        scopeO.close()
        K.S.relax = RELAX[2]
        K.S.barrier()
        if only != "rwkv":
            ffn_phase(K, top, C34)
        else:
            with ExitStack() as pz:
                z = K.sb(pz, [128, D], F32)
                K.memset(z[:], 0.0, [z])
                K.dma("sp", out_d[0:128, :], z[:], [z], [out_d])
        K.S.emit(nc, top)
    nc._marks = getattr(K.S, "marks", [])
    return nc


def _fm(v, nchunk):
    return np.ascontiguousarray(np.asarray(v, np.float32).reshape(nchunk, 128).T)


def _pad_cols(a, n):
    out = np.zeros(a.shape[:-1] + (n,), np.float32)
    out[..., :a.shape[-1]] = a
    return out


def prep_shared(inputs):
    w_in = np.asarray(inputs["w_in"][0], np.float32)
    RW = 3520
    segs = [w_in[:, 0:3072]]
    for (a, b) in ((3072, 3168), (3168, 3264), (3264, 3360), (3360, 3456), (3456, 3520)):
        segs.append(_pad_cols(w_in[:, a:b], 128))
    segs.append(w_in[:, RW:RW + 3072 + 1024])
    abc = np.zeros((D, 128), np.float32)
    abc[:, 0:16] = w_in[:, 7616:7632]
    abc[:, 32:48] = w_in[:, 7632:7648]
    segs.append(abc)
    segs.append(w_in[:, 7648:])
    win = np.ascontiguousarray(np.concatenate(segs, axis=1))
    assert win.shape == (D, NIN * 128)
    mu = np.asarray(inputs["rw_mu"][0], np.float32)
    mus = [mu[0:3072]]
    for (a, b) in ((3072, 3168), (3168, 3264), (3264, 3360), (3360, 3456), (3456, 3520)):
        mus.append(_pad_cols(mu[a:b], 128))
    mu_fm = _fm(np.concatenate(mus), 29)
    p = np.arange(128)
    cmask = np.zeros((128, 8), np.float32)
    for m in range(4):
        cmask[:, m] = (p % 4 == m)
    cmask[:, 4] = (p % 2 == 0)
    cmask[:, 5] = (p % 2 == 1)
    convw = np.asarray(inputs["gdn_conv_w"][0], np.float32)
    convw_fm = np.ascontiguousarray(convw.reshape(5, 24, 128).transpose(2, 1, 0))
    sh = {
        "w_ada": np.ascontiguousarray(inputs["w_ada"][0], np.float32),
        "b_ada": _fm(inputs["b_ada"][0], 96),
        "norm1_w": _fm(inputs["norm1_w"][0], 16),
        "norm2_w": _fm(inputs["norm2_w"][0], 16),
        "final_norm_w": np.ascontiguousarray(np.asarray(inputs["final_norm_w"], np.float32).reshape(1, D)),
        "w_in": win,
        "rw_mu": mu_fm,
        "cmask": cmask,
        "gdn_conv_w": convw_fm,
        "ident": np.eye(128, dtype=np.float32),
    }
    g = lambda k: np.asarray(inputs[k][0], np.float32)
    sh["rw_w0"] = np.ascontiguousarray(g("rw_w0").reshape(2, 8, 128).transpose(2, 0, 1))
    sh["rw_a0"] = np.ascontiguousarray(g("rw_a0").reshape(2, 8, 128).transpose(2, 0, 1))
    sh["rw_k_k"] = _fm(g("rw_k_k"), 8)
    sh["rw_k_a"] = _fm(g("rw_k_a"), 8)
    sh["rw_r_k"] = _fm(g("rw_r_k").reshape(-1), 8)
    sh["rw_w2"] = np.ascontiguousarray(g("rw_w2"))
    sh["rw_a2"] = np.ascontiguousarray(g("rw_a2"))
    sh["rw_g2"] = np.ascontiguousarray(g("rw_g2"))
    sh["rw_ln_w"] = np.ascontiguousarray(g("rw_ln_w").reshape(1, 1024))
    sh["rw_ln_b"] = np.ascontiguousarray(g("rw_ln_b").reshape(1, 1024))
    r_ = np.arange(128)[:, None]; c_ = np.arange(128)[None, :]
    SL = (c_ < r_).astype(np.float32); SU = (c_ > r_).astype(np.float32)
    IL = (c_ <= r_).astype(np.float32); IU = (c_ >= r_).astype(np.float32)
    m1 = np.stack([np.stack([SL, SU, SL, SU], 0), np.stack([SU, SL, SU, SL], 0)], 0)
    m2 = np.stack([np.stack([SU, IU, -IU], 0), np.stack([SL, IL, -IL], 0)], 0)
    sh["m1"] = np.ascontiguousarray(m1.transpose(2, 0, 1, 3))
    sh["m2"] = np.ascontiguousarray(m2.transpose(2, 0, 1, 3))
    rmask = np.ones((128, 512), np.float32); rmask[:, ::128] = 0.0
    sh["rmask"] = rmask
    bo = np.zeros((128, 128), np.float32); bo[:64, :64] = 1.0; bo[64:, 64:] = 1.0
    sh["blockones"] = bo
    hs = np.zeros((128, 2), np.float32); hs[:64, 0] = 1.0; hs[64:, 1] = 1.0
    sh["headsel"] = hs
    nmk = np.zeros((2, 7, 128, 128), np.float32)
    for lv in range(7):
        bsz = 1 << lv
        low = ((r_ // (2 * bsz) == c_ // (2 * bsz)) & ((r_ // bsz) % 2 == 1) & ((c_ // bsz) % 2 == 0)).astype(np.float32)
        nmk[0, lv] = -low
        nmk[1, lv] = -low.T
    sh["nmask"] = np.ascontiguousarray(nmk.transpose(2, 0, 1, 3))
    selg = np.zeros((64, 16, 128), np.float32); selb = np.zeros((64, 16, 128), np.float32)
    for r0 in range(16):
        selg[r0, r0, :] = 1.0
        selb[32 + r0, r0, :] = 1.0
    sh["selg"] = selg; sh["selb"] = selb
    BIG = 1.0e4
    bigm = np.stack([np.stack([BIG * SU, -BIG * SL], 0), np.stack([BIG * SL, -BIG * SU], 0)], 0)
    sh["bigm"] = np.ascontiguousarray(bigm.transpose(2, 0, 1, 3))
    sh["offd"] = (1.0 - np.eye(128)).astype(np.float32)
    sh["ones"] = np.ones((128, 128), np.float32)
    al = np.zeros((64, 1), np.float32); al[0:16, 0] = g("gdn_a_log").reshape(-1)
    db = np.zeros((64, 1), np.float32); db[0:16, 0] = g("gdn_dt_bias").reshape(-1)
    sh["gdn_a_log"] = al; sh["gdn_dt_bias"] = db
    sh["gdn_norm_w"] = np.ascontiguousarray(g("gdn_norm_w").reshape(1, 128))
    for k_ in ("merge_p_a", "merge_p_b", "w_out", "ffn_w_gate_up", "ffn_w_down"):
        sh[k_] = np.ascontiguousarray(g(k_))
    return sh


def make_in_maps(inputs):
    sh = prep_shared(inputs)
    maps = []
    for b in range(8):
        m = dict(sh)
        m["x"] = np.ascontiguousarray(inputs["x"][b], np.float32)
        m["ctx"] = np.ascontiguousarray(inputs["ctx"][b], np.float32)
        cc = np.stack([np.asarray(inputs["c"][b], np.float32), np.asarray(inputs["c_ctx"], np.float32)], axis=-1)
        m["cc"] = np.ascontiguousarray(cc.reshape(16, 128, 2).transpose(1, 0, 2))
        maps.append(m)
    return maps


_NC = None


def kernel(**inputs):
    global _NC
    if _NC is None:
        _NC = build()
    maps = make_in_maps(inputs)
    res = run_bass_kernel_spmd(_NC, maps, core_ids=list(range(8)))
    return np.stack([r["out"] for r in res.results], axis=0).astype(np.float32)
```

```python
import numpy as np
from contextlib import ExitStack
import concourse.bass as bass
import concourse.mybir as mybir
from concourse.bass_utils import run_bass_kernel_spmd

F32 = mybir.dt.float32
BF16 = mybir.dt.bfloat16
AF = mybir.ActivationFunctionType
ALU = mybir.AluOpType
AX = mybir.AxisListType

COMPUTE = ("pe", "act", "dve", "pool")
NDSEM = 24

D = 2048
TC = 256
TL = 2048
TT = TC + TL
NCH = TT // 128
NIN = 94
FFN = 5632
EPS = 1e-6
DEC = 0.6065306597126334


class Buf:
    __slots__ = ("name", "lw", "rd")

    def __init__(self, name=""):
        self.name = name
        self.lw = None
        self.rd = {}


class Sched:
    def __init__(self):
        self.ops = []
        self.last = {}
        self.dmas = []
        self.bar = set()
        self.bar_seen = set()

    def pe_strict(self, on):
        if on:
            self._saved_relax = getattr(self, "relax", False)
            self.relax = False
        else:
            self.relax = self._saved_relax
            if self.relax and "pe" in self.last:
                self.pe_fence = self.last["pe"]

    def barrier(self):
        if not hasattr(self, "marks"):
            self.marks = []
        self.marks.append({e: sum(1 for o in self.ops if o[0] == e and not o[3]) for e in COMPUTE})
        self.bar = set(self.last.values()) | set(self.dmas)
        self.dmas = []
        self.bar_seen = set()

    def add(self, eng, fn, reads=(), writes=(), dma=False):
        i = len(self.ops)
        deps = set()
        if eng not in self.bar_seen:
            deps |= self.bar
            self.bar_seen.add(eng)
        self.last[eng] = i
        if dma:
            self.dmas.append(i)
        for b in reads:
            if b.lw is not None:
                deps.add(b.lw)
        for b in writes:
            if b.lw is not None:
                deps.add(b.lw)
            deps.update(b.rd.values())
        key = ("d", i) if dma else eng
        for b in reads:
            b.rd[key] = i
        for b in writes:
            b.lw = i
            b.rd = {}
        if eng == "pe" and getattr(self, "relax", False):
            deps = set(d for d in deps if not (self.ops[d][0] == "pe" and not self.ops[d][3]))
            if getattr(self, "pe_fence", None) is not None:
                deps.add(self.pe_fence)
                self.pe_fence = None
        self.ops.append((eng, fn, deps, dma))
        return i

    def emit(self, nc, stack):
        ops = self.ops
        engs = {"pe": nc.tensor, "act": nc.scalar, "dve": nc.vector, "pool": nc.gpsimd, "sp": nc.sync}
        names = list(engs)
        csem = {e: stack.enter_context(nc.semaphore("c_" + e)) for e in COMPUTE}
        dsem = {e: [stack.enter_context(nc.semaphore("d_%s%d" % (e, k))) for k in range(NDSEM)]
                for e in ("sp", "act", "pool")}
        comp = [None] * len(ops)
        cnt = {e: 0 for e in COMPUTE}
        dcnt = {e: 0 for e in dsem}
        prevslot = [None] * len(ops)
        for i, (eng, fn, deps, dma) in enumerate(ops):
            if dma:
                j = dcnt[eng]
                dcnt[eng] += 1
                comp[i] = (dsem[eng][j % NDSEM], 16 * (j // NDSEM + 1))
                if j >= NDSEM:
                    prevslot[i] = (dsem[eng][j % NDSEM], 16 * (j // NDSEM))
            else:
                cnt[eng] += 1
                comp[i] = (csem[eng], cnt[eng])
        per = {e: [] for e in names}
        for i, op in enumerate(ops):
            per[op[0]].append(i)
        block = stack.enter_context(nc.Block())

        def run(ename):
            def body(e):
                known = {}
                for i in per[ename]:
                    eng, fn, deps, dma = ops[i]
                    need = {}
                    cands = [comp[d] for d in deps]
                    if prevslot[i] is not None:
                        cands.append(prevslot[i])
                    for sm, v in cands:
                        k = id(sm)
                        if known.get(k, 0) >= v:
                            continue
                        if k not in need or need[k][1] < v:
                            need[k] = (sm, v)
                    for k, (sm, v) in need.items():
                        e.wait_ge(sm, v)
                        known[k] = v
                    ins = fn(e)
                    sm, v = comp[i]
                    ins.then_inc(sm, 16 if dma else 1)
                if ename in dsem:
                    last = {}
                    for i in per[ename]:
                        if ops[i][3]:
                            sm, v = comp[i]
                            last[id(sm)] = (sm, v)
                    for sm, v in last.values():
                        e.wait_ge(sm, v)
            return body

        block.tensor(run("pe"))
        block.scalar(run("act"))
        block.vector(run("dve"))
        block.gpsimd(run("pool"))
        block.sync(run("sp"))


class T:
    def __init__(self, h, name):
        self.h = h
        self.b = Buf(name)

    def __getitem__(self, k):
        return self.h[k]


def _bufs(xs):
    return [x if isinstance(x, Buf) else x.b for x in xs]


class KB:
    def __init__(self, nc):
        self.nc = nc
        self.S = Sched()
        self.n = 0

    def sb(self, st, shape, dt, name=None):
        self.n += 1
        name = name or "t%d" % self.n
        if not hasattr(self, "used"):
            self.used = set()
        while name in self.used:
            name = name + "_"
        self.used.add(name)
        return T(st.enter_context(self.nc.sbuf_tensor(name, list(shape), dt)), name)

    def ps(self, st, shape, dt, name=None):
        self.n += 1
        name = name or "p%d" % self.n
        return T(st.enter_context(self.nc.psum_tensor(name, list(shape), dt)), name)

    def dram(self, name, shape, dt, kind="Internal"):
        h = self.nc.dram_tensor(name, list(shape), dt, kind=kind)
        t = T(h.ap(), name)
        return t

    def act(self, out, in_, func, r, w, scale=1.0, bias=0.0, accum=None):
        kw = {}
        if accum is not None:
            kw["accum_out"] = accum
        self.S.add("act", lambda e: e.activation(out=out, in_=in_, func=func, scale=scale, bias=bias, **kw),
                   _bufs(r), _bufs(w))

    def tt(self, out, in0, in1, op, r, w, eng="dve"):
        self.S.add(eng, lambda e: e.tensor_tensor(out=out, in0=in0, in1=in1, op=op), _bufs(r), _bufs(w))

    def ts(self, out, in0, s1, s2, op0, op1, r, w, eng="dve", accum=None):
        kw = {}
        if accum is not None:
            kw["accum_out"] = accum
        if op1 is None:
            self.S.add(eng, lambda e: e.tensor_scalar(out=out, in0=in0, scalar1=s1, scalar2=None, op0=op0, **kw),
                       _bufs(r), _bufs(w))
        else:
            self.S.add(eng, lambda e: e.tensor_scalar(out=out, in0=in0, scalar1=s1, scalar2=s2, op0=op0, op1=op1, **kw),
                       _bufs(r), _bufs(w))

    def stt(self, out, in0, scalar, in1, op0, op1, r, w):
        self.S.add("dve", lambda e: e.scalar_tensor_tensor(out=out, in0=in0, scalar=scalar, in1=in1, op0=op0, op1=op1),
                   _bufs(r), _bufs(w))

    def copy(self, out, in_, r, w, eng="dve"):
        self.S.add(eng, lambda e: e.tensor_copy(out=out, in_=in_), _bufs(r), _bufs(w))

    def memset(self, out, val, w, eng="pool"):
        self.S.add(eng, lambda e: e.memset(out, val), [], _bufs(w))

    def recip(self, out, in_, r, w):
        self.S.add("dve", lambda e: e.reciprocal(out=out, in_=in_), _bufs(r), _bufs(w))

    def scan(self, out, d0, d1, init, op0, op1, r, w):
        self.S.add("dve", lambda e: e.tensor_tensor_scan(out=out, data0=d0, data1=d1, initial=init, op0=op0, op1=op1),
                   _bufs(r), _bufs(w))

    def mm(self, out, lhsT, rhs, start, stop, r, w):
        self.S.add("pe", lambda e: e.matmul(out, lhsT=lhsT, rhs=rhs, start=start, stop=stop), _bufs(r), _bufs(w))

    def tr(self, out, in_, ident, r, w):
        self.S.add("pe", lambda e: e.transpose(out=out, in_=in_, identity=ident), _bufs(r), _bufs(w))

    def dma(self, q, out, in_, r, w, **kw):
        self.S.add(q, lambda e: e.dma_start(out=out, in_=in_, **kw), _bufs(r), _bufs(w), dma=True)


def inverse_workspace(K, st, C):
    W = {}
    W["nmask"] = K.sb(st, [128, 2, 7, 128], BF16, "nmask_sb")
    K.dma("pool", W["nmask"][:], C["nmask"][:, :, :, :], [], [W["nmask"]])
    W["ident_b"] = C["ident_b"]
    W["sets"] = []
    for g in range(2):
        W["sets"].append({nm: K.sb(st, [128, 4, 128], BF16, "iw%d_%s" % (g, nm))
                          for nm in ("Xa", "Xb", "Ya", "Yb", "LsX", "LsY", "M1", "M2", "Mt", "R")})
    W["PI"] = [K.ps(st, [128, 4, 128], F32) for _ in range(2)]
    W["cnt"] = 0
    return W


def inverse_units(K, C, LL, n, d, XTb, W, nunits=None):
    LLf = LL[:].rearrange("p j f t -> p (j f) t")
    nm = W["nmask"]
    nunits = 2 * n if nunits is None else nunits
    gs = min(4, nunits)
    idb = W["ident_b"][:].unsqueeze(1).to_broadcast([128, gs, 128])
    bc = lambda m, lv: nm[:, m, lv, :].unsqueeze(1).to_broadcast([128, gs, 128])

    def pi():
        W["cnt"] += 1
        return W["PI"][W["cnt"] % 2]
    mx, my = (0, 1) if d == 0 else (1, 0)

    class V_:
        def __init__(s_, t):
            s_.t = t
            s_.b = t.b

        def __getitem__(s_, k):
            if k == slice(None):
                return s_.t[:, 0:gs, :]
            return s_.t[k]
    groups = []
    for gi, g0 in enumerate(range(0, nunits, gs)):
        S_ = W["sets"][gi % 2]
        st_ = {k_: V_(v_) for k_, v_ in S_.items()}
        st_["g0"] = g0
        st_["Lv"] = LLf[:, 2 * g0:2 * g0 + 2 * gs:2, :]
        st_["LTv"] = LLf[:, 2 * g0 + 1:2 * g0 + 2 * gs:2, :]
        st_["X"], st_["Xn"], st_["Y"], st_["Yn"] = st_["Xa"], st_["Xb"], st_["Ya"], st_["Yb"]
        groups.append(st_)
    for G in groups:
        K.tt(G["LsX"][:], G["Lv"], bc(mx, 0), ALU.mult, [LL, nm], [G["LsX"]], eng="pool")
        K.tt(G["X"][:], G["LsX"][:], idb, ALU.add, [G["LsX"], W["ident_b"]], [G["X"]], eng="pool")
        K.tt(G["LsY"][:], G["LTv"], bc(my, 0), ALU.mult, [LL, nm], [G["LsY"]], eng="pool")
        K.tt(G["Y"][:], G["LsY"][:], idb, ALU.add, [G["LsY"], W["ident_b"]], [G["Y"]], eng="pool")
        K.tt(G["Mt"][:], G["Lv"], idb, ALU.add, [LL, W["ident_b"]], [G["Mt"]], eng="pool")
    yield
    for lv in range(1, 7):
        for G in groups:
            K.tt(G["LsX"][:], G["Lv"], bc(mx, lv), ALU.mult, [LL, nm], [G["LsX"]], eng="pool")
            K.tt(G["LsY"][:], G["LTv"], bc(my, lv), ALU.mult, [LL, nm], [G["LsY"]], eng="pool")
        yield
        for G in groups:
            X, Y, LsX, LsY, M1, M2 = G["X"], G["Y"], G["LsX"], G["LsY"], G["M1"], G["M2"]
            Q = pi()
            for u in range(gs):
                K.mm(Q[:, u, :], LsY[:, u, :], X[:, u, :], True, True, [LsY, X], [Q])
            K.act(M1[:], Q[:, 0:gs, :], AF.Copy, [Q], [M1])
            Q = pi()
            for u in range(gs):
                K.mm(Q[:, u, :], LsX[:, u, :], Y[:, u, :], True, True, [LsX, Y], [Q])
            K.act(M2[:], Q[:, 0:gs, :], AF.Copy, [Q], [M2])
            yield
        for G in groups:
            X, Y, Xn, Yn, M1, M2 = G["X"], G["Y"], G["Xn"], G["Yn"], G["M1"], G["M2"]
            Q = pi()
            for u in range(gs):
                K.mm(Q[:, u, :], Y[:, u, :], M1[:, u, :], True, True, [Y, M1], [Q])
            K.tt(Xn[:], X[:], Q[:, 0:gs, :], ALU.add, [X, Q], [Xn])
            Q = pi()
            for u in range(gs):
                K.mm(Q[:, u, :], X[:, u, :], M2[:, u, :], True, True, [X, M2], [Q])
            K.tt(Yn[:], Y[:], Q[:, 0:gs, :], ALU.add, [Y, Q], [Yn])
            G["X"], G["Xn"], G["Y"], G["Yn"] = Xn, X, Yn, Y
            yield
    for G in groups:
        Q = pi()
        for u in range(gs):
            K.mm(Q[:, u, :], G["Mt"][:, u, :], G["Y"][:, u, :], True, True, [G["Mt"], G["Y"]], [Q])
        K.stt(G["R"][:], Q[:, 0:gs, :], -1.0, idb, ALU.mult, ALU.add, [Q, W["ident_b"]], [G["R"]])
    for G in groups:
        Q = pi()
        for u in range(gs):
            K.mm(Q[:, u, :], G["X"][:, u, :], G["R"][:, u, :], True, True, [G["X"], G["R"]], [Q])
        K.tt(XTb[:, G["g0"]:G["g0"] + gs, :], G["Y"][:], Q[:, 0:gs, :], ALU.add, [G["Y"], Q], [XTb])
    yield


SEGS = [(0, 2)] + [(2 + 4 * i, 4) for i in range(4)]


def rwkv_phase(K, st, C):
    XS, smallT, oaT, ident_b, ident_f = C["XS"], C["smallT"], C["oaT"], C["ident_b"], C["ident_f"]
    dbg = C["dbg"]
    sb = lambda shape, dt, name=None: K.sb(st, shape, dt, name)
    w0 = sb([128, 2, 8], F32); a0 = sb([128, 2, 8], F32)
    kkw = sb([128, 8], F32); ka = sb([128, 8], F32); omka = sb([128, 8], F32); rk = sb([128, 8], F32)
    for t_, d_ in ((w0, C["w0"]), (a0, C["a0"])):
        K.dma("sp", t_[:], d_[:, :, :], [], [t_])
    for t_, d_ in ((kkw, C["kkw"]), (ka, C["ka"]), (rk, C["rk"])):
        K.dma("sp", t_[:], d_[:, :], [], [t_])
    K.ts(omka[:], ka[:], -1.0, 1.0, ALU.mult, ALU.add, [ka], [omka])
    w2b = sb([128, 2, 1024], BF16); a2b = sb([128, 2, 1024], BF16); g2b = sb([64, 1024], BF16)
    K.memset(w2b[:], 0.0, [w2b])
    K.memset(a2b[:], 0.0, [a2b])
    K.dma("pool", w2b[0:96, :, :], C["w2"][:, :, :].rearrange("d r c -> r d c"), [], [w2b])
    K.dma("pool", a2b[0:96, :, :], C["a2"][:, :, :].rearrange("d r c -> r d c"), [], [a2b])
    K.dma("pool", g2b[:], C["g2"][:, :], [], [g2b])
    lnw = sb([128, 128], F32); lnb = sb([128, 128], F32)
    m1f = sb([128, 2, 4, 128], BF16); m2f = sb([128, 2, 3, 128], BF16)
    K.dma("pool", m1f[:], C["m1"][:, :, :, :], [], [m1f])
    K.dma("pool", m2f[:], C["m2"][:, :, :, :], [], [m2f])
    rmask = sb([128, 512], F32); bones = sb([128, 128], F32); hsel = sb([128, 2], F32)
    K.dma("sp", rmask[:], C["rmask"][:, :], [], [rmask])
    K.dma("sp", bones[:], C["bones"][:, :], [], [bones])
    K.dma("sp", hsel[:], C["hsel"][:, :], [], [hsel])
    f32t = lambda nm=None: sb([128, 512], F32, nm)
    bft = lambda nm=None: sb([128, 512], BF16, nm)
    Xr, Xk, Xv = f32t("Xr"), f32t("Xk"), f32t("Xv")
    sig, A, B, Cc, Dd = f32t("sig"), f32t("A"), f32t("B"), f32t("Cc"), f32t("Dd")
    e1, e2, e3, e4 = f32t("e1"), f32t("e2"), f32t("e3"), f32t("e4")
    icl, icl0, kq, sq, rn, kkt, kd, bd, tmp = (f32t(nm) for nm in ("icl", "icl0", "kq", "sq", "rn", "kkt", "kd", "bd", "tmp"))
    gam = sb([128, 4], F32, "gam")
    rt, at, kt, bt, KH, BH, vb = (bft(nm) for nm in ("rt", "at", "kt", "bt", "KH", "BH", "vb"))
    KHt = sb([128, 4, 128], BF16, "KHt"); BHnt = sb([128, 4, 128], BF16, "BHnt"); Vt = sb([128, 4, 128], BF16, "Vt")
    LL = sb([128, 4, 4, 128], BF16, "LL")
    AA = sb([128, 4, 2, 3, 128], BF16, "AA")
    XTb = sb([128, 8, 128], BF16, "XTb")
    IW = inverse_workspace(K, st, C)
    Hf = sb([128, 128], F32, "Hf"); Hb = sb([128, 128], BF16, "Hb")
    P1s = sb([128, 128], BF16, "P1s"); Us = sb([128, 128], BF16, "Us")
    ybuf = sb([128, 16, 128], F32, "ybuf")
    ytot = sb([128, 4, 128], F32, "ytot"); yc = sb([128, 4, 128], F32, "yc"); ysq = sb([128, 4, 128], F32)
    mean = sb([128, 8], F32); var = sb([128, 8], F32)
    bsum = sb([128, 4, 2], F32)
    oat = sb([128, 4, 128], BF16)
    PF = [K.ps(st, [128, 512], F32) for _ in range(1)]
    PTr = K.ps(st, [128, 8, 128], BF16)
    PG = [K.ps(st, [128, 4, 128], F32) for _ in range(2)]
    PSq = K.ps(st, [128, 512], F32)
    PSh = K.ps(st, [128, 512], F32)
    PS_P1, PS_U, PS_Y, PS_H = PSq, PSq, PSq, PSh
    cnt = {"pf": 0, "pg": 0, "pi": 0, "tr": 0}

    def nxt(lst, key):
        cnt[key] += 1
        return lst[cnt[key] % len(lst)]

    def transp(src, dst, n, scale=None):
        half = cnt["tr"] % 2
        cnt["tr"] += 1
        for j in range(n):
            K.tr(PTr[:, half * 4 + j, :], src[:, j * 128:(j + 1) * 128], ident_b[:], [src, ident_b], [PTr])
        if scale is None:
            K.copy(dst[:, :n, :], PTr[:, half * 4:half * 4 + n, :], [PTr], [dst])
        else:
            K.act(dst[:, :n, :], PTr[:, half * 4:half * 4 + n, :], AF.Copy, [PTr], [dst], scale=scale)

    Xs = [(Xr, Xk, Xv), (f32t("Xr1"), f32t("Xk1"), f32t("Xv1"))]
    rtP = [rt, bft("rt1")]; atP = [at, bft("at1")]; ktP = [kt, bft("kt1")]; btP = [bt, bft("bt1")]
    KHtP = [KHt, sb([128, 4, 128], BF16, "KHt1")]; BHntP = [BHnt, sb([128, 4, 128], BF16, "BHnt1")]
    VtP = [Vt, sb([128, 4, 128], BF16, "Vt1")]
    gamP = [gam, sb([128, 4], F32, "gam1")]
    bsumP = [bsum, sb([128, 4, 2], F32, "bsum1")]
    items = []
    for hp in range(C["nhp"]):
        for d in range(2):
            order = SEGS if d == 0 else [SEGS[0], SEGS[4], SEGS[3], SEGS[2], SEGS[1]]
            for si, (c0, n) in enumerate(order):
                items.append((hp, d, c0, n, si == 0))

    def loads(i):
        hp, d, c0, n, first = items[i]
        t0, N = c0 * 128, n * 128
        for X_, row in zip(Xs[i % 2], (0, 1024, 2048)):
            K.dma("sp", X_[:, :N], XS[row + hp * 128:row + hp * 128 + 128, t0:t0 + N], [XS], [X_])

    def stepA(i):
        hp, d, c0, n, first = items[i]
        p = i % 2
        hc = slice(hp * 128, (hp + 1) * 128)
        t0, N = c0 * 128, n * 128
        latent = c0 >= 2
        tk = slice(t0, t0 + N)
        Xr, Xk, Xv = Xs[p]
        rt, at, kt, bt, KHt, BHnt, Vt, gam, bsum = rtP[p], atP[p], ktP[p], btP[p], KHtP[p], BHntP[p], VtP[p], gamP[p], bsumP[p]
        if i + 1 < len(items):
            loads(i + 1)
        P = nxt(PF, "pf")
        K.mm(P[:, :N], w2b[:, d, hc], smallT[:, d, tk], True, True, [w2b, smallT], [P])
        K.act(sig[:, :N], P[:, :N], AF.Sigmoid, [P, w0], [sig], bias=w0[:, d, hp:hp + 1])
        K.scan(A[:, :N], rmask[:, :N], sig[:, :N], 0.0, ALU.mult, ALU.add, [rmask, sig], [A])
        yield
        K.tt(B[:, :N], A[:, :N], sig[:, :N], ALU.subtract, [A, sig], [B], eng="pool")
        v3 = lambda t_: t_[:, :N].rearrange("p (c t) -> p c t", t=128)
        tot = v3(A)[:, :, 127:128]
        K.tt(v3(Cc), tot.to_broadcast([128, n, 128]), v3(A), ALU.subtract, [A], [Cc])
        K.tt(Dd[:, :N], Cc[:, :N], sig[:, :N], ALU.add, [Cc, sig], [Dd], eng="pool")
        yield
        Gi, Gx, Gt = (A, B, Cc) if d == 0 else (Dd, Cc, B)
        K.act(e1[:, :N], Gi[:, :N], AF.Exp, [Gi], [e1], scale=-DEC)
        K.act(e2[:, :N], Gx[:, :N], AF.Exp, [Gx], [e2], scale=-DEC)
        yield
        K.act(e3[:, :N], Gi[:, :N], AF.Exp, [Gi], [e3], scale=DEC)
        K.act(e4[:, :N], Gt[:, :N], AF.Exp, [Gt], [e4], scale=-DEC)
        K.act(gam[:, :n], v3(A)[:, :, 127], AF.Exp, [A], [gam], scale=-DEC)
        yield
        P = nxt(PF, "pf")
        K.mm(P[:, :N], a2b[:, d, hc], smallT[:, 2 + d, tk], True, True, [a2b, smallT], [P])
        K.act(icl[:, :N], P[:, :N], AF.Sigmoid, [P, a0], [icl], bias=a0[:, d, hp:hp + 1])
        yield
        K.ts(kq[:, :N], Xk[:, :N], kkw[:, hp:hp + 1], None, ALU.mult, None, [Xk, kkw], [kq], eng="pool")
        K.act(sq[:, :N], kq[:, :N], AF.Square, [kq], [sq])
        P = nxt(PF, "pf")
        K.mm(P[:, :N], bones[:], sq[:, :N], True, True, [bones, sq], [P])
        K.act(rn[:, :N], P[:, :N], AF.Sqrt, [P], [rn], bias=EPS)
        K.recip(rn[:, :N], rn[:, :N], [rn], [rn])
        yield
        K.tt(kkt[:, :N], kq[:, :N], rn[:, :N], ALU.mult, [kq, rn], [kkt])
        K.ts(tmp[:, :N], icl[:, :N], ka[:, hp:hp + 1], omka[:, hp:hp + 1], ALU.mult, ALU.add, [icl, ka, omka], [tmp])
        K.tt(kd[:, :N], tmp[:, :N], Xk[:, :N], ALU.mult, [tmp, Xk], [kd])
        yield
        K.tt(bd[:, :N], kkt[:, :N], icl[:, :N], ALU.mult, [kkt, icl], [bd], eng="pool")
        K.tt(rt[:, :N], Xr[:, :N], e1[:, :N], ALU.mult, [Xr, e1], [rt])
        K.tt(at[:, :N], kkt[:, :N], e2[:, :N], ALU.mult, [kkt, e2], [at], eng="pool")
        yield
        K.tt(kt[:, :N], kd[:, :N], e3[:, :N], ALU.mult, [kd, e3], [kt])
        K.tt(bt[:, :N], bd[:, :N], e3[:, :N], ALU.mult, [bd, e3], [bt], eng="pool")
        K.tt(KH[:, :N], kd[:, :N], e4[:, :N], ALU.mult, [kd, e4], [KH])
        yield
        K.tt(BH[:, :N], bd[:, :N], e4[:, :N], ALU.mult, [bd, e4], [BH], eng="pool")
        K.act(vb[:, :N], Xv[:, :N], AF.Copy, [Xv], [vb])
        transp(KH, KHt, n)
        yield
        transp(BH, BHnt, n, scale=-1.0)
        transp(vb, Vt, n)
        yield
        if d == 1 and latent:
            P = nxt(PF, "pf")
            K.mm(P[:, :N], a2b[:, 0, hc], smallT[:, 2, tk], True, True, [a2b, smallT], [P])
            K.act(icl0[:, :N], P[:, :N], AF.Sigmoid, [P, a0], [icl0], bias=a0[:, 0, hp:hp + 1])
            K.tt(tmp[:, :N], icl[:, :N], icl0[:, :N], ALU.add, [icl, icl0], [tmp])
            yield
            K.ts(tmp[:, :N], tmp[:, :N], 0.5, None, ALU.mult, None, [tmp], [tmp])
            K.ts(tmp[:, :N], tmp[:, :N], ka[:, hp:hp + 1], omka[:, hp:hp + 1], ALU.mult, ALU.add, [tmp, ka, omka], [tmp])
            K.tt(tmp[:, :N], tmp[:, :N], Xk[:, :N], ALU.mult, [tmp, Xk], [tmp])
            yield
            K.stt(sq[:, :N], tmp[:, :N], rk[:, hp:hp + 1], Xr[:, :N], ALU.mult, ALU.mult, [tmp, rk, Xr], [sq])
            P = nxt(PF, "pf")
            for j in range(n):
                K.mm(P[:, 2 * j:2 * j + 2], sq[:, j * 128:(j + 1) * 128], hsel[:], True, True, [sq, hsel], [P])
            K.copy(bsum[:].rearrange("p j e -> p (j e)"), P[:, 0:2 * n], [P], [bsum])
            yield

    def stepBC(i):
        hp, d, c0, n, first = items[i]
        p = i % 2
        hc = slice(hp * 128, (hp + 1) * 128)
        t0, N = c0 * 128, n * 128
        latent = c0 >= 2
        rt, at, kt, bt, KHt, BHnt, Vt, gam, bsum = rtP[p], atP[p], ktP[p], btP[p], KHtP[p], BHntP[p], VtP[p], gamP[p], bsumP[p]
        if first:
            K.memset(Hf[:], 0.0, [Hf])
            K.memset(Hb[:], 0.0, [Hb])
            if d == 1:
                K.dma("sp", lnw[:], C["lnw"][0:1, hc].to_broadcast([128, 128]), [], [lnw])
                K.dma("sp", lnb[:], C["lnb"][0:1, hc].to_broadcast([128, 128]), [], [lnb])
        for j in range(n):
            cs = slice(j * 128, (j + 1) * 128)
            K.S.pe_strict(True)
            G = nxt(PG, "pg")
            for e in range(2):
                ps_ = slice(64 * e, 64 * e + 64)
                K.mm(G[:, 2 * e, :], at[ps_, cs], bt[ps_, cs], True, True, [at, bt], [G])
                K.mm(G[:, 2 * e + 1, :], bt[ps_, cs], at[ps_, cs], True, True, [at, bt], [G])
            K.tt(LL[:, j, :, :], G[:], m1f[:, d, :, :], ALU.mult, [G, m1f], [LL])
            for e in range(2):
                ps_ = slice(64 * e, 64 * e + 64)
                G = nxt(PG, "pg")
                K.mm(G[:, 0, :], kt[ps_, cs], at[ps_, cs], True, True, [kt, at], [G])
                K.mm(G[:, 1, :], kt[ps_, cs], rt[ps_, cs], True, True, [kt, rt], [G])
                K.mm(G[:, 2, :], bt[ps_, cs], rt[ps_, cs], True, True, [bt, rt], [G])
                K.tt(AA[:, j, e, :, :], G[:, 0:3, :], m2f[:, d, :, :], ALU.mult, [G, m2f], [AA])
            K.S.pe_strict(False)
            yield
        for _ in inverse_units(K, C, LL, n, d, XTb, IW):
            yield
        jl = list(range(n)) if d == 0 else list(range(n - 1, -1, -1))
        for j in jl:
            cs = slice(j * 128, (j + 1) * 128)
            K.mm(PS_P1[:, 0:128], at[:, cs], Hb[:], True, False, [at, Hb], [PS_P1])
            for e in range(2):
                vs = slice(64 * e, 64 * e + 64)
                K.mm(PS_P1[:, 64 * e:64 + 64 * e], AA[:, j, e, 0, :], Vt[:, j, vs], False, e == 1, [AA, Vt], [PS_P1])
            K.act(P1s[:], PS_P1[:, 0:128], AF.Copy, [PS_P1], [P1s])
            yield
            for e in range(2):
                vs = slice(64 * e, 64 * e + 64)
                K.mm(PS_U[:, 128 + 64 * e:192 + 64 * e], XTb[:, 2 * j + e, :], P1s[:, vs], True, True, [XTb, P1s], [PS_U])
            K.copy(Us[:], PS_U[:, 128:256], [PS_U], [Us])
            yield
            if latent:
                K.mm(PS_Y[:, 256:384], rt[:, cs], Hb[:], True, False, [rt, Hb], [PS_Y])
                for e in range(2):
                    vs = slice(64 * e, 64 * e + 64)
                    yo = PS_Y[:, 256 + 64 * e:320 + 64 * e]
                    K.mm(yo, AA[:, j, e, 1, :], Vt[:, j, vs], False, False, [AA, Vt], [PS_Y])
                    K.mm(yo, AA[:, j, e, 2, :], Us[:, vs], False, e == 1, [AA, Us], [PS_Y])
            K.mm(PS_H[:, 384:512], KHt[:, j, :], Vt[:, j, :], True, False, [KHt, Vt], [PS_H])
            K.mm(PS_H[:, 384:512], BHnt[:, j, :], Us[:], False, True, [BHnt, Us], [PS_H])
            for e in range(2):
                ps_ = slice(64 * e, 64 * e + 64)
                vs = slice(64 * e, 64 * e + 64)
                K.stt(Hf[ps_, vs], Hf[ps_, vs], gam[ps_, j:j + 1], PS_H[ps_, 384 + 64 * e:448 + 64 * e], ALU.mult, ALU.add,
                      [Hf, gam, PS_H], [Hf])
            K.act(Hb[:], Hf[:], AF.Copy, [Hf], [Hb])
            if latent:
                cg = c0 - 2 + j
                if d == 0:
                    K.act(ybuf[:, cg, :], PS_Y[:, 256:384], AF.Copy, [PS_Y], [ybuf])
                else:
                    K.tt(ytot[:, j, :], ybuf[:, cg, :], PS_Y[:, 256:384], ALU.add, [ybuf, PS_Y], [ytot])
            yield
        if d == 1 and latent:
            if dbg and hp < 8:
                K.dma("sp", dbg["yf"][hp, :, c0 - 2:c0 - 2 + n, :], ytot[:], [ytot], [])
            yv = ytot[:].rearrange("p j (e c) -> p (j e) c", c=64)
            ycv = yc[:].rearrange("p j (e c) -> p (j e) c", c=64)
            sqv = ysq[:].rearrange("p j (e c) -> p (j e) c", c=64)
            K.S.add("dve", lambda e_: e_.tensor_reduce(out=mean[:], in_=yv, axis=AX.X, op=ALU.add), _bufs([ytot]), _bufs([mean]))
            K.ts(mean[:], mean[:], 1.0 / 64, None, ALU.mult, None, [mean], [mean])
            K.tt(ycv, yv, mean[:].unsqueeze(2).to_broadcast([128, 8, 64]), ALU.subtract, [ytot, mean], [yc])
            K.tt(sqv, ycv, ycv, ALU.mult, [yc], [ysq], eng="pool")
            yield
            K.S.add("dve", lambda e_: e_.tensor_reduce(out=var[:], in_=sqv, axis=AX.X, op=ALU.add), _bufs([ysq]), _bufs([var]))
            K.act(var[:], var[:], AF.Sqrt, [var], [var], scale=1.0 / 64, bias=64e-5)
            K.recip(var[:], var[:], [var], [var])
            K.tt(ycv, ycv, var[:].unsqueeze(2).to_broadcast([128, 8, 64]), ALU.mult, [yc, var], [yc])
            yield
            K.tt(yc[:], yc[:], lnw[:].unsqueeze(1).to_broadcast([128, 4, 128]), ALU.mult, [yc, lnw], [yc])
            K.tt(yc[:], yc[:], lnb[:].unsqueeze(1).to_broadcast([128, 4, 128]), ALU.add, [yc, lnb], [yc])
            K.copy(ysq[:], Vt[:], [Vt], [ysq], eng="pool")
            K.tt(sqv, sqv, bsum[:].rearrange("p j e -> p (j e)").unsqueeze(2).to_broadcast([128, 8, 64]), ALU.mult,
                 [ysq, bsum], [ysq])
            K.tt(yc[:], yc[:], ysq[:], ALU.add, [yc, ysq], [yc])
            yield
            P = nxt(PF, "pf")
            for j in range(n):
                K.mm(P[:, j * 128:(j + 1) * 128], smallT[0:64, 4, t0 + j * 128:t0 + (j + 1) * 128], g2b[0:64, hc], True, True,
                     [smallT, g2b], [P])
            K.tt(oat[:], yc[:], P[:].rearrange("p (j c) -> p j c", c=128), ALU.mult, [yc, P], [oat])
            half = cnt["tr"] % 2
            cnt["tr"] += 1
            for j in range(n):
                K.tr(PTr[:, half * 4 + j, :], oat[:, j, :], ident_b[:], [oat, ident_b], [PTr])
            K.copy(oaT[:, hp, t0 - 256:t0 - 256 + N].rearrange("p (j t) -> p j t", t=128), PTr[:, half * 4:half * 4 + n, :],
                   [PTr], [oaT])
            yield

    def drain(g):
        for _ in g:
            pass

    loads(0)
    drain(stepA(0))
    for i in range(len(items)):
        g1 = stepBC(i)
        g2 = stepA(i + 1) if i + 1 < len(items) else iter(())
        a1 = a2 = True
        while a1 or a2:
            if a1:
                for _ in range(RATIO):
                    try:
                        next(g1)
                    except StopIteration:
                        a1 = False
                        break
            if a2:
                try:
                    next(g2)
                except StopIteration:
                    a2 = False


def gdn_phase(K, st, C):
    US, SZ, abT, obT, ident_b, ident_f = C["US"], C["SZ"], C["abT"], C["obT"], C["ident_b"], C["ident_f"]
    sb = lambda shape, dt, name=None: K.sb(st, shape, dt, name)
    selg = sb([64, 16, 128], F32); selb = sb([64, 16, 128], F32)
    K.dma("sp", selg[:], C["selg"][:, :, :], [], [selg])
    K.dma("sp", selb[:], C["selb"][:, :, :], [], [selb])
    bigm = sb([128, 2, 2, 128], F32); offd = sb([128, 128], F32); ones = sb([128, 128], F32)
    K.dma("sp", bigm[:], C["bigm"][:, :, :, :], [], [bigm])
    K.dma("sp", offd[:], C["offd"][:, :], [], [offd])
    K.dma("sp", ones[:], C["ones"][:, :], [], [ones])
    rmask = sb([128, 512], F32)
    K.dma("sp", rmask[:], C["rmask"][:, :], [], [rmask])
    alog = sb([64, 1], F32); dtb = sb([64, 1], F32); nea = sb([64, 1], F32)
    K.dma("sp", alog[:], C["alog"][:, :], [], [alog])
    K.dma("sp", dtb[:], C["dtb"][:, :], [], [dtb])
    gnw = sb([128, 128], F32)
    K.dma("sp", gnw[:], C["gnw"][0:1, :].to_broadcast([128, 128]), [], [gnw])
    K.act(nea[:], alog[:], AF.Exp, [alog], [nea])
    K.ts(nea[:], nea[:], -1.0, None, ALU.mult, None, [nea], [nea])
    GB = [sb([64, TT], F32, "GB%d" % d) for d in range(2)]
    tokT = [sb([128, NCH, 64], F32, "tokT%d" % d) for d in range(2)]
    with ExitStack() as s0:
        gt = K.sb(s0, [16, TT], F32); A = K.sb(s0, [16, TT], F32); Bx = K.sb(s0, [16, TT], F32)
        K.act(gt[:], abT[0:16, :], AF.Exp, [abT, dtb], [gt], bias=dtb[0:16, :])
        K.act(gt[:], gt[:], AF.Ln, [gt], [gt], bias=1.0)
        K.ts(gt[:], gt[:], nea[0:16, :], None, ALU.mult, None, [gt, nea], [gt])
        for d in range(2):
            K.memset(GB[d][:], 0.0, [GB[d]])
            K.act(GB[d][32:48, :], abT[32:48, :], AF.Sigmoid, [abT], [GB[d]])
        for t0 in range(0, TT, 512):
            N = min(512, TT - t0)
            K.scan(A[:, t0:t0 + N], rmask[0:16, :N], gt[:, t0:t0 + N], 0.0, ALU.mult, ALU.add, [rmask, gt], [A])
        K.copy(GB[0][0:16, :], A[:], [A], [GB[0]], eng="pool")
        K.tt(Bx[:], A[:], gt[:], ALU.subtract, [A, gt], [Bx])
        v3 = lambda t_: t_[:].rearrange("p (c t) -> p c t", t=128)
        tot = v3(A)[:, :, 127:128]
        K.tt(v3(GB[1])[0:16], tot.to_broadcast([16, NCH, 128]), v3(Bx), ALU.subtract, [A, Bx], [GB[1]])
        ptk = K.ps(s0, [128, 8, 64], F32)
        for d in range(2):
            for c8 in range(0, NCH, 8):
                nn = min(8, NCH - c8)
                for j in range(nn):
                    c = c8 + j
                    K.tr(ptk[:, j, :], GB[d][:, c * 128:(c + 1) * 128], ident_f[0:64, 0:64], [GB[d], ident_f], [ptk])
                K.copy(tokT[d][:, c8:c8 + nn, :], ptk[:, 0:nn, :], [ptk], [tokT[d]])
    K.S.barrier()
    f32t = lambda nm=None: sb([128, 512], F32, nm)
    bft = lambda nm=None: sb([128, 512], BF16, nm)
    Xq, Xk, Xv = f32t("gXq"), f32t("gXk"), f32t("gXv")
    sq, rn, qn, kn, bcG, eG, bcB, tmp, sz = (f32t("g_" + nm) for nm in
                                             ("sq", "rn", "qn", "kn", "bcG", "eG", "bcB", "tmp", "sz"))
    knb, qnb, kbT, nKBG, Qd, Ktl, vb = (bft("g_" + nm) for nm in ("knb", "qnb", "kbT", "nKBG", "Qd", "Ktl", "vb"))
    Ktt = sb([128, 4, 128], BF16, "g_Ktt"); Vt = sb([128, 4, 128], BF16, "g_Vt")
    Dc = sb([128, 4, 2, 128], F32, "g_Dc"); DiT = sb([128, 4, 128], F32, "g_DiT"); Dtmp = sb([128, 4, 128], F32, "g_Dtmp")
    LLg = sb([128, 2, 4, 128], BF16, "g_LL")
    QKt = sb([128, 4, 128], BF16, "g_QKt")
    XTb = sb([128, 4, 128], BF16, "g_XTb")
    IW = inverse_workspace(K, st, C)
    Sf = sb([128, 128], F32, "g_Sf"); Sb = sb([128, 128], BF16, "g_Sb")
    P1s = sb([128, 128], BF16, "g_P1s"); VNs = sb([128, 128], BF16, "g_VNs")
    obuf = sb([128, 16, 128], F32, "g_obuf")
    otot = sb([128, 4, 128], F32, "g_otot"); osq = sb([128, 4, 128], F32, "g_osq")
    ss = sb([128, 4], F32); onb = sb([128, 4, 128], BF16, "g_onb")
    PF = K.ps(st, [128, 512], F32)
    PTr = K.ps(st, [128, 8, 128], BF16)
    PG = [K.ps(st, [128, 4, 128], F32) for _ in range(2)]
    PSq = K.ps(st, [128, 512], F32)
    PSh = K.ps(st, [128, 512], F32)
    cnt = {"pg": 0, "tr": 0}

    def transp(src, dst, n):
        half = cnt["tr"] % 2
        cnt["tr"] += 1
        for j in range(n):
            K.tr(PTr[:, half * 4 + j, :], src[:, j * 128:(j + 1) * 128], ident_b[:], [src, ident_b], [PTr])
        K.copy(dst[:, :n, :], PTr[:, half * 4:half * 4 + n, :], [PTr], [dst])

    for h in range(C["nh"]):
        for d in range(2):
            r = d * 8 + h
            K.memset(Sf[:], 0.0, [Sf])
            K.memset(Sb[:], 0.0, [Sb])
            order = SEGS if d == 0 else [SEGS[0], SEGS[4], SEGS[3], SEGS[2], SEGS[1]]
            for (c0, n) in order:
                t0, N = c0 * 128, n * 128
                latent = c0 >= 2
                tk = slice(t0, t0 + N)
                v3 = lambda t_: t_[:, :N].rearrange("p (c t) -> p c t", t=128)
                for X_, row in ((Xq, 0), (Xk, 1024), (Xv, 2048)):
                    K.dma("sp", X_[:, :N], US[row + h * 128:row + h * 128 + 128, tk], [US], [X_])
                for X_, o_, sc_ in ((Xq, qn, 128 ** -0.5), (Xk, kn, 1.0)):
                    K.act(sq[:, :N], X_[:, :N], AF.Square, [X_], [sq])
                    K.mm(PF[:, :N], ones[:], sq[:, :N], True, True, [ones, sq], [PF])
                    K.act(rn[:, :N], PF[:, :N], AF.Sqrt, [PF], [rn], bias=EPS)
                    K.recip(rn[:, :N], rn[:, :N], [rn], [rn])
                    K.stt(o_[:, :N], X_[:, :N], sc_, rn[:, :N], ALU.mult, ALU.mult, [X_, rn], [o_])
                K.mm(PF[:, :N], selg[:, r, :], GB[d][:, tk], True, True, [selg, GB[d]], [PF])
                K.act(bcG[:, :N], PF[:, :N], AF.Copy, [PF], [bcG])
                K.act(eG[:, :N], PF[:, :N], AF.Exp, [PF], [eG])
                K.mm(PF[:, :N], selb[:, r, :], GB[d][:, tk], True, True, [selb, GB[d]], [PF])
                K.act(bcB[:, :N], PF[:, :N], AF.Copy, [PF], [bcB])
                lastcol = 127 if d == 0 else 0
                glast = v3(bcG)[:, :, lastcol:lastcol + 1]
                K.tt(kbT[:, :N], kn[:, :N], bcB[:, :N], ALU.mult, [kn, bcB], [kbT])
                K.copy(knb[:, :N], kn[:, :N], [kn], [knb], eng="pool")
                K.act(qnb[:, :N], qn[:, :N], AF.Copy, [qn], [qnb])
                K.stt(nKBG[:, :N], kbT[:, :N], -1.0, eG[:, :N], ALU.mult, ALU.mult, [kbT, eG], [nKBG])
                K.tt(Qd[:, :N], qn[:, :N], eG[:, :N], ALU.mult, [qn, eG], [Qd], eng="pool")
                K.tt(v3(tmp), glast.to_broadcast([128, n, 128]), v3(bcG), ALU.subtract, [bcG], [tmp])
                K.act(tmp[:, :N], tmp[:, :N], AF.Exp, [tmp], [tmp])
                K.tt(Ktl[:, :N], kn[:, :N], tmp[:, :N], ALU.mult, [kn, tmp], [Ktl])
                K.act(vb[:, :N], Xv[:, :N], AF.Copy, [Xv], [vb])
                transp(Ktl, Ktt, n)
                transp(vb, Vt, n)
                gct = tokT[d][:, c0:c0 + n, r:r + 1].to_broadcast([128, n, 128])
                bg3 = v3(bcG)
                K.tt(Dtmp[:, :n, :], bg3, bigm[:, d, 0, :].unsqueeze(1).to_broadcast([128, n, 128]), ALU.add, [bcG, bigm], [Dtmp])
                K.tt(Dtmp[:, :n, :], Dtmp[:, :n, :], gct, ALU.subtract, [Dtmp, tokT[d]], [Dtmp], eng="pool")
                K.act(Dtmp[:, :n, :], Dtmp[:, :n, :], AF.Exp, [Dtmp], [Dtmp], scale=-1.0)
                K.tt(Dc[:, :n, 0, :], Dtmp[:, :n, :], offd[:].unsqueeze(1).to_broadcast([128, n, 128]), ALU.mult, [Dtmp, offd], [Dc],
                     eng="pool")
                K.tt(DiT[:, :n, :], bg3, bigm[:, d, 1, :].unsqueeze(1).to_broadcast([128, n, 128]), ALU.add, [bcG, bigm], [DiT])
                K.tt(DiT[:, :n, :], DiT[:, :n, :], gct, ALU.subtract, [DiT, tokT[d]], [DiT], eng="pool")
                K.act(DiT[:, :n, :], DiT[:, :n, :], AF.Exp, [DiT], [DiT])
                K.tt(Dc[:, :n, 1, :], DiT[:, :n, :], offd[:].unsqueeze(1).to_broadcast([128, n, 128]), ALU.mult, [DiT, offd], [Dc],
                     eng="pool")
                LLv = LLg[:].rearrange("p a b t -> p (a b) t")
                for j in range(n):
                    cs = slice(j * 128, (j + 1) * 128)
                    cnt["pg"] += 1
                    G = PG[cnt["pg"] % 2]
                    K.mm(G[:, 0, :], kbT[:, cs], knb[:, cs], True, True, [kbT, knb], [G])
                    K.mm(G[:, 1, :], knb[:, cs], kbT[:, cs], True, True, [kbT, knb], [G])
                    K.mm(G[:, 2, :], knb[:, cs], qnb[:, cs], True, True, [qnb, knb], [G])
                    K.tt(LLv[:, 2 * j:2 * j + 2, :], G[:, 0:2, :], Dc[:, j, :, :], ALU.mult, [G, Dc], [LLg])
                    K.tt(QKt[:, j, :], G[:, 2, :], DiT[:, j, :], ALU.mult, [G, DiT], [QKt])
                for _ in inverse_units(K, C, LLg, n, d, XTb, IW, nunits=n):
                    pass
                jl = list(range(n)) if d == 0 else list(range(n - 1, -1, -1))
                for j in jl:
                    cs = slice(j * 128, (j + 1) * 128)
                    c = c0 + j
                    K.mm(PSq[:, 0:128], nKBG[:, cs], Sb[:], True, True, [nKBG, Sb], [PSq])
                    K.stt(P1s[:], Vt[:, j, :], tokT[d][:, c, 32 + r:33 + r], PSq[:, 0:128], ALU.mult, ALU.add,
                          [Vt, tokT[d], PSq], [P1s])
                    K.mm(PSq[:, 128:256], XTb[:, j, :], P1s[:], True, True, [XTb, P1s], [PSq])
                    K.act(VNs[:], PSq[:, 128:256], AF.Copy, [PSq], [VNs])
                    if latent:
                        K.mm(PSq[:, 256:384], Qd[:, cs], Sb[:], True, False, [Qd, Sb], [PSq])
                        K.mm(PSq[:, 256:384], QKt[:, j, :], VNs[:], False, True, [QKt, VNs], [PSq])
                    K.mm(PSh[:, 0:128], Ktt[:, j, :], VNs[:], True, True, [Ktt, VNs], [PSh])
                    gl = eG[:, j * 128 + lastcol:j * 128 + lastcol + 1]
                    K.stt(Sf[:], Sf[:], gl, PSh[:, 0:128], ALU.mult, ALU.add, [Sf, eG, PSh], [Sf])
                    K.act(Sb[:], Sf[:], AF.Copy, [Sf], [Sb])
                    if latent:
                        cg = c - 2
                        if d == 0:
                            K.act(obuf[:, cg, :], PSq[:, 256:384], AF.Copy, [PSq], [obuf])
                        else:
                            K.tt(otot[:, j, :], obuf[:, cg, :], PSq[:, 256:384], ALU.add, [obuf, PSq], [otot])
                if d == 1 and latent:
                    if C["dbg"]:
                        K.dma("sp", C["dbg"]["of"][h, :, c0 - 2:c0 - 2 + n, :], otot[:], [otot], [])
                    K.tt(osq[:], otot[:], otot[:], ALU.mult, [otot], [osq], eng="pool")
                    K.S.add("dve", lambda e_: e_.tensor_reduce(out=ss[:], in_=osq[:], axis=AX.X, op=ALU.add), _bufs([osq]), _bufs([ss]))
                    K.act(ss[:], ss[:], AF.Sqrt, [ss], [ss], scale=1.0 / 128, bias=EPS)
                    K.recip(ss[:], ss[:], [ss], [ss])
                    K.tt(osq[:], otot[:], ss[:].unsqueeze(2).to_broadcast([128, 4, 128]), ALU.mult, [otot, ss], [osq])
                    K.tt(onb[:], osq[:], gnw[:].unsqueeze(1).to_broadcast([128, 4, 128]), ALU.mult, [osq, gnw], [onb])
                    K.dma("sp", sz[:, :N], SZ[h * 128:(h + 1) * 128, t0 - 256:t0 - 256 + N], [SZ], [sz])
                    half = cnt["tr"] % 2
                    cnt["tr"] += 1
                    for j in range(n):
                        K.tr(PTr[:, half * 4 + j, :], onb[:, j, :], ident_b[:], [onb, ident_b], [PTr])
                    K.tt(obT[:, h, t0 - 256:t0 - 256 + N].rearrange("p (j t) -> p j t", t=128), PTr[:, half * 4:half * 4 + n, :],
                         sz[:, :N].rearrange("p (j t) -> p j t", t=128), ALU.mult, [PTr, sz], [obT])


def write_mods(K, C):
    modT, ident_f, MODS = C["modT"], C["ident_f"], C["MODS"]
    with ExitStack() as s0:
        pt = K.ps(s0, [16, 2, 128], F32)
        rows = K.sb(s0, [16, 2, 128], F32)
        for i, sec in enumerate((2, 5)):
            K.tr(pt[:, i, :], modT[:, sec * 16:(sec + 1) * 16, 0], ident_f[:], [modT, ident_f], [pt])
        K.copy(rows[:], pt[:], [pt], [rows])
        for i in range(2):
            K.dma("sp", MODS[i * 16:(i + 1) * 16, :], rows[:, i, :], [rows], [MODS])
    K.S.barrier()


def merge_phase(K, top, C):
    oaT, obT, ident_b, ident_f, modT, s2 = C["oaT"], C["obT"], C["ident_b"], C["ident_f"], C["modT"], C["s2"]
    SG, X1, x_d, out_d = C["SG"], C["X1"], C["x"], C["out"]
    MODS = C["MODS"]
    dbg = C["dbg"]
    bc = K.sb(top, [128, 1, D], F32, "bc_rows")
    K.dma("sp", bc[:, 0, :], MODS[0:16, :].rearrange("(o a) b -> o (a b)", o=1).to_broadcast([128, D]), [MODS], [bc])
    K.S.barrier()
    p3 = ExitStack()
    mT = K.sb(p3, [128, 16, TL], BF16, "mT")
    with ExitStack() as s1:
        wa = [K.sb(s1, [128, 8, 256], BF16) for _ in range(2)]
        wbb = [K.sb(s1, [128, 8, 256], BF16) for _ in range(2)]
        sga = [K.sb(s1, [128, TL], BF16)] * 2
        sgb = [K.sb(s1, [128, TL], BF16)] * 2
        t1 = [K.sb(s1, [128, 512], F32) for _ in range(2)]
        t2 = [K.sb(s1, [128, 512], F32) for _ in range(2)]
        pa = [K.ps(s1, [128, 512], F32) for _ in range(2)]
        pb = [K.ps(s1, [128, 512], F32) for _ in range(2)]
        it = 0
        for sc in range(8):
            K.dma("pool", wa[sc % 2][:], C["p_a"][:, sc * 256:(sc + 1) * 256].rearrange("(k p) n -> p k n", p=128), [], [wa[sc % 2]])
            K.dma("pool", wbb[sc % 2][:], C["p_b"][:, sc * 256:(sc + 1) * 256].rearrange("(k p) n -> p k n", p=128), [], [wbb[sc % 2]])
            for ff in range(2):
                f = sc * 2 + ff
                ga, gb = sga[f % 2], sgb[f % 2]
                K.dma("sp", ga[:], SG[f * 128:(f + 1) * 128, :], [SG], [ga])
                K.dma("sp", gb[:], SG[2048 + f * 128:2048 + (f + 1) * 128, :], [SG], [gb])
                for n in range(4):
                    ts_ = slice(n * 512, (n + 1) * 512)
                    A_, B_, T1, T2 = pa[it % 2], pb[it % 2], t1[it % 2], t2[it % 2]
                    it += 1
                    for k in range(8):
                        K.mm(A_[:], wa[sc % 2][:, k, ff * 128:(ff + 1) * 128], oaT[:, k, ts_], k == 0, k == 7, [wa[sc % 2], oaT], [A_])
                    for k in range(8):
                        K.mm(B_[:], wbb[sc % 2][:, k, ff * 128:(ff + 1) * 128], obT[:, k, ts_], k == 0, k == 7, [wbb[sc % 2], obT], [B_])
                    K.tt(T1[:], A_[:], ga[:, ts_], ALU.mult, [A_, ga], [T1])
                    K.tt(T2[:], B_[:], gb[:, ts_], ALU.mult, [B_, gb], [T2])
                    K.tt(mT[:, f, ts_], T1[:], T2[:], ALU.add, [T1, T2], [mT], eng="pool")
    K.S.barrier()
    with ExitStack() as s2_:
        wo = [[K.sb(s2_, [128, 8, 512], BF16) for _ in range(2)] for _ in range(2)]
        xt = [K.sb(s2_, [128, 512], F32) for _ in range(3)]
        tt_ = [K.sb(s2_, [128, 512], F32) for _ in range(3)]
        pp = [K.ps(s2_, [128, 512], F32) for _ in range(4)]
        it = 0
        for n in range(4):
            ns = slice(n * 512, (n + 1) * 512)
            w = wo[n % 2]
            for kh in range(2):
                K.dma("pool", w[kh][:], C["w_out"][kh * 1024:(kh + 1) * 1024, ns].rearrange("(k p) n -> p k n", p=128), [], [w[kh]])
            for t in range(16):
                P, X_, T_ = pp[it % 4], xt[it % 3], tt_[it % 3]
                it += 1
                K.dma("sp", X_[:], x_d[t * 128:(t + 1) * 128, ns], [], [X_])
                for k in range(16):
                    K.mm(P[:], mT[:, k, t * 128:(t + 1) * 128], w[k // 8][:, k % 8, :], k == 0, k == 15, [mT, w[k // 8]], [P])
                K.tt(T_[:], P[:], bc[:, 0, ns], ALU.mult, [P, bc], [T_])
                K.tt(T_[:], T_[:], X_[:], ALU.add, [T_, X_], [T_], eng="pool")
                K.dma("sp", X1[t * 128:(t + 1) * 128, ns], T_[:], [T_], [X1])
    p3.close()


def ffn_phase(K, top, C):
    ident_b, ident_f, modT, s2 = C["ident_b"], C["ident_f"], C["modT"], C["s2"]
    X1, out_d, MODS = C["X1"], C["out"], C["MODS"]
    G = 512
    with ExitStack() as s4:
        bc = K.sb(s4, [128, 3, D], F32, "bc_rows4")
        K.dma("sp", bc[:, 1, :], MODS[16:32, :].rearrange("(o a) b -> o (a b)", o=1).to_broadcast([128, D]), [MODS], [bc])
        K.dma("sp", bc[:, 2, :], C["fnw"][0:1, :].to_broadcast([128, D]), [], [bc])
        h2T = K.sb(s4, [128, 16, G], BF16, "h2T")
        actT = K.sb(s4, [128, 44, G], BF16, "actT")
        x1t = [K.sb(s4, [128, D], F32, "x1t%d" % i) for i in range(4)]
        xb = K.sb(s4, [128, D], BF16)
        junk = K.sb(s4, [128, D], BF16)
        ss = K.sb(s4, [128, 1], F32); rs = K.sb(s4, [128, 1], F32)
        wg = [[K.sb(s4, [128, 8, 256], BF16) for _ in range(2)] for _ in range(2)]
        wu = [[K.sb(s4, [128, 8, 256], BF16) for _ in range(2)] for _ in range(2)]
        wd = [K.sb(s4, [128, 4, 512], BF16) for _ in range(4)]
        sgt = [K.sb(s4, [128, G], F32) for _ in range(2)]
        tq = [K.sb(s4, [128, 512], F32) for _ in range(2)]
        ot = [K.sb(s4, [128, D], F32) for _ in range(2)]
        ptr = [K.ps(s4, [128, 8, 128], BF16) for _ in range(1)]
        pgu = [K.ps(s4, [128, 512], F32) for _ in range(3)]
        pdn = [K.ps(s4, [128, 512], F32) for _ in range(4)]
        igu = 0
        for grp in range(NGRP):
            for t in range(4):
                X_ = x1t[t]
                row0 = grp * G + t * 128
                K.dma("sp", X_[:], X1[row0:row0 + 128, :], [X1], [X_])
                K.act(junk[:], X_[:], AF.Square, [X_], [junk, ss], accum=ss[:])
                K.act(rs[:], ss[:], AF.Sqrt, [ss], [rs], scale=1.0 / D, bias=EPS)
                K.recip(rs[:], rs[:], [rs], [rs])
                K.act(xb[:], X_[:], AF.Copy, [X_, rs], [xb], scale=rs[:])
                for g in range(4):
                    P = ptr[0]
                    for j in range(4):
                        k = g * 4 + j
                        K.tr(P[:, j, :], xb[:, k * 128:(k + 1) * 128], ident_b[:], [xb, ident_b], [P])
                    for j in range(4):
                        k = g * 4 + j
                        K.act(h2T[:, k, t * 128:(t + 1) * 128], P[:, j, :], AF.Identity, [P, s2, modT], [h2T],
                              scale=s2[:, k:k + 1], bias=modT[:, 48 + k, 0:1])
            for sc in range(22):
                w1, w2 = wg[sc % 2], wu[sc % 2]
                for kh in range(2):
                    K.dma("pool", w1[kh][:], C["w_gu"][kh * 1024:(kh + 1) * 1024, sc * 256:(sc + 1) * 256].rearrange("(k p) n -> p k n", p=128),
                          [], [w1[kh]])
                    K.dma("pool", w2[kh][:], C["w_gu"][kh * 1024:(kh + 1) * 1024, FFN + sc * 256:FFN + (sc + 1) * 256].rearrange("(k p) n -> p k n", p=128),
                          [], [w2[kh]])
                for jj in range(2):
                    j = sc * 2 + jj
                    Pg, Pu = pgu[igu % 3], pgu[(igu + 1) % 3]
                    SGt = sgt[(igu // 2) % 2]
                    igu += 2
                    for k in range(16):
                        K.mm(Pg[:, :G], w1[k // 8][:, k % 8, jj * 128:(jj + 1) * 128], h2T[:, k, :], k == 0, k == 15, [w1[k // 8], h2T], [Pg])
                    for k in range(16):
                        K.mm(Pu[:, :G], w2[k // 8][:, k % 8, jj * 128:(jj + 1) * 128], h2T[:, k, :], k == 0, k == 15, [w2[k // 8], h2T], [Pu])
                    K.act(SGt[:], Pg[:, :G], AF.Silu, [Pg], [SGt])
                    K.tt(actT[:, j, :], SGt[:], Pu[:, :G], ALU.mult, [SGt, Pu], [actT])
            iw = 0
            for n in range(4):
                ns = slice(n * 512, (n + 1) * 512)
                for k4 in range(11):
                    W = wd[iw % 4]
                    iw += 1
                    K.dma("pool", W[:], C["w_dn"][k4 * 512:(k4 + 1) * 512, ns].rearrange("(k p) n -> p k n", p=128), [], [W])
                    for kk in range(4):
                        k = k4 * 4 + kk
                        for t in range(4):
                            K.mm(pdn[t][:], actT[:, k, t * 128:(t + 1) * 128], W[:, kk, :], k == 0, k == 43, [actT, W], [pdn[t]])
                for t in range(4):
                    T_ = tq[t % 2]
                    K.tt(T_[:], pdn[t][:], bc[:, 1, ns], ALU.mult, [pdn[t], bc], [T_])
                    K.tt(x1t[t][:, ns], x1t[t][:, ns], T_[:], ALU.add, [x1t[t], T_], [x1t[t]], eng="pool")
            for t in range(4):
                X_ = x1t[t]
                O_ = ot[t % 2]
                row0 = grp * G + t * 128
                K.act(junk[:], X_[:], AF.Square, [X_], [junk, ss], accum=ss[:])
                K.act(rs[:], ss[:], AF.Sqrt, [ss], [rs], scale=1.0 / D, bias=EPS)
                K.recip(rs[:], rs[:], [rs], [rs])
                K.stt(O_[:], X_[:], rs[:], bc[:, 2, :], ALU.mult, ALU.mult, [X_, rs, bc], [O_])
                K.dma("sp", out_d[row0:row0 + 128, :], O_[:], [O_], [out_d])

NHP = 8
NGH = 8
NGRP = 4
RELAX = [True, True, True, True, True]
RATIO = 4


def build(debug=False, only=None):
    nc = bass.Bass("TRN2", target_bir_lowering=False)
    K = KB(nc)
    inp = lambda name, shape: K.dram(name, shape, F32, kind="ExternalInput")
    x_d = inp("x", [TL, D])
    ctx_d = inp("ctx", [TC, D])
    cc_d = inp("cc", [128, 16, 2])
    wada_d = inp("w_ada", [D, 6 * D])
    bada_d = inp("b_ada", [128, 96])
    n1w_d = inp("norm1_w", [128, 16])
    n2w_d = inp("norm2_w", [128, 16])
    fnw_d = inp("final_norm_w", [1, D])
    win_d = inp("w_in", [D, NIN * 128])
    mu_d = inp("rw_mu", [128, 29])
    cmask_d = inp("cmask", [128, 8])
    convw_d = inp("gdn_conv_w", [128, 24, 5])
    ident_d = inp("ident", [128, 128])
    w0_d = inp("rw_w0", [128, 2, 8])
    a0_d = inp("rw_a0", [128, 2, 8])
    kkw_d = inp("rw_k_k", [128, 8])
    ka_d = inp("rw_k_a", [128, 8])
    rk_d = inp("rw_r_k", [128, 8])
    w2_d = inp("rw_w2", [2, 96, 1024])
    a2_d = inp("rw_a2", [2, 96, 1024])
    g2_d = inp("rw_g2", [64, 1024])
    lnw_d = inp("rw_ln_w", [1, 1024])
    lnb_d = inp("rw_ln_b", [1, 1024])
    m1_d = inp("m1", [128, 2, 4, 128])
    m2_d = inp("m2", [128, 2, 3, 128])
    rmask_d = inp("rmask", [128, 512])
    bones_d = inp("blockones", [128, 128])
    hsel_d = inp("headsel", [128, 2])
    nmask_d = inp("nmask", [128, 2, 7, 128])
    selg_d = inp("selg", [64, 16, 128])
    selb_d = inp("selb", [64, 16, 128])
    bigm_d = inp("bigm", [128, 2, 2, 128])
    offd_d = inp("offd", [128, 128])
    ones_d = inp("ones", [128, 128])
    alog_d = inp("gdn_a_log", [64, 1])
    dtb_d = inp("gdn_dt_bias", [64, 1])
    gnw_d = inp("gdn_norm_w", [1, 128])
    pa_d = inp("merge_p_a", [1024, D])
    pb_d = inp("merge_p_b", [1024, D])
    wout_d = inp("w_out", [D, D])
    wgu_d = inp("ffn_w_gate_up", [D, 2 * FFN])
    wdn_d = inp("ffn_w_down", [FFN, D])
    MODS = K.dram("MODS", [32, 128], F32)
    out_d = K.dram("out", [TL, D], F32, kind="ExternalOutput")
    dbg = {}
    if debug:
        dbg["xs"] = K.dram("dbg_xs", [24 * 128, TT], F32, kind="ExternalOutput")
        dbg["u"] = K.dram("dbg_u", [24 * 128, TT], F32, kind="ExternalOutput")
        dbg["mod"] = K.dram("dbg_mod", [128, 96 * 2], F32, kind="ExternalOutput")
        dbg["sm"] = K.dram("dbg_sm", [128, 5, TT], F32, kind="ExternalOutput")
        dbg["oa"] = K.dram("dbg_oa", [128, 8, TL], F32, kind="ExternalOutput")
        dbg["yf"] = K.dram("dbg_yf", [8, 128, 16, 128], F32, kind="ExternalOutput")
        dbg["ob"] = K.dram("dbg_ob", [128, 8, TL], F32, kind="ExternalOutput")
        dbg["of"] = K.dram("dbg_of", [8, 128, 16, 128], F32, kind="ExternalOutput")
    if only:
        XS = inp("XS_in", [24 * 128, TT])
        small_in = inp("small_in", [128, 5, TT])
        US = inp("US_in", [24 * 128, TT])
        SZ = inp("SZ_in", [8 * 128, TL])
        ab_in = inp("ab_in", [128, TT])
    else:
        XS = dbg["xs"] if debug else K.dram("XS", [24 * 128, TT], F32)
    if not only:
        US = dbg["u"] if debug else K.dram("US", [24 * 128, TT], F32)
        SZ = K.dram("SZ", [8 * 128, TL], F32)
    SG = K.dram("SG", [32 * 128, TL], BF16)
    X1 = inp("X1_in", [TL, D]) if only == "ffn" else K.dram("X1", [TL, D], F32)

    with ExitStack() as top:
        ident_f = K.sb(top, [128, 128], F32, "ident_f")
        ident_b = K.sb(top, [128, 128], BF16, "ident_b")
        K.dma("sp", ident_f[:], ident_d[:, :], [], [ident_f])
        K.copy(ident_b[:], ident_f[:], [ident_f], [ident_b])
        modT = K.sb(top, [128, 96, 2], F32, "modT")
        s1 = K.sb(top, [128, 16, 2], F32, "s1")
        s2 = K.sb(top, [128, 16], F32, "s2")
        scopeO = ExitStack()
        abT = K.sb(scopeO, [128, TT], F32, "abT")
        oaT = K.sb(scopeO, [128, 8, TL], BF16, "oaT")
        scopeA = ExitStack()
        smallT = K.sb(scopeA, [128, 5, TT], BF16, "smallT")

        if only == "rwkv":
            K.dma("pool", smallT[:], small_in[:, :, :], [], [smallT])
            K.dma("sp", abT[:], ab_in[:, :], [], [abT])
        with ExitStack() as p0:
          if only != "rwkv":
                ccf = K.sb(p0, [128, 16, 2], F32)
                ccs = K.sb(p0, [128, 16, 2], F32)
                ccb = K.sb(p0, [128, 16, 2], BF16)
                bada = K.sb(p0, [128, 96], F32)
                n1w = K.sb(p0, [128, 16], F32)
                n2w = K.sb(p0, [128, 16], F32)
                K.dma("sp", ccf[:], cc_d[:, :, :], [], [ccf])
                K.dma("sp", bada[:], bada_d[:, :], [], [bada])
                K.dma("sp", n1w[:], n1w_d[:, :], [], [n1w])
                K.dma("sp", n2w[:], n2w_d[:, :], [], [n2w])
                K.act(ccs[:], ccf[:], AF.Silu, [ccf], [ccs])
                K.copy(ccb[:], ccs[:], [ccs], [ccb])
                wb = [[K.sb(p0, [128, 8, 512], BF16) for _ in range(2)] for _ in range(2)]
                pm = K.ps(p0, [128, 96, 2], F32)
                for sc in range(24):
                    w = wb[sc % 2]
                    for kh in range(2):
                        K.dma("pool", w[kh][:],
                              wada_d[kh * 1024:(kh + 1) * 1024, sc * 512:(sc + 1) * 512].rearrange("(k p) n -> p k n", p=128),
                              [], [w[kh]])
                    for jj in range(4):
                        j = sc * 4 + jj
                        for k in range(16):
                            K.mm(pm[:, j, :], w[k // 8][:, k % 8, jj * 128:(jj + 1) * 128], ccb[:, k, :], k == 0, k == 15,
                                 [w[k // 8], ccb], [pm])
                K.tt(modT[:], pm[:], bada[:].unsqueeze(2).to_broadcast([128, 96, 2]), ALU.add, [pm, bada], [modT])
                for v in range(2):
                    K.stt(s1[:, :, v], modT[:, 16:32, v], 1.0, n1w[:], ALU.add, ALU.mult, [modT, n1w], [s1])
                K.stt(s2[:], modT[:, 64:80, 0], 1.0, n2w[:], ALU.add, ALU.mult, [modT, n2w], [s2])
                if debug:
                    K.dma("sp", dbg["mod"][:, :], modT[:].rearrange("p a b -> p (a b)"), [modT], [])

        K.S.barrier()
        with ExitStack() as p1:
          if not only:
                hT = K.sb(p1, [128, 16, TT], BF16, "hT")
                mu = K.sb(p1, [128, 29], F32)
                omm = K.sb(p1, [128, 29], F32)
                cmask = K.sb(p1, [128, 8], F32)
                coef = K.sb(p1, [128, 6, 29], F32)
                convw = K.sb(p1, [128, 24, 5], F32)
                K.dma("sp", mu[:], mu_d[:, :], [], [mu])
                K.dma("sp", cmask[:], cmask_d[:, :], [], [cmask])
                K.dma("sp", convw[:], convw_d[:, :, :], [], [convw])
                K.ts(omm[:], mu[:], -1.0, 1.0, ALU.mult, ALU.add, [mu], [omm])
                for m in range(6):
                    K.ts(coef[:, m, :], mu[:], cmask[:, m:m + 1], None, ALU.mult, None, [mu, cmask], [coef])
                with ExitStack() as pa:
                    xt = [K.sb(pa, [128, D], F32) for _ in range(2)]
                    xb = [K.sb(pa, [128, D], BF16) for _ in range(2)]
                    junk = K.sb(pa, [128, D], BF16)
                    ss = [K.sb(pa, [128, 1], F32) for _ in range(2)]
                    rs = [K.sb(pa, [128, 1], F32) for _ in range(2)]
                    pt = [K.ps(pa, [128, 8, 128], BF16) for _ in range(2)]
                    npt = 0
                    for t in range(NCH):
                        X, XB, SS, RS = xt[t % 2], xb[t % 2], ss[t % 2], rs[t % 2]
                        src = ctx_d[t * 128:(t + 1) * 128, :] if t < 2 else x_d[(t - 2) * 128:(t - 1) * 128, :]
                        v = 1 if t < 2 else 0
                        K.dma("sp", X[:], src, [], [X])
                        K.act(junk[:], X[:], AF.Square, [X], [junk, SS], accum=SS[:])
                        K.act(RS[:], SS[:], AF.Sqrt, [SS], [RS], scale=1.0 / D, bias=EPS)
                        K.recip(RS[:], RS[:], [RS], [RS])
                        K.act(XB[:], X[:], AF.Copy, [X, RS], [XB], scale=RS[:])
                        for g in range(4):
                            P = pt[npt % 2]
                            npt += 1
                            for j in range(4):
                                k = g * 4 + j
                                K.tr(P[:, j, :], XB[:, k * 128:(k + 1) * 128], ident_b[:], [XB, ident_b], [P])
                            for j in range(4):
                                k = g * 4 + j
                                K.act(hT[:, k, t * 128:(t + 1) * 128], P[:, j, :], AF.Identity, [P, s1, modT], [hT],
                                      scale=s1[:, k, v:v + 1], bias=modT[:, k, v:v + 1])
                K.S.relax = RELAX[0]
                K.S.barrier()
                with ExitStack() as pb:
                    wb = [[K.sb(pb, [128, 8, 512], BF16) for _ in range(2)] for _ in range(2)]
                    stage = [K.sb(pb, [128, TT], F32) for _ in range(2)]
                    post = [K.sb(pb, [128, TT], F32) for _ in range(1)] * 2
                    postb = [K.sb(pb, [128, TL], BF16) for _ in range(1)] * 2
                    pp = [K.ps(pb, [128, 512], F32) for _ in range(4)]
                    npp = 0
                    ntile = [(0, 256)] + [(256 + i * 512, 512) for i in range(4)]
                    for sc in range(24):
                        ncol = min(512, NIN * 128 - sc * 512)
                        w = wb[sc % 2]
                        for kh in range(2):
                            K.dma("pool", w[kh][:, :, :ncol],
                                  win_d[kh * 1024:(kh + 1) * 1024, sc * 512:sc * 512 + ncol].rearrange("(k p) n -> p k n", p=128),
                                  [], [w[kh]])
                        for jj in range(ncol // 128):
                            q = sc * 4 + jj
                            lat_only = (53 <= q <= 60) or q >= 62
                            stg = stage[q % 2]
                            for (t0, tn) in ntile:
                                if lat_only and t0 == 0:
                                    continue
                                P = pp[npp % 4]
                                npp += 1
                                for k in range(16):
                                    K.mm(P[:, :tn], w[k // 8][:, k % 8, jj * 128:(jj + 1) * 128], hT[:, k, t0:t0 + tn],
                                         k == 0, k == 15, [w[k // 8], hT], [P])
                                if q <= 28 or 29 <= q <= 52 or q == 61:
                                    K.act(stg[:, t0:t0 + tn], P[:, :tn], AF.Copy, [P], [stg])
                                elif 53 <= q <= 60:
                                    K.act(stg[:, t0:t0 + tn], P[:, :tn], AF.Silu, [P], [stg])
                                else:
                                    K.act(postb[q % 2][:, t0 - 256:t0 - 256 + tn], P[:, :tn], AF.Sigmoid, [P], [postb[q % 2]])
                            if q <= 28:
                                xs = post[q % 2]
                                K.ts(xs[:], stg[:], omm[:, q:q + 1], None, ALU.mult, None, [stg, omm], [xs])
                                pl = stg[:, 256:TT].rearrange("p (r c) -> p r c", c=64)
                                xl = xs[:, 256:TT].rearrange("p (r c) -> p r c", c=64)
                                sh = [(xl[:, :, 1:64], pl[:, :, 0:63]), (xl[:, :, 0:63], pl[:, :, 1:64]),
                                      (xl[:, 1:32, :], pl[:, 0:31, :]), (xl[:, 0:31, :], pl[:, 1:32, :]),
                                      (xs[:, 1:256], stg[:, 0:255]), (xs[:, 0:255], stg[:, 1:256])]
                                for m, (o, i) in enumerate(sh):
                                    K.stt(o, i, coef[:, m, q:q + 1], o, ALU.mult, ALU.add, [stg, xs, coef], [xs])
                                if q < 24:
                                    K.dma("sp", XS[q * 128:(q + 1) * 128, :], xs[:], [xs], [XS])
                                elif q < 26:
                                    K.act(smallT[:, q - 24, :], xs[:], AF.Tanh, [xs], [smallT])
                                elif q < 28:
                                    K.act(smallT[:, q - 24, :], xs[:], AF.Copy, [xs], [smallT])
                                else:
                                    K.act(smallT[:, 4, :], xs[:], AF.Sigmoid, [xs], [smallT])
                            elif q <= 52:
                                g = q - 29
                                acc = post[q % 2]
                                K.ts(acc[:], stg[:], convw[:, g, 2:3], None, ALU.mult, None, [stg, convw], [acc])
                                for (a, b) in ((0, 256), (256, TT)):
                                    for j, o in ((0, 2), (1, 1), (3, -1), (4, -2)):
                                        if o > 0:
                                            ov, iv = acc[:, a + o:b], stg[:, a:b - o]
                                        else:
                                            ov, iv = acc[:, a:b + o], stg[:, a - o:b]
                                        K.stt(ov, iv, convw[:, g, j:j + 1], ov, ALU.mult, ALU.add, [stg, acc, convw], [acc])
                                K.act(acc[:], acc[:], AF.Silu, [acc], [acc])
                                K.dma("sp", US[g * 128:(g + 1) * 128, :], acc[:], [acc], [US])
                            elif q <= 60:
                                K.dma("sp", SZ[(q - 53) * 128:(q - 52) * 128, :], stg[:, 256:TT], [stg], [SZ])
                            elif q == 61:
                                K.copy(abT[:], stg[:], [stg], [abT], eng="pool")
                            else:
                                K.dma("sp", SG[(q - 62) * 128:(q - 61) * 128, :], postb[q % 2][:], [postb[q % 2]], [SG])
                K.S.barrier()
                if debug:
                    smf = K.sb(p1, [128, 5, TT], F32)
                    K.copy(smf[:], smallT[:], [smallT], [smf])
                    K.dma("sp", dbg["sm"][:, :, :], smf[:], [smf], [])

        K.S.relax = RELAX[3]
        K.S.barrier()
        with ExitStack() as p2:
          if only != "ffn":
            rwkv_phase(K, p2, dict(XS=XS, smallT=smallT, oaT=oaT, ident_b=ident_b, ident_f=ident_f,
                                   w0=w0_d, a0=a0_d, kkw=kkw_d, ka=ka_d, rk=rk_d, w2=w2_d, a2=a2_d, g2=g2_d,
                                   lnw=lnw_d, lnb=lnb_d, m1=m1_d, m2=m2_d, rmask=rmask_d, bones=bones_d, hsel=hsel_d, nmask=nmask_d,
                                   dbg=dbg, nhp=NHP))
        K.S.barrier()
        if debug:
            with ExitStack() as pd:
                of = K.sb(pd, [128, 8, TL], F32)
                K.copy(of[:, 0:NHP], oaT[:, 0:NHP], [oaT], [of])
                K.dma("sp", dbg["oa"][:, 0:NHP, :], of[:, 0:NHP], [of], [])
            K.S.barrier()
        scopeA.close()
        obT = K.sb(scopeO, [128, 8, TL], BF16, "obT")
        K.S.relax = RELAX[4]
        with ExitStack() as p2b:
          if only != "ffn":
            gdn_phase(K, p2b, dict(US=US, SZ=SZ, abT=abT, obT=obT, ident_b=ident_b, ident_f=ident_f, selg=selg_d, selb=selb_d,
                                   bigm=bigm_d, offd=offd_d, ones=ones_d, rmask=rmask_d, alog=alog_d, dtb=dtb_d, gnw=gnw_d,
                                   nmask=nmask_d, dbg=dbg, nh=NGH))
        K.S.barrier()
        if debug:
            with ExitStack() as pd:
                of = K.sb(pd, [128, 8, TL], F32)
                K.copy(of[:, 0:NGH], obT[:, 0:NGH], [obT], [of])
                K.dma("sp", dbg["ob"][:, 0:NGH, :], of[:, 0:NGH], [of], [])
            K.S.barrier()
        C34 = dict(oaT=oaT, obT=obT, ident_b=ident_b, ident_f=ident_f, modT=modT, s2=s2, SG=SG, X1=X1,
                   x=x_d, out=out_d, MODS=MODS, fnw=fnw_d, p_a=pa_d, p_b=pb_d, w_out=wout_d, w_gu=wgu_d,
                   w_dn=wdn_d, dbg=dbg)
        K.S.relax = RELAX[1]
        if only != "rwkv":
            write_mods(K, C34)
        if not only:
            merge_phase(K, scopeO, C34)
        scopeO.close()
        K.S.relax = RELAX[2]
        K.S.barrier()
        if only != "rwkv":
            ffn_phase(K, top, C34)
        else:
            with ExitStack() as pz:
                z = K.sb(pz, [128, D], F32)
                K.memset(z[:], 0.0, [z])
                K.dma("sp", out_d[0:128, :], z[:], [z], [out_d])
        K.S.emit(nc, top)
    nc._marks = getattr(K.S, "marks", [])
    return nc


def _fm(v, nchunk):
    return np.ascontiguousarray(np.asarray(v, np.float32).reshape(nchunk, 128).T)


def _pad_cols(a, n):
    out = np.zeros(a.shape[:-1] + (n,), np.float32)
    out[..., :a.shape[-1]] = a
    return out


def prep_shared(inputs):
    w_in = np.asarray(inputs["w_in"][0], np.float32)
    RW = 3520
    segs = [w_in[:, 0:3072]]
    for (a, b) in ((3072, 3168), (3168, 3264), (3264, 3360), (3360, 3456), (3456, 3520)):
        segs.append(_pad_cols(w_in[:, a:b], 128))
    segs.append(w_in[:, RW:RW + 3072 + 1024])
    abc = np.zeros((D, 128), np.float32)
    abc[:, 0:16] = w_in[:, 7616:7632]
    abc[:, 32:48] = w_in[:, 7632:7648]
    segs.append(abc)
    segs.append(w_in[:, 7648:])
    win = np.ascontiguousarray(np.concatenate(segs, axis=1))
    assert win.shape == (D, NIN * 128)
    mu = np.asarray(inputs["rw_mu"][0], np.float32)
    mus = [mu[0:3072]]
    for (a, b) in ((3072, 3168), (3168, 3264), (3264, 3360), (3360, 3456), (3456, 3520)):
        mus.append(_pad_cols(mu[a:b], 128))
    mu_fm = _fm(np.concatenate(mus), 29)
    p = np.arange(128)
    cmask = np.zeros((128, 8), np.float32)
    for m in range(4):
        cmask[:, m] = (p % 4 == m)
    cmask[:, 4] = (p % 2 == 0)
    cmask[:, 5] = (p % 2 == 1)
    convw = np.asarray(inputs["gdn_conv_w"][0], np.float32)
    convw_fm = np.ascontiguousarray(convw.reshape(5, 24, 128).transpose(2, 1, 0))
    sh = {
        "w_ada": np.ascontiguousarray(inputs["w_ada"][0], np.float32),
        "b_ada": _fm(inputs["b_ada"][0], 96),
        "norm1_w": _fm(inputs["norm1_w"][0], 16),
        "norm2_w": _fm(inputs["norm2_w"][0], 16),
        "final_norm_w": np.ascontiguousarray(np.asarray(inputs["final_norm_w"], np.float32).reshape(1, D)),
        "w_in": win,
        "rw_mu": mu_fm,
        "cmask": cmask,
        "gdn_conv_w": convw_fm,
        "ident": np.eye(128, dtype=np.float32),
    }
    g = lambda k: np.asarray(inputs[k][0], np.float32)
    sh["rw_w0"] = np.ascontiguousarray(g("rw_w0").reshape(2, 8, 128).transpose(2, 0, 1))
    sh["rw_a0"] = np.ascontiguousarray(g("rw_a0").reshape(2, 8, 128).transpose(2, 0, 1))
    sh["rw_k_k"] = _fm(g("rw_k_k"), 8)
    sh["rw_k_a"] = _fm(g("rw_k_a"), 8)
    sh["rw_r_k"] = _fm(g("rw_r_k").reshape(-1), 8)
    sh["rw_w2"] = np.ascontiguousarray(g("rw_w2"))
    sh["rw_a2"] = np.ascontiguousarray(g("rw_a2"))
    sh["rw_g2"] = np.ascontiguousarray(g("rw_g2"))
    sh["rw_ln_w"] = np.ascontiguousarray(g("rw_ln_w").reshape(1, 1024))
    sh["rw_ln_b"] = np.ascontiguousarray(g("rw_ln_b").reshape(1, 1024))
    r_ = np.arange(128)[:, None]; c_ = np.arange(128)[None, :]
    SL = (c_ < r_).astype(np.float32); SU = (c_ > r_).astype(np.float32)
    IL = (c_ <= r_).astype(np.float32); IU = (c_ >= r_).astype(np.float32)
    m1 = np.stack([np.stack([SL, SU, SL, SU], 0), np.stack([SU, SL, SU, SL], 0)], 0)
    m2 = np.stack([np.stack([SU, IU, -IU], 0), np.stack([SL, IL, -IL], 0)], 0)
    sh["m1"] = np.ascontiguousarray(m1.transpose(2, 0, 1, 3))
    sh["m2"] = np.ascontiguousarray(m2.transpose(2, 0, 1, 3))
    rmask = np.ones((128, 512), np.float32); rmask[:, ::128] = 0.0
    sh["rmask"] = rmask
    bo = np.zeros((128, 128), np.float32); bo[:64, :64] = 1.0; bo[64:, 64:] = 1.0
    sh["blockones"] = bo
    hs = np.zeros((128, 2), np.float32); hs[:64, 0] = 1.0; hs[64:, 1] = 1.0
    sh["headsel"] = hs
    nmk = np.zeros((2, 7, 128, 128), np.float32)
    for lv in range(7):
        bsz = 1 << lv
        low = ((r_ // (2 * bsz) == c_ // (2 * bsz)) & ((r_ // bsz) % 2 == 1) & ((c_ // bsz) % 2 == 0)).astype(np.float32)
        nmk[0, lv] = -low
        nmk[1, lv] = -low.T
    sh["nmask"] = np.ascontiguousarray(nmk.transpose(2, 0, 1, 3))
    selg = np.zeros((64, 16, 128), np.float32); selb = np.zeros((64, 16, 128), np.float32)
    for r0 in range(16):
        selg[r0, r0, :] = 1.0
        selb[32 + r0, r0, :] = 1.0
    sh["selg"] = selg; sh["selb"] = selb
    BIG = 1.0e4
    bigm = np.stack([np.stack([BIG * SU, -BIG * SL], 0), np.stack([BIG * SL, -BIG * SU], 0)], 0)
    sh["bigm"] = np.ascontiguousarray(bigm.transpose(2, 0, 1, 3))
    sh["offd"] = (1.0 - np.eye(128)).astype(np.float32)
    sh["ones"] = np.ones((128, 128), np.float32)
    al = np.zeros((64, 1), np.float32); al[0:16, 0] = g("gdn_a_log").reshape(-1)
    db = np.zeros((64, 1), np.float32); db[0:16, 0] = g("gdn_dt_bias").reshape(-1)
    sh["gdn_a_log"] = al; sh["gdn_dt_bias"] = db
    sh["gdn_norm_w"] = np.ascontiguousarray(g("gdn_norm_w").reshape(1, 128))
    for k_ in ("merge_p_a", "merge_p_b", "w_out", "ffn_w_gate_up", "ffn_w_down"):
        sh[k_] = np.ascontiguousarray(g(k_))
    return sh


def make_in_maps(inputs):
    sh = prep_shared(inputs)
    maps = []
    for b in range(8):
        m = dict(sh)
        m["x"] = np.ascontiguousarray(inputs["x"][b], np.float32)
        m["ctx"] = np.ascontiguousarray(inputs["ctx"][b], np.float32)
        cc = np.stack([np.asarray(inputs["c"][b], np.float32), np.asarray(inputs["c_ctx"], np.float32)], axis=-1)
        m["cc"] = np.ascontiguousarray(cc.reshape(16, 128, 2).transpose(1, 0, 2))
        maps.append(m)
    return maps


_NC = None


def kernel(**inputs):
    global _NC
    if _NC is None:
        _NC = build()
    maps = make_in_maps(inputs)
    res = run_bass_kernel_spmd(_NC, maps, core_ids=list(range(8)))
    return np.stack([r["out"] for r in res.results], axis=0).astype(np.float32)
```

```python
import numpy as np
from contextlib import ExitStack
import concourse.bass as bass
import concourse.mybir as mybir
from concourse.bass_utils import run_bass_kernel_spmd

F32 = mybir.dt.float32
BF16 = mybir.dt.bfloat16
AF = mybir.ActivationFunctionType
ALU = mybir.AluOpType
AX = mybir.AxisListType

COMPUTE = ("pe", "act", "dve", "pool")
NDSEM = 24

D = 2048
TC = 256
TL = 2048
TT = TC + TL
NCH = TT // 128
NIN = 94
FFN = 5632
EPS = 1e-6
DEC = 0.6065306597126334


class Buf:
    __slots__ = ("name", "lw", "rd")

    def __init__(self, name=""):
        self.name = name
        self.lw = None
        self.rd = {}


class Sched:
    def __init__(self):
        self.ops = []
        self.last = {}
        self.dmas = []
        self.bar = set()
        self.bar_seen = set()

    def pe_strict(self, on):
        if on:
            self._saved_relax = getattr(self, "relax", False)
            self.relax = False
        else:
            self.relax = self._saved_relax
            if self.relax and "pe" in self.last:
                self.pe_fence = self.last["pe"]

    def barrier(self):
        if not hasattr(self, "marks"):
            self.marks = []
        self.marks.append({e: sum(1 for o in self.ops if o[0] == e and not o[3]) for e in COMPUTE})
        self.bar = set(self.last.values()) | set(self.dmas)
        self.dmas = []
        self.bar_seen = set()

    def add(self, eng, fn, reads=(), writes=(), dma=False):
        i = len(self.ops)
        deps = set()
        if eng not in self.bar_seen:
            deps |= self.bar
            self.bar_seen.add(eng)
        self.last[eng] = i
        if dma:
            self.dmas.append(i)
        for b in reads:
            if b.lw is not None:
                deps.add(b.lw)
        for b in writes:
            if b.lw is not None:
                deps.add(b.lw)
            deps.update(b.rd.values())
        key = ("d", i) if dma else eng
        for b in reads:
            b.rd[key] = i
        for b in writes:
            b.lw = i
            b.rd = {}
        if eng == "pe" and getattr(self, "relax", False):
            deps = set(d for d in deps if not (self.ops[d][0] == "pe" and not self.ops[d][3]))
            if getattr(self, "pe_fence", None) is not None:
                deps.add(self.pe_fence)
                self.pe_fence = None
        self.ops.append((eng, fn, deps, dma))
        return i

    def emit(self, nc, stack):
        ops = self.ops
        engs = {"pe": nc.tensor, "act": nc.scalar, "dve": nc.vector, "pool": nc.gpsimd, "sp": nc.sync}
        names = list(engs)
        csem = {e: stack.enter_context(nc.semaphore("c_" + e)) for e in COMPUTE}
        dsem = {e: [stack.enter_context(nc.semaphore("d_%s%d" % (e, k))) for k in range(NDSEM)]
                for e in ("sp", "act", "pool")}
        comp = [None] * len(ops)
        cnt = {e: 0 for e in COMPUTE}
        dcnt = {e: 0 for e in dsem}
        prevslot = [None] * len(ops)
        for i, (eng, fn, deps, dma) in enumerate(ops):
            if dma:
                j = dcnt[eng]
                dcnt[eng] += 1
                comp[i] = (dsem[eng][j % NDSEM], 16 * (j // NDSEM + 1))
                if j >= NDSEM:
                    prevslot[i] = (dsem[eng][j % NDSEM], 16 * (j // NDSEM))
            else:
                cnt[eng] += 1
                comp[i] = (csem[eng], cnt[eng])
        per = {e: [] for e in names}
        for i, op in enumerate(ops):
            per[op[0]].append(i)
        block = stack.enter_context(nc.Block())

        def run(ename):
            def body(e):
                known = {}
                for i in per[ename]:
                    eng, fn, deps, dma = ops[i]
                    need = {}
                    cands = [comp[d] for d in deps]
                    if prevslot[i] is not None:
                        cands.append(prevslot[i])
                    for sm, v in cands:
                        k = id(sm)
                        if known.get(k, 0) >= v:
                            continue
                        if k not in need or need[k][1] < v:
                            need[k] = (sm, v)
                    for k, (sm, v) in need.items():
                        e.wait_ge(sm, v)
                        known[k] = v
                    ins = fn(e)
                    sm, v = comp[i]
                    ins.then_inc(sm, 16 if dma else 1)
                if ename in dsem:
                    last = {}
                    for i in per[ename]:
                        if ops[i][3]:
                            sm, v = comp[i]
                            last[id(sm)] = (sm, v)
                    for sm, v in last.values():
                        e.wait_ge(sm, v)
            return body

        block.tensor(run("pe"))
        block.scalar(run("act"))
        block.vector(run("dve"))
        block.gpsimd(run("pool"))
        block.sync(run("sp"))


class T:
    def __init__(self, h, name):
        self.h = h
        self.b = Buf(name)

    def __getitem__(self, k):
        return self.h[k]


def _bufs(xs):
    return [x if isinstance(x, Buf) else x.b for x in xs]


class KB:
    def __init__(self, nc):
        self.nc = nc
        self.S = Sched()
        self.n = 0

    def sb(self, st, shape, dt, name=None):
        self.n += 1
        name = name or "t%d" % self.n
        if not hasattr(self, "used"):
            self.used = set()
        while name in self.used:
            name = name + "_"
        self.used.add(name)
        return T(st.enter_context(self.nc.sbuf_tensor(name, list(shape), dt)), name)

    def ps(self, st, shape, dt, name=None):
        self.n += 1
        name = name or "p%d" % self.n
        return T(st.enter_context(self.nc.psum_tensor(name, list(shape), dt)), name)

    def dram(self, name, shape, dt, kind="Internal"):
        h = self.nc.dram_tensor(name, list(shape), dt, kind=kind)
        t = T(h.ap(), name)
        return t

    def act(self, out, in_, func, r, w, scale=1.0, bias=0.0, accum=None):
        kw = {}
        if accum is not None:
            kw["accum_out"] = accum
        self.S.add("act", lambda e: e.activation(out=out, in_=in_, func=func, scale=scale, bias=bias, **kw),
                   _bufs(r), _bufs(w))

    def tt(self, out, in0, in1, op, r, w, eng="dve"):
        self.S.add(eng, lambda e: e.tensor_tensor(out=out, in0=in0, in1=in1, op=op), _bufs(r), _bufs(w))

    def ts(self, out, in0, s1, s2, op0, op1, r, w, eng="dve", accum=None):
        kw = {}
        if accum is not None:
            kw["accum_out"] = accum
        if op1 is None:
            self.S.add(eng, lambda e: e.tensor_scalar(out=out, in0=in0, scalar1=s1, scalar2=None, op0=op0, **kw),
                       _bufs(r), _bufs(w))
        else:
            self.S.add(eng, lambda e: e.tensor_scalar(out=out, in0=in0, scalar1=s1, scalar2=s2, op0=op0, op1=op1, **kw),
                       _bufs(r), _bufs(w))

    def stt(self, out, in0, scalar, in1, op0, op1, r, w):
        self.S.add("dve", lambda e: e.scalar_tensor_tensor(out=out, in0=in0, scalar=scalar, in1=in1, op0=op0, op1=op1),
                   _bufs(r), _bufs(w))

    def copy(self, out, in_, r, w, eng="dve"):
        self.S.add(eng, lambda e: e.tensor_copy(out=out, in_=in_), _bufs(r), _bufs(w))

    def memset(self, out, val, w, eng="pool"):
        self.S.add(eng, lambda e: e.memset(out, val), [], _bufs(w))

    def recip(self, out, in_, r, w):
        self.S.add("dve", lambda e: e.reciprocal(out=out, in_=in_), _bufs(r), _bufs(w))

    def scan(self, out, d0, d1, init, op0, op1, r, w):
        self.S.add("dve", lambda e: e.tensor_tensor_scan(out=out, data0=d0, data1=d1, initial=init, op0=op0, op1=op1),
                   _bufs(r), _bufs(w))

    def mm(self, out, lhsT, rhs, start, stop, r, w):
        self.S.add("pe", lambda e: e.matmul(out, lhsT=lhsT, rhs=rhs, start=start, stop=stop), _bufs(r), _bufs(w))

    def tr(self, out, in_, ident, r, w):
        self.S.add("pe", lambda e: e.transpose(out=out, in_=in_, identity=ident), _bufs(r), _bufs(w))

    def dma(self, q, out, in_, r, w, **kw):
        self.S.add(q, lambda e: e.dma_start(out=out, in_=in_, **kw), _bufs(r), _bufs(w), dma=True)


def inverse_workspace(K, st, C):
    W = {}
    W["nmask"] = K.sb(st, [128, 2, 7, 128], BF16, "nmask_sb")
    K.dma("pool", W["nmask"][:], C["nmask"][:, :, :, :], [], [W["nmask"]])
    W["ident_b"] = C["ident_b"]
    W["sets"] = []
    for g in range(2):
        W["sets"].append({nm: K.sb(st, [128, 4, 128], BF16, "iw%d_%s" % (g, nm))
                          for nm in ("Xa", "Xb", "Ya", "Yb", "LsX", "LsY", "M1", "M2", "Mt", "R")})
    W["PI"] = [K.ps(st, [128, 4, 128], F32) for _ in range(2)]
    W["cnt"] = 0
    return W


def inverse_units(K, C, LL, n, d, XTb, W, nunits=None):
    LLf = LL[:].rearrange("p j f t -> p (j f) t")
    nm = W["nmask"]
    nunits = 2 * n if nunits is None else nunits
    gs = min(4, nunits)
    idb = W["ident_b"][:].unsqueeze(1).to_broadcast([128, gs, 128])
    bc = lambda m, lv: nm[:, m, lv, :].unsqueeze(1).to_broadcast([128, gs, 128])

    def pi():
        W["cnt"] += 1
        return W["PI"][W["cnt"] % 2]
    mx, my = (0, 1) if d == 0 else (1, 0)

    class V_:
        def __init__(s_, t):
            s_.t = t
            s_.b = t.b

        def __getitem__(s_, k):
            if k == slice(None):
                return s_.t[:, 0:gs, :]
            return s_.t[k]
    groups = []
    for gi, g0 in enumerate(range(0, nunits, gs)):
        S_ = W["sets"][gi % 2]
        st_ = {k_: V_(v_) for k_, v_ in S_.items()}
        st_["g0"] = g0
        st_["Lv"] = LLf[:, 2 * g0:2 * g0 + 2 * gs:2, :]
        st_["LTv"] = LLf[:, 2 * g0 + 1:2 * g0 + 2 * gs:2, :]
        st_["X"], st_["Xn"], st_["Y"], st_["Yn"] = st_["Xa"], st_["Xb"], st_["Ya"], st_["Yb"]
        groups.append(st_)
    for G in groups:
        K.tt(G["LsX"][:], G["Lv"], bc(mx, 0), ALU.mult, [LL, nm], [G["LsX"]], eng="pool")
        K.tt(G["X"][:], G["LsX"][:], idb, ALU.add, [G["LsX"], W["ident_b"]], [G["X"]], eng="pool")
        K.tt(G["LsY"][:], G["LTv"], bc(my, 0), ALU.mult, [LL, nm], [G["LsY"]], eng="pool")
        K.tt(G["Y"][:], G["LsY"][:], idb, ALU.add, [G["LsY"], W["ident_b"]], [G["Y"]], eng="pool")
        K.tt(G["Mt"][:], G["Lv"], idb, ALU.add, [LL, W["ident_b"]], [G["Mt"]], eng="pool")
    yield
    for lv in range(1, 7):
        for G in groups:
            K.tt(G["LsX"][:], G["Lv"], bc(mx, lv), ALU.mult, [LL, nm], [G["LsX"]], eng="pool")
            K.tt(G["LsY"][:], G["LTv"], bc(my, lv), ALU.mult, [LL, nm], [G["LsY"]], eng="pool")
        yield
        for G in groups:
            X, Y, LsX, LsY, M1, M2 = G["X"], G["Y"], G["LsX"], G["LsY"], G["M1"], G["M2"]
            Q = pi()
            for u in range(gs):
                K.mm(Q[:, u, :], LsY[:, u, :], X[:, u, :], True, True, [LsY, X], [Q])
            K.act(M1[:], Q[:, 0:gs, :], AF.Copy, [Q], [M1])
            Q = pi()
            for u in range(gs):
                K.mm(Q[:, u, :], LsX[:, u, :], Y[:, u, :], True, True, [LsX, Y], [Q])
            K.act(M2[:], Q[:, 0:gs, :], AF.Copy, [Q], [M2])
            yield
        for G in groups:
            X, Y, Xn, Yn, M1, M2 = G["X"], G["Y"], G["Xn"], G["Yn"], G["M1"], G["M2"]
            Q = pi()
            for u in range(gs):
                K.mm(Q[:, u, :], Y[:, u, :], M1[:, u, :], True, True, [Y, M1], [Q])
            K.tt(Xn[:], X[:], Q[:, 0:gs, :], ALU.add, [X, Q], [Xn])
            Q = pi()
            for u in range(gs):
                K.mm(Q[:, u, :], X[:, u, :], M2[:, u, :], True, True, [X, M2], [Q])
            K.tt(Yn[:], Y[:], Q[:, 0:gs, :], ALU.add, [Y, Q], [Yn])
            G["X"], G["Xn"], G["Y"], G["Yn"] = Xn, X, Yn, Y
            yield
    for G in groups:
        Q = pi()
        for u in range(gs):
            K.mm(Q[:, u, :], G["Mt"][:, u, :], G["Y"][:, u, :], True, True, [G["Mt"], G["Y"]], [Q])
        K.stt(G["R"][:], Q[:, 0:gs, :], -1.0, idb, ALU.mult, ALU.add, [Q, W["ident_b"]], [G["R"]])
    for G in groups:
        Q = pi()
        for u in range(gs):
            K.mm(Q[:, u, :], G["X"][:, u, :], G["R"][:, u, :], True, True, [G["X"], G["R"]], [Q])
        K.tt(XTb[:, G["g0"]:G["g0"] + gs, :], G["Y"][:], Q[:, 0:gs, :], ALU.add, [G["Y"], Q], [XTb])
    yield


SEGS = [(0, 2)] + [(2 + 4 * i, 4) for i in range(4)]


def rwkv_phase(K, st, C):
    XS, smallT, oaT, ident_b, ident_f = C["XS"], C["smallT"], C["oaT"], C["ident_b"], C["ident_f"]
    dbg = C["dbg"]
    sb = lambda shape, dt, name=None: K.sb(st, shape, dt, name)
    w0 = sb([128, 2, 8], F32); a0 = sb([128, 2, 8], F32)
    kkw = sb([128, 8], F32); ka = sb([128, 8], F32); omka = sb([128, 8], F32); rk = sb([128, 8], F32)
    for t_, d_ in ((w0, C["w0"]), (a0, C["a0"])):
        K.dma("sp", t_[:], d_[:, :, :], [], [t_])
    for t_, d_ in ((kkw, C["kkw"]), (ka, C["ka"]), (rk, C["rk"])):
        K.dma("sp", t_[:], d_[:, :], [], [t_])
    K.ts(omka[:], ka[:], -1.0, 1.0, ALU.mult, ALU.add, [ka], [omka])
    w2b = sb([128, 2, 1024], BF16); a2b = sb([128, 2, 1024], BF16); g2b = sb([64, 1024], BF16)
    K.memset(w2b[:], 0.0, [w2b])
    K.memset(a2b[:], 0.0, [a2b])
    K.dma("pool", w2b[0:96, :, :], C["w2"][:, :, :].rearrange("d r c -> r d c"), [], [w2b])
    K.dma("pool", a2b[0:96, :, :], C["a2"][:, :, :].rearrange("d r c -> r d c"), [], [a2b])
    K.dma("pool", g2b[:], C["g2"][:, :], [], [g2b])
    lnw = sb([128, 128], F32); lnb = sb([128, 128], F32)
    m1f = sb([128, 2, 4, 128], BF16); m2f = sb([128, 2, 3, 128], BF16)
    K.dma("pool", m1f[:], C["m1"][:, :, :, :], [], [m1f])
    K.dma("pool", m2f[:], C["m2"][:, :, :, :], [], [m2f])
    rmask = sb([128, 512], F32); bones = sb([128, 128], F32); hsel = sb([128, 2], F32)
    K.dma("sp", rmask[:], C["rmask"][:, :], [], [rmask])
    K.dma("sp", bones[:], C["bones"][:, :], [], [bones])
    K.dma("sp", hsel[:], C["hsel"][:, :], [], [hsel])
    f32t = lambda nm=None: sb([128, 512], F32, nm)
    bft = lambda nm=None: sb([128, 512], BF16, nm)
    Xr, Xk, Xv = f32t("Xr"), f32t("Xk"), f32t("Xv")
    sig, A, B, Cc, Dd = f32t("sig"), f32t("A"), f32t("B"), f32t("Cc"), f32t("Dd")
    e1, e2, e3, e4 = f32t("e1"), f32t("e2"), f32t("e3"), f32t("e4")
    icl, icl0, kq, sq, rn, kkt, kd, bd, tmp = (f32t(nm) for nm in ("icl", "icl0", "kq", "sq", "rn", "kkt", "kd", "bd", "tmp"))
    gam = sb([128, 4], F32, "gam")
    rt, at, kt, bt, KH, BH, vb = (bft(nm) for nm in ("rt", "at", "kt", "bt", "KH", "BH", "vb"))
    KHt = sb([128, 4, 128], BF16, "KHt"); BHnt = sb([128, 4, 128], BF16, "BHnt"); Vt = sb([128, 4, 128], BF16, "Vt")
    LL = sb([128, 4, 4, 128], BF16, "LL")
    AA = sb([128, 4, 2, 3, 128], BF16, "AA")
    XTb = sb([128, 8, 128], BF16, "XTb")
    IW = inverse_workspace(K, st, C)
    Hf = sb([128, 128], F32, "Hf"); Hb = sb([128, 128], BF16, "Hb")
    P1s = sb([128, 128], BF16, "P1s"); Us = sb([128, 128], BF16, "Us")
    ybuf = sb([128, 16, 128], F32, "ybuf")
    ytot = sb([128, 4, 128], F32, "ytot"); yc = sb([128, 4, 128], F32, "yc"); ysq = sb([128, 4, 128], F32)
    mean = sb([128, 8], F32); var = sb([128, 8], F32)
    bsum = sb([128, 4, 2], F32)
    oat = sb([128, 4, 128], BF16)
    PF = [K.ps(st, [128, 512], F32) for _ in range(1)]
    PTr = K.ps(st, [128, 8, 128], BF16)
    PG = [K.ps(st, [128, 4, 128], F32) for _ in range(2)]
    PSq = K.ps(st, [128, 512], F32)
    PSh = K.ps(st, [128, 512], F32)
    PS_P1, PS_U, PS_Y, PS_H = PSq, PSq, PSq, PSh
    cnt = {"pf": 0, "pg": 0, "pi": 0, "tr": 0}

    def nxt(lst, key):
        cnt[key] += 1
        return lst[cnt[key] % len(lst)]

    def transp(src, dst, n, scale=None):
        half = cnt["tr"] % 2
        cnt["tr"] += 1
        for j in range(n):
            K.tr(PTr[:, half * 4 + j, :], src[:, j * 128:(j + 1) * 128], ident_b[:], [src, ident_b], [PTr])
        if scale is None:
            K.copy(dst[:, :n, :], PTr[:, half * 4:half * 4 + n, :], [PTr], [dst])
        else:
            K.act(dst[:, :n, :], PTr[:, half * 4:half * 4 + n, :], AF.Copy, [PTr], [dst], scale=scale)

    Xs = [(Xr, Xk, Xv), (f32t("Xr1"), f32t("Xk1"), f32t("Xv1"))]
    rtP = [rt, bft("rt1")]; atP = [at, bft("at1")]; ktP = [kt, bft("kt1")]; btP = [bt, bft("bt1")]
    KHtP = [KHt, sb([128, 4, 128], BF16, "KHt1")]; BHntP = [BHnt, sb([128, 4, 128], BF16, "BHnt1")]
    VtP = [Vt, sb([128, 4, 128], BF16, "Vt1")]
    gamP = [gam, sb([128, 4], F32, "gam1")]
    bsumP = [bsum, sb([128, 4, 2], F32, "bsum1")]
    items = []
    for hp in range(C["nhp"]):
        for d in range(2):
            order = SEGS if d == 0 else [SEGS[0], SEGS[4], SEGS[3], SEGS[2], SEGS[1]]
            for si, (c0, n) in enumerate(order):
                items.append((hp, d, c0, n, si == 0))

    def loads(i):
        hp, d, c0, n, first = items[i]
        t0, N = c0 * 128, n * 128
        for X_, row in zip(Xs[i % 2], (0, 1024, 2048)):
            K.dma("sp", X_[:, :N], XS[row + hp * 128:row + hp * 128 + 128, t0:t0 + N], [XS], [X_])

    def stepA(i):
        hp, d, c0, n, first = items[i]
        p = i % 2
        hc = slice(hp * 128, (hp + 1) * 128)
        t0, N = c0 * 128, n * 128
        latent = c0 >= 2
        tk = slice(t0, t0 + N)
        Xr, Xk, Xv = Xs[p]
        rt, at, kt, bt, KHt, BHnt, Vt, gam, bsum = rtP[p], atP[p], ktP[p], btP[p], KHtP[p], BHntP[p], VtP[p], gamP[p], bsumP[p]
        if i + 1 < len(items):
            loads(i + 1)
        P = nxt(PF, "pf")
        K.mm(P[:, :N], w2b[:, d, hc], smallT[:, d, tk], True, True, [w2b, smallT], [P])
        K.act(sig[:, :N], P[:, :N], AF.Sigmoid, [P, w0], [sig], bias=w0[:, d, hp:hp + 1])
        K.scan(A[:, :N], rmask[:, :N], sig[:, :N], 0.0, ALU.mult, ALU.add, [rmask, sig], [A])
        yield
        K.tt(B[:, :N], A[:, :N], sig[:, :N], ALU.subtract, [A, sig], [B], eng="pool")
        v3 = lambda t_: t_[:, :N].rearrange("p (c t) -> p c t", t=128)
        tot = v3(A)[:, :, 127:128]
        K.tt(v3(Cc), tot.to_broadcast([128, n, 128]), v3(A), ALU.subtract, [A], [Cc])
        K.tt(Dd[:, :N], Cc[:, :N], sig[:, :N], ALU.add, [Cc, sig], [Dd], eng="pool")
        yield
        Gi, Gx, Gt = (A, B, Cc) if d == 0 else (Dd, Cc, B)
        K.act(e1[:, :N], Gi[:, :N], AF.Exp, [Gi], [e1], scale=-DEC)
        K.act(e2[:, :N], Gx[:, :N], AF.Exp, [Gx], [e2], scale=-DEC)
        yield
        K.act(e3[:, :N], Gi[:, :N], AF.Exp, [Gi], [e3], scale=DEC)
        K.act(e4[:, :N], Gt[:, :N], AF.Exp, [Gt], [e4], scale=-DEC)
        K.act(gam[:, :n], v3(A)[:, :, 127], AF.Exp, [A], [gam], scale=-DEC)
        yield
        P = nxt(PF, "pf")
        K.mm(P[:, :N], a2b[:, d, hc], smallT[:, 2 + d, tk], True, True, [a2b, smallT], [P])
        K.act(icl[:, :N], P[:, :N], AF.Sigmoid, [P, a0], [icl], bias=a0[:, d, hp:hp + 1])
        yield
        K.ts(kq[:, :N], Xk[:, :N], kkw[:, hp:hp + 1], None, ALU.mult, None, [Xk, kkw], [kq], eng="pool")
        K.act(sq[:, :N], kq[:, :N], AF.Square, [kq], [sq])
        P = nxt(PF, "pf")
        K.mm(P[:, :N], bones[:], sq[:, :N], True, True, [bones, sq], [P])
        K.act(rn[:, :N], P[:, :N], AF.Sqrt, [P], [rn], bias=EPS)
        K.recip(rn[:, :N], rn[:, :N], [rn], [rn])
        yield
        K.tt(kkt[:, :N], kq[:, :N], rn[:, :N], ALU.mult, [kq, rn], [kkt])
        K.ts(tmp[:, :N], icl[:, :N], ka[:, hp:hp + 1], omka[:, hp:hp + 1], ALU.mult, ALU.add, [icl, ka, omka], [tmp])
        K.tt(kd[:, :N], tmp[:, :N], Xk[:, :N], ALU.mult, [tmp, Xk], [kd])
        yield
        K.tt(bd[:, :N], kkt[:, :N], icl[:, :N], ALU.mult, [kkt, icl], [bd], eng="pool")
        K.tt(rt[:, :N], Xr[:, :N], e1[:, :N], ALU.mult, [Xr, e1], [rt])
        K.tt(at[:, :N], kkt[:, :N], e2[:, :N], ALU.mult, [kkt, e2], [at], eng="pool")
        yield
        K.tt(kt[:, :N], kd[:, :N], e3[:, :N], ALU.mult, [kd, e3], [kt])
        K.tt(bt[:, :N], bd[:, :N], e3[:, :N], ALU.mult, [bd, e3], [bt], eng="pool")
        K.tt(KH[:, :N], kd[:, :N], e4[:, :N], ALU.mult, [kd, e4], [KH])
        yield
        K.tt(BH[:, :N], bd[:, :N], e4[:, :N], ALU.mult, [bd, e4], [BH], eng="pool")
        K.act(vb[:, :N], Xv[:, :N], AF.Copy, [Xv], [vb])
        transp(KH, KHt, n)
        yield
        transp(BH, BHnt, n, scale=-1.0)
        transp(vb, Vt, n)
        yield
        if d == 1 and latent:
            P = nxt(PF, "pf")
            K.mm(P[:, :N], a2b[:, 0, hc], smallT[:, 2, tk], True, True, [a2b, smallT], [P])
            K.act(icl0[:, :N], P[:, :N], AF.Sigmoid, [P, a0], [icl0], bias=a0[:, 0, hp:hp + 1])
            K.tt(tmp[:, :N], icl[:, :N], icl0[:, :N], ALU.add, [icl, icl0], [tmp])
            yield
            K.ts(tmp[:, :N], tmp[:, :N], 0.5, None, ALU.mult, None, [tmp], [tmp])
            K.ts(tmp[:, :N], tmp[:, :N], ka[:, hp:hp + 1], omka[:, hp:hp + 1], ALU.mult, ALU.add, [tmp, ka, omka], [tmp])
            K.tt(tmp[:, :N], tmp[:, :N], Xk[:, :N], ALU.mult, [tmp, Xk], [tmp])
            yield
            K.stt(sq[:, :N], tmp[:, :N], rk[:, hp:hp + 1], Xr[:, :N], ALU.mult, ALU.mult, [tmp, rk, Xr], [sq])
            P = nxt(PF, "pf")
            for j in range(n):
                K.mm(P[:, 2 * j:2 * j + 2], sq[:, j * 128:(j + 1) * 128], hsel[:], True, True, [sq, hsel], [P])
            K.copy(bsum[:].rearrange("p j e -> p (j e)"), P[:, 0:2 * n], [P], [bsum])
            yield

    def stepBC(i):
        hp, d, c0, n, first = items[i]
        p = i % 2
        hc = slice(hp * 128, (hp + 1) * 128)
        t0, N = c0 * 128, n * 128
        latent = c0 >= 2
        rt, at, kt, bt, KHt, BHnt, Vt, gam, bsum = rtP[p], atP[p], ktP[p], btP[p], KHtP[p], BHntP[p], VtP[p], gamP[p], bsumP[p]
        if first:
            K.memset(Hf[:], 0.0, [Hf])
            K.memset(Hb[:], 0.0, [Hb])
            if d == 1:
                K.dma("sp", lnw[:], C["lnw"][0:1, hc].to_broadcast([128, 128]), [], [lnw])
                K.dma("sp", lnb[:], C["lnb"][0:1, hc].to_broadcast([128, 128]), [], [lnb])
        for j in range(n):
            cs = slice(j * 128, (j + 1) * 128)
            K.S.pe_strict(True)
            G = nxt(PG, "pg")
            for e in range(2):
                ps_ = slice(64 * e, 64 * e + 64)
                K.mm(G[:, 2 * e, :], at[ps_, cs], bt[ps_, cs], True, True, [at, bt], [G])
                K.mm(G[:, 2 * e + 1, :], bt[ps_, cs], at[ps_, cs], True, True, [at, bt], [G])
            K.tt(LL[:, j, :, :], G[:], m1f[:, d, :, :], ALU.mult, [G, m1f], [LL])
            for e in range(2):
                ps_ = slice(64 * e, 64 * e + 64)
                G = nxt(PG, "pg")
                K.mm(G[:, 0, :], kt[ps_, cs], at[ps_, cs], True, True, [kt, at], [G])
                K.mm(G[:, 1, :], kt[ps_, cs], rt[ps_, cs], True, True, [kt, rt], [G])
                K.mm(G[:, 2, :], bt[ps_, cs], rt[ps_, cs], True, True, [bt, rt], [G])
                K.tt(AA[:, j, e, :, :], G[:, 0:3, :], m2f[:, d, :, :], ALU.mult, [G, m2f], [AA])
            K.S.pe_strict(False)
            yield
        for _ in inverse_units(K, C, LL, n, d, XTb, IW):
            yield
        jl = list(range(n)) if d == 0 else list(range(n - 1, -1, -1))
        for j in jl:
            cs = slice(j * 128, (j + 1) * 128)
            K.mm(PS_P1[:, 0:128], at[:, cs], Hb[:], True, False, [at, Hb], [PS_P1])
            for e in range(2):
                vs = slice(64 * e, 64 * e + 64)
                K.mm(PS_P1[:, 64 * e:64 + 64 * e], AA[:, j, e, 0, :], Vt[:, j, vs], False, e == 1, [AA, Vt], [PS_P1])
            K.act(P1s[:], PS_P1[:, 0:128], AF.Copy, [PS_P1], [P1s])
            yield
            for e in range(2):
                vs = slice(64 * e, 64 * e + 64)
                K.mm(PS_U[:, 128 + 64 * e:192 + 64 * e], XTb[:, 2 * j + e, :], P1s[:, vs], True, True, [XTb, P1s], [PS_U])
            K.copy(Us[:], PS_U[:, 128:256], [PS_U], [Us])
            yield
            if latent:
                K.mm(PS_Y[:, 256:384], rt[:, cs], Hb[:], True, False, [rt, Hb], [PS_Y])
                for e in range(2):
                    vs = slice(64 * e, 64 * e + 64)
                    yo = PS_Y[:, 256 + 64 * e:320 + 64 * e]
                    K.mm(yo, AA[:, j, e, 1, :], Vt[:, j, vs], False, False, [AA, Vt], [PS_Y])
                    K.mm(yo, AA[:, j, e, 2, :], Us[:, vs], False, e == 1, [AA, Us], [PS_Y])
            K.mm(PS_H[:, 384:512], KHt[:, j, :], Vt[:, j, :], True, False, [KHt, Vt], [PS_H])
            K.mm(PS_H[:, 384:512], BHnt[:, j, :], Us[:], False, True, [BHnt, Us], [PS_H])
            for e in range(2):
                ps_ = slice(64 * e, 64 * e + 64)
                vs = slice(64 * e, 64 * e + 64)
                K.stt(Hf[ps_, vs], Hf[ps_, vs], gam[ps_, j:j + 1], PS_H[ps_, 384 + 64 * e:448 + 64 * e], ALU.mult, ALU.add,
                      [Hf, gam, PS_H], [Hf])
            K.act(Hb[:], Hf[:], AF.Copy, [Hf], [Hb])
            if latent:
                cg = c0 - 2 + j
                if d == 0:
                    K.act(ybuf[:, cg, :], PS_Y[:, 256:384], AF.Copy, [PS_Y], [ybuf])
                else:
                    K.tt(ytot[:, j, :], ybuf[:, cg, :], PS_Y[:, 256:384], ALU.add, [ybuf, PS_Y], [ytot])
            yield
        if d == 1 and latent:
            if dbg and hp < 8:
                K.dma("sp", dbg["yf"][hp, :, c0 - 2:c0 - 2 + n, :], ytot[:], [ytot], [])
            yv = ytot[:].rearrange("p j (e c) -> p (j e) c", c=64)
            ycv = yc[:].rearrange("p j (e c) -> p (j e) c", c=64)
            sqv = ysq[:].rearrange("p j (e c) -> p (j e) c", c=64)
            K.S.add("dve", lambda e_: e_.tensor_reduce(out=mean[:], in_=yv, axis=AX.X, op=ALU.add), _bufs([ytot]), _bufs([mean]))
            K.ts(mean[:], mean[:], 1.0 / 64, None, ALU.mult, None, [mean], [mean])
            K.tt(ycv, yv, mean[:].unsqueeze(2).to_broadcast([128, 8, 64]), ALU.subtract, [ytot, mean], [yc])
            K.tt(sqv, ycv, ycv, ALU.mult, [yc], [ysq], eng="pool")
            yield
            K.S.add("dve", lambda e_: e_.tensor_reduce(out=var[:], in_=sqv, axis=AX.X, op=ALU.add), _bufs([ysq]), _bufs([var]))
            K.act(var[:], var[:], AF.Sqrt, [var], [var], scale=1.0 / 64, bias=64e-5)
            K.recip(var[:], var[:], [var], [var])
            K.tt(ycv, ycv, var[:].unsqueeze(2).to_broadcast([128, 8, 64]), ALU.mult, [yc, var], [yc])
            yield
            K.tt(yc[:], yc[:], lnw[:].unsqueeze(1).to_broadcast([128, 4, 128]), ALU.mult, [yc, lnw], [yc])
            K.tt(yc[:], yc[:], lnb[:].unsqueeze(1).to_broadcast([128, 4, 128]), ALU.add, [yc, lnb], [yc])
            K.copy(ysq[:], Vt[:], [Vt], [ysq], eng="pool")
            K.tt(sqv, sqv, bsum[:].rearrange("p j e -> p (j e)").unsqueeze(2).to_broadcast([128, 8, 64]), ALU.mult,
                 [ysq, bsum], [ysq])
            K.tt(yc[:], yc[:], ysq[:], ALU.add, [yc, ysq], [yc])
            yield
            P = nxt(PF, "pf")
            for j in range(n):
                K.mm(P[:, j * 128:(j + 1) * 128], smallT[0:64, 4, t0 + j * 128:t0 + (j + 1) * 128], g2b[0:64, hc], True, True,
                     [smallT, g2b], [P])
            K.tt(oat[:], yc[:], P[:].rearrange("p (j c) -> p j c", c=128), ALU.mult, [yc, P], [oat])
            half = cnt["tr"] % 2
            cnt["tr"] += 1
            for j in range(n):
                K.tr(PTr[:, half * 4 + j, :], oat[:, j, :], ident_b[:], [oat, ident_b], [PTr])
            K.copy(oaT[:, hp, t0 - 256:t0 - 256 + N].rearrange("p (j t) -> p j t", t=128), PTr[:, half * 4:half * 4 + n, :],
                   [PTr], [oaT])
            yield

    def drain(g):
        for _ in g:
            pass

    loads(0)
    drain(stepA(0))
    for i in range(len(items)):
        g1 = stepBC(i)
        g2 = stepA(i + 1) if i + 1 < len(items) else iter(())
        a1 = a2 = True
        while a1 or a2:
            if a1:
                for _ in range(RATIO):
                    try:
                        next(g1)
                    except StopIteration:
                        a1 = False
                        break
            if a2:
                try:
                    next(g2)
                except StopIteration:
                    a2 = False


def gdn_phase(K, st, C):
    US, SZ, abT, obT, ident_b, ident_f = C["US"], C["SZ"], C["abT"], C["obT"], C["ident_b"], C["ident_f"]
    sb = lambda shape, dt, name=None: K.sb(st, shape, dt, name)
    bigm = sb([128, 2, 2, 128], F32); offd = sb([128, 128], F32); ones = sb([128, 128], F32)
    K.dma("sp", bigm[:], C["bigm"][:, :, :, :], [], [bigm])
    K.dma("sp", offd[:], C["offd"][:, :], [], [offd])
    K.dma("sp", ones[:], C["ones"][:, :], [], [ones])
    rmask = sb([128, 512], F32)
    K.dma("sp", rmask[:], C["rmask"][:, :], [], [rmask])
    alog = sb([64, 1], F32); dtb = sb([64, 1], F32); nea = sb([64, 1], F32)
    K.dma("sp", alog[:], C["alog"][:, :], [], [alog])
    K.dma("sp", dtb[:], C["dtb"][:, :], [], [dtb])
    gnw = sb([128, 128], F32)
    K.dma("sp", gnw[:], C["gnw"][0:1, :].to_broadcast([128, 128]), [], [gnw])
    K.act(nea[:], alog[:], AF.Exp, [alog], [nea])
    K.ts(nea[:], nea[:], -1.0, None, ALU.mult, None, [nea], [nea])
    GB = [sb([64, TT], F32, "GB%d" % d) for d in range(2)]
    tokT = [sb([128, NCH, 64], F32, "tokT%d" % d) for d in range(2)]
    with ExitStack() as s0:
        gt = K.sb(s0, [16, TT], F32); A = K.sb(s0, [16, TT], F32); Bx = K.sb(s0, [16, TT], F32)
        K.act(gt[:], abT[0:16, :], AF.Exp, [abT, dtb], [gt], bias=dtb[0:16, :])
        K.act(gt[:], gt[:], AF.Ln, [gt], [gt], bias=1.0)
        K.ts(gt[:], gt[:], nea[0:16, :], None, ALU.mult, None, [gt, nea], [gt])
        for d in range(2):
            K.memset(GB[d][:], 0.0, [GB[d]])
            K.act(GB[d][32:48, :], abT[32:48, :], AF.Sigmoid, [abT], [GB[d]])
        for t0 in range(0, TT, 512):
            N = min(512, TT - t0)
            K.scan(A[:, t0:t0 + N], rmask[0:16, :N], gt[:, t0:t0 + N], 0.0, ALU.mult, ALU.add, [rmask, gt], [A])
        K.copy(GB[0][0:16, :], A[:], [A], [GB[0]], eng="pool")
        K.tt(Bx[:], A[:], gt[:], ALU.subtract, [A, gt], [Bx])
        v3 = lambda t_: t_[:].rearrange("p (c t) -> p c t", t=128)
        tot = v3(A)[:, :, 127:128]
        K.tt(v3(GB[1])[0:16], tot.to_broadcast([16, NCH, 128]), v3(Bx), ALU.subtract, [A, Bx], [GB[1]])
        ptk = K.ps(s0, [128, 8, 64], F32)
        for d in range(2):
            for c8 in range(0, NCH, 8):
                nn = min(8, NCH - c8)
                for j in range(nn):
                    c = c8 + j
                    K.tr(ptk[:, j, :], GB[d][:, c * 128:(c + 1) * 128], ident_f[0:64, 0:64], [GB[d], ident_f], [ptk])
                K.copy(tokT[d][:, c8:c8 + nn, :], ptk[:, 0:nn, :], [ptk], [tokT[d]])
    GBS = C["GBS"]
    for d in range(2):
        K.dma("sp", GBS[d, :, :], GB[d][:], [GB[d]], [GBS])
    K.S.barrier()
    f32t = lambda nm=None: sb([128, 512], F32, nm)
    bft = lambda nm=None: sb([128, 512], BF16, nm)
    XsP = [tuple(f32t("gX%s%d" % (nm, p)) for nm in ("q", "k", "v", "G", "B")) for p in range(2)]
    sq, rn, qn, kn, eG, tmp, sz = (f32t("g_" + nm) for nm in ("sq", "rn", "qn", "kn", "eG", "tmp", "sz"))
    Ktl, vb = bft("g_Ktl"), bft("g_vb")
    knbP, qnbP, kbTP, nKBGP, QdP = ([bft("g_%s%d" % (nm, p)) for p in range(2)] for nm in ("knb", "qnb", "kbT", "nKBG", "Qd"))
    KttP = [sb([128, 4, 128], BF16, "g_Ktt%d" % p) for p in range(2)]
    VtP = [sb([128, 4, 128], BF16, "g_Vt%d" % p) for p in range(2)]
    glP = [sb([128, 4], F32, "g_gl%d" % p) for p in range(2)]
    Dc = sb([128, 4, 2, 128], F32, "g_Dc"); DiT = sb([128, 4, 128], F32, "g_DiT"); Dtmp = sb([128, 4, 128], F32, "g_Dtmp")
    LLg = sb([128, 2, 4, 128], BF16, "g_LL")
    QKt = sb([128, 4, 128], BF16, "g_QKt")
    XTb = sb([128, 4, 128], BF16, "g_XTb")
    IW = inverse_workspace(K, st, C)
    Sf = sb([128, 128], F32, "g_Sf"); Sb = sb([128, 128], BF16, "g_Sb")
    P1s = sb([128, 128], BF16, "g_P1s"); VNs = sb([128, 128], BF16, "g_VNs")
    obuf = sb([128, 16, 128], F32, "g_obuf")
    otot = sb([128, 4, 128], F32, "g_otot"); osq = sb([128, 4, 128], F32, "g_osq")
    ss = sb([128, 4], F32); onb = sb([128, 4, 128], BF16, "g_onb")
    PF = K.ps(st, [128, 512], F32)
    PTr = K.ps(st, [128, 8, 128], BF16)
    PG = [K.ps(st, [128, 4, 128], F32) for _ in range(2)]
    PSq = K.ps(st, [128, 512], F32)
    PSh = K.ps(st, [128, 512], F32)
    cnt = {"pg": 0, "tr": 0}

    def transp(src, dst, n):
        half = cnt["tr"] % 2
        cnt["tr"] += 1
        for j in range(n):
            K.tr(PTr[:, half * 4 + j, :], src[:, j * 128:(j + 1) * 128], ident_b[:], [src, ident_b], [PTr])
        K.copy(dst[:, :n, :], PTr[:, half * 4:half * 4 + n, :], [PTr], [dst])

    items = []
    for h in range(C["nh"]):
        for d in range(2):
            order = SEGS if d == 0 else [SEGS[0], SEGS[4], SEGS[3], SEGS[2], SEGS[1]]
            for si, (c0, n) in enumerate(order):
                items.append((h, d, c0, n, si == 0))

    def loads(i):
        h, d, c0, n, first = items[i]
        r = d * 8 + h
        t0, N = c0 * 128, n * 128
        Xq, Xk, Xv, bcG, bcB = XsP[i % 2]
        for X_, row in ((Xq, 0), (Xk, 1024), (Xv, 2048)):
            K.dma("sp", X_[:, :N], US[row + h * 128:row + h * 128 + 128, t0:t0 + N], [US], [X_])
        K.dma("sp", bcG[:, :N], GBS[d, r:r + 1, t0:t0 + N].to_broadcast([128, N]), [GBS], [bcG])
        K.dma("sp", bcB[:, :N], GBS[d, 32 + r:33 + r, t0:t0 + N].to_broadcast([128, N]), [GBS], [bcB])

    def stepA(i):
        h, d, c0, n, first = items[i]
        p = i % 2
        t0, N = c0 * 128, n * 128
        Xq, Xk, Xv, bcG, bcB = XsP[p]
        knb, qnb, kbT, nKBG, Qd, Ktt, Vt, gl = knbP[p], qnbP[p], kbTP[p], nKBGP[p], QdP[p], KttP[p], VtP[p], glP[p]
        v3 = lambda t_: t_[:, :N].rearrange("p (c t) -> p c t", t=128)
        if i + 1 < len(items):
            loads(i + 1)
        for X_, o_, sc_ in ((Xq, qn, 128 ** -0.5), (Xk, kn, 1.0)):
            K.act(sq[:, :N], X_[:, :N], AF.Square, [X_], [sq])
            K.mm(PF[:, :N], ones[:], sq[:, :N], True, True, [ones, sq], [PF])
            K.act(rn[:, :N], PF[:, :N], AF.Sqrt, [PF], [rn], bias=EPS)
            K.recip(rn[:, :N], rn[:, :N], [rn], [rn])
            K.stt(o_[:, :N], X_[:, :N], sc_, rn[:, :N], ALU.mult, ALU.mult, [X_, rn], [o_])
            yield
        K.act(eG[:, :N], bcG[:, :N], AF.Exp, [bcG], [eG])
        lastcol = 127 if d == 0 else 0
        K.copy(gl[:, :n], v3(eG)[:, :, lastcol], [eG], [gl], eng="pool")
        glast = v3(bcG)[:, :, lastcol:lastcol + 1]
        K.tt(kbT[:, :N], kn[:, :N], bcB[:, :N], ALU.mult, [kn, bcB], [kbT])
        yield
        K.copy(knb[:, :N], kn[:, :N], [kn], [knb], eng="pool")
        K.act(qnb[:, :N], qn[:, :N], AF.Copy, [qn], [qnb])
        K.stt(nKBG[:, :N], kbT[:, :N], -1.0, eG[:, :N], ALU.mult, ALU.mult, [kbT, eG], [nKBG])
        yield
        K.tt(Qd[:, :N], qn[:, :N], eG[:, :N], ALU.mult, [qn, eG], [Qd], eng="pool")
        K.tt(v3(tmp), glast.to_broadcast([128, n, 128]), v3(bcG), ALU.subtract, [bcG], [tmp])
        K.act(tmp[:, :N], tmp[:, :N], AF.Exp, [tmp], [tmp])
        yield
        K.tt(Ktl[:, :N], kn[:, :N], tmp[:, :N], ALU.mult, [kn, tmp], [Ktl])
        K.act(vb[:, :N], Xv[:, :N], AF.Copy, [Xv], [vb])
        transp(Ktl, Ktt, n)
        yield
        transp(vb, Vt, n)
        yield

    def stepBC(i):
        h, d, c0, n, first = items[i]
        p = i % 2
        r = d * 8 + h
        t0, N = c0 * 128, n * 128
        latent = c0 >= 2
        Xq, Xk, Xv, bcG, bcB = XsP[p]
        knb, qnb, kbT, nKBG, Qd, Ktt, Vt, gl = knbP[p], qnbP[p], kbTP[p], nKBGP[p], QdP[p], KttP[p], VtP[p], glP[p]
        v3 = lambda t_: t_[:, :N].rearrange("p (c t) -> p c t", t=128)
        if first:
            K.memset(Sf[:], 0.0, [Sf])
            K.memset(Sb[:], 0.0, [Sb])
        gct = tokT[d][:, c0:c0 + n, r:r + 1].to_broadcast([128, n, 128])
        bg3 = v3(bcG)
        K.tt(Dtmp[:, :n, :], bg3, bigm[:, d, 0, :].unsqueeze(1).to_broadcast([128, n, 128]), ALU.add, [bcG, bigm], [Dtmp])
        K.tt(Dtmp[:, :n, :], Dtmp[:, :n, :], gct, ALU.subtract, [Dtmp, tokT[d]], [Dtmp], eng="pool")
        K.act(Dtmp[:, :n, :], Dtmp[:, :n, :], AF.Exp, [Dtmp], [Dtmp], scale=-1.0)
        K.tt(Dc[:, :n, 0, :], Dtmp[:, :n, :], offd[:].unsqueeze(1).to_broadcast([128, n, 128]), ALU.mult, [Dtmp, offd], [Dc],
             eng="pool")
        yield
        K.tt(DiT[:, :n, :], bg3, bigm[:, d, 1, :].unsqueeze(1).to_broadcast([128, n, 128]), ALU.add, [bcG, bigm], [DiT])
        K.tt(DiT[:, :n, :], DiT[:, :n, :], gct, ALU.subtract, [DiT, tokT[d]], [DiT], eng="pool")
        K.act(DiT[:, :n, :], DiT[:, :n, :], AF.Exp, [DiT], [DiT])
        K.tt(Dc[:, :n, 1, :], DiT[:, :n, :], offd[:].unsqueeze(1).to_broadcast([128, n, 128]), ALU.mult, [DiT, offd], [Dc],
             eng="pool")
        yield
        LLv = LLg[:].rearrange("p a b t -> p (a b) t")
        for j in range(n):
            cs = slice(j * 128, (j + 1) * 128)
            cnt["pg"] += 1
            G = PG[cnt["pg"] % 2]
            K.mm(G[:, 0, :], kbT[:, cs], knb[:, cs], True, True, [kbT, knb], [G])
            K.mm(G[:, 1, :], knb[:, cs], kbT[:, cs], True, True, [kbT, knb], [G])
            K.mm(G[:, 2, :], knb[:, cs], qnb[:, cs], True, True, [qnb, knb], [G])
            K.tt(LLv[:, 2 * j:2 * j + 2, :], G[:, 0:2, :], Dc[:, j, :, :], ALU.mult, [G, Dc], [LLg])
            K.tt(QKt[:, j, :], G[:, 2, :], DiT[:, j, :], ALU.mult, [G, DiT], [QKt])
            yield
        for _ in inverse_units(K, C, LLg, n, d, XTb, IW, nunits=n):
            yield
        jl = list(range(n)) if d == 0 else list(range(n - 1, -1, -1))
        for j in jl:
            cs = slice(j * 128, (j + 1) * 128)
            c = c0 + j
            K.mm(PSq[:, 0:128], nKBG[:, cs], Sb[:], True, True, [nKBG, Sb], [PSq])
            K.stt(P1s[:], Vt[:, j, :], tokT[d][:, c, 32 + r:33 + r], PSq[:, 0:128], ALU.mult, ALU.add,
                  [Vt, tokT[d], PSq], [P1s])
            yield
            K.mm(PSq[:, 128:256], XTb[:, j, :], P1s[:], True, True, [XTb, P1s], [PSq])
            K.act(VNs[:], PSq[:, 128:256], AF.Copy, [PSq], [VNs])
            yield
            if latent:
                K.mm(PSq[:, 256:384], Qd[:, cs], Sb[:], True, False, [Qd, Sb], [PSq])
                K.mm(PSq[:, 256:384], QKt[:, j, :], VNs[:], False, True, [QKt, VNs], [PSq])
            K.mm(PSh[:, 0:128], Ktt[:, j, :], VNs[:], True, True, [Ktt, VNs], [PSh])
            K.stt(Sf[:], Sf[:], gl[:, j:j + 1], PSh[:, 0:128], ALU.mult, ALU.add, [Sf, gl, PSh], [Sf])
            K.act(Sb[:], Sf[:], AF.Copy, [Sf], [Sb])
            if latent:
                cg = c - 2
                if d == 0:
                    K.act(obuf[:, cg, :], PSq[:, 256:384], AF.Copy, [PSq], [obuf])
                else:
                    K.tt(otot[:, j, :], obuf[:, cg, :], PSq[:, 256:384], ALU.add, [obuf, PSq], [otot])
            yield
        if d == 1 and latent:
            if C["dbg"]:
                K.dma("sp", C["dbg"]["of"][h, :, c0 - 2:c0 - 2 + n, :], otot[:], [otot], [])
            K.tt(osq[:], otot[:], otot[:], ALU.mult, [otot], [osq], eng="pool")
            K.S.add("dve", lambda e_: e_.tensor_reduce(out=ss[:], in_=osq[:], axis=AX.X, op=ALU.add), _bufs([osq]), _bufs([ss]))
            K.act(ss[:], ss[:], AF.Sqrt, [ss], [ss], scale=1.0 / 128, bias=EPS)
            K.recip(ss[:], ss[:], [ss], [ss])
            yield
            K.tt(osq[:], otot[:], ss[:].unsqueeze(2).to_broadcast([128, 4, 128]), ALU.mult, [otot, ss], [osq])
            K.tt(onb[:], osq[:], gnw[:].unsqueeze(1).to_broadcast([128, 4, 128]), ALU.mult, [osq, gnw], [onb])
            K.dma("sp", sz[:, :N], SZ[h * 128:(h + 1) * 128, t0 - 256:t0 - 256 + N], [SZ], [sz])
            yield
            half = cnt["tr"] % 2
            cnt["tr"] += 1
            for j in range(n):
                K.tr(PTr[:, half * 4 + j, :], onb[:, j, :], ident_b[:], [onb, ident_b], [PTr])
            K.tt(obT[:, h, t0 - 256:t0 - 256 + N].rearrange("p (j t) -> p j t", t=128), PTr[:, half * 4:half * 4 + n, :],
                 sz[:, :N].rearrange("p (j t) -> p j t", t=128), ALU.mult, [PTr, sz], [obT])
            yield

    loads(0)
    for _ in stepA(0):
        pass
    for i in range(len(items)):
        g1 = stepBC(i)
        g2 = stepA(i + 1) if i + 1 < len(items) else iter(())
        a1 = a2 = True
        while a1 or a2:
            if a1:
                for _ in range(RATIO_G):
                    try:
                        next(g1)
                    except StopIteration:
                        a1 = False
                        break
            if a2:
                try:
                    next(g2)
                except StopIteration:
                    a2 = False


def write_mods(K, C):
    modT, ident_f, MODS = C["modT"], C["ident_f"], C["MODS"]
    with ExitStack() as s0:
        pt = K.ps(s0, [16, 2, 128], F32)
        rows = K.sb(s0, [16, 2, 128], F32)
        for i, sec in enumerate((2, 5)):
            K.tr(pt[:, i, :], modT[:, sec * 16:(sec + 1) * 16, 0], ident_f[:], [modT, ident_f], [pt])
        K.copy(rows[:], pt[:], [pt], [rows])
        for i in range(2):
            K.dma("sp", MODS[i * 16:(i + 1) * 16, :], rows[:, i, :], [rows], [MODS])
    K.S.barrier()


def merge_phase(K, top, C):
    oaT, obT, ident_b, ident_f, modT, s2 = C["oaT"], C["obT"], C["ident_b"], C["ident_f"], C["modT"], C["s2"]
    SG, X1, x_d, out_d = C["SG"], C["X1"], C["x"], C["out"]
    MODS = C["MODS"]
    dbg = C["dbg"]
    bc = K.sb(top, [128, 1, D], F32, "bc_rows")
    K.dma("sp", bc[:, 0, :], MODS[0:16, :].rearrange("(o a) b -> o (a b)", o=1).to_broadcast([128, D]), [MODS], [bc])
    K.S.barrier()
    p3 = ExitStack()
    mT = K.sb(p3, [128, 16, TL], BF16, "mT")
    with ExitStack() as s1:
        wa = [K.sb(s1, [128, 8, 256], BF16) for _ in range(2)]
        wbb = [K.sb(s1, [128, 8, 256], BF16) for _ in range(2)]
        sga = [K.sb(s1, [128, TL], BF16)] * 2
        sgb = [K.sb(s1, [128, TL], BF16)] * 2
        t1 = [K.sb(s1, [128, 512], F32) for _ in range(2)]
        t2 = [K.sb(s1, [128, 512], F32) for _ in range(2)]
        pa = [K.ps(s1, [128, 512], F32) for _ in range(2)]
        pb = [K.ps(s1, [128, 512], F32) for _ in range(2)]
        it = 0
        for sc in range(8):
            K.dma("pool", wa[sc % 2][:], C["p_a"][:, sc * 256:(sc + 1) * 256].rearrange("(k p) n -> p k n", p=128), [], [wa[sc % 2]])
            K.dma("pool", wbb[sc % 2][:], C["p_b"][:, sc * 256:(sc + 1) * 256].rearrange("(k p) n -> p k n", p=128), [], [wbb[sc % 2]])
            for ff in range(2):
                f = sc * 2 + ff
                ga, gb = sga[f % 2], sgb[f % 2]
                K.dma("sp", ga[:], SG[f * 128:(f + 1) * 128, :], [SG], [ga])
                K.dma("sp", gb[:], SG[2048 + f * 128:2048 + (f + 1) * 128, :], [SG], [gb])
                for n in range(4):
                    ts_ = slice(n * 512, (n + 1) * 512)
                    A_, B_, T1, T2 = pa[it % 2], pb[it % 2], t1[it % 2], t2[it % 2]
                    it += 1
                    for k in range(8):
                        K.mm(A_[:], wa[sc % 2][:, k, ff * 128:(ff + 1) * 128], oaT[:, k, ts_], k == 0, k == 7, [wa[sc % 2], oaT], [A_])
                    for k in range(8):
                        K.mm(B_[:], wbb[sc % 2][:, k, ff * 128:(ff + 1) * 128], obT[:, k, ts_], k == 0, k == 7, [wbb[sc % 2], obT], [B_])
                    K.tt(T1[:], A_[:], ga[:, ts_], ALU.mult, [A_, ga], [T1])
                    K.tt(T2[:], B_[:], gb[:, ts_], ALU.mult, [B_, gb], [T2])
                    K.tt(mT[:, f, ts_], T1[:], T2[:], ALU.add, [T1, T2], [mT], eng="pool")
    K.S.barrier()
    with ExitStack() as s2_:
        wo = [[K.sb(s2_, [128, 8, 512], BF16) for _ in range(2)] for _ in range(2)]
        xt = [K.sb(s2_, [128, 512], F32) for _ in range(3)]
        tt_ = [K.sb(s2_, [128, 512], F32) for _ in range(3)]
        pp = [K.ps(s2_, [128, 512], F32) for _ in range(4)]
        it = 0
        for n in range(4):
            ns = slice(n * 512, (n + 1) * 512)
            w = wo[n % 2]
            for kh in range(2):
                K.dma("pool", w[kh][:], C["w_out"][kh * 1024:(kh + 1) * 1024, ns].rearrange("(k p) n -> p k n", p=128), [], [w[kh]])
            for t in range(16):
                P, X_, T_ = pp[it % 4], xt[it % 3], tt_[it % 3]
                it += 1
                K.dma("sp", X_[:], x_d[t * 128:(t + 1) * 128, ns], [], [X_])
                for k in range(16):
                    K.mm(P[:], mT[:, k, t * 128:(t + 1) * 128], w[k // 8][:, k % 8, :], k == 0, k == 15, [mT, w[k // 8]], [P])
                K.tt(T_[:], P[:], bc[:, 0, ns], ALU.mult, [P, bc], [T_])
                K.tt(T_[:], T_[:], X_[:], ALU.add, [T_, X_], [T_], eng="pool")
                K.dma("sp", X1[t * 128:(t + 1) * 128, ns], T_[:], [T_], [X1])
    p3.close()


def ffn_phase(K, top, C):
    ident_b, ident_f, modT, s2 = C["ident_b"], C["ident_f"], C["modT"], C["s2"]
    X1, out_d, MODS = C["X1"], C["out"], C["MODS"]
    G = 512
    with ExitStack() as s4:
        bc = K.sb(s4, [128, 3, D], F32, "bc_rows4")
        K.dma("sp", bc[:, 1, :], MODS[16:32, :].rearrange("(o a) b -> o (a b)", o=1).to_broadcast([128, D]), [MODS], [bc])
        K.dma("sp", bc[:, 2, :], C["fnw"][0:1, :].to_broadcast([128, D]), [], [bc])
        h2T = K.sb(s4, [128, 16, G], BF16, "h2T")
        actT = K.sb(s4, [128, 44, G], BF16, "actT")
        x1t = [K.sb(s4, [128, D], F32, "x1t%d" % i) for i in range(4)]
        xb = K.sb(s4, [128, D], BF16)
        junk = K.sb(s4, [128, D], BF16)
        ss = K.sb(s4, [128, 1], F32); rs = K.sb(s4, [128, 1], F32)
        wg = [[K.sb(s4, [128, 8, 256], BF16) for _ in range(2)] for _ in range(2)]
        wu = [[K.sb(s4, [128, 8, 256], BF16) for _ in range(2)] for _ in range(2)]
        wd = [K.sb(s4, [128, 4, 512], BF16) for _ in range(4)]
        sgt = [K.sb(s4, [128, G], F32) for _ in range(2)]
        tq = [K.sb(s4, [128, 512], F32) for _ in range(2)]
        ot = [K.sb(s4, [128, D], F32) for _ in range(2)]
        ptr = [K.ps(s4, [128, 8, 128], BF16) for _ in range(1)]
        pgu = [K.ps(s4, [128, 512], F32) for _ in range(3)]
        pdn = [K.ps(s4, [128, 512], F32) for _ in range(4)]
        igu = 0
        for grp in range(NGRP):
            for t in range(4):
                X_ = x1t[t]
                row0 = grp * G + t * 128
                K.dma("sp", X_[:], X1[row0:row0 + 128, :], [X1], [X_])
                K.act(junk[:], X_[:], AF.Square, [X_], [junk, ss], accum=ss[:])
                K.act(rs[:], ss[:], AF.Sqrt, [ss], [rs], scale=1.0 / D, bias=EPS)
                K.recip(rs[:], rs[:], [rs], [rs])
                K.act(xb[:], X_[:], AF.Copy, [X_, rs], [xb], scale=rs[:])
                for g in range(4):
                    P = ptr[0]
                    for j in range(4):
                        k = g * 4 + j
                        K.tr(P[:, j, :], xb[:, k * 128:(k + 1) * 128], ident_b[:], [xb, ident_b], [P])
                    for j in range(4):
                        k = g * 4 + j
                        K.act(h2T[:, k, t * 128:(t + 1) * 128], P[:, j, :], AF.Identity, [P, s2, modT], [h2T],
                              scale=s2[:, k:k + 1], bias=modT[:, 48 + k, 0:1])
            for sc in range(22):
                w1, w2 = wg[sc % 2], wu[sc % 2]
                for kh in range(2):
                    K.dma("pool", w1[kh][:], C["w_gu"][kh * 1024:(kh + 1) * 1024, sc * 256:(sc + 1) * 256].rearrange("(k p) n -> p k n", p=128),
                          [], [w1[kh]])
                    K.dma("pool", w2[kh][:], C["w_gu"][kh * 1024:(kh + 1) * 1024, FFN + sc * 256:FFN + (sc + 1) * 256].rearrange("(k p) n -> p k n", p=128),
                          [], [w2[kh]])
                for jj in range(2):
                    j = sc * 2 + jj
                    Pg, Pu = pgu[igu % 3], pgu[(igu + 1) % 3]
                    SGt = sgt[(igu // 2) % 2]
                    igu += 2
                    for k in range(16):
                        K.mm(Pg[:, :G], w1[k // 8][:, k % 8, jj * 128:(jj + 1) * 128], h2T[:, k, :], k == 0, k == 15, [w1[k // 8], h2T], [Pg])
                    for k in range(16):
                        K.mm(Pu[:, :G], w2[k // 8][:, k % 8, jj * 128:(jj + 1) * 128], h2T[:, k, :], k == 0, k == 15, [w2[k // 8], h2T], [Pu])
                    K.act(SGt[:], Pg[:, :G], AF.Silu, [Pg], [SGt])
                    K.tt(actT[:, j, :], SGt[:], Pu[:, :G], ALU.mult, [SGt, Pu], [actT])
            iw = 0
            for n in range(4):
                ns = slice(n * 512, (n + 1) * 512)
                for k4 in range(11):
                    W = wd[iw % 4]
                    iw += 1
                    K.dma("pool", W[:], C["w_dn"][k4 * 512:(k4 + 1) * 512, ns].rearrange("(k p) n -> p k n", p=128), [], [W])
                    for kk in range(4):
                        k = k4 * 4 + kk
                        for t in range(4):
                            K.mm(pdn[t][:], actT[:, k, t * 128:(t + 1) * 128], W[:, kk, :], k == 0, k == 43, [actT, W], [pdn[t]])
                for t in range(4):
                    T_ = tq[t % 2]
                    K.tt(T_[:], pdn[t][:], bc[:, 1, ns], ALU.mult, [pdn[t], bc], [T_])
                    K.tt(x1t[t][:, ns], x1t[t][:, ns], T_[:], ALU.add, [x1t[t], T_], [x1t[t]], eng="pool")
            for t in range(4):
                X_ = x1t[t]
                O_ = ot[t % 2]
                row0 = grp * G + t * 128
                K.act(junk[:], X_[:], AF.Square, [X_], [junk, ss], accum=ss[:])
                K.act(rs[:], ss[:], AF.Sqrt, [ss], [rs], scale=1.0 / D, bias=EPS)
                K.recip(rs[:], rs[:], [rs], [rs])
                K.stt(O_[:], X_[:], rs[:], bc[:, 2, :], ALU.mult, ALU.mult, [X_, rs, bc], [O_])
                K.dma("sp", out_d[row0:row0 + 128, :], O_[:], [O_], [out_d])

NHP = 8
NGH = 8
NGRP = 4
RELAX = [True, True, True, True, True]
RATIO = 4
RATIO_G = 5


def build(debug=False, only=None):
    nc = bass.Bass("TRN2", target_bir_lowering=False)
    K = KB(nc)
    inp = lambda name, shape: K.dram(name, shape, F32, kind="ExternalInput")
    x_d = inp("x", [TL, D])
    ctx_d = inp("ctx", [TC, D])
    cc_d = inp("cc", [128, 16, 2])
    wada_d = inp("w_ada", [D, 6 * D])
    bada_d = inp("b_ada", [128, 96])
    n1w_d = inp("norm1_w", [128, 16])
    n2w_d = inp("norm2_w", [128, 16])
    fnw_d = inp("final_norm_w", [1, D])
    win_d = inp("w_in", [D, NIN * 128])
    mu_d = inp("rw_mu", [128, 29])
    cmask_d = inp("cmask", [128, 8])
    convw_d = inp("gdn_conv_w", [128, 24, 5])
    ident_d = inp("ident", [128, 128])
    w0_d = inp("rw_w0", [128, 2, 8])
    a0_d = inp("rw_a0", [128, 2, 8])
    kkw_d = inp("rw_k_k", [128, 8])
    ka_d = inp("rw_k_a", [128, 8])
    rk_d = inp("rw_r_k", [128, 8])
    w2_d = inp("rw_w2", [2, 96, 1024])
    a2_d = inp("rw_a2", [2, 96, 1024])
    g2_d = inp("rw_g2", [64, 1024])
    lnw_d = inp("rw_ln_w", [1, 1024])
    lnb_d = inp("rw_ln_b", [1, 1024])
    m1_d = inp("m1", [128, 2, 4, 128])
    m2_d = inp("m2", [128, 2, 3, 128])
    rmask_d = inp("rmask", [128, 512])
    bones_d = inp("blockones", [128, 128])
    hsel_d = inp("headsel", [128, 2])
    nmask_d = inp("nmask", [128, 2, 7, 128])
    selg_d = inp("selg", [64, 16, 128])
    selb_d = inp("selb", [64, 16, 128])
    bigm_d = inp("bigm", [128, 2, 2, 128])
    offd_d = inp("offd", [128, 128])
    ones_d = inp("ones", [128, 128])
    alog_d = inp("gdn_a_log", [64, 1])
    dtb_d = inp("gdn_dt_bias", [64, 1])
    gnw_d = inp("gdn_norm_w", [1, 128])
    pa_d = inp("merge_p_a", [1024, D])
    pb_d = inp("merge_p_b", [1024, D])
    wout_d = inp("w_out", [D, D])
    wgu_d = inp("ffn_w_gate_up", [D, 2 * FFN])
    wdn_d = inp("ffn_w_down", [FFN, D])
    MODS = K.dram("MODS", [32, 128], F32)
    GBS = K.dram("GBS", [2, 64, TT], F32)
    out_d = K.dram("out", [TL, D], F32, kind="ExternalOutput")
    dbg = {}
    if debug:
        dbg["xs"] = K.dram("dbg_xs", [24 * 128, TT], F32, kind="ExternalOutput")
        dbg["u"] = K.dram("dbg_u", [24 * 128, TT], F32, kind="ExternalOutput")
        dbg["mod"] = K.dram("dbg_mod", [128, 96 * 2], F32, kind="ExternalOutput")
        dbg["sm"] = K.dram("dbg_sm", [128, 5, TT], F32, kind="ExternalOutput")
        dbg["oa"] = K.dram("dbg_oa", [128, 8, TL], F32, kind="ExternalOutput")
        dbg["yf"] = K.dram("dbg_yf", [8, 128, 16, 128], F32, kind="ExternalOutput")
        dbg["ob"] = K.dram("dbg_ob", [128, 8, TL], F32, kind="ExternalOutput")
        dbg["of"] = K.dram("dbg_of", [8, 128, 16, 128], F32, kind="ExternalOutput")
    if only:
        XS = inp("XS_in", [24 * 128, TT])
        small_in = inp("small_in", [128, 5, TT])
        US = inp("US_in", [24 * 128, TT])
        SZ = inp("SZ_in", [8 * 128, TL])
        ab_in = inp("ab_in", [128, TT])
    else:
        XS = dbg["xs"] if debug else K.dram("XS", [24 * 128, TT], F32)
    if not only:
        US = dbg["u"] if debug else K.dram("US", [24 * 128, TT], F32)
        SZ = K.dram("SZ", [8 * 128, TL], F32)
    SG = K.dram("SG", [32 * 128, TL], BF16)
    X1 = inp("X1_in", [TL, D]) if only == "ffn" else K.dram("X1", [TL, D], F32)

    with ExitStack() as top:
        ident_f = K.sb(top, [128, 128], F32, "ident_f")
        ident_b = K.sb(top, [128, 128], BF16, "ident_b")
        K.dma("sp", ident_f[:], ident_d[:, :], [], [ident_f])
        K.copy(ident_b[:], ident_f[:], [ident_f], [ident_b])
        modT = K.sb(top, [128, 96, 2], F32, "modT")
        s1 = K.sb(top, [128, 16, 2], F32, "s1")
        s2 = K.sb(top, [128, 16], F32, "s2")
        scopeO = ExitStack()
        abT = K.sb(scopeO, [128, TT], F32, "abT")
        oaT = K.sb(scopeO, [128, 8, TL], BF16, "oaT")
        scopeA = ExitStack()
        smallT = K.sb(scopeA, [128, 5, TT], BF16, "smallT")

        if only == "rwkv":
            K.dma("pool", smallT[:], small_in[:, :, :], [], [smallT])
            K.dma("sp", abT[:], ab_in[:, :], [], [abT])
        with ExitStack() as p0:
          if only != "rwkv":
                ccf = K.sb(p0, [128, 16, 2], F32)
                ccs = K.sb(p0, [128, 16, 2], F32)
                ccb = K.sb(p0, [128, 16, 2], BF16)
                bada = K.sb(p0, [128, 96], F32)
                n1w = K.sb(p0, [128, 16], F32)
                n2w = K.sb(p0, [128, 16], F32)
                K.dma("sp", ccf[:], cc_d[:, :, :], [], [ccf])
                K.dma("sp", bada[:], bada_d[:, :], [], [bada])
                K.dma("sp", n1w[:], n1w_d[:, :], [], [n1w])
                K.dma("sp", n2w[:], n2w_d[:, :], [], [n2w])
                K.act(ccs[:], ccf[:], AF.Silu, [ccf], [ccs])
                K.copy(ccb[:], ccs[:], [ccs], [ccb])
                wb = [[K.sb(p0, [128, 8, 512], BF16) for _ in range(2)] for _ in range(2)]
                pm = K.ps(p0, [128, 96, 2], F32)
                for sc in range(24):
                    w = wb[sc % 2]
                    for kh in range(2):
                        K.dma("pool", w[kh][:],
                              wada_d[kh * 1024:(kh + 1) * 1024, sc * 512:(sc + 1) * 512].rearrange("(k p) n -> p k n", p=128),
                              [], [w[kh]])
                    for jj in range(4):
                        j = sc * 4 + jj
                        for k in range(16):
                            K.mm(pm[:, j, :], w[k // 8][:, k % 8, jj * 128:(jj + 1) * 128], ccb[:, k, :], k == 0, k == 15,
                                 [w[k // 8], ccb], [pm])
                K.tt(modT[:], pm[:], bada[:].unsqueeze(2).to_broadcast([128, 96, 2]), ALU.add, [pm, bada], [modT])
                for v in range(2):
                    K.stt(s1[:, :, v], modT[:, 16:32, v], 1.0, n1w[:], ALU.add, ALU.mult, [modT, n1w], [s1])
                K.stt(s2[:], modT[:, 64:80, 0], 1.0, n2w[:], ALU.add, ALU.mult, [modT, n2w], [s2])
                if debug:
                    K.dma("sp", dbg["mod"][:, :], modT[:].rearrange("p a b -> p (a b)"), [modT], [])

        K.S.barrier()
        with ExitStack() as p1:
          if not only:
                hT = K.sb(p1, [128, 16, TT], BF16, "hT")
                mu = K.sb(p1, [128, 29], F32)
                omm = K.sb(p1, [128, 29], F32)
                cmask = K.sb(p1, [128, 8], F32)
                coef = K.sb(p1, [128, 6, 29], F32)
                convw = K.sb(p1, [128, 24, 5], F32)
                K.dma("sp", mu[:], mu_d[:, :], [], [mu])
                K.dma("sp", cmask[:], cmask_d[:, :], [], [cmask])
                K.dma("sp", convw[:], convw_d[:, :, :], [], [convw])
                K.ts(omm[:], mu[:], -1.0, 1.0, ALU.mult, ALU.add, [mu], [omm])
                for m in range(6):
                    K.ts(coef[:, m, :], mu[:], cmask[:, m:m + 1], None, ALU.mult, None, [mu, cmask], [coef])
                with ExitStack() as pa:
                    xt = [K.sb(pa, [128, D], F32) for _ in range(2)]
                    xb = [K.sb(pa, [128, D], BF16) for _ in range(2)]
                    junk = K.sb(pa, [128, D], BF16)
                    ss = [K.sb(pa, [128, 1], F32) for _ in range(2)]
                    rs = [K.sb(pa, [128, 1], F32) for _ in range(2)]
                    pt = [K.ps(pa, [128, 8, 128], BF16) for _ in range(2)]
                    npt = 0
                    for t in range(NCH):
                        X, XB, SS, RS = xt[t % 2], xb[t % 2], ss[t % 2], rs[t % 2]
                        src = ctx_d[t * 128:(t + 1) * 128, :] if t < 2 else x_d[(t - 2) * 128:(t - 1) * 128, :]
                        v = 1 if t < 2 else 0
                        K.dma("sp", X[:], src, [], [X])
                        K.act(junk[:], X[:], AF.Square, [X], [junk, SS], accum=SS[:])
                        K.act(RS[:], SS[:], AF.Sqrt, [SS], [RS], scale=1.0 / D, bias=EPS)
                        K.recip(RS[:], RS[:], [RS], [RS])
                        K.act(XB[:], X[:], AF.Copy, [X, RS], [XB], scale=RS[:])
                        for g in range(4):
                            P = pt[npt % 2]
                            npt += 1
                            for j in range(4):
                                k = g * 4 + j
                                K.tr(P[:, j, :], XB[:, k * 128:(k + 1) * 128], ident_b[:], [XB, ident_b], [P])
                            for j in range(4):
                                k = g * 4 + j
                                K.act(hT[:, k, t * 128:(t + 1) * 128], P[:, j, :], AF.Identity, [P, s1, modT], [hT],
                                      scale=s1[:, k, v:v + 1], bias=modT[:, k, v:v + 1])
                K.S.relax = RELAX[0]
                K.S.barrier()
                with ExitStack() as pb:
                    wb = [[K.sb(pb, [128, 8, 512], BF16) for _ in range(2)] for _ in range(2)]
                    stage = [K.sb(pb, [128, TT], F32) for _ in range(2)]
                    post = [K.sb(pb, [128, TT], F32) for _ in range(1)] * 2
                    postb = [K.sb(pb, [128, TL], BF16) for _ in range(1)] * 2
                    pp = [K.ps(pb, [128, 512], F32) for _ in range(4)]
                    npp = 0
                    ntile = [(0, 256)] + [(256 + i * 512, 512) for i in range(4)]
                    for sc in range(24):
                        ncol = min(512, NIN * 128 - sc * 512)
                        w = wb[sc % 2]
                        for kh in range(2):
                            K.dma("pool", w[kh][:, :, :ncol],
                                  win_d[kh * 1024:(kh + 1) * 1024, sc * 512:sc * 512 + ncol].rearrange("(k p) n -> p k n", p=128),
                                  [], [w[kh]])
                        for jj in range(ncol // 128):
                            q = sc * 4 + jj
                            lat_only = (53 <= q <= 60) or q >= 62
                            stg = stage[q % 2]
                            for (t0, tn) in ntile:
                                if lat_only and t0 == 0:
                                    continue
                                P = pp[npp % 4]
                                npp += 1
                                for k in range(16):
                                    K.mm(P[:, :tn], w[k // 8][:, k % 8, jj * 128:(jj + 1) * 128], hT[:, k, t0:t0 + tn],
                                         k == 0, k == 15, [w[k // 8], hT], [P])
                                if q <= 28 or 29 <= q <= 52 or q == 61:
                                    K.act(stg[:, t0:t0 + tn], P[:, :tn], AF.Copy, [P], [stg])
                                elif 53 <= q <= 60:
                                    K.act(stg[:, t0:t0 + tn], P[:, :tn], AF.Silu, [P], [stg])
                                else:
                                    K.act(postb[q % 2][:, t0 - 256:t0 - 256 + tn], P[:, :tn], AF.Sigmoid, [P], [postb[q % 2]])
                            if q <= 28:
                                xs = post[q % 2]
                                K.ts(xs[:], stg[:], omm[:, q:q + 1], None, ALU.mult, None, [stg, omm], [xs])
                                pl = stg[:, 256:TT].rearrange("p (r c) -> p r c", c=64)
                                xl = xs[:, 256:TT].rearrange("p (r c) -> p r c", c=64)
                                sh = [(xl[:, :, 1:64], pl[:, :, 0:63]), (xl[:, :, 0:63], pl[:, :, 1:64]),
                                      (xl[:, 1:32, :], pl[:, 0:31, :]), (xl[:, 0:31, :], pl[:, 1:32, :]),
                                      (xs[:, 1:256], stg[:, 0:255]), (xs[:, 0:255], stg[:, 1:256])]
                                for m, (o, i) in enumerate(sh):
                                    K.stt(o, i, coef[:, m, q:q + 1], o, ALU.mult, ALU.add, [stg, xs, coef], [xs])
                                if q < 24:
                                    K.dma("sp", XS[q * 128:(q + 1) * 128, :], xs[:], [xs], [XS])
                                elif q < 26:
                                    K.act(smallT[:, q - 24, :], xs[:], AF.Tanh, [xs], [smallT])
                                elif q < 28:
                                    K.act(smallT[:, q - 24, :], xs[:], AF.Copy, [xs], [smallT])
                                else:
                                    K.act(smallT[:, 4, :], xs[:], AF.Sigmoid, [xs], [smallT])
                            elif q <= 52:
                                g = q - 29
                                acc = post[q % 2]
                                K.ts(acc[:], stg[:], convw[:, g, 2:3], None, ALU.mult, None, [stg, convw], [acc])
                                for (a, b) in ((0, 256), (256, TT)):
                                    for j, o in ((0, 2), (1, 1), (3, -1), (4, -2)):
                                        if o > 0:
                                            ov, iv = acc[:, a + o:b], stg[:, a:b - o]
                                        else:
                                            ov, iv = acc[:, a:b + o], stg[:, a - o:b]
                                        K.stt(ov, iv, convw[:, g, j:j + 1], ov, ALU.mult, ALU.add, [stg, acc, convw], [acc])
                                K.act(acc[:], acc[:], AF.Silu, [acc], [acc])
                                K.dma("sp", US[g * 128:(g + 1) * 128, :], acc[:], [acc], [US])
                            elif q <= 60:
                                K.dma("sp", SZ[(q - 53) * 128:(q - 52) * 128, :], stg[:, 256:TT], [stg], [SZ])
                            elif q == 61:
                                K.copy(abT[:], stg[:], [stg], [abT], eng="pool")
                            else:
                                K.dma("sp", SG[(q - 62) * 128:(q - 61) * 128, :], postb[q % 2][:], [postb[q % 2]], [SG])
                K.S.barrier()
                if debug:
                    smf = K.sb(p1, [128, 5, TT], F32)
                    K.copy(smf[:], smallT[:], [smallT], [smf])
                    K.dma("sp", dbg["sm"][:, :, :], smf[:], [smf], [])

        K.S.relax = RELAX[3]
        K.S.barrier()
        with ExitStack() as p2:
          if only != "ffn":
            rwkv_phase(K, p2, dict(XS=XS, smallT=smallT, oaT=oaT, ident_b=ident_b, ident_f=ident_f,
                                   w0=w0_d, a0=a0_d, kkw=kkw_d, ka=ka_d, rk=rk_d, w2=w2_d, a2=a2_d, g2=g2_d,
                                   lnw=lnw_d, lnb=lnb_d, m1=m1_d, m2=m2_d, rmask=rmask_d, bones=bones_d, hsel=hsel_d, nmask=nmask_d,
                                   dbg=dbg, nhp=NHP))
        K.S.barrier()
        if debug:
            with ExitStack() as pd:
                of = K.sb(pd, [128, 8, TL], F32)
                K.copy(of[:, 0:NHP], oaT[:, 0:NHP], [oaT], [of])
                K.dma("sp", dbg["oa"][:, 0:NHP, :], of[:, 0:NHP], [of], [])
            K.S.barrier()
        scopeA.close()
        obT = K.sb(scopeO, [128, 8, TL], BF16, "obT")
        K.S.relax = RELAX[4]
        with ExitStack() as p2b:
          if only != "ffn":
            gdn_phase(K, p2b, dict(US=US, SZ=SZ, abT=abT, obT=obT, ident_b=ident_b, ident_f=ident_f, selg=selg_d, selb=selb_d,
                                   bigm=bigm_d, offd=offd_d, ones=ones_d, rmask=rmask_d, alog=alog_d, dtb=dtb_d, gnw=gnw_d,
                                   nmask=nmask_d, dbg=dbg, nh=NGH, GBS=GBS))
        K.S.barrier()
        if debug:
            with ExitStack() as pd:
                of = K.sb(pd, [128, 8, TL], F32)
                K.copy(of[:, 0:NGH], obT[:, 0:NGH], [obT], [of])
                K.dma("sp", dbg["ob"][:, 0:NGH, :], of[:, 0:NGH], [of], [])
            K.S.barrier()
        C34 = dict(oaT=oaT, obT=obT, ident_b=ident_b, ident_f=ident_f, modT=modT, s2=s2, SG=SG, X1=X1,
                   x=x_d, out=out_d, MODS=MODS, fnw=fnw_d, p_a=pa_d, p_b=pb_d, w_out=wout_d, w_gu=wgu_d,
                   w_dn=wdn_d, dbg=dbg)
        K.S.relax = RELAX[1]
        if only != "rwkv":
            write_mods(K, C34)
        if not only:
            merge_phase(K, scopeO, C34)
        scopeO.close()
        K.S.relax = RELAX[2]
        K.S.barrier()
        if only != "rwkv":
            ffn_phase(K, top, C34)
        else:
            with ExitStack() as pz:
                z = K.sb(pz, [128, D], F32)
                K.memset(z[:], 0.0, [z])
                K.dma("sp", out_d[0:128, :], z[:], [z], [out_d])
        K.S.emit(nc, top)
    nc._marks = getattr(K.S, "marks", [])
    return nc


def _fm(v, nchunk):
    return np.ascontiguousarray(np.asarray(v, np.float32).reshape(nchunk, 128).T)


def _pad_cols(a, n):
    out = np.zeros(a.shape[:-1] + (n,), np.float32)
    out[..., :a.shape[-1]] = a
    return out


def prep_shared(inputs):
    w_in = np.asarray(inputs["w_in"][0], np.float32)
    RW = 3520
    segs = [w_in[:, 0:3072]]
    for (a, b) in ((3072, 3168), (3168, 3264), (3264, 3360), (3360, 3456), (3456, 3520)):
        segs.append(_pad_cols(w_in[:, a:b], 128))
    segs.append(w_in[:, RW:RW + 3072 + 1024])
    abc = np.zeros((D, 128), np.float32)
    abc[:, 0:16] = w_in[:, 7616:7632]
    abc[:, 32:48] = w_in[:, 7632:7648]
    segs.append(abc)
    segs.append(w_in[:, 7648:])
    win = np.ascontiguousarray(np.concatenate(segs, axis=1))
    assert win.shape == (D, NIN * 128)
    mu = np.asarray(inputs["rw_mu"][0], np.float32)
    mus = [mu[0:3072]]
    for (a, b) in ((3072, 3168), (3168, 3264), (3264, 3360), (3360, 3456), (3456, 3520)):
        mus.append(_pad_cols(mu[a:b], 128))
    mu_fm = _fm(np.concatenate(mus), 29)
    p = np.arange(128)
    cmask = np.zeros((128, 8), np.float32)
    for m in range(4):
        cmask[:, m] = (p % 4 == m)
    cmask[:, 4] = (p % 2 == 0)
    cmask[:, 5] = (p % 2 == 1)
    convw = np.asarray(inputs["gdn_conv_w"][0], np.float32)
    convw_fm = np.ascontiguousarray(convw.reshape(5, 24, 128).transpose(2, 1, 0))
    sh = {
        "w_ada": np.ascontiguousarray(inputs["w_ada"][0], np.float32),
        "b_ada": _fm(inputs["b_ada"][0], 96),
        "norm1_w": _fm(inputs["norm1_w"][0], 16),
        "norm2_w": _fm(inputs["norm2_w"][0], 16),
        "final_norm_w": np.ascontiguousarray(np.asarray(inputs["final_norm_w"], np.float32).reshape(1, D)),
        "w_in": win,
        "rw_mu": mu_fm,
        "cmask": cmask,
        "gdn_conv_w": convw_fm,
        "ident": np.eye(128, dtype=np.float32),
    }
    g = lambda k: np.asarray(inputs[k][0], np.float32)
    sh["rw_w0"] = np.ascontiguousarray(g("rw_w0").reshape(2, 8, 128).transpose(2, 0, 1))
    sh["rw_a0"] = np.ascontiguousarray(g("rw_a0").reshape(2, 8, 128).transpose(2, 0, 1))
    sh["rw_k_k"] = _fm(g("rw_k_k"), 8)
    sh["rw_k_a"] = _fm(g("rw_k_a"), 8)
    sh["rw_r_k"] = _fm(g("rw_r_k").reshape(-1), 8)
    sh["rw_w2"] = np.ascontiguousarray(g("rw_w2"))
    sh["rw_a2"] = np.ascontiguousarray(g("rw_a2"))
    sh["rw_g2"] = np.ascontiguousarray(g("rw_g2"))
    sh["rw_ln_w"] = np.ascontiguousarray(g("rw_ln_w").reshape(1, 1024))
    sh["rw_ln_b"] = np.ascontiguousarray(g("rw_ln_b").reshape(1, 1024))
    r_ = np.arange(128)[:, None]; c_ = np.arange(128)[None, :]
    SL = (c_ < r_).astype(np.float32); SU = (c_ > r_).astype(np.float32)
    IL = (c_ <= r_).astype(np.float32); IU = (c_ >= r_).astype(np.float32)
    m1 = np.stack([np.stack([SL, SU, SL, SU], 0), np.stack([SU, SL, SU, SL], 0)], 0)
    m2 = np.stack([np.stack([SU, IU, -IU], 0), np.stack([SL, IL, -IL], 0)], 0)
    sh["m1"] = np.ascontiguousarray(m1.transpose(2, 0, 1, 3))
    sh["m2"] = np.ascontiguousarray(m2.transpose(2, 0, 1, 3))
    rmask = np.ones((128, 512), np.float32); rmask[:, ::128] = 0.0
    sh["rmask"] = rmask
    bo = np.zeros((128, 128), np.float32); bo[:64, :64] = 1.0; bo[64:, 64:] = 1.0
    sh["blockones"] = bo
    hs = np.zeros((128, 2), np.float32); hs[:64, 0] = 1.0; hs[64:, 1] = 1.0
    sh["headsel"] = hs
    nmk = np.zeros((2, 7, 128, 128), np.float32)
    for lv in range(7):
        bsz = 1 << lv
        low = ((r_ // (2 * bsz) == c_ // (2 * bsz)) & ((r_ // bsz) % 2 == 1) & ((c_ // bsz) % 2 == 0)).astype(np.float32)
        nmk[0, lv] = -low
        nmk[1, lv] = -low.T
    sh["nmask"] = np.ascontiguousarray(nmk.transpose(2, 0, 1, 3))
    selg = np.zeros((64, 16, 128), np.float32); selb = np.zeros((64, 16, 128), np.float32)
    for r0 in range(16):
        selg[r0, r0, :] = 1.0
        selb[32 + r0, r0, :] = 1.0
    sh["selg"] = selg; sh["selb"] = selb
    BIG = 1.0e4
    bigm = np.stack([np.stack([BIG * SU, -BIG * SL], 0), np.stack([BIG * SL, -BIG * SU], 0)], 0)
    sh["bigm"] = np.ascontiguousarray(bigm.transpose(2, 0, 1, 3))
    sh["offd"] = (1.0 - np.eye(128)).astype(np.float32)
    sh["ones"] = np.ones((128, 128), np.float32)
    al = np.zeros((64, 1), np.float32); al[0:16, 0] = g("gdn_a_log").reshape(-1)
    db = np.zeros((64, 1), np.float32); db[0:16, 0] = g("gdn_dt_bias").reshape(-1)
    sh["gdn_a_log"] = al; sh["gdn_dt_bias"] = db
    sh["gdn_norm_w"] = np.ascontiguousarray(g("gdn_norm_w").reshape(1, 128))
    for k_ in ("merge_p_a", "merge_p_b", "w_out", "ffn_w_gate_up", "ffn_w_down"):
        sh[k_] = np.ascontiguousarray(g(k_))
    return sh


def make_in_maps(inputs):
    sh = prep_shared(inputs)
    maps = []
    for b in range(8):
        m = dict(sh)
        m["x"] = np.ascontiguousarray(inputs["x"][b], np.float32)
        m["ctx"] = np.ascontiguousarray(inputs["ctx"][b], np.float32)
        cc = np.stack([np.asarray(inputs["c"][b], np.float32), np.asarray(inputs["c_ctx"], np.float32)], axis=-1)
        m["cc"] = np.ascontiguousarray(cc.reshape(16, 128, 2).transpose(1, 0, 2))
        maps.append(m)
    return maps


_NC = None


def kernel(**inputs):
    global _NC
    if _NC is None:
        _NC = build()
    maps = make_in_maps(inputs)
    res = run_bass_kernel_spmd(_NC, maps, core_ids=list(range(8)))
    return np.stack([r["out"] for r in res.results], axis=0).astype(np.float32)
```

```python
import numpy as np
from contextlib import ExitStack
import concourse.bass as bass
import concourse.mybir as mybir
from concourse.bass_utils import run_bass_kernel_spmd

F32 = mybir.dt.float32
BF16 = mybir.dt.bfloat16
AF = mybir.ActivationFunctionType
ALU = mybir.AluOpType
AX = mybir.AxisListType

COMPUTE = ("pe", "act", "dve", "pool")
NDSEM = 24

D = 2048
TC = 256
TL = 2048
TT = TC + TL
NCH = TT // 128
NIN = 94
FFN = 5632
EPS = 1e-6
DEC = 0.6065306597126334


class Buf:
    __slots__ = ("name", "lw", "rd")

    def __init__(self, name=""):
        self.name = name
        self.lw = None
        self.rd = {}


class Sched:
    def __init__(self):
        self.ops = []
        self.last = {}
        self.dmas = []
        self.bar = set()
        self.bar_seen = set()

    def pe_strict(self, on):
        if on:
            self._saved_relax = getattr(self, "relax", False)
            self.relax = False
        else:
            self.relax = self._saved_relax
            if self.relax and "pe" in self.last:
                self.pe_fence = self.last["pe"]

    def barrier(self):
        if not hasattr(self, "marks"):
            self.marks = []
        self.marks.append({e: sum(1 for o in self.ops if o[0] == e and not o[3]) for e in COMPUTE})
        self.bar = set(self.last.values()) | set(self.dmas)
        self.dmas = []
        self.bar_seen = set()

    def add(self, eng, fn, reads=(), writes=(), dma=False):
        i = len(self.ops)
        deps = set()
        if eng not in self.bar_seen:
            deps |= self.bar
            self.bar_seen.add(eng)
        self.last[eng] = i
        if dma:
            self.dmas.append(i)
        for b in reads:
            if b.lw is not None:
                deps.add(b.lw)
        for b in writes:
            if b.lw is not None:
                deps.add(b.lw)
            deps.update(b.rd.values())
        key = ("d", i) if dma else eng
        for b in reads:
            b.rd[key] = i
        for b in writes:
            b.lw = i
            b.rd = {}
        if eng == "pe" and getattr(self, "relax", False):
            deps = set(d for d in deps if not (self.ops[d][0] == "pe" and not self.ops[d][3]))
            if getattr(self, "pe_fence", None) is not None:
                deps.add(self.pe_fence)
                self.pe_fence = None
        self.ops.append((eng, fn, deps, dma))
        return i

    def emit(self, nc, stack):
        ops = self.ops
        engs = {"pe": nc.tensor, "act": nc.scalar, "dve": nc.vector, "pool": nc.gpsimd, "sp": nc.sync}
        names = list(engs)
        csem = {e: stack.enter_context(nc.semaphore("c_" + e)) for e in COMPUTE}
        dsem = {e: [stack.enter_context(nc.semaphore("d_%s%d" % (e, k))) for k in range(NDSEM)]
                for e in ("sp", "act", "pool")}
        comp = [None] * len(ops)
        cnt = {e: 0 for e in COMPUTE}
        dcnt = {e: 0 for e in dsem}
        prevslot = [None] * len(ops)
        for i, (eng, fn, deps, dma) in enumerate(ops):
            if dma:
                j = dcnt[eng]
                dcnt[eng] += 1
                comp[i] = (dsem[eng][j % NDSEM], 16 * (j // NDSEM + 1))
                if j >= NDSEM:
                    prevslot[i] = (dsem[eng][j % NDSEM], 16 * (j // NDSEM))
            else:
                cnt[eng] += 1
                comp[i] = (csem[eng], cnt[eng])
        per = {e: [] for e in names}
        for i, op in enumerate(ops):
            per[op[0]].append(i)
        block = stack.enter_context(nc.Block())

        def run(ename):
            def body(e):
                known = {}
                for i in per[ename]:
                    eng, fn, deps, dma = ops[i]
                    need = {}
                    cands = [comp[d] for d in deps]
                    if prevslot[i] is not None:
                        cands.append(prevslot[i])
                    for sm, v in cands:
                        k = id(sm)
                        if known.get(k, 0) >= v:
                            continue
                        if k not in need or need[k][1] < v:
                            need[k] = (sm, v)
                    for k, (sm, v) in need.items():
                        e.wait_ge(sm, v)
                        known[k] = v
                    ins = fn(e)
                    sm, v = comp[i]
                    ins.then_inc(sm, 16 if dma else 1)
                if ename in dsem:
                    last = {}
                    for i in per[ename]:
                        if ops[i][3]:
                            sm, v = comp[i]
                            last[id(sm)] = (sm, v)
                    for sm, v in last.values():
                        e.wait_ge(sm, v)
            return body

        block.tensor(run("pe"))
        block.scalar(run("act"))
        block.vector(run("dve"))
        block.gpsimd(run("pool"))
        block.sync(run("sp"))


class T:
    def __init__(self, h, name):
        self.h = h
        self.b = Buf(name)

    def __getitem__(self, k):
        return self.h[k]


def _bufs(xs):
    return [x if isinstance(x, Buf) else x.b for x in xs]


class KB:
    def __init__(self, nc):
        self.nc = nc
        self.S = Sched()
        self.n = 0

    def sb(self, st, shape, dt, name=None):
        self.n += 1
        name = name or "t%d" % self.n
        if not hasattr(self, "used"):
            self.used = set()
        while name in self.used:
            name = name + "_"
        self.used.add(name)
        return T(st.enter_context(self.nc.sbuf_tensor(name, list(shape), dt)), name)

    def ps(self, st, shape, dt, name=None):
        self.n += 1
        name = name or "p%d" % self.n
        return T(st.enter_context(self.nc.psum_tensor(name, list(shape), dt)), name)

    def dram(self, name, shape, dt, kind="Internal"):
        h = self.nc.dram_tensor(name, list(shape), dt, kind=kind)
        t = T(h.ap(), name)
        return t

    def act(self, out, in_, func, r, w, scale=1.0, bias=0.0, accum=None):
        kw = {}
        if accum is not None:
            kw["accum_out"] = accum
        self.S.add("act", lambda e: e.activation(out=out, in_=in_, func=func, scale=scale, bias=bias, **kw),
                   _bufs(r), _bufs(w))

    def tt(self, out, in0, in1, op, r, w, eng="dve"):
        self.S.add(eng, lambda e: e.tensor_tensor(out=out, in0=in0, in1=in1, op=op), _bufs(r), _bufs(w))

    def ts(self, out, in0, s1, s2, op0, op1, r, w, eng="dve", accum=None):
        kw = {}
        if accum is not None:
            kw["accum_out"] = accum
        if op1 is None:
            self.S.add(eng, lambda e: e.tensor_scalar(out=out, in0=in0, scalar1=s1, scalar2=None, op0=op0, **kw),
                       _bufs(r), _bufs(w))
        else:
            self.S.add(eng, lambda e: e.tensor_scalar(out=out, in0=in0, scalar1=s1, scalar2=s2, op0=op0, op1=op1, **kw),
                       _bufs(r), _bufs(w))

    def stt(self, out, in0, scalar, in1, op0, op1, r, w):
        self.S.add("dve", lambda e: e.scalar_tensor_tensor(out=out, in0=in0, scalar=scalar, in1=in1, op0=op0, op1=op1),
                   _bufs(r), _bufs(w))

    def copy(self, out, in_, r, w, eng="dve"):
        self.S.add(eng, lambda e: e.tensor_copy(out=out, in_=in_), _bufs(r), _bufs(w))

    def memset(self, out, val, w, eng="pool"):
        self.S.add(eng, lambda e: e.memset(out, val), [], _bufs(w))

    def recip(self, out, in_, r, w):
        self.S.add("dve", lambda e: e.reciprocal(out=out, in_=in_), _bufs(r), _bufs(w))

    def scan(self, out, d0, d1, init, op0, op1, r, w):
        self.S.add("dve", lambda e: e.tensor_tensor_scan(out=out, data0=d0, data1=d1, initial=init, op0=op0, op1=op1),
                   _bufs(r), _bufs(w))

    def mm(self, out, lhsT, rhs, start, stop, r, w):
        self.S.add("pe", lambda e: e.matmul(out, lhsT=lhsT, rhs=rhs, start=start, stop=stop), _bufs(r), _bufs(w))

    def tr(self, out, in_, ident, r, w):
        self.S.add("pe", lambda e: e.transpose(out=out, in_=in_, identity=ident), _bufs(r), _bufs(w))

    def dma(self, q, out, in_, r, w, **kw):
        self.S.add(q, lambda e: e.dma_start(out=out, in_=in_, **kw), _bufs(r), _bufs(w), dma=True)


def inverse_workspace(K, st, C):
    W = {}
    W["nmask"] = K.sb(st, [128, 2, 7, 128], BF16, "nmask_sb")
    K.dma("pool", W["nmask"][:], C["nmask"][:, :, :, :], [], [W["nmask"]])
    W["ident_b"] = C["ident_b"]
    W["sets"] = []
    for g in range(2):
        W["sets"].append({nm: K.sb(st, [128, 4, 128], BF16, "iw%d_%s" % (g, nm))
                          for nm in ("Xa", "Xb", "Ya", "Yb", "LsX", "LsY", "M1", "M2", "Mt", "R")})
    W["PI"] = [K.ps(st, [128, 4, 128], F32) for _ in range(2)]
    W["cnt"] = 0
    return W


def inverse_units(K, C, LL, n, d, XTb, W, nunits=None):
    LLf = LL[:].rearrange("p j f t -> p (j f) t")
    nm = W["nmask"]
    nunits = 2 * n if nunits is None else nunits
    gs = min(4, nunits)
    idb = W["ident_b"][:].unsqueeze(1).to_broadcast([128, gs, 128])
    bc = lambda m, lv: nm[:, m, lv, :].unsqueeze(1).to_broadcast([128, gs, 128])

    def pi():
        W["cnt"] += 1
        return W["PI"][W["cnt"] % 2]
    mx, my = (0, 1) if d == 0 else (1, 0)

    class V_:
        def __init__(s_, t):
            s_.t = t
            s_.b = t.b

        def __getitem__(s_, k):
            if k == slice(None):
                return s_.t[:, 0:gs, :]
            return s_.t[k]
    groups = []
    for gi, g0 in enumerate(range(0, nunits, gs)):
        S_ = W["sets"][gi % 2]
        st_ = {k_: V_(v_) for k_, v_ in S_.items()}
        st_["g0"] = g0
        st_["Lv"] = LLf[:, 2 * g0:2 * g0 + 2 * gs:2, :]
        st_["LTv"] = LLf[:, 2 * g0 + 1:2 * g0 + 2 * gs:2, :]
        st_["X"], st_["Xn"], st_["Y"], st_["Yn"] = st_["Xa"], st_["Xb"], st_["Ya"], st_["Yb"]
        groups.append(st_)
    for G in groups:
        K.tt(G["LsX"][:], G["Lv"], bc(mx, 0), ALU.mult, [LL, nm], [G["LsX"]], eng="pool")
        K.tt(G["X"][:], G["LsX"][:], idb, ALU.add, [G["LsX"], W["ident_b"]], [G["X"]], eng="pool")
        K.tt(G["LsY"][:], G["LTv"], bc(my, 0), ALU.mult, [LL, nm], [G["LsY"]], eng="pool")
        K.tt(G["Y"][:], G["LsY"][:], idb, ALU.add, [G["LsY"], W["ident_b"]], [G["Y"]], eng="pool")
        K.tt(G["Mt"][:], G["Lv"], idb, ALU.add, [LL, W["ident_b"]], [G["Mt"]], eng="pool")
    yield
    for lv in range(1, 7):
        for G in groups:
            K.tt(G["LsX"][:], G["Lv"], bc(mx, lv), ALU.mult, [LL, nm], [G["LsX"]], eng="pool")
            K.tt(G["LsY"][:], G["LTv"], bc(my, lv), ALU.mult, [LL, nm], [G["LsY"]], eng="pool")
        yield
        for G in groups:
            X, Y, LsX, LsY, M1, M2 = G["X"], G["Y"], G["LsX"], G["LsY"], G["M1"], G["M2"]
            Q = pi()
            for u in range(gs):
                K.mm(Q[:, u, :], LsY[:, u, :], X[:, u, :], True, True, [LsY, X], [Q])
            K.act(M1[:], Q[:, 0:gs, :], AF.Copy, [Q], [M1])
            Q = pi()
            for u in range(gs):
                K.mm(Q[:, u, :], LsX[:, u, :], Y[:, u, :], True, True, [LsX, Y], [Q])
            K.act(M2[:], Q[:, 0:gs, :], AF.Copy, [Q], [M2])
            yield
        for G in groups:
            X, Y, Xn, Yn, M1, M2 = G["X"], G["Y"], G["Xn"], G["Yn"], G["M1"], G["M2"]
            Q = pi()
            for u in range(gs):
                K.mm(Q[:, u, :], Y[:, u, :], M1[:, u, :], True, True, [Y, M1], [Q])
            K.tt(Xn[:], X[:], Q[:, 0:gs, :], ALU.add, [X, Q], [Xn])
            Q = pi()
            for u in range(gs):
                K.mm(Q[:, u, :], X[:, u, :], M2[:, u, :], True, True, [X, M2], [Q])
            K.tt(Yn[:], Y[:], Q[:, 0:gs, :], ALU.add, [Y, Q], [Yn])
            G["X"], G["Xn"], G["Y"], G["Yn"] = Xn, X, Yn, Y
            yield
    for G in groups:
        Q = pi()
        for u in range(gs):
            K.mm(Q[:, u, :], G["Mt"][:, u, :], G["Y"][:, u, :], True, True, [G["Mt"], G["Y"]], [Q])
        K.stt(G["R"][:], Q[:, 0:gs, :], -1.0, idb, ALU.mult, ALU.add, [Q, W["ident_b"]], [G["R"]])
    for G in groups:
        Q = pi()
        for u in range(gs):
            K.mm(Q[:, u, :], G["X"][:, u, :], G["R"][:, u, :], True, True, [G["X"], G["R"]], [Q])
        K.tt(XTb[:, G["g0"]:G["g0"] + gs, :], G["Y"][:], Q[:, 0:gs, :], ALU.add, [G["Y"], Q], [XTb])
    yield


SEGS = [(0, 2)] + [(2 + 4 * i, 4) for i in range(4)]


def rwkv_phase(K, st, C):
    XS, smallT, oaT, ident_b, ident_f = C["XS"], C["smallT"], C["oaT"], C["ident_b"], C["ident_f"]
    dbg = C["dbg"]
    sb = lambda shape, dt, name=None: K.sb(st, shape, dt, name)
    w0 = sb([128, 2, 8], F32); a0 = sb([128, 2, 8], F32)
    kkw = sb([128, 8], F32); ka = sb([128, 8], F32); omka = sb([128, 8], F32); rk = sb([128, 8], F32)
    for t_, d_ in ((w0, C["w0"]), (a0, C["a0"])):
        K.dma("sp", t_[:], d_[:, :, :], [], [t_])
    for t_, d_ in ((kkw, C["kkw"]), (ka, C["ka"]), (rk, C["rk"])):
        K.dma("sp", t_[:], d_[:, :], [], [t_])
    K.ts(omka[:], ka[:], -1.0, 1.0, ALU.mult, ALU.add, [ka], [omka])
    w2b = sb([128, 2, 1024], BF16); a2b = sb([128, 2, 1024], BF16); g2b = sb([64, 1024], BF16)
    K.memset(w2b[:], 0.0, [w2b])
    K.memset(a2b[:], 0.0, [a2b])
    K.dma("pool", w2b[0:96, :, :], C["w2"][:, :, :].rearrange("d r c -> r d c"), [], [w2b])
    K.dma("pool", a2b[0:96, :, :], C["a2"][:, :, :].rearrange("d r c -> r d c"), [], [a2b])
    K.dma("pool", g2b[:], C["g2"][:, :], [], [g2b])
    lnw = sb([128, 128], F32); lnb = sb([128, 128], F32)
    m1f = sb([128, 2, 4, 128], BF16); m2f = sb([128, 2, 3, 128], BF16)
    K.dma("pool", m1f[:], C["m1"][:, :, :, :], [], [m1f])
    K.dma("pool", m2f[:], C["m2"][:, :, :, :], [], [m2f])
    rmask = sb([128, 512], F32); bones = sb([128, 128], F32); hsel = sb([128, 2], F32)
    K.dma("sp", rmask[:], C["rmask"][:, :], [], [rmask])
    K.dma("sp", bones[:], C["bones"][:, :], [], [bones])
    K.dma("sp", hsel[:], C["hsel"][:, :], [], [hsel])
    f32t = lambda nm=None: sb([128, 512], F32, nm)
    bft = lambda nm=None: sb([128, 512], BF16, nm)
    Xr, Xk, Xv = f32t("Xr"), f32t("Xk"), f32t("Xv")
    sig, A, B, Cc, Dd = f32t("sig"), f32t("A"), f32t("B"), f32t("Cc"), f32t("Dd")
    e1, e2, e3, e4 = f32t("e1"), f32t("e2"), f32t("e3"), f32t("e4")
    icl, icl0, kq, sq, rn, kkt, kd, bd, tmp = (f32t(nm) for nm in ("icl", "icl0", "kq", "sq", "rn", "kkt", "kd", "bd", "tmp"))
    gam = sb([128, 4], F32, "gam")
    rt, at, kt, bt, KH, BH, vb = (bft(nm) for nm in ("rt", "at", "kt", "bt", "KH", "BH", "vb"))
    KHt = sb([128, 4, 128], BF16, "KHt"); BHnt = sb([128, 4, 128], BF16, "BHnt"); Vt = sb([128, 4, 128], BF16, "Vt")
    LL = sb([128, 4, 4, 128], BF16, "LL")
    AA = sb([128, 4, 2, 3, 128], BF16, "AA")
    XTb = sb([128, 8, 128], BF16, "XTb")
    IW = inverse_workspace(K, st, C)
    Hf = sb([128, 128], F32, "Hf"); Hb = sb([128, 128], BF16, "Hb")
    P1s = sb([128, 128], BF16, "P1s"); Us = sb([128, 128], BF16, "Us")
    ybuf = sb([128, 16, 128], BF16, "ybuf")
    ytot = sb([128, 4, 128], F32, "ytot"); yc = sb([128, 4, 128], F32, "yc"); ysq = sb([128, 4, 128], F32)
    mean = sb([128, 8], F32); var = sb([128, 8], F32)
    bsum = sb([128, 4, 2], F32)
    oat = sb([128, 4, 128], BF16)
    PF = [K.ps(st, [128, 512], F32) for _ in range(1)]
    PTr = K.ps(st, [128, 8, 128], BF16)
    PG = [K.ps(st, [128, 4, 128], F32) for _ in range(2)]
    PSq = K.ps(st, [128, 512], F32)
    PSh = K.ps(st, [128, 512], F32)
    PS_P1, PS_U, PS_Y, PS_H = PSq, PSq, PSq, PSh
    cnt = {"pf": 0, "pg": 0, "pi": 0, "tr": 0}

    def nxt(lst, key):
        cnt[key] += 1
        return lst[cnt[key] % len(lst)]

    def transp(src, dst, n, scale=None):
        half = cnt["tr"] % 2
        cnt["tr"] += 1
        for j in range(n):
            K.tr(PTr[:, half * 4 + j, :], src[:, j * 128:(j + 1) * 128], ident_b[:], [src, ident_b], [PTr])
        if scale is None:
            K.copy(dst[:, :n, :], PTr[:, half * 4:half * 4 + n, :], [PTr], [dst])
        else:
            K.act(dst[:, :n, :], PTr[:, half * 4:half * 4 + n, :], AF.Copy, [PTr], [dst], scale=scale)

    Xs = [(Xr, Xk, Xv), (Xr, Xk, Xv)]
    rtP = [rt, bft("rt1")]; atP = [at, bft("at1")]; ktP = [kt, bft("kt1")]; btP = [bt, bft("bt1")]
    KHtP = [KHt, sb([128, 4, 128], BF16, "KHt1")]; BHntP = [BHnt, sb([128, 4, 128], BF16, "BHnt1")]
    VtP = [Vt, sb([128, 4, 128], BF16, "Vt1")]
    gamP = [gam, sb([128, 4], F32, "gam1")]
    AAP = [AA, sb([128, 4, 2, 3, 128], BF16, "AA1")]
    XTbP = [XTb, sb([128, 8, 128], BF16, "XTb1")]
    bsumP = [bsum, sb([128, 4, 2], F32, "bsum1")]
    items = []
    for hp in range(C["nhp"]):
        for d in range(2):
            order = SEGS if d == 0 else [SEGS[0], SEGS[4], SEGS[3], SEGS[2], SEGS[1]]
            for si, (c0, n) in enumerate(order):
                items.append((hp, d, c0, n, si == 0))

    def loads(i):
        hp, d, c0, n, first = items[i]
        t0, N = c0 * 128, n * 128
        for X_, row in zip(Xs[i % 2], (0, 1024, 2048)):
            K.dma("sp", X_[:, :N], XS[row + hp * 128:row + hp * 128 + 128, t0:t0 + N], [XS], [X_])

    def stepA(i):
        hp, d, c0, n, first = items[i]
        p = i % 2
        hc = slice(hp * 128, (hp + 1) * 128)
        t0, N = c0 * 128, n * 128
        latent = c0 >= 2
        tk = slice(t0, t0 + N)
        Xr, Xk, Xv = Xs[p]
        rt, at, kt, bt, KHt, BHnt, Vt, gam, bsum = rtP[p], atP[p], ktP[p], btP[p], KHtP[p], BHntP[p], VtP[p], gamP[p], bsumP[p]
        P = nxt(PF, "pf")
        K.mm(P[:, :N], w2b[:, d, hc], smallT[:, d, tk], True, True, [w2b, smallT], [P])
        K.act(sig[:, :N], P[:, :N], AF.Sigmoid, [P, w0], [sig], bias=w0[:, d, hp:hp + 1])
        K.scan(A[:, :N], rmask[:, :N], sig[:, :N], 0.0, ALU.mult, ALU.add, [rmask, sig], [A])
        P = nxt(PF, "pf")
        K.mm(P[:, :N], a2b[:, d, hc], smallT[:, 2 + d, tk], True, True, [a2b, smallT], [P])
        K.act(icl[:, :N], P[:, :N], AF.Sigmoid, [P, a0], [icl], bias=a0[:, d, hp:hp + 1])
        if d == 1 and latent:
            P = nxt(PF, "pf")
            K.mm(P[:, :N], a2b[:, 0, hc], smallT[:, 2, tk], True, True, [a2b, smallT], [P])
            K.act(icl0[:, :N], P[:, :N], AF.Sigmoid, [P, a0], [icl0], bias=a0[:, 0, hp:hp + 1])
        yield
        K.tt(B[:, :N], A[:, :N], sig[:, :N], ALU.subtract, [A, sig], [B])
        v3 = lambda t_: t_[:, :N].rearrange("p (c t) -> p c t", t=128)
        tot = v3(A)[:, :, 127:128]
        K.tt(v3(Cc), tot.to_broadcast([128, n, 128]), v3(A), ALU.subtract, [A], [Cc])
        K.tt(Dd[:, :N], Cc[:, :N], sig[:, :N], ALU.add, [Cc, sig], [Dd], eng="pool")
        yield
        Gi, Gx, Gt = (A, B, Cc) if d == 0 else (Dd, Cc, B)
        K.act(e1[:, :N], Gi[:, :N], AF.Exp, [Gi], [e1], scale=-DEC)
        K.act(e2[:, :N], Gx[:, :N], AF.Exp, [Gx], [e2], scale=-DEC)
        yield
        K.act(e3[:, :N], Gi[:, :N], AF.Exp, [Gi], [e3], scale=DEC)
        K.act(e4[:, :N], Gt[:, :N], AF.Exp, [Gt], [e4], scale=-DEC)
        K.act(gam[:, :n], v3(A)[:, :, 127], AF.Exp, [A], [gam], scale=-DEC)
        yield
        K.act(kq[:, :N], Xk[:, :N], AF.Copy, [Xk, kkw], [kq], scale=kkw[:, hp:hp + 1])
        K.act(sq[:, :N], kq[:, :N], AF.Square, [kq], [sq])
        P = nxt(PF, "pf")
        K.mm(P[:, :N], bones[:], sq[:, :N], True, True, [bones, sq], [P])
        K.act(rn[:, :N], P[:, :N], AF.Sqrt, [P], [rn], bias=EPS)
        K.recip(rn[:, :N], rn[:, :N], [rn], [rn])
        yield
        K.tt(kkt[:, :N], kq[:, :N], rn[:, :N], ALU.mult, [kq, rn], [kkt])
        K.ts(tmp[:, :N], icl[:, :N], ka[:, hp:hp + 1], omka[:, hp:hp + 1], ALU.mult, ALU.add, [icl, ka, omka], [tmp])
        K.tt(kd[:, :N], tmp[:, :N], Xk[:, :N], ALU.mult, [tmp, Xk], [kd])
        yield
        K.tt(bd[:, :N], kkt[:, :N], icl[:, :N], ALU.mult, [kkt, icl], [bd], eng="pool")
        K.tt(rt[:, :N], Xr[:, :N], e1[:, :N], ALU.mult, [Xr, e1], [rt])
        K.tt(at[:, :N], kkt[:, :N], e2[:, :N], ALU.mult, [kkt, e2], [at], eng="pool")
        yield
        K.tt(kt[:, :N], kd[:, :N], e3[:, :N], ALU.mult, [kd, e3], [kt])
        K.tt(bt[:, :N], bd[:, :N], e3[:, :N], ALU.mult, [bd, e3], [bt], eng="pool")
        K.tt(KH[:, :N], kd[:, :N], e4[:, :N], ALU.mult, [kd, e4], [KH])
        yield
        K.tt(BH[:, :N], bd[:, :N], e4[:, :N], ALU.mult, [bd, e4], [BH], eng="pool")
        K.act(vb[:, :N], Xv[:, :N], AF.Copy, [Xv], [vb])
        transp(KH, KHt, n)
        yield
        transp(BH, BHnt, n, scale=-1.0)
        transp(vb, Vt, n)
        yield
        if d == 1 and latent:
            K.tt(tmp[:, :N], icl[:, :N], icl0[:, :N], ALU.add, [icl, icl0], [tmp])
            yield
            K.ts(tmp[:, :N], tmp[:, :N], 0.5, None, ALU.mult, None, [tmp], [tmp])
            K.ts(tmp[:, :N], tmp[:, :N], ka[:, hp:hp + 1], omka[:, hp:hp + 1], ALU.mult, ALU.add, [tmp, ka, omka], [tmp])
            K.tt(tmp[:, :N], tmp[:, :N], Xk[:, :N], ALU.mult, [tmp, Xk], [tmp])
            yield
            K.stt(sq[:, :N], tmp[:, :N], rk[:, hp:hp + 1], Xr[:, :N], ALU.mult, ALU.mult, [tmp, rk, Xr], [sq])
            P = nxt(PF, "pf")
            for j in range(n):
                K.mm(P[:, 2 * j:2 * j + 2], sq[:, j * 128:(j + 1) * 128], hsel[:], True, True, [sq, hsel], [P])
            K.copy(bsum[:].rearrange("p j e -> p (j e)"), P[:, 0:2 * n], [P], [bsum])
            yield
        if i + 1 < len(items):
            loads(i + 1)
        yield

    def stepB(i):
        hp, d, c0, n, first = items[i]
        p = i % 2
        hc = slice(hp * 128, (hp + 1) * 128)
        t0, N = c0 * 128, n * 128
        latent = c0 >= 2
        rt, at, kt, bt, KHt, BHnt, Vt, gam, bsum = rtP[p], atP[p], ktP[p], btP[p], KHtP[p], BHntP[p], VtP[p], gamP[p], bsumP[p]
        AA, XTb = AAP[p], XTbP[p]
        for j in range(n):
            cs = slice(j * 128, (j + 1) * 128)
            K.S.pe_strict(True)
            G = nxt(PG, "pg")
            for e in range(2):
                ps_ = slice(64 * e, 64 * e + 64)
                K.mm(G[:, 2 * e, :], at[ps_, cs], bt[ps_, cs], True, True, [at, bt], [G])
                K.mm(G[:, 2 * e + 1, :], bt[ps_, cs], at[ps_, cs], True, True, [at, bt], [G])
            K.tt(LL[:, j, :, :], G[:], m1f[:, d, :, :], ALU.mult, [G, m1f], [LL])
            for e in range(2):
                ps_ = slice(64 * e, 64 * e + 64)
                G = nxt(PG, "pg")
                K.mm(G[:, 0, :], kt[ps_, cs], at[ps_, cs], True, True, [kt, at], [G])
                K.mm(G[:, 1, :], kt[ps_, cs], rt[ps_, cs], True, True, [kt, rt], [G])
                K.mm(G[:, 2, :], bt[ps_, cs], rt[ps_, cs], True, True, [bt, rt], [G])
                K.tt(AA[:, j, e, :, :], G[:, 0:3, :], m2f[:, d, :, :], ALU.mult, [G, m2f], [AA])
            K.S.pe_strict(False)
            yield
        for _ in inverse_units(K, C, LL, n, d, XTb, IW):
            yield

    def stepC(i):
        hp, d, c0, n, first = items[i]
        p = i % 2
        hc = slice(hp * 128, (hp + 1) * 128)
        t0, N = c0 * 128, n * 128
        latent = c0 >= 2
        rt, at, kt, bt, KHt, BHnt, Vt, gam, bsum = rtP[p], atP[p], ktP[p], btP[p], KHtP[p], BHntP[p], VtP[p], gamP[p], bsumP[p]
        AA, XTb = AAP[p], XTbP[p]
        if first:
            K.memset(Hf[:], 0.0, [Hf])
            K.memset(Hb[:], 0.0, [Hb])
            if d == 1:
                K.dma("sp", lnw[:], C["lnw"][0:1, hc].to_broadcast([128, 128]), [], [lnw])
                K.dma("sp", lnb[:], C["lnb"][0:1, hc].to_broadcast([128, 128]), [], [lnb])
        jl = list(range(n)) if d == 0 else list(range(n - 1, -1, -1))
        for j in jl:
            cs = slice(j * 128, (j + 1) * 128)
            K.mm(PS_P1[:, 0:128], at[:, cs], Hb[:], True, False, [at, Hb], [PS_P1])
            for e in range(2):
                vs = slice(64 * e, 64 * e + 64)
                K.mm(PS_P1[:, 64 * e:64 + 64 * e], AA[:, j, e, 0, :], Vt[:, j, vs], False, e == 1, [AA, Vt], [PS_P1])
            K.act(P1s[:], PS_P1[:, 0:128], AF.Copy, [PS_P1], [P1s])
            yield
            for e in range(2):
                vs = slice(64 * e, 64 * e + 64)
                K.mm(PS_U[:, 128 + 64 * e:192 + 64 * e], XTb[:, 2 * j + e, :], P1s[:, vs], True, True, [XTb, P1s], [PS_U])
            K.copy(Us[:], PS_U[:, 128:256], [PS_U], [Us])
            yield
            if latent:
                K.mm(PS_Y[:, 256:384], rt[:, cs], Hb[:], True, False, [rt, Hb], [PS_Y])
                for e in range(2):
                    vs = slice(64 * e, 64 * e + 64)
                    yo = PS_Y[:, 256 + 64 * e:320 + 64 * e]
                    K.mm(yo, AA[:, j, e, 1, :], Vt[:, j, vs], False, False, [AA, Vt], [PS_Y])
                    K.mm(yo, AA[:, j, e, 2, :], Us[:, vs], False, e == 1, [AA, Us], [PS_Y])
            K.mm(PS_H[:, 384:512], KHt[:, j, :], Vt[:, j, :], True, False, [KHt, Vt], [PS_H])
            K.mm(PS_H[:, 384:512], BHnt[:, j, :], Us[:], False, True, [BHnt, Us], [PS_H])
            for e in range(2):
                ps_ = slice(64 * e, 64 * e + 64)
                vs = slice(64 * e, 64 * e + 64)
                K.stt(Hf[ps_, vs], Hf[ps_, vs], gam[ps_, j:j + 1], PS_H[ps_, 384 + 64 * e:448 + 64 * e], ALU.mult, ALU.add,
                      [Hf, gam, PS_H], [Hf])
            K.act(Hb[:], Hf[:], AF.Copy, [Hf], [Hb])
            if latent:
                cg = c0 - 2 + j
                if d == 0:
                    K.act(ybuf[:, cg, :], PS_Y[:, 256:384], AF.Copy, [PS_Y], [ybuf])
                else:
                    K.tt(ytot[:, j, :], ybuf[:, cg, :], PS_Y[:, 256:384], ALU.add, [ybuf, PS_Y], [ytot])
            yield
        if d == 1 and latent:
            if dbg and hp < 8:
                K.dma("sp", dbg["yf"][hp, :, c0 - 2:c0 - 2 + n, :], ytot[:], [ytot], [])
            yv = ytot[:].rearrange("p j (e c) -> p (j e) c", c=64)
            ycv = yc[:].rearrange("p j (e c) -> p (j e) c", c=64)
            sqv = ysq[:].rearrange("p j (e c) -> p (j e) c", c=64)
            K.S.add("dve", lambda e_: e_.tensor_reduce(out=mean[:], in_=yv, axis=AX.X, op=ALU.add), _bufs([ytot]), _bufs([mean]))
            K.ts(mean[:], mean[:], 1.0 / 64, None, ALU.mult, None, [mean], [mean])
            K.tt(ycv, yv, mean[:].unsqueeze(2).to_broadcast([128, 8, 64]), ALU.subtract, [ytot, mean], [yc])
            K.tt(sqv, ycv, ycv, ALU.mult, [yc], [ysq], eng="pool")
            yield
            K.S.add("dve", lambda e_: e_.tensor_reduce(out=var[:], in_=sqv, axis=AX.X, op=ALU.add), _bufs([ysq]), _bufs([var]))
            K.act(var[:], var[:], AF.Sqrt, [var], [var], scale=1.0 / 64, bias=64e-5)
            K.recip(var[:], var[:], [var], [var])
            K.tt(ycv, ycv, var[:].unsqueeze(2).to_broadcast([128, 8, 64]), ALU.mult, [yc, var], [yc])
            yield
            K.tt(yc[:], yc[:], lnw[:].unsqueeze(1).to_broadcast([128, 4, 128]), ALU.mult, [yc, lnw], [yc])
            K.tt(yc[:], yc[:], lnb[:].unsqueeze(1).to_broadcast([128, 4, 128]), ALU.add, [yc, lnb], [yc])
            K.copy(ysq[:], Vt[:], [Vt], [ysq], eng="pool")
            K.tt(sqv, sqv, bsum[:].rearrange("p j e -> p (j e)").unsqueeze(2).to_broadcast([128, 8, 64]), ALU.mult,
                 [ysq, bsum], [ysq])
            K.tt(yc[:], yc[:], ysq[:], ALU.add, [yc, ysq], [yc])
            yield
            P = nxt(PF, "pf")
            for j in range(n):
                K.mm(P[:, j * 128:(j + 1) * 128], smallT[0:64, 4, t0 + j * 128:t0 + (j + 1) * 128], g2b[0:64, hc], True, True,
                     [smallT, g2b], [P])
            K.tt(oat[:], yc[:], P[:].rearrange("p (j c) -> p j c", c=128), ALU.mult, [yc, P], [oat])
            half = cnt["tr"] % 2
            cnt["tr"] += 1
            for j in range(n):
                K.tr(PTr[:, half * 4 + j, :], oat[:, j, :], ident_b[:], [oat, ident_b], [PTr])
            K.copy(oaT[:, hp, t0 - 256:t0 - 256 + N].rearrange("p (j t) -> p j t", t=128), PTr[:, half * 4:half * 4 + n, :],
                   [PTr], [oaT])
            yield

    def drain(g):
        for _ in g:
            pass

    def chain(*gs):
        for g in gs:
            for _ in g:
                yield

    loads(0)
    drain(stepA(0))
    drain(stepB(0))
    for i in range(len(items)):
        g1 = stepC(i)
        g2 = chain(stepA(i + 1), stepB(i + 1)) if i + 1 < len(items) else iter(())
        a1 = a2_ = True
        while a1 or a2_:
            if a1:
                try:
                    next(g1)
                except StopIteration:
                    a1 = False
            if a2_:
                for _ in range(RATIO):
                    try:
                        next(g2)
                    except StopIteration:
                        a2_ = False
                        break


def gdn_phase(K, st, C):
    US, SZ, abT, obT, ident_b, ident_f = C["US"], C["SZ"], C["abT"], C["obT"], C["ident_b"], C["ident_f"]
    sb = lambda shape, dt, name=None: K.sb(st, shape, dt, name)
    bigm = sb([128, 2, 2, 128], F32); offd = sb([128, 128], F32); ones = sb([128, 128], F32)
    K.dma("sp", bigm[:], C["bigm"][:, :, :, :], [], [bigm])
    K.dma("sp", offd[:], C["offd"][:, :], [], [offd])
    K.dma("sp", ones[:], C["ones"][:, :], [], [ones])
    rmask = sb([128, 512], F32)
    K.dma("sp", rmask[:], C["rmask"][:, :], [], [rmask])
    alog = sb([64, 1], F32); dtb = sb([64, 1], F32); nea = sb([64, 1], F32)
    K.dma("sp", alog[:], C["alog"][:, :], [], [alog])
    K.dma("sp", dtb[:], C["dtb"][:, :], [], [dtb])
    gnw = sb([128, 128], F32)
    K.dma("sp", gnw[:], C["gnw"][0:1, :].to_broadcast([128, 128]), [], [gnw])
    K.act(nea[:], alog[:], AF.Exp, [alog], [nea])
    K.ts(nea[:], nea[:], -1.0, None, ALU.mult, None, [nea], [nea])
    GB = [sb([64, TT], F32, "GB%d" % d) for d in range(2)]
    tokT = [sb([128, NCH, 64], F32, "tokT%d" % d) for d in range(2)]
    with ExitStack() as s0:
        gt = K.sb(s0, [16, TT], F32); A = K.sb(s0, [16, TT], F32); Bx = K.sb(s0, [16, TT], F32)
        K.act(gt[:], abT[0:16, :], AF.Exp, [abT, dtb], [gt], bias=dtb[0:16, :])
        K.act(gt[:], gt[:], AF.Ln, [gt], [gt], bias=1.0)
        K.ts(gt[:], gt[:], nea[0:16, :], None, ALU.mult, None, [gt, nea], [gt])
        for d in range(2):
            K.memset(GB[d][:], 0.0, [GB[d]])
            K.act(GB[d][32:48, :], abT[32:48, :], AF.Sigmoid, [abT], [GB[d]])
        for t0 in range(0, TT, 512):
            N = min(512, TT - t0)
            K.scan(A[:, t0:t0 + N], rmask[0:16, :N], gt[:, t0:t0 + N], 0.0, ALU.mult, ALU.add, [rmask, gt], [A])
        K.copy(GB[0][0:16, :], A[:], [A], [GB[0]], eng="pool")
        K.tt(Bx[:], A[:], gt[:], ALU.subtract, [A, gt], [Bx])
        v3 = lambda t_: t_[:].rearrange("p (c t) -> p c t", t=128)
        tot = v3(A)[:, :, 127:128]
        K.tt(v3(GB[1])[0:16], tot.to_broadcast([16, NCH, 128]), v3(Bx), ALU.subtract, [A, Bx], [GB[1]])
        ptk = K.ps(s0, [128, 8, 64], F32)
        for d in range(2):
            for c8 in range(0, NCH, 8):
                nn = min(8, NCH - c8)
                for j in range(nn):
                    c = c8 + j
                    K.tr(ptk[:, j, :], GB[d][:, c * 128:(c + 1) * 128], ident_f[0:64, 0:64], [GB[d], ident_f], [ptk])
                K.copy(tokT[d][:, c8:c8 + nn, :], ptk[:, 0:nn, :], [ptk], [tokT[d]])
    GBS = C["GBS"]
    for d in range(2):
        K.dma("sp", GBS[d, :, :], GB[d][:], [GB[d]], [GBS])
    K.S.barrier()
    f32t = lambda nm=None: sb([128, 512], F32, nm)
    bft = lambda nm=None: sb([128, 512], BF16, nm)
    XsP = [tuple(f32t("gX%s%d" % (nm, p)) for nm in ("q", "k", "v", "G", "B")) for p in range(2)]
    sq, rn, qn, kn, eG, tmp, sz = (f32t("g_" + nm) for nm in ("sq", "rn", "qn", "kn", "eG", "tmp", "sz"))
    Ktl, vb = bft("g_Ktl"), bft("g_vb")
    knbP, qnbP, kbTP, nKBGP, QdP = ([bft("g_%s%d" % (nm, p)) for p in range(2)] for nm in ("knb", "qnb", "kbT", "nKBG", "Qd"))
    KttP = [sb([128, 4, 128], BF16, "g_Ktt%d" % p) for p in range(2)]
    VtP = [sb([128, 4, 128], BF16, "g_Vt%d" % p) for p in range(2)]
    glP = [sb([128, 4], F32, "g_gl%d" % p) for p in range(2)]
    Dc = sb([128, 4, 2, 128], F32, "g_Dc"); DiT = sb([128, 4, 128], F32, "g_DiT"); Dtmp = sb([128, 4, 128], F32, "g_Dtmp")
    LLg = sb([128, 2, 4, 128], BF16, "g_LL")
    QKt = sb([128, 4, 128], BF16, "g_QKt")
    XTb = sb([128, 4, 128], BF16, "g_XTb")
    IW = inverse_workspace(K, st, C)
    Sf = sb([128, 128], F32, "g_Sf"); Sb = sb([128, 128], BF16, "g_Sb")
    P1s = sb([128, 128], BF16, "g_P1s"); VNs = sb([128, 128], BF16, "g_VNs")
    obuf = sb([128, 16, 128], F32, "g_obuf")
    otot = sb([128, 4, 128], F32, "g_otot"); osq = sb([128, 4, 128], F32, "g_osq")
    ss = sb([128, 4], F32); onb = sb([128, 4, 128], BF16, "g_onb")
    PF = K.ps(st, [128, 512], F32)
    PTr = K.ps(st, [128, 8, 128], BF16)
    PG = [K.ps(st, [128, 4, 128], F32) for _ in range(2)]
    PSq = K.ps(st, [128, 512], F32)
    PSh = K.ps(st, [128, 512], F32)
    cnt = {"pg": 0, "tr": 0}

    def transp(src, dst, n):
        half = cnt["tr"] % 2
        cnt["tr"] += 1
        for j in range(n):
            K.tr(PTr[:, half * 4 + j, :], src[:, j * 128:(j + 1) * 128], ident_b[:], [src, ident_b], [PTr])
        K.copy(dst[:, :n, :], PTr[:, half * 4:half * 4 + n, :], [PTr], [dst])

    items = []
    for h in range(C["nh"]):
        for d in range(2):
            order = SEGS if d == 0 else [SEGS[0], SEGS[4], SEGS[3], SEGS[2], SEGS[1]]
            for si, (c0, n) in enumerate(order):
                items.append((h, d, c0, n, si == 0))

    def loads(i):
        h, d, c0, n, first = items[i]
        r = d * 8 + h
        t0, N = c0 * 128, n * 128
        Xq, Xk, Xv, bcG, bcB = XsP[i % 2]
        for X_, row in ((Xq, 0), (Xk, 1024), (Xv, 2048)):
            K.dma("sp", X_[:, :N], US[row + h * 128:row + h * 128 + 128, t0:t0 + N], [US], [X_])
        K.dma("sp", bcG[:, :N], GBS[d, r:r + 1, t0:t0 + N].to_broadcast([128, N]), [GBS], [bcG])
        K.dma("sp", bcB[:, :N], GBS[d, 32 + r:33 + r, t0:t0 + N].to_broadcast([128, N]), [GBS], [bcB])

    def stepA(i):
        h, d, c0, n, first = items[i]
        p = i % 2
        t0, N = c0 * 128, n * 128
        Xq, Xk, Xv, bcG, bcB = XsP[p]
        knb, qnb, kbT, nKBG, Qd, Ktt, Vt, gl = knbP[p], qnbP[p], kbTP[p], nKBGP[p], QdP[p], KttP[p], VtP[p], glP[p]
        v3 = lambda t_: t_[:, :N].rearrange("p (c t) -> p c t", t=128)
        if i + 1 < len(items):
            loads(i + 1)
        for X_, o_, sc_ in ((Xq, qn, 128 ** -0.5), (Xk, kn, 1.0)):
            K.act(sq[:, :N], X_[:, :N], AF.Square, [X_], [sq])
            K.mm(PF[:, :N], ones[:], sq[:, :N], True, True, [ones, sq], [PF])
            K.act(rn[:, :N], PF[:, :N], AF.Sqrt, [PF], [rn], bias=EPS)
            K.recip(rn[:, :N], rn[:, :N], [rn], [rn])
            K.stt(o_[:, :N], X_[:, :N], sc_, rn[:, :N], ALU.mult, ALU.mult, [X_, rn], [o_])
            yield
        K.act(eG[:, :N], bcG[:, :N], AF.Exp, [bcG], [eG])
        lastcol = 127 if d == 0 else 0
        K.copy(gl[:, :n], v3(eG)[:, :, lastcol], [eG], [gl], eng="pool")
        glast = v3(bcG)[:, :, lastcol:lastcol + 1]
        K.tt(kbT[:, :N], kn[:, :N], bcB[:, :N], ALU.mult, [kn, bcB], [kbT])
        yield
        K.copy(knb[:, :N], kn[:, :N], [kn], [knb], eng="pool")
        K.act(qnb[:, :N], qn[:, :N], AF.Copy, [qn], [qnb])
        K.stt(nKBG[:, :N], kbT[:, :N], -1.0, eG[:, :N], ALU.mult, ALU.mult, [kbT, eG], [nKBG])
        yield
        K.tt(Qd[:, :N], qn[:, :N], eG[:, :N], ALU.mult, [qn, eG], [Qd], eng="pool")
        K.tt(v3(tmp), glast.to_broadcast([128, n, 128]), v3(bcG), ALU.subtract, [bcG], [tmp])
        K.act(tmp[:, :N], tmp[:, :N], AF.Exp, [tmp], [tmp])
        yield
        K.tt(Ktl[:, :N], kn[:, :N], tmp[:, :N], ALU.mult, [kn, tmp], [Ktl])
        K.act(vb[:, :N], Xv[:, :N], AF.Copy, [Xv], [vb])
        transp(Ktl, Ktt, n)
        yield
        transp(vb, Vt, n)
        yield

    def stepBC(i):
        h, d, c0, n, first = items[i]
        p = i % 2
        r = d * 8 + h
        t0, N = c0 * 128, n * 128
        latent = c0 >= 2
        Xq, Xk, Xv, bcG, bcB = XsP[p]
        knb, qnb, kbT, nKBG, Qd, Ktt, Vt, gl = knbP[p], qnbP[p], kbTP[p], nKBGP[p], QdP[p], KttP[p], VtP[p], glP[p]
        v3 = lambda t_: t_[:, :N].rearrange("p (c t) -> p c t", t=128)
        if first:
            K.memset(Sf[:], 0.0, [Sf])
            K.memset(Sb[:], 0.0, [Sb])
        gct = tokT[d][:, c0:c0 + n, r:r + 1].to_broadcast([128, n, 128])
        bg3 = v3(bcG)
        K.tt(Dtmp[:, :n, :], bg3, bigm[:, d, 0, :].unsqueeze(1).to_broadcast([128, n, 128]), ALU.add, [bcG, bigm], [Dtmp])
        K.tt(Dtmp[:, :n, :], Dtmp[:, :n, :], gct, ALU.subtract, [Dtmp, tokT[d]], [Dtmp], eng="pool")
        K.act(Dtmp[:, :n, :], Dtmp[:, :n, :], AF.Exp, [Dtmp], [Dtmp], scale=-1.0)
        K.tt(Dc[:, :n, 0, :], Dtmp[:, :n, :], offd[:].unsqueeze(1).to_broadcast([128, n, 128]), ALU.mult, [Dtmp, offd], [Dc],
             eng="pool")
        yield
        K.tt(DiT[:, :n, :], bg3, bigm[:, d, 1, :].unsqueeze(1).to_broadcast([128, n, 128]), ALU.add, [bcG, bigm], [DiT])
        K.tt(DiT[:, :n, :], DiT[:, :n, :], gct, ALU.subtract, [DiT, tokT[d]], [DiT], eng="pool")
        K.act(DiT[:, :n, :], DiT[:, :n, :], AF.Exp, [DiT], [DiT])
        K.tt(Dc[:, :n, 1, :], DiT[:, :n, :], offd[:].unsqueeze(1).to_broadcast([128, n, 128]), ALU.mult, [DiT, offd], [Dc],
             eng="pool")
        yield
        LLv = LLg[:].rearrange("p a b t -> p (a b) t")
        for j in range(n):
            cs = slice(j * 128, (j + 1) * 128)
            cnt["pg"] += 1
            G = PG[cnt["pg"] % 2]
            K.mm(G[:, 0, :], kbT[:, cs], knb[:, cs], True, True, [kbT, knb], [G])
            K.mm(G[:, 1, :], knb[:, cs], kbT[:, cs], True, True, [kbT, knb], [G])
            K.mm(G[:, 2, :], knb[:, cs], qnb[:, cs], True, True, [qnb, knb], [G])
            K.tt(LLv[:, 2 * j:2 * j + 2, :], G[:, 0:2, :], Dc[:, j, :, :], ALU.mult, [G, Dc], [LLg])
            K.tt(QKt[:, j, :], G[:, 2, :], DiT[:, j, :], ALU.mult, [G, DiT], [QKt])
            yield
        for _ in inverse_units(K, C, LLg, n, d, XTb, IW, nunits=n):
            yield
        jl = list(range(n)) if d == 0 else list(range(n - 1, -1, -1))
        for j in jl:
            cs = slice(j * 128, (j + 1) * 128)
            c = c0 + j
            K.mm(PSq[:, 0:128], nKBG[:, cs], Sb[:], True, True, [nKBG, Sb], [PSq])
            K.stt(P1s[:], Vt[:, j, :], tokT[d][:, c, 32 + r:33 + r], PSq[:, 0:128], ALU.mult, ALU.add,
                  [Vt, tokT[d], PSq], [P1s])
            yield
            K.mm(PSq[:, 128:256], XTb[:, j, :], P1s[:], True, True, [XTb, P1s], [PSq])
            K.act(VNs[:], PSq[:, 128:256], AF.Copy, [PSq], [VNs])
            yield
            if latent:
                K.mm(PSq[:, 256:384], Qd[:, cs], Sb[:], True, False, [Qd, Sb], [PSq])
                K.mm(PSq[:, 256:384], QKt[:, j, :], VNs[:], False, True, [QKt, VNs], [PSq])
            K.mm(PSh[:, 0:128], Ktt[:, j, :], VNs[:], True, True, [Ktt, VNs], [PSh])
            K.stt(Sf[:], Sf[:], gl[:, j:j + 1], PSh[:, 0:128], ALU.mult, ALU.add, [Sf, gl, PSh], [Sf])
            K.act(Sb[:], Sf[:], AF.Copy, [Sf], [Sb])
            if latent:
                cg = c - 2
                if d == 0:
                    K.act(obuf[:, cg, :], PSq[:, 256:384], AF.Copy, [PSq], [obuf])
                else:
                    K.tt(otot[:, j, :], obuf[:, cg, :], PSq[:, 256:384], ALU.add, [obuf, PSq], [otot])
            yield
        if d == 1 and latent:
            if C["dbg"]:
                K.dma("sp", C["dbg"]["of"][h, :, c0 - 2:c0 - 2 + n, :], otot[:], [otot], [])
            K.tt(osq[:], otot[:], otot[:], ALU.mult, [otot], [osq], eng="pool")
            K.S.add("dve", lambda e_: e_.tensor_reduce(out=ss[:], in_=osq[:], axis=AX.X, op=ALU.add), _bufs([osq]), _bufs([ss]))
            K.act(ss[:], ss[:], AF.Sqrt, [ss], [ss], scale=1.0 / 128, bias=EPS)
            K.recip(ss[:], ss[:], [ss], [ss])
            yield
            K.tt(osq[:], otot[:], ss[:].unsqueeze(2).to_broadcast([128, 4, 128]), ALU.mult, [otot, ss], [osq])
            K.tt(onb[:], osq[:], gnw[:].unsqueeze(1).to_broadcast([128, 4, 128]), ALU.mult, [osq, gnw], [onb])
            K.dma("sp", sz[:, :N], SZ[h * 128:(h + 1) * 128, t0 - 256:t0 - 256 + N], [SZ], [sz])
            yield
            half = cnt["tr"] % 2
            cnt["tr"] += 1
            for j in range(n):
                K.tr(PTr[:, half * 4 + j, :], onb[:, j, :], ident_b[:], [onb, ident_b], [PTr])
            K.tt(obT[:, h, t0 - 256:t0 - 256 + N].rearrange("p (j t) -> p j t", t=128), PTr[:, half * 4:half * 4 + n, :],
                 sz[:, :N].rearrange("p (j t) -> p j t", t=128), ALU.mult, [PTr, sz], [obT])
            yield

    loads(0)
    for _ in stepA(0):
        pass
    for i in range(len(items)):
        g1 = stepBC(i)
        g2 = stepA(i + 1) if i + 1 < len(items) else iter(())
        a1 = a2 = True
        while a1 or a2:
            if a1:
                for _ in range(RATIO_G):
                    try:
                        next(g1)
                    except StopIteration:
                        a1 = False
                        break
            if a2:
                try:
                    next(g2)
                except StopIteration:
                    a2 = False


def write_mods(K, C):
    modT, ident_f, MODS = C["modT"], C["ident_f"], C["MODS"]
    with ExitStack() as s0:
        pt = K.ps(s0, [16, 2, 128], F32)
        rows = K.sb(s0, [16, 2, 128], F32)
        for i, sec in enumerate((2, 5)):
            K.tr(pt[:, i, :], modT[:, sec * 16:(sec + 1) * 16, 0], ident_f[:], [modT, ident_f], [pt])
        K.copy(rows[:], pt[:], [pt], [rows])
        for i in range(2):
            K.dma("sp", MODS[i * 16:(i + 1) * 16, :], rows[:, i, :], [rows], [MODS])
    K.S.barrier()


def merge_phase(K, top, C):
    oaT, obT, ident_b, ident_f, modT, s2 = C["oaT"], C["obT"], C["ident_b"], C["ident_f"], C["modT"], C["s2"]
    SG, X1, x_d, out_d = C["SG"], C["X1"], C["x"], C["out"]
    MODS = C["MODS"]
    dbg = C["dbg"]
    bc = K.sb(top, [128, 1, D], F32, "bc_rows")
    K.dma("sp", bc[:, 0, :], MODS[0:16, :].rearrange("(o a) b -> o (a b)", o=1).to_broadcast([128, D]), [MODS], [bc])
    K.S.barrier()
    p3 = ExitStack()
    mT = K.sb(p3, [128, 16, TL], BF16, "mT")
    with ExitStack() as s1:
        wa = [K.sb(s1, [128, 8, 256], BF16) for _ in range(2)]
        wbb = [K.sb(s1, [128, 8, 256], BF16) for _ in range(2)]
        sga = [K.sb(s1, [128, TL], BF16)] * 2
        sgb = [K.sb(s1, [128, TL], BF16)] * 2
        t1 = [K.sb(s1, [128, 512], F32) for _ in range(2)]
        t2 = [K.sb(s1, [128, 512], F32) for _ in range(2)]
        pa = [K.ps(s1, [128, 512], F32) for _ in range(2)]
        pb = [K.ps(s1, [128, 512], F32) for _ in range(2)]
        it = 0
        for sc in range(8):
            K.dma("pool", wa[sc % 2][:], C["p_a"][:, sc * 256:(sc + 1) * 256].rearrange("(k p) n -> p k n", p=128), [], [wa[sc % 2]])
            K.dma("pool", wbb[sc % 2][:], C["p_b"][:, sc * 256:(sc + 1) * 256].rearrange("(k p) n -> p k n", p=128), [], [wbb[sc % 2]])
            for ff in range(2):
                f = sc * 2 + ff
                ga, gb = sga[f % 2], sgb[f % 2]
                K.dma("sp", ga[:], SG[f * 128:(f + 1) * 128, :], [SG], [ga])
                K.dma("sp", gb[:], SG[2048 + f * 128:2048 + (f + 1) * 128, :], [SG], [gb])
                for n in range(4):
                    ts_ = slice(n * 512, (n + 1) * 512)
                    A_, B_, T1, T2 = pa[it % 2], pb[it % 2], t1[it % 2], t2[it % 2]
                    it += 1
                    for k in range(8):
                        K.mm(A_[:], wa[sc % 2][:, k, ff * 128:(ff + 1) * 128], oaT[:, k, ts_], k == 0, k == 7, [wa[sc % 2], oaT], [A_])
                    for k in range(8):
                        K.mm(B_[:], wbb[sc % 2][:, k, ff * 128:(ff + 1) * 128], obT[:, k, ts_], k == 0, k == 7, [wbb[sc % 2], obT], [B_])
                    K.tt(T1[:], A_[:], ga[:, ts_], ALU.mult, [A_, ga], [T1])
                    K.tt(T2[:], B_[:], gb[:, ts_], ALU.mult, [B_, gb], [T2])
                    K.tt(mT[:, f, ts_], T1[:], T2[:], ALU.add, [T1, T2], [mT], eng="pool")
    K.S.barrier()
    with ExitStack() as s2_:
        wo = [[K.sb(s2_, [128, 8, 512], BF16) for _ in range(2)] for _ in range(2)]
        xt = [K.sb(s2_, [128, 512], F32) for _ in range(3)]
        tt_ = [K.sb(s2_, [128, 512], F32) for _ in range(3)]
        pp = [K.ps(s2_, [128, 512], F32) for _ in range(4)]
        it = 0
        for n in range(4):
            ns = slice(n * 512, (n + 1) * 512)
            w = wo[n % 2]
            for kh in range(2):
                K.dma("pool", w[kh][:], C["w_out"][kh * 1024:(kh + 1) * 1024, ns].rearrange("(k p) n -> p k n", p=128), [], [w[kh]])
            for t in range(16):
                P, X_, T_ = pp[it % 4], xt[it % 3], tt_[it % 3]
                it += 1
                K.dma("sp", X_[:], x_d[t * 128:(t + 1) * 128, ns], [], [X_])
                for k in range(16):
                    K.mm(P[:], mT[:, k, t * 128:(t + 1) * 128], w[k // 8][:, k % 8, :], k == 0, k == 15, [mT, w[k // 8]], [P])
                K.tt(T_[:], P[:], bc[:, 0, ns], ALU.mult, [P, bc], [T_])
                K.tt(T_[:], T_[:], X_[:], ALU.add, [T_, X_], [T_], eng="pool")
                K.dma("sp", X1[t * 128:(t + 1) * 128, ns], T_[:], [T_], [X1])
    p3.close()


def ffn_phase(K, top, C):
    ident_b, ident_f, modT, s2 = C["ident_b"], C["ident_f"], C["modT"], C["s2"]
    X1, out_d, MODS = C["X1"], C["out"], C["MODS"]
    G = 512
    with ExitStack() as s4:
        bc = K.sb(s4, [128, 3, D], F32, "bc_rows4")
        K.dma("sp", bc[:, 1, :], MODS[16:32, :].rearrange("(o a) b -> o (a b)", o=1).to_broadcast([128, D]), [MODS], [bc])
        K.dma("sp", bc[:, 2, :], C["fnw"][0:1, :].to_broadcast([128, D]), [], [bc])
        h2T = K.sb(s4, [128, 16, G], BF16, "h2T")
        actT = K.sb(s4, [128, 44, G], BF16, "actT")
        x1t = [K.sb(s4, [128, D], F32, "x1t%d" % i) for i in range(4)]
        xb = K.sb(s4, [128, D], BF16)
        junk = K.sb(s4, [128, D], BF16)
        ss = K.sb(s4, [128, 1], F32); rs = K.sb(s4, [128, 1], F32)
        wg = [[K.sb(s4, [128, 8, 256], BF16) for _ in range(2)] for _ in range(2)]
        wu = [[K.sb(s4, [128, 8, 256], BF16) for _ in range(2)] for _ in range(2)]
        wd = [K.sb(s4, [128, 4, 512], BF16) for _ in range(4)]
        sgt = [K.sb(s4, [128, G], F32) for _ in range(2)]
        tq = [K.sb(s4, [128, 512], F32) for _ in range(2)]
        ot = [K.sb(s4, [128, D], F32) for _ in range(2)]
        ptr = [K.ps(s4, [128, 8, 128], BF16) for _ in range(1)]
        pgu = [K.ps(s4, [128, 512], F32) for _ in range(3)]
        pdn = [K.ps(s4, [128, 512], F32) for _ in range(4)]
        igu = 0
        for grp in range(NGRP):
            for t in range(4):
                X_ = x1t[t]
                row0 = grp * G + t * 128
                K.dma("sp", X_[:], X1[row0:row0 + 128, :], [X1], [X_])
                K.act(junk[:], X_[:], AF.Square, [X_], [junk, ss], accum=ss[:])
                K.act(rs[:], ss[:], AF.Sqrt, [ss], [rs], scale=1.0 / D, bias=EPS)
                K.recip(rs[:], rs[:], [rs], [rs])
                K.act(xb[:], X_[:], AF.Copy, [X_, rs], [xb], scale=rs[:])
                for g in range(4):
                    P = ptr[0]
                    for j in range(4):
                        k = g * 4 + j
                        K.tr(P[:, j, :], xb[:, k * 128:(k + 1) * 128], ident_b[:], [xb, ident_b], [P])
                    for j in range(4):
                        k = g * 4 + j
                        K.act(h2T[:, k, t * 128:(t + 1) * 128], P[:, j, :], AF.Identity, [P, s2, modT], [h2T],
                              scale=s2[:, k:k + 1], bias=modT[:, 48 + k, 0:1])
            for sc in range(22):
                w1, w2 = wg[sc % 2], wu[sc % 2]
                for kh in range(2):
                    K.dma("pool", w1[kh][:], C["w_gu"][kh * 1024:(kh + 1) * 1024, sc * 256:(sc + 1) * 256].rearrange("(k p) n -> p k n", p=128),
                          [], [w1[kh]])
                    K.dma("pool", w2[kh][:], C["w_gu"][kh * 1024:(kh + 1) * 1024, FFN + sc * 256:FFN + (sc + 1) * 256].rearrange("(k p) n -> p k n", p=128),
                          [], [w2[kh]])
                for jj in range(2):
                    j = sc * 2 + jj
                    Pg, Pu = pgu[igu % 3], pgu[(igu + 1) % 3]
                    SGt = sgt[(igu // 2) % 2]
                    igu += 2
                    for k in range(16):
                        K.mm(Pg[:, :G], w1[k // 8][:, k % 8, jj * 128:(jj + 1) * 128], h2T[:, k, :], k == 0, k == 15, [w1[k // 8], h2T], [Pg])
                    for k in range(16):
                        K.mm(Pu[:, :G], w2[k // 8][:, k % 8, jj * 128:(jj + 1) * 128], h2T[:, k, :], k == 0, k == 15, [w2[k // 8], h2T], [Pu])
                    K.act(SGt[:], Pg[:, :G], AF.Silu, [Pg], [SGt])
                    K.tt(actT[:, j, :], SGt[:], Pu[:, :G], ALU.mult, [SGt, Pu], [actT])
            iw = 0
            for n in range(4):
                ns = slice(n * 512, (n + 1) * 512)
                for k4 in range(11):
                    W = wd[iw % 4]
                    iw += 1
                    K.dma("pool", W[:], C["w_dn"][k4 * 512:(k4 + 1) * 512, ns].rearrange("(k p) n -> p k n", p=128), [], [W])
                    for kk in range(4):
                        k = k4 * 4 + kk
                        for t in range(4):
                            K.mm(pdn[t][:], actT[:, k, t * 128:(t + 1) * 128], W[:, kk, :], k == 0, k == 43, [actT, W], [pdn[t]])
                for t in range(4):
                    T_ = tq[t % 2]
                    K.tt(T_[:], pdn[t][:], bc[:, 1, ns], ALU.mult, [pdn[t], bc], [T_])
                    K.tt(x1t[t][:, ns], x1t[t][:, ns], T_[:], ALU.add, [x1t[t], T_], [x1t[t]], eng="pool")
            for t in range(4):
                X_ = x1t[t]
                O_ = ot[t % 2]
                row0 = grp * G + t * 128
                K.act(junk[:], X_[:], AF.Square, [X_], [junk, ss], accum=ss[:])
                K.act(rs[:], ss[:], AF.Sqrt, [ss], [rs], scale=1.0 / D, bias=EPS)
                K.recip(rs[:], rs[:], [rs], [rs])
                K.stt(O_[:], X_[:], rs[:], bc[:, 2, :], ALU.mult, ALU.mult, [X_, rs, bc], [O_])
                K.dma("sp", out_d[row0:row0 + 128, :], O_[:], [O_], [out_d])

NHP = 8
NGH = 8
NGRP = 4
RELAX = [True, True, True, True, True]
RATIO = 3
RATIO_G = 5


def build(debug=False, only=None):
    nc = bass.Bass("TRN2", target_bir_lowering=False)
    K = KB(nc)
    inp = lambda name, shape: K.dram(name, shape, F32, kind="ExternalInput")
    x_d = inp("x", [TL, D])
    ctx_d = inp("ctx", [TC, D])
    cc_d = inp("cc", [128, 16, 2])
    wada_d = inp("w_ada", [D, 6 * D])
    bada_d = inp("b_ada", [128, 96])
    n1w_d = inp("norm1_w", [128, 16])
    n2w_d = inp("norm2_w", [128, 16])
    fnw_d = inp("final_norm_w", [1, D])
    win_d = inp("w_in", [D, NIN * 128])
    mu_d = inp("rw_mu", [128, 29])
    cmask_d = inp("cmask", [128, 8])
    convw_d = inp("gdn_conv_w", [128, 24, 5])
    ident_d = inp("ident", [128, 128])
    w0_d = inp("rw_w0", [128, 2, 8])
    a0_d = inp("rw_a0", [128, 2, 8])
    kkw_d = inp("rw_k_k", [128, 8])
    ka_d = inp("rw_k_a", [128, 8])
    rk_d = inp("rw_r_k", [128, 8])
    w2_d = inp("rw_w2", [2, 96, 1024])
    a2_d = inp("rw_a2", [2, 96, 1024])
    g2_d = inp("rw_g2", [64, 1024])
    lnw_d = inp("rw_ln_w", [1, 1024])
    lnb_d = inp("rw_ln_b", [1, 1024])
    m1_d = inp("m1", [128, 2, 4, 128])
    m2_d = inp("m2", [128, 2, 3, 128])
    rmask_d = inp("rmask", [128, 512])
    bones_d = inp("blockones", [128, 128])
    hsel_d = inp("headsel", [128, 2])
    nmask_d = inp("nmask", [128, 2, 7, 128])
    selg_d = inp("selg", [64, 16, 128])
    selb_d = inp("selb", [64, 16, 128])
    bigm_d = inp("bigm", [128, 2, 2, 128])
    offd_d = inp("offd", [128, 128])
    ones_d = inp("ones", [128, 128])
    alog_d = inp("gdn_a_log", [64, 1])
    dtb_d = inp("gdn_dt_bias", [64, 1])
    gnw_d = inp("gdn_norm_w", [1, 128])
    pa_d = inp("merge_p_a", [1024, D])
    pb_d = inp("merge_p_b", [1024, D])
    wout_d = inp("w_out", [D, D])
    wgu_d = inp("ffn_w_gate_up", [D, 2 * FFN])
    wdn_d = inp("ffn_w_down", [FFN, D])
    MODS = K.dram("MODS", [32, 128], F32)
    GBS = K.dram("GBS", [2, 64, TT], F32)
    out_d = K.dram("out", [TL, D], F32, kind="ExternalOutput")
    dbg = {}
    if debug:
        dbg["xs"] = K.dram("dbg_xs", [24 * 128, TT], F32, kind="ExternalOutput")
        dbg["u"] = K.dram("dbg_u", [24 * 128, TT], F32, kind="ExternalOutput")
        dbg["mod"] = K.dram("dbg_mod", [128, 96 * 2], F32, kind="ExternalOutput")
        dbg["sm"] = K.dram("dbg_sm", [128, 5, TT], F32, kind="ExternalOutput")
        dbg["oa"] = K.dram("dbg_oa", [128, 8, TL], F32, kind="ExternalOutput")
        dbg["yf"] = K.dram("dbg_yf", [8, 128, 16, 128], F32, kind="ExternalOutput")
        dbg["ob"] = K.dram("dbg_ob", [128, 8, TL], F32, kind="ExternalOutput")
        dbg["of"] = K.dram("dbg_of", [8, 128, 16, 128], F32, kind="ExternalOutput")
    if only:
        XS = inp("XS_in", [24 * 128, TT])
        small_in = inp("small_in", [128, 5, TT])
        US = inp("US_in", [24 * 128, TT])
        SZ = inp("SZ_in", [8 * 128, TL])
        ab_in = inp("ab_in", [128, TT])
    else:
        XS = dbg["xs"] if debug else K.dram("XS", [24 * 128, TT], F32)
    if not only:
        US = dbg["u"] if debug else K.dram("US", [24 * 128, TT], F32)
        SZ = K.dram("SZ", [8 * 128, TL], F32)
    SG = K.dram("SG", [32 * 128, TL], BF16)
    X1 = inp("X1_in", [TL, D]) if only == "ffn" else K.dram("X1", [TL, D], F32)

    with ExitStack() as top:
        ident_f = K.sb(top, [128, 128], F32, "ident_f")
        ident_b = K.sb(top, [128, 128], BF16, "ident_b")
        K.dma("sp", ident_f[:], ident_d[:, :], [], [ident_f])
        K.copy(ident_b[:], ident_f[:], [ident_f], [ident_b])
        modT = K.sb(top, [128, 96, 2], F32, "modT")
        s1 = K.sb(top, [128, 16, 2], F32, "s1")
        s2 = K.sb(top, [128, 16], F32, "s2")
        scopeO = ExitStack()
        abT = K.sb(scopeO, [128, TT], F32, "abT")
        oaT = K.sb(scopeO, [128, 8, TL], BF16, "oaT")
        scopeA = ExitStack()
        smallT = K.sb(scopeA, [128, 5, TT], BF16, "smallT")

        if only == "rwkv":
            K.dma("pool", smallT[:], small_in[:, :, :], [], [smallT])
            K.dma("sp", abT[:], ab_in[:, :], [], [abT])
        with ExitStack() as p0:
          if only != "rwkv":
                ccf = K.sb(p0, [128, 16, 2], F32)
                ccs = K.sb(p0, [128, 16, 2], F32)
                ccb = K.sb(p0, [128, 16, 2], BF16)
                bada = K.sb(p0, [128, 96], F32)
                n1w = K.sb(p0, [128, 16], F32)
                n2w = K.sb(p0, [128, 16], F32)
                K.dma("sp", ccf[:], cc_d[:, :, :], [], [ccf])
                K.dma("sp", bada[:], bada_d[:, :], [], [bada])
                K.dma("sp", n1w[:], n1w_d[:, :], [], [n1w])
                K.dma("sp", n2w[:], n2w_d[:, :], [], [n2w])
                K.act(ccs[:], ccf[:], AF.Silu, [ccf], [ccs])
                K.copy(ccb[:], ccs[:], [ccs], [ccb])
                wb = [[K.sb(p0, [128, 8, 512], BF16) for _ in range(2)] for _ in range(2)]
                pm = K.ps(p0, [128, 96, 2], F32)
                for sc in range(24):
                    w = wb[sc % 2]
                    for kh in range(2):
                        K.dma("pool", w[kh][:],
                              wada_d[kh * 1024:(kh + 1) * 1024, sc * 512:(sc + 1) * 512].rearrange("(k p) n -> p k n", p=128),
                              [], [w[kh]])
                    for jj in range(4):
                        j = sc * 4 + jj
                        for k in range(16):
                            K.mm(pm[:, j, :], w[k // 8][:, k % 8, jj * 128:(jj + 1) * 128], ccb[:, k, :], k == 0, k == 15,
                                 [w[k // 8], ccb], [pm])
                K.tt(modT[:], pm[:], bada[:].unsqueeze(2).to_broadcast([128, 96, 2]), ALU.add, [pm, bada], [modT])
                for v in range(2):
                    K.stt(s1[:, :, v], modT[:, 16:32, v], 1.0, n1w[:], ALU.add, ALU.mult, [modT, n1w], [s1])
                K.stt(s2[:], modT[:, 64:80, 0], 1.0, n2w[:], ALU.add, ALU.mult, [modT, n2w], [s2])
                if debug:
                    K.dma("sp", dbg["mod"][:, :], modT[:].rearrange("p a b -> p (a b)"), [modT], [])

        K.S.barrier()
        with ExitStack() as p1:
          if not only:
                hT = K.sb(p1, [128, 16, TT], BF16, "hT")
                mu = K.sb(p1, [128, 29], F32)
                omm = K.sb(p1, [128, 29], F32)
                cmask = K.sb(p1, [128, 8], F32)
                coef = K.sb(p1, [128, 6, 29], F32)
                convw = K.sb(p1, [128, 24, 5], F32)
                K.dma("sp", mu[:], mu_d[:, :], [], [mu])
                K.dma("sp", cmask[:], cmask_d[:, :], [], [cmask])
                K.dma("sp", convw[:], convw_d[:, :, :], [], [convw])
                K.ts(omm[:], mu[:], -1.0, 1.0, ALU.mult, ALU.add, [mu], [omm])
                for m in range(6):
                    K.ts(coef[:, m, :], mu[:], cmask[:, m:m + 1], None, ALU.mult, None, [mu, cmask], [coef])
                with ExitStack() as pa:
                    xt = [K.sb(pa, [128, D], F32) for _ in range(2)]
                    xb = [K.sb(pa, [128, D], BF16) for _ in range(2)]
                    junk = K.sb(pa, [128, D], BF16)
                    ss = [K.sb(pa, [128, 1], F32) for _ in range(2)]
                    rs = [K.sb(pa, [128, 1], F32) for _ in range(2)]
                    pt = [K.ps(pa, [128, 8, 128], BF16) for _ in range(2)]
                    npt = 0
                    for t in range(NCH):
                        X, XB, SS, RS = xt[t % 2], xb[t % 2], ss[t % 2], rs[t % 2]
                        src = ctx_d[t * 128:(t + 1) * 128, :] if t < 2 else x_d[(t - 2) * 128:(t - 1) * 128, :]
                        v = 1 if t < 2 else 0
                        K.dma("sp", X[:], src, [], [X])
                        K.act(junk[:], X[:], AF.Square, [X], [junk, SS], accum=SS[:])
                        K.act(RS[:], SS[:], AF.Sqrt, [SS], [RS], scale=1.0 / D, bias=EPS)
                        K.recip(RS[:], RS[:], [RS], [RS])
                        K.act(XB[:], X[:], AF.Copy, [X, RS], [XB], scale=RS[:])
                        for g in range(4):
                            P = pt[npt % 2]
                            npt += 1
                            for j in range(4):
                                k = g * 4 + j
                                K.tr(P[:, j, :], XB[:, k * 128:(k + 1) * 128], ident_b[:], [XB, ident_b], [P])
                            for j in range(4):
                                k = g * 4 + j
                                K.act(hT[:, k, t * 128:(t + 1) * 128], P[:, j, :], AF.Identity, [P, s1, modT], [hT],
                                      scale=s1[:, k, v:v + 1], bias=modT[:, k, v:v + 1])
                K.S.relax = RELAX[0]
                K.S.barrier()
                with ExitStack() as pb:
                    wb = [[K.sb(pb, [128, 8, 512], BF16) for _ in range(2)] for _ in range(2)]
                    stage = [K.sb(pb, [128, TT], F32) for _ in range(2)]
                    post = [K.sb(pb, [128, TT], F32) for _ in range(1)] * 2
                    postb = [K.sb(pb, [128, TL], BF16) for _ in range(1)] * 2
                    pp = [K.ps(pb, [128, 512], F32) for _ in range(4)]
                    npp = 0
                    ntile = [(0, 256)] + [(256 + i * 512, 512) for i in range(4)]
                    for sc in range(24):
                        ncol = min(512, NIN * 128 - sc * 512)
                        w = wb[sc % 2]
                        for kh in range(2):
                            K.dma("pool", w[kh][:, :, :ncol],
                                  win_d[kh * 1024:(kh + 1) * 1024, sc * 512:sc * 512 + ncol].rearrange("(k p) n -> p k n", p=128),
                                  [], [w[kh]])
                        for jj in range(ncol // 128):
                            q = sc * 4 + jj
                            lat_only = (53 <= q <= 60) or q >= 62
                            stg = stage[q % 2]
                            for (t0, tn) in ntile:
                                if lat_only and t0 == 0:
                                    continue
                                P = pp[npp % 4]
                                npp += 1
                                for k in range(16):
                                    K.mm(P[:, :tn], w[k // 8][:, k % 8, jj * 128:(jj + 1) * 128], hT[:, k, t0:t0 + tn],
                                         k == 0, k == 15, [w[k // 8], hT], [P])
                                if q <= 28 or 29 <= q <= 52 or q == 61:
                                    K.act(stg[:, t0:t0 + tn], P[:, :tn], AF.Copy, [P], [stg])
                                elif 53 <= q <= 60:
                                    K.act(stg[:, t0:t0 + tn], P[:, :tn], AF.Silu, [P], [stg])
                                else:
                                    K.act(postb[q % 2][:, t0 - 256:t0 - 256 + tn], P[:, :tn], AF.Sigmoid, [P], [postb[q % 2]])
                            if q <= 28:
                                xs = post[q % 2]
                                K.ts(xs[:], stg[:], omm[:, q:q + 1], None, ALU.mult, None, [stg, omm], [xs])
                                pl = stg[:, 256:TT].rearrange("p (r c) -> p r c", c=64)
                                xl = xs[:, 256:TT].rearrange("p (r c) -> p r c", c=64)
                                sh = [(xl[:, :, 1:64], pl[:, :, 0:63]), (xl[:, :, 0:63], pl[:, :, 1:64]),
                                      (xl[:, 1:32, :], pl[:, 0:31, :]), (xl[:, 0:31, :], pl[:, 1:32, :]),
                                      (xs[:, 1:256], stg[:, 0:255]), (xs[:, 0:255], stg[:, 1:256])]
                                for m, (o, i) in enumerate(sh):
                                    K.stt(o, i, coef[:, m, q:q + 1], o, ALU.mult, ALU.add, [stg, xs, coef], [xs])
                                if q < 24:
                                    K.dma("sp", XS[q * 128:(q + 1) * 128, :], xs[:], [xs], [XS])
                                elif q < 26:
                                    K.act(smallT[:, q - 24, :], xs[:], AF.Tanh, [xs], [smallT])
                                elif q < 28:
                                    K.act(smallT[:, q - 24, :], xs[:], AF.Copy, [xs], [smallT])
                                else:
                                    K.act(smallT[:, 4, :], xs[:], AF.Sigmoid, [xs], [smallT])
                            elif q <= 52:
                                g = q - 29
                                acc = post[q % 2]
                                K.ts(acc[:], stg[:], convw[:, g, 2:3], None, ALU.mult, None, [stg, convw], [acc])
                                for (a, b) in ((0, 256), (256, TT)):
                                    for j, o in ((0, 2), (1, 1), (3, -1), (4, -2)):
                                        if o > 0:
                                            ov, iv = acc[:, a + o:b], stg[:, a:b - o]
                                        else:
                                            ov, iv = acc[:, a:b + o], stg[:, a - o:b]
                                        K.stt(ov, iv, convw[:, g, j:j + 1], ov, ALU.mult, ALU.add, [stg, acc, convw], [acc])
                                K.act(acc[:], acc[:], AF.Silu, [acc], [acc])
                                K.dma("sp", US[g * 128:(g + 1) * 128, :], acc[:], [acc], [US])
                            elif q <= 60:
                                K.dma("sp", SZ[(q - 53) * 128:(q - 52) * 128, :], stg[:, 256:TT], [stg], [SZ])
                            elif q == 61:
                                K.copy(abT[:], stg[:], [stg], [abT], eng="pool")
                            else:
                                K.dma("sp", SG[(q - 62) * 128:(q - 61) * 128, :], postb[q % 2][:], [postb[q % 2]], [SG])
                K.S.barrier()
                if debug:
                    smf = K.sb(p1, [128, 5, TT], F32)
                    K.copy(smf[:], smallT[:], [smallT], [smf])
                    K.dma("sp", dbg["sm"][:, :, :], smf[:], [smf], [])

        K.S.relax = RELAX[3]
        K.S.barrier()
        with ExitStack() as p2:
          if only != "ffn":
            rwkv_phase(K, p2, dict(XS=XS, smallT=smallT, oaT=oaT, ident_b=ident_b, ident_f=ident_f,
                                   w0=w0_d, a0=a0_d, kkw=kkw_d, ka=ka_d, rk=rk_d, w2=w2_d, a2=a2_d, g2=g2_d,
                                   lnw=lnw_d, lnb=lnb_d, m1=m1_d, m2=m2_d, rmask=rmask_d, bones=bones_d, hsel=hsel_d, nmask=nmask_d,
                                   dbg=dbg, nhp=NHP))
        K.S.barrier()
        if debug:
            with ExitStack() as pd:
                of = K.sb(pd, [128, 8, TL], F32)
                K.copy(of[:, 0:NHP], oaT[:, 0:NHP], [oaT], [of])
                K.dma("sp", dbg["oa"][:, 0:NHP, :], of[:, 0:NHP], [of], [])
            K.S.barrier()
        scopeA.close()
        obT = K.sb(scopeO, [128, 8, TL], BF16, "obT")
        K.S.relax = RELAX[4]
        with ExitStack() as p2b:
          if only != "ffn":
            gdn_phase(K, p2b, dict(US=US, SZ=SZ, abT=abT, obT=obT, ident_b=ident_b, ident_f=ident_f, selg=selg_d, selb=selb_d,
                                   bigm=bigm_d, offd=offd_d, ones=ones_d, rmask=rmask_d, alog=alog_d, dtb=dtb_d, gnw=gnw_d,
                                   nmask=nmask_d, dbg=dbg, nh=NGH, GBS=GBS))
        K.S.barrier()
        if debug:
            with ExitStack() as pd:
                of = K.sb(pd, [128, 8, TL], F32)
                K.copy(of[:, 0:NGH], obT[:, 0:NGH], [obT], [of])
                K.dma("sp", dbg["ob"][:, 0:NGH, :], of[:, 0:NGH], [of], [])
            K.S.barrier()
        C34 = dict(oaT=oaT, obT=obT, ident_b=ident_b, ident_f=ident_f, modT=modT, s2=s2, SG=SG, X1=X1,
                   x=x_d, out=out_d, MODS=MODS, fnw=fnw_d, p_a=pa_d, p_b=pb_d, w_out=wout_d, w_gu=wgu_d,
                   w_dn=wdn_d, dbg=dbg)
        K.S.relax = RELAX[1]
        if only != "rwkv":
            write_mods(K, C34)
        if not only:
            merge_phase(K, scopeO, C34)
        scopeO.close()
        K.S.relax = RELAX[2]
        K.S.barrier()
        if only != "rwkv":
            ffn_phase(K, top, C34)
        else:
            with ExitStack() as pz:
                z = K.sb(pz, [128, D], F32)
                K.memset(z[:], 0.0, [z])
                K.dma("sp", out_d[0:128, :], z[:], [z], [out_d])
        K.S.emit(nc, top)
    nc._marks = getattr(K.S, "marks", [])
    return nc


def _fm(v, nchunk):
    return np.ascontiguousarray(np.asarray(v, np.float32).reshape(nchunk, 128).T)


def _pad_cols(a, n):
    out = np.zeros(a.shape[:-1] + (n,), np.float32)
    out[..., :a.shape[-1]] = a
    return out


def prep_shared(inputs):
    w_in = np.asarray(inputs["w_in"][0], np.float32)
    RW = 3520
    segs = [w_in[:, 0:3072]]
    for (a, b) in ((3072, 3168), (3168, 3264), (3264, 3360), (3360, 3456), (3456, 3520)):
        segs.append(_pad_cols(w_in[:, a:b], 128))
    segs.append(w_in[:, RW:RW + 3072 + 1024])
    abc = np.zeros((D, 128), np.float32)
    abc[:, 0:16] = w_in[:, 7616:7632]
    abc[:, 32:48] = w_in[:, 7632:7648]
    segs.append(abc)
    segs.append(w_in[:, 7648:])
    win = np.ascontiguousarray(np.concatenate(segs, axis=1))
    assert win.shape == (D, NIN * 128)
    mu = np.asarray(inputs["rw_mu"][0], np.float32)
    mus = [mu[0:3072]]
    for (a, b) in ((3072, 3168), (3168, 3264), (3264, 3360), (3360, 3456), (3456, 3520)):
        mus.append(_pad_cols(mu[a:b], 128))
    mu_fm = _fm(np.concatenate(mus), 29)
    p = np.arange(128)
    cmask = np.zeros((128, 8), np.float32)
    for m in range(4):
        cmask[:, m] = (p % 4 == m)
    cmask[:, 4] = (p % 2 == 0)
    cmask[:, 5] = (p % 2 == 1)
    convw = np.asarray(inputs["gdn_conv_w"][0], np.float32)
    convw_fm = np.ascontiguousarray(convw.reshape(5, 24, 128).transpose(2, 1, 0))
    sh = {
        "w_ada": np.ascontiguousarray(inputs["w_ada"][0], np.float32),
        "b_ada": _fm(inputs["b_ada"][0], 96),
        "norm1_w": _fm(inputs["norm1_w"][0], 16),
        "norm2_w": _fm(inputs["norm2_w"][0], 16),
        "final_norm_w": np.ascontiguousarray(np.asarray(inputs["final_norm_w"], np.float32).reshape(1, D)),
        "w_in": win,
        "rw_mu": mu_fm,
        "cmask": cmask,
        "gdn_conv_w": convw_fm,
        "ident": np.eye(128, dtype=np.float32),
    }
    g = lambda k: np.asarray(inputs[k][0], np.float32)
    sh["rw_w0"] = np.ascontiguousarray(g("rw_w0").reshape(2, 8, 128).transpose(2, 0, 1))
    sh["rw_a0"] = np.ascontiguousarray(g("rw_a0").reshape(2, 8, 128).transpose(2, 0, 1))
    sh["rw_k_k"] = _fm(g("rw_k_k"), 8)
    sh["rw_k_a"] = _fm(g("rw_k_a"), 8)
    sh["rw_r_k"] = _fm(g("rw_r_k").reshape(-1), 8)
    sh["rw_w2"] = np.ascontiguousarray(g("rw_w2"))
    sh["rw_a2"] = np.ascontiguousarray(g("rw_a2"))
    sh["rw_g2"] = np.ascontiguousarray(g("rw_g2"))
    sh["rw_ln_w"] = np.ascontiguousarray(g("rw_ln_w").reshape(1, 1024))
    sh["rw_ln_b"] = np.ascontiguousarray(g("rw_ln_b").reshape(1, 1024))
    r_ = np.arange(128)[:, None]; c_ = np.arange(128)[None, :]
    SL = (c_ < r_).astype(np.float32); SU = (c_ > r_).astype(np.float32)
    IL = (c_ <= r_).astype(np.float32); IU = (c_ >= r_).astype(np.float32)
    m1 = np.stack([np.stack([SL, SU, SL, SU], 0), np.stack([SU, SL, SU, SL], 0)], 0)
    m2 = np.stack([np.stack([SU, IU, -IU], 0), np.stack([SL, IL, -IL], 0)], 0)
    sh["m1"] = np.ascontiguousarray(m1.transpose(2, 0, 1, 3))
    sh["m2"] = np.ascontiguousarray(m2.transpose(2, 0, 1, 3))
    rmask = np.ones((128, 512), np.float32); rmask[:, ::128] = 0.0
    sh["rmask"] = rmask
    bo = np.zeros((128, 128), np.float32); bo[:64, :64] = 1.0; bo[64:, 64:] = 1.0
    sh["blockones"] = bo
    hs = np.zeros((128, 2), np.float32); hs[:64, 0] = 1.0; hs[64:, 1] = 1.0
    sh["headsel"] = hs
    nmk = np.zeros((2, 7, 128, 128), np.float32)
    for lv in range(7):
        bsz = 1 << lv
        low = ((r_ // (2 * bsz) == c_ // (2 * bsz)) & ((r_ // bsz) % 2 == 1) & ((c_ // bsz) % 2 == 0)).astype(np.float32)
        nmk[0, lv] = -low
        nmk[1, lv] = -low.T
    sh["nmask"] = np.ascontiguousarray(nmk.transpose(2, 0, 1, 3))
    selg = np.zeros((64, 16, 128), np.float32); selb = np.zeros((64, 16, 128), np.float32)
    for r0 in range(16):
        selg[r0, r0, :] = 1.0
        selb[32 + r0, r0, :] = 1.0
    sh["selg"] = selg; sh["selb"] = selb
    BIG = 1.0e4
    bigm = np.stack([np.stack([BIG * SU, -BIG * SL], 0), np.stack([BIG * SL, -BIG * SU], 0)], 0)
    sh["bigm"] = np.ascontiguousarray(bigm.transpose(2, 0, 1, 3))
    sh["offd"] = (1.0 - np.eye(128)).astype(np.float32)
    sh["ones"] = np.ones((128, 128), np.float32)
    al = np.zeros((64, 1), np.float32); al[0:16, 0] = g("gdn_a_log").reshape(-1)
    db = np.zeros((64, 1), np.float32); db[0:16, 0] = g("gdn_dt_bias").reshape(-1)
    sh["gdn_a_log"] = al; sh["gdn_dt_bias"] = db
    sh["gdn_norm_w"] = np.ascontiguousarray(g("gdn_norm_w").reshape(1, 128))
    for k_ in ("merge_p_a", "merge_p_b", "w_out", "ffn_w_gate_up", "ffn_w_down"):
        sh[k_] = np.ascontiguousarray(g(k_))
    return sh


def make_in_maps(inputs):
    sh = prep_shared(inputs)
    maps = []
    for b in range(8):
        m = dict(sh)
        m["x"] = np.ascontiguousarray(inputs["x"][b], np.float32)
        m["ctx"] = np.ascontiguousarray(inputs["ctx"][b], np.float32)
        cc = np.stack([np.asarray(inputs["c"][b], np.float32), np.asarray(inputs["c_ctx"], np.float32)], axis=-1)
        m["cc"] = np.ascontiguousarray(cc.reshape(16, 128, 2).transpose(1, 0, 2))
        maps.append(m)
    return maps


_NC = None


def kernel(**inputs):
    global _NC
    if _NC is None:
        _NC = build()
    maps = make_in_maps(inputs)
    res = run_bass_kernel_spmd(_NC, maps, core_ids=list(range(8)))
    return np.stack([r["out"] for r in res.results], axis=0).astype(np.float32)
```

```python
import numpy as np
from contextlib import ExitStack
import concourse.bass as bass
import concourse.mybir as mybir
from concourse.bass_utils import run_bass_kernel_spmd

F32 = mybir.dt.float32
BF16 = mybir.dt.bfloat16
AF = mybir.ActivationFunctionType
ALU = mybir.AluOpType
AX = mybir.AxisListType

COMPUTE = ("pe", "act", "dve", "pool")
NDSEM = 24

D = 2048
TC = 256
TL = 2048
TT = TC + TL
NCH = TT // 128
NIN = 94
FFN = 5632
EPS = 1e-6
DEC = 0.6065306597126334


class Buf:
    __slots__ = ("name", "lw", "rd")

    def __init__(self, name=""):
        self.name = name
        self.lw = None
        self.rd = {}


class Sched:
    def __init__(self):
        self.ops = []
        self.last = {}
        self.dmas = []
        self.bar = set()
        self.bar_seen = set()

    def pe_strict(self, on):
        if on:
            self._saved_relax = getattr(self, "relax", False)
            self.relax = False
        else:
            self.relax = self._saved_relax
            if self.relax and "pe" in self.last:
                self.pe_fence = self.last["pe"]

    def barrier(self):
        if not hasattr(self, "marks"):
            self.marks = []
        self.marks.append({e: sum(1 for o in self.ops if o[0] == e and not o[3]) for e in COMPUTE})
        self.bar = set(self.last.values()) | set(self.dmas)
        self.dmas = []
        self.bar_seen = set()

    def add(self, eng, fn, reads=(), writes=(), dma=False):
        i = len(self.ops)
        deps = set()
        if eng not in self.bar_seen:
            deps |= self.bar
            self.bar_seen.add(eng)
        self.last[eng] = i
        if dma:
            self.dmas.append(i)
        for b in reads:
            if b.lw is not None:
                deps.add(b.lw)
        for b in writes:
            if b.lw is not None:
                deps.add(b.lw)
            deps.update(b.rd.values())
        key = ("d", i) if dma else eng
        for b in reads:
            b.rd[key] = i
        for b in writes:
            b.lw = i
            b.rd = {}
        if eng == "pe" and getattr(self, "relax", False):
            deps = set(d for d in deps if not (self.ops[d][0] == "pe" and not self.ops[d][3]))
            if getattr(self, "pe_fence", None) is not None:
                deps.add(self.pe_fence)
                self.pe_fence = None
        self.ops.append((eng, fn, deps, dma))
        return i

    def emit(self, nc, stack):
        ops = self.ops
        engs = {"pe": nc.tensor, "act": nc.scalar, "dve": nc.vector, "pool": nc.gpsimd, "sp": nc.sync}
        names = list(engs)
        csem = {e: stack.enter_context(nc.semaphore("c_" + e)) for e in COMPUTE}
        dsem = {e: [stack.enter_context(nc.semaphore("d_%s%d" % (e, k))) for k in range(NDSEM)]
                for e in ("sp", "act", "pool")}
        comp = [None] * len(ops)
        cnt = {e: 0 for e in COMPUTE}
        dcnt = {e: 0 for e in dsem}
        prevslot = [None] * len(ops)
        for i, (eng, fn, deps, dma) in enumerate(ops):
            if dma:
                j = dcnt[eng]
                dcnt[eng] += 1
                comp[i] = (dsem[eng][j % NDSEM], 16 * (j // NDSEM + 1))
                if j >= NDSEM:
                    prevslot[i] = (dsem[eng][j % NDSEM], 16 * (j // NDSEM))
            else:
                cnt[eng] += 1
                comp[i] = (csem[eng], cnt[eng])
        per = {e: [] for e in names}
        for i, op in enumerate(ops):
            per[op[0]].append(i)
        block = stack.enter_context(nc.Block())

        def run(ename):
            def body(e):
                known = {}
                for i in per[ename]:
                    eng, fn, deps, dma = ops[i]
                    need = {}
                    cands = [comp[d] for d in deps]
                    if prevslot[i] is not None:
                        cands.append(prevslot[i])
                    for sm, v in cands:
                        k = id(sm)
                        if known.get(k, 0) >= v:
                            continue
                        if k not in need or need[k][1] < v:
                            need[k] = (sm, v)
                    for k, (sm, v) in need.items():
                        e.wait_ge(sm, v)
                        known[k] = v
                    ins = fn(e)
                    sm, v = comp[i]
                    ins.then_inc(sm, 16 if dma else 1)
                if ename in dsem:
                    last = {}
                    for i in per[ename]:
                        if ops[i][3]:
                            sm, v = comp[i]
                            last[id(sm)] = (sm, v)
                    for sm, v in last.values():
                        e.wait_ge(sm, v)
            return body

        block.tensor(run("pe"))
        block.scalar(run("act"))
        block.vector(run("dve"))
        block.gpsimd(run("pool"))
        block.sync(run("sp"))


class T:
    def __init__(self, h, name):
        self.h = h
        self.b = Buf(name)

    def __getitem__(self, k):
        return self.h[k]


def _bufs(xs):
    return [x if isinstance(x, Buf) else x.b for x in xs]


class KB:
    def __init__(self, nc):
        self.nc = nc
        self.S = Sched()
        self.n = 0

    def sb(self, st, shape, dt, name=None):
        self.n += 1
        name = name or "t%d" % self.n
        if not hasattr(self, "used"):
            self.used = set()
        while name in self.used:
            name = name + "_"
        self.used.add(name)
        return T(st.enter_context(self.nc.sbuf_tensor(name, list(shape), dt)), name)

    def ps(self, st, shape, dt, name=None):
        self.n += 1
        name = name or "p%d" % self.n
        return T(st.enter_context(self.nc.psum_tensor(name, list(shape), dt)), name)

    def dram(self, name, shape, dt, kind="Internal"):
        h = self.nc.dram_tensor(name, list(shape), dt, kind=kind)
        t = T(h.ap(), name)
        return t

    def act(self, out, in_, func, r, w, scale=1.0, bias=0.0, accum=None):
        kw = {}
        if accum is not None:
            kw["accum_out"] = accum
        self.S.add("act", lambda e: e.activation(out=out, in_=in_, func=func, scale=scale, bias=bias, **kw),
                   _bufs(r), _bufs(w))

    def tt(self, out, in0, in1, op, r, w, eng="dve"):
        self.S.add(eng, lambda e: e.tensor_tensor(out=out, in0=in0, in1=in1, op=op), _bufs(r), _bufs(w))

    def ts(self, out, in0, s1, s2, op0, op1, r, w, eng="dve", accum=None):
        kw = {}
        if accum is not None:
            kw["accum_out"] = accum
        if op1 is None:
            self.S.add(eng, lambda e: e.tensor_scalar(out=out, in0=in0, scalar1=s1, scalar2=None, op0=op0, **kw),
                       _bufs(r), _bufs(w))
        else:
            self.S.add(eng, lambda e: e.tensor_scalar(out=out, in0=in0, scalar1=s1, scalar2=s2, op0=op0, op1=op1, **kw),
                       _bufs(r), _bufs(w))

    def stt(self, out, in0, scalar, in1, op0, op1, r, w):
        self.S.add("dve", lambda e: e.scalar_tensor_tensor(out=out, in0=in0, scalar=scalar, in1=in1, op0=op0, op1=op1),
                   _bufs(r), _bufs(w))

    def copy(self, out, in_, r, w, eng="dve"):
        self.S.add(eng, lambda e: e.tensor_copy(out=out, in_=in_), _bufs(r), _bufs(w))

    def memset(self, out, val, w, eng="pool"):
        self.S.add(eng, lambda e: e.memset(out, val), [], _bufs(w))

    def recip(self, out, in_, r, w):
        self.S.add("dve", lambda e: e.reciprocal(out=out, in_=in_), _bufs(r), _bufs(w))

    def scan(self, out, d0, d1, init, op0, op1, r, w):
        self.S.add("dve", lambda e: e.tensor_tensor_scan(out=out, data0=d0, data1=d1, initial=init, op0=op0, op1=op1),
                   _bufs(r), _bufs(w))

    def mm(self, out, lhsT, rhs, start, stop, r, w):
        self.S.add("pe", lambda e: e.matmul(out, lhsT=lhsT, rhs=rhs, start=start, stop=stop), _bufs(r), _bufs(w))

    def tr(self, out, in_, ident, r, w):
        self.S.add("pe", lambda e: e.transpose(out=out, in_=in_, identity=ident), _bufs(r), _bufs(w))

    def dma(self, q, out, in_, r, w, **kw):
        self.S.add(q, lambda e: e.dma_start(out=out, in_=in_, **kw), _bufs(r), _bufs(w), dma=True)


def inverse_workspace(K, st, C):
    W = {}
    W["nmask"] = K.sb(st, [128, 2, 7, 128], BF16, "nmask_sb")
    K.dma("pool", W["nmask"][:], C["nmask"][:, :, :, :], [], [W["nmask"]])
    W["ident_b"] = C["ident_b"]
    W["sets"] = []
    for g in range(2):
        W["sets"].append({nm: K.sb(st, [128, 4, 128], BF16, "iw%d_%s" % (g, nm))
                          for nm in ("Xa", "Xb", "Ya", "Yb", "LsX", "LsY", "LsX2", "LsY2", "M1", "M2", "Mt", "R")})
    W["PI"] = [K.ps(st, [128, 4, 128], F32) for _ in range(2)]
    W["cnt"] = 0
    return W


def inverse_units(K, C, LL, n, d, XTb, W, nunits=None):
    LLf = LL[:].rearrange("p j f t -> p (j f) t")
    nm = W["nmask"]
    nunits = 2 * n if nunits is None else nunits
    gs = min(4, nunits)
    idb = W["ident_b"][:].unsqueeze(1).to_broadcast([128, gs, 128])
    bc = lambda m, lv: nm[:, m, lv, :].unsqueeze(1).to_broadcast([128, gs, 128])

    def pi():
        W["cnt"] += 1
        return W["PI"][W["cnt"] % 2]
    mx, my = (0, 1) if d == 0 else (1, 0)

    class V_:
        def __init__(s_, t):
            s_.t = t
            s_.b = t.b

        def __getitem__(s_, k):
            if k == slice(None):
                return s_.t[:, 0:gs, :]
            return s_.t[k]
    groups = []
    for gi, g0 in enumerate(range(0, nunits, gs)):
        S_ = W["sets"][gi % 2]
        st_ = {k_: V_(v_) for k_, v_ in S_.items()}
        st_["g0"] = g0
        st_["Lv"] = LLf[:, 2 * g0:2 * g0 + 2 * gs:2, :]
        st_["LTv"] = LLf[:, 2 * g0 + 1:2 * g0 + 2 * gs:2, :]
        st_["X"], st_["Xn"], st_["Y"], st_["Yn"] = st_["Xa"], st_["Xb"], st_["Ya"], st_["Yb"]
        groups.append(st_)
    for G in groups:
        K.tt(G["LsX"][:], G["Lv"], bc(mx, 0), ALU.mult, [LL, nm], [G["LsX"]], eng="pool")
        K.tt(G["X"][:], G["LsX"][:], idb, ALU.add, [G["LsX"], W["ident_b"]], [G["X"]], eng="pool")
        K.tt(G["LsY"][:], G["LTv"], bc(my, 0), ALU.mult, [LL, nm], [G["LsY"]], eng="pool")
        K.tt(G["Y"][:], G["LsY"][:], idb, ALU.add, [G["LsY"], W["ident_b"]], [G["Y"]], eng="pool")
        K.tt(G["Mt"][:], G["Lv"], idb, ALU.add, [LL, W["ident_b"]], [G["Mt"]], eng="pool")
    yield
    for lv in range(1, 7):
        sx, sy = ("LsX2", "LsY2") if lv % 2 else ("LsX", "LsY")
        for G in groups:
            K.tt(G[sx][:], G["Lv"], bc(mx, lv), ALU.mult, [LL, nm], [G[sx]], eng="pool")
            K.tt(G[sy][:], G["LTv"], bc(my, lv), ALU.mult, [LL, nm], [G[sy]], eng="pool")
        yield
        for G in groups:
            X, Y, LsX, LsY, M1, M2 = G["X"], G["Y"], G[sx], G[sy], G["M1"], G["M2"]
            Q = pi()
            for u in range(gs):
                K.mm(Q[:, u, :], LsY[:, u, :], X[:, u, :], True, True, [LsY, X], [Q])
            K.act(M1[:], Q[:, 0:gs, :], AF.Copy, [Q], [M1])
            Q = pi()
            for u in range(gs):
                K.mm(Q[:, u, :], LsX[:, u, :], Y[:, u, :], True, True, [LsX, Y], [Q])
            K.act(M2[:], Q[:, 0:gs, :], AF.Copy, [Q], [M2])
            yield
        for G in groups:
            X, Y, Xn, Yn, M1, M2 = G["X"], G["Y"], G["Xn"], G["Yn"], G["M1"], G["M2"]
            Q = pi()
            for u in range(gs):
                K.mm(Q[:, u, :], Y[:, u, :], M1[:, u, :], True, True, [Y, M1], [Q])
            K.tt(Xn[:], X[:], Q[:, 0:gs, :], ALU.add, [X, Q], [Xn])
            Q = pi()
            for u in range(gs):
                K.mm(Q[:, u, :], X[:, u, :], M2[:, u, :], True, True, [X, M2], [Q])
            K.tt(Yn[:], Y[:], Q[:, 0:gs, :], ALU.add, [Y, Q], [Yn])
            G["X"], G["Xn"], G["Y"], G["Yn"] = Xn, X, Yn, Y
            yield
    for G in groups:
        Q = pi()
        for u in range(gs):
            K.mm(Q[:, u, :], G["Mt"][:, u, :], G["Y"][:, u, :], True, True, [G["Mt"], G["Y"]], [Q])
        K.stt(G["R"][:], Q[:, 0:gs, :], -1.0, idb, ALU.mult, ALU.add, [Q, W["ident_b"]], [G["R"]])
    for G in groups:
        Q = pi()
        for u in range(gs):
            K.mm(Q[:, u, :], G["X"][:, u, :], G["R"][:, u, :], True, True, [G["X"], G["R"]], [Q])
        K.tt(XTb[:, G["g0"]:G["g0"] + gs, :], G["Y"][:], Q[:, 0:gs, :], ALU.add, [G["Y"], Q], [XTb])
    yield


SEGS = [(0, 2)] + [(2 + 4 * i, 4) for i in range(4)]


def rwkv_phase(K, st, C):
    XS, smallT, oaT, ident_b, ident_f = C["XS"], C["smallT"], C["oaT"], C["ident_b"], C["ident_f"]
    dbg = C["dbg"]
    sb = lambda shape, dt, name=None: K.sb(st, shape, dt, name)
    w0 = sb([128, 2, 8], F32); a0 = sb([128, 2, 8], F32)
    kkw = sb([128, 8], F32); ka = sb([128, 8], F32); omka = sb([128, 8], F32); rk = sb([128, 8], F32)
    for t_, d_ in ((w0, C["w0"]), (a0, C["a0"])):
        K.dma("sp", t_[:], d_[:, :, :], [], [t_])
    for t_, d_ in ((kkw, C["kkw"]), (ka, C["ka"]), (rk, C["rk"])):
        K.dma("sp", t_[:], d_[:, :], [], [t_])
    K.ts(omka[:], ka[:], -1.0, 1.0, ALU.mult, ALU.add, [ka], [omka])
    w2b = sb([128, 2, 1024], BF16); a2b = sb([128, 2, 1024], BF16); g2b = sb([64, 1024], BF16)
    K.memset(w2b[:], 0.0, [w2b])
    K.memset(a2b[:], 0.0, [a2b])
    K.dma("pool", w2b[0:96, :, :], C["w2"][:, :, :].rearrange("d r c -> r d c"), [], [w2b])
    K.dma("pool", a2b[0:96, :, :], C["a2"][:, :, :].rearrange("d r c -> r d c"), [], [a2b])
    K.dma("pool", g2b[:], C["g2"][:, :], [], [g2b])
    lnw = sb([128, 128], F32); lnb = sb([128, 128], F32)
    m1f = sb([128, 2, 4, 128], BF16); m2f = sb([128, 2, 3, 128], BF16)
    K.dma("pool", m1f[:], C["m1"][:, :, :, :], [], [m1f])
    K.dma("pool", m2f[:], C["m2"][:, :, :, :], [], [m2f])
    rmask = sb([128, 512], F32); bones = sb([128, 128], F32); hsel = sb([128, 2], F32)
    K.dma("sp", rmask[:], C["rmask"][:, :], [], [rmask])
    K.dma("sp", bones[:], C["bones"][:, :], [], [bones])
    K.dma("sp", hsel[:], C["hsel"][:, :], [], [hsel])
    f32t = lambda nm=None: sb([128, 512], F32, nm)
    bft = lambda nm=None: sb([128, 512], BF16, nm)
    Xr, Xk, Xv = f32t("Xr"), f32t("Xk"), f32t("Xv")
    sig, A, B, Cc, Dd = f32t("sig"), f32t("A"), f32t("B"), f32t("Cc"), f32t("Dd")
    e1, e2, e3, e4 = f32t("e1"), f32t("e2"), f32t("e3"), f32t("e4")
    icl, icl0, kq, sq, rn, kd, bd, tmp = (f32t(nm) for nm in ("icl", "icl0", "kq", "sq", "rn", "kd", "bd", "tmp"))
    kkt = kq
    gam = sb([128, 4], F32, "gam")
    rt, at, kt, bt, KH, BH, vb = (bft(nm) for nm in ("rt", "at", "kt", "bt", "KH", "BH", "vb"))
    KHt = sb([128, 4, 128], BF16, "KHt"); BHnt = sb([128, 4, 128], BF16, "BHnt"); Vt = sb([128, 4, 128], BF16, "Vt")
    LL = sb([128, 4, 4, 128], BF16, "LL")
    AA = sb([128, 4, 2, 3, 128], BF16, "AA")
    XTb = sb([128, 8, 128], BF16, "XTb")
    IW = inverse_workspace(K, st, C)
    Hf = sb([128, 128], F32, "Hf"); Hb = sb([128, 128], BF16, "Hb")
    P1s = sb([128, 128], BF16, "P1s"); Us = sb([128, 128], BF16, "Us")
    ybuf = sb([128, 16, 128], BF16, "ybuf")
    ytot = sb([128, 4, 128], F32, "ytot"); yc = sb([128, 4, 128], F32, "yc"); ysq = sb([128, 4, 128], F32)
    mean = sb([128, 8], F32); var = sb([128, 8], F32)
    bsum = sb([128, 4, 2], F32)
    oat = sb([128, 4, 128], BF16)
    PF = [K.ps(st, [128, 512], F32) for _ in range(1)]
    PTr = K.ps(st, [128, 8, 128], BF16)
    PG = [K.ps(st, [128, 4, 128], F32) for _ in range(2)]
    PSq = K.ps(st, [128, 512], F32)
    PSh = K.ps(st, [128, 512], F32)
    PS_P1, PS_U, PS_Y, PS_H = PSq, PSq, PSq, PSh
    cnt = {"pf": 0, "pg": 0, "pi": 0, "tr": 0}

    def nxt(lst, key):
        cnt[key] += 1
        return lst[cnt[key] % len(lst)]

    def transp(src, dst, n, scale=None):
        half = cnt["tr"] % 2
        cnt["tr"] += 1
        for j in range(n):
            K.tr(PTr[:, half * 4 + j, :], src[:, j * 128:(j + 1) * 128], ident_b[:], [src, ident_b], [PTr])
        if scale is None:
            K.copy(dst[:, :n, :], PTr[:, half * 4:half * 4 + n, :], [PTr], [dst])
        else:
            K.act(dst[:, :n, :], PTr[:, half * 4:half * 4 + n, :], AF.Copy, [PTr], [dst], scale=scale)

    Xs = [(Xr, Xk, Xv), (Xr, Xk, Xv)]
    rtP = [rt, bft("rt1")]; atP = [at, bft("at1")]; ktP = [kt, bft("kt1")]; btP = [bt, bft("bt1")]
    KHtP = [KHt, sb([128, 4, 128], BF16, "KHt1")]; BHntP = [BHnt, sb([128, 4, 128], BF16, "BHnt1")]
    VtP = [Vt, sb([128, 4, 128], BF16, "Vt1")]
    gamP = [gam, sb([128, 4], F32, "gam1")]
    AAP = [AA, sb([128, 4, 2, 3, 128], BF16, "AA1")]
    XTbP = [XTb, sb([128, 8, 128], BF16, "XTb1")]
    bsumP = [bsum, sb([128, 4, 2], F32, "bsum1")]
    items = []
    for hp in range(C["nhp"]):
        for d in range(2):
            order = SEGS if d == 0 else [SEGS[0], SEGS[4], SEGS[3], SEGS[2], SEGS[1]]
            for si, (c0, n) in enumerate(order):
                items.append((hp, d, c0, n, si == 0))

    def loads(i):
        hp, d, c0, n, first = items[i]
        t0, N = c0 * 128, n * 128
        for X_, row in zip(Xs[i % 2], (0, 1024, 2048)):
            K.dma("sp", X_[:, :N], XS[row + hp * 128:row + hp * 128 + 128, t0:t0 + N], [XS], [X_])

    def stepA(i):
        hp, d, c0, n, first = items[i]
        p = i % 2
        hc = slice(hp * 128, (hp + 1) * 128)
        t0, N = c0 * 128, n * 128
        latent = c0 >= 2
        tk = slice(t0, t0 + N)
        Xr, Xk, Xv = Xs[p]
        rt, at, kt, bt, KHt, BHnt, Vt, gam, bsum = rtP[p], atP[p], ktP[p], btP[p], KHtP[p], BHntP[p], VtP[p], gamP[p], bsumP[p]
        P = nxt(PF, "pf")
        K.mm(P[:, :N], w2b[:, d, hc], smallT[:, d, tk], True, True, [w2b, smallT], [P])
        K.act(sig[:, :N], P[:, :N], AF.Sigmoid, [P, w0], [sig], bias=w0[:, d, hp:hp + 1])
        K.scan(A[:, :N], rmask[:, :N], sig[:, :N], 0.0, ALU.mult, ALU.add, [rmask, sig], [A])
        P = nxt(PF, "pf")
        K.mm(P[:, :N], a2b[:, d, hc], smallT[:, 2 + d, tk], True, True, [a2b, smallT], [P])
        K.act(icl[:, :N], P[:, :N], AF.Sigmoid, [P, a0], [icl], bias=a0[:, d, hp:hp + 1])
        if d == 1 and latent:
            P = nxt(PF, "pf")
            K.mm(P[:, :N], a2b[:, 0, hc], smallT[:, 2, tk], True, True, [a2b, smallT], [P])
            K.act(icl0[:, :N], P[:, :N], AF.Sigmoid, [P, a0], [icl0], bias=a0[:, 0, hp:hp + 1])
        yield
        K.tt(B[:, :N], A[:, :N], sig[:, :N], ALU.subtract, [A, sig], [B])
        v3 = lambda t_: t_[:, :N].rearrange("p (c t) -> p c t", t=128)
        tot = v3(A)[:, :, 127:128]
        K.tt(v3(Cc), tot.to_broadcast([128, n, 128]), v3(A), ALU.subtract, [A], [Cc])
        K.tt(Dd[:, :N], Cc[:, :N], sig[:, :N], ALU.add, [Cc, sig], [Dd], eng="pool")
        yield
        Gi, Gx, Gt = (A, B, Cc) if d == 0 else (Dd, Cc, B)
        K.act(e1[:, :N], Gi[:, :N], AF.Exp, [Gi], [e1], scale=-DEC)
        K.act(e2[:, :N], Gx[:, :N], AF.Exp, [Gx], [e2], scale=-DEC)
        yield
        K.act(e3[:, :N], Gi[:, :N], AF.Exp, [Gi], [e3], scale=DEC)
        K.act(e4[:, :N], Gt[:, :N], AF.Exp, [Gt], [e4], scale=-DEC)
        K.act(gam[:, :n], v3(A)[:, :, 127], AF.Exp, [A], [gam], scale=-DEC)
        yield
        K.act(kq[:, :N], Xk[:, :N], AF.Copy, [Xk, kkw], [kq], scale=kkw[:, hp:hp + 1])
        K.act(sq[:, :N], kq[:, :N], AF.Square, [kq], [sq])
        P = nxt(PF, "pf")
        K.mm(P[:, :N], bones[:], sq[:, :N], True, True, [bones, sq], [P])
        K.act(rn[:, :N], P[:, :N], AF.Sqrt, [P], [rn], bias=EPS)
        K.recip(rn[:, :N], rn[:, :N], [rn], [rn])
        yield
        K.tt(kkt[:, :N], kq[:, :N], rn[:, :N], ALU.mult, [kq, rn], [kkt])
        K.ts(tmp[:, :N], icl[:, :N], ka[:, hp:hp + 1], omka[:, hp:hp + 1], ALU.mult, ALU.add, [icl, ka, omka], [tmp])
        K.tt(kd[:, :N], tmp[:, :N], Xk[:, :N], ALU.mult, [tmp, Xk], [kd])
        yield
        K.tt(bd[:, :N], kkt[:, :N], icl[:, :N], ALU.mult, [kkt, icl], [bd], eng="pool")
        K.tt(rt[:, :N], Xr[:, :N], e1[:, :N], ALU.mult, [Xr, e1], [rt])
        K.tt(at[:, :N], kkt[:, :N], e2[:, :N], ALU.mult, [kkt, e2], [at], eng="pool")
        yield
        K.tt(kt[:, :N], kd[:, :N], e3[:, :N], ALU.mult, [kd, e3], [kt])
        K.tt(bt[:, :N], bd[:, :N], e3[:, :N], ALU.mult, [bd, e3], [bt], eng="pool")
        K.tt(KH[:, :N], kd[:, :N], e4[:, :N], ALU.mult, [kd, e4], [KH])
        yield
        K.tt(BH[:, :N], bd[:, :N], e4[:, :N], ALU.mult, [bd, e4], [BH], eng="pool")
        K.act(vb[:, :N], Xv[:, :N], AF.Copy, [Xv], [vb])
        transp(KH, KHt, n)
        yield
        transp(BH, BHnt, n, scale=-1.0)
        transp(vb, Vt, n)
        yield
        if d == 1 and latent:
            K.tt(tmp[:, :N], icl[:, :N], icl0[:, :N], ALU.add, [icl, icl0], [tmp])
            yield
            K.ts(tmp[:, :N], tmp[:, :N], 0.5, None, ALU.mult, None, [tmp], [tmp])
            K.ts(tmp[:, :N], tmp[:, :N], ka[:, hp:hp + 1], omka[:, hp:hp + 1], ALU.mult, ALU.add, [tmp, ka, omka], [tmp])
            K.tt(tmp[:, :N], tmp[:, :N], Xk[:, :N], ALU.mult, [tmp, Xk], [tmp])
            yield
            K.stt(sq[:, :N], tmp[:, :N], rk[:, hp:hp + 1], Xr[:, :N], ALU.mult, ALU.mult, [tmp, rk, Xr], [sq])
            P = nxt(PF, "pf")
            for j in range(n):
                K.mm(P[:, 2 * j:2 * j + 2], sq[:, j * 128:(j + 1) * 128], hsel[:], True, True, [sq, hsel], [P])
            K.copy(bsum[:].rearrange("p j e -> p (j e)"), P[:, 0:2 * n], [P], [bsum])
            yield
        if i + 1 < len(items):
            loads(i + 1)
        yield

    def stepB(i):
        hp, d, c0, n, first = items[i]
        p = i % 2
        hc = slice(hp * 128, (hp + 1) * 128)
        t0, N = c0 * 128, n * 128
        latent = c0 >= 2
        rt, at, kt, bt, KHt, BHnt, Vt, gam, bsum = rtP[p], atP[p], ktP[p], btP[p], KHtP[p], BHntP[p], VtP[p], gamP[p], bsumP[p]
        AA, XTb = AAP[p], XTbP[p]
        for j in range(n):
            cs = slice(j * 128, (j + 1) * 128)
            K.S.pe_strict(True)
            G = nxt(PG, "pg")
            for e in range(2):
                ps_ = slice(64 * e, 64 * e + 64)
                K.mm(G[:, 2 * e, :], at[ps_, cs], bt[ps_, cs], True, True, [at, bt], [G])
                K.mm(G[:, 2 * e + 1, :], bt[ps_, cs], at[ps_, cs], True, True, [at, bt], [G])
            K.tt(LL[:, j, :, :], G[:], m1f[:, d, :, :], ALU.mult, [G, m1f], [LL])
            for e in range(2):
                ps_ = slice(64 * e, 64 * e + 64)
                G = nxt(PG, "pg")
                K.mm(G[:, 0, :], kt[ps_, cs], at[ps_, cs], True, True, [kt, at], [G])
                K.mm(G[:, 1, :], kt[ps_, cs], rt[ps_, cs], True, True, [kt, rt], [G])
                K.mm(G[:, 2, :], bt[ps_, cs], rt[ps_, cs], True, True, [bt, rt], [G])
                K.tt(AA[:, j, e, :, :], G[:, 0:3, :], m2f[:, d, :, :], ALU.mult, [G, m2f], [AA])
            K.S.pe_strict(False)
            yield
        for _ in inverse_units(K, C, LL, n, d, XTb, IW):
            yield

    def stepC(i):
        hp, d, c0, n, first = items[i]
        p = i % 2
        hc = slice(hp * 128, (hp + 1) * 128)
        t0, N = c0 * 128, n * 128
        latent = c0 >= 2
        rt, at, kt, bt, KHt, BHnt, Vt, gam, bsum = rtP[p], atP[p], ktP[p], btP[p], KHtP[p], BHntP[p], VtP[p], gamP[p], bsumP[p]
        AA, XTb = AAP[p], XTbP[p]
        if first:
            K.memset(Hf[:], 0.0, [Hf])
            K.memset(Hb[:], 0.0, [Hb])
            if d == 1:
                K.dma("sp", lnw[:], C["lnw"][0:1, hc].to_broadcast([128, 128]), [], [lnw])
                K.dma("sp", lnb[:], C["lnb"][0:1, hc].to_broadcast([128, 128]), [], [lnb])
        jl = list(range(n)) if d == 0 else list(range(n - 1, -1, -1))
        for j in jl:
            cs = slice(j * 128, (j + 1) * 128)
            K.mm(PS_P1[:, 0:128], at[:, cs], Hb[:], True, False, [at, Hb], [PS_P1])
            for e in range(2):
                vs = slice(64 * e, 64 * e + 64)
                K.mm(PS_P1[:, 64 * e:64 + 64 * e], AA[:, j, e, 0, :], Vt[:, j, vs], False, e == 1, [AA, Vt], [PS_P1])
            K.act(P1s[:], PS_P1[:, 0:128], AF.Copy, [PS_P1], [P1s])
            yield
            for e in range(2):
                vs = slice(64 * e, 64 * e + 64)
                K.mm(PS_U[:, 128 + 64 * e:192 + 64 * e], XTb[:, 2 * j + e, :], P1s[:, vs], True, True, [XTb, P1s], [PS_U])
            K.copy(Us[:], PS_U[:, 128:256], [PS_U], [Us])
            yield
            if latent:
                K.mm(PS_Y[:, 256:384], rt[:, cs], Hb[:], True, False, [rt, Hb], [PS_Y])
                for e in range(2):
                    vs = slice(64 * e, 64 * e + 64)
                    yo = PS_Y[:, 256 + 64 * e:320 + 64 * e]
                    K.mm(yo, AA[:, j, e, 1, :], Vt[:, j, vs], False, False, [AA, Vt], [PS_Y])
                    K.mm(yo, AA[:, j, e, 2, :], Us[:, vs], False, e == 1, [AA, Us], [PS_Y])
            K.mm(PS_H[:, 384:512], KHt[:, j, :], Vt[:, j, :], True, False, [KHt, Vt], [PS_H])
            K.mm(PS_H[:, 384:512], BHnt[:, j, :], Us[:], False, True, [BHnt, Us], [PS_H])
            for e in range(2):
                ps_ = slice(64 * e, 64 * e + 64)
                vs = slice(64 * e, 64 * e + 64)
                K.stt(Hf[ps_, vs], Hf[ps_, vs], gam[ps_, j:j + 1], PS_H[ps_, 384 + 64 * e:448 + 64 * e], ALU.mult, ALU.add,
                      [Hf, gam, PS_H], [Hf])
            K.act(Hb[:], Hf[:], AF.Copy, [Hf], [Hb])
            if latent:
                cg = c0 - 2 + j
                if d == 0:
                    K.act(ybuf[:, cg, :], PS_Y[:, 256:384], AF.Copy, [PS_Y], [ybuf])
                else:
                    K.tt(ytot[:, j, :], ybuf[:, cg, :], PS_Y[:, 256:384], ALU.add, [ybuf, PS_Y], [ytot])
            yield
        if d == 1 and latent:
            if dbg and hp < 8:
                K.dma("sp", dbg["yf"][hp, :, c0 - 2:c0 - 2 + n, :], ytot[:], [ytot], [])
            yv = ytot[:].rearrange("p j (e c) -> p (j e) c", c=64)
            ycv = yc[:].rearrange("p j (e c) -> p (j e) c", c=64)
            sqv = ysq[:].rearrange("p j (e c) -> p (j e) c", c=64)
            K.S.add("dve", lambda e_: e_.tensor_reduce(out=mean[:], in_=yv, axis=AX.X, op=ALU.add), _bufs([ytot]), _bufs([mean]))
            K.ts(mean[:], mean[:], 1.0 / 64, None, ALU.mult, None, [mean], [mean])
            K.tt(ycv, yv, mean[:].unsqueeze(2).to_broadcast([128, 8, 64]), ALU.subtract, [ytot, mean], [yc])
            K.tt(sqv, ycv, ycv, ALU.mult, [yc], [ysq], eng="pool")
            yield
            K.S.add("dve", lambda e_: e_.tensor_reduce(out=var[:], in_=sqv, axis=AX.X, op=ALU.add), _bufs([ysq]), _bufs([var]))
            K.act(var[:], var[:], AF.Sqrt, [var], [var], scale=1.0 / 64, bias=64e-5)
            K.recip(var[:], var[:], [var], [var])
            K.tt(ycv, ycv, var[:].unsqueeze(2).to_broadcast([128, 8, 64]), ALU.mult, [yc, var], [yc])
            yield
            K.tt(yc[:], yc[:], lnw[:].unsqueeze(1).to_broadcast([128, 4, 128]), ALU.mult, [yc, lnw], [yc])
            K.tt(yc[:], yc[:], lnb[:].unsqueeze(1).to_broadcast([128, 4, 128]), ALU.add, [yc, lnb], [yc])
            K.copy(ysq[:], Vt[:], [Vt], [ysq], eng="pool")
            K.tt(sqv, sqv, bsum[:].rearrange("p j e -> p (j e)").unsqueeze(2).to_broadcast([128, 8, 64]), ALU.mult,
                 [ysq, bsum], [ysq])
            K.tt(yc[:], yc[:], ysq[:], ALU.add, [yc, ysq], [yc])
            yield
            P = nxt(PF, "pf")
            for j in range(n):
                K.mm(P[:, j * 128:(j + 1) * 128], smallT[0:64, 4, t0 + j * 128:t0 + (j + 1) * 128], g2b[0:64, hc], True, True,
                     [smallT, g2b], [P])
            K.tt(oat[:], yc[:], P[:].rearrange("p (j c) -> p j c", c=128), ALU.mult, [yc, P], [oat])
            half = cnt["tr"] % 2
            cnt["tr"] += 1
            for j in range(n):
                K.tr(PTr[:, half * 4 + j, :], oat[:, j, :], ident_b[:], [oat, ident_b], [PTr])
            K.copy(oaT[:, hp, t0 - 256:t0 - 256 + N].rearrange("p (j t) -> p j t", t=128), PTr[:, half * 4:half * 4 + n, :],
                   [PTr], [oaT])
            yield

    def drain(g):
        for _ in g:
            pass

    def chain(*gs):
        for g in gs:
            for _ in g:
                yield

    loads(0)
    drain(stepA(0))
    drain(stepB(0))
    for i in range(len(items)):
        g1 = stepC(i)
        g2 = chain(stepA(i + 1), stepB(i + 1)) if i + 1 < len(items) else iter(())
        a1 = a2_ = True
        while a1 or a2_:
            if a1:
                try:
                    next(g1)
                except StopIteration:
                    a1 = False
            if a2_:
                for _ in range(RATIO):
                    try:
                        next(g2)
                    except StopIteration:
                        a2_ = False
                        break


def gdn_phase(K, st, C):
    US, SZ, abT, obT, ident_b, ident_f = C["US"], C["SZ"], C["abT"], C["obT"], C["ident_b"], C["ident_f"]
    sb = lambda shape, dt, name=None: K.sb(st, shape, dt, name)
    bigm = sb([128, 2, 2, 128], F32); offd = sb([128, 128], F32); ones = sb([128, 128], F32)
    K.dma("sp", bigm[:], C["bigm"][:, :, :, :], [], [bigm])
    K.dma("sp", offd[:], C["offd"][:, :], [], [offd])
    K.dma("sp", ones[:], C["ones"][:, :], [], [ones])
    rmask = sb([128, 512], F32)
    K.dma("sp", rmask[:], C["rmask"][:, :], [], [rmask])
    alog = sb([64, 1], F32); dtb = sb([64, 1], F32); nea = sb([64, 1], F32)
    K.dma("sp", alog[:], C["alog"][:, :], [], [alog])
    K.dma("sp", dtb[:], C["dtb"][:, :], [], [dtb])
    gnw = sb([128, 128], F32)
    K.dma("sp", gnw[:], C["gnw"][0:1, :].to_broadcast([128, 128]), [], [gnw])
    K.act(nea[:], alog[:], AF.Exp, [alog], [nea])
    K.ts(nea[:], nea[:], -1.0, None, ALU.mult, None, [nea], [nea])
    GB = [sb([64, TT], F32, "GB%d" % d) for d in range(2)]
    tokT = [sb([128, NCH, 64], F32, "tokT%d" % d) for d in range(2)]
    with ExitStack() as s0:
        gt = K.sb(s0, [16, TT], F32); A = K.sb(s0, [16, TT], F32); Bx = K.sb(s0, [16, TT], F32)
        K.act(gt[:], abT[0:16, :], AF.Exp, [abT, dtb], [gt], bias=dtb[0:16, :])
        K.act(gt[:], gt[:], AF.Ln, [gt], [gt], bias=1.0)
        K.ts(gt[:], gt[:], nea[0:16, :], None, ALU.mult, None, [gt, nea], [gt])
        for d in range(2):
            K.memset(GB[d][:], 0.0, [GB[d]])
            K.act(GB[d][32:48, :], abT[32:48, :], AF.Sigmoid, [abT], [GB[d]])
        for t0 in range(0, TT, 512):
            N = min(512, TT - t0)
            K.scan(A[:, t0:t0 + N], rmask[0:16, :N], gt[:, t0:t0 + N], 0.0, ALU.mult, ALU.add, [rmask, gt], [A])
        K.copy(GB[0][0:16, :], A[:], [A], [GB[0]], eng="pool")
        K.tt(Bx[:], A[:], gt[:], ALU.subtract, [A, gt], [Bx])
        v3 = lambda t_: t_[:].rearrange("p (c t) -> p c t", t=128)
        tot = v3(A)[:, :, 127:128]
        K.tt(v3(GB[1])[0:16], tot.to_broadcast([16, NCH, 128]), v3(Bx), ALU.subtract, [A, Bx], [GB[1]])
        ptk = K.ps(s0, [128, 8, 64], F32)
        for d in range(2):
            for c8 in range(0, NCH, 8):
                nn = min(8, NCH - c8)
                for j in range(nn):
                    c = c8 + j
                    K.tr(ptk[:, j, :], GB[d][:, c * 128:(c + 1) * 128], ident_f[0:64, 0:64], [GB[d], ident_f], [ptk])
                K.copy(tokT[d][:, c8:c8 + nn, :], ptk[:, 0:nn, :], [ptk], [tokT[d]])
    GBS = C["GBS"]
    for d in range(2):
        K.dma("sp", GBS[d, :, :], GB[d][:], [GB[d]], [GBS])
    K.S.barrier()
    f32t = lambda nm=None: sb([128, 512], F32, nm)
    bft = lambda nm=None: sb([128, 512], BF16, nm)
    _xq, _xk, _xv = f32t("gXq"), f32t("gXk"), f32t("gXv")
    XsP = [(_xq, _xk, _xv, f32t("gXG%d" % p), f32t("gXB%d" % p)) for p in range(2)]
    sq, rn, qn, kn, eG, tmp, sz = (f32t("g_" + nm) for nm in ("sq", "rn", "qn", "kn", "eG", "tmp", "sz"))
    Ktl, vb = bft("g_Ktl"), bft("g_vb")
    knbP, qnbP, kbTP, nKBGP, QdP = ([bft("g_%s%d" % (nm, p)) for p in range(2)] for nm in ("knb", "qnb", "kbT", "nKBG", "Qd"))
    KttP = [sb([128, 4, 128], BF16, "g_Ktt%d" % p) for p in range(2)]
    VtP = [sb([128, 4, 128], BF16, "g_Vt%d" % p) for p in range(2)]
    glP = [sb([128, 4], F32, "g_gl%d" % p) for p in range(2)]
    Dc = sb([128, 4, 2, 128], F32, "g_Dc"); DiT = sb([128, 4, 128], F32, "g_DiT"); Dtmp = sb([128, 4, 128], F32, "g_Dtmp")
    LLg = sb([128, 2, 4, 128], BF16, "g_LL")
    QKt = sb([128, 4, 128], BF16, "g_QKt")
    XTb = sb([128, 4, 128], BF16, "g_XTb")
    IW = inverse_workspace(K, st, C)
    Sf = sb([128, 128], F32, "g_Sf"); Sb = sb([128, 128], BF16, "g_Sb")
    P1s = sb([128, 128], BF16, "g_P1s"); VNs = sb([128, 128], BF16, "g_VNs")
    obuf = sb([128, 16, 128], F32, "g_obuf")
    otot = sb([128, 4, 128], F32, "g_otot"); osq = sb([128, 4, 128], F32, "g_osq")
    ss = sb([128, 4], F32); onb = sb([128, 4, 128], BF16, "g_onb")
    PF = K.ps(st, [128, 512], F32)
    PTr = K.ps(st, [128, 8, 128], BF16)
    PG = [K.ps(st, [128, 4, 128], F32) for _ in range(2)]
    PSq = K.ps(st, [128, 512], F32)
    PSh = K.ps(st, [128, 512], F32)
    cnt = {"pg": 0, "tr": 0}

    def transp(src, dst, n):
        half = cnt["tr"] % 2
        cnt["tr"] += 1
        for j in range(n):
            K.tr(PTr[:, half * 4 + j, :], src[:, j * 128:(j + 1) * 128], ident_b[:], [src, ident_b], [PTr])
        K.copy(dst[:, :n, :], PTr[:, half * 4:half * 4 + n, :], [PTr], [dst])

    items = []
    for h in range(C["nh"]):
        for d in range(2):
            order = SEGS if d == 0 else [SEGS[0], SEGS[4], SEGS[3], SEGS[2], SEGS[1]]
            for si, (c0, n) in enumerate(order):
                items.append((h, d, c0, n, si == 0))

    def loads(i):
        h, d, c0, n, first = items[i]
        r = d * 8 + h
        t0, N = c0 * 128, n * 128
        Xq, Xk, Xv, bcG, bcB = XsP[i % 2]
        for X_, row in ((Xq, 0), (Xk, 1024), (Xv, 2048)):
            K.dma("sp", X_[:, :N], US[row + h * 128:row + h * 128 + 128, t0:t0 + N], [US], [X_])
        K.dma("sp", bcG[:, :N], GBS[d, r:r + 1, t0:t0 + N].to_broadcast([128, N]), [GBS], [bcG])
        K.dma("sp", bcB[:, :N], GBS[d, 32 + r:33 + r, t0:t0 + N].to_broadcast([128, N]), [GBS], [bcB])

    def stepA(i):
        h, d, c0, n, first = items[i]
        p = i % 2
        t0, N = c0 * 128, n * 128
        Xq, Xk, Xv, bcG, bcB = XsP[p]
        knb, qnb, kbT, nKBG, Qd, Ktt, Vt, gl = knbP[p], qnbP[p], kbTP[p], nKBGP[p], QdP[p], KttP[p], VtP[p], glP[p]
        v3 = lambda t_: t_[:, :N].rearrange("p (c t) -> p c t", t=128)
        for X_, o_, sc_ in ((Xq, qn, 128 ** -0.5), (Xk, kn, 1.0)):
            K.act(sq[:, :N], X_[:, :N], AF.Square, [X_], [sq])
            K.mm(PF[:, :N], ones[:], sq[:, :N], True, True, [ones, sq], [PF])
            K.act(rn[:, :N], PF[:, :N], AF.Sqrt, [PF], [rn], bias=EPS)
            K.recip(rn[:, :N], rn[:, :N], [rn], [rn])
            K.stt(o_[:, :N], X_[:, :N], sc_, rn[:, :N], ALU.mult, ALU.mult, [X_, rn], [o_])
            yield
        K.act(eG[:, :N], bcG[:, :N], AF.Exp, [bcG], [eG])
        lastcol = 127 if d == 0 else 0
        K.copy(gl[:, :n], v3(eG)[:, :, lastcol], [eG], [gl], eng="pool")
        glast = v3(bcG)[:, :, lastcol:lastcol + 1]
        K.tt(kbT[:, :N], kn[:, :N], bcB[:, :N], ALU.mult, [kn, bcB], [kbT])
        yield
        K.copy(knb[:, :N], kn[:, :N], [kn], [knb], eng="pool")
        K.act(qnb[:, :N], qn[:, :N], AF.Copy, [qn], [qnb])
        K.stt(nKBG[:, :N], kbT[:, :N], -1.0, eG[:, :N], ALU.mult, ALU.mult, [kbT, eG], [nKBG])
        yield
        K.tt(Qd[:, :N], qn[:, :N], eG[:, :N], ALU.mult, [qn, eG], [Qd], eng="pool")
        K.tt(v3(tmp), glast.to_broadcast([128, n, 128]), v3(bcG), ALU.subtract, [bcG], [tmp])
        K.act(tmp[:, :N], tmp[:, :N], AF.Exp, [tmp], [tmp])
        yield
        K.tt(Ktl[:, :N], kn[:, :N], tmp[:, :N], ALU.mult, [kn, tmp], [Ktl])
        K.act(vb[:, :N], Xv[:, :N], AF.Copy, [Xv], [vb])
        transp(Ktl, Ktt, n)
        yield
        transp(vb, Vt, n)
        yield
        if i + 1 < len(items):
            loads(i + 1)
        yield

    def stepBC(i):
        h, d, c0, n, first = items[i]
        p = i % 2
        r = d * 8 + h
        t0, N = c0 * 128, n * 128
        latent = c0 >= 2
        Xq, Xk, Xv, bcG, bcB = XsP[p]
        knb, qnb, kbT, nKBG, Qd, Ktt, Vt, gl = knbP[p], qnbP[p], kbTP[p], nKBGP[p], QdP[p], KttP[p], VtP[p], glP[p]
        v3 = lambda t_: t_[:, :N].rearrange("p (c t) -> p c t", t=128)
        if first:
            K.memset(Sf[:], 0.0, [Sf])
            K.memset(Sb[:], 0.0, [Sb])
        gct = tokT[d][:, c0:c0 + n, r:r + 1].to_broadcast([128, n, 128])
        bg3 = v3(bcG)
        K.tt(Dtmp[:, :n, :], bg3, bigm[:, d, 0, :].unsqueeze(1).to_broadcast([128, n, 128]), ALU.add, [bcG, bigm], [Dtmp])
        K.tt(Dtmp[:, :n, :], Dtmp[:, :n, :], gct, ALU.subtract, [Dtmp, tokT[d]], [Dtmp], eng="pool")
        K.act(Dtmp[:, :n, :], Dtmp[:, :n, :], AF.Exp, [Dtmp], [Dtmp], scale=-1.0)
        K.tt(Dc[:, :n, 0, :], Dtmp[:, :n, :], offd[:].unsqueeze(1).to_broadcast([128, n, 128]), ALU.mult, [Dtmp, offd], [Dc],
             eng="pool")
        yield
        K.tt(DiT[:, :n, :], bg3, bigm[:, d, 1, :].unsqueeze(1).to_broadcast([128, n, 128]), ALU.add, [bcG, bigm], [DiT])
        K.tt(DiT[:, :n, :], DiT[:, :n, :], gct, ALU.subtract, [DiT, tokT[d]], [DiT], eng="pool")
        K.act(DiT[:, :n, :], DiT[:, :n, :], AF.Exp, [DiT], [DiT])
        K.tt(Dc[:, :n, 1, :], DiT[:, :n, :], offd[:].unsqueeze(1).to_broadcast([128, n, 128]), ALU.mult, [DiT, offd], [Dc],
             eng="pool")
        yield
        LLv = LLg[:].rearrange("p a b t -> p (a b) t")
        for j in range(n):
            cs = slice(j * 128, (j + 1) * 128)
            cnt["pg"] += 1
            G = PG[cnt["pg"] % 2]
            K.mm(G[:, 0, :], kbT[:, cs], knb[:, cs], True, True, [kbT, knb], [G])
            K.mm(G[:, 1, :], knb[:, cs], kbT[:, cs], True, True, [kbT, knb], [G])
            K.mm(G[:, 2, :], knb[:, cs], qnb[:, cs], True, True, [qnb, knb], [G])
            K.tt(LLv[:, 2 * j:2 * j + 2, :], G[:, 0:2, :], Dc[:, j, :, :], ALU.mult, [G, Dc], [LLg])
            K.tt(QKt[:, j, :], G[:, 2, :], DiT[:, j, :], ALU.mult, [G, DiT], [QKt])
            yield
        for _ in inverse_units(K, C, LLg, n, d, XTb, IW, nunits=n):
            yield
        jl = list(range(n)) if d == 0 else list(range(n - 1, -1, -1))
        for j in jl:
            cs = slice(j * 128, (j + 1) * 128)
            c = c0 + j
            K.mm(PSq[:, 0:128], nKBG[:, cs], Sb[:], True, True, [nKBG, Sb], [PSq])
            K.stt(P1s[:], Vt[:, j, :], tokT[d][:, c, 32 + r:33 + r], PSq[:, 0:128], ALU.mult, ALU.add,
                  [Vt, tokT[d], PSq], [P1s])
            yield
            K.mm(PSq[:, 128:256], XTb[:, j, :], P1s[:], True, True, [XTb, P1s], [PSq])
            K.act(VNs[:], PSq[:, 128:256], AF.Copy, [PSq], [VNs])
            yield
            if latent:
                K.mm(PSq[:, 256:384], Qd[:, cs], Sb[:], True, False, [Qd, Sb], [PSq])
                K.mm(PSq[:, 256:384], QKt[:, j, :], VNs[:], False, True, [QKt, VNs], [PSq])
            K.mm(PSh[:, 0:128], Ktt[:, j, :], VNs[:], True, True, [Ktt, VNs], [PSh])
            K.stt(Sf[:], Sf[:], gl[:, j:j + 1], PSh[:, 0:128], ALU.mult, ALU.add, [Sf, gl, PSh], [Sf])
            K.act(Sb[:], Sf[:], AF.Copy, [Sf], [Sb])
            if latent:
                cg = c - 2
                if d == 0:
                    K.act(obuf[:, cg, :], PSq[:, 256:384], AF.Copy, [PSq], [obuf])
                else:
                    K.tt(otot[:, j, :], obuf[:, cg, :], PSq[:, 256:384], ALU.add, [obuf, PSq], [otot])
            yield
        if d == 1 and latent:
            if C["dbg"]:
                K.dma("sp", C["dbg"]["of"][h, :, c0 - 2:c0 - 2 + n, :], otot[:], [otot], [])
            K.tt(osq[:], otot[:], otot[:], ALU.mult, [otot], [osq], eng="pool")
            K.S.add("dve", lambda e_: e_.tensor_reduce(out=ss[:], in_=osq[:], axis=AX.X, op=ALU.add), _bufs([osq]), _bufs([ss]))
            K.act(ss[:], ss[:], AF.Sqrt, [ss], [ss], scale=1.0 / 128, bias=EPS)
            K.recip(ss[:], ss[:], [ss], [ss])
            yield
            K.tt(osq[:], otot[:], ss[:].unsqueeze(2).to_broadcast([128, 4, 128]), ALU.mult, [otot, ss], [osq])
            K.tt(onb[:], osq[:], gnw[:].unsqueeze(1).to_broadcast([128, 4, 128]), ALU.mult, [osq, gnw], [onb])
            K.dma("sp", sz[:, :N], SZ[h * 128:(h + 1) * 128, t0 - 256:t0 - 256 + N], [SZ], [sz])
            yield
            half = cnt["tr"] % 2
            cnt["tr"] += 1
            for j in range(n):
                K.tr(PTr[:, half * 4 + j, :], onb[:, j, :], ident_b[:], [onb, ident_b], [PTr])
            K.tt(obT[:, h, t0 - 256:t0 - 256 + N].rearrange("p (j t) -> p j t", t=128), PTr[:, half * 4:half * 4 + n, :],
                 sz[:, :N].rearrange("p (j t) -> p j t", t=128), ALU.mult, [PTr, sz], [obT])
            yield

    loads(0)
    for _ in stepA(0):
        pass
    for i in range(len(items)):
        g1 = stepBC(i)
        g2 = stepA(i + 1) if i + 1 < len(items) else iter(())
        a1 = a2 = True
        while a1 or a2:
            if a1:
                for _ in range(RATIO_G):
                    try:
                        next(g1)
                    except StopIteration:
                        a1 = False
                        break
            if a2:
                try:
                    next(g2)
                except StopIteration:
                    a2 = False


def write_mods(K, C):
    modT, ident_f, MODS = C["modT"], C["ident_f"], C["MODS"]
    with ExitStack() as s0:
        pt = K.ps(s0, [16, 2, 128], F32)
        rows = K.sb(s0, [16, 2, 128], F32)
        for i, sec in enumerate((2, 5)):
            K.tr(pt[:, i, :], modT[:, sec * 16:(sec + 1) * 16, 0], ident_f[:], [modT, ident_f], [pt])
        K.copy(rows[:], pt[:], [pt], [rows])
        for i in range(2):
            K.dma("sp", MODS[i * 16:(i + 1) * 16, :], rows[:, i, :], [rows], [MODS])
    K.S.barrier()


def merge_phase(K, top, C):
    oaT, obT, ident_b, ident_f, modT, s2 = C["oaT"], C["obT"], C["ident_b"], C["ident_f"], C["modT"], C["s2"]
    SG, X1, x_d, out_d = C["SG"], C["X1"], C["x"], C["out"]
    MODS = C["MODS"]
    dbg = C["dbg"]
    bc = K.sb(top, [128, 1, D], F32, "bc_rows")
    K.dma("sp", bc[:, 0, :], MODS[0:16, :].rearrange("(o a) b -> o (a b)", o=1).to_broadcast([128, D]), [MODS], [bc])
    K.S.barrier()
    p3 = ExitStack()
    mT = K.sb(p3, [128, 16, TL], BF16, "mT")
    with ExitStack() as s1:
        wa = [K.sb(s1, [128, 8, 256], BF16) for _ in range(2)]
        wbb = [K.sb(s1, [128, 8, 256], BF16) for _ in range(2)]
        sga = [K.sb(s1, [128, TL], BF16)] * 2
        sgb = [K.sb(s1, [128, TL], BF16)] * 2
        t1 = [K.sb(s1, [128, 512], F32) for _ in range(2)]
        t2 = [K.sb(s1, [128, 512], F32) for _ in range(2)]
        pa = [K.ps(s1, [128, 512], F32) for _ in range(2)]
        pb = [K.ps(s1, [128, 512], F32) for _ in range(2)]
        it = 0
        for sc in range(8):
            K.dma("pool", wa[sc % 2][:], C["p_a"][:, sc * 256:(sc + 1) * 256].rearrange("(k p) n -> p k n", p=128), [], [wa[sc % 2]])
            K.dma("pool", wbb[sc % 2][:], C["p_b"][:, sc * 256:(sc + 1) * 256].rearrange("(k p) n -> p k n", p=128), [], [wbb[sc % 2]])
            for ff in range(2):
                f = sc * 2 + ff
                ga, gb = sga[f % 2], sgb[f % 2]
                K.dma("sp", ga[:], SG[f * 128:(f + 1) * 128, :], [SG], [ga])
                K.dma("sp", gb[:], SG[2048 + f * 128:2048 + (f + 1) * 128, :], [SG], [gb])
                for n in range(4):
                    ts_ = slice(n * 512, (n + 1) * 512)
                    A_, B_, T1, T2 = pa[it % 2], pb[it % 2], t1[it % 2], t2[it % 2]
                    it += 1
                    for k in range(8):
                        K.mm(A_[:], wa[sc % 2][:, k, ff * 128:(ff + 1) * 128], oaT[:, k, ts_], k == 0, k == 7, [wa[sc % 2], oaT], [A_])
                    for k in range(8):
                        K.mm(B_[:], wbb[sc % 2][:, k, ff * 128:(ff + 1) * 128], obT[:, k, ts_], k == 0, k == 7, [wbb[sc % 2], obT], [B_])
                    K.tt(T1[:], A_[:], ga[:, ts_], ALU.mult, [A_, ga], [T1])
                    K.tt(T2[:], B_[:], gb[:, ts_], ALU.mult, [B_, gb], [T2])
                    K.tt(mT[:, f, ts_], T1[:], T2[:], ALU.add, [T1, T2], [mT], eng="pool")
    K.S.barrier()
    with ExitStack() as s2_:
        wo = [[K.sb(s2_, [128, 8, 512], BF16) for _ in range(2)] for _ in range(2)]
        xt = [K.sb(s2_, [128, 512], F32) for _ in range(3)]
        tt_ = [K.sb(s2_, [128, 512], F32) for _ in range(3)]
        pp = [K.ps(s2_, [128, 512], F32) for _ in range(4)]
        it = 0
        for n in range(4):
            ns = slice(n * 512, (n + 1) * 512)
            w = wo[n % 2]
            for kh in range(2):
                K.dma("pool", w[kh][:], C["w_out"][kh * 1024:(kh + 1) * 1024, ns].rearrange("(k p) n -> p k n", p=128), [], [w[kh]])
            for t in range(16):
                P, X_, T_ = pp[it % 4], xt[it % 3], tt_[it % 3]
                it += 1
                K.dma("sp", X_[:], x_d[t * 128:(t + 1) * 128, ns], [], [X_])
                for k in range(16):
                    K.mm(P[:], mT[:, k, t * 128:(t + 1) * 128], w[k // 8][:, k % 8, :], k == 0, k == 15, [mT, w[k // 8]], [P])
                K.tt(T_[:], P[:], bc[:, 0, ns], ALU.mult, [P, bc], [T_])
                K.tt(T_[:], T_[:], X_[:], ALU.add, [T_, X_], [T_], eng="pool")
                K.dma("sp", X1[t * 128:(t + 1) * 128, ns], T_[:], [T_], [X1])
    p3.close()


def ffn_phase(K, top, C):
    ident_b, ident_f, modT, s2 = C["ident_b"], C["ident_f"], C["modT"], C["s2"]
    X1, out_d, MODS = C["X1"], C["out"], C["MODS"]
    G = 512
    with ExitStack() as s4:
        bc = K.sb(s4, [128, 3, D], F32, "bc_rows4")
        K.dma("sp", bc[:, 1, :], MODS[16:32, :].rearrange("(o a) b -> o (a b)", o=1).to_broadcast([128, D]), [MODS], [bc])
        K.dma("sp", bc[:, 2, :], C["fnw"][0:1, :].to_broadcast([128, D]), [], [bc])
        h2T = K.sb(s4, [128, 16, G], BF16, "h2T")
        actT = K.sb(s4, [128, 44, G], BF16, "actT")
        x1t = [K.sb(s4, [128, D], F32, "x1t%d" % i) for i in range(4)]
        xb = K.sb(s4, [128, D], BF16)
        junk = K.sb(s4, [128, D], BF16)
        ss = K.sb(s4, [128, 1], F32); rs = K.sb(s4, [128, 1], F32)
        wg = [[K.sb(s4, [128, 8, 256], BF16) for _ in range(2)] for _ in range(2)]
        wu = [[K.sb(s4, [128, 8, 256], BF16) for _ in range(2)] for _ in range(2)]
        wd = [K.sb(s4, [128, 4, 512], BF16) for _ in range(4)]
        sgt = [K.sb(s4, [128, G], F32) for _ in range(2)]
        tq = [K.sb(s4, [128, 512], F32) for _ in range(2)]
        ot = [K.sb(s4, [128, D], F32) for _ in range(2)]
        ptr = [K.ps(s4, [128, 8, 128], BF16) for _ in range(1)]
        pgu = [K.ps(s4, [128, 512], F32) for _ in range(3)]
        pdn = [K.ps(s4, [128, 512], F32) for _ in range(4)]
        igu = 0
        for grp in range(NGRP):
            for t in range(4):
                X_ = x1t[t]
                row0 = grp * G + t * 128
                K.dma("sp", X_[:], X1[row0:row0 + 128, :], [X1], [X_])
                K.act(junk[:], X_[:], AF.Square, [X_], [junk, ss], accum=ss[:])
                K.act(rs[:], ss[:], AF.Sqrt, [ss], [rs], scale=1.0 / D, bias=EPS)
                K.recip(rs[:], rs[:], [rs], [rs])
                K.act(xb[:], X_[:], AF.Copy, [X_, rs], [xb], scale=rs[:])
                for g in range(4):
                    P = ptr[0]
                    for j in range(4):
                        k = g * 4 + j
                        K.tr(P[:, j, :], xb[:, k * 128:(k + 1) * 128], ident_b[:], [xb, ident_b], [P])
                    for j in range(4):
                        k = g * 4 + j
                        K.act(h2T[:, k, t * 128:(t + 1) * 128], P[:, j, :], AF.Identity, [P, s2, modT], [h2T],
                              scale=s2[:, k:k + 1], bias=modT[:, 48 + k, 0:1])
            for sc in range(22):
                w1, w2 = wg[sc % 2], wu[sc % 2]
                for kh in range(2):
                    K.dma("pool", w1[kh][:], C["w_gu"][kh * 1024:(kh + 1) * 1024, sc * 256:(sc + 1) * 256].rearrange("(k p) n -> p k n", p=128),
                          [], [w1[kh]])
                    K.dma("pool", w2[kh][:], C["w_gu"][kh * 1024:(kh + 1) * 1024, FFN + sc * 256:FFN + (sc + 1) * 256].rearrange("(k p) n -> p k n", p=128),
                          [], [w2[kh]])
                for jj in range(2):
                    j = sc * 2 + jj
                    Pg, Pu = pgu[igu % 3], pgu[(igu + 1) % 3]
                    SGt = sgt[(igu // 2) % 2]
                    igu += 2
                    for k in range(16):
                        K.mm(Pg[:, :G], w1[k // 8][:, k % 8, jj * 128:(jj + 1) * 128], h2T[:, k, :], k == 0, k == 15, [w1[k // 8], h2T], [Pg])
                    for k in range(16):
                        K.mm(Pu[:, :G], w2[k // 8][:, k % 8, jj * 128:(jj + 1) * 128], h2T[:, k, :], k == 0, k == 15, [w2[k // 8], h2T], [Pu])
                    K.act(SGt[:], Pg[:, :G], AF.Silu, [Pg], [SGt])
                    K.tt(actT[:, j, :], SGt[:], Pu[:, :G], ALU.mult, [SGt, Pu], [actT])
            iw = 0
            for n in range(4):
                ns = slice(n * 512, (n + 1) * 512)
                for k4 in range(11):
                    W = wd[iw % 4]
                    iw += 1
                    K.dma("pool", W[:], C["w_dn"][k4 * 512:(k4 + 1) * 512, ns].rearrange("(k p) n -> p k n", p=128), [], [W])
                    for kk in range(4):
                        k = k4 * 4 + kk
                        for t in range(4):
                            K.mm(pdn[t][:], actT[:, k, t * 128:(t + 1) * 128], W[:, kk, :], k == 0, k == 43, [actT, W], [pdn[t]])
                for t in range(4):
                    T_ = tq[t % 2]
                    K.tt(T_[:], pdn[t][:], bc[:, 1, ns], ALU.mult, [pdn[t], bc], [T_])
                    K.tt(x1t[t][:, ns], x1t[t][:, ns], T_[:], ALU.add, [x1t[t], T_], [x1t[t]], eng="pool")
            for t in range(4):
                X_ = x1t[t]
                O_ = ot[t % 2]
                row0 = grp * G + t * 128
                K.act(junk[:], X_[:], AF.Square, [X_], [junk, ss], accum=ss[:])
                K.act(rs[:], ss[:], AF.Sqrt, [ss], [rs], scale=1.0 / D, bias=EPS)
                K.recip(rs[:], rs[:], [rs], [rs])
                K.stt(O_[:], X_[:], rs[:], bc[:, 2, :], ALU.mult, ALU.mult, [X_, rs, bc], [O_])
                K.dma("sp", out_d[row0:row0 + 128, :], O_[:], [O_], [out_d])

NHP = 8
NGH = 8
NGRP = 4
RELAX = [True, True, True, True, True]
RATIO = 3
RATIO_G = 8


def build(debug=False, only=None):
    nc = bass.Bass("TRN2", target_bir_lowering=False)
    K = KB(nc)
    inp = lambda name, shape: K.dram(name, shape, F32, kind="ExternalInput")
    x_d = inp("x", [TL, D])
    ctx_d = inp("ctx", [TC, D])
    cc_d = inp("cc", [128, 16, 2])
    wada_d = inp("w_ada", [D, 6 * D])
    bada_d = inp("b_ada", [128, 96])
    n1w_d = inp("norm1_w", [128, 16])
    n2w_d = inp("norm2_w", [128, 16])
    fnw_d = inp("final_norm_w", [1, D])
    win_d = inp("w_in", [D, NIN * 128])
    mu_d = inp("rw_mu", [128, 29])
    cmask_d = inp("cmask", [128, 8])
    convw_d = inp("gdn_conv_w", [128, 24, 5])
    ident_d = inp("ident", [128, 128])
    w0_d = inp("rw_w0", [128, 2, 8])
    a0_d = inp("rw_a0", [128, 2, 8])
    kkw_d = inp("rw_k_k", [128, 8])
    ka_d = inp("rw_k_a", [128, 8])
    rk_d = inp("rw_r_k", [128, 8])
    w2_d = inp("rw_w2", [2, 96, 1024])
    a2_d = inp("rw_a2", [2, 96, 1024])
    g2_d = inp("rw_g2", [64, 1024])
    lnw_d = inp("rw_ln_w", [1, 1024])
    lnb_d = inp("rw_ln_b", [1, 1024])
    m1_d = inp("m1", [128, 2, 4, 128])
    m2_d = inp("m2", [128, 2, 3, 128])
    rmask_d = inp("rmask", [128, 512])
    bones_d = inp("blockones", [128, 128])
    hsel_d = inp("headsel", [128, 2])
    nmask_d = inp("nmask", [128, 2, 7, 128])
    selg_d = inp("selg", [64, 16, 128])
    selb_d = inp("selb", [64, 16, 128])
    bigm_d = inp("bigm", [128, 2, 2, 128])
    offd_d = inp("offd", [128, 128])
    ones_d = inp("ones", [128, 128])
    alog_d = inp("gdn_a_log", [64, 1])
    dtb_d = inp("gdn_dt_bias", [64, 1])
    gnw_d = inp("gdn_norm_w", [1, 128])
    pa_d = inp("merge_p_a", [1024, D])
    pb_d = inp("merge_p_b", [1024, D])
    wout_d = inp("w_out", [D, D])
    wgu_d = inp("ffn_w_gate_up", [D, 2 * FFN])
    wdn_d = inp("ffn_w_down", [FFN, D])
    MODS = K.dram("MODS", [32, 128], F32)
    GBS = K.dram("GBS", [2, 64, TT], F32)
    out_d = K.dram("out", [TL, D], F32, kind="ExternalOutput")
    dbg = {}
    if debug:
        dbg["xs"] = K.dram("dbg_xs", [24 * 128, TT], F32, kind="ExternalOutput")
        dbg["u"] = K.dram("dbg_u", [24 * 128, TT], F32, kind="ExternalOutput")
        dbg["mod"] = K.dram("dbg_mod", [128, 96 * 2], F32, kind="ExternalOutput")
        dbg["sm"] = K.dram("dbg_sm", [128, 5, TT], F32, kind="ExternalOutput")
        dbg["oa"] = K.dram("dbg_oa", [128, 8, TL], F32, kind="ExternalOutput")
        dbg["yf"] = K.dram("dbg_yf", [8, 128, 16, 128], F32, kind="ExternalOutput")
        dbg["ob"] = K.dram("dbg_ob", [128, 8, TL], F32, kind="ExternalOutput")
        dbg["of"] = K.dram("dbg_of", [8, 128, 16, 128], F32, kind="ExternalOutput")
    if only:
        XS = inp("XS_in", [24 * 128, TT])
        small_in = inp("small_in", [128, 5, TT])
        US = inp("US_in", [24 * 128, TT])
        SZ = inp("SZ_in", [8 * 128, TL])
        ab_in = inp("ab_in", [128, TT])
    else:
        XS = dbg["xs"] if debug else K.dram("XS", [24 * 128, TT], F32)
    if not only:
        US = dbg["u"] if debug else K.dram("US", [24 * 128, TT], F32)
        SZ = K.dram("SZ", [8 * 128, TL], F32)
    SG = K.dram("SG", [32 * 128, TL], BF16)
    X1 = inp("X1_in", [TL, D]) if only == "ffn" else K.dram("X1", [TL, D], F32)

    with ExitStack() as top:
        ident_f = K.sb(top, [128, 128], F32, "ident_f")
        ident_b = K.sb(top, [128, 128], BF16, "ident_b")
        K.dma("sp", ident_f[:], ident_d[:, :], [], [ident_f])
        K.copy(ident_b[:], ident_f[:], [ident_f], [ident_b])
        modT = K.sb(top, [128, 96, 2], F32, "modT")
        s1 = K.sb(top, [128, 16, 2], F32, "s1")
        s2 = K.sb(top, [128, 16], F32, "s2")
        scopeO = ExitStack()
        abT = K.sb(scopeO, [128, TT], F32, "abT")
        oaT = K.sb(scopeO, [128, 8, TL], BF16, "oaT")
        scopeA = ExitStack()
        smallT = K.sb(scopeA, [128, 5, TT], BF16, "smallT")

        if only == "rwkv":
            K.dma("pool", smallT[:], small_in[:, :, :], [], [smallT])
            K.dma("sp", abT[:], ab_in[:, :], [], [abT])
        with ExitStack() as p0:
          if only != "rwkv":
                ccf = K.sb(p0, [128, 16, 2], F32)
                ccs = K.sb(p0, [128, 16, 2], F32)
                ccb = K.sb(p0, [128, 16, 2], BF16)
                bada = K.sb(p0, [128, 96], F32)
                n1w = K.sb(p0, [128, 16], F32)
                n2w = K.sb(p0, [128, 16], F32)
                K.dma("sp", ccf[:], cc_d[:, :, :], [], [ccf])
                K.dma("sp", bada[:], bada_d[:, :], [], [bada])
                K.dma("sp", n1w[:], n1w_d[:, :], [], [n1w])
                K.dma("sp", n2w[:], n2w_d[:, :], [], [n2w])
                K.act(ccs[:], ccf[:], AF.Silu, [ccf], [ccs])
                K.copy(ccb[:], ccs[:], [ccs], [ccb])
                wb = [[K.sb(p0, [128, 8, 512], BF16) for _ in range(2)] for _ in range(2)]
                pm = K.ps(p0, [128, 96, 2], F32)
                for sc in range(24):
                    w = wb[sc % 2]
                    for kh in range(2):
                        K.dma("pool", w[kh][:],
                              wada_d[kh * 1024:(kh + 1) * 1024, sc * 512:(sc + 1) * 512].rearrange("(k p) n -> p k n", p=128),
                              [], [w[kh]])
                    for jj in range(4):
                        j = sc * 4 + jj
                        for k in range(16):
                            K.mm(pm[:, j, :], w[k // 8][:, k % 8, jj * 128:(jj + 1) * 128], ccb[:, k, :], k == 0, k == 15,
                                 [w[k // 8], ccb], [pm])
                K.tt(modT[:], pm[:], bada[:].unsqueeze(2).to_broadcast([128, 96, 2]), ALU.add, [pm, bada], [modT])
                for v in range(2):
                    K.stt(s1[:, :, v], modT[:, 16:32, v], 1.0, n1w[:], ALU.add, ALU.mult, [modT, n1w], [s1])
                K.stt(s2[:], modT[:, 64:80, 0], 1.0, n2w[:], ALU.add, ALU.mult, [modT, n2w], [s2])
                if debug:
                    K.dma("sp", dbg["mod"][:, :], modT[:].rearrange("p a b -> p (a b)"), [modT], [])

        K.S.barrier()
        with ExitStack() as p1:
          if not only:
                hT = K.sb(p1, [128, 16, TT], BF16, "hT")
                mu = K.sb(p1, [128, 29], F32)
                omm = K.sb(p1, [128, 29], F32)
                cmask = K.sb(p1, [128, 8], F32)
                coef = K.sb(p1, [128, 6, 29], F32)
                convw = K.sb(p1, [128, 24, 5], F32)
                K.dma("sp", mu[:], mu_d[:, :], [], [mu])
                K.dma("sp", cmask[:], cmask_d[:, :], [], [cmask])
                K.dma("sp", convw[:], convw_d[:, :, :], [], [convw])
                K.ts(omm[:], mu[:], -1.0, 1.0, ALU.mult, ALU.add, [mu], [omm])
                for m in range(6):
                    K.ts(coef[:, m, :], mu[:], cmask[:, m:m + 1], None, ALU.mult, None, [mu, cmask], [coef])
                with ExitStack() as pa:
                    xt = [K.sb(pa, [128, D], F32) for _ in range(2)]
                    xb = [K.sb(pa, [128, D], BF16) for _ in range(2)]
                    junk = K.sb(pa, [128, D], BF16)
                    ss = [K.sb(pa, [128, 1], F32) for _ in range(2)]
                    rs = [K.sb(pa, [128, 1], F32) for _ in range(2)]
                    pt = [K.ps(pa, [128, 8, 128], BF16) for _ in range(2)]
                    npt = 0
                    for t in range(NCH):
                        X, XB, SS, RS = xt[t % 2], xb[t % 2], ss[t % 2], rs[t % 2]
                        src = ctx_d[t * 128:(t + 1) * 128, :] if t < 2 else x_d[(t - 2) * 128:(t - 1) * 128, :]
                        v = 1 if t < 2 else 0
                        K.dma("sp", X[:], src, [], [X])
                        K.act(junk[:], X[:], AF.Square, [X], [junk, SS], accum=SS[:])
                        K.act(RS[:], SS[:], AF.Sqrt, [SS], [RS], scale=1.0 / D, bias=EPS)
                        K.recip(RS[:], RS[:], [RS], [RS])
                        K.act(XB[:], X[:], AF.Copy, [X, RS], [XB], scale=RS[:])
                        for g in range(4):
                            P = pt[npt % 2]
                            npt += 1
                            for j in range(4):
                                k = g * 4 + j
                                K.tr(P[:, j, :], XB[:, k * 128:(k + 1) * 128], ident_b[:], [XB, ident_b], [P])
                            for j in range(4):
                                k = g * 4 + j
                                K.act(hT[:, k, t * 128:(t + 1) * 128], P[:, j, :], AF.Identity, [P, s1, modT], [hT],
                                      scale=s1[:, k, v:v + 1], bias=modT[:, k, v:v + 1])
                K.S.relax = RELAX[0]
                K.S.barrier()
                with ExitStack() as pb:
                    wb = [[K.sb(pb, [128, 8, 512], BF16) for _ in range(2)] for _ in range(2)]
                    stage = [K.sb(pb, [128, TT], F32) for _ in range(2)]
                    post = [K.sb(pb, [128, TT], F32) for _ in range(1)] * 2
                    postb = [K.sb(pb, [128, TL], BF16) for _ in range(1)] * 2
                    pp = [K.ps(pb, [128, 512], F32) for _ in range(4)]
                    npp = 0
                    ntile = [(0, 256)] + [(256 + i * 512, 512) for i in range(4)]
                    for sc in range(24):
                        ncol = min(512, NIN * 128 - sc * 512)
                        w = wb[sc % 2]
                        for kh in range(2):
                            K.dma("pool", w[kh][:, :, :ncol],
                                  win_d[kh * 1024:(kh + 1) * 1024, sc * 512:sc * 512 + ncol].rearrange("(k p) n -> p k n", p=128),
                                  [], [w[kh]])
                        for jj in range(ncol // 128):
                            q = sc * 4 + jj
                            lat_only = (53 <= q <= 60) or q >= 62
                            stg = stage[q % 2]
                            for (t0, tn) in ntile:
                                if lat_only and t0 == 0:
                                    continue
                                P = pp[npp % 4]
                                npp += 1
                                for k in range(16):
                                    K.mm(P[:, :tn], w[k // 8][:, k % 8, jj * 128:(jj + 1) * 128], hT[:, k, t0:t0 + tn],
                                         k == 0, k == 15, [w[k // 8], hT], [P])
                                if q <= 28 or 29 <= q <= 52 or q == 61:
                                    K.act(stg[:, t0:t0 + tn], P[:, :tn], AF.Copy, [P], [stg])
                                elif 53 <= q <= 60:
                                    K.act(stg[:, t0:t0 + tn], P[:, :tn], AF.Silu, [P], [stg])
                                else:
                                    K.act(postb[q % 2][:, t0 - 256:t0 - 256 + tn], P[:, :tn], AF.Sigmoid, [P], [postb[q % 2]])
                            if q <= 28:
                                xs = post[q % 2]
                                K.ts(xs[:], stg[:], omm[:, q:q + 1], None, ALU.mult, None, [stg, omm], [xs])
                                pl = stg[:, 256:TT].rearrange("p (r c) -> p r c", c=64)
                                xl = xs[:, 256:TT].rearrange("p (r c) -> p r c", c=64)
                                sh = [(xl[:, :, 1:64], pl[:, :, 0:63]), (xl[:, :, 0:63], pl[:, :, 1:64]),
                                      (xl[:, 1:32, :], pl[:, 0:31, :]), (xl[:, 0:31, :], pl[:, 1:32, :]),
                                      (xs[:, 1:256], stg[:, 0:255]), (xs[:, 0:255], stg[:, 1:256])]
                                for m, (o, i) in enumerate(sh):
                                    K.stt(o, i, coef[:, m, q:q + 1], o, ALU.mult, ALU.add, [stg, xs, coef], [xs])
                                if q < 24:
                                    K.dma("sp", XS[q * 128:(q + 1) * 128, :], xs[:], [xs], [XS])
                                elif q < 26:
                                    K.act(smallT[:, q - 24, :], xs[:], AF.Tanh, [xs], [smallT])
                                elif q < 28:
                                    K.act(smallT[:, q - 24, :], xs[:], AF.Copy, [xs], [smallT])
                                else:
                                    K.act(smallT[:, 4, :], xs[:], AF.Sigmoid, [xs], [smallT])
                            elif q <= 52:
                                g = q - 29
                                acc = post[q % 2]
                                K.ts(acc[:], stg[:], convw[:, g, 2:3], None, ALU.mult, None, [stg, convw], [acc])
                                for (a, b) in ((0, 256), (256, TT)):
                                    for j, o in ((0, 2), (1, 1), (3, -1), (4, -2)):
                                        if o > 0:
                                            ov, iv = acc[:, a + o:b], stg[:, a:b - o]
                                        else:
                                            ov, iv = acc[:, a:b + o], stg[:, a - o:b]
                                        K.stt(ov, iv, convw[:, g, j:j + 1], ov, ALU.mult, ALU.add, [stg, acc, convw], [acc])
                                K.act(acc[:], acc[:], AF.Silu, [acc], [acc])
                                K.dma("sp", US[g * 128:(g + 1) * 128, :], acc[:], [acc], [US])
                            elif q <= 60:
                                K.dma("sp", SZ[(q - 53) * 128:(q - 52) * 128, :], stg[:, 256:TT], [stg], [SZ])
                            elif q == 61:
                                K.copy(abT[:], stg[:], [stg], [abT], eng="pool")
                            else:
                                K.dma("sp", SG[(q - 62) * 128:(q - 61) * 128, :], postb[q % 2][:], [postb[q % 2]], [SG])
                K.S.barrier()
                if debug:
                    smf = K.sb(p1, [128, 5, TT], F32)
                    K.copy(smf[:], smallT[:], [smallT], [smf])
                    K.dma("sp", dbg["sm"][:, :, :], smf[:], [smf], [])

        K.S.relax = RELAX[3]
        K.S.barrier()
        with ExitStack() as p2:
          if only != "ffn":
            rwkv_phase(K, p2, dict(XS=XS, smallT=smallT, oaT=oaT, ident_b=ident_b, ident_f=ident_f,
                                   w0=w0_d, a0=a0_d, kkw=kkw_d, ka=ka_d, rk=rk_d, w2=w2_d, a2=a2_d, g2=g2_d,
                                   lnw=lnw_d, lnb=lnb_d, m1=m1_d, m2=m2_d, rmask=rmask_d, bones=bones_d, hsel=hsel_d, nmask=nmask_d,
                                   dbg=dbg, nhp=NHP))
        K.S.barrier()
        if debug:
            with ExitStack() as pd:
                of = K.sb(pd, [128, 8, TL], F32)
                K.copy(of[:, 0:NHP], oaT[:, 0:NHP], [oaT], [of])
                K.dma("sp", dbg["oa"][:, 0:NHP, :], of[:, 0:NHP], [of], [])
            K.S.barrier()
        scopeA.close()
        obT = K.sb(scopeO, [128, 8, TL], BF16, "obT")
        K.S.relax = RELAX[4]
        with ExitStack() as p2b:
          if only != "ffn":
            gdn_phase(K, p2b, dict(US=US, SZ=SZ, abT=abT, obT=obT, ident_b=ident_b, ident_f=ident_f, selg=selg_d, selb=selb_d,
                                   bigm=bigm_d, offd=offd_d, ones=ones_d, rmask=rmask_d, alog=alog_d, dtb=dtb_d, gnw=gnw_d,
                                   nmask=nmask_d, dbg=dbg, nh=NGH, GBS=GBS))
        K.S.barrier()
        if debug:
            with ExitStack() as pd:
                of = K.sb(pd, [128, 8, TL], F32)
                K.copy(of[:, 0:NGH], obT[:, 0:NGH], [obT], [of])
                K.dma("sp", dbg["ob"][:, 0:NGH, :], of[:, 0:NGH], [of], [])
            K.S.barrier()
        C34 = dict(oaT=oaT, obT=obT, ident_b=ident_b, ident_f=ident_f, modT=modT, s2=s2, SG=SG, X1=X1,
                   x=x_d, out=out_d, MODS=MODS, fnw=fnw_d, p_a=pa_d, p_b=pb_d, w_out=wout_d, w_gu=wgu_d,
                   w_dn=wdn_d, dbg=dbg)
        K.S.relax = RELAX[1]
        if only != "rwkv":
            write_mods(K, C34)
        if not only:
            merge_phase(K, scopeO, C34)
        scopeO.close()
        K.S.relax = RELAX[2]
        K.S.barrier()
        if only != "rwkv":
            ffn_phase(K, top, C34)
        else:
            with ExitStack() as pz:
                z = K.sb(pz, [128, D], F32)
                K.memset(z[:], 0.0, [z])
                K.dma("sp", out_d[0:128, :], z[:], [z], [out_d])
        K.S.emit(nc, top)
    nc._marks = getattr(K.S, "marks", [])
    return nc


def _fm(v, nchunk):
    return np.ascontiguousarray(np.asarray(v, np.float32).reshape(nchunk, 128).T)


def _pad_cols(a, n):
    out = np.zeros(a.shape[:-1] + (n,), np.float32)
    out[..., :a.shape[-1]] = a
    return out


def prep_shared(inputs):
    w_in = np.asarray(inputs["w_in"][0], np.float32)
    RW = 3520
    segs = [w_in[:, 0:3072]]
    for (a, b) in ((3072, 3168), (3168, 3264), (3264, 3360), (3360, 3456), (3456, 3520)):
        segs.append(_pad_cols(w_in[:, a:b], 128))
    segs.append(w_in[:, RW:RW + 3072 + 1024])
    abc = np.zeros((D, 128), np.float32)
    abc[:, 0:16] = w_in[:, 7616:7632]
    abc[:, 32:48] = w_in[:, 7632:7648]
    segs.append(abc)
    segs.append(w_in[:, 7648:])
    win = np.ascontiguousarray(np.concatenate(segs, axis=1))
    assert win.shape == (D, NIN * 128)
    mu = np.asarray(inputs["rw_mu"][0], np.float32)
    mus = [mu[0:3072]]
    for (a, b) in ((3072, 3168), (3168, 3264), (3264, 3360), (3360, 3456), (3456, 3520)):
        mus.append(_pad_cols(mu[a:b], 128))
    mu_fm = _fm(np.concatenate(mus), 29)
    p = np.arange(128)
    cmask = np.zeros((128, 8), np.float32)
    for m in range(4):
        cmask[:, m] = (p % 4 == m)
    cmask[:, 4] = (p % 2 == 0)
    cmask[:, 5] = (p % 2 == 1)
    convw = np.asarray(inputs["gdn_conv_w"][0], np.float32)
    convw_fm = np.ascontiguousarray(convw.reshape(5, 24, 128).transpose(2, 1, 0))
    sh = {
        "w_ada": np.ascontiguousarray(inputs["w_ada"][0], np.float32),
        "b_ada": _fm(inputs["b_ada"][0], 96),
        "norm1_w": _fm(inputs["norm1_w"][0], 16),
        "norm2_w": _fm(inputs["norm2_w"][0], 16),
        "final_norm_w": np.ascontiguousarray(np.asarray(inputs["final_norm_w"], np.float32).reshape(1, D)),
        "w_in": win,
        "rw_mu": mu_fm,
        "cmask": cmask,
        "gdn_conv_w": convw_fm,
        "ident": np.eye(128, dtype=np.float32),
    }
    g = lambda k: np.asarray(inputs[k][0], np.float32)
    sh["rw_w0"] = np.ascontiguousarray(g("rw_w0").reshape(2, 8, 128).transpose(2, 0, 1))
    sh["rw_a0"] = np.ascontiguousarray(g("rw_a0").reshape(2, 8, 128).transpose(2, 0, 1))
    sh["rw_k_k"] = _fm(g("rw_k_k"), 8)
    sh["rw_k_a"] = _fm(g("rw_k_a"), 8)
    sh["rw_r_k"] = _fm(g("rw_r_k").reshape(-1), 8)
    sh["rw_w2"] = np.ascontiguousarray(g("rw_w2"))
    sh["rw_a2"] = np.ascontiguousarray(g("rw_a2"))
    sh["rw_g2"] = np.ascontiguousarray(g("rw_g2"))
    sh["rw_ln_w"] = np.ascontiguousarray(g("rw_ln_w").reshape(1, 1024))
    sh["rw_ln_b"] = np.ascontiguousarray(g("rw_ln_b").reshape(1, 1024))
    r_ = np.arange(128)[:, None]; c_ = np.arange(128)[None, :]
    SL = (c_ < r_).astype(np.float32); SU = (c_ > r_).astype(np.float32)
    IL = (c_ <= r_).astype(np.float32); IU = (c_ >= r_).astype(np.float32)
    m1 = np.stack([np.stack([SL, SU, SL, SU], 0), np.stack([SU, SL, SU, SL], 0)], 0)
    m2 = np.stack([np.stack([SU, IU, -IU], 0), np.stack([SL, IL, -IL], 0)], 0)
    sh["m1"] = np.ascontiguousarray(m1.transpose(2, 0, 1, 3))
    sh["m2"] = np.ascontiguousarray(m2.transpose(2, 0, 1, 3))
    rmask = np.ones((128, 512), np.float32); rmask[:, ::128] = 0.0
    sh["rmask"] = rmask
    bo = np.zeros((128, 128), np.float32); bo[:64, :64] = 1.0; bo[64:, 64:] = 1.0
    sh["blockones"] = bo
    hs = np.zeros((128, 2), np.float32); hs[:64, 0] = 1.0; hs[64:, 1] = 1.0
    sh["headsel"] = hs
    nmk = np.zeros((2, 7, 128, 128), np.float32)
    for lv in range(7):
        bsz = 1 << lv
        low = ((r_ // (2 * bsz) == c_ // (2 * bsz)) & ((r_ // bsz) % 2 == 1) & ((c_ // bsz) % 2 == 0)).astype(np.float32)
        nmk[0, lv] = -low
        nmk[1, lv] = -low.T
    sh["nmask"] = np.ascontiguousarray(nmk.transpose(2, 0, 1, 3))
    selg = np.zeros((64, 16, 128), np.float32); selb = np.zeros((64, 16, 128), np.float32)
    for r0 in range(16):
        selg[r0, r0, :] = 1.0
        selb[32 + r0, r0, :] = 1.0
    sh["selg"] = selg; sh["selb"] = selb
    BIG = 1.0e4
    bigm = np.stack([np.stack([BIG * SU, -BIG * SL], 0), np.stack([BIG * SL, -BIG * SU], 0)], 0)
    sh["bigm"] = np.ascontiguousarray(bigm.transpose(2, 0, 1, 3))
    sh["offd"] = (1.0 - np.eye(128)).astype(np.float32)
    sh["ones"] = np.ones((128, 128), np.float32)
    al = np.zeros((64, 1), np.float32); al[0:16, 0] = g("gdn_a_log").reshape(-1)
    db = np.zeros((64, 1), np.float32); db[0:16, 0] = g("gdn_dt_bias").reshape(-1)
    sh["gdn_a_log"] = al; sh["gdn_dt_bias"] = db
    sh["gdn_norm_w"] = np.ascontiguousarray(g("gdn_norm_w").reshape(1, 128))
    for k_ in ("merge_p_a", "merge_p_b", "w_out", "ffn_w_gate_up", "ffn_w_down"):
        sh[k_] = np.ascontiguousarray(g(k_))
    return sh


def make_in_maps(inputs):
    sh = prep_shared(inputs)
    maps = []
    for b in range(8):
        m = dict(sh)
        m["x"] = np.ascontiguousarray(inputs["x"][b], np.float32)
        m["ctx"] = np.ascontiguousarray(inputs["ctx"][b], np.float32)
        cc = np.stack([np.asarray(inputs["c"][b], np.float32), np.asarray(inputs["c_ctx"], np.float32)], axis=-1)
        m["cc"] = np.ascontiguousarray(cc.reshape(16, 128, 2).transpose(1, 0, 2))
        maps.append(m)
    return maps


_NC = None


def kernel(**inputs):
    global _NC
    if _NC is None:
        _NC = build()
    maps = make_in_maps(inputs)
    res = run_bass_kernel_spmd(_NC, maps, core_ids=list(range(8)))
    return np.stack([r["out"] for r in res.results], axis=0).astype(np.float32)
```

```python
import numpy as np
from contextlib import ExitStack
import concourse.bass as bass
import concourse.mybir as mybir
from concourse.bass_utils import run_bass_kernel_spmd

F32 = mybir.dt.float32
BF16 = mybir.dt.bfloat16
AF = mybir.ActivationFunctionType
ALU = mybir.AluOpType
AX = mybir.AxisListType

COMPUTE = ("pe", "act", "dve", "pool")
WAW_RELAX = ("act", "dve", "pool")
NDSEM = 24

D = 2048
TC = 256
TL = 2048
TT = TC + TL
NCH = TT // 128
NIN = 94
FFN = 5632
EPS = 1e-6
DEC = 0.6065306597126334


class Buf:
    __slots__ = ("name", "lw", "rd")

    def __init__(self, name=""):
        self.name = name
        self.lw = None
        self.rd = {}


class Sched:
    def __init__(self):
        self.ops = []
        self.last = {}
        self.dmas = []
        self.bar = set()
        self.bar_seen = set()

    def pe_strict(self, on):
        if on:
            self._saved_relax = getattr(self, "relax", False)
            self.relax = False
        else:
            self.relax = self._saved_relax
            if self.relax and "pe" in self.last:
                self.pe_fence = self.last["pe"]

    def barrier(self):
        if not hasattr(self, "marks"):
            self.marks = []
        self.marks.append({e: sum(1 for o in self.ops if o[0] == e and not o[3]) for e in COMPUTE})
        self.bar = set(self.last.values()) | set(self.dmas)
        self.dmas = []
        self.bar_seen = set()

    def add(self, eng, fn, reads=(), writes=(), dma=False):
        i = len(self.ops)
        deps = set()
        if eng not in self.bar_seen:
            deps |= self.bar
            self.bar_seen.add(eng)
        self.last[eng] = i
        if dma:
            self.dmas.append(i)
        for b in reads:
            if b.lw is not None:
                deps.add(b.lw)
        for b in writes:
            cand = list(b.rd.values())
            if b.lw is not None:
                cand.append(b.lw)
            for c_ in cand:
                if (not dma) and self.ops[c_][0] == eng and not self.ops[c_][3] and eng in WAW_RELAX:
                    continue
                deps.add(c_)
        key = ("d", i) if dma else eng
        for b in reads:
            b.rd[key] = i
        for b in writes:
            b.lw = i
            b.rd = {}
        if eng == "pe" and getattr(self, "relax", False):
            deps = set(d for d in deps if not (self.ops[d][0] == "pe" and not self.ops[d][3]))
            if getattr(self, "pe_fence", None) is not None:
                deps.add(self.pe_fence)
                self.pe_fence = None
        self.ops.append((eng, fn, deps, dma))
        return i

    def emit(self, nc, stack):
        ops = self.ops
        engs = {"pe": nc.tensor, "act": nc.scalar, "dve": nc.vector, "pool": nc.gpsimd, "sp": nc.sync}
        names = list(engs)
        csem = {e: stack.enter_context(nc.semaphore("c_" + e)) for e in COMPUTE}
        dsem = {e: [stack.enter_context(nc.semaphore("d_%s%d" % (e, k))) for k in range(NDSEM)]
                for e in ("sp", "act", "pool")}
        comp = [None] * len(ops)
        cnt = {e: 0 for e in COMPUTE}
        dcnt = {e: 0 for e in dsem}
        prevslot = [None] * len(ops)
        for i, (eng, fn, deps, dma) in enumerate(ops):
            if dma:
                j = dcnt[eng]
                dcnt[eng] += 1
                comp[i] = (dsem[eng][j % NDSEM], 16 * (j // NDSEM + 1))
                if j >= NDSEM:
                    prevslot[i] = (dsem[eng][j % NDSEM], 16 * (j // NDSEM))
            else:
                cnt[eng] += 1
                comp[i] = (csem[eng], cnt[eng])
        per = {e: [] for e in names}
        for i, op in enumerate(ops):
            per[op[0]].append(i)
        block = stack.enter_context(nc.Block())

        def run(ename):
            def body(e):
                known = {}
                for i in per[ename]:
                    eng, fn, deps, dma = ops[i]
                    need = {}
                    cands = [comp[d] for d in deps]
                    if prevslot[i] is not None:
                        cands.append(prevslot[i])
                    for sm, v in cands:
                        k = id(sm)
                        if known.get(k, 0) >= v:
                            continue
                        if k not in need or need[k][1] < v:
                            need[k] = (sm, v)
                    for k, (sm, v) in need.items():
                        e.wait_ge(sm, v)
                        known[k] = v
                    ins = fn(e)
                    sm, v = comp[i]
                    ins.then_inc(sm, 16 if dma else 1)
                if ename in dsem:
                    last = {}
                    for i in per[ename]:
                        if ops[i][3]:
                            sm, v = comp[i]
                            last[id(sm)] = (sm, v)
                    for sm, v in last.values():
                        e.wait_ge(sm, v)
            return body

        block.tensor(run("pe"))
        block.scalar(run("act"))
        block.vector(run("dve"))
        block.gpsimd(run("pool"))
        block.sync(run("sp"))


class T:
    def __init__(self, h, name):
        self.h = h
        self.b = Buf(name)

    def __getitem__(self, k):
        return self.h[k]


def _bufs(xs):
    return [x if isinstance(x, Buf) else x.b for x in xs]


class KB:
    def __init__(self, nc):
        self.nc = nc
        self.S = Sched()
        self.n = 0

    def sb(self, st, shape, dt, name=None):
        self.n += 1
        name = name or "t%d" % self.n
        if not hasattr(self, "used"):
            self.used = set()
        while name in self.used:
            name = name + "_"
        self.used.add(name)
        return T(st.enter_context(self.nc.sbuf_tensor(name, list(shape), dt)), name)

    def ps(self, st, shape, dt, name=None):
        self.n += 1
        name = name or "p%d" % self.n
        return T(st.enter_context(self.nc.psum_tensor(name, list(shape), dt)), name)

    def dram(self, name, shape, dt, kind="Internal"):
        h = self.nc.dram_tensor(name, list(shape), dt, kind=kind)
        t = T(h.ap(), name)
        return t

    def act(self, out, in_, func, r, w, scale=1.0, bias=0.0, accum=None):
        kw = {}
        if accum is not None:
            kw["accum_out"] = accum
        self.S.add("act", lambda e: e.activation(out=out, in_=in_, func=func, scale=scale, bias=bias, **kw),
                   _bufs(r), _bufs(w))

    def tt(self, out, in0, in1, op, r, w, eng="dve"):
        self.S.add(eng, lambda e: e.tensor_tensor(out=out, in0=in0, in1=in1, op=op), _bufs(r), _bufs(w))

    def ts(self, out, in0, s1, s2, op0, op1, r, w, eng="dve", accum=None):
        kw = {}
        if accum is not None:
            kw["accum_out"] = accum
        if op1 is None:
            self.S.add(eng, lambda e: e.tensor_scalar(out=out, in0=in0, scalar1=s1, scalar2=None, op0=op0, **kw),
                       _bufs(r), _bufs(w))
        else:
            self.S.add(eng, lambda e: e.tensor_scalar(out=out, in0=in0, scalar1=s1, scalar2=s2, op0=op0, op1=op1, **kw),
                       _bufs(r), _bufs(w))

    def stt(self, out, in0, scalar, in1, op0, op1, r, w):
        self.S.add("dve", lambda e: e.scalar_tensor_tensor(out=out, in0=in0, scalar=scalar, in1=in1, op0=op0, op1=op1),
                   _bufs(r), _bufs(w))

    def copy(self, out, in_, r, w, eng="dve"):
        self.S.add(eng, lambda e: e.tensor_copy(out=out, in_=in_), _bufs(r), _bufs(w))

    def memset(self, out, val, w, eng="pool"):
        self.S.add(eng, lambda e: e.memset(out, val), [], _bufs(w))

    def recip(self, out, in_, r, w):
        self.S.add("dve", lambda e: e.reciprocal(out=out, in_=in_), _bufs(r), _bufs(w))

    def scan(self, out, d0, d1, init, op0, op1, r, w):
        self.S.add("dve", lambda e: e.tensor_tensor_scan(out=out, data0=d0, data1=d1, initial=init, op0=op0, op1=op1),
                   _bufs(r), _bufs(w))

    def mm(self, out, lhsT, rhs, start, stop, r, w):
        self.S.add("pe", lambda e: e.matmul(out, lhsT=lhsT, rhs=rhs, start=start, stop=stop), _bufs(r), _bufs(w))

    def tr(self, out, in_, ident, r, w):
        self.S.add("pe", lambda e: e.transpose(out=out, in_=in_, identity=ident), _bufs(r), _bufs(w))

    def dma(self, q, out, in_, r, w, **kw):
        self.S.add(q, lambda e: e.dma_start(out=out, in_=in_, **kw), _bufs(r), _bufs(w), dma=True)


def inverse_workspace(K, st, C):
    W = {}
    W["nmask"] = K.sb(st, [128, 2, 7, 128], BF16, "nmask_sb")
    K.dma("pool", W["nmask"][:], C["nmask"][:, :, :, :], [], [W["nmask"]])
    W["ident_b"] = C["ident_b"]
    W["sets"] = []
    for g in range(2):
        W["sets"].append({nm: K.sb(st, [128, 4, 128], BF16, "iw%d_%s" % (g, nm))
                          for nm in ("Xa", "Xb", "Ya", "Yb", "LsX", "LsY", "LsX2", "LsY2", "M1", "M2", "Mt", "R")})
    W["PI"] = [K.ps(st, [128, 4, 128], F32) for _ in range(2)]
    W["cnt"] = 0
    return W


def inverse_units(K, C, LL, n, d, XTb, W, nunits=None):
    LLf = LL[:].rearrange("p j f t -> p (j f) t")
    nm = W["nmask"]
    nunits = 2 * n if nunits is None else nunits
    gs = min(4, nunits)
    idb = W["ident_b"][:].unsqueeze(1).to_broadcast([128, gs, 128])
    bc = lambda m, lv: nm[:, m, lv, :].unsqueeze(1).to_broadcast([128, gs, 128])

    def pi():
        W["cnt"] += 1
        return W["PI"][W["cnt"] % 2]
    mx, my = (0, 1) if d == 0 else (1, 0)

    class V_:
        def __init__(s_, t):
            s_.t = t
            s_.b = t.b

        def __getitem__(s_, k):
            if k == slice(None):
                return s_.t[:, 0:gs, :]
            return s_.t[k]
    groups = []
    for gi, g0 in enumerate(range(0, nunits, gs)):
        S_ = W["sets"][gi % 2]
        st_ = {k_: V_(v_) for k_, v_ in S_.items()}
        st_["g0"] = g0
        st_["Lv"] = LLf[:, 2 * g0:2 * g0 + 2 * gs:2, :]
        st_["LTv"] = LLf[:, 2 * g0 + 1:2 * g0 + 2 * gs:2, :]
        st_["X"], st_["Xn"], st_["Y"], st_["Yn"] = st_["Xa"], st_["Xb"], st_["Ya"], st_["Yb"]
        groups.append(st_)
    for G in groups:
        K.tt(G["LsX"][:], G["Lv"], bc(mx, 0), ALU.mult, [LL, nm], [G["LsX"]], eng="pool")
        K.tt(G["X"][:], G["LsX"][:], idb, ALU.add, [G["LsX"], W["ident_b"]], [G["X"]], eng="pool")
        K.tt(G["LsY"][:], G["LTv"], bc(my, 0), ALU.mult, [LL, nm], [G["LsY"]], eng="pool")
        K.tt(G["Y"][:], G["LsY"][:], idb, ALU.add, [G["LsY"], W["ident_b"]], [G["Y"]], eng="pool")
        K.tt(G["Mt"][:], G["Lv"], idb, ALU.add, [LL, W["ident_b"]], [G["Mt"]], eng="pool")
    yield
    for lv in range(1, 7):
        sx, sy = ("LsX2", "LsY2") if lv % 2 else ("LsX", "LsY")
        for G in groups:
            K.tt(G[sx][:], G["Lv"], bc(mx, lv), ALU.mult, [LL, nm], [G[sx]], eng="pool")
            K.tt(G[sy][:], G["LTv"], bc(my, lv), ALU.mult, [LL, nm], [G[sy]], eng="pool")
        yield
        for G in groups:
            X, Y, LsX, LsY, M1, M2 = G["X"], G["Y"], G[sx], G[sy], G["M1"], G["M2"]
            Q = pi()
            for u in range(gs):
                K.mm(Q[:, u, :], LsY[:, u, :], X[:, u, :], True, True, [LsY, X], [Q])
            K.act(M1[:], Q[:, 0:gs, :], AF.Copy, [Q], [M1])
            Q = pi()
            for u in range(gs):
                K.mm(Q[:, u, :], LsX[:, u, :], Y[:, u, :], True, True, [LsX, Y], [Q])
            K.act(M2[:], Q[:, 0:gs, :], AF.Copy, [Q], [M2])
            yield
        for G in groups:
            X, Y, Xn, Yn, M1, M2 = G["X"], G["Y"], G["Xn"], G["Yn"], G["M1"], G["M2"]
            Q = pi()
            for u in range(gs):
                K.mm(Q[:, u, :], Y[:, u, :], M1[:, u, :], True, True, [Y, M1], [Q])
            K.tt(Xn[:], X[:], Q[:, 0:gs, :], ALU.add, [X, Q], [Xn])
            Q = pi()
            for u in range(gs):
                K.mm(Q[:, u, :], X[:, u, :], M2[:, u, :], True, True, [X, M2], [Q])
            K.tt(Yn[:], Y[:], Q[:, 0:gs, :], ALU.add, [Y, Q], [Yn])
            G["X"], G["Xn"], G["Y"], G["Yn"] = Xn, X, Yn, Y
            yield
    for G in groups:
        Q = pi()
        for u in range(gs):
            K.mm(Q[:, u, :], G["Mt"][:, u, :], G["Y"][:, u, :], True, True, [G["Mt"], G["Y"]], [Q])
        K.stt(G["R"][:], Q[:, 0:gs, :], -1.0, idb, ALU.mult, ALU.add, [Q, W["ident_b"]], [G["R"]])
    for G in groups:
        Q = pi()
        for u in range(gs):
            K.mm(Q[:, u, :], G["X"][:, u, :], G["R"][:, u, :], True, True, [G["X"], G["R"]], [Q])
        K.tt(XTb[:, G["g0"]:G["g0"] + gs, :], G["Y"][:], Q[:, 0:gs, :], ALU.add, [G["Y"], Q], [XTb])
    yield


SEGS = [(0, 2)] + [(2 + 4 * i, 4) for i in range(4)]


def rwkv_phase(K, st, C):
    XS, smallT, oaT, ident_b, ident_f = C["XS"], C["smallT"], C["oaT"], C["ident_b"], C["ident_f"]
    dbg = C["dbg"]
    sb = lambda shape, dt, name=None: K.sb(st, shape, dt, name)
    w0 = sb([128, 2, 8], F32); a0 = sb([128, 2, 8], F32)
    kkw = sb([128, 8], F32); ka = sb([128, 8], F32); omka = sb([128, 8], F32); rk = sb([128, 8], F32)
    for t_, d_ in ((w0, C["w0"]), (a0, C["a0"])):
        K.dma("sp", t_[:], d_[:, :, :], [], [t_])
    for t_, d_ in ((kkw, C["kkw"]), (ka, C["ka"]), (rk, C["rk"])):
        K.dma("sp", t_[:], d_[:, :], [], [t_])
    K.ts(omka[:], ka[:], -1.0, 1.0, ALU.mult, ALU.add, [ka], [omka])
    w2b = sb([128, 2, 1024], BF16); a2b = sb([128, 2, 1024], BF16); g2b = sb([64, 1024], BF16)
    K.memset(w2b[:], 0.0, [w2b])
    K.memset(a2b[:], 0.0, [a2b])
    K.dma("pool", w2b[0:96, :, :], C["w2"][:, :, :].rearrange("d r c -> r d c"), [], [w2b])
    K.dma("pool", a2b[0:96, :, :], C["a2"][:, :, :].rearrange("d r c -> r d c"), [], [a2b])
    K.dma("pool", g2b[:], C["g2"][:, :], [], [g2b])
    lnw = sb([128, 128], F32); lnb = sb([128, 128], F32)
    m1f = sb([128, 2, 4, 128], BF16); m2f = sb([128, 2, 3, 128], BF16)
    K.dma("pool", m1f[:], C["m1"][:, :, :, :], [], [m1f])
    K.dma("pool", m2f[:], C["m2"][:, :, :, :], [], [m2f])
    rmask = sb([128, 512], F32); bones = sb([128, 128], F32); hsel = sb([128, 2], F32)
    K.dma("sp", rmask[:], C["rmask"][:, :], [], [rmask])
    K.dma("sp", bones[:], C["bones"][:, :], [], [bones])
    K.dma("sp", hsel[:], C["hsel"][:, :], [], [hsel])
    f32t = lambda nm=None: sb([128, 512], F32, nm)
    bft = lambda nm=None: sb([128, 512], BF16, nm)
    Xr, Xk, Xv = f32t("Xr"), f32t("Xk"), f32t("Xv")
    sig, A, B, Cc, Dd = f32t("sig"), f32t("A"), f32t("B"), f32t("Cc"), f32t("Dd")
    e1, e2, e3, e4 = f32t("e1"), f32t("e2"), f32t("e3"), f32t("e4")
    icl, icl0, kq, sq, rn, kd, bd, tmp = (f32t(nm) for nm in ("icl", "icl0", "kq", "sq", "rn", "kd", "bd", "tmp"))
    kkt = kq
    gam = sb([128, 4], F32, "gam")
    rt, at, kt, bt, KH, BH, vb = (bft(nm) for nm in ("rt", "at", "kt", "bt", "KH", "BH", "vb"))
    KHt = sb([128, 4, 128], BF16, "KHt"); BHnt = sb([128, 4, 128], BF16, "BHnt"); Vt = sb([128, 4, 128], BF16, "Vt")
    LL = sb([128, 4, 4, 128], BF16, "LL")
    AA = sb([128, 4, 2, 3, 128], BF16, "AA")
    XTb = sb([128, 8, 128], BF16, "XTb")
    IW = inverse_workspace(K, st, C)
    Hf = sb([128, 128], F32, "Hf"); Hb = sb([128, 128], BF16, "Hb")
    P1s = sb([128, 128], BF16, "P1s"); Us = sb([128, 128], BF16, "Us")
    ybuf = sb([128, 16, 128], BF16, "ybuf")
    ytot = sb([128, 4, 128], F32, "ytot"); yc = sb([128, 4, 128], F32, "yc"); ysq = sb([128, 4, 128], F32)
    mean = sb([128, 8], F32); var = sb([128, 8], F32)
    bsum = sb([128, 4, 2], F32)
    oat = sb([128, 4, 128], BF16)
    PF = [K.ps(st, [128, 512], F32) for _ in range(1)]
    PTr = K.ps(st, [128, 8, 128], BF16)
    PG = [K.ps(st, [128, 4, 128], F32) for _ in range(2)]
    PSq = K.ps(st, [128, 512], F32)
    PSh = K.ps(st, [128, 512], F32)
    PS_P1, PS_U, PS_Y, PS_H = PSq, PSq, PSq, PSh
    cnt = {"pf": 0, "pg": 0, "pi": 0, "tr": 0}

    def nxt(lst, key):
        cnt[key] += 1
        return lst[cnt[key] % len(lst)]

    def transp(src, dst, n, scale=None):
        half = cnt["tr"] % 2
        cnt["tr"] += 1
        for j in range(n):
            K.tr(PTr[:, half * 4 + j, :], src[:, j * 128:(j + 1) * 128], ident_b[:], [src, ident_b], [PTr])
        if scale is None:
            K.copy(dst[:, :n, :], PTr[:, half * 4:half * 4 + n, :], [PTr], [dst])
        else:
            K.act(dst[:, :n, :], PTr[:, half * 4:half * 4 + n, :], AF.Copy, [PTr], [dst], scale=scale)

    Xs = [(Xr, Xk, Xv), (Xr, Xk, Xv)]
    rtP = [rt, bft("rt1")]; atP = [at, bft("at1")]; ktP = [kt, bft("kt1")]; btP = [bt, bft("bt1")]
    KHtP = [KHt, sb([128, 4, 128], BF16, "KHt1")]; BHntP = [BHnt, sb([128, 4, 128], BF16, "BHnt1")]
    VtP = [Vt, sb([128, 4, 128], BF16, "Vt1")]
    gamP = [gam, sb([128, 4], F32, "gam1")]
    AAP = [AA, sb([128, 4, 2, 3, 128], BF16, "AA1")]
    XTbP = [XTb, sb([128, 8, 128], BF16, "XTb1")]
    bsumP = [bsum, sb([128, 4, 2], F32, "bsum1")]
    items = []
    for hp in range(C["nhp"]):
        for d in range(2):
            order = SEGS if d == 0 else [SEGS[0], SEGS[4], SEGS[3], SEGS[2], SEGS[1]]
            for si, (c0, n) in enumerate(order):
                items.append((hp, d, c0, n, si == 0))

    def loads(i):
        hp, d, c0, n, first = items[i]
        t0, N = c0 * 128, n * 128
        for X_, row in zip(Xs[i % 2], (0, 1024, 2048)):
            K.dma("sp", X_[:, :N], XS[row + hp * 128:row + hp * 128 + 128, t0:t0 + N], [XS], [X_])

    def stepA(i):
        hp, d, c0, n, first = items[i]
        p = i % 2
        hc = slice(hp * 128, (hp + 1) * 128)
        t0, N = c0 * 128, n * 128
        latent = c0 >= 2
        tk = slice(t0, t0 + N)
        Xr, Xk, Xv = Xs[p]
        rt, at, kt, bt, KHt, BHnt, Vt, gam, bsum = rtP[p], atP[p], ktP[p], btP[p], KHtP[p], BHntP[p], VtP[p], gamP[p], bsumP[p]
        P = nxt(PF, "pf")
        K.mm(P[:, :N], w2b[:, d, hc], smallT[:, d, tk], True, True, [w2b, smallT], [P])
        K.act(sig[:, :N], P[:, :N], AF.Sigmoid, [P, w0], [sig], bias=w0[:, d, hp:hp + 1])
        K.scan(A[:, :N], rmask[:, :N], sig[:, :N], 0.0, ALU.mult, ALU.add, [rmask, sig], [A])
        P = nxt(PF, "pf")
        K.mm(P[:, :N], a2b[:, d, hc], smallT[:, 2 + d, tk], True, True, [a2b, smallT], [P])
        K.act(icl[:, :N], P[:, :N], AF.Sigmoid, [P, a0], [icl], bias=a0[:, d, hp:hp + 1])
        if d == 1 and latent:
            P = nxt(PF, "pf")
            K.mm(P[:, :N], a2b[:, 0, hc], smallT[:, 2, tk], True, True, [a2b, smallT], [P])
            K.act(icl0[:, :N], P[:, :N], AF.Sigmoid, [P, a0], [icl0], bias=a0[:, 0, hp:hp + 1])
        yield
        K.tt(B[:, :N], A[:, :N], sig[:, :N], ALU.subtract, [A, sig], [B])
        v3 = lambda t_: t_[:, :N].rearrange("p (c t) -> p c t", t=128)
        tot = v3(A)[:, :, 127:128]
        K.tt(v3(Cc), tot.to_broadcast([128, n, 128]), v3(A), ALU.subtract, [A], [Cc])
        K.tt(Dd[:, :N], Cc[:, :N], sig[:, :N], ALU.add, [Cc, sig], [Dd], eng="pool")
        yield
        Gi, Gx, Gt = (A, B, Cc) if d == 0 else (Dd, Cc, B)
        K.act(e1[:, :N], Gi[:, :N], AF.Exp, [Gi], [e1], scale=-DEC)
        K.act(e2[:, :N], Gx[:, :N], AF.Exp, [Gx], [e2], scale=-DEC)
        yield
        K.act(e3[:, :N], Gi[:, :N], AF.Exp, [Gi], [e3], scale=DEC)
        K.act(e4[:, :N], Gt[:, :N], AF.Exp, [Gt], [e4], scale=-DEC)
        K.act(gam[:, :n], v3(A)[:, :, 127], AF.Exp, [A], [gam], scale=-DEC)
        yield
        K.act(kq[:, :N], Xk[:, :N], AF.Copy, [Xk, kkw], [kq], scale=kkw[:, hp:hp + 1])
        K.act(sq[:, :N], kq[:, :N], AF.Square, [kq], [sq])
        P = nxt(PF, "pf")
        K.mm(P[:, :N], bones[:], sq[:, :N], True, True, [bones, sq], [P])
        K.act(rn[:, :N], P[:, :N], AF.Sqrt, [P], [rn], bias=EPS)
        K.recip(rn[:, :N], rn[:, :N], [rn], [rn])
        yield
        K.tt(kkt[:, :N], kq[:, :N], rn[:, :N], ALU.mult, [kq, rn], [kkt])
        K.ts(tmp[:, :N], icl[:, :N], ka[:, hp:hp + 1], omka[:, hp:hp + 1], ALU.mult, ALU.add, [icl, ka, omka], [tmp])
        K.tt(kd[:, :N], tmp[:, :N], Xk[:, :N], ALU.mult, [tmp, Xk], [kd])
        yield
        K.tt(bd[:, :N], kkt[:, :N], icl[:, :N], ALU.mult, [kkt, icl], [bd], eng="pool")
        K.tt(rt[:, :N], Xr[:, :N], e1[:, :N], ALU.mult, [Xr, e1], [rt])
        K.tt(at[:, :N], kkt[:, :N], e2[:, :N], ALU.mult, [kkt, e2], [at], eng="pool")
        yield
        K.tt(kt[:, :N], kd[:, :N], e3[:, :N], ALU.mult, [kd, e3], [kt])
        K.tt(bt[:, :N], bd[:, :N], e3[:, :N], ALU.mult, [bd, e3], [bt], eng="pool")
        K.tt(KH[:, :N], kd[:, :N], e4[:, :N], ALU.mult, [kd, e4], [KH])
        yield
        K.tt(BH[:, :N], bd[:, :N], e4[:, :N], ALU.mult, [bd, e4], [BH], eng="pool")
        K.act(vb[:, :N], Xv[:, :N], AF.Copy, [Xv], [vb])
        transp(KH, KHt, n)
        yield
        transp(BH, BHnt, n, scale=-1.0)
        transp(vb, Vt, n)
        yield
        if d == 1 and latent:
            K.tt(tmp[:, :N], icl[:, :N], icl0[:, :N], ALU.add, [icl, icl0], [tmp])
            yield
            K.ts(tmp[:, :N], tmp[:, :N], 0.5, None, ALU.mult, None, [tmp], [tmp])
            K.ts(tmp[:, :N], tmp[:, :N], ka[:, hp:hp + 1], omka[:, hp:hp + 1], ALU.mult, ALU.add, [tmp, ka, omka], [tmp])
            K.tt(tmp[:, :N], tmp[:, :N], Xk[:, :N], ALU.mult, [tmp, Xk], [tmp])
            yield
            K.stt(sq[:, :N], tmp[:, :N], rk[:, hp:hp + 1], Xr[:, :N], ALU.mult, ALU.mult, [tmp, rk, Xr], [sq])
            P = nxt(PF, "pf")
            for j in range(n):
                K.mm(P[:, 2 * j:2 * j + 2], sq[:, j * 128:(j + 1) * 128], hsel[:], True, True, [sq, hsel], [P])
            K.copy(bsum[:].rearrange("p j e -> p (j e)"), P[:, 0:2 * n], [P], [bsum])
            yield
        if i + 1 < len(items):
            loads(i + 1)
        yield

    def stepB(i):
        hp, d, c0, n, first = items[i]
        p = i % 2
        hc = slice(hp * 128, (hp + 1) * 128)
        t0, N = c0 * 128, n * 128
        latent = c0 >= 2
        rt, at, kt, bt, KHt, BHnt, Vt, gam, bsum = rtP[p], atP[p], ktP[p], btP[p], KHtP[p], BHntP[p], VtP[p], gamP[p], bsumP[p]
        AA, XTb = AAP[p], XTbP[p]
        for j in range(n):
            cs = slice(j * 128, (j + 1) * 128)
            K.S.pe_strict(True)
            G = nxt(PG, "pg")
            for e in range(2):
                ps_ = slice(64 * e, 64 * e + 64)
                K.mm(G[:, 2 * e, :], at[ps_, cs], bt[ps_, cs], True, True, [at, bt], [G])
                K.mm(G[:, 2 * e + 1, :], bt[ps_, cs], at[ps_, cs], True, True, [at, bt], [G])
            K.tt(LL[:, j, :, :], G[:], m1f[:, d, :, :], ALU.mult, [G, m1f], [LL])
            for e in range(2):
                ps_ = slice(64 * e, 64 * e + 64)
                G = nxt(PG, "pg")
                K.mm(G[:, 0, :], kt[ps_, cs], at[ps_, cs], True, True, [kt, at], [G])
                K.mm(G[:, 1, :], kt[ps_, cs], rt[ps_, cs], True, True, [kt, rt], [G])
                K.mm(G[:, 2, :], bt[ps_, cs], rt[ps_, cs], True, True, [bt, rt], [G])
                K.tt(AA[:, j, e, :, :], G[:, 0:3, :], m2f[:, d, :, :], ALU.mult, [G, m2f], [AA])
            K.S.pe_strict(False)
            yield
        for _ in inverse_units(K, C, LL, n, d, XTb, IW):
            yield

    def stepC(i):
        hp, d, c0, n, first = items[i]
        p = i % 2
        hc = slice(hp * 128, (hp + 1) * 128)
        t0, N = c0 * 128, n * 128
        latent = c0 >= 2
        rt, at, kt, bt, KHt, BHnt, Vt, gam, bsum = rtP[p], atP[p], ktP[p], btP[p], KHtP[p], BHntP[p], VtP[p], gamP[p], bsumP[p]
        AA, XTb = AAP[p], XTbP[p]
        if first:
            K.memset(Hf[:], 0.0, [Hf])
            K.memset(Hb[:], 0.0, [Hb])
            if d == 1:
                K.dma("sp", lnw[:], C["lnw"][0:1, hc].to_broadcast([128, 128]), [], [lnw])
                K.dma("sp", lnb[:], C["lnb"][0:1, hc].to_broadcast([128, 128]), [], [lnb])
        jl = list(range(n)) if d == 0 else list(range(n - 1, -1, -1))
        for j in jl:
            cs = slice(j * 128, (j + 1) * 128)
            K.mm(PS_P1[:, 0:128], at[:, cs], Hb[:], True, False, [at, Hb], [PS_P1])
            for e in range(2):
                vs = slice(64 * e, 64 * e + 64)
                K.mm(PS_P1[:, 64 * e:64 + 64 * e], AA[:, j, e, 0, :], Vt[:, j, vs], False, e == 1, [AA, Vt], [PS_P1])
            K.act(P1s[:], PS_P1[:, 0:128], AF.Copy, [PS_P1], [P1s])
            yield
            for e in range(2):
                vs = slice(64 * e, 64 * e + 64)
                K.mm(PS_U[:, 128 + 64 * e:192 + 64 * e], XTb[:, 2 * j + e, :], P1s[:, vs], True, True, [XTb, P1s], [PS_U])
            K.copy(Us[:], PS_U[:, 128:256], [PS_U], [Us])
            yield
            if latent:
                K.mm(PS_Y[:, 256:384], rt[:, cs], Hb[:], True, False, [rt, Hb], [PS_Y])
                for e in range(2):
                    vs = slice(64 * e, 64 * e + 64)
                    yo = PS_Y[:, 256 + 64 * e:320 + 64 * e]
                    K.mm(yo, AA[:, j, e, 1, :], Vt[:, j, vs], False, False, [AA, Vt], [PS_Y])
                    K.mm(yo, AA[:, j, e, 2, :], Us[:, vs], False, e == 1, [AA, Us], [PS_Y])
            K.mm(PS_H[:, 384:512], KHt[:, j, :], Vt[:, j, :], True, False, [KHt, Vt], [PS_H])
            K.mm(PS_H[:, 384:512], BHnt[:, j, :], Us[:], False, True, [BHnt, Us], [PS_H])
            for e in range(2):
                ps_ = slice(64 * e, 64 * e + 64)
                vs = slice(64 * e, 64 * e + 64)
                K.stt(Hf[ps_, vs], Hf[ps_, vs], gam[ps_, j:j + 1], PS_H[ps_, 384 + 64 * e:448 + 64 * e], ALU.mult, ALU.add,
                      [Hf, gam, PS_H], [Hf])
            K.act(Hb[:], Hf[:], AF.Copy, [Hf], [Hb])
            if latent:
                cg = c0 - 2 + j
                if d == 0:
                    K.act(ybuf[:, cg, :], PS_Y[:, 256:384], AF.Copy, [PS_Y], [ybuf])
                else:
                    K.tt(ytot[:, j, :], ybuf[:, cg, :], PS_Y[:, 256:384], ALU.add, [ybuf, PS_Y], [ytot])
            yield
        if d == 1 and latent:
            if dbg and hp < 8:
                K.dma("sp", dbg["yf"][hp, :, c0 - 2:c0 - 2 + n, :], ytot[:], [ytot], [])
            yv = ytot[:].rearrange("p j (e c) -> p (j e) c", c=64)
            ycv = yc[:].rearrange("p j (e c) -> p (j e) c", c=64)
            sqv = ysq[:].rearrange("p j (e c) -> p (j e) c", c=64)
            K.S.add("dve", lambda e_: e_.tensor_reduce(out=mean[:], in_=yv, axis=AX.X, op=ALU.add), _bufs([ytot]), _bufs([mean]))
            K.ts(mean[:], mean[:], 1.0 / 64, None, ALU.mult, None, [mean], [mean])
            K.tt(ycv, yv, mean[:].unsqueeze(2).to_broadcast([128, 8, 64]), ALU.subtract, [ytot, mean], [yc])
            K.tt(sqv, ycv, ycv, ALU.mult, [yc], [ysq], eng="pool")
            yield
            K.S.add("dve", lambda e_: e_.tensor_reduce(out=var[:], in_=sqv, axis=AX.X, op=ALU.add), _bufs([ysq]), _bufs([var]))
            K.act(var[:], var[:], AF.Sqrt, [var], [var], scale=1.0 / 64, bias=64e-5)
            K.recip(var[:], var[:], [var], [var])
            K.tt(ycv, ycv, var[:].unsqueeze(2).to_broadcast([128, 8, 64]), ALU.mult, [yc, var], [yc])
            yield
            K.tt(yc[:], yc[:], lnw[:].unsqueeze(1).to_broadcast([128, 4, 128]), ALU.mult, [yc, lnw], [yc])
            K.tt(yc[:], yc[:], lnb[:].unsqueeze(1).to_broadcast([128, 4, 128]), ALU.add, [yc, lnb], [yc])
            K.copy(ysq[:], Vt[:], [Vt], [ysq], eng="pool")
            K.tt(sqv, sqv, bsum[:].rearrange("p j e -> p (j e)").unsqueeze(2).to_broadcast([128, 8, 64]), ALU.mult,
                 [ysq, bsum], [ysq])
            K.tt(yc[:], yc[:], ysq[:], ALU.add, [yc, ysq], [yc])
            yield
            P = nxt(PF, "pf")
            for j in range(n):
                K.mm(P[:, j * 128:(j + 1) * 128], smallT[0:64, 4, t0 + j * 128:t0 + (j + 1) * 128], g2b[0:64, hc], True, True,
                     [smallT, g2b], [P])
            K.tt(oat[:], yc[:], P[:].rearrange("p (j c) -> p j c", c=128), ALU.mult, [yc, P], [oat])
            half = cnt["tr"] % 2
            cnt["tr"] += 1
            for j in range(n):
                K.tr(PTr[:, half * 4 + j, :], oat[:, j, :], ident_b[:], [oat, ident_b], [PTr])
            K.copy(oaT[:, hp, t0 - 256:t0 - 256 + N].rearrange("p (j t) -> p j t", t=128), PTr[:, half * 4:half * 4 + n, :],
                   [PTr], [oaT])
            yield

    def drain(g):
        for _ in g:
            pass

    def chain(*gs):
        for g in gs:
            for _ in g:
                yield

    loads(0)
    drain(stepA(0))
    drain(stepB(0))
    for i in range(len(items)):
        g1 = stepC(i)
        g2 = chain(stepA(i + 1), stepB(i + 1)) if i + 1 < len(items) else iter(())
        a1 = a2_ = True
        while a1 or a2_:
            if a1:
                try:
                    next(g1)
                except StopIteration:
                    a1 = False
            if a2_:
                for _ in range(RATIO):
                    try:
                        next(g2)
                    except StopIteration:
                        a2_ = False
                        break


def gdn_phase(K, st, C):
    US, SZ, abT, obT, ident_b, ident_f = C["US"], C["SZ"], C["abT"], C["obT"], C["ident_b"], C["ident_f"]
    sb = lambda shape, dt, name=None: K.sb(st, shape, dt, name)
    bigm = sb([128, 2, 2, 128], F32); offd = sb([128, 128], F32); ones = sb([128, 128], F32)
    K.dma("sp", bigm[:], C["bigm"][:, :, :, :], [], [bigm])
    K.dma("sp", offd[:], C["offd"][:, :], [], [offd])
    K.dma("sp", ones[:], C["ones"][:, :], [], [ones])
    rmask = sb([128, 512], F32)
    K.dma("sp", rmask[:], C["rmask"][:, :], [], [rmask])
    alog = sb([64, 1], F32); dtb = sb([64, 1], F32); nea = sb([64, 1], F32)
    K.dma("sp", alog[:], C["alog"][:, :], [], [alog])
    K.dma("sp", dtb[:], C["dtb"][:, :], [], [dtb])
    gnw = sb([128, 128], F32)
    K.dma("sp", gnw[:], C["gnw"][0:1, :].to_broadcast([128, 128]), [], [gnw])
    K.act(nea[:], alog[:], AF.Exp, [alog], [nea])
    K.ts(nea[:], nea[:], -1.0, None, ALU.mult, None, [nea], [nea])
    GB = [sb([64, TT], F32, "GB%d" % d) for d in range(2)]
    tokT = [sb([128, NCH, 64], F32, "tokT%d" % d) for d in range(2)]
    with ExitStack() as s0:
        gt = K.sb(s0, [16, TT], F32); A = K.sb(s0, [16, TT], F32); Bx = K.sb(s0, [16, TT], F32)
        K.act(gt[:], abT[0:16, :], AF.Exp, [abT, dtb], [gt], bias=dtb[0:16, :])
        K.act(gt[:], gt[:], AF.Ln, [gt], [gt], bias=1.0)
        K.ts(gt[:], gt[:], nea[0:16, :], None, ALU.mult, None, [gt, nea], [gt])
        for d in range(2):
            K.memset(GB[d][:], 0.0, [GB[d]])
            K.act(GB[d][32:48, :], abT[32:48, :], AF.Sigmoid, [abT], [GB[d]])
        for t0 in range(0, TT, 512):
            N = min(512, TT - t0)
            K.scan(A[:, t0:t0 + N], rmask[0:16, :N], gt[:, t0:t0 + N], 0.0, ALU.mult, ALU.add, [rmask, gt], [A])
        K.copy(GB[0][0:16, :], A[:], [A], [GB[0]], eng="pool")
        K.tt(Bx[:], A[:], gt[:], ALU.subtract, [A, gt], [Bx])
        v3 = lambda t_: t_[:].rearrange("p (c t) -> p c t", t=128)
        tot = v3(A)[:, :, 127:128]
        K.tt(v3(GB[1])[0:16], tot.to_broadcast([16, NCH, 128]), v3(Bx), ALU.subtract, [A, Bx], [GB[1]])
        ptk = K.ps(s0, [128, 8, 64], F32)
        for d in range(2):
            for c8 in range(0, NCH, 8):
                nn = min(8, NCH - c8)
                for j in range(nn):
                    c = c8 + j
                    K.tr(ptk[:, j, :], GB[d][:, c * 128:(c + 1) * 128], ident_f[0:64, 0:64], [GB[d], ident_f], [ptk])
                K.copy(tokT[d][:, c8:c8 + nn, :], ptk[:, 0:nn, :], [ptk], [tokT[d]])
    GBS = C["GBS"]
    for d in range(2):
        K.dma("sp", GBS[d, :, :], GB[d][:], [GB[d]], [GBS])
    K.S.barrier()
    f32t = lambda nm=None: sb([128, 512], F32, nm)
    bft = lambda nm=None: sb([128, 512], BF16, nm)
    _xq, _xk, _xv = f32t("gXq"), f32t("gXk"), f32t("gXv")
    XsP = [(_xq, _xk, _xv, f32t("gXG%d" % p), f32t("gXB%d" % p)) for p in range(2)]
    sq, rn, qn, kn, eG, tmp, sz = (f32t("g_" + nm) for nm in ("sq", "rn", "qn", "kn", "eG", "tmp", "sz"))
    Ktl, vb = bft("g_Ktl"), bft("g_vb")
    knbP, qnbP, kbTP, nKBGP, QdP = ([bft("g_%s%d" % (nm, p)) for p in range(2)] for nm in ("knb", "qnb", "kbT", "nKBG", "Qd"))
    KttP = [sb([128, 4, 128], BF16, "g_Ktt%d" % p) for p in range(2)]
    VtP = [sb([128, 4, 128], BF16, "g_Vt%d" % p) for p in range(2)]
    glP = [sb([128, 4], F32, "g_gl%d" % p) for p in range(2)]
    Dc = sb([128, 4, 2, 128], F32, "g_Dc"); DiT = sb([128, 4, 128], F32, "g_DiT"); Dtmp = sb([128, 4, 128], F32, "g_Dtmp")
    LLg = sb([128, 2, 4, 128], BF16, "g_LL")
    QKt = sb([128, 4, 128], BF16, "g_QKt")
    XTb = sb([128, 4, 128], BF16, "g_XTb")
    IW = inverse_workspace(K, st, C)
    Sf = sb([128, 128], F32, "g_Sf"); Sb = sb([128, 128], BF16, "g_Sb")
    P1s = sb([128, 128], BF16, "g_P1s"); VNs = sb([128, 128], BF16, "g_VNs")
    obuf = sb([128, 16, 128], F32, "g_obuf")
    otot = sb([128, 4, 128], F32, "g_otot"); osq = sb([128, 4, 128], F32, "g_osq")
    ss = sb([128, 4], F32); onb = sb([128, 4, 128], BF16, "g_onb")
    PF = K.ps(st, [128, 512], F32)
    PTr = K.ps(st, [128, 8, 128], BF16)
    PG = [K.ps(st, [128, 4, 128], F32) for _ in range(2)]
    PSq = K.ps(st, [128, 512], F32)
    PSh = K.ps(st, [128, 512], F32)
    cnt = {"pg": 0, "tr": 0}

    def transp(src, dst, n):
        half = cnt["tr"] % 2
        cnt["tr"] += 1
        for j in range(n):
            K.tr(PTr[:, half * 4 + j, :], src[:, j * 128:(j + 1) * 128], ident_b[:], [src, ident_b], [PTr])
        K.copy(dst[:, :n, :], PTr[:, half * 4:half * 4 + n, :], [PTr], [dst])

    items = []
    for h in range(C["nh"]):
        for d in range(2):
            order = SEGS if d == 0 else [SEGS[0], SEGS[4], SEGS[3], SEGS[2], SEGS[1]]
            for si, (c0, n) in enumerate(order):
                items.append((h, d, c0, n, si == 0))

    def loads(i):
        h, d, c0, n, first = items[i]
        r = d * 8 + h
        t0, N = c0 * 128, n * 128
        Xq, Xk, Xv, bcG, bcB = XsP[i % 2]
        for X_, row in ((Xq, 0), (Xk, 1024), (Xv, 2048)):
            K.dma("sp", X_[:, :N], US[row + h * 128:row + h * 128 + 128, t0:t0 + N], [US], [X_])
        K.dma("sp", bcG[:, :N], GBS[d, r:r + 1, t0:t0 + N].to_broadcast([128, N]), [GBS], [bcG])
        K.dma("sp", bcB[:, :N], GBS[d, 32 + r:33 + r, t0:t0 + N].to_broadcast([128, N]), [GBS], [bcB])

    def stepA(i):
        h, d, c0, n, first = items[i]
        p = i % 2
        t0, N = c0 * 128, n * 128
        Xq, Xk, Xv, bcG, bcB = XsP[p]
        knb, qnb, kbT, nKBG, Qd, Ktt, Vt, gl = knbP[p], qnbP[p], kbTP[p], nKBGP[p], QdP[p], KttP[p], VtP[p], glP[p]
        v3 = lambda t_: t_[:, :N].rearrange("p (c t) -> p c t", t=128)
        for X_, o_, sc_ in ((Xq, qn, 128 ** -0.5), (Xk, kn, 1.0)):
            K.act(sq[:, :N], X_[:, :N], AF.Square, [X_], [sq])
            K.mm(PF[:, :N], ones[:], sq[:, :N], True, True, [ones, sq], [PF])
            K.act(rn[:, :N], PF[:, :N], AF.Sqrt, [PF], [rn], bias=EPS)
            K.recip(rn[:, :N], rn[:, :N], [rn], [rn])
            K.stt(o_[:, :N], X_[:, :N], sc_, rn[:, :N], ALU.mult, ALU.mult, [X_, rn], [o_])
            yield
        K.act(eG[:, :N], bcG[:, :N], AF.Exp, [bcG], [eG])
        lastcol = 127 if d == 0 else 0
        K.copy(gl[:, :n], v3(eG)[:, :, lastcol], [eG], [gl], eng="pool")
        glast = v3(bcG)[:, :, lastcol:lastcol + 1]
        K.tt(kbT[:, :N], kn[:, :N], bcB[:, :N], ALU.mult, [kn, bcB], [kbT])
        yield
        K.copy(knb[:, :N], kn[:, :N], [kn], [knb], eng="pool")
        K.act(qnb[:, :N], qn[:, :N], AF.Copy, [qn], [qnb])
        K.stt(nKBG[:, :N], kbT[:, :N], -1.0, eG[:, :N], ALU.mult, ALU.mult, [kbT, eG], [nKBG])
        yield
        K.tt(Qd[:, :N], qn[:, :N], eG[:, :N], ALU.mult, [qn, eG], [Qd], eng="pool")
        K.tt(v3(tmp), glast.to_broadcast([128, n, 128]), v3(bcG), ALU.subtract, [bcG], [tmp])
        K.act(tmp[:, :N], tmp[:, :N], AF.Exp, [tmp], [tmp])
        yield
        K.tt(Ktl[:, :N], kn[:, :N], tmp[:, :N], ALU.mult, [kn, tmp], [Ktl])
        K.act(vb[:, :N], Xv[:, :N], AF.Copy, [Xv], [vb])
        transp(Ktl, Ktt, n)
        yield
        transp(vb, Vt, n)
        yield
        if i + 1 < len(items):
            loads(i + 1)
        yield

    def stepBC(i):
        h, d, c0, n, first = items[i]
        p = i % 2
        r = d * 8 + h
        t0, N = c0 * 128, n * 128
        latent = c0 >= 2
        Xq, Xk, Xv, bcG, bcB = XsP[p]
        knb, qnb, kbT, nKBG, Qd, Ktt, Vt, gl = knbP[p], qnbP[p], kbTP[p], nKBGP[p], QdP[p], KttP[p], VtP[p], glP[p]
        v3 = lambda t_: t_[:, :N].rearrange("p (c t) -> p c t", t=128)
        if first:
            K.memset(Sf[:], 0.0, [Sf])
            K.memset(Sb[:], 0.0, [Sb])
        gct = tokT[d][:, c0:c0 + n, r:r + 1].to_broadcast([128, n, 128])
        bg3 = v3(bcG)
        K.tt(Dtmp[:, :n, :], bg3, bigm[:, d, 0, :].unsqueeze(1).to_broadcast([128, n, 128]), ALU.add, [bcG, bigm], [Dtmp])
        K.tt(Dtmp[:, :n, :], Dtmp[:, :n, :], gct, ALU.subtract, [Dtmp, tokT[d]], [Dtmp], eng="pool")
        K.act(Dtmp[:, :n, :], Dtmp[:, :n, :], AF.Exp, [Dtmp], [Dtmp], scale=-1.0)
        K.tt(Dc[:, :n, 0, :], Dtmp[:, :n, :], offd[:].unsqueeze(1).to_broadcast([128, n, 128]), ALU.mult, [Dtmp, offd], [Dc],
             eng="pool")
        yield
        K.tt(DiT[:, :n, :], bg3, bigm[:, d, 1, :].unsqueeze(1).to_broadcast([128, n, 128]), ALU.add, [bcG, bigm], [DiT])
        K.tt(DiT[:, :n, :], DiT[:, :n, :], gct, ALU.subtract, [DiT, tokT[d]], [DiT], eng="pool")
        K.act(DiT[:, :n, :], DiT[:, :n, :], AF.Exp, [DiT], [DiT])
        K.tt(Dc[:, :n, 1, :], DiT[:, :n, :], offd[:].unsqueeze(1).to_broadcast([128, n, 128]), ALU.mult, [DiT, offd], [Dc],
             eng="pool")
        yield
        LLv = LLg[:].rearrange("p a b t -> p (a b) t")
        for j in range(n):
            cs = slice(j * 128, (j + 1) * 128)
            cnt["pg"] += 1
            G = PG[cnt["pg"] % 2]
            K.mm(G[:, 0, :], kbT[:, cs], knb[:, cs], True, True, [kbT, knb], [G])
            K.mm(G[:, 1, :], knb[:, cs], kbT[:, cs], True, True, [kbT, knb], [G])
            K.mm(G[:, 2, :], knb[:, cs], qnb[:, cs], True, True, [qnb, knb], [G])
            K.tt(LLv[:, 2 * j:2 * j + 2, :], G[:, 0:2, :], Dc[:, j, :, :], ALU.mult, [G, Dc], [LLg])
            K.tt(QKt[:, j, :], G[:, 2, :], DiT[:, j, :], ALU.mult, [G, DiT], [QKt])
            yield
        for _ in inverse_units(K, C, LLg, n, d, XTb, IW, nunits=n):
            yield
        jl = list(range(n)) if d == 0 else list(range(n - 1, -1, -1))
        for j in jl:
            cs = slice(j * 128, (j + 1) * 128)
            c = c0 + j
            K.mm(PSq[:, 0:128], nKBG[:, cs], Sb[:], True, True, [nKBG, Sb], [PSq])
            K.stt(P1s[:], Vt[:, j, :], tokT[d][:, c, 32 + r:33 + r], PSq[:, 0:128], ALU.mult, ALU.add,
                  [Vt, tokT[d], PSq], [P1s])
            yield
            K.mm(PSq[:, 128:256], XTb[:, j, :], P1s[:], True, True, [XTb, P1s], [PSq])
            K.act(VNs[:], PSq[:, 128:256], AF.Copy, [PSq], [VNs])
            yield
            if latent:
                K.mm(PSq[:, 256:384], Qd[:, cs], Sb[:], True, False, [Qd, Sb], [PSq])
                K.mm(PSq[:, 256:384], QKt[:, j, :], VNs[:], False, True, [QKt, VNs], [PSq])
            K.mm(PSh[:, 0:128], Ktt[:, j, :], VNs[:], True, True, [Ktt, VNs], [PSh])
            K.stt(Sf[:], Sf[:], gl[:, j:j + 1], PSh[:, 0:128], ALU.mult, ALU.add, [Sf, gl, PSh], [Sf])
            K.act(Sb[:], Sf[:], AF.Copy, [Sf], [Sb])
            if latent:
                cg = c - 2
                if d == 0:
                    K.act(obuf[:, cg, :], PSq[:, 256:384], AF.Copy, [PSq], [obuf])
                else:
                    K.tt(otot[:, j, :], obuf[:, cg, :], PSq[:, 256:384], ALU.add, [obuf, PSq], [otot])
            yield
        if d == 1 and latent:
            if C["dbg"]:
                K.dma("sp", C["dbg"]["of"][h, :, c0 - 2:c0 - 2 + n, :], otot[:], [otot], [])
            K.tt(osq[:], otot[:], otot[:], ALU.mult, [otot], [osq], eng="pool")
            K.S.add("dve", lambda e_: e_.tensor_reduce(out=ss[:], in_=osq[:], axis=AX.X, op=ALU.add), _bufs([osq]), _bufs([ss]))
            K.act(ss[:], ss[:], AF.Sqrt, [ss], [ss], scale=1.0 / 128, bias=EPS)
            K.recip(ss[:], ss[:], [ss], [ss])
            yield
            K.tt(osq[:], otot[:], ss[:].unsqueeze(2).to_broadcast([128, 4, 128]), ALU.mult, [otot, ss], [osq])
            K.tt(onb[:], osq[:], gnw[:].unsqueeze(1).to_broadcast([128, 4, 128]), ALU.mult, [osq, gnw], [onb])
            K.dma("sp", sz[:, :N], SZ[h * 128:(h + 1) * 128, t0 - 256:t0 - 256 + N], [SZ], [sz])
            yield
            half = cnt["tr"] % 2
            cnt["tr"] += 1
            for j in range(n):
                K.tr(PTr[:, half * 4 + j, :], onb[:, j, :], ident_b[:], [onb, ident_b], [PTr])
            K.tt(obT[:, h, t0 - 256:t0 - 256 + N].rearrange("p (j t) -> p j t", t=128), PTr[:, half * 4:half * 4 + n, :],
                 sz[:, :N].rearrange("p (j t) -> p j t", t=128), ALU.mult, [PTr, sz], [obT])
            yield

    loads(0)
    for _ in stepA(0):
        pass
    for i in range(len(items)):
        g1 = stepBC(i)
        g2 = stepA(i + 1) if i + 1 < len(items) else iter(())
        a1 = a2 = True
        while a1 or a2:
            if a1:
                for _ in range(RATIO_G):
                    try:
                        next(g1)
                    except StopIteration:
                        a1 = False
                        break
            if a2:
                try:
                    next(g2)
                except StopIteration:
                    a2 = False


def write_mods(K, C):
    modT, ident_f, MODS = C["modT"], C["ident_f"], C["MODS"]
    with ExitStack() as s0:
        pt = K.ps(s0, [16, 2, 128], F32)
        rows = K.sb(s0, [16, 2, 128], F32)
        for i, sec in enumerate((2, 5)):
            K.tr(pt[:, i, :], modT[:, sec * 16:(sec + 1) * 16, 0], ident_f[:], [modT, ident_f], [pt])
        K.copy(rows[:], pt[:], [pt], [rows])
        for i in range(2):
            K.dma("sp", MODS[i * 16:(i + 1) * 16, :], rows[:, i, :], [rows], [MODS])
    K.S.barrier()


def merge_phase(K, top, C):
    oaT, obT, ident_b, ident_f, modT, s2 = C["oaT"], C["obT"], C["ident_b"], C["ident_f"], C["modT"], C["s2"]
    SG, X1, x_d, out_d = C["SG"], C["X1"], C["x"], C["out"]
    MODS = C["MODS"]
    dbg = C["dbg"]
    bc = K.sb(top, [128, 1, D], F32, "bc_rows")
    K.dma("sp", bc[:, 0, :], MODS[0:16, :].rearrange("(o a) b -> o (a b)", o=1).to_broadcast([128, D]), [MODS], [bc])
    K.S.barrier()
    p3 = ExitStack()
    mT = K.sb(p3, [128, 16, TL], BF16, "mT")
    with ExitStack() as s1:
        wa = [K.sb(s1, [128, 8, 256], BF16) for _ in range(2)]
        wbb = [K.sb(s1, [128, 8, 256], BF16) for _ in range(2)]
        sga = [K.sb(s1, [128, TL], BF16)] * 2
        sgb = [K.sb(s1, [128, TL], BF16)] * 2
        t1 = [K.sb(s1, [128, 512], F32) for _ in range(2)]
        t2 = [K.sb(s1, [128, 512], F32) for _ in range(2)]
        pa = [K.ps(s1, [128, 512], F32) for _ in range(2)]
        pb = [K.ps(s1, [128, 512], F32) for _ in range(2)]
        it = 0
        for sc in range(8):
            K.dma("pool", wa[sc % 2][:], C["p_a"][:, sc * 256:(sc + 1) * 256].rearrange("(k p) n -> p k n", p=128), [], [wa[sc % 2]])
            K.dma("pool", wbb[sc % 2][:], C["p_b"][:, sc * 256:(sc + 1) * 256].rearrange("(k p) n -> p k n", p=128), [], [wbb[sc % 2]])
            for ff in range(2):
                f = sc * 2 + ff
                ga, gb = sga[f % 2], sgb[f % 2]
                K.dma("sp", ga[:], SG[f * 128:(f + 1) * 128, :], [SG], [ga])
                K.dma("sp", gb[:], SG[2048 + f * 128:2048 + (f + 1) * 128, :], [SG], [gb])
                for n in range(4):
                    ts_ = slice(n * 512, (n + 1) * 512)
                    A_, B_, T1, T2 = pa[it % 2], pb[it % 2], t1[it % 2], t2[it % 2]
                    it += 1
                    for k in range(8):
                        K.mm(A_[:], wa[sc % 2][:, k, ff * 128:(ff + 1) * 128], oaT[:, k, ts_], k == 0, k == 7, [wa[sc % 2], oaT], [A_])
                    for k in range(8):
                        K.mm(B_[:], wbb[sc % 2][:, k, ff * 128:(ff + 1) * 128], obT[:, k, ts_], k == 0, k == 7, [wbb[sc % 2], obT], [B_])
                    K.tt(T1[:], A_[:], ga[:, ts_], ALU.mult, [A_, ga], [T1])
                    K.tt(T2[:], B_[:], gb[:, ts_], ALU.mult, [B_, gb], [T2])
                    K.tt(mT[:, f, ts_], T1[:], T2[:], ALU.add, [T1, T2], [mT], eng="pool")
    K.S.barrier()
    with ExitStack() as s2_:
        wo = [[K.sb(s2_, [128, 8, 512], BF16) for _ in range(2)] for _ in range(2)]
        xt = [K.sb(s2_, [128, 512], F32) for _ in range(3)]
        tt_ = [K.sb(s2_, [128, 512], F32) for _ in range(3)]
        pp = [K.ps(s2_, [128, 512], F32) for _ in range(4)]
        it = 0
        for n in range(4):
            ns = slice(n * 512, (n + 1) * 512)
            w = wo[n % 2]
            for kh in range(2):
                K.dma("pool", w[kh][:], C["w_out"][kh * 1024:(kh + 1) * 1024, ns].rearrange("(k p) n -> p k n", p=128), [], [w[kh]])
            for t in range(16):
                P, X_, T_ = pp[it % 4], xt[it % 3], tt_[it % 3]
                it += 1
                K.dma("sp", X_[:], x_d[t * 128:(t + 1) * 128, ns], [], [X_])
                for k in range(16):
                    K.mm(P[:], mT[:, k, t * 128:(t + 1) * 128], w[k // 8][:, k % 8, :], k == 0, k == 15, [mT, w[k // 8]], [P])
                K.tt(T_[:], P[:], bc[:, 0, ns], ALU.mult, [P, bc], [T_])
                K.tt(T_[:], T_[:], X_[:], ALU.add, [T_, X_], [T_], eng="pool")
                K.dma("sp", X1[t * 128:(t + 1) * 128, ns], T_[:], [T_], [X1])
    p3.close()


def ffn_phase(K, top, C):
    ident_b, ident_f, modT, s2 = C["ident_b"], C["ident_f"], C["modT"], C["s2"]
    X1, out_d, MODS = C["X1"], C["out"], C["MODS"]
    G = 512
    with ExitStack() as s4:
        bc = K.sb(s4, [128, 3, D], F32, "bc_rows4")
        K.dma("sp", bc[:, 1, :], MODS[16:32, :].rearrange("(o a) b -> o (a b)", o=1).to_broadcast([128, D]), [MODS], [bc])
        K.dma("sp", bc[:, 2, :], C["fnw"][0:1, :].to_broadcast([128, D]), [], [bc])
        h2T = K.sb(s4, [128, 16, G], BF16, "h2T")
        actT = K.sb(s4, [128, 44, G], BF16, "actT")
        x1t = [K.sb(s4, [128, D], F32, "x1t%d" % i) for i in range(4)]
        xb = K.sb(s4, [128, D], BF16)
        junk = K.sb(s4, [128, D], BF16)
        ss = K.sb(s4, [128, 1], F32); rs = K.sb(s4, [128, 1], F32)
        wg = [[K.sb(s4, [128, 8, 256], BF16) for _ in range(2)] for _ in range(2)]
        wu = [[K.sb(s4, [128, 8, 256], BF16) for _ in range(2)] for _ in range(2)]
        wd = [K.sb(s4, [128, 4, 512], BF16) for _ in range(4)]
        sgt = [K.sb(s4, [128, G], F32) for _ in range(2)]
        tq = [K.sb(s4, [128, 512], F32) for _ in range(2)]
        ot = [K.sb(s4, [128, D], F32) for _ in range(2)]
        ptr = [K.ps(s4, [128, 8, 128], BF16) for _ in range(1)]
        pgu = [K.ps(s4, [128, 512], F32) for _ in range(3)]
        pdn = [K.ps(s4, [128, 512], F32) for _ in range(4)]
        igu = 0
        for grp in range(NGRP):
            for t in range(4):
                X_ = x1t[t]
                row0 = grp * G + t * 128
                K.dma("sp", X_[:], X1[row0:row0 + 128, :], [X1], [X_])
                K.act(junk[:], X_[:], AF.Square, [X_], [junk, ss], accum=ss[:])
                K.act(rs[:], ss[:], AF.Sqrt, [ss], [rs], scale=1.0 / D, bias=EPS)
                K.recip(rs[:], rs[:], [rs], [rs])
                K.act(xb[:], X_[:], AF.Copy, [X_, rs], [xb], scale=rs[:])
                for g in range(4):
                    P = ptr[0]
                    for j in range(4):
                        k = g * 4 + j
                        K.tr(P[:, j, :], xb[:, k * 128:(k + 1) * 128], ident_b[:], [xb, ident_b], [P])
                    for j in range(4):
                        k = g * 4 + j
                        K.act(h2T[:, k, t * 128:(t + 1) * 128], P[:, j, :], AF.Identity, [P, s2, modT], [h2T],
                              scale=s2[:, k:k + 1], bias=modT[:, 48 + k, 0:1])
            for sc in range(22):
                w1, w2 = wg[sc % 2], wu[sc % 2]
                for kh in range(2):
                    K.dma("pool", w1[kh][:], C["w_gu"][kh * 1024:(kh + 1) * 1024, sc * 256:(sc + 1) * 256].rearrange("(k p) n -> p k n", p=128),
                          [], [w1[kh]])
                    K.dma("pool", w2[kh][:], C["w_gu"][kh * 1024:(kh + 1) * 1024, FFN + sc * 256:FFN + (sc + 1) * 256].rearrange("(k p) n -> p k n", p=128),
                          [], [w2[kh]])
                for jj in range(2):
                    j = sc * 2 + jj
                    Pg, Pu = pgu[igu % 3], pgu[(igu + 1) % 3]
                    SGt = sgt[(igu // 2) % 2]
                    igu += 2
                    for k in range(16):
                        K.mm(Pg[:, :G], w1[k // 8][:, k % 8, jj * 128:(jj + 1) * 128], h2T[:, k, :], k == 0, k == 15, [w1[k // 8], h2T], [Pg])
                    for k in range(16):
                        K.mm(Pu[:, :G], w2[k // 8][:, k % 8, jj * 128:(jj + 1) * 128], h2T[:, k, :], k == 0, k == 15, [w2[k // 8], h2T], [Pu])
                    K.act(SGt[:], Pg[:, :G], AF.Silu, [Pg], [SGt])
                    K.tt(actT[:, j, :], SGt[:], Pu[:, :G], ALU.mult, [SGt, Pu], [actT])
            iw = 0
            for n in range(4):
                ns = slice(n * 512, (n + 1) * 512)
                for k4 in range(11):
                    W = wd[iw % 4]
                    iw += 1
                    K.dma("pool", W[:], C["w_dn"][k4 * 512:(k4 + 1) * 512, ns].rearrange("(k p) n -> p k n", p=128), [], [W])
                    for kk in range(4):
                        k = k4 * 4 + kk
                        for t in range(4):
                            K.mm(pdn[t][:], actT[:, k, t * 128:(t + 1) * 128], W[:, kk, :], k == 0, k == 43, [actT, W], [pdn[t]])
                for t in range(4):
                    T_ = tq[t % 2]
                    K.tt(T_[:], pdn[t][:], bc[:, 1, ns], ALU.mult, [pdn[t], bc], [T_])
                    K.tt(x1t[t][:, ns], x1t[t][:, ns], T_[:], ALU.add, [x1t[t], T_], [x1t[t]], eng="pool")
            for t in range(4):
                X_ = x1t[t]
                O_ = ot[t % 2]
                row0 = grp * G + t * 128
                K.act(junk[:], X_[:], AF.Square, [X_], [junk, ss], accum=ss[:])
                K.act(rs[:], ss[:], AF.Sqrt, [ss], [rs], scale=1.0 / D, bias=EPS)
                K.recip(rs[:], rs[:], [rs], [rs])
                K.stt(O_[:], X_[:], rs[:], bc[:, 2, :], ALU.mult, ALU.mult, [X_, rs, bc], [O_])
                K.dma("sp", out_d[row0:row0 + 128, :], O_[:], [O_], [out_d])

NHP = 8
NGH = 8
NGRP = 4
RELAX = [True, True, True, True, True]
RATIO = 3
RATIO_G = 8


def build(debug=False, only=None):
    nc = bass.Bass("TRN2", target_bir_lowering=False)
    K = KB(nc)
    inp = lambda name, shape: K.dram(name, shape, F32, kind="ExternalInput")
    x_d = inp("x", [TL, D])
    ctx_d = inp("ctx", [TC, D])
    cc_d = inp("cc", [128, 16, 2])
    wada_d = inp("w_ada", [D, 6 * D])
    bada_d = inp("b_ada", [128, 96])
    n1w_d = inp("norm1_w", [128, 16])
    n2w_d = inp("norm2_w", [128, 16])
    fnw_d = inp("final_norm_w", [1, D])
    win_d = inp("w_in", [D, NIN * 128])
    mu_d = inp("rw_mu", [128, 29])
    cmask_d = inp("cmask", [128, 8])
    convw_d = inp("gdn_conv_w", [128, 24, 5])
    ident_d = inp("ident", [128, 128])
    w0_d = inp("rw_w0", [128, 2, 8])
    a0_d = inp("rw_a0", [128, 2, 8])
    kkw_d = inp("rw_k_k", [128, 8])
    ka_d = inp("rw_k_a", [128, 8])
    rk_d = inp("rw_r_k", [128, 8])
    w2_d = inp("rw_w2", [2, 96, 1024])
    a2_d = inp("rw_a2", [2, 96, 1024])
    g2_d = inp("rw_g2", [64, 1024])
    lnw_d = inp("rw_ln_w", [1, 1024])
    lnb_d = inp("rw_ln_b", [1, 1024])
    m1_d = inp("m1", [128, 2, 4, 128])
    m2_d = inp("m2", [128, 2, 3, 128])
    rmask_d = inp("rmask", [128, 512])
    bones_d = inp("blockones", [128, 128])
    hsel_d = inp("headsel", [128, 2])
    nmask_d = inp("nmask", [128, 2, 7, 128])
    selg_d = inp("selg", [64, 16, 128])
    selb_d = inp("selb", [64, 16, 128])
    bigm_d = inp("bigm", [128, 2, 2, 128])
    offd_d = inp("offd", [128, 128])
    ones_d = inp("ones", [128, 128])
    alog_d = inp("gdn_a_log", [64, 1])
    dtb_d = inp("gdn_dt_bias", [64, 1])
    gnw_d = inp("gdn_norm_w", [1, 128])
    pa_d = inp("merge_p_a", [1024, D])
    pb_d = inp("merge_p_b", [1024, D])
    wout_d = inp("w_out", [D, D])
    wgu_d = inp("ffn_w_gate_up", [D, 2 * FFN])
    wdn_d = inp("ffn_w_down", [FFN, D])
    MODS = K.dram("MODS", [32, 128], F32)
    GBS = K.dram("GBS", [2, 64, TT], F32)
    out_d = K.dram("out", [TL, D], F32, kind="ExternalOutput")
    dbg = {}
    if debug:
        dbg["xs"] = K.dram("dbg_xs", [24 * 128, TT], F32, kind="ExternalOutput")
        dbg["u"] = K.dram("dbg_u", [24 * 128, TT], F32, kind="ExternalOutput")
        dbg["mod"] = K.dram("dbg_mod", [128, 96 * 2], F32, kind="ExternalOutput")
        dbg["sm"] = K.dram("dbg_sm", [128, 5, TT], F32, kind="ExternalOutput")
        dbg["oa"] = K.dram("dbg_oa", [128, 8, TL], F32, kind="ExternalOutput")
        dbg["yf"] = K.dram("dbg_yf", [8, 128, 16, 128], F32, kind="ExternalOutput")
        dbg["ob"] = K.dram("dbg_ob", [128, 8, TL], F32, kind="ExternalOutput")
        dbg["of"] = K.dram("dbg_of", [8, 128, 16, 128], F32, kind="ExternalOutput")
    if only:
        XS = inp("XS_in", [24 * 128, TT])
        small_in = inp("small_in", [128, 5, TT])
        US = inp("US_in", [24 * 128, TT])
        SZ = inp("SZ_in", [8 * 128, TL])
        ab_in = inp("ab_in", [128, TT])
    else:
        XS = dbg["xs"] if debug else K.dram("XS", [24 * 128, TT], F32)
    if not only:
        US = dbg["u"] if debug else K.dram("US", [24 * 128, TT], F32)
        SZ = K.dram("SZ", [8 * 128, TL], F32)
    SG = K.dram("SG", [32 * 128, TL], BF16)
    X1 = inp("X1_in", [TL, D]) if only == "ffn" else K.dram("X1", [TL, D], F32)

    with ExitStack() as top:
        ident_f = K.sb(top, [128, 128], F32, "ident_f")
        ident_b = K.sb(top, [128, 128], BF16, "ident_b")
        K.dma("sp", ident_f[:], ident_d[:, :], [], [ident_f])
        K.copy(ident_b[:], ident_f[:], [ident_f], [ident_b])
        modT = K.sb(top, [128, 96, 2], F32, "modT")
        s1 = K.sb(top, [128, 16, 2], F32, "s1")
        s2 = K.sb(top, [128, 16], F32, "s2")
        scopeO = ExitStack()
        abT = K.sb(scopeO, [128, TT], F32, "abT")
        oaT = K.sb(scopeO, [128, 8, TL], BF16, "oaT")
        scopeA = ExitStack()
        smallT = K.sb(scopeA, [128, 5, TT], BF16, "smallT")

        if only == "rwkv":
            K.dma("pool", smallT[:], small_in[:, :, :], [], [smallT])
            K.dma("sp", abT[:], ab_in[:, :], [], [abT])
        with ExitStack() as p0:
          if only != "rwkv":
                ccf = K.sb(p0, [128, 16, 2], F32)
                ccs = K.sb(p0, [128, 16, 2], F32)
                ccb = K.sb(p0, [128, 16, 2], BF16)
                bada = K.sb(p0, [128, 96], F32)
                n1w = K.sb(p0, [128, 16], F32)
                n2w = K.sb(p0, [128, 16], F32)
                K.dma("sp", ccf[:], cc_d[:, :, :], [], [ccf])
                K.dma("sp", bada[:], bada_d[:, :], [], [bada])
                K.dma("sp", n1w[:], n1w_d[:, :], [], [n1w])
                K.dma("sp", n2w[:], n2w_d[:, :], [], [n2w])
                K.act(ccs[:], ccf[:], AF.Silu, [ccf], [ccs])
                K.copy(ccb[:], ccs[:], [ccs], [ccb])
                wb = [[K.sb(p0, [128, 8, 512], BF16) for _ in range(2)] for _ in range(2)]
                pm = K.ps(p0, [128, 96, 2], F32)
                for sc in range(24):
                    w = wb[sc % 2]
                    for kh in range(2):
                        K.dma("pool", w[kh][:],
                              wada_d[kh * 1024:(kh + 1) * 1024, sc * 512:(sc + 1) * 512].rearrange("(k p) n -> p k n", p=128),
                              [], [w[kh]])
                    for jj in range(4):
                        j = sc * 4 + jj
                        for k in range(16):
                            K.mm(pm[:, j, :], w[k // 8][:, k % 8, jj * 128:(jj + 1) * 128], ccb[:, k, :], k == 0, k == 15,
                                 [w[k // 8], ccb], [pm])
                K.tt(modT[:], pm[:], bada[:].unsqueeze(2).to_broadcast([128, 96, 2]), ALU.add, [pm, bada], [modT])
                for v in range(2):
                    K.stt(s1[:, :, v], modT[:, 16:32, v], 1.0, n1w[:], ALU.add, ALU.mult, [modT, n1w], [s1])
                K.stt(s2[:], modT[:, 64:80, 0], 1.0, n2w[:], ALU.add, ALU.mult, [modT, n2w], [s2])
                if debug:
                    K.dma("sp", dbg["mod"][:, :], modT[:].rearrange("p a b -> p (a b)"), [modT], [])

        K.S.barrier()
        with ExitStack() as p1:
          if not only:
                hT = K.sb(p1, [128, 16, TT], BF16, "hT")
                mu = K.sb(p1, [128, 29], F32)
                omm = K.sb(p1, [128, 29], F32)
                cmask = K.sb(p1, [128, 8], F32)
                coef = K.sb(p1, [128, 6, 29], F32)
                convw = K.sb(p1, [128, 24, 5], F32)
                K.dma("sp", mu[:], mu_d[:, :], [], [mu])
                K.dma("sp", cmask[:], cmask_d[:, :], [], [cmask])
                K.dma("sp", convw[:], convw_d[:, :, :], [], [convw])
                K.ts(omm[:], mu[:], -1.0, 1.0, ALU.mult, ALU.add, [mu], [omm])
                for m in range(6):
                    K.ts(coef[:, m, :], mu[:], cmask[:, m:m + 1], None, ALU.mult, None, [mu, cmask], [coef])
                with ExitStack() as pa:
                    xt = [K.sb(pa, [128, D], F32) for _ in range(2)]
                    xb = [K.sb(pa, [128, D], BF16) for _ in range(2)]
                    junk = K.sb(pa, [128, D], BF16)
                    ss = [K.sb(pa, [128, 1], F32) for _ in range(2)]
                    rs = [K.sb(pa, [128, 1], F32) for _ in range(2)]
                    pt = [K.ps(pa, [128, 8, 128], BF16) for _ in range(2)]
                    npt = 0
                    for t in range(NCH):
                        X, XB, SS, RS = xt[t % 2], xb[t % 2], ss[t % 2], rs[t % 2]
                        src = ctx_d[t * 128:(t + 1) * 128, :] if t < 2 else x_d[(t - 2) * 128:(t - 1) * 128, :]
                        v = 1 if t < 2 else 0
                        K.dma("sp", X[:], src, [], [X])
                        K.act(junk[:], X[:], AF.Square, [X], [junk, SS], accum=SS[:])
                        K.act(RS[:], SS[:], AF.Sqrt, [SS], [RS], scale=1.0 / D, bias=EPS)
                        K.recip(RS[:], RS[:], [RS], [RS])
                        K.act(XB[:], X[:], AF.Copy, [X, RS], [XB], scale=RS[:])
                        for g in range(4):
                            P = pt[npt % 2]
                            npt += 1
                            for j in range(4):
                                k = g * 4 + j
                                K.tr(P[:, j, :], XB[:, k * 128:(k + 1) * 128], ident_b[:], [XB, ident_b], [P])
                            for j in range(4):
                                k = g * 4 + j
                                K.act(hT[:, k, t * 128:(t + 1) * 128], P[:, j, :], AF.Identity, [P, s1, modT], [hT],
                                      scale=s1[:, k, v:v + 1], bias=modT[:, k, v:v + 1])
                K.S.relax = RELAX[0]
                K.S.barrier()
                with ExitStack() as pb:
                    wb = [[K.sb(pb, [128, 8, 512], BF16) for _ in range(2)] for _ in range(2)]
                    stage = [K.sb(pb, [128, TT], F32) for _ in range(2)]
                    post = [K.sb(pb, [128, TT], F32) for _ in range(1)] * 2
                    postb = [K.sb(pb, [128, TL], BF16) for _ in range(1)] * 2
                    pp = [K.ps(pb, [128, 512], F32) for _ in range(4)]
                    npp = 0
                    ntile = [(0, 256)] + [(256 + i * 512, 512) for i in range(4)]
                    for sc in range(24):
                        ncol = min(512, NIN * 128 - sc * 512)
                        w = wb[sc % 2]
                        for kh in range(2):
                            K.dma("pool", w[kh][:, :, :ncol],
                                  win_d[kh * 1024:(kh + 1) * 1024, sc * 512:sc * 512 + ncol].rearrange("(k p) n -> p k n", p=128),
                                  [], [w[kh]])
                        for jj in range(ncol // 128):
                            q = sc * 4 + jj
                            lat_only = (53 <= q <= 60) or q >= 62
                            stg = stage[q % 2]
                            for (t0, tn) in ntile:
                                if lat_only and t0 == 0:
                                    continue
                                P = pp[npp % 4]
                                npp += 1
                                for k in range(16):
                                    K.mm(P[:, :tn], w[k // 8][:, k % 8, jj * 128:(jj + 1) * 128], hT[:, k, t0:t0 + tn],
                                         k == 0, k == 15, [w[k // 8], hT], [P])
                                if q <= 28 or 29 <= q <= 52 or q == 61:
                                    K.act(stg[:, t0:t0 + tn], P[:, :tn], AF.Copy, [P], [stg])
                                elif 53 <= q <= 60:
                                    K.act(stg[:, t0:t0 + tn], P[:, :tn], AF.Silu, [P], [stg])
                                else:
                                    K.act(postb[q % 2][:, t0 - 256:t0 - 256 + tn], P[:, :tn], AF.Sigmoid, [P], [postb[q % 2]])
                            if q <= 28:
                                xs = post[q % 2]
                                K.ts(xs[:], stg[:], omm[:, q:q + 1], None, ALU.mult, None, [stg, omm], [xs])
                                pl = stg[:, 256:TT].rearrange("p (r c) -> p r c", c=64)
                                xl = xs[:, 256:TT].rearrange("p (r c) -> p r c", c=64)
                                sh = [(xl[:, :, 1:64], pl[:, :, 0:63]), (xl[:, :, 0:63], pl[:, :, 1:64]),
                                      (xl[:, 1:32, :], pl[:, 0:31, :]), (xl[:, 0:31, :], pl[:, 1:32, :]),
                                      (xs[:, 1:256], stg[:, 0:255]), (xs[:, 0:255], stg[:, 1:256])]
                                for m, (o, i) in enumerate(sh):
                                    K.stt(o, i, coef[:, m, q:q + 1], o, ALU.mult, ALU.add, [stg, xs, coef], [xs])
                                if q < 24:
                                    K.dma("sp", XS[q * 128:(q + 1) * 128, :], xs[:], [xs], [XS])
                                elif q < 26:
                                    K.act(smallT[:, q - 24, :], xs[:], AF.Tanh, [xs], [smallT])
                                elif q < 28:
                                    K.act(smallT[:, q - 24, :], xs[:], AF.Copy, [xs], [smallT])
                                else:
                                    K.act(smallT[:, 4, :], xs[:], AF.Sigmoid, [xs], [smallT])
                            elif q <= 52:
                                g = q - 29
                                acc = post[q % 2]
                                K.ts(acc[:], stg[:], convw[:, g, 2:3], None, ALU.mult, None, [stg, convw], [acc])
                                for (a, b) in ((0, 256), (256, TT)):
                                    for j, o in ((0, 2), (1, 1), (3, -1), (4, -2)):
                                        if o > 0:
                                            ov, iv = acc[:, a + o:b], stg[:, a:b - o]
                                        else:
                                            ov, iv = acc[:, a:b + o], stg[:, a - o:b]
                                        K.stt(ov, iv, convw[:, g, j:j + 1], ov, ALU.mult, ALU.add, [stg, acc, convw], [acc])
                                K.act(acc[:], acc[:], AF.Silu, [acc], [acc])
                                K.dma("sp", US[g * 128:(g + 1) * 128, :], acc[:], [acc], [US])
                            elif q <= 60:
                                K.dma("sp", SZ[(q - 53) * 128:(q - 52) * 128, :], stg[:, 256:TT], [stg], [SZ])
                            elif q == 61:
                                K.copy(abT[:], stg[:], [stg], [abT], eng="pool")
                            else:
                                K.dma("sp", SG[(q - 62) * 128:(q - 61) * 128, :], postb[q % 2][:], [postb[q % 2]], [SG])
                K.S.barrier()
                if debug:
                    smf = K.sb(p1, [128, 5, TT], F32)
                    K.copy(smf[:], smallT[:], [smallT], [smf])
                    K.dma("sp", dbg["sm"][:, :, :], smf[:], [smf], [])

        K.S.relax = RELAX[3]
        K.S.barrier()
        with ExitStack() as p2:
          if only != "ffn":
            rwkv_phase(K, p2, dict(XS=XS, smallT=smallT, oaT=oaT, ident_b=ident_b, ident_f=ident_f,
                                   w0=w0_d, a0=a0_d, kkw=kkw_d, ka=ka_d, rk=rk_d, w2=w2_d, a2=a2_d, g2=g2_d,
                                   lnw=lnw_d, lnb=lnb_d, m1=m1_d, m2=m2_d, rmask=rmask_d, bones=bones_d, hsel=hsel_d, nmask=nmask_d,
                                   dbg=dbg, nhp=NHP))
        K.S.barrier()
        if debug:
            with ExitStack() as pd:
                of = K.sb(pd, [128, 8, TL], F32)
                K.copy(of[:, 0:NHP], oaT[:, 0:NHP], [oaT], [of])
                K.dma("sp", dbg["oa"][:, 0:NHP, :], of[:, 0:NHP], [of], [])
            K.S.barrier()
        scopeA.close()
        obT = K.sb(scopeO, [128, 8, TL], BF16, "obT")
        K.S.relax = RELAX[4]
        with ExitStack() as p2b:
          if only != "ffn":
            gdn_phase(K, p2b, dict(US=US, SZ=SZ, abT=abT, obT=obT, ident_b=ident_b, ident_f=ident_f, selg=selg_d, selb=selb_d,
                                   bigm=bigm_d, offd=offd_d, ones=ones_d, rmask=rmask_d, alog=alog_d, dtb=dtb_d, gnw=gnw_d,
                                   nmask=nmask_d, dbg=dbg, nh=NGH, GBS=GBS))
        K.S.barrier()
        if debug:
            with ExitStack() as pd:
                of = K.sb(pd, [128, 8, TL], F32)
                K.copy(of[:, 0:NGH], obT[:, 0:NGH], [obT], [of])
                K.dma("sp", dbg["ob"][:, 0:NGH, :], of[:, 0:NGH], [of], [])
            K.S.barrier()
        C34 = dict(oaT=oaT, obT=obT, ident_b=ident_b, ident_f=ident_f, modT=modT, s2=s2, SG=SG, X1=X1,
                   x=x_d, out=out_d, MODS=MODS, fnw=fnw_d, p_a=pa_d, p_b=pb_d, w_out=wout_d, w_gu=wgu_d,
                   w_dn=wdn_d, dbg=dbg)
        K.S.relax = RELAX[1]
        if only != "rwkv":
            write_mods(K, C34)
        if not only:
            merge_phase(K, scopeO, C34)
        scopeO.close()
        K.S.relax = RELAX[2]
        K.S.barrier()
        if only != "rwkv":
            ffn_phase(K, top, C34)
        else:
            with ExitStack() as pz:
                z = K.sb(pz, [128, D], F32)
                K.memset(z[:], 0.0, [z])
                K.dma("sp", out_d[0:128, :], z[:], [z], [out_d])
        K.S.emit(nc, top)
    nc._marks = getattr(K.S, "marks", [])
    return nc


def _fm(v, nchunk):
    return np.ascontiguousarray(np.asarray(v, np.float32).reshape(nchunk, 128).T)


def _pad_cols(a, n):
    out = np.zeros(a.shape[:-1] + (n,), np.float32)
    out[..., :a.shape[-1]] = a
    return out


def prep_shared(inputs):
    w_in = np.asarray(inputs["w_in"][0], np.float32)
    RW = 3520
    segs = [w_in[:, 0:3072]]
    for (a, b) in ((3072, 3168), (3168, 3264), (3264, 3360), (3360, 3456), (3456, 3520)):
        segs.append(_pad_cols(w_in[:, a:b], 128))
    segs.append(w_in[:, RW:RW + 3072 + 1024])
    abc = np.zeros((D, 128), np.float32)
    abc[:, 0:16] = w_in[:, 7616:7632]
    abc[:, 32:48] = w_in[:, 7632:7648]
    segs.append(abc)
    segs.append(w_in[:, 7648:])
    win = np.ascontiguousarray(np.concatenate(segs, axis=1))
    assert win.shape == (D, NIN * 128)
    mu = np.asarray(inputs["rw_mu"][0], np.float32)
    mus = [mu[0:3072]]
    for (a, b) in ((3072, 3168), (3168, 3264), (3264, 3360), (3360, 3456), (3456, 3520)):
        mus.append(_pad_cols(mu[a:b], 128))
    mu_fm = _fm(np.concatenate(mus), 29)
    p = np.arange(128)
    cmask = np.zeros((128, 8), np.float32)
    for m in range(4):
        cmask[:, m] = (p % 4 == m)
    cmask[:, 4] = (p % 2 == 0)
    cmask[:, 5] = (p % 2 == 1)
    convw = np.asarray(inputs["gdn_conv_w"][0], np.float32)
    convw_fm = np.ascontiguousarray(convw.reshape(5, 24, 128).transpose(2, 1, 0))
    sh = {
        "w_ada": np.ascontiguousarray(inputs["w_ada"][0], np.float32),
        "b_ada": _fm(inputs["b_ada"][0], 96),
        "norm1_w": _fm(inputs["norm1_w"][0], 16),
        "norm2_w": _fm(inputs["norm2_w"][0], 16),
        "final_norm_w": np.ascontiguousarray(np.asarray(inputs["final_norm_w"], np.float32).reshape(1, D)),
        "w_in": win,
        "rw_mu": mu_fm,
        "cmask": cmask,
        "gdn_conv_w": convw_fm,
        "ident": np.eye(128, dtype=np.float32),
    }
    g = lambda k: np.asarray(inputs[k][0], np.float32)
    sh["rw_w0"] = np.ascontiguousarray(g("rw_w0").reshape(2, 8, 128).transpose(2, 0, 1))
    sh["rw_a0"] = np.ascontiguousarray(g("rw_a0").reshape(2, 8, 128).transpose(2, 0, 1))
    sh["rw_k_k"] = _fm(g("rw_k_k"), 8)
    sh["rw_k_a"] = _fm(g("rw_k_a"), 8)
    sh["rw_r_k"] = _fm(g("rw_r_k").reshape(-1), 8)
    sh["rw_w2"] = np.ascontiguousarray(g("rw_w2"))
    sh["rw_a2"] = np.ascontiguousarray(g("rw_a2"))
    sh["rw_g2"] = np.ascontiguousarray(g("rw_g2"))
    sh["rw_ln_w"] = np.ascontiguousarray(g("rw_ln_w").reshape(1, 1024))
    sh["rw_ln_b"] = np.ascontiguousarray(g("rw_ln_b").reshape(1, 1024))
    r_ = np.arange(128)[:, None]; c_ = np.arange(128)[None, :]
    SL = (c_ < r_).astype(np.float32); SU = (c_ > r_).astype(np.float32)
    IL = (c_ <= r_).astype(np.float32); IU = (c_ >= r_).astype(np.float32)
    m1 = np.stack([np.stack([SL, SU, SL, SU], 0), np.stack([SU, SL, SU, SL], 0)], 0)
    m2 = np.stack([np.stack([SU, IU, -IU], 0), np.stack([SL, IL, -IL], 0)], 0)
    sh["m1"] = np.ascontiguousarray(m1.transpose(2, 0, 1, 3))
    sh["m2"] = np.ascontiguousarray(m2.transpose(2, 0, 1, 3))
    rmask = np.ones((128, 512), np.float32); rmask[:, ::128] = 0.0
    sh["rmask"] = rmask
    bo = np.zeros((128, 128), np.float32); bo[:64, :64] = 1.0; bo[64:, 64:] = 1.0
    sh["blockones"] = bo
    hs = np.zeros((128, 2), np.float32); hs[:64, 0] = 1.0; hs[64:, 1] = 1.0
    sh["headsel"] = hs
    nmk = np.zeros((2, 7, 128, 128), np.float32)
    for lv in range(7):
        bsz = 1 << lv
        low = ((r_ // (2 * bsz) == c_ // (2 * bsz)) & ((r_ // bsz) % 2 == 1) & ((c_ // bsz) % 2 == 0)).astype(np.float32)
        nmk[0, lv] = -low
        nmk[1, lv] = -low.T
    sh["nmask"] = np.ascontiguousarray(nmk.transpose(2, 0, 1, 3))
    selg = np.zeros((64, 16, 128), np.float32); selb = np.zeros((64, 16, 128), np.float32)
    for r0 in range(16):
        selg[r0, r0, :] = 1.0
        selb[32 + r0, r0, :] = 1.0
    sh["selg"] = selg; sh["selb"] = selb
    BIG = 1.0e4
    bigm = np.stack([np.stack([BIG * SU, -BIG * SL], 0), np.stack([BIG * SL, -BIG * SU], 0)], 0)
    sh["bigm"] = np.ascontiguousarray(bigm.transpose(2, 0, 1, 3))
    sh["offd"] = (1.0 - np.eye(128)).astype(np.float32)
    sh["ones"] = np.ones((128, 128), np.float32)
    al = np.zeros((64, 1), np.float32); al[0:16, 0] = g("gdn_a_log").reshape(-1)
    db = np.zeros((64, 1), np.float32); db[0:16, 0] = g("gdn_dt_bias").reshape(-1)
    sh["gdn_a_log"] = al; sh["gdn_dt_bias"] = db
    sh["gdn_norm_w"] = np.ascontiguousarray(g("gdn_norm_w").reshape(1, 128))
    for k_ in ("merge_p_a", "merge_p_b", "w_out", "ffn_w_gate_up", "ffn_w_down"):
        sh[k_] = np.ascontiguousarray(g(k_))
    return sh


def make_in_maps(inputs):
    sh = prep_shared(inputs)
    maps = []
    for b in range(8):
        m = dict(sh)
        m["x"] = np.ascontiguousarray(inputs["x"][b], np.float32)
        m["ctx"] = np.ascontiguousarray(inputs["ctx"][b], np.float32)
        cc = np.stack([np.asarray(inputs["c"][b], np.float32), np.asarray(inputs["c_ctx"], np.float32)], axis=-1)
        m["cc"] = np.ascontiguousarray(cc.reshape(16, 128, 2).transpose(1, 0, 2))
        maps.append(m)
    return maps


_NC = None


def kernel(**inputs):
    global _NC
    if _NC is None:
        _NC = build()
    maps = make_in_maps(inputs)
    res = run_bass_kernel_spmd(_NC, maps, core_ids=list(range(8)))
    return np.stack([r["out"] for r in res.results], axis=0).astype(np.float32)
```

```python
import numpy as np
from contextlib import ExitStack
import concourse.bass as bass
import concourse.mybir as mybir
from concourse.bass_utils import run_bass_kernel_spmd

F32 = mybir.dt.float32
BF16 = mybir.dt.bfloat16
AF = mybir.ActivationFunctionType
ALU = mybir.AluOpType
AX = mybir.AxisListType

COMPUTE = ("pe", "act", "dve", "pool")
WAW_RELAX = ("act", "dve", "pool")
NDSEM = 24

D = 2048
TC = 256
TL = 2048
TT = TC + TL
NCH = TT // 128
NIN = 94
FFN = 5632
EPS = 1e-6
DEC = 0.6065306597126334


class Buf:
    __slots__ = ("name", "lw", "rd")

    def __init__(self, name=""):
        self.name = name
        self.lw = None
        self.rd = {}


class Sched:
    def __init__(self):
        self.ops = []
        self.last = {}
        self.dmas = []
        self.bar = set()
        self.bar_seen = set()

    def pe_strict(self, on):
        if on:
            self._saved_relax = getattr(self, "relax", False)
            self.relax = False
        else:
            self.relax = self._saved_relax
            if self.relax and "pe" in self.last:
                self.pe_fence = self.last["pe"]

    def barrier(self):
        if not hasattr(self, "marks"):
            self.marks = []
        self.marks.append({e: sum(1 for o in self.ops if o[0] == e and not o[3]) for e in COMPUTE})
        self.bar = set(self.last.values()) | set(self.dmas)
        self.dmas = []
        self.bar_seen = set()

    def add(self, eng, fn, reads=(), writes=(), dma=False):
        i = len(self.ops)
        deps = set()
        if eng not in self.bar_seen:
            deps |= self.bar
            self.bar_seen.add(eng)
        self.last[eng] = i
        if dma:
            self.dmas.append(i)
        for b in reads:
            if b.lw is not None:
                deps.add(b.lw)
        for b in writes:
            cand = list(b.rd.values())
            if b.lw is not None:
                cand.append(b.lw)
            for c_ in cand:
                if (not dma) and self.ops[c_][0] == eng and not self.ops[c_][3] and eng in WAW_RELAX:
                    continue
                deps.add(c_)
        key = ("d", i) if dma else eng
        for b in reads:
            b.rd[key] = i
        for b in writes:
            b.lw = i
            b.rd = {}
        if eng == "pe" and getattr(self, "relax", False):
            deps = set(d for d in deps if not (self.ops[d][0] == "pe" and not self.ops[d][3]))
            if getattr(self, "pe_fence", None) is not None:
                deps.add(self.pe_fence)
                self.pe_fence = None
        self.ops.append((eng, fn, deps, dma))
        return i

    def emit(self, nc, stack):
        ops = self.ops
        engs = {"pe": nc.tensor, "act": nc.scalar, "dve": nc.vector, "pool": nc.gpsimd, "sp": nc.sync}
        names = list(engs)
        csem = {e: stack.enter_context(nc.semaphore("c_" + e)) for e in COMPUTE}
        dsem = {e: [stack.enter_context(nc.semaphore("d_%s%d" % (e, k))) for k in range(NDSEM)]
                for e in ("sp", "act", "pool")}
        comp = [None] * len(ops)
        cnt = {e: 0 for e in COMPUTE}
        dcnt = {e: 0 for e in dsem}
        prevslot = [None] * len(ops)
        for i, (eng, fn, deps, dma) in enumerate(ops):
            if dma:
                j = dcnt[eng]
                dcnt[eng] += 1
                comp[i] = (dsem[eng][j % NDSEM], 16 * (j // NDSEM + 1))
                if j >= NDSEM:
                    prevslot[i] = (dsem[eng][j % NDSEM], 16 * (j // NDSEM))
            else:
                cnt[eng] += 1
                comp[i] = (csem[eng], cnt[eng])
        per = {e: [] for e in names}
        for i, op in enumerate(ops):
            per[op[0]].append(i)
        block = stack.enter_context(nc.Block())

        def run(ename):
            def body(e):
                known = {}
                for i in per[ename]:
                    eng, fn, deps, dma = ops[i]
                    need = {}
                    cands = [comp[d] for d in deps]
                    if prevslot[i] is not None:
                        cands.append(prevslot[i])
                    for sm, v in cands:
                        k = id(sm)
                        if known.get(k, 0) >= v:
                            continue
                        if k not in need or need[k][1] < v:
                            need[k] = (sm, v)
                    for k, (sm, v) in need.items():
                        e.wait_ge(sm, v)
                        known[k] = v
                    ins = fn(e)
                    sm, v = comp[i]
                    ins.then_inc(sm, 16 if dma else 1)
                if ename in dsem:
                    last = {}
                    for i in per[ename]:
                        if ops[i][3]:
                            sm, v = comp[i]
                            last[id(sm)] = (sm, v)
                    for sm, v in last.values():
                        e.wait_ge(sm, v)
            return body

        block.tensor(run("pe"))
        block.scalar(run("act"))
        block.vector(run("dve"))
        block.gpsimd(run("pool"))
        block.sync(run("sp"))


class T:
    def __init__(self, h, name):
        self.h = h
        self.b = Buf(name)

    def __getitem__(self, k):
        return self.h[k]


def _bufs(xs):
    return [x if isinstance(x, Buf) else x.b for x in xs]


class KB:
    def __init__(self, nc):
        self.nc = nc
        self.S = Sched()
        self.n = 0

    def sb(self, st, shape, dt, name=None):
        self.n += 1
        name = name or "t%d" % self.n
        if not hasattr(self, "used"):
            self.used = set()
        while name in self.used:
            name = name + "_"
        self.used.add(name)
        return T(st.enter_context(self.nc.sbuf_tensor(name, list(shape), dt)), name)

    def ps(self, st, shape, dt, name=None):
        self.n += 1
        name = name or "p%d" % self.n
        return T(st.enter_context(self.nc.psum_tensor(name, list(shape), dt)), name)

    def dram(self, name, shape, dt, kind="Internal"):
        h = self.nc.dram_tensor(name, list(shape), dt, kind=kind)
        t = T(h.ap(), name)
        return t

    def act(self, out, in_, func, r, w, scale=1.0, bias=0.0, accum=None):
        kw = {}
        if accum is not None:
            kw["accum_out"] = accum
        self.S.add("act", lambda e: e.activation(out=out, in_=in_, func=func, scale=scale, bias=bias, **kw),
                   _bufs(r), _bufs(w))

    def tt(self, out, in0, in1, op, r, w, eng="dve"):
        self.S.add(eng, lambda e: e.tensor_tensor(out=out, in0=in0, in1=in1, op=op), _bufs(r), _bufs(w))

    def ts(self, out, in0, s1, s2, op0, op1, r, w, eng="dve", accum=None):
        kw = {}
        if accum is not None:
            kw["accum_out"] = accum
        if op1 is None:
            self.S.add(eng, lambda e: e.tensor_scalar(out=out, in0=in0, scalar1=s1, scalar2=None, op0=op0, **kw),
                       _bufs(r), _bufs(w))
        else:
            self.S.add(eng, lambda e: e.tensor_scalar(out=out, in0=in0, scalar1=s1, scalar2=s2, op0=op0, op1=op1, **kw),
                       _bufs(r), _bufs(w))

    def stt(self, out, in0, scalar, in1, op0, op1, r, w):
        self.S.add("dve", lambda e: e.scalar_tensor_tensor(out=out, in0=in0, scalar=scalar, in1=in1, op0=op0, op1=op1),
                   _bufs(r), _bufs(w))

    def copy(self, out, in_, r, w, eng="dve"):
        self.S.add(eng, lambda e: e.tensor_copy(out=out, in_=in_), _bufs(r), _bufs(w))

    def memset(self, out, val, w, eng="pool"):
        self.S.add(eng, lambda e: e.memset(out, val), [], _bufs(w))

    def recip(self, out, in_, r, w):
        self.S.add("dve", lambda e: e.reciprocal(out=out, in_=in_), _bufs(r), _bufs(w))

    def scan(self, out, d0, d1, init, op0, op1, r, w):
        self.S.add("dve", lambda e: e.tensor_tensor_scan(out=out, data0=d0, data1=d1, initial=init, op0=op0, op1=op1),
                   _bufs(r), _bufs(w))

    def mm(self, out, lhsT, rhs, start, stop, r, w):
        self.S.add("pe", lambda e: e.matmul(out, lhsT=lhsT, rhs=rhs, start=start, stop=stop), _bufs(r), _bufs(w))

    def tr(self, out, in_, ident, r, w):
        self.S.add("pe", lambda e: e.transpose(out=out, in_=in_, identity=ident), _bufs(r), _bufs(w))

    def dma(self, q, out, in_, r, w, **kw):
        self.S.add(q, lambda e: e.dma_start(out=out, in_=in_, **kw), _bufs(r), _bufs(w), dma=True)


def inverse_workspace(K, st, C):
    W = {}
    W["nmask"] = K.sb(st, [128, 2, 7, 128], BF16, "nmask_sb")
    K.dma("pool", W["nmask"][:], C["nmask"][:, :, :, :], [], [W["nmask"]])
    W["ident_b"] = C["ident_b"]
    W["sets"] = []
    for g in range(2):
        W["sets"].append({nm: K.sb(st, [128, 4, 128], BF16, "iw%d_%s" % (g, nm))
                          for nm in ("Xa", "Xb", "Ya", "Yb", "LsX", "LsY", "LsX2", "LsY2", "M1", "M2", "Mt", "R")})
    W["PI"] = [K.ps(st, [128, 4, 128], F32) for _ in range(2)]
    W["cnt"] = 0
    return W


def inverse_units(K, C, LL, n, d, XTb, W, nunits=None):
    LLf = LL[:].rearrange("p j f t -> p (j f) t")
    nm = W["nmask"]
    nunits = 2 * n if nunits is None else nunits
    gs = min(4, nunits)
    idb = W["ident_b"][:].unsqueeze(1).to_broadcast([128, gs, 128])
    bc = lambda m, lv: nm[:, m, lv, :].unsqueeze(1).to_broadcast([128, gs, 128])

    def pi():
        W["cnt"] += 1
        return W["PI"][W["cnt"] % 2]
    mx, my = (0, 1) if d == 0 else (1, 0)

    class V_:
        def __init__(s_, t):
            s_.t = t
            s_.b = t.b

        def __getitem__(s_, k):
            if k == slice(None):
                return s_.t[:, 0:gs, :]
            return s_.t[k]
    groups = []
    for gi, g0 in enumerate(range(0, nunits, gs)):
        S_ = W["sets"][gi % 2]
        st_ = {k_: V_(v_) for k_, v_ in S_.items()}
        st_["g0"] = g0
        st_["Lv"] = LLf[:, 2 * g0:2 * g0 + 2 * gs:2, :]
        st_["LTv"] = LLf[:, 2 * g0 + 1:2 * g0 + 2 * gs:2, :]
        st_["X"], st_["Xn"], st_["Y"], st_["Yn"] = st_["Xa"], st_["Xb"], st_["Ya"], st_["Yb"]
        groups.append(st_)
    for G in groups:
        K.tt(G["LsX"][:], G["Lv"], bc(mx, 0), ALU.mult, [LL, nm], [G["LsX"]], eng="pool")
        K.tt(G["X"][:], G["LsX"][:], idb, ALU.add, [G["LsX"], W["ident_b"]], [G["X"]], eng="pool")
        K.tt(G["LsY"][:], G["LTv"], bc(my, 0), ALU.mult, [LL, nm], [G["LsY"]], eng="pool")
        K.tt(G["Y"][:], G["LsY"][:], idb, ALU.add, [G["LsY"], W["ident_b"]], [G["Y"]], eng="pool")
        K.tt(G["Mt"][:], G["Lv"], idb, ALU.add, [LL, W["ident_b"]], [G["Mt"]], eng="pool")
    yield
    for lv in range(1, 7):
        sx, sy = ("LsX2", "LsY2") if lv % 2 else ("LsX", "LsY")
        for G in groups:
            K.tt(G[sx][:], G["Lv"], bc(mx, lv), ALU.mult, [LL, nm], [G[sx]], eng="pool")
            K.tt(G[sy][:], G["LTv"], bc(my, lv), ALU.mult, [LL, nm], [G[sy]], eng="pool")
        yield
        for G in groups:
            X, Y, LsX, LsY, M1, M2 = G["X"], G["Y"], G[sx], G[sy], G["M1"], G["M2"]
            Q = pi()
            for u in range(gs):
                K.mm(Q[:, u, :], LsY[:, u, :], X[:, u, :], True, True, [LsY, X], [Q])
            K.act(M1[:], Q[:, 0:gs, :], AF.Copy, [Q], [M1])
            Q = pi()
            for u in range(gs):
                K.mm(Q[:, u, :], LsX[:, u, :], Y[:, u, :], True, True, [LsX, Y], [Q])
            K.act(M2[:], Q[:, 0:gs, :], AF.Copy, [Q], [M2])
            yield
        for G in groups:
            X, Y, Xn, Yn, M1, M2 = G["X"], G["Y"], G["Xn"], G["Yn"], G["M1"], G["M2"]
            Q = pi()
            for u in range(gs):
                K.mm(Q[:, u, :], Y[:, u, :], M1[:, u, :], True, True, [Y, M1], [Q])
            K.tt(Xn[:], X[:], Q[:, 0:gs, :], ALU.add, [X, Q], [Xn])
            Q = pi()
            for u in range(gs):
                K.mm(Q[:, u, :], X[:, u, :], M2[:, u, :], True, True, [X, M2], [Q])
            K.tt(Yn[:], Y[:], Q[:, 0:gs, :], ALU.add, [Y, Q], [Yn])
            G["X"], G["Xn"], G["Y"], G["Yn"] = Xn, X, Yn, Y
            yield
    for G in groups:
        Q = pi()
        for u in range(gs):
            K.mm(Q[:, u, :], G["Mt"][:, u, :], G["Y"][:, u, :], True, True, [G["Mt"], G["Y"]], [Q])
        K.stt(G["R"][:], Q[:, 0:gs, :], -1.0, idb, ALU.mult, ALU.add, [Q, W["ident_b"]], [G["R"]])
    for G in groups:
        Q = pi()
        for u in range(gs):
            K.mm(Q[:, u, :], G["X"][:, u, :], G["R"][:, u, :], True, True, [G["X"], G["R"]], [Q])
        K.tt(XTb[:, G["g0"]:G["g0"] + gs, :], G["Y"][:], Q[:, 0:gs, :], ALU.add, [G["Y"], Q], [XTb])
    yield


SEGS = [(0, 2)] + [(2 + 4 * i, 4) for i in range(4)]


def rwkv_phase(K, st, C):
    XS, smallT, oaT, ident_b, ident_f = C["XS"], C["smallT"], C["oaT"], C["ident_b"], C["ident_f"]
    dbg = C["dbg"]
    sb = lambda shape, dt, name=None: K.sb(st, shape, dt, name)
    w0 = sb([128, 2, 8], F32); a0 = sb([128, 2, 8], F32)
    kkw = sb([128, 8], F32); ka = sb([128, 8], F32); omka = sb([128, 8], F32); rk = sb([128, 8], F32)
    for t_, d_ in ((w0, C["w0"]), (a0, C["a0"])):
        K.dma("sp", t_[:], d_[:, :, :], [], [t_])
    for t_, d_ in ((kkw, C["kkw"]), (ka, C["ka"]), (rk, C["rk"])):
        K.dma("sp", t_[:], d_[:, :], [], [t_])
    K.ts(omka[:], ka[:], -1.0, 1.0, ALU.mult, ALU.add, [ka], [omka])
    w2b = sb([128, 2, 1024], BF16); a2b = sb([128, 2, 1024], BF16); g2b = sb([64, 1024], BF16)
    K.memset(w2b[:], 0.0, [w2b])
    K.memset(a2b[:], 0.0, [a2b])
    K.dma("pool", w2b[0:96, :, :], C["w2"][:, :, :].rearrange("d r c -> r d c"), [], [w2b])
    K.dma("pool", a2b[0:96, :, :], C["a2"][:, :, :].rearrange("d r c -> r d c"), [], [a2b])
    K.dma("pool", g2b[:], C["g2"][:, :], [], [g2b])
    lnw = sb([128, 128], F32); lnb = sb([128, 128], F32)
    m1f = sb([128, 2, 4, 128], BF16); m2f = sb([128, 2, 3, 128], BF16)
    K.dma("pool", m1f[:], C["m1"][:, :, :, :], [], [m1f])
    K.dma("pool", m2f[:], C["m2"][:, :, :, :], [], [m2f])
    rmask = sb([128, 512], F32); bones = sb([128, 128], F32); hsel = sb([128, 2], F32)
    K.dma("sp", rmask[:], C["rmask"][:, :], [], [rmask])
    K.dma("sp", bones[:], C["bones"][:, :], [], [bones])
    K.dma("sp", hsel[:], C["hsel"][:, :], [], [hsel])
    f32t = lambda nm=None: sb([128, 512], F32, nm)
    bft = lambda nm=None: sb([128, 512], BF16, nm)
    Xr, Xk, Xv = f32t("Xr"), f32t("Xk"), f32t("Xv")
    sig, A, B, Cc, Dd = f32t("sig"), f32t("A"), f32t("B"), f32t("Cc"), f32t("Dd")
    e1, e2, e3, e4 = f32t("e1"), f32t("e2"), f32t("e3"), f32t("e4")
    icl, icl0, kq, sq, rn, kd, bd, tmp = (f32t(nm) for nm in ("icl", "icl0", "kq", "sq", "rn", "kd", "bd", "tmp"))
    kkt = kq
    gam = sb([128, 4], F32, "gam")
    rt, at, kt, bt, KH, BH, vb = (bft(nm) for nm in ("rt", "at", "kt", "bt", "KH", "BH", "vb"))
    KHt = sb([128, 4, 128], BF16, "KHt"); BHnt = sb([128, 4, 128], BF16, "BHnt"); Vt = sb([128, 4, 128], BF16, "Vt")
    LL = sb([128, 4, 4, 128], BF16, "LL")
    AA = sb([128, 4, 2, 3, 128], BF16, "AA")
    XTb = sb([128, 8, 128], BF16, "XTb")
    IW = inverse_workspace(K, st, C)
    Hf = sb([128, 128], F32, "Hf"); Hb = sb([128, 128], BF16, "Hb")
    P1s = sb([128, 128], BF16, "P1s"); Us = sb([128, 128], BF16, "Us")
    ybuf = sb([128, 16, 128], BF16, "ybuf")
    ytot = sb([128, 4, 128], F32, "ytot"); yc = sb([128, 4, 128], F32, "yc"); ysq = sb([128, 4, 128], F32)
    mean = sb([128, 8], F32); var = sb([128, 8], F32)
    bsum = sb([128, 4, 2], F32)
    oat = sb([128, 4, 128], BF16)
    PF = [K.ps(st, [128, 512], F32) for _ in range(1)]
    PTr = K.ps(st, [128, 8, 128], BF16)
    PG = [K.ps(st, [128, 4, 128], F32) for _ in range(2)]
    PSq = K.ps(st, [128, 512], F32)
    PSh = K.ps(st, [128, 512], F32)
    PS_P1, PS_U, PS_Y, PS_H = PSq, PSq, PSq, PSh
    cnt = {"pf": 0, "pg": 0, "pi": 0, "tr": 0}

    def nxt(lst, key):
        cnt[key] += 1
        return lst[cnt[key] % len(lst)]

    def transp(src, dst, n, scale=None):
        half = cnt["tr"] % 2
        cnt["tr"] += 1
        for j in range(n):
            K.tr(PTr[:, half * 4 + j, :], src[:, j * 128:(j + 1) * 128], ident_b[:], [src, ident_b], [PTr])
        if scale is None:
            K.copy(dst[:, :n, :], PTr[:, half * 4:half * 4 + n, :], [PTr], [dst])
        else:
            K.act(dst[:, :n, :], PTr[:, half * 4:half * 4 + n, :], AF.Copy, [PTr], [dst], scale=scale)

    Xs = [(Xr, Xk, Xv), (Xr, Xk, Xv)]
    rtP = [rt, bft("rt1")]; atP = [at, bft("at1")]; ktP = [kt, bft("kt1")]; btP = [bt, bft("bt1")]
    KHtP = [KHt, sb([128, 4, 128], BF16, "KHt1")]; BHntP = [BHnt, sb([128, 4, 128], BF16, "BHnt1")]
    VtP = [Vt, sb([128, 4, 128], BF16, "Vt1")]
    gamP = [gam, sb([128, 4], F32, "gam1")]
    AAP = [AA, sb([128, 4, 2, 3, 128], BF16, "AA1")]
    XTbP = [XTb, sb([128, 8, 128], BF16, "XTb1")]
    bsumP = [bsum, sb([128, 4, 2], F32, "bsum1")]
    items = []
    for hp in range(C["nhp"]):
        for d in range(2):
            order = SEGS if d == 0 else [SEGS[0], SEGS[4], SEGS[3], SEGS[2], SEGS[1]]
            for si, (c0, n) in enumerate(order):
                items.append((hp, d, c0, n, si == 0))

    def loads(i):
        hp, d, c0, n, first = items[i]
        t0, N = c0 * 128, n * 128
        for X_, row in zip(Xs[i % 2], (0, 1024, 2048)):
            K.dma("sp", X_[:, :N], XS[row + hp * 128:row + hp * 128 + 128, t0:t0 + N], [XS], [X_])

    def stepA(i):
        hp, d, c0, n, first = items[i]
        p = i % 2
        hc = slice(hp * 128, (hp + 1) * 128)
        t0, N = c0 * 128, n * 128
        latent = c0 >= 2
        tk = slice(t0, t0 + N)
        Xr, Xk, Xv = Xs[p]
        rt, at, kt, bt, KHt, BHnt, Vt, gam, bsum = rtP[p], atP[p], ktP[p], btP[p], KHtP[p], BHntP[p], VtP[p], gamP[p], bsumP[p]
        P = nxt(PF, "pf")
        K.mm(P[:, :N], w2b[:, d, hc], smallT[:, d, tk], True, True, [w2b, smallT], [P])
        K.act(sig[:, :N], P[:, :N], AF.Sigmoid, [P, w0], [sig], bias=w0[:, d, hp:hp + 1])
        K.scan(A[:, :N], rmask[:, :N], sig[:, :N], 0.0, ALU.mult, ALU.add, [rmask, sig], [A])
        P = nxt(PF, "pf")
        K.mm(P[:, :N], a2b[:, d, hc], smallT[:, 2 + d, tk], True, True, [a2b, smallT], [P])
        K.act(icl[:, :N], P[:, :N], AF.Sigmoid, [P, a0], [icl], bias=a0[:, d, hp:hp + 1])
        if d == 1 and latent:
            P = nxt(PF, "pf")
            K.mm(P[:, :N], a2b[:, 0, hc], smallT[:, 2, tk], True, True, [a2b, smallT], [P])
            K.act(icl0[:, :N], P[:, :N], AF.Sigmoid, [P, a0], [icl0], bias=a0[:, 0, hp:hp + 1])
        yield
        K.tt(B[:, :N], A[:, :N], sig[:, :N], ALU.subtract, [A, sig], [B])
        v3 = lambda t_: t_[:, :N].rearrange("p (c t) -> p c t", t=128)
        tot = v3(A)[:, :, 127:128]
        K.tt(v3(Cc), tot.to_broadcast([128, n, 128]), v3(A), ALU.subtract, [A], [Cc])
        K.tt(Dd[:, :N], Cc[:, :N], sig[:, :N], ALU.add, [Cc, sig], [Dd], eng="pool")
        yield
        Gi, Gx, Gt = (A, B, Cc) if d == 0 else (Dd, Cc, B)
        K.act(e1[:, :N], Gi[:, :N], AF.Exp, [Gi], [e1], scale=-DEC)
        K.act(e2[:, :N], Gx[:, :N], AF.Exp, [Gx], [e2], scale=-DEC)
        yield
        K.act(e3[:, :N], Gi[:, :N], AF.Exp, [Gi], [e3], scale=DEC)
        K.act(e4[:, :N], Gt[:, :N], AF.Exp, [Gt], [e4], scale=-DEC)
        K.act(gam[:, :n], v3(A)[:, :, 127], AF.Exp, [A], [gam], scale=-DEC)
        yield
        K.act(kq[:, :N], Xk[:, :N], AF.Copy, [Xk, kkw], [kq], scale=kkw[:, hp:hp + 1])
        K.act(sq[:, :N], kq[:, :N], AF.Square, [kq], [sq])
        P = nxt(PF, "pf")
        K.mm(P[:, :N], bones[:], sq[:, :N], True, True, [bones, sq], [P])
        K.act(rn[:, :N], P[:, :N], AF.Sqrt, [P], [rn], bias=EPS)
        K.recip(rn[:, :N], rn[:, :N], [rn], [rn])
        yield
        K.tt(kkt[:, :N], kq[:, :N], rn[:, :N], ALU.mult, [kq, rn], [kkt])
        K.ts(tmp[:, :N], icl[:, :N], ka[:, hp:hp + 1], omka[:, hp:hp + 1], ALU.mult, ALU.add, [icl, ka, omka], [tmp])
        K.tt(kd[:, :N], tmp[:, :N], Xk[:, :N], ALU.mult, [tmp, Xk], [kd])
        yield
        K.tt(bd[:, :N], kkt[:, :N], icl[:, :N], ALU.mult, [kkt, icl], [bd], eng="pool")
        K.tt(rt[:, :N], Xr[:, :N], e1[:, :N], ALU.mult, [Xr, e1], [rt])
        K.tt(at[:, :N], kkt[:, :N], e2[:, :N], ALU.mult, [kkt, e2], [at], eng="pool")
        yield
        K.tt(kt[:, :N], kd[:, :N], e3[:, :N], ALU.mult, [kd, e3], [kt])
        K.tt(bt[:, :N], bd[:, :N], e3[:, :N], ALU.mult, [bd, e3], [bt], eng="pool")
        K.tt(KH[:, :N], kd[:, :N], e4[:, :N], ALU.mult, [kd, e4], [KH])
        yield
        K.tt(BH[:, :N], bd[:, :N], e4[:, :N], ALU.mult, [bd, e4], [BH], eng="pool")
        K.act(vb[:, :N], Xv[:, :N], AF.Copy, [Xv], [vb])
        transp(KH, KHt, n)
        yield
        transp(BH, BHnt, n, scale=-1.0)
        transp(vb, Vt, n)
        yield
        if d == 1 and latent:
            K.tt(tmp[:, :N], icl[:, :N], icl0[:, :N], ALU.add, [icl, icl0], [tmp])
            yield
            K.ts(tmp[:, :N], tmp[:, :N], 0.5, None, ALU.mult, None, [tmp], [tmp])
            K.ts(tmp[:, :N], tmp[:, :N], ka[:, hp:hp + 1], omka[:, hp:hp + 1], ALU.mult, ALU.add, [tmp, ka, omka], [tmp])
            K.tt(tmp[:, :N], tmp[:, :N], Xk[:, :N], ALU.mult, [tmp, Xk], [tmp])
            yield
            K.stt(sq[:, :N], tmp[:, :N], rk[:, hp:hp + 1], Xr[:, :N], ALU.mult, ALU.mult, [tmp, rk, Xr], [sq])
            P = nxt(PF, "pf")
            for j in range(n):
                K.mm(P[:, 2 * j:2 * j + 2], sq[:, j * 128:(j + 1) * 128], hsel[:], True, True, [sq, hsel], [P])
            K.copy(bsum[:].rearrange("p j e -> p (j e)"), P[:, 0:2 * n], [P], [bsum])
            yield
        if i + 1 < len(items):
            loads(i + 1)
        yield

    def stepB(i):
        hp, d, c0, n, first = items[i]
        p = i % 2
        hc = slice(hp * 128, (hp + 1) * 128)
        t0, N = c0 * 128, n * 128
        latent = c0 >= 2
        rt, at, kt, bt, KHt, BHnt, Vt, gam, bsum = rtP[p], atP[p], ktP[p], btP[p], KHtP[p], BHntP[p], VtP[p], gamP[p], bsumP[p]
        AA, XTb = AAP[p], XTbP[p]
        for j in range(n):
            cs = slice(j * 128, (j + 1) * 128)
            K.S.pe_strict(True)
            G = nxt(PG, "pg")
            for e in range(2):
                ps_ = slice(64 * e, 64 * e + 64)
                K.mm(G[:, 2 * e, :], at[ps_, cs], bt[ps_, cs], True, True, [at, bt], [G])
                K.mm(G[:, 2 * e + 1, :], bt[ps_, cs], at[ps_, cs], True, True, [at, bt], [G])
            K.tt(LL[:, j, :, :], G[:], m1f[:, d, :, :], ALU.mult, [G, m1f], [LL])
            for e in range(2):
                ps_ = slice(64 * e, 64 * e + 64)
                G = nxt(PG, "pg")
                K.mm(G[:, 0, :], kt[ps_, cs], at[ps_, cs], True, True, [kt, at], [G])
                K.mm(G[:, 1, :], kt[ps_, cs], rt[ps_, cs], True, True, [kt, rt], [G])
                K.mm(G[:, 2, :], bt[ps_, cs], rt[ps_, cs], True, True, [bt, rt], [G])
                K.tt(AA[:, j, e, :, :], G[:, 0:3, :], m2f[:, d, :, :], ALU.mult, [G, m2f], [AA])
            K.S.pe_strict(False)
            yield
        for _ in inverse_units(K, C, LL, n, d, XTb, IW):
            yield

    def stepC(i):
        hp, d, c0, n, first = items[i]
        p = i % 2
        hc = slice(hp * 128, (hp + 1) * 128)
        t0, N = c0 * 128, n * 128
        latent = c0 >= 2
        rt, at, kt, bt, KHt, BHnt, Vt, gam, bsum = rtP[p], atP[p], ktP[p], btP[p], KHtP[p], BHntP[p], VtP[p], gamP[p], bsumP[p]
        AA, XTb = AAP[p], XTbP[p]
        if first:
            K.memset(Hf[:], 0.0, [Hf])
            K.memset(Hb[:], 0.0, [Hb])
            if d == 1:
                K.dma("sp", lnw[:], C["lnw"][0:1, hc].to_broadcast([128, 128]), [], [lnw])
                K.dma("sp", lnb[:], C["lnb"][0:1, hc].to_broadcast([128, 128]), [], [lnb])
        jl = list(range(n)) if d == 0 else list(range(n - 1, -1, -1))
        for j in jl:
            cs = slice(j * 128, (j + 1) * 128)
            K.mm(PS_P1[:, 0:128], at[:, cs], Hb[:], True, False, [at, Hb], [PS_P1])
            for e in range(2):
                vs = slice(64 * e, 64 * e + 64)
                K.mm(PS_P1[:, 64 * e:64 + 64 * e], AA[:, j, e, 0, :], Vt[:, j, vs], False, e == 1, [AA, Vt], [PS_P1])
            K.act(P1s[:], PS_P1[:, 0:128], AF.Copy, [PS_P1], [P1s])
            yield
            for e in range(2):
                vs = slice(64 * e, 64 * e + 64)
                K.mm(PS_U[:, 128 + 64 * e:192 + 64 * e], XTb[:, 2 * j + e, :], P1s[:, vs], True, True, [XTb, P1s], [PS_U])
            K.copy(Us[:], PS_U[:, 128:256], [PS_U], [Us])
            yield
            if latent:
                K.mm(PS_Y[:, 256:384], rt[:, cs], Hb[:], True, False, [rt, Hb], [PS_Y])
                for e in range(2):
                    vs = slice(64 * e, 64 * e + 64)
                    yo = PS_Y[:, 256 + 64 * e:320 + 64 * e]
                    K.mm(yo, AA[:, j, e, 1, :], Vt[:, j, vs], False, False, [AA, Vt], [PS_Y])
                    K.mm(yo, AA[:, j, e, 2, :], Us[:, vs], False, e == 1, [AA, Us], [PS_Y])
            K.mm(PS_H[:, 384:512], KHt[:, j, :], Vt[:, j, :], True, False, [KHt, Vt], [PS_H])
            K.mm(PS_H[:, 384:512], BHnt[:, j, :], Us[:], False, True, [BHnt, Us], [PS_H])
            for e in range(2):
                ps_ = slice(64 * e, 64 * e + 64)
                vs = slice(64 * e, 64 * e + 64)
                K.stt(Hf[ps_, vs], Hf[ps_, vs], gam[ps_, j:j + 1], PS_H[ps_, 384 + 64 * e:448 + 64 * e], ALU.mult, ALU.add,
                      [Hf, gam, PS_H], [Hf])
            K.act(Hb[:], Hf[:], AF.Copy, [Hf], [Hb])
            if latent:
                cg = c0 - 2 + j
                if d == 0:
                    K.act(ybuf[:, cg, :], PS_Y[:, 256:384], AF.Copy, [PS_Y], [ybuf])
                else:
                    K.tt(ytot[:, j, :], ybuf[:, cg, :], PS_Y[:, 256:384], ALU.add, [ybuf, PS_Y], [ytot])
            yield
        if d == 1 and latent:
            if dbg and hp < 8:
                K.dma("sp", dbg["yf"][hp, :, c0 - 2:c0 - 2 + n, :], ytot[:], [ytot], [])
            yv = ytot[:].rearrange("p j (e c) -> p (j e) c", c=64)
            ycv = yc[:].rearrange("p j (e c) -> p (j e) c", c=64)
            sqv = ysq[:].rearrange("p j (e c) -> p (j e) c", c=64)
            K.S.add("dve", lambda e_: e_.tensor_reduce(out=mean[:], in_=yv, axis=AX.X, op=ALU.add), _bufs([ytot]), _bufs([mean]))
            K.ts(mean[:], mean[:], 1.0 / 64, None, ALU.mult, None, [mean], [mean])
            K.tt(ycv, yv, mean[:].unsqueeze(2).to_broadcast([128, 8, 64]), ALU.subtract, [ytot, mean], [yc])
            K.tt(sqv, ycv, ycv, ALU.mult, [yc], [ysq], eng="pool")
            yield
            K.S.add("dve", lambda e_: e_.tensor_reduce(out=var[:], in_=sqv, axis=AX.X, op=ALU.add), _bufs([ysq]), _bufs([var]))
            K.act(var[:], var[:], AF.Sqrt, [var], [var], scale=1.0 / 64, bias=64e-5)
            K.recip(var[:], var[:], [var], [var])
            K.tt(ycv, ycv, var[:].unsqueeze(2).to_broadcast([128, 8, 64]), ALU.mult, [yc, var], [yc])
            yield
            K.tt(yc[:], yc[:], lnw[:].unsqueeze(1).to_broadcast([128, 4, 128]), ALU.mult, [yc, lnw], [yc])
            K.tt(yc[:], yc[:], lnb[:].unsqueeze(1).to_broadcast([128, 4, 128]), ALU.add, [yc, lnb], [yc])
            K.copy(ysq[:], Vt[:], [Vt], [ysq], eng="pool")
            K.tt(sqv, sqv, bsum[:].rearrange("p j e -> p (j e)").unsqueeze(2).to_broadcast([128, 8, 64]), ALU.mult,
                 [ysq, bsum], [ysq])
            K.tt(yc[:], yc[:], ysq[:], ALU.add, [yc, ysq], [yc])
            yield
            P = nxt(PF, "pf")
            for j in range(n):
                K.mm(P[:, j * 128:(j + 1) * 128], smallT[0:64, 4, t0 + j * 128:t0 + (j + 1) * 128], g2b[0:64, hc], True, True,
                     [smallT, g2b], [P])
            K.tt(oat[:], yc[:], P[:].rearrange("p (j c) -> p j c", c=128), ALU.mult, [yc, P], [oat])
            half = cnt["tr"] % 2
            cnt["tr"] += 1
            for j in range(n):
                K.tr(PTr[:, half * 4 + j, :], oat[:, j, :], ident_b[:], [oat, ident_b], [PTr])
            K.copy(oaT[:, hp, t0 - 256:t0 - 256 + N].rearrange("p (j t) -> p j t", t=128), PTr[:, half * 4:half * 4 + n, :],
                   [PTr], [oaT])
            yield

    def drain(g):
        for _ in g:
            pass

    def chain(*gs):
        for g in gs:
            for _ in g:
                yield

    loads(0)
    drain(stepA(0))
    drain(stepB(0))
    for i in range(len(items)):
        g1 = stepC(i)
        g2 = chain(stepA(i + 1), stepB(i + 1)) if i + 1 < len(items) else iter(())
        a1 = a2_ = True
        while a1 or a2_:
            if a1:
                try:
                    next(g1)
                except StopIteration:
                    a1 = False
            if a2_:
                for _ in range(RATIO):
                    try:
                        next(g2)
                    except StopIteration:
                        a2_ = False
                        break


def gdn_phase(K, st, C):
    US, SZ, abT, obT, ident_b, ident_f = C["US"], C["SZ"], C["abT"], C["obT"], C["ident_b"], C["ident_f"]
    sb = lambda shape, dt, name=None: K.sb(st, shape, dt, name)
    bigm = sb([128, 2, 2, 128], F32); offd = sb([128, 128], F32); ones = sb([128, 128], F32)
    K.dma("sp", bigm[:], C["bigm"][:, :, :, :], [], [bigm])
    K.dma("sp", offd[:], C["offd"][:, :], [], [offd])
    K.dma("sp", ones[:], C["ones"][:, :], [], [ones])
    rmask = sb([128, 512], F32)
    K.dma("sp", rmask[:], C["rmask"][:, :], [], [rmask])
    alog = sb([64, 1], F32); dtb = sb([64, 1], F32); nea = sb([64, 1], F32)
    K.dma("sp", alog[:], C["alog"][:, :], [], [alog])
    K.dma("sp", dtb[:], C["dtb"][:, :], [], [dtb])
    gnw = sb([128, 128], F32)
    K.dma("sp", gnw[:], C["gnw"][0:1, :].to_broadcast([128, 128]), [], [gnw])
    K.act(nea[:], alog[:], AF.Exp, [alog], [nea])
    K.ts(nea[:], nea[:], -1.0, None, ALU.mult, None, [nea], [nea])
    GB = [sb([64, TT], F32, "GB%d" % d) for d in range(2)]
    tokT = [sb([128, NCH, 64], F32, "tokT%d" % d) for d in range(2)]
    with ExitStack() as s0:
        gt = K.sb(s0, [16, TT], F32); A = K.sb(s0, [16, TT], F32); Bx = K.sb(s0, [16, TT], F32)
        K.act(gt[:], abT[0:16, :], AF.Exp, [abT, dtb], [gt], bias=dtb[0:16, :])
        K.act(gt[:], gt[:], AF.Ln, [gt], [gt], bias=1.0)
        K.ts(gt[:], gt[:], nea[0:16, :], None, ALU.mult, None, [gt, nea], [gt])
        for d in range(2):
            K.memset(GB[d][:], 0.0, [GB[d]])
            K.act(GB[d][32:48, :], abT[32:48, :], AF.Sigmoid, [abT], [GB[d]])
        for t0 in range(0, TT, 512):
            N = min(512, TT - t0)
            K.scan(A[:, t0:t0 + N], rmask[0:16, :N], gt[:, t0:t0 + N], 0.0, ALU.mult, ALU.add, [rmask, gt], [A])
        K.copy(GB[0][0:16, :], A[:], [A], [GB[0]], eng="pool")
        K.tt(Bx[:], A[:], gt[:], ALU.subtract, [A, gt], [Bx])
        v3 = lambda t_: t_[:].rearrange("p (c t) -> p c t", t=128)
        tot = v3(A)[:, :, 127:128]
        K.tt(v3(GB[1])[0:16], tot.to_broadcast([16, NCH, 128]), v3(Bx), ALU.subtract, [A, Bx], [GB[1]])
        ptk = K.ps(s0, [128, 8, 64], F32)
        for d in range(2):
            for c8 in range(0, NCH, 8):
                nn = min(8, NCH - c8)
                for j in range(nn):
                    c = c8 + j
                    K.tr(ptk[:, j, :], GB[d][:, c * 128:(c + 1) * 128], ident_f[0:64, 0:64], [GB[d], ident_f], [ptk])
                K.copy(tokT[d][:, c8:c8 + nn, :], ptk[:, 0:nn, :], [ptk], [tokT[d]])
    GBS = C["GBS"]
    for d in range(2):
        K.dma("sp", GBS[d, :, :], GB[d][:], [GB[d]], [GBS])
    K.S.barrier()
    f32t = lambda nm=None: sb([128, 512], F32, nm)
    bft = lambda nm=None: sb([128, 512], BF16, nm)
    _xq, _xk, _xv = f32t("gXq"), f32t("gXk"), f32t("gXv")
    XsP = [(_xq, _xk, _xv, f32t("gXG%d" % p), f32t("gXB%d" % p)) for p in range(2)]
    sq, rn, qn, kn, eG, tmp, sz = (f32t("g_" + nm) for nm in ("sq", "rn", "qn", "kn", "eG", "tmp", "sz"))
    Ktl, vb = bft("g_Ktl"), bft("g_vb")
    knbP, qnbP, kbTP, nKBGP, QdP = ([bft("g_%s%d" % (nm, p)) for p in range(2)] for nm in ("knb", "qnb", "kbT", "nKBG", "Qd"))
    KttP = [sb([128, 4, 128], BF16, "g_Ktt%d" % p) for p in range(2)]
    VtP = [sb([128, 4, 128], BF16, "g_Vt%d" % p) for p in range(2)]
    glP = [sb([128, 4], F32, "g_gl%d" % p) for p in range(2)]
    Dc = sb([128, 4, 2, 128], F32, "g_Dc"); DiT = sb([128, 4, 128], F32, "g_DiT"); Dtmp = sb([128, 4, 128], F32, "g_Dtmp")
    LLg = sb([128, 2, 4, 128], BF16, "g_LL")
    QKt = sb([128, 4, 128], BF16, "g_QKt")
    XTb = sb([128, 4, 128], BF16, "g_XTb")
    IW = inverse_workspace(K, st, C)
    Sf = sb([128, 128], F32, "g_Sf"); Sb = sb([128, 128], BF16, "g_Sb")
    P1s = sb([128, 128], BF16, "g_P1s"); VNs = sb([128, 128], BF16, "g_VNs")
    obuf = sb([128, 16, 128], F32, "g_obuf")
    otot = sb([128, 4, 128], F32, "g_otot"); osq = sb([128, 4, 128], F32, "g_osq")
    ss = sb([128, 4], F32); onb = sb([128, 4, 128], BF16, "g_onb")
    PF = K.ps(st, [128, 512], F32)
    PTr = K.ps(st, [128, 8, 128], BF16)
    PG = [K.ps(st, [128, 4, 128], F32) for _ in range(2)]
    PSq = K.ps(st, [128, 512], F32)
    PSh = K.ps(st, [128, 512], F32)
    cnt = {"pg": 0, "tr": 0}

    def transp(src, dst, n):
        half = cnt["tr"] % 2
        cnt["tr"] += 1
        for j in range(n):
            K.tr(PTr[:, half * 4 + j, :], src[:, j * 128:(j + 1) * 128], ident_b[:], [src, ident_b], [PTr])
        K.copy(dst[:, :n, :], PTr[:, half * 4:half * 4 + n, :], [PTr], [dst])

    items = []
    for h in range(C["nh"]):
        for d in range(2):
            order = SEGS if d == 0 else [SEGS[0], SEGS[4], SEGS[3], SEGS[2], SEGS[1]]
            for si, (c0, n) in enumerate(order):
                items.append((h, d, c0, n, si == 0))

    def loads(i):
        h, d, c0, n, first = items[i]
        r = d * 8 + h
        t0, N = c0 * 128, n * 128
        Xq, Xk, Xv, bcG, bcB = XsP[i % 2]
        for X_, row in ((Xq, 0), (Xk, 1024), (Xv, 2048)):
            K.dma("sp", X_[:, :N], US[row + h * 128:row + h * 128 + 128, t0:t0 + N], [US], [X_])
        K.dma("sp", bcG[:, :N], GBS[d, r:r + 1, t0:t0 + N].to_broadcast([128, N]), [GBS], [bcG])
        K.dma("sp", bcB[:, :N], GBS[d, 32 + r:33 + r, t0:t0 + N].to_broadcast([128, N]), [GBS], [bcB])

    def stepA(i):
        h, d, c0, n, first = items[i]
        p = i % 2
        t0, N = c0 * 128, n * 128
        Xq, Xk, Xv, bcG, bcB = XsP[p]
        knb, qnb, kbT, nKBG, Qd, Ktt, Vt, gl = knbP[p], qnbP[p], kbTP[p], nKBGP[p], QdP[p], KttP[p], VtP[p], glP[p]
        v3 = lambda t_: t_[:, :N].rearrange("p (c t) -> p c t", t=128)
        for X_, o_, sc_ in ((Xq, qn, 128 ** -0.5), (Xk, kn, 1.0)):
            K.act(sq[:, :N], X_[:, :N], AF.Square, [X_], [sq])
            K.mm(PF[:, :N], ones[:], sq[:, :N], True, True, [ones, sq], [PF])
            K.act(rn[:, :N], PF[:, :N], AF.Sqrt, [PF], [rn], bias=EPS)
            K.recip(rn[:, :N], rn[:, :N], [rn], [rn])
            K.stt(o_[:, :N], X_[:, :N], sc_, rn[:, :N], ALU.mult, ALU.mult, [X_, rn], [o_])
            yield
        K.act(eG[:, :N], bcG[:, :N], AF.Exp, [bcG], [eG])
        lastcol = 127 if d == 0 else 0
        K.copy(gl[:, :n], v3(eG)[:, :, lastcol], [eG], [gl], eng="pool")
        glast = v3(bcG)[:, :, lastcol:lastcol + 1]
        K.tt(kbT[:, :N], kn[:, :N], bcB[:, :N], ALU.mult, [kn, bcB], [kbT])
        yield
        K.copy(knb[:, :N], kn[:, :N], [kn], [knb], eng="pool")
        K.act(qnb[:, :N], qn[:, :N], AF.Copy, [qn], [qnb])
        K.stt(nKBG[:, :N], kbT[:, :N], -1.0, eG[:, :N], ALU.mult, ALU.mult, [kbT, eG], [nKBG])
        yield
        K.tt(Qd[:, :N], qn[:, :N], eG[:, :N], ALU.mult, [qn, eG], [Qd], eng="pool")
        K.tt(v3(tmp), glast.to_broadcast([128, n, 128]), v3(bcG), ALU.subtract, [bcG], [tmp])
        K.act(tmp[:, :N], tmp[:, :N], AF.Exp, [tmp], [tmp])
        yield
        K.tt(Ktl[:, :N], kn[:, :N], tmp[:, :N], ALU.mult, [kn, tmp], [Ktl])
        K.act(vb[:, :N], Xv[:, :N], AF.Copy, [Xv], [vb])
        transp(Ktl, Ktt, n)
        yield
        transp(vb, Vt, n)
        yield
        if i + 1 < len(items):
            loads(i + 1)
        yield

    def stepBC(i):
        h, d, c0, n, first = items[i]
        p = i % 2
        r = d * 8 + h
        t0, N = c0 * 128, n * 128
        latent = c0 >= 2
        Xq, Xk, Xv, bcG, bcB = XsP[p]
        knb, qnb, kbT, nKBG, Qd, Ktt, Vt, gl = knbP[p], qnbP[p], kbTP[p], nKBGP[p], QdP[p], KttP[p], VtP[p], glP[p]
        v3 = lambda t_: t_[:, :N].rearrange("p (c t) -> p c t", t=128)
        if first:
            K.memset(Sf[:], 0.0, [Sf])
            K.memset(Sb[:], 0.0, [Sb])
        gct = tokT[d][:, c0:c0 + n, r:r + 1].to_broadcast([128, n, 128])
        bg3 = v3(bcG)
        K.tt(Dtmp[:, :n, :], bg3, bigm[:, d, 0, :].unsqueeze(1).to_broadcast([128, n, 128]), ALU.add, [bcG, bigm], [Dtmp])
        K.tt(Dtmp[:, :n, :], Dtmp[:, :n, :], gct, ALU.subtract, [Dtmp, tokT[d]], [Dtmp], eng="pool")
        K.act(Dtmp[:, :n, :], Dtmp[:, :n, :], AF.Exp, [Dtmp], [Dtmp], scale=-1.0)
        K.tt(Dc[:, :n, 0, :], Dtmp[:, :n, :], offd[:].unsqueeze(1).to_broadcast([128, n, 128]), ALU.mult, [Dtmp, offd], [Dc],
             eng="pool")
        yield
        K.tt(DiT[:, :n, :], bg3, bigm[:, d, 1, :].unsqueeze(1).to_broadcast([128, n, 128]), ALU.add, [bcG, bigm], [DiT])
        K.tt(DiT[:, :n, :], DiT[:, :n, :], gct, ALU.subtract, [DiT, tokT[d]], [DiT], eng="pool")
        K.act(DiT[:, :n, :], DiT[:, :n, :], AF.Exp, [DiT], [DiT])
        K.tt(Dc[:, :n, 1, :], DiT[:, :n, :], offd[:].unsqueeze(1).to_broadcast([128, n, 128]), ALU.mult, [DiT, offd], [Dc],
             eng="pool")
        yield
        LLv = LLg[:].rearrange("p a b t -> p (a b) t")
        for j in range(n):
            cs = slice(j * 128, (j + 1) * 128)
            cnt["pg"] += 1
            G = PG[cnt["pg"] % 2]
            K.mm(G[:, 0, :], kbT[:, cs], knb[:, cs], True, True, [kbT, knb], [G])
            K.mm(G[:, 1, :], knb[:, cs], kbT[:, cs], True, True, [kbT, knb], [G])
            K.mm(G[:, 2, :], knb[:, cs], qnb[:, cs], True, True, [qnb, knb], [G])
            K.tt(LLv[:, 2 * j:2 * j + 2, :], G[:, 0:2, :], Dc[:, j, :, :], ALU.mult, [G, Dc], [LLg])
            K.tt(QKt[:, j, :], G[:, 2, :], DiT[:, j, :], ALU.mult, [G, DiT], [QKt])
            yield
        for _ in inverse_units(K, C, LLg, n, d, XTb, IW, nunits=n):
            yield
        jl = list(range(n)) if d == 0 else list(range(n - 1, -1, -1))
        for j in jl:
            cs = slice(j * 128, (j + 1) * 128)
            c = c0 + j
            K.mm(PSq[:, 0:128], nKBG[:, cs], Sb[:], True, True, [nKBG, Sb], [PSq])
            K.stt(P1s[:], Vt[:, j, :], tokT[d][:, c, 32 + r:33 + r], PSq[:, 0:128], ALU.mult, ALU.add,
                  [Vt, tokT[d], PSq], [P1s])
            yield
            K.mm(PSq[:, 128:256], XTb[:, j, :], P1s[:], True, True, [XTb, P1s], [PSq])
            K.act(VNs[:], PSq[:, 128:256], AF.Copy, [PSq], [VNs])
            yield
            if latent:
                K.mm(PSq[:, 256:384], Qd[:, cs], Sb[:], True, False, [Qd, Sb], [PSq])
                K.mm(PSq[:, 256:384], QKt[:, j, :], VNs[:], False, True, [QKt, VNs], [PSq])
            K.mm(PSh[:, 0:128], Ktt[:, j, :], VNs[:], True, True, [Ktt, VNs], [PSh])
            K.stt(Sf[:], Sf[:], gl[:, j:j + 1], PSh[:, 0:128], ALU.mult, ALU.add, [Sf, gl, PSh], [Sf])
            K.act(Sb[:], Sf[:], AF.Copy, [Sf], [Sb])
            if latent:
                cg = c - 2
                if d == 0:
                    K.act(obuf[:, cg, :], PSq[:, 256:384], AF.Copy, [PSq], [obuf])
                else:
                    K.tt(otot[:, j, :], obuf[:, cg, :], PSq[:, 256:384], ALU.add, [obuf, PSq], [otot])
            yield
        if d == 1 and latent:
            if C["dbg"]:
                K.dma("sp", C["dbg"]["of"][h, :, c0 - 2:c0 - 2 + n, :], otot[:], [otot], [])
            K.tt(osq[:], otot[:], otot[:], ALU.mult, [otot], [osq], eng="pool")
            K.S.add("dve", lambda e_: e_.tensor_reduce(out=ss[:], in_=osq[:], axis=AX.X, op=ALU.add), _bufs([osq]), _bufs([ss]))
            K.act(ss[:], ss[:], AF.Sqrt, [ss], [ss], scale=1.0 / 128, bias=EPS)
            K.recip(ss[:], ss[:], [ss], [ss])
            yield
            K.tt(osq[:], otot[:], ss[:].unsqueeze(2).to_broadcast([128, 4, 128]), ALU.mult, [otot, ss], [osq])
            K.tt(onb[:], osq[:], gnw[:].unsqueeze(1).to_broadcast([128, 4, 128]), ALU.mult, [osq, gnw], [onb])
            K.dma("sp", sz[:, :N], SZ[h * 128:(h + 1) * 128, t0 - 256:t0 - 256 + N], [SZ], [sz])
            yield
            half = cnt["tr"] % 2
            cnt["tr"] += 1
            for j in range(n):
                K.tr(PTr[:, half * 4 + j, :], onb[:, j, :], ident_b[:], [onb, ident_b], [PTr])
            K.tt(obT[:, h, t0 - 256:t0 - 256 + N].rearrange("p (j t) -> p j t", t=128), PTr[:, half * 4:half * 4 + n, :],
                 sz[:, :N].rearrange("p (j t) -> p j t", t=128), ALU.mult, [PTr, sz], [obT])
            yield

    loads(0)
    for _ in stepA(0):
        pass
    for i in range(len(items)):
        g1 = stepBC(i)
        g2 = stepA(i + 1) if i + 1 < len(items) else iter(())
        a1 = a2 = True
        while a1 or a2:
            if a1:
                for _ in range(RATIO_G):
                    try:
                        next(g1)
                    except StopIteration:
                        a1 = False
                        break
            if a2:
                try:
                    next(g2)
                except StopIteration:
                    a2 = False


def write_mods(K, C):
    modT, ident_f, MODS = C["modT"], C["ident_f"], C["MODS"]
    with ExitStack() as s0:
        pt = K.ps(s0, [16, 2, 128], F32)
        rows = K.sb(s0, [16, 2, 128], F32)
        for i, sec in enumerate((2, 5)):
            K.tr(pt[:, i, :], modT[:, sec * 16:(sec + 1) * 16, 0], ident_f[:], [modT, ident_f], [pt])
        K.copy(rows[:], pt[:], [pt], [rows])
        for i in range(2):
            K.dma("sp", MODS[i * 16:(i + 1) * 16, :], rows[:, i, :], [rows], [MODS])
    K.S.barrier()


def merge_phase(K, top, C):
    oaT, obT, ident_b, ident_f, modT, s2 = C["oaT"], C["obT"], C["ident_b"], C["ident_f"], C["modT"], C["s2"]
    SG, X1, x_d, out_d = C["SG"], C["X1"], C["x"], C["out"]
    MODS = C["MODS"]
    dbg = C["dbg"]
    bc = K.sb(top, [128, 1, D], F32, "bc_rows")
    K.dma("sp", bc[:, 0, :], MODS[0:16, :].rearrange("(o a) b -> o (a b)", o=1).to_broadcast([128, D]), [MODS], [bc])
    K.S.barrier()
    p3 = ExitStack()
    mT = K.sb(p3, [128, 16, TL], BF16, "mT")
    with ExitStack() as s1:
        wa = [K.sb(s1, [128, 8, 256], BF16) for _ in range(2)]
        wbb = [K.sb(s1, [128, 8, 256], BF16) for _ in range(2)]
        sga = [K.sb(s1, [128, TL], BF16)] * 2
        sgb = [K.sb(s1, [128, TL], BF16)] * 2
        t1 = [K.sb(s1, [128, 512], F32) for _ in range(2)]
        t2 = [K.sb(s1, [128, 512], F32) for _ in range(2)]
        pa = [K.ps(s1, [128, 512], F32) for _ in range(2)]
        pb = [K.ps(s1, [128, 512], F32) for _ in range(2)]
        it = 0
        for sc in range(8):
            K.dma("pool", wa[sc % 2][:], C["p_a"][:, sc * 256:(sc + 1) * 256].rearrange("(k p) n -> p k n", p=128), [], [wa[sc % 2]])
            K.dma("pool", wbb[sc % 2][:], C["p_b"][:, sc * 256:(sc + 1) * 256].rearrange("(k p) n -> p k n", p=128), [], [wbb[sc % 2]])
            for ff in range(2):
                f = sc * 2 + ff
                ga, gb = sga[f % 2], sgb[f % 2]
                K.dma("sp", ga[:], SG[f * 128:(f + 1) * 128, :], [SG], [ga])
                K.dma("sp", gb[:], SG[2048 + f * 128:2048 + (f + 1) * 128, :], [SG], [gb])
                for n in range(4):
                    ts_ = slice(n * 512, (n + 1) * 512)
                    A_, B_, T1, T2 = pa[it % 2], pb[it % 2], t1[it % 2], t2[it % 2]
                    it += 1
                    for k in range(8):
                        K.mm(A_[:], wa[sc % 2][:, k, ff * 128:(ff + 1) * 128], oaT[:, k, ts_], k == 0, k == 7, [wa[sc % 2], oaT], [A_])
                    for k in range(8):
                        K.mm(B_[:], wbb[sc % 2][:, k, ff * 128:(ff + 1) * 128], obT[:, k, ts_], k == 0, k == 7, [wbb[sc % 2], obT], [B_])
                    K.tt(T1[:], A_[:], ga[:, ts_], ALU.mult, [A_, ga], [T1])
                    K.tt(T2[:], B_[:], gb[:, ts_], ALU.mult, [B_, gb], [T2])
                    K.tt(mT[:, f, ts_], T1[:], T2[:], ALU.add, [T1, T2], [mT], eng="pool")
    K.S.barrier()
    with ExitStack() as s2_:
        wo = [[K.sb(s2_, [128, 8, 512], BF16) for _ in range(2)] for _ in range(2)]
        xt = [K.sb(s2_, [128, 512], F32) for _ in range(3)]
        tt_ = [K.sb(s2_, [128, 512], F32) for _ in range(3)]
        pp = [K.ps(s2_, [128, 512], F32) for _ in range(4)]
        it = 0
        for n in range(4):
            ns = slice(n * 512, (n + 1) * 512)
            w = wo[n % 2]
            for kh in range(2):
                K.dma("pool", w[kh][:], C["w_out"][kh * 1024:(kh + 1) * 1024, ns].rearrange("(k p) n -> p k n", p=128), [], [w[kh]])
            for t in range(16):
                P, X_, T_ = pp[it % 4], xt[it % 3], tt_[it % 3]
                it += 1
                K.dma("sp", X_[:], x_d[t * 128:(t + 1) * 128, ns], [], [X_])
                for k in range(16):
                    K.mm(P[:], mT[:, k, t * 128:(t + 1) * 128], w[k // 8][:, k % 8, :], k == 0, k == 15, [mT, w[k // 8]], [P])
                K.tt(T_[:], P[:], bc[:, 0, ns], ALU.mult, [P, bc], [T_])
                K.tt(T_[:], T_[:], X_[:], ALU.add, [T_, X_], [T_], eng="pool")
                K.dma("sp", X1[t * 128:(t + 1) * 128, ns], T_[:], [T_], [X1])
    p3.close()


def ffn_phase(K, top, C):
    ident_b, ident_f, modT, s2 = C["ident_b"], C["ident_f"], C["modT"], C["s2"]
    X1, out_d, MODS = C["X1"], C["out"], C["MODS"]
    G = 512
    with ExitStack() as s4:
        bc = K.sb(s4, [128, 3, D], F32, "bc_rows4")
        K.dma("sp", bc[:, 1, :], MODS[16:32, :].rearrange("(o a) b -> o (a b)", o=1).to_broadcast([128, D]), [MODS], [bc])
        K.dma("sp", bc[:, 2, :], C["fnw"][0:1, :].to_broadcast([128, D]), [], [bc])
        h2T = K.sb(s4, [128, 16, G], BF16, "h2T")
        actT = K.sb(s4, [128, 44, G], BF16, "actT")
        x1t = [K.sb(s4, [128, D], F32, "x1t%d" % i) for i in range(4)]
        xb = K.sb(s4, [128, D], BF16)
        junk = K.sb(s4, [128, D], BF16)
        ss = K.sb(s4, [128, 1], F32); rs = K.sb(s4, [128, 1], F32)
        wg = [[K.sb(s4, [128, 8, 256], BF16) for _ in range(2)] for _ in range(2)]
        wu = [[K.sb(s4, [128, 8, 256], BF16) for _ in range(2)] for _ in range(2)]
        wd = [K.sb(s4, [128, 4, 512], BF16) for _ in range(4)]
        sgt = [K.sb(s4, [128, G], F32) for _ in range(2)]
        tq = [K.sb(s4, [128, 512], F32) for _ in range(2)]
        ot = [K.sb(s4, [128, D], F32) for _ in range(2)]
        ptr = [K.ps(s4, [128, 8, 128], BF16) for _ in range(1)]
        pgu = [K.ps(s4, [128, 512], F32) for _ in range(3)]
        pdn = [K.ps(s4, [128, 512], F32) for _ in range(4)]
        igu = 0
        for grp in range(NGRP):
            for t in range(4):
                X_ = x1t[t]
                row0 = grp * G + t * 128
                K.dma("sp", X_[:], X1[row0:row0 + 128, :], [X1], [X_])
                K.act(junk[:], X_[:], AF.Square, [X_], [junk, ss], accum=ss[:])
                K.act(rs[:], ss[:], AF.Sqrt, [ss], [rs], scale=1.0 / D, bias=EPS)
                K.recip(rs[:], rs[:], [rs], [rs])
                K.act(xb[:], X_[:], AF.Copy, [X_, rs], [xb], scale=rs[:])
                for g in range(4):
                    P = ptr[0]
                    for j in range(4):
                        k = g * 4 + j
                        K.tr(P[:, j, :], xb[:, k * 128:(k + 1) * 128], ident_b[:], [xb, ident_b], [P])
                    for j in range(4):
                        k = g * 4 + j
                        K.act(h2T[:, k, t * 128:(t + 1) * 128], P[:, j, :], AF.Identity, [P, s2, modT], [h2T],
                              scale=s2[:, k:k + 1], bias=modT[:, 48 + k, 0:1])
            for sc in range(22):
                w1, w2 = wg[sc % 2], wu[sc % 2]
                for kh in range(2):
                    K.dma("pool", w1[kh][:], C["w_gu"][kh * 1024:(kh + 1) * 1024, sc * 256:(sc + 1) * 256].rearrange("(k p) n -> p k n", p=128),
                          [], [w1[kh]])
                    K.dma("pool", w2[kh][:], C["w_gu"][kh * 1024:(kh + 1) * 1024, FFN + sc * 256:FFN + (sc + 1) * 256].rearrange("(k p) n -> p k n", p=128),
                          [], [w2[kh]])
                for jj in range(2):
                    j = sc * 2 + jj
                    Pg, Pu = pgu[igu % 3], pgu[(igu + 1) % 3]
                    SGt = sgt[(igu // 2) % 2]
                    igu += 2
                    for k in range(16):
                        K.mm(Pg[:, :G], w1[k // 8][:, k % 8, jj * 128:(jj + 1) * 128], h2T[:, k, :], k == 0, k == 15, [w1[k // 8], h2T], [Pg])
                    for k in range(16):
                        K.mm(Pu[:, :G], w2[k // 8][:, k % 8, jj * 128:(jj + 1) * 128], h2T[:, k, :], k == 0, k == 15, [w2[k // 8], h2T], [Pu])
                    K.act(SGt[:], Pg[:, :G], AF.Silu, [Pg], [SGt])
                    K.tt(actT[:, j, :], SGt[:], Pu[:, :G], ALU.mult, [SGt, Pu], [actT])
            iw = 0
            for n in range(4):
                ns = slice(n * 512, (n + 1) * 512)
                for k4 in range(11):
                    W = wd[iw % 4]
                    iw += 1
                    K.dma("pool", W[:], C["w_dn"][k4 * 512:(k4 + 1) * 512, ns].rearrange("(k p) n -> p k n", p=128), [], [W])
                    for kk in range(4):
                        k = k4 * 4 + kk
                        for t in range(4):
                            K.mm(pdn[t][:], actT[:, k, t * 128:(t + 1) * 128], W[:, kk, :], k == 0, k == 43, [actT, W], [pdn[t]])
                for t in range(4):
                    T_ = tq[t % 2]
                    K.tt(T_[:], pdn[t][:], bc[:, 1, ns], ALU.mult, [pdn[t], bc], [T_])
                    K.tt(x1t[t][:, ns], x1t[t][:, ns], T_[:], ALU.add, [x1t[t], T_], [x1t[t]], eng="pool")
            for t in range(4):
                X_ = x1t[t]
                O_ = ot[t % 2]
                row0 = grp * G + t * 128
                K.act(junk[:], X_[:], AF.Square, [X_], [junk, ss], accum=ss[:])
                K.act(rs[:], ss[:], AF.Sqrt, [ss], [rs], scale=1.0 / D, bias=EPS)
                K.recip(rs[:], rs[:], [rs], [rs])
                K.stt(O_[:], X_[:], rs[:], bc[:, 2, :], ALU.mult, ALU.mult, [X_, rs, bc], [O_])
                K.dma("sp", out_d[row0:row0 + 128, :], O_[:], [O_], [out_d])

NHP = 8
NGH = 8
NGRP = 4
RELAX = [True, True, True, True, True]
RATIO = 3
RATIO_G = 8


def build(debug=False, only=None):
    nc = bass.Bass("TRN2", target_bir_lowering=False)
    K = KB(nc)
    inp = lambda name, shape: K.dram(name, shape, F32, kind="ExternalInput")
    x_d = inp("x", [TL, D])
    ctx_d = inp("ctx", [TC, D])
    cc_d = inp("cc", [128, 16, 2])
    wada_d = inp("w_ada", [D, 6 * D])
    bada_d = inp("b_ada", [128, 96])
    n1w_d = inp("norm1_w", [128, 16])
    n2w_d = inp("norm2_w", [128, 16])
    fnw_d = inp("final_norm_w", [1, D])
    win_d = inp("w_in", [D, NIN * 128])
    mu_d = inp("rw_mu", [128, 29])
    cmask_d = inp("cmask", [128, 8])
    convw_d = inp("gdn_conv_w", [128, 24, 5])
    ident_d = inp("ident", [128, 128])
    w0_d = inp("rw_w0", [128, 2, 8])
    a0_d = inp("rw_a0", [128, 2, 8])
    kkw_d = inp("rw_k_k", [128, 8])
    ka_d = inp("rw_k_a", [128, 8])
    rk_d = inp("rw_r_k", [128, 8])
    w2_d = inp("rw_w2", [2, 96, 1024])
    a2_d = inp("rw_a2", [2, 96, 1024])
    g2_d = inp("rw_g2", [64, 1024])
    lnw_d = inp("rw_ln_w", [1, 1024])
    lnb_d = inp("rw_ln_b", [1, 1024])
    m1_d = inp("m1", [128, 2, 4, 128])
    m2_d = inp("m2", [128, 2, 3, 128])
    rmask_d = inp("rmask", [128, 512])
    bones_d = inp("blockones", [128, 128])
    hsel_d = inp("headsel", [128, 2])
    nmask_d = inp("nmask", [128, 2, 7, 128])
    selg_d = inp("selg", [64, 16, 128])
    selb_d = inp("selb", [64, 16, 128])
    bigm_d = inp("bigm", [128, 2, 2, 128])
    offd_d = inp("offd", [128, 128])
    ones_d = inp("ones", [128, 128])
    alog_d = inp("gdn_a_log", [64, 1])
    dtb_d = inp("gdn_dt_bias", [64, 1])
    gnw_d = inp("gdn_norm_w", [1, 128])
    pa_d = inp("merge_p_a", [1024, D])
    pb_d = inp("merge_p_b", [1024, D])
    wout_d = inp("w_out", [D, D])
    wgu_d = inp("ffn_w_gate_up", [D, 2 * FFN])
    wdn_d = inp("ffn_w_down", [FFN, D])
    MODS = K.dram("MODS", [32, 128], F32)
    GBS = K.dram("GBS", [2, 64, TT], F32)
    out_d = K.dram("out", [TL, D], F32, kind="ExternalOutput")
    dbg = {}
    if debug:
        dbg["xs"] = K.dram("dbg_xs", [24 * 128, TT], F32, kind="ExternalOutput")
        dbg["u"] = K.dram("dbg_u", [24 * 128, TT], F32, kind="ExternalOutput")
        dbg["mod"] = K.dram("dbg_mod", [128, 96 * 2], F32, kind="ExternalOutput")
        dbg["sm"] = K.dram("dbg_sm", [128, 5, TT], F32, kind="ExternalOutput")
        dbg["oa"] = K.dram("dbg_oa", [128, 8, TL], F32, kind="ExternalOutput")
        dbg["yf"] = K.dram("dbg_yf", [8, 128, 16, 128], F32, kind="ExternalOutput")
        dbg["ob"] = K.dram("dbg_ob", [128, 8, TL], F32, kind="ExternalOutput")
        dbg["of"] = K.dram("dbg_of", [8, 128, 16, 128], F32, kind="ExternalOutput")
    if only:
        XS = inp("XS_in", [24 * 128, TT])
        small_in = inp("small_in", [128, 5, TT])
        US = inp("US_in", [24 * 128, TT])
        SZ = inp("SZ_in", [8 * 128, TL])
        ab_in = inp("ab_in", [128, TT])
    else:
        XS = dbg["xs"] if debug else K.dram("XS", [24 * 128, TT], F32)
    if not only:
        US = dbg["u"] if debug else K.dram("US", [24 * 128, TT], F32)
        SZ = K.dram("SZ", [8 * 128, TL], F32)
    SG = K.dram("SG", [32 * 128, TL], BF16)
    X1 = inp("X1_in", [TL, D]) if only == "ffn" else K.dram("X1", [TL, D], F32)

    with ExitStack() as top:
        ident_f = K.sb(top, [128, 128], F32, "ident_f")
        ident_b = K.sb(top, [128, 128], BF16, "ident_b")
        K.dma("sp", ident_f[:], ident_d[:, :], [], [ident_f])
        K.copy(ident_b[:], ident_f[:], [ident_f], [ident_b])
        modT = K.sb(top, [128, 96, 2], F32, "modT")
        s1 = K.sb(top, [128, 16, 2], F32, "s1")
        s2 = K.sb(top, [128, 16], F32, "s2")
        scopeO = ExitStack()
        abT = K.sb(scopeO, [128, TT], F32, "abT")
        oaT = K.sb(scopeO, [128, 8, TL], BF16, "oaT")
        scopeA = ExitStack()
        smallT = K.sb(scopeA, [128, 5, TT], BF16, "smallT")

        if only == "rwkv":
            K.dma("pool", smallT[:], small_in[:, :, :], [], [smallT])
            K.dma("sp", abT[:], ab_in[:, :], [], [abT])
        K.S.relax = True
        with ExitStack() as p0:
          if only != "rwkv":
                ccf = K.sb(p0, [128, 16, 2], F32)
                ccs = K.sb(p0, [128, 16, 2], F32)
                ccb = K.sb(p0, [128, 16, 2], BF16)
                bada = K.sb(p0, [128, 96], F32)
                n1w = K.sb(p0, [128, 16], F32)
                n2w = K.sb(p0, [128, 16], F32)
                K.dma("sp", ccf[:], cc_d[:, :, :], [], [ccf])
                K.dma("sp", bada[:], bada_d[:, :], [], [bada])
                K.dma("sp", n1w[:], n1w_d[:, :], [], [n1w])
                K.dma("sp", n2w[:], n2w_d[:, :], [], [n2w])
                K.act(ccs[:], ccf[:], AF.Silu, [ccf], [ccs])
                K.copy(ccb[:], ccs[:], [ccs], [ccb])
                wb = [[K.sb(p0, [128, 8, 512], BF16) for _ in range(2)] for _ in range(4)]
                pm = K.ps(p0, [128, 96, 2], F32)
                for sc in range(24):
                    w = wb[sc % 4]
                    for kh in range(2):
                        K.dma("pool", w[kh][:],
                              wada_d[kh * 1024:(kh + 1) * 1024, sc * 512:(sc + 1) * 512].rearrange("(k p) n -> p k n", p=128),
                              [], [w[kh]])
                    for jj in range(4):
                        j = sc * 4 + jj
                        for k in range(16):
                            K.mm(pm[:, j, :], w[k // 8][:, k % 8, jj * 128:(jj + 1) * 128], ccb[:, k, :], k == 0, k == 15,
                                 [w[k // 8], ccb], [pm])
                K.tt(modT[:], pm[:], bada[:].unsqueeze(2).to_broadcast([128, 96, 2]), ALU.add, [pm, bada], [modT])
                for v in range(2):
                    K.stt(s1[:, :, v], modT[:, 16:32, v], 1.0, n1w[:], ALU.add, ALU.mult, [modT, n1w], [s1])
                K.stt(s2[:], modT[:, 64:80, 0], 1.0, n2w[:], ALU.add, ALU.mult, [modT, n2w], [s2])
                if debug:
                    K.dma("sp", dbg["mod"][:, :], modT[:].rearrange("p a b -> p (a b)"), [modT], [])

        K.S.barrier()
        with ExitStack() as p1:
          if not only:
                hT = K.sb(p1, [128, 16, TT], BF16, "hT")
                mu = K.sb(p1, [128, 29], F32)
                omm = K.sb(p1, [128, 29], F32)
                cmask = K.sb(p1, [128, 8], F32)
                coef = K.sb(p1, [128, 6, 29], F32)
                convw = K.sb(p1, [128, 24, 5], F32)
                K.dma("sp", mu[:], mu_d[:, :], [], [mu])
                K.dma("sp", cmask[:], cmask_d[:, :], [], [cmask])
                K.dma("sp", convw[:], convw_d[:, :, :], [], [convw])
                K.ts(omm[:], mu[:], -1.0, 1.0, ALU.mult, ALU.add, [mu], [omm])
                for m in range(6):
                    K.ts(coef[:, m, :], mu[:], cmask[:, m:m + 1], None, ALU.mult, None, [mu, cmask], [coef])
                with ExitStack() as pa:
                    xt = [K.sb(pa, [128, D], F32) for _ in range(2)]
                    xb = [K.sb(pa, [128, D], BF16) for _ in range(2)]
                    junk = K.sb(pa, [128, D], BF16)
                    ss = [K.sb(pa, [128, 1], F32) for _ in range(2)]
                    rs = [K.sb(pa, [128, 1], F32) for _ in range(2)]
                    pt = [K.ps(pa, [128, 8, 128], BF16) for _ in range(2)]
                    npt = 0
                    for t in range(NCH):
                        X, XB, SS, RS = xt[t % 2], xb[t % 2], ss[t % 2], rs[t % 2]
                        src = ctx_d[t * 128:(t + 1) * 128, :] if t < 2 else x_d[(t - 2) * 128:(t - 1) * 128, :]
                        v = 1 if t < 2 else 0
                        K.dma("sp", X[:], src, [], [X])
                        K.act(junk[:], X[:], AF.Square, [X], [junk, SS], accum=SS[:])
                        K.act(RS[:], SS[:], AF.Sqrt, [SS], [RS], scale=1.0 / D, bias=EPS)
                        K.recip(RS[:], RS[:], [RS], [RS])
                        K.act(XB[:], X[:], AF.Copy, [X, RS], [XB], scale=RS[:])
                        for g in range(4):
                            P = pt[npt % 2]
                            npt += 1
                            for j in range(4):
                                k = g * 4 + j
                                K.tr(P[:, j, :], XB[:, k * 128:(k + 1) * 128], ident_b[:], [XB, ident_b], [P])
                            for j in range(4):
                                k = g * 4 + j
                                K.act(hT[:, k, t * 128:(t + 1) * 128], P[:, j, :], AF.Identity, [P, s1, modT], [hT],
                                      scale=s1[:, k, v:v + 1], bias=modT[:, k, v:v + 1])
                K.S.relax = RELAX[0]
                K.S.barrier()
                with ExitStack() as pb:
                    wb = [[K.sb(pb, [128, 8, 512], BF16) for _ in range(2)] for _ in range(2)]
                    stage = [K.sb(pb, [128, TT], F32) for _ in range(2)]
                    post = [K.sb(pb, [128, TT], F32) for _ in range(1)] * 2
                    postb = [K.sb(pb, [128, TL], BF16) for _ in range(1)] * 2
                    pp = [K.ps(pb, [128, 512], F32) for _ in range(4)]
                    npp = 0
                    ntile = [(0, 256)] + [(256 + i * 512, 512) for i in range(4)]
                    for sc in range(24):
                        ncol = min(512, NIN * 128 - sc * 512)
                        w = wb[sc % 2]
                        for kh in range(2):
                            K.dma("pool", w[kh][:, :, :ncol],
                                  win_d[kh * 1024:(kh + 1) * 1024, sc * 512:sc * 512 + ncol].rearrange("(k p) n -> p k n", p=128),
                                  [], [w[kh]])
                        for jj in range(ncol // 128):
                            q = sc * 4 + jj
                            lat_only = (53 <= q <= 60) or q >= 62
                            stg = stage[q % 2]
                            for (t0, tn) in ntile:
                                if lat_only and t0 == 0:
                                    continue
                                P = pp[npp % 4]
                                npp += 1
                                for k in range(16):
                                    K.mm(P[:, :tn], w[k // 8][:, k % 8, jj * 128:(jj + 1) * 128], hT[:, k, t0:t0 + tn],
                                         k == 0, k == 15, [w[k // 8], hT], [P])
                                if q <= 28 or 29 <= q <= 52 or q == 61:
                                    K.act(stg[:, t0:t0 + tn], P[:, :tn], AF.Copy, [P], [stg])
                                elif 53 <= q <= 60:
                                    K.act(stg[:, t0:t0 + tn], P[:, :tn], AF.Silu, [P], [stg])
                                else:
                                    K.act(postb[q % 2][:, t0 - 256:t0 - 256 + tn], P[:, :tn], AF.Sigmoid, [P], [postb[q % 2]])
                            if q <= 28:
                                xs = post[q % 2]
                                K.ts(xs[:], stg[:], omm[:, q:q + 1], None, ALU.mult, None, [stg, omm], [xs])
                                pl = stg[:, 256:TT].rearrange("p (r c) -> p r c", c=64)
                                xl = xs[:, 256:TT].rearrange("p (r c) -> p r c", c=64)
                                sh = [(xl[:, :, 1:64], pl[:, :, 0:63]), (xl[:, :, 0:63], pl[:, :, 1:64]),
                                      (xl[:, 1:32, :], pl[:, 0:31, :]), (xl[:, 0:31, :], pl[:, 1:32, :]),
                                      (xs[:, 1:256], stg[:, 0:255]), (xs[:, 0:255], stg[:, 1:256])]
                                for m, (o, i) in enumerate(sh):
                                    K.stt(o, i, coef[:, m, q:q + 1], o, ALU.mult, ALU.add, [stg, xs, coef], [xs])
                                if q < 24:
                                    K.dma("sp", XS[q * 128:(q + 1) * 128, :], xs[:], [xs], [XS])
                                elif q < 26:
                                    K.act(smallT[:, q - 24, :], xs[:], AF.Tanh, [xs], [smallT])
                                elif q < 28:
                                    K.act(smallT[:, q - 24, :], xs[:], AF.Copy, [xs], [smallT])
                                else:
                                    K.act(smallT[:, 4, :], xs[:], AF.Sigmoid, [xs], [smallT])
                            elif q <= 52:
                                g = q - 29
                                acc = post[q % 2]
                                K.ts(acc[:], stg[:], convw[:, g, 2:3], None, ALU.mult, None, [stg, convw], [acc])
                                for (a, b) in ((0, 256), (256, TT)):
                                    for j, o in ((0, 2), (1, 1), (3, -1), (4, -2)):
                                        if o > 0:
                                            ov, iv = acc[:, a + o:b], stg[:, a:b - o]
                                        else:
                                            ov, iv = acc[:, a:b + o], stg[:, a - o:b]
                                        K.stt(ov, iv, convw[:, g, j:j + 1], ov, ALU.mult, ALU.add, [stg, acc, convw], [acc])
                                K.act(acc[:], acc[:], AF.Silu, [acc], [acc])
                                K.dma("sp", US[g * 128:(g + 1) * 128, :], acc[:], [acc], [US])
                            elif q <= 60:
                                K.dma("sp", SZ[(q - 53) * 128:(q - 52) * 128, :], stg[:, 256:TT], [stg], [SZ])
                            elif q == 61:
                                K.copy(abT[:], stg[:], [stg], [abT], eng="pool")
                            else:
                                K.dma("sp", SG[(q - 62) * 128:(q - 61) * 128, :], postb[q % 2][:], [postb[q % 2]], [SG])
                K.S.barrier()
                if debug:
                    smf = K.sb(p1, [128, 5, TT], F32)
                    K.copy(smf[:], smallT[:], [smallT], [smf])
                    K.dma("sp", dbg["sm"][:, :, :], smf[:], [smf], [])

        K.S.relax = RELAX[3]
        K.S.barrier()
        with ExitStack() as p2:
          if only != "ffn":
            rwkv_phase(K, p2, dict(XS=XS, smallT=smallT, oaT=oaT, ident_b=ident_b, ident_f=ident_f,
                                   w0=w0_d, a0=a0_d, kkw=kkw_d, ka=ka_d, rk=rk_d, w2=w2_d, a2=a2_d, g2=g2_d,
                                   lnw=lnw_d, lnb=lnb_d, m1=m1_d, m2=m2_d, rmask=rmask_d, bones=bones_d, hsel=hsel_d, nmask=nmask_d,
                                   dbg=dbg, nhp=NHP))
        K.S.barrier()
        if debug:
            with ExitStack() as pd:
                of = K.sb(pd, [128, 8, TL], F32)
                K.copy(of[:, 0:NHP], oaT[:, 0:NHP], [oaT], [of])
                K.dma("sp", dbg["oa"][:, 0:NHP, :], of[:, 0:NHP], [of], [])
            K.S.barrier()
        scopeA.close()
        obT = K.sb(scopeO, [128, 8, TL], BF16, "obT")
        K.S.relax = RELAX[4]
        with ExitStack() as p2b:
          if only != "ffn":
            gdn_phase(K, p2b, dict(US=US, SZ=SZ, abT=abT, obT=obT, ident_b=ident_b, ident_f=ident_f, selg=selg_d, selb=selb_d,
                                   bigm=bigm_d, offd=offd_d, ones=ones_d, rmask=rmask_d, alog=alog_d, dtb=dtb_d, gnw=gnw_d,
                                   nmask=nmask_d, dbg=dbg, nh=NGH, GBS=GBS))
        K.S.barrier()
        if debug:
            with ExitStack() as pd:
                of = K.sb(pd, [128, 8, TL], F32)
                K.copy(of[:, 0:NGH], obT[:, 0:NGH], [obT], [of])
                K.dma("sp", dbg["ob"][:, 0:NGH, :], of[:, 0:NGH], [of], [])
            K.S.barrier()
        C34 = dict(oaT=oaT, obT=obT, ident_b=ident_b, ident_f=ident_f, modT=modT, s2=s2, SG=SG, X1=X1,
                   x=x_d, out=out_d, MODS=MODS, fnw=fnw_d, p_a=pa_d, p_b=pb_d, w_out=wout_d, w_gu=wgu_d,
                   w_dn=wdn_d, dbg=dbg)
        K.S.relax = RELAX[1]
        if only != "rwkv":
            write_mods(K, C34)
        if not only:
            merge_phase(K, scopeO, C34)
        scopeO.close()
        K.S.relax = RELAX[2]
        K.S.barrier()
        if only != "rwkv":
            ffn_phase(K, top, C34)
        else:
            with ExitStack() as pz:
                z = K.sb(pz, [128, D], F32)
                K.memset(z[:], 0.0, [z])
                K.dma("sp", out_d[0:128, :], z[:], [z], [out_d])
        K.S.emit(nc, top)
    nc._marks = getattr(K.S, "marks", [])
    return nc


def _fm(v, nchunk):
    return np.ascontiguousarray(np.asarray(v, np.float32).reshape(nchunk, 128).T)


def _pad_cols(a, n):
    out = np.zeros(a.shape[:-1] + (n,), np.float32)
    out[..., :a.shape[-1]] = a
    return out


def prep_shared(inputs):
    w_in = np.asarray(inputs["w_in"][0], np.float32)
    RW = 3520
    segs = [w_in[:, 0:3072]]
    for (a, b) in ((3072, 3168), (3168, 3264), (3264, 3360), (3360, 3456), (3456, 3520)):
        segs.append(_pad_cols(w_in[:, a:b], 128))
    segs.append(w_in[:, RW:RW + 3072 + 1024])
    abc = np.zeros((D, 128), np.float32)
    abc[:, 0:16] = w_in[:, 7616:7632]
    abc[:, 32:48] = w_in[:, 7632:7648]
    segs.append(abc)
    segs.append(w_in[:, 7648:])
    win = np.ascontiguousarray(np.concatenate(segs, axis=1))
    assert win.shape == (D, NIN * 128)
    mu = np.asarray(inputs["rw_mu"][0], np.float32)
    mus = [mu[0:3072]]
    for (a, b) in ((3072, 3168), (3168, 3264), (3264, 3360), (3360, 3456), (3456, 3520)):
        mus.append(_pad_cols(mu[a:b], 128))
    mu_fm = _fm(np.concatenate(mus), 29)
    p = np.arange(128)
    cmask = np.zeros((128, 8), np.float32)
    for m in range(4):
        cmask[:, m] = (p % 4 == m)
    cmask[:, 4] = (p % 2 == 0)
    cmask[:, 5] = (p % 2 == 1)
    convw = np.asarray(inputs["gdn_conv_w"][0], np.float32)
    convw_fm = np.ascontiguousarray(convw.reshape(5, 24, 128).transpose(2, 1, 0))
    sh = {
        "w_ada": np.ascontiguousarray(inputs["w_ada"][0], np.float32),
        "b_ada": _fm(inputs["b_ada"][0], 96),
        "norm1_w": _fm(inputs["norm1_w"][0], 16),
        "norm2_w": _fm(inputs["norm2_w"][0], 16),
        "final_norm_w": np.ascontiguousarray(np.asarray(inputs["final_norm_w"], np.float32).reshape(1, D)),
        "w_in": win,
        "rw_mu": mu_fm,
        "cmask": cmask,
        "gdn_conv_w": convw_fm,
        "ident": np.eye(128, dtype=np.float32),
    }
    g = lambda k: np.asarray(inputs[k][0], np.float32)
    sh["rw_w0"] = np.ascontiguousarray(g("rw_w0").reshape(2, 8, 128).transpose(2, 0, 1))
    sh["rw_a0"] = np.ascontiguousarray(g("rw_a0").reshape(2, 8, 128).transpose(2, 0, 1))
    sh["rw_k_k"] = _fm(g("rw_k_k"), 8)
    sh["rw_k_a"] = _fm(g("rw_k_a"), 8)
    sh["rw_r_k"] = _fm(g("rw_r_k").reshape(-1), 8)
    sh["rw_w2"] = np.ascontiguousarray(g("rw_w2"))
    sh["rw_a2"] = np.ascontiguousarray(g("rw_a2"))
    sh["rw_g2"] = np.ascontiguousarray(g("rw_g2"))
    sh["rw_ln_w"] = np.ascontiguousarray(g("rw_ln_w").reshape(1, 1024))
    sh["rw_ln_b"] = np.ascontiguousarray(g("rw_ln_b").reshape(1, 1024))
    r_ = np.arange(128)[:, None]; c_ = np.arange(128)[None, :]
    SL = (c_ < r_).astype(np.float32); SU = (c_ > r_).astype(np.float32)
    IL = (c_ <= r_).astype(np.float32); IU = (c_ >= r_).astype(np.float32)
    m1 = np.stack([np.stack([SL, SU, SL, SU], 0), np.stack([SU, SL, SU, SL], 0)], 0)
    m2 = np.stack([np.stack([SU, IU, -IU], 0), np.stack([SL, IL, -IL], 0)], 0)
    sh["m1"] = np.ascontiguousarray(m1.transpose(2, 0, 1, 3))
    sh["m2"] = np.ascontiguousarray(m2.transpose(2, 0, 1, 3))
    rmask = np.ones((128, 512), np.float32); rmask[:, ::128] = 0.0
    sh["rmask"] = rmask
    bo = np.zeros((128, 128), np.float32); bo[:64, :64] = 1.0; bo[64:, 64:] = 1.0
    sh["blockones"] = bo
    hs = np.zeros((128, 2), np.float32); hs[:64, 0] = 1.0; hs[64:, 1] = 1.0
    sh["headsel"] = hs
    nmk = np.zeros((2, 7, 128, 128), np.float32)
    for lv in range(7):
        bsz = 1 << lv
        low = ((r_ // (2 * bsz) == c_ // (2 * bsz)) & ((r_ // bsz) % 2 == 1) & ((c_ // bsz) % 2 == 0)).astype(np.float32)
        nmk[0, lv] = -low
        nmk[1, lv] = -low.T
    sh["nmask"] = np.ascontiguousarray(nmk.transpose(2, 0, 1, 3))
    selg = np.zeros((64, 16, 128), np.float32); selb = np.zeros((64, 16, 128), np.float32)
    for r0 in range(16):
        selg[r0, r0, :] = 1.0
        selb[32 + r0, r0, :] = 1.0
    sh["selg"] = selg; sh["selb"] = selb
    BIG = 1.0e4
    bigm = np.stack([np.stack([BIG * SU, -BIG * SL], 0), np.stack([BIG * SL, -BIG * SU], 0)], 0)
    sh["bigm"] = np.ascontiguousarray(bigm.transpose(2, 0, 1, 3))
    sh["offd"] = (1.0 - np.eye(128)).astype(np.float32)
    sh["ones"] = np.ones((128, 128), np.float32)
    al = np.zeros((64, 1), np.float32); al[0:16, 0] = g("gdn_a_log").reshape(-1)
    db = np.zeros((64, 1), np.float32); db[0:16, 0] = g("gdn_dt_bias").reshape(-1)
    sh["gdn_a_log"] = al; sh["gdn_dt_bias"] = db
    sh["gdn_norm_w"] = np.ascontiguousarray(g("gdn_norm_w").reshape(1, 128))
    for k_ in ("merge_p_a", "merge_p_b", "w_out", "ffn_w_gate_up", "ffn_w_down"):
        sh[k_] = np.ascontiguousarray(g(k_))
    return sh


def make_in_maps(inputs):
    sh = prep_shared(inputs)
    maps = []
    for b in range(8):
        m = dict(sh)
        m["x"] = np.ascontiguousarray(inputs["x"][b], np.float32)
        m["ctx"] = np.ascontiguousarray(inputs["ctx"][b], np.float32)
        cc = np.stack([np.asarray(inputs["c"][b], np.float32), np.asarray(inputs["c_ctx"], np.float32)], axis=-1)
        m["cc"] = np.ascontiguousarray(cc.reshape(16, 128, 2).transpose(1, 0, 2))
        maps.append(m)
    return maps


_NC = None


def kernel(**inputs):
    global _NC
    if _NC is None:
        _NC = build()
    maps = make_in_maps(inputs)
    res = run_bass_kernel_spmd(_NC, maps, core_ids=list(range(8)))
    return np.stack([r["out"] for r in res.results], axis=0).astype(np.float32)
```
